# Optimizing a Trainium2 kernel written in Bass

```python
import jax
import jax.numpy as jnp
from jax import lax
import numpy as np

D_MODEL = 1024
BATCH = 8
SEQ = 4096
DEPTH = 2

CTX_LEN = 256
GRID_W = 64
N_MOD = 6
RMS_EPS = 1e-6
NEG_INF = -1e30
D_POOL = D_MODEL // 2
POOL_WINDOWS = (2, 4, 8, 16)
D_POOL_GROUP = D_POOL // len(POOL_WINDOWS)
NA_HEAD_DIM = 64
D_NA = D_MODEL - D_POOL
NA_HEADS = D_NA // NA_HEAD_DIM
WIN_R = 8
WIN_C = 16
D_IN_EVEN = D_POOL + 3 * D_NA
RW_HEAD = 64
RW_HEADS = D_MODEL // RW_HEAD
DECAY_LORA = 64
AAA_LORA = 64
GATE_LORA = 128
GN_EPS = 64e-5
D_FF = 4 * D_MODEL

kernel_name = "hybrid_pool_natten_rwkv7_prefix_dit"


def rms_norm(x, g):
    xf = x.astype(jnp.float32)
    y = xf * lax.rsqrt(jnp.mean(xf * xf, axis=-1, keepdims=True) + RMS_EPS)
    return (y * g.astype(jnp.float32)).astype(x.dtype)


def modulate(h, shift, scale):
    return h * (1 + scale) + shift


def squared_relu_mlp(h, w1, w2):
    return jnp.square(jax.nn.relu(h @ w1)) @ w2


def split_heads(t, n_heads, head_dim):
    return t.reshape(t.shape[0], t.shape[1], n_heads, head_dim)


def pool_mixer(u, pool_w, pool_scale):
    L = u.shape[1]
    uf = u.astype(jnp.float32)
    cs = jnp.concatenate([jnp.zeros_like(uf[:, :1]), jnp.cumsum(uf, axis=1)], axis=1)
    t = jnp.arange(L)
    outs = []
    for gi, w in enumerate(POOL_WINDOWS):
        lo = jnp.clip(t - w // 2, 0, L)
        hi = jnp.clip(t + w // 2, 0, L)
        sl = slice(gi * D_POOL_GROUP, (gi + 1) * D_POOL_GROUP)
        csg = cs[..., sl]
        mean = (jnp.take(csg, hi, axis=1) - jnp.take(csg, lo, axis=1)) / (hi - lo).astype(jnp.float32)[None, :, None]
        outs.append((mean - uf[..., sl]).astype(u.dtype) @ pool_w[gi])
    return jnp.concatenate(outs, axis=-1) * pool_scale


def context_attention(q, k, v):
    B, L = q.shape[:2]
    s = jnp.einsum('bqhd,bkhd->bhqk', q, k).astype(jnp.float32) * (NA_HEAD_DIM ** -0.5)
    p = jax.nn.softmax(s, axis=-1).astype(v.dtype)
    return jnp.einsum('bhqk,bkhd->bqhd', p, v).reshape(B, L, D_NA)


def neighbourhood_attention(q, k, v, k_ctx, v_ctx, rpb):
    B, L = q.shape[:2]
    rows = L // GRID_W
    kr = min(WIN_R, rows)
    scale = NA_HEAD_DIM ** -0.5
    grid = lambda t: t.reshape(B, rows, GRID_W, NA_HEADS, NA_HEAD_DIM)
    qg, kg, vg = grid(q), grid(k), grid(v)
    col = jnp.arange(GRID_W)
    c_start = jnp.clip(col - WIN_C // 2, 0, GRID_W - WIN_C)
    col_ok = (col[None, :] >= c_start[:, None]) & (col[None, :] < c_start[:, None] + WIN_C)
    dc_idx = jnp.clip(col[None, :] - col[:, None], -(WIN_C - 1), WIN_C - 1) + WIN_C - 1
    rpb_c = rpb.astype(jnp.float32)[:, :, dc_idx]

    def row_block(args):
        q_row, r = args
        start = jnp.clip(r - kr // 2, 0, rows - kr)
        k_band = lax.dynamic_slice_in_dim(kg, start, kr, axis=1)
        v_band = lax.dynamic_slice_in_dim(vg, start, kr, axis=1)
        s_lat = jnp.einsum('bqhd,brkhd->bhqrk', q_row, k_band).astype(jnp.float32) * scale
        dr_idx = start + jnp.arange(kr) - r + WIN_R - 1
        bias = jnp.take(rpb_c, dr_idx, axis=1).transpose(0, 2, 1, 3)
        s_lat = jnp.where(col_ok[:, None, :], s_lat + bias, NEG_INF)
        s_ctx = jnp.einsum('bqhd,bchd->bhqc', q_row, k_ctx).astype(jnp.float32) * scale
        s = jnp.concatenate([s_lat.reshape(B, NA_HEADS, GRID_W, kr * GRID_W), s_ctx], axis=-1)
        p = jax.nn.softmax(s, axis=-1).astype(v.dtype)
        p_lat = p[..., :kr * GRID_W].reshape(B, NA_HEADS, GRID_W, kr, GRID_W)
        p_ctx = p[..., kr * GRID_W:]
        return (jnp.einsum('bhqrk,brkhd->bqhd', p_lat, v_band)
                + jnp.einsum('bhqc,bchd->bqhd', p_ctx, v_ctx))

    out = lax.map(row_block, (jnp.moveaxis(qg, 1, 0), jnp.arange(rows)))
    return jnp.moveaxis(out, 0, 1).reshape(B, L, D_NA)


def even_mixer(h, hc, w_in, w_out, pool_w, pool_scale, rpb, with_ctx):
    heads = lambda t: split_heads(t, NA_HEADS, NA_HEAD_DIM)
    u = h @ w_in
    q = heads(u[..., D_POOL:D_POOL + D_NA])
    k = heads(u[..., D_POOL + D_NA:D_POOL + 2 * D_NA])
    v = heads(u[..., D_POOL + 2 * D_NA:])
    if with_ctx:
        uc = hc @ w_in
        kv_c = uc[..., D_POOL + D_NA:]
    else:
        kv_c = hc @ w_in[:, D_POOL + D_NA:]
    k_c, v_c = heads(kv_c[..., :D_NA]), heads(kv_c[..., D_NA:])
    y = jnp.concatenate([pool_mixer(u[..., :D_POOL], pool_w, pool_scale),
                         neighbourhood_attention(q, k, v, k_c, v_c, rpb)], axis=-1) @ w_out
    if with_ctx:
        yc = jnp.concatenate([pool_mixer(uc[..., :D_POOL], pool_w, pool_scale),
                              context_attention(heads(uc[..., D_POOL:D_POOL + D_NA]), k_c, v_c)], axis=-1) @ w_out
    else:
        yc = None
    return y, yc


def centred_shift(x):
    prev = jnp.pad(x[:, :-1], ((0, 0), (1, 0), (0, 0)))
    nxt = jnp.pad(x[:, 1:], ((0, 0), (0, 1), (0, 0)))
    return 0.5 * (prev + nxt) - x


def rwkv_features(h, mu, wr, wk, wv, w0, w1, w2, a0, a1, a2, g1, g2, k_k, k_a, with_rg):
    B, L, _ = h.shape
    heads = lambda t: split_heads(t, RW_HEADS, RW_HEAD).astype(jnp.float32)
    xx = centred_shift(h)
    mix = lambda j: h + xx * mu[j]
    xw, xk, xv, xa = mix(1), mix(2), mix(3), mix(4)
    k = xk @ wk
    v = heads(xv @ wv)
    kk = heads(k * k_k)
    kk = kk * lax.rsqrt(jnp.maximum(jnp.sum(kk * kk, axis=-1, keepdims=True), 1e-24))
    kh = heads(k)
    k_a_h = k_a.reshape(RW_HEADS, RW_HEAD).astype(jnp.float32)
    dirs = []
    for d in range(2):
        w_raw = heads(w0[d] + jnp.tanh(xw @ w1[d]) @ w2[d])
        decay = jnp.exp(-jnp.exp(-jax.nn.softplus(-w_raw) - 0.5))
        a = jax.nn.sigmoid(heads(a0[d] + (xa @ a1[d]) @ a2[d]))
        k_d = kh * (1 + (a - 1) * k_a_h)
        dirs.append((decay, k_d, a))
    if with_rg:
        r = heads(mix(0) @ wr)
        g = jax.nn.sigmoid(mix(5) @ g1) @ g2
    else:
        r, g = None, None
    return v, kk, dirs, r, g


def wkv_scan(s0, decay, k, v, kk, a, r, reverse):
    tm = lambda t: jnp.moveaxis(t.astype(jnp.float32), 1, 0)
    emit = r is not None
    xs = (tm(decay), tm(k), tm(v), tm(kk), tm(kk * a)) + ((tm(r),) if emit else ())

    def step(S, inp):
        w_t, k_t, v_t, kk_t, kka_t = inp[:5]
        S = (S * w_t[:, :, None, :]
             - jnp.einsum('bhij,bhj->bhi', S, kk_t)[..., None] * kka_t[:, :, None, :]
             + v_t[..., None] * k_t[:, :, None, :])
        y = jnp.einsum('bhij,bhj->bhi', S, inp[5]) if emit else None
        return S, y

    S, ys = lax.scan(step, s0, xs, reverse=reverse)
    return S, (jnp.moveaxis(ys, 0, 1) if emit else None)


def rwkv_readout(y, r, k_sum, v, g, r_k, ln_g, ln_b, wo, dtype):
    B, L = y.shape[:2]
    mean = jnp.mean(y, axis=-1, keepdims=True)
    var = jnp.mean(jnp.square(y - mean), axis=-1, keepdims=True)
    yn = ((y - mean) * lax.rsqrt(var + GN_EPS)).reshape(B, L, D_MODEL) * ln_g + ln_b
    bonus = (jnp.sum(r * k_sum * r_k.astype(jnp.float32), axis=-1, keepdims=True) * v).reshape(B, L, D_MODEL)
    return ((yn + bonus) * g).astype(dtype) @ wo


def odd_mixer(h, hc, mu, wr, wk, wv, wo, w0, w1, w2, a0, a1, a2, g1, g2, k_k, k_a, r_k, ln_g, ln_b, with_ctx):
    feat = lambda t, rg: rwkv_features(t, mu, wr, wk, wv, w0, w1, w2, a0, a1, a2, g1, g2, k_k, k_a, rg)
    v, kk, dirs, r, g = feat(h, True)
    v_c, kk_c, dirs_c, r_c, g_c = feat(hc, with_ctx)
    s0 = jnp.zeros((h.shape[0], RW_HEADS, RW_HEAD, RW_HEAD), jnp.float32)
    ys, ys_c = [], []
    for d, reverse in enumerate((False, True)):
        decay_c, k_dc, a_c = dirs_c[d]
        s_ctx, y_c = wkv_scan(s0, decay_c, k_dc, v_c, kk_c, a_c, r_c, reverse)
        decay, k_d, a = dirs[d]
        _, y_d = wkv_scan(s_ctx, decay, k_d, v, kk, a, r, reverse)
        ys.append(y_d)
        ys_c.append(y_c)
    y = rwkv_readout(ys[0] + ys[1], r, dirs[0][1] + dirs[1][1], v, g, r_k, ln_g, ln_b, wo, h.dtype)
    if with_ctx:
        yc = rwkv_readout(ys_c[0] + ys_c[1], r_c, dirs_c[0][1] + dirs_c[1][1], v_c, g_c, r_k, ln_g, ln_b, wo, hc.dtype)
    else:
        yc = None
    return y, yc


def setup_inputs(seed: int = 0) -> dict:
    key = jax.random.key(seed)
    ks = iter(jax.random.split(key, 40))
    n_even = (DEPTH + 1) // 2
    n_odd = DEPTH // 2
    D = D_MODEL
    nrm = lambda shape, s: jax.random.normal(next(ks), shape, jnp.float32) * s
    return {
        "x": nrm((BATCH, SEQ, D), 1.0),
        "c": nrm((BATCH, D), 1.0),
        "ctx": nrm((BATCH, CTX_LEN, D), 1.0),
        "c_ctx": nrm((D,), 1.0),
        "ada_w": nrm((DEPTH, D, N_MOD * D), 0.5 * D ** -0.5),
        "ada_b": nrm((DEPTH, N_MOD * D), 0.02),
        "norm_g": 1.0 + nrm((DEPTH, 4, D), 0.1),
        "mlp_w1": nrm((DEPTH, D, D_FF), D ** -0.5),
        "mlp_w2": nrm((DEPTH, D_FF, D), D_FF ** -0.5),
        "ev_w_in": nrm((n_even, D, D_IN_EVEN), D ** -0.5),
        "ev_w_out": nrm((n_even, D, D), D ** -0.5),
        "ev_pool_w": nrm((n_even, len(POOL_WINDOWS), D_POOL_GROUP, D_POOL_GROUP), D_POOL_GROUP ** -0.5),
        "ev_pool_scale": 1.0 + nrm((n_even, D_POOL), 0.1),
        "ev_rpb": nrm((n_even, NA_HEADS, 2 * WIN_R - 1, 2 * WIN_C - 1), 0.1),
        "rw_mu": jax.random.uniform(next(ks), (n_odd, 6, D), jnp.float32),
        "rw_wr": nrm((n_odd, D, D), D ** -0.5),
        "rw_wk": nrm((n_odd, D, D), D ** -0.5),
        "rw_wv": nrm((n_odd, D, D), D ** -0.5),
        "rw_wo": nrm((n_odd, D, D), D ** -0.5),
        "rw_w0": -2.0 + nrm((n_odd, 2, D), 0.5),
        "rw_w1": nrm((n_odd, 2, D, DECAY_LORA), D ** -0.5),
        "rw_w2": nrm((n_odd, 2, DECAY_LORA, D), 0.1 * DECAY_LORA ** -0.5),
        "rw_a0": nrm((n_odd, 2, D), 0.1),
        "rw_a1": nrm((n_odd, 2, D, AAA_LORA), D ** -0.5),
        "rw_a2": nrm((n_odd, 2, AAA_LORA, D), 0.1 * AAA_LORA ** -0.5),
        "rw_g1": nrm((n_odd, D, GATE_LORA), D ** -0.5),
        "rw_g2": nrm((n_odd, GATE_LORA, D), GATE_LORA ** -0.5),
        "rw_kk": 0.85 + nrm((n_odd, D), 0.05),
        "rw_ka": 1.0 + nrm((n_odd, D), 0.05),
        "rw_rk": nrm((n_odd, RW_HEADS, RW_HEAD), 0.1),
        "rw_lng": 1.0 + nrm((n_odd, D), 0.1),
        "rw_lnb": nrm((n_odd, D), 0.01),
    }


def reference(x, c, ctx, c_ctx, ada_w, ada_b, norm_g, mlp_w1, mlp_w2,
              ev_w_in, ev_w_out, ev_pool_w, ev_pool_scale, ev_rpb,
              rw_mu, rw_wr, rw_wk, rw_wv, rw_wo, rw_w0, rw_w1, rw_w2,
              rw_a0, rw_a1, rw_a2, rw_g1, rw_g2, rw_kk, rw_ka, rw_rk, rw_lng, rw_lnb):
    silu_c = jax.nn.silu(c)
    silu_cc = jax.nn.silu(c_ctx)
    for i in range(DEPTH):
        j = i // 2
        with_ctx = i < DEPTH - 1
        mod = (silu_c @ ada_w[i] + ada_b[i])[:, None, :]
        mod_c = silu_cc @ ada_w[i] + ada_b[i]
        sh1, sc1, gt1, sh2, sc2, gt2 = jnp.split(mod, N_MOD, axis=-1)
        sh1c, sc1c, gt1c, sh2c, sc2c, gt2c = jnp.split(mod_c, N_MOD, axis=-1)
        h = modulate(rms_norm(x, norm_g[i, 0]), sh1, sc1)
        hc = modulate(rms_norm(ctx, norm_g[i, 0]), sh1c, sc1c)
        if i % 2 == 0:
            y, yc = even_mixer(h, hc, ev_w_in[j], ev_w_out[j], ev_pool_w[j], ev_pool_scale[j], ev_rpb[j], with_ctx)
        else:
            y, yc = odd_mixer(h, hc, rw_mu[j], rw_wr[j], rw_wk[j], rw_wv[j], rw_wo[j],
                              rw_w0[j], rw_w1[j], rw_w2[j], rw_a0[j], rw_a1[j], rw_a2[j],
                              rw_g1[j], rw_g2[j], rw_kk[j], rw_ka[j], rw_rk[j], rw_lng[j], rw_lnb[j], with_ctx)
        x = x + gt1 * rms_norm(y, norm_g[i, 1])
        h = modulate(rms_norm(x, norm_g[i, 2]), sh2, sc2)
        x = x + gt2 * rms_norm(squared_relu_mlp(h, mlp_w1[i], mlp_w2[i]), norm_g[i, 3])
        if with_ctx:
            ctx = ctx + gt1c * rms_norm(yc, norm_g[i, 1])
            hc = modulate(rms_norm(ctx, norm_g[i, 2]), sh2c, sc2c)
            ctx = ctx + gt2c * rms_norm(squared_relu_mlp(hc, mlp_w1[i], mlp_w2[i]), norm_g[i, 3])
    return x
```

```python
import contextlib
import numpy as np
import concourse.bass as bass
import concourse.mybir as mybir

F32 = mybir.dt.float32
BF16 = mybir.dt.bfloat16
AF = mybir.ActivationFunctionType
ALU = mybir.AluOpType
AX = mybir.AxisListType

SEM_LIMIT = 10000


class _Ctr:
    def __init__(self, S, name, step):
        self.S = S
        self.name = name
        self.step = step
        self.gen = 0
        self.sem = S._newsem(f"{name}_0")
        self.val = 0

    def next_event(self):
        if self.val + self.step > SEM_LIMIT:
            self.gen += 1
            self.sem = self.S._newsem(f"{self.name}_{self.gen}")
            self.val = 0
        self.val += self.step
        return (self.sem, self.val)


class _PsView:
    def __init__(self, t, shape):
        self.t = t
        self.n1 = shape[1]

    def __getitem__(self, key):
        if not isinstance(key, tuple):
            key = (key,)
        key = list(key)
        if len(key) < 2:
            key.append(slice(None))
        k1 = key[1]
        if isinstance(k1, slice):
            start, stop, step = k1.indices(self.n1)
            key[1] = slice(start, stop, step)
        return self.t[tuple(key)]


class _Eng:
    def __init__(self, S, name, obj):
        self.name = name
        self.obj = obj
        self.ctr = _Ctr(S, "s_" + name, 1)
        self.seen = {}
        self.n_issued = 0
        self.last_ins = None
        self.last_has_inc = False
        self.inc_idx = []
        self.inc_ev = []


class LazyEv:
    __slots__ = ("eng", "idx")

    def __init__(self, eng, idx):
        self.eng = eng
        self.idx = idx


class _Res:
    __slots__ = ("w", "r")

    def __init__(self):
        self.w = None
        self.r = {}


class Sched:
    def __init__(self, nc, n_dma_slots=8):
        self.nc = nc
        self.stack = contextlib.ExitStack()
        self.scopes = [self.stack]
        self.res = {}
        self.engs = {
            "pe": _Eng(self, "pe", nc.tensor),
            "act": _Eng(self, "act", nc.scalar),
            "dve": _Eng(self, "dve", nc.vector),
            "pool": _Eng(self, "pool", nc.gpsimd),
            "sp": _Eng(self, "sp", nc.sync),
        }
        self.dma_slots = {}
        for q in ("sp", "pool"):
            self.dma_slots[q] = [_Ctr(self, f"d_{q}{i}", 16) for i in range(n_dma_slots)]
        self.dma_rr = {"sp": 0, "pool": 0}
        self.n_inst = 0
        self.uid = 0
        self.pending = None

    def _newsem(self, name):
        return self.stack.enter_context(self.nc.semaphore(name))

    def sb(self, name, shape, dt):
        self.uid += 1
        return self.scopes[-1].enter_context(self.nc.sbuf_tensor(f"sb{self.uid}_{name}", list(shape), dt))

    def ps(self, name, shape, dt=F32):
        self.uid += 1
        esz = 4 if dt == F32 else 2
        per_part = esz
        for d_ in shape[1:]:
            per_part *= d_
        assert per_part <= 2048, (name, shape)
        shape = list(shape)
        if per_part < 2048:
            rest = per_part // shape[1]
            assert 2048 % rest == 0, (name, shape)
            full = [shape[0], 2048 // rest] + shape[2:]
            t = self.scopes[-1].enter_context(self.nc.psum_tensor(f"ps{self.uid}_{name}", full, dt))
            return _PsView(t, shape)
        return self.scopes[-1].enter_context(self.nc.psum_tensor(f"ps{self.uid}_{name}", shape, dt))

    @contextlib.contextmanager
    def scope(self):
        st = contextlib.ExitStack()
        self.scopes.append(st)
        try:
            yield
        finally:
            self.barrier()
            self.scopes.pop()
            st.close()

    def barrier(self):
        evs = []
        for e in self.engs.values():
            if e.n_issued > 0:
                evs.append(self._resolve(LazyEv(e, e.n_issued - 1)))
        for q in self.dma_slots:
            for ctr in self.dma_slots[q]:
                if ctr.val > 0:
                    evs.append((ctr.sem, ctr.val))
        for e in self.engs.values():
            for ev in evs:
                self._wait(e, ev)

    def _r(self, key):
        r = self.res.get(key)
        if r is None:
            r = self.res[key] = _Res()
        return r

    def _resolve(self, ev):
        if not isinstance(ev, LazyEv):
            return ev
        import bisect
        e = ev.eng
        k = bisect.bisect_left(e.inc_idx, ev.idx)
        if k < len(e.inc_idx):
            return e.inc_ev[k]
        assert e.last_ins is not None and not e.last_has_inc and e.n_issued - 1 >= ev.idx
        sv = e.ctr.next_event()
        e.last_ins.then_inc(sv[0], 1)
        e.last_has_inc = True
        e.inc_idx.append(e.n_issued - 1)
        e.inc_ev.append(sv)
        return sv

    def _wait(self, eng, ev):
        if ev is None:
            return
        sem, val = self._resolve(ev)
        k = id(sem)
        if eng.seen.get(k, 0) >= val:
            return
        if self.pending is not None:
            cur = self.pending.get(k)
            if cur is None or cur[1] < val:
                self.pending[k] = (sem, val)
            return
        eng.obj.wait_ge(sem, val)
        eng.seen[k] = val

    def _flush(self, eng):
        pend = list(self.pending.values())
        self.pending = None
        for (sem, val) in pend[:-1]:
            eng.obj.wait_ge(sem, val)
            eng.seen[id(sem)] = val
        if pend:
            sem, val = pend[-1]
            eng.seen[id(sem)] = val
            return (sem, val)
        return None

    def _deps(self, eng, reads, writes, skip_same_eng_write=False):
        for key in reads:
            r = self._r(key)
            self._wait(eng, r.w)
        for key in writes:
            r = self._r(key)
            if not (skip_same_eng_write and isinstance(r.w, LazyEv) and r.w.eng is eng):
                self._wait(eng, r.w)
            for ev in r.r.values():
                self._wait(eng, ev)

    def _commit(self, ev, reads, writes):
        rk = ev.eng.name if isinstance(ev, LazyEv) else id(ev[0])
        for key in reads:
            self._r(key).r[rk] = ev
        for key in writes:
            r = self._r(key)
            r.w = ev
            r.r = {}

    def op(self, engname, fn, reads=(), writes=(), accum=False):
        eng = self.engs[engname]
        self.pending = {}
        self._deps(eng, reads, writes, skip_same_eng_write=accum)
        last = self._flush(eng)
        ins = fn(eng.obj)
        if last is not None:
            ins._wait_ge(last[0], last[1])
        eng.last_ins = ins
        eng.last_has_inc = False
        ev = LazyEv(eng, eng.n_issued)
        eng.n_issued += 1
        self._commit(ev, reads, writes)
        self.n_inst += 1
        return ev

    def dma(self, q, out, in_, reads=(), writes=(), **kw):
        eng = self.engs[q]
        slots = self.dma_slots[q]
        i = self.dma_rr[q]
        self.dma_rr[q] = (i + 1) % len(slots)
        ctr = slots[i]
        self.pending = {}
        if ctr.val > 0:
            self._wait(eng, (ctr.sem, ctr.val))
        self._deps(eng, reads, writes)
        last = self._flush(eng)
        ev = ctr.next_event()
        ins = eng.obj.dma_start(out=out, in_=in_, **kw)
        if last is not None:
            ins._wait_ge(last[0], last[1])
        ins.then_inc(ev[0], 16)
        self._commit(ev, reads, writes)
        self.n_inst += 1
        return ev

    def finish(self, final_keys):
        eng = self.engs["sp"]
        for key in final_keys:
            r = self._r(key)
            self._wait(eng, r.w)
        for q in self.dma_slots:
            for ctr in self.dma_slots[q]:
                if ctr.val > 0:
                    self._wait(eng, (ctr.sem, ctr.val))

    def close(self):
        self.stack.close()

from concourse.bass_utils import run_bass_kernel_spmd

D = 1024
NCTX = 256
NLAT = 4096
NTOK = NCTX + NLAT
NT = NTOK // 128
EPS = 1e-6
P = 128


def _pool_bands():
    L = 1024
    out = np.zeros((4, 5, 128, 128), np.float32)
    for g, w in enumerate((2, 4, 8, 16)):
        def full(L):
            t = np.arange(L)
            lo = np.clip(t - w // 2, 0, L)
            hi = np.clip(t + w // 2, 0, L)
            s = np.arange(L)[:, None]
            m = ((s >= lo[None, :]) & (s < hi[None, :])).astype(np.float64) / (hi - lo)[None, :]
            m -= np.eye(L)
            return m
        m = full(L)
        out[g, 0] = m[3 * 128:4 * 128, 4 * 128:5 * 128]
        out[g, 1] = m[5 * 128:6 * 128, 4 * 128:5 * 128]
        out[g, 2] = m[4 * 128:5 * 128, 4 * 128:5 * 128]
        out[g, 3] = m[0:128, 0:128]
        out[g, 4] = m[L - 128:, L - 128:]
    return out


_VARS = [(-2, "pm"), (-1, "f"), (0, "f"), (1, "f"), (2, "pp")] + [(d, "f") for d in range(-3, 4)]


def _attn_tables():
    kc = np.arange(64)
    qc = np.arange(64)
    c_start = np.clip(qc - 8, 0, 48)
    col_ok = (kc[:, None] >= c_start[None, :]) & (kc[:, None] < c_start[None, :] + 16)
    dc_idx = np.clip(kc[:, None] - qc[None, :], -15, 15) + 15
    dr_idx = np.zeros((12, 128, 128), np.int64)
    dc_full = np.zeros((128, 128), np.int64)
    mask = np.zeros((12, 128, 128), np.float32)
    for a in range(2):
        for b in range(2):
            dc_full[a * 64:(a + 1) * 64, b * 64:(b + 1) * 64] = dc_idx
    for v, (dl, kind) in enumerate(_VARS):
        for a in range(2):
            for b in range(2):
                dr = 2 * dl + a - b + 7
                vis = True
                if kind == "pm":
                    vis = not (a == 0 and b == 1)
                elif kind == "pp":
                    vis = (a == 0 and b == 1)
                dr_idx[v, a * 64:(a + 1) * 64, b * 64:(b + 1) * 64] = min(max(dr, 0), 14)
                if vis and 0 <= dr <= 14:
                    mask[v, a * 64:(a + 1) * 64, b * 64:(b + 1) * 64] = col_ok
    return dr_idx, dc_full, mask


def mm(S, out, lhsT, rhs, start, stop, reads, writes):
    return S.op("pe", lambda e: e.matmul(out, lhsT=lhsT, rhs=rhs, start=start, stop=stop),
                reads=reads, writes=writes, accum=not start)


class Prog:
    def __init__(self, nc, debug=()):
        self.nc = nc
        self.S = Sched(nc)
        self.debug = debug
        self.dbg_out = {}
        dt = nc.dram_tensor
        I = lambda name, shape: dt(name, list(shape), F32, kind="ExternalInput").ap()
        self.x_in = I("x", [NLAT, D])
        self.ctx_in = I("ctx", [NCTX, D])
        self.cvec = I("cvec", [P, 8, 2])
        self.ada_w = I("ada_w", [2, D, 6 * D])
        self.ada_b = I("ada_b", [2, 6 * D])
        self.norm_g = I("norm_g", [2, 4, D])
        self.mlp_w1 = I("mlp_w1", [2, D, 4 * D])
        self.mlp_w2 = I("mlp_w2", [2, 4 * D, D])
        self.ev_w_in = I("ev_w_in", [D, 2 * D])
        self.ev_w_out = I("ev_w_out", [D, D])
        self.ev_pool_w = I("ev_pool_w", [4, P, P])
        self.ev_pool_scale = I("ev_pool_scale", [512])
        self.rpb_tab = I("rpb_tab", [P, 8 * 12, P])
        self.msk_tab = I("msk_tab", [P, 12, P])
        self.bands = I("bands", [P, 20, P])
        self.ident = I("ident", [P, P])
        self.out = dt("out", [NLAT, D], F32, kind="ExternalOutput").ap()
        X = lambda name, shape, d=F32: (dt(name, list(shape), d, kind="ExternalOutput").ap() if name in debug
                                        else dt(name, list(shape), d).ap())
        self.modd = X("modd", [2, 2, 6 * D])
        self.x_d = X("x_d", [NTOK, D])
        self.upool_d = X("upool_d", [NTOK, 512], BF16)
        self.v_d = X("v_d", [NTOK, 512], BF16)
        self.qT_d = X("qT_d", [4, P, NTOK], BF16)
        self.kT_d = X("kT_d", [4, P, NTOK], BF16)
        self.zT_d = X("zT_d", [8, P, NTOK], BF16)

    def dbg(self, name, shape, dtp=F32):
        t = self.nc.dram_tensor("dbg_" + name, list(shape), dtp, kind="ExternalOutput").ap()
        self.dbg_out[name] = t
        return t

    def consts(self):
        S = self.S
        self.idb = S.sb("idb", [P, P], BF16)
        S.dma("pool", self.idb[:], self.ident, writes=["idb"])
        self.ones_bf = S.sb("ones_bf", [P, P], BF16)
        S.op("dve", lambda e: e.memset(self.ones_bf[:], 1.0), writes=["ones_bf"])
        self.eps_t = S.sb("eps_t", [P, 1], F32)
        S.op("dve", lambda e: e.memset(self.eps_t[:], EPS), writes=["eps_t"])

    def phase_mod(self):
        S = self.S
        with S.scope():
            cv = S.sb("cv", [P, 8, 2], F32)
            cvb = S.sb("cvb", [P, 8, 2], BF16)
            S.dma("sp", cv[:], self.cvec, writes=["cv"])
            S.op("act", lambda e: e.activation(out=cvb[:], in_=cv[:], func=AF.Silu), reads=["cv"], writes=["cvb"])
            aw = S.sb("aw", [P, 8, 6 * D], BF16)
            ab = S.sb("ab", [2, 6 * D], F32)
            mrow = S.sb("mrow", [2, 6 * D], F32)
            pss = [S.ps(f"pm{i}", [2, 512], F32) for i in range(4)]
            for l in range(2):
                for c in range(8):
                    S.dma("pool", aw[:, c, :], self.ada_w[l, c * P:(c + 1) * P, :], writes=[("aw", c)])
                S.dma("sp", ab[:], self.ada_b[l:l + 1, :].broadcast_to([2, 6 * D]), writes=["ab"])
                for n in range(12):
                    ps = pss[n % 4]
                    k = ("pm", n % 4)
                    for c in range(8):
                        mm(S, ps[:], cvb[:, c, :], aw[:, c, n * 512:(n + 1) * 512], c == 0, c == 7,
                           reads=["cvb", ("aw", c)], writes=[k])
                    S.op("dve", lambda e: e.tensor_tensor(out=mrow[:, n * 512:(n + 1) * 512], in0=ps[:],
                                                          in1=ab[:, n * 512:(n + 1) * 512], op=ALU.add),
                         reads=[k, "ab"], writes=["mrow"])
                S.dma("sp", self.modd[l], mrow[:], reads=["mrow"], writes=[("modd", l)])

    def load_layer_vecs(self, l):
        S = self.S
        V = {}
        for s in range(2):
            for which in range(2):
                V[("A", which, s)] = (S.sb(f"A{which}_{s}", [P, 8], F32), f"A{which}_{s}")
                V[("B", which, s)] = (S.sb(f"B{which}_{s}", [P, 8], F32), f"B{which}_{s}")
                if not (l == 1 and s == 1):
                    V[("G", which, s)] = (S.sb(f"GG{which}_{s}", [P, D], F32), f"GG{which}_{s}")
        with S.scope(), self.nc.allow_non_contiguous_dma(reason="tiny per-feature vectors"):
            tmp = S.sb("lv_tmp", [P, 8], F32)
            rowt = S.sb("lv_row", [P, D], F32)
            for s in range(2):
                for which, (ish, isc, ig) in enumerate(((0, 1, 0), (3, 4, 2))):
                    A, ka = V[("A", which, s)]
                    B, kb = V[("B", which, s)]
                    S.dma("sp", B[:], self.modd[l, s, ish * D:(ish + 1) * D].rearrange("(c p) -> p c", p=P),
                          reads=[("modd", l)], writes=[kb])
                    S.dma("sp", A[:], self.modd[l, s, isc * D:(isc + 1) * D].rearrange("(c p) -> p c", p=P),
                          reads=[("modd", l)], writes=[ka])
                    S.dma("sp", tmp[:], self.norm_g[l, ig, :].rearrange("(c p) -> p c", p=P), writes=["lv_tmp"])
                    S.op("dve", lambda e: e.scalar_tensor_tensor(out=A[:], in0=A[:], scalar=1.0, in1=tmp[:],
                                                                 op0=ALU.add, op1=ALU.mult),
                         reads=[ka, "lv_tmp"], writes=[ka])
                for which, (igt, ig) in enumerate(((2, 1), (5, 3))):
                    if l == 1 and s == 1:
                        continue
                    G, kg = V[("G", which, s)]
                    S.dma("sp", G[:], self.modd[l, s:s + 1, igt * D:(igt + 1) * D].broadcast_to([P, D]),
                          reads=[("modd", l)], writes=[kg])
                    S.dma("sp", rowt[:], self.norm_g[l, ig:ig + 1, :].broadcast_to([P, D]), writes=["lv_row"])
                    S.op("dve", lambda e: e.tensor_tensor(out=G[:], in0=G[:], in1=rowt[:], op=ALU.mult),
                         reads=[kg, "lv_row"], writes=[kg])
        return V

    def make_norm_bufs(self, tag, nb=2):
        S = self.S
        B = {"i": 0, "nb": nb, "tag": tag}
        B["sq"] = [S.sb(f"{tag}_sq{i}", [P, D], BF16) for i in range(1)] * nb
        B["st"] = [S.sb(f"{tag}_st{i}", [P, 4], F32) for i in range(nb)]
        B["xn"] = [S.sb(f"{tag}_xn{i}", [P, D], BF16) for i in range(nb)]
        B["tp"] = [S.ps(f"{tag}_tp{i}", [P, 8, P], BF16) for i in range(nb)]
        B["tm"] = [S.sb(f"{tag}_tm{i}", [P, 8, P], F32) for i in range(nb)]
        return B

    def norm_to_hT(self, B, x_sb, xkey, A, B_, out_ap, out_key, out2_ap=None, out2_key=None):
        S = self.S
        i = B["i"] % B["nb"]
        B["i"] += 1
        tag = B["tag"]
        sq, st, xn, tp, tm = B["sq"][i], B["st"][i], B["xn"][i], B["tp"][i], B["tm"][i]
        ksq, kst, kxn, ktp, ktm = [(tag, n, i) for n in ("sq", "st", "xn", "tp", "tm")]
        ksq = (tag, "sq", 0)
        S.op("pool", lambda e: e.memset(st[:], 0.0), writes=[kst])
        S.op("act", lambda e: e.activation(out=sq[:], in_=x_sb, func=AF.Square, accum_out=st[:, 0:1]),
             reads=[xkey], writes=[ksq, kst])
        S.op("act", lambda e: e.activation(out=st[:, 1:2], in_=st[:, 0:1], func=AF.Sqrt, scale=1.0 / D,
                                           bias=self.eps_t[:, 0:1]), reads=[kst, "eps_t"], writes=[kst])
        S.op("dve", lambda e: e.reciprocal(out=st[:, 2:3], in_=st[:, 1:2]), reads=[kst], writes=[kst])
        S.op("dve", lambda e: e.tensor_scalar(out=xn[:], in0=x_sb, scalar1=st[:, 2:3], scalar2=None, op0=ALU.mult),
             reads=[xkey, kst], writes=[kxn])
        for c in range(8):
            S.op("pe", lambda e: e.transpose(out=tp[:, c, :], in_=xn[:, c * P:(c + 1) * P], identity=self.idb[:]),
                 reads=[kxn, "idb"], writes=[ktp], accum=(c > 0))
        Aap = A[0][:, :, None].broadcast_to([P, 8, P])
        Bap = B_[0][:, :, None].broadcast_to([P, 8, P])
        S.op("dve", lambda e: e.tensor_tensor(out=tm[:], in0=tp[:], in1=Aap, op=ALU.mult),
             reads=[ktp, A[1]], writes=[ktm])
        S.op("pool", lambda e: e.tensor_tensor(out=out_ap, in0=tm[:], in1=Bap, op=ALU.add),
             reads=[ktm, B_[1]], writes=[out_key])
        if out2_ap is not None:
            S.op("pool", lambda e: e.tensor_tensor(out=out2_ap, in0=tm[:], in1=Bap, op=ALU.add),
                 reads=[ktm, B_[1]], writes=[out2_key])

    def x_src(self, layer, T):
        if layer == 0:
            if T < 2:
                return self.ctx_in[T * P:(T + 1) * P, :], None
            return self.x_in[(T - 2) * P:(T - 1) * P, :], None
        return self.x_d[T * P:(T + 1) * P, :], ("x_d", T)

    def phase_L0_proj(self, V):
        S = self.S
        with S.scope():
            w = S.sb("w_in", [P, 8, 2 * D], BF16)
            for c in range(8):
                S.dma("pool", w[:, c, :], self.ev_w_in[c * P:(c + 1) * P, :], writes=[("w_in", c)])
            wk = [("w_in", c) for c in range(8)]
            NB = self.make_norm_bufs("n0")
            xt = [S.sb(f"xt{i}", [P, D], F32) for i in range(2)]
            hT = [S.sb(f"hT{i}", [P, 8, 512], BF16) for i in range(2)]
            ptok = [S.ps(f"ptok{i}", [P, 512], F32) for i in range(2)]
            pft = [S.ps(f"pft{i}", [P, 512], F32) for i in range(2)]
            otok = [S.sb(f"otok{i}", [P, 512], BF16) for i in range(2)]
            oft = [S.sb(f"oft{i}", [P, 512], BF16) for i in range(2)]
            supers = [(0, 2)] + [(2 + 4 * i, 4) for i in range(8)]
            cnt = 0
            ctok = 0
            cft = 0
            for si, (T0, nt) in enumerate(supers):
                hb = hT[si % 2]
                hk = ("hT", si % 2)
                s = 1 if T0 < 2 else 0
                for t in range(nt):
                    T = T0 + t
                    xb = xt[cnt % 2]
                    xk = ("xt", cnt % 2)
                    cnt += 1
                    src, sk = self.x_src(0, T)
                    S.dma("sp", xb[:], src, reads=[sk] if sk else [], writes=[xk])
                    self.norm_to_hT(NB, xb[:], xk, V[("A", 0, s)], V[("B", 0, s)],
                                    hb[:, :, t * P:(t + 1) * P], (hk, t))
                n = nt * P
                hks = [(hk, t) for t in range(nt)]
                for t in range(nt):
                    T = T0 + t
                    for (c0, dst, dk) in ((0, self.upool_d, "upool"), (1536, self.v_d, "v")):
                        ps = ptok[ctok % 2]; pk = ("ptok", ctok % 2)
                        ob = otok[ctok % 2]; ok = ("otok", ctok % 2)
                        ctok += 1
                        for c in range(8):
                            mm(S, ps[:], hb[:, c, t * P:(t + 1) * P], w[:, c, c0:c0 + 512], c == 0, c == 7,
                               reads=[(hk, t), wk[c]], writes=[pk])
                        S.op("act", lambda e: e.activation(out=ob[:], in_=ps[:], func=AF.Copy), reads=[pk], writes=[ok])
                        S.dma("sp", dst[T * P:(T + 1) * P, :], ob[:], reads=[ok], writes=[(dk, T)])
                for jb in range(8):
                    c0 = 512 + jb * P
                    ps = pft[cft % 2]; pk = ("pft", cft % 2)
                    ob = oft[cft % 2]; ok = ("oft", cft % 2)
                    cft += 1
                    for c in range(8):
                        mm(S, ps[:, :n], w[:, c, c0:c0 + P], hb[:, c, :n], c == 0, c == 7,
                           reads=hks + [wk[c]], writes=[pk])
                    sc = 0.125 if jb < 4 else 1.0
                    S.op("act", lambda e: e.activation(out=ob[:, :n], in_=ps[:, :n], func=AF.Copy, scale=sc),
                         reads=[pk], writes=[ok])
                    dst = self.qT_d if jb < 4 else self.kT_d
                    dk = "qT" if jb < 4 else "kT"
                    S.dma("sp", dst[jb % 4, :, T0 * P:T0 * P + n], ob[:, :n], reads=[ok],
                          writes=[(dk, jb % 4, T0 + t) for t in range(nt)])

    def phase_L0_pool(self):
        S = self.S
        with S.scope():
            up = S.sb("up_all", [P, NT, 512], BF16)
            for q in range(0, NT, 2):
                S.dma("sp", up[:, q:q + 2, :], self.upool_d[q * P:(q + 2) * P, :].rearrange("(n p) f -> p n f", p=P),
                      reads=[("upool", q), ("upool", q + 1)], writes=[("up", q), ("up", q + 1)])
            bd = S.sb("bands", [P, 20, P], BF16)
            S.dma("pool", bd[:], self.bands, writes=["bands"])
            pw = S.sb("pool_w", [P, 4, P], BF16)
            S.dma("pool", pw[:], self.ev_pool_w.rearrange("g c o -> c g o"), writes=["pool_w"])
            psc = S.sb("pool_sc", [P, 4], F32)
            with self.nc.allow_non_contiguous_dma(reason="tiny"):
                S.dma("sp", psc[:], self.ev_pool_scale.rearrange("(g p) -> p g", p=P), writes=["pool_sc"])
            pb = [S.ps(f"pb{i}", [P, 4, P], F32) for i in range(2)]
            pc = [S.ps(f"pc{i}", [P, 4, P], F32) for i in range(2)]
            pm = [S.sb(f"pmx{i}", [P, 4, P], BF16) for i in range(2)]
            zp = [S.sb(f"zp{i}", [P, 4, P], BF16) for i in range(2)]
            it = 0
            for (T0, n) in ((0, 2), (2, 32)):
                for i in range(n):
                    T = T0 + i
                    b = it % 2
                    it += 1
                    for g in range(4):
                        srcs = []
                        if i > 0:
                            srcs.append((T - 1, 0))
                        cv = 3 if i == 0 else (4 if i == n - 1 else 2)
                        srcs.append((T, cv))
                        if i < n - 1:
                            srcs.append((T + 1, 1))
                        for si, (Ts, v) in enumerate(srcs):
                            mm(S, pb[b][:, g, :], up[:, Ts, g * P:(g + 1) * P], bd[:, g * 5 + v, :],
                               si == 0, si == len(srcs) - 1, reads=[("up", Ts), "bands"], writes=[("pb", b)])
                    S.op("dve", lambda e: e.tensor_copy(out=pm[b][:], in_=pb[b][:]), reads=[("pb", b)], writes=[("pmx", b)])
                    for g in range(4):
                        mm(S, pc[b][:, g, :], pw[:, g, :], pm[b][:, g, :], True, True,
                           reads=["pool_w", ("pmx", b)], writes=[("pc", b)])
                    S.op("dve", lambda e: e.tensor_tensor(out=zp[b][:], in0=pc[b][:],
                                                          in1=psc[:, :, None].broadcast_to([P, 4, P]), op=ALU.mult),
                         reads=[("pc", b), "pool_sc"], writes=[("zp", b)])
                    S.dma("sp", self.zT_d[0:4, :, T * P:(T + 1) * P].rearrange("c p t -> p c t"), zp[b][:],
                          reads=[("zp", b)], writes=[("zT", c, T) for c in range(4)])

    def phase_L0_attn(self):
        S = self.S
        with S.scope():
            kT = S.sb("kT_all", [P, 4, NTOK], BF16)
            qT = S.sb("qT_all", [P, 4, NTOK], BF16)
            va = S.sb("v_all", [P, NT, 512], BF16)
            for j in range(4):
                S.dma("sp", kT[:, j, :], self.kT_d[j], reads=[("kT", j, T) for T in range(NT)], writes=[("kTa", j)])
                S.dma("sp", qT[:, j, :], self.qT_d[j], reads=[("qT", j, T) for T in range(NT)], writes=[("qTa", j)])
            for q in range(0, NT, 2):
                S.dma("sp", va[:, q:q + 2, :], self.v_d[q * P:(q + 2) * P, :].rearrange("(n p) f -> p n f", p=P),
                      reads=[("v", q), ("v", q + 1)], writes=[("va", q), ("va", q + 1)])
            E = S.sb("Etab", [P, 96, P], BF16)
            with S.scope():
                rt = S.sb("rt", [P, 96, P], F32)
                mk = S.sb("mk", [P, 12, P], F32)
                S.dma("sp", rt[:], self.rpb_tab, writes=["rt"])
                S.dma("sp", mk[:], self.msk_tab, writes=["mk"])
                S.op("act", lambda e: e.activation(out=rt[:], in_=rt[:], func=AF.Exp), reads=["rt"], writes=["rt"])
                for h in range(8):
                    S.op("dve", lambda e: e.tensor_tensor(out=E[:, h * 12:(h + 1) * 12, :], in0=rt[:, h * 12:(h + 1) * 12, :],
                                                          in1=mk[:], op=ALU.mult), reads=["rt", "mk"], writes=["Etab"])
            pss = [[S.ps(f"pss{i}_{k}", [P, 512], F32) for k in range(2)] for i in range(2)]
            pso = [S.ps(f"pso{i}", [P, 2, P], F32) for i in range(2)]
            pex = [S.sb(f"pex{i}", [P, 7, P], BF16) for i in range(2)]
            pT = [S.sb(f"pT{i}", [P, 5, P], BF16) for i in range(2)]
            rc = [S.sb(f"rc{i}", [P, P], F32) for i in range(2)]
            zo = [S.sb(f"zo{i}", [P, P], BF16) for i in range(2)]
            it = 0
            izo = 0
            for T in range(NT):
                if T < 2:
                    chunks = [(0, None), (1, None)]
                else:
                    i = T - 2
                    if 2 <= i <= 29:
                        lat = [(T + d, v) for v, d in enumerate((-2, -1, 0, 1, 2))]
                    elif i == 0:
                        lat = [(T + d, 8 + d) for d in (0, 1, 2, 3)]
                    elif i == 1:
                        lat = [(T + d, 8 + d) for d in (-1, 0, 1, 2)]
                    elif i == 30:
                        lat = [(T + d, 8 + d) for d in (-2, -1, 0, 1)]
                    else:
                        lat = [(T + d, 8 + d) for d in (-3, -2, -1, 0)]
                    chunks = [(0, None), (1, None)] + lat
                nk = len(chunks)
                nlat = nk - 2
                for j in range(4):
                    zb = zo[izo % 2]; zk = ("zo", izo % 2)
                    izo += 1
                    for hh in range(2):
                        h = 2 * j + hh
                        pb_ = hh * 64
                        b = it % 2
                        it += 1
                        for ci, (Tk, v) in enumerate(chunks):
                            bank = pss[b][ci // 4]
                            mm(S, bank[:, (ci % 4) * P:(ci % 4 + 1) * P],
                               kT[pb_:pb_ + 64, j, Tk * P:(Tk + 1) * P], qT[pb_:pb_ + 64, j, T * P:(T + 1) * P],
                               True, True, reads=[("kTa", j), ("qTa", j)], writes=[("pss", b, ci // 4)])
                        n0 = min(nk, 4)
                        S.op("act", lambda e: e.activation(out=pex[b][:, 0:n0, :], in_=pss[b][0][:, 0:n0 * P].rearrange("p (c q) -> p c q", q=P), func=AF.Exp),
                             reads=[("pss", b, 0)], writes=[("pex", b)])
                        if nk > 4:
                            S.op("act", lambda e: e.activation(out=pex[b][:, 4:nk, :], in_=pss[b][1][:, 0:(nk - 4) * P].rearrange("p (c q) -> p c q", q=P), func=AF.Exp),
                                 reads=[("pss", b, 1)], writes=[("pex", b)])
                        if nlat > 0:
                            v0 = chunks[2][1]
                            S.op("dve", lambda e: e.tensor_tensor(out=pT[b][:, 0:nlat, :], in0=pex[b][:, 2:nk, :],
                                                                  in1=E[:, h * 12 + v0:h * 12 + v0 + nlat, :], op=ALU.mult),
                                 reads=[("pex", b), "Etab"], writes=[("pT", b)])
                        for ci, (Tk, v) in enumerate(chunks):
                            rhs = pex[b][:, ci, :] if v is None else pT[b][:, ci - 2, :]
                            rk = [("pex", b)] if v is None else [("pT", b)]
                            mm(S, pso[b][:, 0, :], va[:, Tk, j * P:(j + 1) * P], rhs, ci == 0, ci == nk - 1,
                               reads=[("va", Tk)] + rk, writes=[("pso", b)])
                        for ci, (Tk, v) in enumerate(chunks):
                            rhs = pex[b][:, ci, :] if v is None else pT[b][:, ci - 2, :]
                            rk = [("pex", b)] if v is None else [("pT", b)]
                            mm(S, pso[b][:, 1, :], self.ones_bf[:], rhs, ci == 0, ci == nk - 1,
                               reads=["ones_bf"] + rk, writes=[("pso", b)])
                        S.op("dve", lambda e: e.reciprocal(out=rc[b][pb_:pb_ + 64, :], in_=pso[b][pb_:pb_ + 64, 1, :]),
                             reads=[("pso", b)], writes=[("rc", b)])
                        S.op("dve", lambda e: e.tensor_tensor(out=zb[pb_:pb_ + 64, :], in0=pso[b][pb_:pb_ + 64, 0, :],
                                                              in1=rc[b][pb_:pb_ + 64, :], op=ALU.mult),
                             reads=[("pso", b), ("rc", b)], writes=[zk])
                    S.dma("sp", self.zT_d[4 + j, :, T * P:(T + 1) * P], zb[:], reads=[zk], writes=[("zT", 4 + j, T)])

    def phase_out_mlp(self, layer, V, w_out_ap, y_tile_fn, dst_fn):
        S = self.S
        with S.scope():
            wo = S.sb("wo", [P, 8, D], BF16)
            for c in range(8):
                S.dma("pool", wo[:, c, :], w_out_ap[c * P:(c + 1) * P, :], writes=[("wo", c)])
            w1 = S.sb("w1", [P, 8, 4 * D], BF16)
            w2 = S.sb("w2", [P, 32, D], BF16)
            for c in range(8):
                S.dma("pool", w1[:, c, :], self.mlp_w1[layer, c * P:(c + 1) * P, :], writes=[("w1", c)])
            for f in range(0, 32, 4):
                S.dma("pool", w2[:, f:f + 4, :], self.mlp_w2[layer, f * P:(f + 4) * P, :].rearrange("(n p) d -> p n d", p=P),
                      writes=[("w2", f + q) for q in range(4)])
            NB = self.make_norm_bufs("nm", nb=1)
            zt = [S.sb(f"zt{i}", [P, 8, P], BF16) for i in range(2)]
            xt = [S.sb(f"xo{i}", [P, D], F32) for i in range(2)]
            x1 = [S.sb(f"x1_{i}", [P, D], F32) for i in range(2)]
            tmp = [S.sb(f"tg{i}", [P, D], F32) for i in range(2)]
            sq = NB["sq"][0]
            stt = [S.sb(f"ost{i}", [P, 4], F32) for i in range(4)]
            hT = [S.sb(f"hm{i}", [P, 8, 256], BF16) for i in range(1)] * 2
            py = [[S.ps(f"py{t}_{hf}", [P, 512], F32) for hf in range(2)] for t in range(2)]
            pa = [S.ps(f"pa{i}", [P, 256], F32) for i in range(2)]
            r32 = [S.sb(f"r32_{i}", [P, 256], F32) for i in range(2)]
            aT = [S.sb(f"aT{i}", [P, 256], BF16) for i in range(2)]
            ist = 0

            def norm_gate_res(t, G, xin, xin_key, xout, xout_key):
                nonlocal ist
                st = stt[ist % 4]; sk = ("ost", ist % 4)
                ist += 1
                S.op("pool", lambda e: e.memset(st[:], 0.0), writes=[sk])
                for hf in range(2):
                    S.op("act", lambda e: e.activation(out=sq[:, hf * 512:(hf + 1) * 512], in_=py[t][hf][:], func=AF.Square,
                                                       accum_out=st[:, hf:hf + 1]), reads=[("py", t, hf)], writes=[("nm", "sq", 0), sk])
                S.op("dve", lambda e: e.tensor_tensor(out=st[:, 2:3], in0=st[:, 0:1], in1=st[:, 1:2], op=ALU.add),
                     reads=[sk], writes=[sk])
                S.op("act", lambda e: e.activation(out=st[:, 2:3], in_=st[:, 2:3], func=AF.Sqrt, scale=1.0 / D,
                                                   bias=self.eps_t[:, 0:1]), reads=[sk, "eps_t"], writes=[sk])
                S.op("dve", lambda e: e.reciprocal(out=st[:, 3:4], in_=st[:, 2:3]), reads=[sk], writes=[sk])
                tb = tmp[t]; tk = ("tg", t)
                for hf in range(2):
                    S.op("dve", lambda e: e.scalar_tensor_tensor(out=tb[:, hf * 512:(hf + 1) * 512], in0=py[t][hf][:],
                                                                 scalar=st[:, 3:4], in1=G[0][:, hf * 512:(hf + 1) * 512],
                                                                 op0=ALU.mult, op1=ALU.mult),
                         reads=[("py", t, hf), sk, G[1]], writes=[tk])
                S.op("pool", lambda e: e.tensor_tensor(out=xout, in0=tb[:], in1=xin, op=ALU.add),
                     reads=[tk, xin_key], writes=[xout_key])

            ia = 0
            for sidx in range(NT // 2):
                T0 = 2 * sidx
                s = 1 if T0 < 2 else 0
                if layer == 1 and s == 1:
                    continue
                hb = hT[0]; hk = ("hm", 0)
                for t in range(2):
                    T = T0 + t
                    src, skey = self.x_src(layer, T)
                    S.dma("sp", xt[t][:], src, reads=[skey] if skey else [], writes=[("xo", t)])
                    y_tile_fn(T, t, zt[t], ("zt", t), wo, py[t])
                    norm_gate_res(t, V[("G", 0, s)], xt[t][:], ("xo", t), x1[t][:], ("x1", t))
                    self.norm_to_hT(NB, x1[t][:], ("x1", t), V[("A", 1, s)], V[("B", 1, s)],
                                    hb[:, :, t * P:(t + 1) * P], (hk, t))
                def mm1(f):
                    a = (ia + f) % 2
                    for c in range(8):
                        mm(S, pa[a][:], w1[:, c, f * P:(f + 1) * P], hb[:, c, :], c == 0, c == 7,
                           reads=[("w1", c), (hk, 0), (hk, 1)], writes=[("pa", a)])
                    S.op("act", lambda e: e.activation(out=r32[a][:], in_=pa[a][:], func=AF.Relu),
                         reads=[("pa", a)], writes=[("r32", a)])
                    S.op("dve", lambda e: e.tensor_tensor(out=aT[a][:], in0=r32[a][:], in1=r32[a][:], op=ALU.mult),
                         reads=[("r32", a)], writes=[("aT", a)])

                def mm2(f):
                    a = (ia + f) % 2
                    for t in range(2):
                        for hf in range(2):
                            mm(S, py[t][hf][:], aT[a][:, t * P:(t + 1) * P], w2[:, f, hf * 512:(hf + 1) * 512],
                               f == 0, f == 31, reads=[("aT", a), ("w2", f)], writes=[("py", t, hf)])
                mm1(0)
                for f in range(32):
                    if f + 1 < 32:
                        mm1(f + 1)
                    mm2(f)
                for t in range(2):
                    T = T0 + t
                    norm_gate_res(t, V[("G", 1, s)], x1[t][:], ("x1", t), tmp[t][:], ("tg", t))
                    dst, dkey = dst_fn(T)
                    S.dma("sp", dst, tmp[t][:], reads=[("tg", t)], writes=[dkey])

    def y_tile_L0(self, T, t, zt, zk, wo, py):
        S = self.S
        S.dma("sp", zt[:], self.zT_d[:, :, T * P:(T + 1) * P].rearrange("c p t -> p c t"),
              reads=[("zT", c, T) for c in range(8)], writes=[zk])
        for hf in range(2):
            for c in range(8):
                mm(S, py[hf][:], zt[:, c, :], wo[:, c, hf * 512:(hf + 1) * 512], c == 0, c == 7,
                   reads=[zk, ("wo", c)], writes=[("py", t, hf)])


def build_program(stop_after=None, debug=()):
    nc = bass.Bass("TRN2", target_bir_lowering=False)
    Pg = Prog(nc, debug)
    S = Pg.S
    Pg.consts()
    Pg.phase_mod()
    final_keys = []
    with S.scope():
        V0 = Pg.load_layer_vecs(0)
        Pg.phase_L0_proj(V0)
        Pg.phase_L0_pool()
        Pg.phase_L0_attn()

        def dst0(T):
            if stop_after == "L0":
                if T < 2:
                    return Pg.x_d[T * P:(T + 1) * P, :], ("x_d", T)
                return Pg.out[(T - 2) * P:(T - 1) * P, :], ("out", T)
            return Pg.x_d[T * P:(T + 1) * P, :], ("x_d", T)
        Pg.phase_out_mlp(0, V0, Pg.ev_w_out, Pg.y_tile_L0, dst0)
    S.barrier()
    S.finish([])
    S.close()
    return nc, Pg


def host_inputs(inputs):
    f = lambda a: np.ascontiguousarray(np.asarray(a, dtype=np.float32))
    dr_idx, dc_full, mask = _attn_tables()
    rpb = f(inputs["ev_rpb"])[0]
    tab = rpb[:, dr_idx, dc_full[None, :, :]]
    tab = np.ascontiguousarray(tab.transpose(2, 0, 1, 3).reshape(128, 96, 128))
    msk = np.ascontiguousarray(mask.transpose(1, 0, 2))
    bands = np.ascontiguousarray(_pool_bands().transpose(2, 0, 1, 3).reshape(128, 20, 128))
    shared = {
        "ada_w": f(inputs["ada_w"]), "ada_b": f(inputs["ada_b"]), "norm_g": f(inputs["norm_g"]),
        "mlp_w1": f(inputs["mlp_w1"]), "mlp_w2": f(inputs["mlp_w2"]),
        "ev_w_in": f(inputs["ev_w_in"])[0], "ev_w_out": f(inputs["ev_w_out"])[0],
        "ev_pool_w": f(inputs["ev_pool_w"])[0], "ev_pool_scale": f(inputs["ev_pool_scale"])[0],
        "rpb_tab": tab, "msk_tab": msk, "bands": bands, "ident": np.eye(128, dtype=np.float32),
    }
    x = f(inputs["x"]); c = f(inputs["c"]); ctx = f(inputs["ctx"]); cc = f(inputs["c_ctx"])
    maps = []
    for b in range(x.shape[0]):
        cv = np.stack([c[b].reshape(8, 128).T, cc.reshape(8, 128).T], axis=-1)
        m = dict(shared)
        m.update({"x": x[b], "ctx": ctx[b], "cvec": np.ascontiguousarray(cv)})
        maps.append(m)
    return maps


_CACHE = {}


def kernel(**inputs):
    maps = host_inputs(inputs)
    if "nc" not in _CACHE:
        _CACHE["nc"] = build_program()
    nc, Pg = _CACHE["nc"]
    res = run_bass_kernel_spmd(nc, maps, core_ids=list(range(8)))
    return np.stack([np.asarray(r["out"]) for r in res.results], axis=0)

LWC = -0.6065306597126334
GN_EPS = 64e-5


def _scan_consts():
    s = np.arange(128)[:, None]
    t = np.arange(128)[None, :]
    tri = np.stack([(s <= t), (s >= t)]).astype(np.float32)
    strict = np.stack([(s < t), (s > t)]).astype(np.float32)
    mT = strict.transpose(0, 2, 1)
    m4 = np.concatenate([tri, strict, tri, mT], axis=2)
    lm = []
    for l in range(7):
        b = 1 << l
        lm.append(((s // (2 * b)) == (t // (2 * b))) & (((s // b) % 2) == 0) & (((t // b) % 2) == 1))
    lm = np.stack(lm).astype(np.float32)
    lmT = np.stack([lm.transpose(0, 2, 1), lm])
    return tri, m4, np.ascontiguousarray(lmT)


def _tt(S, eng, out, a, b, op, reads, writes):
    return S.op(eng, lambda e: e.tensor_tensor(out=out, in0=a, in1=b, op=op), reads=reads, writes=writes)


def _stt(S, eng, out, a, sc, b, op0, op1, reads, writes):
    return S.op("dve", lambda e: e.scalar_tensor_tensor(out=out, in0=a, scalar=sc, in1=b, op0=op0, op1=op1),
                reads=reads, writes=writes)


def _act(S, out, in_, func, reads, writes, **kw):
    return S.op("act", lambda e: e.activation(out=out, in_=in_, func=func, **kw), reads=reads, writes=writes)


def _h3(ap):
    return ap.rearrange("p (h k) -> p h k", k=64)


class Prog1(Prog):
    def __init__(self, nc, debug=()):
        super().__init__(nc, debug)
        dt = nc.dram_tensor
        I = lambda name, shape: dt(name, list(shape), F32, kind="ExternalInput").ap()
        self.rw_mu = I("rw_mu", [6, D])
        self.rw_wr = I("rw_wr", [D, D]); self.rw_wk = I("rw_wk", [D, D])
        self.rw_wv = I("rw_wv", [D, D]); self.rw_wo = I("rw_wo", [D, D])
        self.rw_w0 = I("rw_w0", [2, D]); self.rw_a0 = I("rw_a0", [2, D])
        self.w1cat = I("w1cat", [D, P]); self.a1cat = I("a1cat", [D, P]); self.rw_g1 = I("rw_g1", [D, P])
        self.w2cat = I("w2cat", [P, D]); self.a2cat = I("a2cat", [P, D]); self.rw_g2 = I("rw_g2", [P, D])
        self.rw_kk = I("rw_kk", [1, D]); self.rw_ka = I("rw_ka", [1, D]); self.rw_rk = I("rw_rk", [1, D])
        self.rw_lng = I("rw_lng", [1, D]); self.rw_lnb = I("rw_lnb", [1, D])
        self.tri_c = I("tri_c", [2, P, P]); self.m4_c = I("m4_c", [2, P, 512]); self.lmT_c = I("lmT_c", [2, 7, P, P])
        X = lambda name, shape, d=F32: (dt(name, list(shape), d, kind="ExternalOutput").ap() if name in debug
                                        else dt(name, list(shape), d).ap())
        self.hT_d = X("hT_d", [8, P, NTOK])
        self.featT_d = X("featT_d", [2, NT, P, 8 * 4 * P], BF16)
        self.vtok_d = X("vtok_d", [NTOK, D], BF16)
        self.bk_d = X("bk_d", [2, NT, P, 2 * D], BF16)
        self.gC_d = X("gC_d", [2, NT, P, 8])
        self.g_d = X("g_d", [NTOK, D])
        self.bonus_d = X("bonus_d", [NTOK, D])
        self.y_d = X("y_d", [2, NTOK, D])

    def phase_R0(self, V):
        S = self.S
        with S.scope():
            NB = self.make_norm_bufs("r0")
            xt = [S.sb(f"r0x{i}", [P, D], F32) for i in range(2)]
            ho = [S.sb(f"r0h{i}", [P, 8, P], F32) for i in range(2)]
            for T in range(NT):
                s = 1 if T < 2 else 0
                b = T % 2
                src, sk = self.x_src(1, T)
                S.dma("sp", xt[b][:], src, reads=[sk], writes=[("r0x", b)])
                self.norm_to_hT(NB, xt[b][:], ("r0x", b), V[("A", 0, s)], V[("B", 0, s)], ho[b][:], ("r0h", b))
                S.dma("sp", self.hT_d[:, :, T * P:(T + 1) * P].rearrange("c p t -> p c t"), ho[b][:],
                      reads=[("r0h", b)], writes=[("hT_d", T)])

    def phase_R1(self):
        S = self.S
        with S.scope():
            W = {}
            for nm, src in (("wr", self.rw_wr), ("wk", self.rw_wk), ("wv", self.rw_wv)):
                W[nm] = S.sb(nm, [P, 8, D], BF16)
                for c in range(0, 8, 4):
                    S.dma("pool", W[nm][:, c:c + 4, :], src[c * P:(c + 4) * P, :].rearrange("(c p) n -> p c n", p=P), writes=[nm])
            for nm, src in (("w1c", self.w1cat), ("a1c", self.a1cat), ("g1", self.rw_g1)):
                W[nm] = S.sb(nm, [P, 8, P], BF16)
                S.dma("pool", W[nm][:], src.rearrange("(c p) n -> p c n", p=P), writes=[nm])
            for nm, src in (("w2c", self.w2cat), ("a2c", self.a2cat), ("g2", self.rw_g2)):
                W[nm] = S.sb(nm, [P, D], BF16)
                S.dma("pool", W[nm][:], src, writes=[nm])
            R = {}
            for nm, src in (("kk_r", self.rw_kk), ("ka_r", self.rw_ka), ("rk_r", self.rw_rk),
                            ("w0_0", self.rw_w0[0:1, :]), ("w0_1", self.rw_w0[1:2, :]),
                            ("a0_0", self.rw_a0[0:1, :]), ("a0_1", self.rw_a0[1:2, :])):
                R[nm] = S.sb(nm, [P, D], F32)
                S.dma("sp", R[nm][:], src.broadcast_to([P, D]), writes=[nm])
            mu = S.sb("mu", [P, 6, 8], F32)
            with self.nc.allow_non_contiguous_dma(reason="tiny"):
                S.dma("sp", mu[:], self.rw_mu.rearrange("j (c p) -> p j c", p=P), writes=["mu"])
            tri = S.sb("tri", [P, 2, P], F32)
            S.dma("sp", tri[:], self.tri_c.rearrange("d s t -> s d t"), writes=["tri"])
            onef = S.sb("onef", [P, P], F32)
            S.op("dve", lambda e: e.memset(onef[:], 1.0), writes=["onef"])
            hbuf = S.sb("hbuf", [P, 8, P + 2], F32)
            xx = S.sb("xx", [P, 8, P], F32)
            mxt = S.sb("mxt", [P, 8, P], F32)
            mix = S.sb("mix", [P, 6, 8, P], BF16)
            hid = S.sb("hid", [P, 3, P], BF16)
            F = {n: S.sb(n, [P, D], F32) for n in ("r_sb", "k_sb", "v_sb", "kkn", "tA", "tB", "lw", "tC", "tD", "kd0", "kd1", "tE", "tF", "tG", "tH")}
            ob = [S.sb(f"ob{i}", [P, D], BF16) for i in range(4)]
            vb = S.sb("vb", [P, D], BF16)
            ft = S.sb("ft", [P, 8, 4, P], BF16)
            bkt = S.sb("bkt", [P, 2, D], BF16)
            st16 = S.sb("st16", [P, 64], F32)
            gcs = S.sb("gcs", [P, 8], F32)
            pA = [[S.ps(f"pA{i}_{h}", [P, 512], F32) for h in range(2)] for i in range(2)]
            pCl = [S.ps(f"pCl{h}", [P, 512], F32) for h in range(2)]
            pF = S.ps("pF", [P, 512], F32)
            pT = S.ps("pT", [P, 8, P], BF16)
            ipa = 0

            def proj(lhs_fn, rhs, rkey, K0=0, K=P, nchunks=8, lkeys=()):
                nonlocal ipa
                i = ipa % 2
                ipa += 1
                for hf in range(2):
                    for c in range(nchunks):
                        mm(S, pA[i][hf][:], lhs_fn(c), rhs(c, hf), c == 0, c == nchunks - 1,
                           reads=list(lkeys) + [rkey], writes=[("pA", i, hf)])
                return pA[i], [("pA", i, 0), ("pA", i, 1)]

            def evac2(fn_half):
                for hf in range(2):
                    fn_half(hf, slice(hf * 512, (hf + 1) * 512))

            for T in range(NT):
                seq_lo, seq_hi = (0, NCTX) if T < 2 else (NCTX, NTOK)
                t0 = T * P
                lo = max(t0 - 1, seq_lo); hi = min(t0 + P + 1, seq_hi)
                if lo > t0 - 1:
                    S.op("pool", lambda e: e.memset(hbuf[:, :, 0:1], 0.0), writes=["hbuf"])
                if hi < t0 + P + 1:
                    S.op("pool", lambda e: e.memset(hbuf[:, :, P + 1:P + 2], 0.0), writes=["hbuf"])
                S.dma("sp", hbuf[:, :, lo - (t0 - 1):hi - (t0 - 1)], self.hT_d[:, :, lo:hi].rearrange("c p t -> p c t"),
                      reads=[("hT_d", q) for q in range(max(T - 1, 0), min(T + 2, NT))], writes=["hbuf"])
                _tt(S, "dve", xx[:], hbuf[:, :, 0:P], hbuf[:, :, 2:P + 2], ALU.add, ["hbuf"], ["xx"])
                _stt(S, "dve", xx[:], xx[:], 0.5, hbuf[:, :, 1:P + 1], ALU.mult, ALU.subtract, ["xx", "hbuf"], ["xx"])
                for j in range(6):
                    _tt(S, "dve", mxt[:], xx[:], mu[:, j, :][:, :, None].broadcast_to([P, 8, P]), ALU.mult, ["xx", "mu"], ["mxt"])
                    _tt(S, "dve", mix[:, j, :, :], mxt[:], hbuf[:, :, 1:P + 1], ALU.add, ["mxt", "hbuf"], [("mix", j)])
                for hi_, (wn, mj, fn) in enumerate((("w1c", 1, AF.Tanh), ("a1c", 4, AF.Copy), ("g1", 5, AF.Sigmoid))):
                    for c in range(8):
                        mm(S, pF[:, 0:P], W[wn][:, c, :], mix[:, mj, c, :], c == 0, c == 7,
                           reads=[wn, ("mix", mj)], writes=["pF"])
                    _act(S, hid[:, hi_, :], pF[:, 0:P], fn, ["pF"], [("hid", hi_)])
                for nm, mj, wn in (("r_sb", 0, "wr"), ("k_sb", 2, "wk"), ("v_sb", 3, "wv")):
                    ps, pk = proj(lambda c: mix[:, mj, c, :], lambda c, hf: W[wn][:, c, hf * 512:(hf + 1) * 512], wn,
                                  lkeys=[("mix", mj)])
                    evac2(lambda hf, sl: _act(S, F[nm][:, sl], ps[hf][:], AF.Copy, [pk[hf]], [nm]))
                S.op("pool", lambda e: e.tensor_copy(out=vb[:], in_=F["v_sb"][:]), reads=["v_sb"], writes=["vb"])
                S.dma("sp", self.vtok_d[t0:t0 + P, :], vb[:], reads=["vb"], writes=[("vtok", T)])
                ps, pk = proj(lambda c: hid[:, 2, :], lambda c, hf: W["g2"][:, hf * 512:(hf + 1) * 512], "g2", nchunks=1,
                              lkeys=[("hid", 2)])
                evac2(lambda hf, sl: _act(S, F["tA"][:, sl], ps[hf][:], AF.Copy, [pk[hf]], ["tA"]))
                S.dma("sp", self.g_d[t0:t0 + P, :], F["tA"][:], reads=["tA"], writes=[("g_d", T)])
                _tt(S, "dve", F["tA"][:], F["k_sb"][:], R["kk_r"][:], ALU.mult, ["k_sb", "kk_r"], ["tA"])
                _tt(S, "pool", F["tB"][:], F["tA"][:], F["tA"][:], ALU.mult, ["tA"], ["tB"])
                S.op("dve", lambda e: e.tensor_reduce(out=st16[:, 0:16], in_=_h3(F["tB"][:]), axis=AX.X, op=ALU.add),
                     reads=["tB"], writes=["st16"])
                S.op("dve", lambda e: e.tensor_scalar(out=st16[:, 0:16], in0=st16[:, 0:16], scalar1=1e-24, scalar2=None, op0=ALU.max),
                     reads=["st16"], writes=["st16"])
                _act(S, st16[:, 0:16], st16[:, 0:16], AF.Sqrt, ["st16"], ["st16"])
                S.op("dve", lambda e: e.reciprocal(out=st16[:, 16:32], in_=st16[:, 0:16]), reads=["st16"], writes=["st16"])
                _tt(S, "dve", _h3(F["kkn"][:]), _h3(F["tA"][:]), st16[:, 16:32][:, :, None].broadcast_to([P, 16, 64]), ALU.mult,
                    ["tA", "st16"], ["kkn"])
                for d in range(2):
                    ps, pk = proj(lambda c: hid[d * 64:(d + 1) * 64, 0, :], lambda c, hf: W["w2c"][d * 64:(d + 1) * 64, hf * 512:(hf + 1) * 512],
                                  "w2c", nchunks=1, lkeys=[("hid", 0)])
                    evac2(lambda hf, sl: _tt(S, "dve", F["tB"][:, sl], ps[hf][:], R[f"w0_{d}"][:, sl], ALU.add, [pk[hf], f"w0_{d}"], ["tB"]))
                    _act(S, F["tB"][:], F["tB"][:], AF.Sigmoid, ["tB"], ["tB"])
                    _act(S, F["lw"][:], F["tB"][:], AF.Copy, ["tB"], ["lw"], scale=LWC)
                    ps, pk = proj(lambda c: hid[d * 64:(d + 1) * 64, 1, :], lambda c, hf: W["a2c"][d * 64:(d + 1) * 64, hf * 512:(hf + 1) * 512],
                                  "a2c", nchunks=1, lkeys=[("hid", 1)])
                    evac2(lambda hf, sl: _tt(S, "dve", F["tC"][:, sl], ps[hf][:], R[f"a0_{d}"][:, sl], ALU.add, [pk[hf], f"a0_{d}"], ["tC"]))
                    _act(S, F["tC"][:], F["tC"][:], AF.Sigmoid, ["tC"], ["tC"])
                    kd = F[f"kd{d}"]; kdk = f"kd{d}"
                    _stt(S, "dve", F["tD"][:], F["tC"][:], -1.0, R["ka_r"][:], ALU.add, ALU.mult, ["tC", "ka_r"], ["tD"])
                    _stt(S, "pool", kd[:], F["tD"][:], 1.0, F["k_sb"][:], ALU.add, ALU.mult, ["tD", "k_sb"], [kdk])
                    _tt(S, "pool", F["tC"][:], F["kkn"][:], F["tC"][:], ALU.mult, ["kkn", "tC"], ["tC"])
                    for hf in range(2):
                        mm(S, pCl[hf][:], tri[:, d, :], F["lw"][:, hf * 512:(hf + 1) * 512], True, True,
                           reads=["tri", "lw"], writes=[("pCl", hf)])
                    evac2(lambda hf, sl: _act(S, F["tE"][:, sl], pCl[hf][:], AF.Exp, [("pCl", hf)], ["tE"]))
                    evac2(lambda hf, sl: _act(S, F["tF"][:, sl], pCl[hf][:], AF.Exp, [("pCl", hf)], ["tF"], scale=-1.0))
                    for hf in range(2):
                        mm(S, pCl[hf][:], onef[:], F["lw"][:, hf * 512:(hf + 1) * 512], True, True,
                           reads=["onef", "lw"], writes=[("pCl", hf)])
                    evac2(lambda hf, sl: _act(S, F["tH"][:, sl], pCl[hf][:], AF.Exp, [("pCl", hf)], ["tH"]))
                    _act(S, F["tG"][:], F["lw"][:], AF.Exp, ["lw"], ["tG"], scale=-1.0)
                    _tt(S, "dve", F["tG"][:], F["tG"][:], F["tE"][:], ALU.mult, ["tG", "tE"], ["tG"])
                    _tt(S, "pool", F["tH"][:], F["tH"][:], F["tF"][:], ALU.mult, ["tH", "tF"], ["tH"])
                    for j in range(8):
                        mm(S, pF[:, 256 + j:257 + j], F["lw"][:, j * P:(j + 1) * P], onef[:, 0:1], True, True,
                           reads=["lw", "onef"], writes=["pF"])
                    _act(S, gcs[:], pF[:, 256:264], AF.Exp, ["pF"], ["gcs"])
                    S.dma("sp", self.gC_d[d, T], gcs[:], reads=["gcs"], writes=[("gC_d", d, T)])
                    _stt(S, "dve", ob[0][:], F["kkn"][:], -1.0, F["tG"][:], ALU.mult, ALU.mult, ["kkn", "tG"], [("ob", 0)])
                    _tt(S, "pool", ob[1][:], F["r_sb"][:], F["tE"][:], ALU.mult, ["r_sb", "tE"], [("ob", 1)])
                    _tt(S, "dve", ob[2][:], F["tC"][:], F["tF"][:], ALU.mult, ["tC", "tF"], [("ob", 2)])
                    _tt(S, "pool", ob[3][:], kd[:], F["tF"][:], ALU.mult, [kdk, "tF"], [("ob", 3)])
                    _tt(S, "dve", bkt[:, 0, :], F["tC"][:], F["tH"][:], ALU.mult, ["tC", "tH"], ["bkt"])
                    _tt(S, "pool", bkt[:, 1, :], kd[:], F["tH"][:], ALU.mult, [kdk, "tH"], ["bkt"])
                    S.dma("sp", self.bk_d[d, T], bkt[:].rearrange("p a n -> p (a n)"), reads=["bkt"], writes=[("bk_d", d, T)])
                    for q in range(4):
                        for c in range(8):
                            S.op("pe", lambda e: e.transpose(out=pT[:, c, :], in_=ob[q][:, c * P:(c + 1) * P], identity=self.idb[:]),
                                 reads=[("ob", q), "idb"], writes=["pT"], accum=(c > 0))
                        if q % 2 == 0:
                            _act(S, ft[:, :, q, :], pT[:], AF.Copy, ["pT"], ["ft"])
                        else:
                            S.op("dve", lambda e: e.tensor_copy(out=ft[:, :, q, :], in_=pT[:]), reads=["pT"], writes=["ft"])
                    S.dma("sp", self.featT_d[d, T], ft[:].rearrange("p j q t -> p (j q t)"), reads=["ft"], writes=[("featT_d", d, T)])
                _tt(S, "pool", F["tD"][:], F["kd0"][:], F["kd1"][:], ALU.add, ["kd0", "kd1"], ["tD"])
                _tt(S, "pool", F["tD"][:], F["tD"][:], F["r_sb"][:], ALU.mult, ["tD", "r_sb"], ["tD"])
                _tt(S, "pool", F["tD"][:], F["tD"][:], R["rk_r"][:], ALU.mult, ["tD", "rk_r"], ["tD"])
                S.op("dve", lambda e: e.tensor_reduce(out=st16[:, 32:48], in_=_h3(F["tD"][:]), axis=AX.X, op=ALU.add),
                     reads=["tD"], writes=["st16"])
                _tt(S, "dve", _h3(F["tD"][:]), _h3(F["v_sb"][:]), st16[:, 32:48][:, :, None].broadcast_to([P, 16, 64]), ALU.mult,
                    ["v_sb", "st16"], ["tD"])
                S.dma("sp", self.bonus_d[t0:t0 + P, :], F["tD"][:], reads=["tD"], writes=[("bonus_d", T)])

    def phase_R2(self):
        S = self.S
        with S.scope():
            m4 = S.sb("m4", [P, 2, 512], F32)
            lmT = S.sb("lmT", [P, 2, 7, P], BF16)
            S.dma("sp", m4[:], self.m4_c.rearrange("d s n -> s d n"), writes=["m4"])
            S.dma("pool", lmT[:], self.lmT_c.rearrange("d l s n -> s d l n"), writes=["lmT"])
            idb = self.idb
            ST32 = S.sb("ST32", [P, 8, 64], F32)
            STb = S.sb("STb", [P, 8, 64], BF16)
            Fb = [S.sb(f"Fb{i}", [P, 8, 4, P], BF16) for i in range(2)]
            Vb = [S.sb(f"Vb{i}", [P, D], BF16) for i in range(2)]
            BKb = [S.sb(f"BKb{i}", [P, 2, D], BF16) for i in range(2)]
            gCb = [S.sb(f"gCb{i}", [P, 8], F32) for i in range(2)]
            ysb = [S.sb(f"ysb{i}", [P, D], F32) for i in range(2)]
            NG = 4
            GMa = [S.sb(f"GMa{i}", [P, NG, 512], BF16) for i in range(2)]
            MLT = [[S.sb(f"MLT{i}_{l}", [P, NG, P], BF16) for l in range(7)] for i in range(2)]
            XTb = [S.sb(f"XT{i}", [P, NG, P], BF16) for i in range(2)]
            Xb = [S.sb(f"X{i}", [P, NG, P], BF16) for i in range(2)]
            T1s = S.sb("T1s", [P, NG, P], BF16)
            Zq = S.sb("Zq", [P, NG, 64], BF16)
            Pb = S.sb("Pb", [P, NG, 64], BF16)
            pG = [S.ps(f"pG{i}", [P, 512], F32) for i in range(2)]
            pT1 = S.ps("pT1", [P, NG, P], F32)
            pU = S.ps("pU", [P, NG, P], F32)
            pUT = S.ps("pUT", [P, NG, P], F32)
            pZS = S.ps("pZS", [P, 8, 64], F32)
            pY = [S.ps(f"pY{h}", [P, 512], F32) for h in range(2)]
            idbc = idb[:, None, :].broadcast_to([P, NG, P])

            items = []
            it = 0
            for d in range(2):
                order = list(range(NT)) if d == 0 else [1, 0] + list(range(NT - 1, 1, -1))
                for ci, T in enumerate(order):
                    for g0 in range(0, 16, NG):
                        items.append(dict(d=d, T=T, g0=g0, b=it % 2, first=(ci == 0 and g0 == 0), gi=len(items)))
                    it += 1

            def heads_of(g0):
                return [(g, g0 + g, (g0 + g) // 2, ((g0 + g) % 2) * 64) for g in range(NG)]

            def load_chunk(w):
                d, T, b = w["d"], w["T"], w["b"]
                S.dma("sp", Fb[b][:].rearrange("p j q t -> p (j q t)"), self.featT_d[d, T], reads=[("featT_d", d, T)], writes=[("Fb", b)])
                S.dma("sp", Vb[b][:], self.vtok_d[T * P:(T + 1) * P, :], reads=[("vtok", T)], writes=[("Vb", b)])
                S.dma("sp", BKb[b][:].rearrange("p a n -> p (a n)"), self.bk_d[d, T], reads=[("bk_d", d, T)], writes=[("BKb", b)])
                S.dma("sp", gCb[b][:], self.gC_d[d, T], reads=[("gC_d", d, T)], writes=[("gCb", b)])

            def stageG(w):
                d, b, gi = w["d"], w["b"], w["gi"]
                if w["g0"] == 0:
                    load_chunk(w)
                Fk = ("Fb", b)
                GM = GMa[gi % 2]; gmk = ("GMa", gi % 2)
                ML = MLT[gi % 2]
                for (g, h, j, pb_) in heads_of(w["g0"]):
                    bank = pG[g % 2]; bkk = ("pG", g % 2)
                    F_ = Fb[b]
                    AR = F_[pb_:pb_ + 64, j, 0:2, :].rearrange("p q t -> p (q t)")
                    mm(S, bank[:, 0:128], F_[pb_:pb_ + 64, j, 2, :], F_[pb_:pb_ + 64, j, 1, :], True, True, reads=[Fk], writes=[bkk])
                    mm(S, bank[:, 128:384], F_[pb_:pb_ + 64, j, 3, :], AR, True, True, reads=[Fk], writes=[bkk])
                    mm(S, bank[:, 384:512], F_[pb_:pb_ + 64, j, 0, :], F_[pb_:pb_ + 64, j, 2, :], True, True, reads=[Fk], writes=[bkk])
                    _tt(S, "dve", GM[:, g, :], bank[:], m4[:, d, :], ALU.mult, ["m4"], [bkk, gmk])
                MTv = GM[:, :, 384:512]
                for l in range(7):
                    _tt(S, "pool", ML[l][:], MTv, lmT[:, d, l:l + 1, :].broadcast_to([P, NG, P]), ALU.mult, [gmk, "lmT"], [("MLT", gi % 2, l)])

            def stageL(w):
                d, b, gi, g0 = w["d"], w["b"], w["gi"], w["g0"]
                Fk, Vk = ("Fb", b), ("Vb", b)
                GM = GMa[gi % 2]; gmk = ("GMa", gi % 2)
                ML = MLT[gi % 2]
                hs = heads_of(g0)
                if w["first"]:
                    S.op("dve", lambda e: e.memset(ST32[:], 0.0), writes=["ST32"])
                    S.op("dve", lambda e: e.memset(STb[:], 0.0), writes=["STb"])
                for (g, h, j, pb_) in hs:
                    mm(S, pZS[:, g, :], Fb[b][pb_:pb_ + 64, j, 0, :], STb[pb_:pb_ + 64, j, :], True, False, reads=[Fk, "STb"], writes=["pZS"])
                    mm(S, pZS[:, g, :], GM[:, g, 128:256], Vb[b][:, h * 64:(h + 1) * 64], False, True, reads=[gmk, Vk], writes=["pZS"])
                S.op("dve", lambda e: e.tensor_copy(out=Zq[:], in_=pZS[:, 0:NG, :]), reads=[], writes=["pZS", "Zq"])
                xi = 0
                _tt(S, "pool", XTb[xi][:], ML[0][:], idbc, ALU.add, [("MLT", gi % 2, 0), "idb"], [("XT", xi)])
                for (g, h, j, pb_) in hs:
                    mm(S, pU[:, g, :], ML[0][:, g, :], idb[:], True, True, reads=[("MLT", gi % 2, 0), "idb"], writes=["pU"])
                _tt(S, "dve", Xb[xi][:], pU[:], idbc, ALU.add, ["idb"], ["pU", ("X", xi)])
                for l in range(1, 7):
                    X, XT = Xb[xi], XTb[xi]
                    Xn, XTn = Xb[1 - xi], XTb[1 - xi]
                    for (g, h, j, pb_) in hs:
                        mm(S, pT1[:, g, :], ML[l][:, g, :], X[:, g, :], True, True, reads=[("MLT", gi % 2, l), ("X", xi)], writes=["pT1"])
                    _tt(S, "dve", T1s[:], pT1[:], idbc, ALU.add, ["idb"], ["pT1", "T1s"])
                    for (g, h, j, pb_) in hs:
                        mm(S, pU[:, g, :], XT[:, g, :], T1s[:, g, :], True, True, reads=[("XT", xi), "T1s"], writes=["pU"])
                    if l < 6:
                        for (g, h, j, pb_) in hs:
                            mm(S, pUT[:, g, :], T1s[:, g, :], XT[:, g, :], True, True, reads=[("XT", xi), "T1s"], writes=["pUT"])
                    _act(S, Xn[:], pU[:], AF.Copy, [], ["pU", ("X", 1 - xi)])
                    if l < 6:
                        S.op("dve", lambda e: e.tensor_copy(out=XTn[:], in_=pUT[:]), reads=[], writes=["pUT", ("XT", 1 - xi)])
                    xi = 1 - xi
                X = Xb[xi]
                for (g, h, j, pb_) in hs:
                    mm(S, pZS[:, g, :], X[:, g, :], Zq[:, g, :], True, True, reads=[("X", xi), "Zq"], writes=["pZS"])
                S.op("dve", lambda e: e.tensor_copy(out=Pb[:], in_=pZS[:, 0:NG, :]), reads=[], writes=["pZS", "Pb"])

            def stageF(w):
                d, T, b, gi, g0 = w["d"], w["T"], w["b"], w["gi"], w["g0"]
                Fk, Vk, BKk, gk = ("Fb", b), ("Vb", b), ("BKb", b), ("gCb", b)
                GM = GMa[gi % 2]; gmk = ("GMa", gi % 2)
                hs = heads_of(g0)
                for (g, h, j, pb_) in hs:
                    yk = ("pY", h // 8)
                    yo = pY[h // 8][:, (h % 8) * 64:(h % 8 + 1) * 64]
                    mm(S, yo, GM[:, g, 0:128], Pb[:, g, :], True, False, reads=[gmk, "Pb"], writes=[yk])
                    mm(S, yo, GM[:, g, 256:384], Vb[b][:, h * 64:(h + 1) * 64], False, False, reads=[gmk, Vk], writes=[yk])
                    mm(S, yo, Fb[b][pb_:pb_ + 64, j, 1, :], STb[pb_:pb_ + 64, j, :], False, True, reads=[Fk, "STb"], writes=[yk])
                for (g, h, j, pb_) in hs:
                    mm(S, pZS[:, 4 + g, :], BKb[b][:, 0, j * P:(j + 1) * P], Pb[:, g, :], True, False, reads=[BKk, "Pb"], writes=["pZS"])
                    mm(S, pZS[:, 4 + g, :], BKb[b][:, 1, j * P:(j + 1) * P], Vb[b][:, h * 64:(h + 1) * 64], False, True,
                       reads=[BKk, Vk], writes=["pZS"])
                for (g, h, j, pb_) in hs:
                    _stt(S, "dve", ST32[pb_:pb_ + 64, j, :], ST32[pb_:pb_ + 64, j, :], gCb[b][pb_:pb_ + 64, j:j + 1],
                         pZS[pb_:pb_ + 64, 4 + g, :], ALU.mult, ALU.add, [gk], ["ST32", "pZS"])
                S.op("pool", lambda e: e.tensor_copy(out=STb[:, g0 // 2:g0 // 2 + 2, :], in_=ST32[:, g0 // 2:g0 // 2 + 2, :]),
                     reads=["ST32"], writes=["STb"])
                if g0 + NG == 16:
                    yb_ = ysb[b]
                    for hf in range(2):
                        _act(S, yb_[:, hf * 512:(hf + 1) * 512], pY[hf][:], AF.Copy, [], [("pY", hf), ("ysb", b)])
                    S.dma("sp", self.y_d[d, T * P:(T + 1) * P, :], yb_[:], reads=[("ysb", b)], writes=[("y_d", d, T)])

            stageG(items[0])
            for k, w in enumerate(items):
                if k + 1 < len(items):
                    stageG(items[k + 1])
                stageL(w)
                stageF(w)

    def phase_R3(self):
        S = self.S
        with S.scope():
            R = {}
            for nm, src in (("lng_r", self.rw_lng), ("lnb_r", self.rw_lnb)):
                R[nm] = S.sb(nm, [P, D], F32)
                S.dma("sp", R[nm][:], src.broadcast_to([P, D]), writes=[nm])
            B = [{n: S.sb(f"{n}{i}", [P, D], F32) for n in ("yf", "yb", "gg", "bo")} for i in range(2)]
            zb = [S.sb(f"zb{i}", [P, D], BF16) for i in range(2)]
            zt = [S.sb(f"zt3_{i}", [P, 8, P], BF16) for i in range(2)]
            st = [S.sb(f"st3_{i}", [P, 64], F32) for i in range(2)]
            pT = [S.ps(f"pT3_{i}", [P, 8, P], BF16) for i in range(2)]
            for T in range(2, NT):
                b = T % 2
                Bf = B[b]
                k = lambda n: (n, b)
                t0 = T * P
                S.dma("sp", Bf["yf"][:], self.y_d[0, t0:t0 + P, :], reads=[("y_d", 0, T)], writes=[k("yf")])
                S.dma("sp", Bf["yb"][:], self.y_d[1, t0:t0 + P, :], reads=[("y_d", 1, T)], writes=[k("yb")])
                S.dma("sp", Bf["gg"][:], self.g_d[t0:t0 + P, :], reads=[("g_d", T)], writes=[k("gg")])
                S.dma("sp", Bf["bo"][:], self.bonus_d[t0:t0 + P, :], reads=[("bonus_d", T)], writes=[k("bo")])
                y = Bf["yf"]; t2 = Bf["yb"]
                _tt(S, "dve", y[:], y[:], t2[:], ALU.add, [k("yf"), k("yb")], [k("yf")])
                S.op("dve", lambda e: e.tensor_reduce(out=st[b][:, 0:16], in_=_h3(y[:]), axis=AX.X, op=ALU.add), reads=[k("yf")], writes=[k("st")])
                S.op("dve", lambda e: e.tensor_scalar(out=st[b][:, 0:16], in0=st[b][:, 0:16], scalar1=-1.0 / 64, scalar2=None, op0=ALU.mult),
                     reads=[k("st")], writes=[k("st")])
                _tt(S, "dve", _h3(y[:]), _h3(y[:]), st[b][:, 0:16][:, :, None].broadcast_to([P, 16, 64]), ALU.add, [k("yf"), k("st")], [k("yf")])
                _tt(S, "pool", t2[:], y[:], y[:], ALU.mult, [k("yf")], [k("yb")])
                S.op("dve", lambda e: e.tensor_reduce(out=st[b][:, 16:32], in_=_h3(t2[:]), axis=AX.X, op=ALU.add), reads=[k("yb")], writes=[k("st")])
                S.op("dve", lambda e: e.tensor_scalar(out=st[b][:, 16:32], in0=st[b][:, 16:32], scalar1=1.0 / 64, scalar2=GN_EPS, op0=ALU.mult, op1=ALU.add),
                     reads=[k("st")], writes=[k("st")])
                _act(S, st[b][:, 16:32], st[b][:, 16:32], AF.Sqrt, [k("st")], [k("st")])
                S.op("dve", lambda e: e.reciprocal(out=st[b][:, 32:48], in_=st[b][:, 16:32]), reads=[k("st")], writes=[k("st")])
                _tt(S, "dve", _h3(y[:]), _h3(y[:]), st[b][:, 32:48][:, :, None].broadcast_to([P, 16, 64]), ALU.mult, [k("yf"), k("st")], [k("yf")])
                _tt(S, "pool", y[:], y[:], R["lng_r"][:], ALU.mult, [k("yf"), "lng_r"], [k("yf")])
                _tt(S, "pool", y[:], y[:], R["lnb_r"][:], ALU.add, [k("yf"), "lnb_r"], [k("yf")])
                _tt(S, "dve", y[:], y[:], Bf["bo"][:], ALU.add, [k("yf"), k("bo")], [k("yf")])
                _tt(S, "dve", zb[b][:], y[:], Bf["gg"][:], ALU.mult, [k("yf"), k("gg")], [k("zb")])
                for c in range(8):
                    S.op("pe", lambda e: e.transpose(out=pT[b][:, c, :], in_=zb[b][:, c * P:(c + 1) * P], identity=self.idb[:]),
                         reads=[k("zb"), "idb"], writes=[k("pT3")], accum=(c > 0))
                _act(S, zt[b][:], pT[b][:], AF.Copy, [k("pT3")], [k("zt3")])
                S.dma("sp", self.zT_d[:, :, t0:t0 + P].rearrange("c p t -> p c t"), zt[b][:], reads=[k("zt3")],
                      writes=[("zT", c, T) for c in range(8)])


def build_program(stop_after=None, debug=(), phases="M0ABCD1abcde"):
    nc = bass.Bass("TRN2", target_bir_lowering=False)
    Pg = Prog1(nc, debug)
    S = Pg.S
    Pg.consts()
    if "M" in phases:
        Pg.phase_mod()
    if "0" in phases:
      with S.scope():
        V0 = Pg.load_layer_vecs(0)
        if "A" in phases: Pg.phase_L0_proj(V0)
        if "B" in phases: Pg.phase_L0_pool()
        if "C" in phases: Pg.phase_L0_attn()
        if "D" in phases: Pg.phase_out_mlp(0, V0, Pg.ev_w_out, Pg.y_tile_L0, lambda T: (Pg.x_d[T * P:(T + 1) * P, :], ("x_d", T)))
    if "1" in phases:
      with S.scope():
        V1 = Pg.load_layer_vecs(1)
        if "a" in phases: Pg.phase_R0(V1)
        if "b" in phases: Pg.phase_R1()
        if "c" in phases: Pg.phase_R2()
        if "d" in phases: Pg.phase_R3()
        if "e" in phases: Pg.phase_out_mlp(1, V1, Pg.rw_wo, Pg.y_tile_L0, lambda T: (Pg.out[(T - 2) * P:(T - 1) * P, :], ("out", T)))
    S.barrier()
    S.finish([])
    S.close()
    return nc, Pg


_host_inputs0 = host_inputs


def host_inputs(inputs):
    maps = _host_inputs0(inputs)
    f = lambda a: np.ascontiguousarray(np.asarray(a, dtype=np.float32))
    tri, m4, lmT = _scan_consts()
    sh = {
        "rw_mu": f(inputs["rw_mu"])[0], "rw_wr": f(inputs["rw_wr"])[0], "rw_wk": f(inputs["rw_wk"])[0],
        "rw_wv": f(inputs["rw_wv"])[0], "rw_wo": f(inputs["rw_wo"])[0],
        "rw_w0": f(inputs["rw_w0"])[0], "rw_a0": f(inputs["rw_a0"])[0],
        "w1cat": f(np.concatenate([inputs["rw_w1"][0, 0], inputs["rw_w1"][0, 1]], axis=1)),
        "a1cat": f(np.concatenate([inputs["rw_a1"][0, 0], inputs["rw_a1"][0, 1]], axis=1)),
        "rw_g1": f(inputs["rw_g1"])[0],
        "w2cat": f(np.asarray(inputs["rw_w2"])[0].reshape(128, 1024)), "a2cat": f(np.asarray(inputs["rw_a2"])[0].reshape(128, 1024)),
        "rw_g2": f(inputs["rw_g2"])[0],
        "rw_kk": f(inputs["rw_kk"]).reshape(1, 1024), "rw_ka": f(inputs["rw_ka"]).reshape(1, 1024),
        "rw_rk": f(inputs["rw_rk"]).reshape(1, 1024), "rw_lng": f(inputs["rw_lng"]).reshape(1, 1024),
        "rw_lnb": f(inputs["rw_lnb"]).reshape(1, 1024),
        "tri_c": f(tri), "m4_c": f(m4), "lmT_c": f(lmT),
    }
    for m in maps:
        m.update(sh)
    return maps
```

```python
import contextlib
import numpy as np
import concourse.bass as bass
import concourse.mybir as mybir

F32 = mybir.dt.float32
BF16 = mybir.dt.bfloat16
AF = mybir.ActivationFunctionType
ALU = mybir.AluOpType
AX = mybir.AxisListType

SEM_LIMIT = 10000


class _Ctr:
    def __init__(self, S, name, step):
        self.S = S
        self.name = name
        self.step = step
        self.gen = 0
        self.sem = S._newsem(f"{name}_0")
        self.val = 0

    def next_event(self):
        if self.val + self.step > SEM_LIMIT:
            self.gen += 1
            self.sem = self.S._newsem(f"{self.name}_{self.gen}")
            self.val = 0
        self.val += self.step
        return (self.sem, self.val)


class _PsView:
    def __init__(self, t, shape):
        self.t = t
        self.n1 = shape[1]

    def __getitem__(self, key):
        if not isinstance(key, tuple):
            key = (key,)
        key = list(key)
        if len(key) < 2:
            key.append(slice(None))
        k1 = key[1]
        if isinstance(k1, slice):
            start, stop, step = k1.indices(self.n1)
            key[1] = slice(start, stop, step)
        return self.t[tuple(key)]


class _Eng:
    def __init__(self, S, name, obj):
        self.name = name
        self.obj = obj
        self.ctr = _Ctr(S, "s_" + name, 1)
        self.seen = {}
        self.n_issued = 0
        self.last_ins = None
        self.last_has_inc = False
        self.inc_idx = []
        self.inc_ev = []


class LazyEv:
    __slots__ = ("eng", "idx")

    def __init__(self, eng, idx):
        self.eng = eng
        self.idx = idx


class _Res:
    __slots__ = ("w", "r")

    def __init__(self):
        self.w = None
        self.r = {}


class Sched:
    def __init__(self, nc, n_dma_slots=8):
        self.nc = nc
        self.stack = contextlib.ExitStack()
        self.scopes = [self.stack]
        self.res = {}
        self.engs = {
            "pe": _Eng(self, "pe", nc.tensor),
            "act": _Eng(self, "act", nc.scalar),
            "dve": _Eng(self, "dve", nc.vector),
            "pool": _Eng(self, "pool", nc.gpsimd),
            "sp": _Eng(self, "sp", nc.sync),
        }
        self.dma_slots = {}
        for q in ("sp", "pool"):
            self.dma_slots[q] = [_Ctr(self, f"d_{q}{i}", 16) for i in range(n_dma_slots)]
        self.dma_rr = {"sp": 0, "pool": 0}
        self.n_inst = 0
        self.uid = 0
        self.pending = None
        self.lazy_engines = ("pe",)

    def _newsem(self, name):
        return self.stack.enter_context(self.nc.semaphore(name))

    def sb(self, name, shape, dt):
        self.uid += 1
        return self.scopes[-1].enter_context(self.nc.sbuf_tensor(f"sb{self.uid}_{name}", list(shape), dt))

    def ps(self, name, shape, dt=F32):
        self.uid += 1
        esz = 4 if dt == F32 else 2
        per_part = esz
        for d_ in shape[1:]:
            per_part *= d_
        assert per_part <= 2048, (name, shape)
        shape = list(shape)
        if per_part < 2048:
            rest = per_part // shape[1]
            assert 2048 % rest == 0, (name, shape)
            full = [shape[0], 2048 // rest] + shape[2:]
            t = self.scopes[-1].enter_context(self.nc.psum_tensor(f"ps{self.uid}_{name}", full, dt))
            return _PsView(t, shape)
        return self.scopes[-1].enter_context(self.nc.psum_tensor(f"ps{self.uid}_{name}", shape, dt))

    @contextlib.contextmanager
    def scope(self):
        st = contextlib.ExitStack()
        self.scopes.append(st)
        try:
            yield
        finally:
            self.barrier()
            self.scopes.pop()
            st.close()

    def barrier(self):
        evs = []
        for e in self.engs.values():
            if e.n_issued > 0:
                evs.append(self._resolve(LazyEv(e, e.n_issued - 1)))
        for q in self.dma_slots:
            for ctr in self.dma_slots[q]:
                if ctr.val > 0:
                    evs.append((ctr.sem, ctr.val))
        for e in self.engs.values():
            for ev in evs:
                self._wait(e, ev)

    def _r(self, key):
        r = self.res.get(key)
        if r is None:
            r = self.res[key] = _Res()
        return r

    def _resolve(self, ev):
        if not isinstance(ev, LazyEv):
            return ev
        import bisect
        e = ev.eng
        k = bisect.bisect_left(e.inc_idx, ev.idx)
        if k < len(e.inc_idx):
            return e.inc_ev[k]
        assert e.last_ins is not None and not e.last_has_inc and e.n_issued - 1 >= ev.idx
        sv = e.ctr.next_event()
        e.last_ins.then_inc(sv[0], 1)
        e.last_has_inc = True
        e.inc_idx.append(e.n_issued - 1)
        e.inc_ev.append(sv)
        return sv

    def _wait(self, eng, ev):
        if ev is None:
            return
        sem, val = self._resolve(ev)
        k = id(sem)
        if eng.seen.get(k, 0) >= val:
            return
        if self.pending is not None:
            cur = self.pending.get(k)
            if cur is None or cur[1] < val:
                self.pending[k] = (sem, val)
            return
        eng.obj.wait_ge(sem, val)
        eng.seen[k] = val

    def _flush(self, eng):
        pend = list(self.pending.values())
        self.pending = None
        for (sem, val) in pend[:-1]:
            eng.obj.wait_ge(sem, val)
            eng.seen[id(sem)] = val
        if pend:
            sem, val = pend[-1]
            eng.seen[id(sem)] = val
            return (sem, val)
        return None

    def _deps(self, eng, reads, writes, skip_same_eng_write=False):
        for key in reads:
            r = self._r(key)
            self._wait(eng, r.w)
        for key in writes:
            r = self._r(key)
            if not (skip_same_eng_write and isinstance(r.w, LazyEv) and r.w.eng is eng):
                self._wait(eng, r.w)
            for ev in r.r.values():
                self._wait(eng, ev)

    def _commit(self, ev, reads, writes):
        rk = ev.eng.name if isinstance(ev, LazyEv) else id(ev[0])
        for key in reads:
            self._r(key).r[rk] = ev
        for key in writes:
            r = self._r(key)
            r.w = ev
            r.r = {}

    def op(self, engname, fn, reads=(), writes=(), accum=False):
        eng = self.engs[engname]
        self.pending = {}
        self._deps(eng, reads, writes, skip_same_eng_write=accum)
        last = self._flush(eng)
        ins = fn(eng.obj)
        if last is not None:
            ins._wait_ge(last[0], last[1])
        eng.last_ins = ins
        eng.last_has_inc = False
        ev = LazyEv(eng, eng.n_issued)
        eng.n_issued += 1
        if engname not in self.lazy_engines:
            self._resolve(ev)
        self._commit(ev, reads, writes)
        self.n_inst += 1
        return ev

    def dma(self, q, out, in_, reads=(), writes=(), **kw):
        eng = self.engs[q]
        slots = self.dma_slots[q]
        i = self.dma_rr[q]
        self.dma_rr[q] = (i + 1) % len(slots)
        ctr = slots[i]
        self.pending = {}
        if ctr.val > 0:
            self._wait(eng, (ctr.sem, ctr.val))
        self._deps(eng, reads, writes)
        last = self._flush(eng)
        ev = ctr.next_event()
        ins = eng.obj.dma_start(out=out, in_=in_, **kw)
        if last is not None:
            ins._wait_ge(last[0], last[1])
        ins.then_inc(ev[0], 16)
        self._commit(ev, reads, writes)
        self.n_inst += 1
        return ev

    def finish(self, final_keys):
        eng = self.engs["sp"]
        for key in final_keys:
            r = self._r(key)
            self._wait(eng, r.w)
        for q in self.dma_slots:
            for ctr in self.dma_slots[q]:
                if ctr.val > 0:
                    self._wait(eng, (ctr.sem, ctr.val))

    def close(self):
        self.stack.close()

from concourse.bass_utils import run_bass_kernel_spmd

D = 1024
NCTX = 256
NLAT = 4096
NTOK = NCTX + NLAT
NT = NTOK // 128
EPS = 1e-6
P = 128


def _pool_bands():
    L = 1024
    out = np.zeros((4, 5, 128, 128), np.float32)
    for g, w in enumerate((2, 4, 8, 16)):
        def full(L):
            t = np.arange(L)
            lo = np.clip(t - w // 2, 0, L)
            hi = np.clip(t + w // 2, 0, L)
            s = np.arange(L)[:, None]
            m = ((s >= lo[None, :]) & (s < hi[None, :])).astype(np.float64) / (hi - lo)[None, :]
            m -= np.eye(L)
            return m
        m = full(L)
        out[g, 0] = m[3 * 128:4 * 128, 4 * 128:5 * 128]
        out[g, 1] = m[5 * 128:6 * 128, 4 * 128:5 * 128]
        out[g, 2] = m[4 * 128:5 * 128, 4 * 128:5 * 128]
        out[g, 3] = m[0:128, 0:128]
        out[g, 4] = m[L - 128:, L - 128:]
    return out


_VARS = [(-2, "pm"), (-1, "f"), (0, "f"), (1, "f"), (2, "pp")] + [(d, "f") for d in range(-3, 4)]


def _attn_tables():
    kc = np.arange(64)
    qc = np.arange(64)
    c_start = np.clip(qc - 8, 0, 48)
    col_ok = (kc[:, None] >= c_start[None, :]) & (kc[:, None] < c_start[None, :] + 16)
    dc_idx = np.clip(kc[:, None] - qc[None, :], -15, 15) + 15
    dr_idx = np.zeros((12, 128, 128), np.int64)
    dc_full = np.zeros((128, 128), np.int64)
    mask = np.zeros((12, 128, 128), np.float32)
    for a in range(2):
        for b in range(2):
            dc_full[a * 64:(a + 1) * 64, b * 64:(b + 1) * 64] = dc_idx
    for v, (dl, kind) in enumerate(_VARS):
        for a in range(2):
            for b in range(2):
                dr = 2 * dl + a - b + 7
                vis = True
                if kind == "pm":
                    vis = not (a == 0 and b == 1)
                elif kind == "pp":
                    vis = (a == 0 and b == 1)
                dr_idx[v, a * 64:(a + 1) * 64, b * 64:(b + 1) * 64] = min(max(dr, 0), 14)
                if vis and 0 <= dr <= 14:
                    mask[v, a * 64:(a + 1) * 64, b * 64:(b + 1) * 64] = col_ok
    return dr_idx, dc_full, mask


def mm(S, out, lhsT, rhs, start, stop, reads, writes):
    return S.op("pe", lambda e: e.matmul(out, lhsT=lhsT, rhs=rhs, start=start, stop=stop),
                reads=reads, writes=writes, accum=not start)


class Prog:
    def __init__(self, nc, debug=()):
        self.nc = nc
        self.S = Sched(nc)
        self.debug = debug
        self.dbg_out = {}
        dt = nc.dram_tensor
        I = lambda name, shape: dt(name, list(shape), F32, kind="ExternalInput").ap()
        self.x_in = I("x", [NLAT, D])
        self.ctx_in = I("ctx", [NCTX, D])
        self.cvec = I("cvec", [P, 8, 2])
        self.ada_w = I("ada_w", [2, D, 6 * D])
        self.ada_b = I("ada_b", [2, 6 * D])
        self.norm_g = I("norm_g", [2, 4, D])
        self.mlp_w1 = I("mlp_w1", [2, D, 4 * D])
        self.mlp_w2 = I("mlp_w2", [2, 4 * D, D])
        self.ev_w_in = I("ev_w_in", [D, 2 * D])
        self.ev_w_out = I("ev_w_out", [D, D])
        self.ev_pool_w = I("ev_pool_w", [4, P, P])
        self.ev_pool_scale = I("ev_pool_scale", [512])
        self.rpb_tab = I("rpb_tab", [P, 8 * 12, P])
        self.msk_tab = I("msk_tab", [P, 12, P])
        self.bands = I("bands", [P, 20, P])
        self.ident = I("ident", [P, P])
        self.out = dt("out", [NLAT, D], F32, kind="ExternalOutput").ap()
        X = lambda name, shape, d=F32: (dt(name, list(shape), d, kind="ExternalOutput").ap() if name in debug
                                        else dt(name, list(shape), d).ap())
        self.modd = X("modd", [2, 2, 6 * D])
        self.x_d = X("x_d", [NTOK, D])
        self.upool_d = X("upool_d", [NTOK, 512], BF16)
        self.v_d = X("v_d", [NTOK, 512], BF16)
        self.qT_d = X("qT_d", [4, P, NTOK], BF16)
        self.kT_d = X("kT_d", [4, P, NTOK], BF16)
        self.zT_d = X("zT_d", [8, P, NTOK], BF16)

    def dbg(self, name, shape, dtp=F32):
        t = self.nc.dram_tensor("dbg_" + name, list(shape), dtp, kind="ExternalOutput").ap()
        self.dbg_out[name] = t
        return t

    def consts(self):
        S = self.S
        self.idb = S.sb("idb", [P, P], BF16)
        S.dma("pool", self.idb[:], self.ident, writes=["idb"])
        self.ones_bf = S.sb("ones_bf", [P, P], BF16)
        S.op("dve", lambda e: e.memset(self.ones_bf[:], 1.0), writes=["ones_bf"])
        self.eps_t = S.sb("eps_t", [P, 1], F32)
        S.op("dve", lambda e: e.memset(self.eps_t[:], EPS), writes=["eps_t"])

    def phase_mod(self):
        S = self.S
        with S.scope():
            cv = S.sb("cv", [P, 8, 2], F32)
            cvb = S.sb("cvb", [P, 8, 2], BF16)
            S.dma("sp", cv[:], self.cvec, writes=["cv"])
            S.op("act", lambda e: e.activation(out=cvb[:], in_=cv[:], func=AF.Silu), reads=["cv"], writes=["cvb"])
            aw = S.sb("aw", [P, 8, 6 * D], BF16)
            ab = S.sb("ab", [2, 6 * D], F32)
            mrow = S.sb("mrow", [2, 6 * D], F32)
            pss = [S.ps(f"pm{i}", [2, 512], F32) for i in range(4)]
            for l in range(2):
                for c in range(8):
                    S.dma("pool", aw[:, c, :], self.ada_w[l, c * P:(c + 1) * P, :], writes=[("aw", c)])
                S.dma("sp", ab[:], self.ada_b[l:l + 1, :].broadcast_to([2, 6 * D]), writes=["ab"])
                for n in range(12):
                    ps = pss[n % 4]
                    k = ("pm", n % 4)
                    for c in range(8):
                        mm(S, ps[:], cvb[:, c, :], aw[:, c, n * 512:(n + 1) * 512], c == 0, c == 7,
                           reads=["cvb", ("aw", c)], writes=[k])
                    S.op("dve", lambda e: e.tensor_tensor(out=mrow[:, n * 512:(n + 1) * 512], in0=ps[:],
                                                          in1=ab[:, n * 512:(n + 1) * 512], op=ALU.add),
                         reads=[k, "ab"], writes=["mrow"])
                S.dma("sp", self.modd[l], mrow[:], reads=["mrow"], writes=[("modd", l)])

    def load_layer_vecs(self, l):
        S = self.S
        V = {}
        for s in range(2):
            for which in range(2):
                V[("A", which, s)] = (S.sb(f"A{which}_{s}", [P, 8], F32), f"A{which}_{s}")
                V[("B", which, s)] = (S.sb(f"B{which}_{s}", [P, 8], F32), f"B{which}_{s}")
                if not (l == 1 and s == 1):
                    V[("G", which, s)] = (S.sb(f"GG{which}_{s}", [P, D], F32), f"GG{which}_{s}")
        with S.scope(), self.nc.allow_non_contiguous_dma(reason="tiny per-feature vectors"):
            tmp = S.sb("lv_tmp", [P, 8], F32)
            rowt = S.sb("lv_row", [P, D], F32)
            for s in range(2):
                for which, (ish, isc, ig) in enumerate(((0, 1, 0), (3, 4, 2))):
                    A, ka = V[("A", which, s)]
                    B, kb = V[("B", which, s)]
                    S.dma("sp", B[:], self.modd[l, s, ish * D:(ish + 1) * D].rearrange("(c p) -> p c", p=P),
                          reads=[("modd", l)], writes=[kb])
                    S.dma("sp", A[:], self.modd[l, s, isc * D:(isc + 1) * D].rearrange("(c p) -> p c", p=P),
                          reads=[("modd", l)], writes=[ka])
                    S.dma("sp", tmp[:], self.norm_g[l, ig, :].rearrange("(c p) -> p c", p=P), writes=["lv_tmp"])
                    S.op("dve", lambda e: e.scalar_tensor_tensor(out=A[:], in0=A[:], scalar=1.0, in1=tmp[:],
                                                                 op0=ALU.add, op1=ALU.mult),
                         reads=[ka, "lv_tmp"], writes=[ka])
                for which, (igt, ig) in enumerate(((2, 1), (5, 3))):
                    if l == 1 and s == 1:
                        continue
                    G, kg = V[("G", which, s)]
                    S.dma("sp", G[:], self.modd[l, s:s + 1, igt * D:(igt + 1) * D].broadcast_to([P, D]),
                          reads=[("modd", l)], writes=[kg])
                    S.dma("sp", rowt[:], self.norm_g[l, ig:ig + 1, :].broadcast_to([P, D]), writes=["lv_row"])
                    S.op("dve", lambda e: e.tensor_tensor(out=G[:], in0=G[:], in1=rowt[:], op=ALU.mult),
                         reads=[kg, "lv_row"], writes=[kg])
        return V

    def make_norm_bufs(self, tag, nb=2):
        S = self.S
        B = {"i": 0, "nb": nb, "tag": tag}
        B["sq"] = [S.sb(f"{tag}_sq{i}", [P, D], BF16) for i in range(1)] * nb
        B["st"] = [S.sb(f"{tag}_st{i}", [P, 4], F32) for i in range(nb)]
        B["xn"] = [S.sb(f"{tag}_xn{i}", [P, D], BF16) for i in range(nb)]
        B["tp"] = [S.ps(f"{tag}_tp{i}", [P, 8, P], BF16) for i in range(nb)]
        B["tm"] = [S.sb(f"{tag}_tm{i}", [P, 8, P], F32) for i in range(nb)]
        return B

    def norm_to_hT(self, B, x_sb, xkey, A, B_, out_ap, out_key, out2_ap=None, out2_key=None):
        S = self.S
        i = B["i"] % B["nb"]
        B["i"] += 1
        tag = B["tag"]
        sq, st, xn, tp, tm = B["sq"][i], B["st"][i], B["xn"][i], B["tp"][i], B["tm"][i]
        ksq, kst, kxn, ktp, ktm = [(tag, n, i) for n in ("sq", "st", "xn", "tp", "tm")]
        ksq = (tag, "sq", 0)
        S.op("pool", lambda e: e.memset(st[:], 0.0), writes=[kst])
        S.op("act", lambda e: e.activation(out=sq[:], in_=x_sb, func=AF.Square, accum_out=st[:, 0:1]),
             reads=[xkey], writes=[ksq, kst])
        S.op("act", lambda e: e.activation(out=st[:, 1:2], in_=st[:, 0:1], func=AF.Sqrt, scale=1.0 / D,
                                           bias=self.eps_t[:, 0:1]), reads=[kst, "eps_t"], writes=[kst])
        S.op("dve", lambda e: e.reciprocal(out=st[:, 2:3], in_=st[:, 1:2]), reads=[kst], writes=[kst])
        S.op("dve", lambda e: e.tensor_scalar(out=xn[:], in0=x_sb, scalar1=st[:, 2:3], scalar2=None, op0=ALU.mult),
             reads=[xkey, kst], writes=[kxn])
        for c in range(8):
            S.op("pe", lambda e: e.transpose(out=tp[:, c, :], in_=xn[:, c * P:(c + 1) * P], identity=self.idb[:]),
                 reads=[kxn, "idb"], writes=[ktp], accum=(c > 0))
        Aap = A[0][:, :, None].broadcast_to([P, 8, P])
        Bap = B_[0][:, :, None].broadcast_to([P, 8, P])
        S.op("dve", lambda e: e.tensor_tensor(out=tm[:], in0=tp[:], in1=Aap, op=ALU.mult),
             reads=[ktp, A[1]], writes=[ktm])
        S.op("pool", lambda e: e.tensor_tensor(out=out_ap, in0=tm[:], in1=Bap, op=ALU.add),
             reads=[ktm, B_[1]], writes=[out_key])
        if out2_ap is not None:
            S.op("pool", lambda e: e.tensor_tensor(out=out2_ap, in0=tm[:], in1=Bap, op=ALU.add),
                 reads=[ktm, B_[1]], writes=[out2_key])

    def x_src(self, layer, T):
        if layer == 0:
            if T < 2:
                return self.ctx_in[T * P:(T + 1) * P, :], None
            return self.x_in[(T - 2) * P:(T - 1) * P, :], None
        return self.x_d[T * P:(T + 1) * P, :], ("x_d", T)

    def phase_L0_proj(self, V):
        S = self.S
        with S.scope():
            w = S.sb("w_in", [P, 8, 2 * D], BF16)
            for c in range(8):
                S.dma("pool", w[:, c, :], self.ev_w_in[c * P:(c + 1) * P, :], writes=[("w_in", c)])
            wk = [("w_in", c) for c in range(8)]
            NB = self.make_norm_bufs("n0")
            xt = [S.sb(f"xt{i}", [P, D], F32) for i in range(2)]
            hT = [S.sb(f"hT{i}", [P, 8, 512], BF16) for i in range(2)]
            ptok = [S.ps(f"ptok{i}", [P, 512], F32) for i in range(2)]
            pft = [S.ps(f"pft{i}", [P, 512], F32) for i in range(2)]
            otok = [S.sb(f"otok{i}", [P, 512], BF16) for i in range(2)]
            oft = [S.sb(f"oft{i}", [P, 512], BF16) for i in range(2)]
            supers = [(0, 2)] + [(2 + 4 * i, 4) for i in range(8)]
            cnt = 0
            ctok = 0
            cft = 0
            for si, (T0, nt) in enumerate(supers):
                hb = hT[si % 2]
                hk = ("hT", si % 2)
                s = 1 if T0 < 2 else 0
                for t in range(nt):
                    T = T0 + t
                    xb = xt[cnt % 2]
                    xk = ("xt", cnt % 2)
                    cnt += 1
                    src, sk = self.x_src(0, T)
                    S.dma("sp", xb[:], src, reads=[sk] if sk else [], writes=[xk])
                    self.norm_to_hT(NB, xb[:], xk, V[("A", 0, s)], V[("B", 0, s)],
                                    hb[:, :, t * P:(t + 1) * P], (hk, t))
                n = nt * P
                hks = [(hk, t) for t in range(nt)]
                for t in range(nt):
                    T = T0 + t
                    for (c0, dst, dk) in ((0, self.upool_d, "upool"), (1536, self.v_d, "v")):
                        ps = ptok[ctok % 2]; pk = ("ptok", ctok % 2)
                        ob = otok[ctok % 2]; ok = ("otok", ctok % 2)
                        ctok += 1
                        for c in range(8):
                            mm(S, ps[:], hb[:, c, t * P:(t + 1) * P], w[:, c, c0:c0 + 512], c == 0, c == 7,
                               reads=[(hk, t), wk[c]], writes=[pk])
                        S.op("act", lambda e: e.activation(out=ob[:], in_=ps[:], func=AF.Copy), reads=[pk], writes=[ok])
                        S.dma("sp", dst[T * P:(T + 1) * P, :], ob[:], reads=[ok], writes=[(dk, T)])
                for jb in range(8):
                    c0 = 512 + jb * P
                    ps = pft[cft % 2]; pk = ("pft", cft % 2)
                    ob = oft[cft % 2]; ok = ("oft", cft % 2)
                    cft += 1
                    for c in range(8):
                        mm(S, ps[:, :n], w[:, c, c0:c0 + P], hb[:, c, :n], c == 0, c == 7,
                           reads=hks + [wk[c]], writes=[pk])
                    sc = 0.125 if jb < 4 else 1.0
                    S.op("act", lambda e: e.activation(out=ob[:, :n], in_=ps[:, :n], func=AF.Copy, scale=sc),
                         reads=[pk], writes=[ok])
                    dst = self.qT_d if jb < 4 else self.kT_d
                    dk = "qT" if jb < 4 else "kT"
                    S.dma("sp", dst[jb % 4, :, T0 * P:T0 * P + n], ob[:, :n], reads=[ok],
                          writes=[(dk, jb % 4, T0 + t) for t in range(nt)])

    def phase_L0_pool(self):
        S = self.S
        with S.scope():
            up = S.sb("up_all", [P, NT, 512], BF16)
            for q in range(0, NT, 2):
                S.dma("sp", up[:, q:q + 2, :], self.upool_d[q * P:(q + 2) * P, :].rearrange("(n p) f -> p n f", p=P),
                      reads=[("upool", q), ("upool", q + 1)], writes=[("up", q), ("up", q + 1)])
            bd = S.sb("bands", [P, 20, P], BF16)
            S.dma("pool", bd[:], self.bands, writes=["bands"])
            pw = S.sb("pool_w", [P, 4, P], BF16)
            S.dma("pool", pw[:], self.ev_pool_w.rearrange("g c o -> c g o"), writes=["pool_w"])
            psc = S.sb("pool_sc", [P, 4], F32)
            with self.nc.allow_non_contiguous_dma(reason="tiny"):
                S.dma("sp", psc[:], self.ev_pool_scale.rearrange("(g p) -> p g", p=P), writes=["pool_sc"])
            pb = [S.ps(f"pb{i}", [P, 4, P], F32) for i in range(2)]
            pc = [S.ps(f"pc{i}", [P, 4, P], F32) for i in range(2)]
            pm = [S.sb(f"pmx{i}", [P, 4, P], BF16) for i in range(2)]
            zp = [S.sb(f"zp{i}", [P, 4, P], BF16) for i in range(2)]
            it = 0
            for (T0, n) in ((0, 2), (2, 32)):
                for i in range(n):
                    T = T0 + i
                    b = it % 2
                    it += 1
                    for g in range(4):
                        srcs = []
                        if i > 0:
                            srcs.append((T - 1, 0))
                        cv = 3 if i == 0 else (4 if i == n - 1 else 2)
                        srcs.append((T, cv))
                        if i < n - 1:
                            srcs.append((T + 1, 1))
                        for si, (Ts, v) in enumerate(srcs):
                            mm(S, pb[b][:, g, :], up[:, Ts, g * P:(g + 1) * P], bd[:, g * 5 + v, :],
                               si == 0, si == len(srcs) - 1, reads=[("up", Ts), "bands"], writes=[("pb", b)])
                    S.op("dve", lambda e: e.tensor_copy(out=pm[b][:], in_=pb[b][:]), reads=[("pb", b)], writes=[("pmx", b)])
                    for g in range(4):
                        mm(S, pc[b][:, g, :], pw[:, g, :], pm[b][:, g, :], True, True,
                           reads=["pool_w", ("pmx", b)], writes=[("pc", b)])
                    S.op("dve", lambda e: e.tensor_tensor(out=zp[b][:], in0=pc[b][:],
                                                          in1=psc[:, :, None].broadcast_to([P, 4, P]), op=ALU.mult),
                         reads=[("pc", b), "pool_sc"], writes=[("zp", b)])
                    S.dma("sp", self.zT_d[0:4, :, T * P:(T + 1) * P].rearrange("c p t -> p c t"), zp[b][:],
                          reads=[("zp", b)], writes=[("zT", c, T) for c in range(4)])

    def phase_L0_attn(self):
        S = self.S
        with S.scope():
            kT = S.sb("kT_all", [P, 4, NTOK], BF16)
            qT = S.sb("qT_all", [P, 4, NTOK], BF16)
            va = S.sb("v_all", [P, NT, 512], BF16)
            for j in range(4):
                S.dma("sp", kT[:, j, :], self.kT_d[j], reads=[("kT", j, T) for T in range(NT)], writes=[("kTa", j)])
                S.dma("sp", qT[:, j, :], self.qT_d[j], reads=[("qT", j, T) for T in range(NT)], writes=[("qTa", j)])
            for q in range(0, NT, 2):
                S.dma("sp", va[:, q:q + 2, :], self.v_d[q * P:(q + 2) * P, :].rearrange("(n p) f -> p n f", p=P),
                      reads=[("v", q), ("v", q + 1)], writes=[("va", q), ("va", q + 1)])
            E = S.sb("Etab", [P, 96, P], BF16)
            with S.scope():
                rt = S.sb("rt", [P, 96, P], F32)
                mk = S.sb("mk", [P, 12, P], F32)
                S.dma("sp", rt[:], self.rpb_tab, writes=["rt"])
                S.dma("sp", mk[:], self.msk_tab, writes=["mk"])
                S.op("act", lambda e: e.activation(out=rt[:], in_=rt[:], func=AF.Exp), reads=["rt"], writes=["rt"])
                for h in range(8):
                    S.op("dve", lambda e: e.tensor_tensor(out=E[:, h * 12:(h + 1) * 12, :], in0=rt[:, h * 12:(h + 1) * 12, :],
                                                          in1=mk[:], op=ALU.mult), reads=["rt", "mk"], writes=["Etab"])
            pss = [[S.ps(f"pss{i}_{k}", [P, 512], F32) for k in range(2)] for i in range(2)]
            pso = [S.ps(f"pso{i}", [P, 2, P], F32) for i in range(2)]
            pex = [S.sb(f"pex{i}", [P, 7, P], BF16) for i in range(2)]
            pT = [S.sb(f"pT{i}", [P, 5, P], BF16) for i in range(2)]
            rc = [S.sb(f"rc{i}", [P, P], F32) for i in range(2)]
            zo = [S.sb(f"zo{i}", [P, P], BF16) for i in range(2)]
            it = 0
            izo = 0
            for T in range(NT):
                if T < 2:
                    chunks = [(0, None), (1, None)]
                else:
                    i = T - 2
                    if 2 <= i <= 29:
                        lat = [(T + d, v) for v, d in enumerate((-2, -1, 0, 1, 2))]
                    elif i == 0:
                        lat = [(T + d, 8 + d) for d in (0, 1, 2, 3)]
                    elif i == 1:
                        lat = [(T + d, 8 + d) for d in (-1, 0, 1, 2)]
                    elif i == 30:
                        lat = [(T + d, 8 + d) for d in (-2, -1, 0, 1)]
                    else:
                        lat = [(T + d, 8 + d) for d in (-3, -2, -1, 0)]
                    chunks = [(0, None), (1, None)] + lat
                nk = len(chunks)
                nlat = nk - 2
                for j in range(4):
                    zb = zo[izo % 2]; zk = ("zo", izo % 2)
                    izo += 1
                    for hh in range(2):
                        h = 2 * j + hh
                        pb_ = hh * 64
                        b = it % 2
                        it += 1
                        for ci, (Tk, v) in enumerate(chunks):
                            bank = pss[b][ci // 4]
                            mm(S, bank[:, (ci % 4) * P:(ci % 4 + 1) * P],
                               kT[pb_:pb_ + 64, j, Tk * P:(Tk + 1) * P], qT[pb_:pb_ + 64, j, T * P:(T + 1) * P],
                               True, True, reads=[("kTa", j), ("qTa", j)], writes=[("pss", b, ci // 4)])
                        n0 = min(nk, 4)
                        S.op("act", lambda e: e.activation(out=pex[b][:, 0:n0, :], in_=pss[b][0][:, 0:n0 * P].rearrange("p (c q) -> p c q", q=P), func=AF.Exp),
                             reads=[("pss", b, 0)], writes=[("pex", b)])
                        if nk > 4:
                            S.op("act", lambda e: e.activation(out=pex[b][:, 4:nk, :], in_=pss[b][1][:, 0:(nk - 4) * P].rearrange("p (c q) -> p c q", q=P), func=AF.Exp),
                                 reads=[("pss", b, 1)], writes=[("pex", b)])
                        if nlat > 0:
                            v0 = chunks[2][1]
                            S.op("dve", lambda e: e.tensor_tensor(out=pT[b][:, 0:nlat, :], in0=pex[b][:, 2:nk, :],
                                                                  in1=E[:, h * 12 + v0:h * 12 + v0 + nlat, :], op=ALU.mult),
                                 reads=[("pex", b), "Etab"], writes=[("pT", b)])
                        for ci, (Tk, v) in enumerate(chunks):
                            rhs = pex[b][:, ci, :] if v is None else pT[b][:, ci - 2, :]
                            rk = [("pex", b)] if v is None else [("pT", b)]
                            mm(S, pso[b][:, 0, :], va[:, Tk, j * P:(j + 1) * P], rhs, ci == 0, ci == nk - 1,
                               reads=[("va", Tk)] + rk, writes=[("pso", b)])
                        for ci, (Tk, v) in enumerate(chunks):
                            rhs = pex[b][:, ci, :] if v is None else pT[b][:, ci - 2, :]
                            rk = [("pex", b)] if v is None else [("pT", b)]
                            mm(S, pso[b][:, 1, :], self.ones_bf[:], rhs, ci == 0, ci == nk - 1,
                               reads=["ones_bf"] + rk, writes=[("pso", b)])
                        S.op("dve", lambda e: e.reciprocal(out=rc[b][pb_:pb_ + 64, :], in_=pso[b][pb_:pb_ + 64, 1, :]),
                             reads=[("pso", b)], writes=[("rc", b)])
                        S.op("dve", lambda e: e.tensor_tensor(out=zb[pb_:pb_ + 64, :], in0=pso[b][pb_:pb_ + 64, 0, :],
                                                              in1=rc[b][pb_:pb_ + 64, :], op=ALU.mult),
                             reads=[("pso", b), ("rc", b)], writes=[zk])
                    S.dma("sp", self.zT_d[4 + j, :, T * P:(T + 1) * P], zb[:], reads=[zk], writes=[("zT", 4 + j, T)])

    def phase_out_mlp(self, layer, V, w_out_ap, y_tile_fn, dst_fn):
        S = self.S
        with S.scope():
            wo = S.sb("wo", [P, 8, D], BF16)
            for c in range(8):
                S.dma("pool", wo[:, c, :], w_out_ap[c * P:(c + 1) * P, :], writes=[("wo", c)])
            w1 = S.sb("w1", [P, 8, 4 * D], BF16)
            w2 = S.sb("w2", [P, 32, D], BF16)
            for c in range(8):
                S.dma("pool", w1[:, c, :], self.mlp_w1[layer, c * P:(c + 1) * P, :], writes=[("w1", c)])
            for f in range(0, 32, 4):
                S.dma("pool", w2[:, f:f + 4, :], self.mlp_w2[layer, f * P:(f + 4) * P, :].rearrange("(n p) d -> p n d", p=P),
                      writes=[("w2", f + q) for q in range(4)])
            NB = self.make_norm_bufs("nm", nb=1)
            zt = [S.sb(f"zt{i}", [P, 8, P], BF16) for i in range(2)]
            xt = [S.sb(f"xo{i}", [P, D], F32) for i in range(2)]
            x1 = [S.sb(f"x1_{i}", [P, D], F32) for i in range(2)]
            tmp = [S.sb(f"tg{i}", [P, D], F32) for i in range(2)]
            sq = NB["sq"][0]
            stt = [S.sb(f"ost{i}", [P, 4], F32) for i in range(4)]
            hT = [S.sb(f"hm{i}", [P, 8, 256], BF16) for i in range(1)] * 2
            py = [[S.ps(f"py{t}_{hf}", [P, 512], F32) for hf in range(2)] for t in range(2)]
            pa = [S.ps(f"pa{i}", [P, 256], F32) for i in range(2)]
            r32 = [S.sb(f"r32_{i}", [P, 256], F32) for i in range(2)]
            aT = [S.sb(f"aT{i}", [P, 256], BF16) for i in range(2)]
            ist = 0

            def norm_gate_res(t, G, xin, xin_key, xout, xout_key):
                nonlocal ist
                st = stt[ist % 4]; sk = ("ost", ist % 4)
                ist += 1
                S.op("pool", lambda e: e.memset(st[:], 0.0), writes=[sk])
                for hf in range(2):
                    S.op("act", lambda e: e.activation(out=sq[:, hf * 512:(hf + 1) * 512], in_=py[t][hf][:], func=AF.Square,
                                                       accum_out=st[:, hf:hf + 1]), reads=[("py", t, hf)], writes=[("nm", "sq", 0), sk])
                S.op("dve", lambda e: e.tensor_tensor(out=st[:, 2:3], in0=st[:, 0:1], in1=st[:, 1:2], op=ALU.add),
                     reads=[sk], writes=[sk])
                S.op("act", lambda e: e.activation(out=st[:, 2:3], in_=st[:, 2:3], func=AF.Sqrt, scale=1.0 / D,
                                                   bias=self.eps_t[:, 0:1]), reads=[sk, "eps_t"], writes=[sk])
                S.op("dve", lambda e: e.reciprocal(out=st[:, 3:4], in_=st[:, 2:3]), reads=[sk], writes=[sk])
                tb = tmp[t]; tk = ("tg", t)
                for hf in range(2):
                    S.op("dve", lambda e: e.scalar_tensor_tensor(out=tb[:, hf * 512:(hf + 1) * 512], in0=py[t][hf][:],
                                                                 scalar=st[:, 3:4], in1=G[0][:, hf * 512:(hf + 1) * 512],
                                                                 op0=ALU.mult, op1=ALU.mult),
                         reads=[("py", t, hf), sk, G[1]], writes=[tk])
                S.op("pool", lambda e: e.tensor_tensor(out=xout, in0=tb[:], in1=xin, op=ALU.add),
                     reads=[tk, xin_key], writes=[xout_key])

            ia = 0
            for sidx in range(NT // 2):
                T0 = 2 * sidx
                s = 1 if T0 < 2 else 0
                if layer == 1 and s == 1:
                    continue
                hb = hT[0]; hk = ("hm", 0)
                for t in range(2):
                    T = T0 + t
                    src, skey = self.x_src(layer, T)
                    S.dma("sp", xt[t][:], src, reads=[skey] if skey else [], writes=[("xo", t)])
                    y_tile_fn(T, t, zt[t], ("zt", t), wo, py[t])
                    norm_gate_res(t, V[("G", 0, s)], xt[t][:], ("xo", t), x1[t][:], ("x1", t))
                    self.norm_to_hT(NB, x1[t][:], ("x1", t), V[("A", 1, s)], V[("B", 1, s)],
                                    hb[:, :, t * P:(t + 1) * P], (hk, t))
                def mm1(f):
                    a = (ia + f) % 2
                    for c in range(8):
                        mm(S, pa[a][:], w1[:, c, f * P:(f + 1) * P], hb[:, c, :], c == 0, c == 7,
                           reads=[("w1", c), (hk, 0), (hk, 1)], writes=[("pa", a)])
                    S.op("act", lambda e: e.activation(out=r32[a][:], in_=pa[a][:], func=AF.Relu),
                         reads=[("pa", a)], writes=[("r32", a)])
                    S.op("dve", lambda e: e.tensor_tensor(out=aT[a][:], in0=r32[a][:], in1=r32[a][:], op=ALU.mult),
                         reads=[("r32", a)], writes=[("aT", a)])

                def mm2(f):
                    a = (ia + f) % 2
                    for t in range(2):
                        for hf in range(2):
                            mm(S, py[t][hf][:], aT[a][:, t * P:(t + 1) * P], w2[:, f, hf * 512:(hf + 1) * 512],
                               f == 0, f == 31, reads=[("aT", a), ("w2", f)], writes=[("py", t, hf)])
                mm1(0)
                for f in range(32):
                    if f + 1 < 32:
                        mm1(f + 1)
                    mm2(f)
                for t in range(2):
                    T = T0 + t
                    norm_gate_res(t, V[("G", 1, s)], x1[t][:], ("x1", t), tmp[t][:], ("tg", t))
                    dst, dkey = dst_fn(T)
                    S.dma("sp", dst, tmp[t][:], reads=[("tg", t)], writes=[dkey])

    def y_tile_L0(self, T, t, zt, zk, wo, py):
        S = self.S
        S.dma("sp", zt[:], self.zT_d[:, :, T * P:(T + 1) * P].rearrange("c p t -> p c t"),
              reads=[("zT", c, T) for c in range(8)], writes=[zk])
        for hf in range(2):
            for c in range(8):
                mm(S, py[hf][:], zt[:, c, :], wo[:, c, hf * 512:(hf + 1) * 512], c == 0, c == 7,
                   reads=[zk, ("wo", c)], writes=[("py", t, hf)])


def build_program(stop_after=None, debug=()):
    nc = bass.Bass("TRN2", target_bir_lowering=False)
    Pg = Prog(nc, debug)
    S = Pg.S
    Pg.consts()
    Pg.phase_mod()
    final_keys = []
    with S.scope():
        V0 = Pg.load_layer_vecs(0)
        Pg.phase_L0_proj(V0)
        Pg.phase_L0_pool()
        Pg.phase_L0_attn()

        def dst0(T):
            if stop_after == "L0":
                if T < 2:
                    return Pg.x_d[T * P:(T + 1) * P, :], ("x_d", T)
                return Pg.out[(T - 2) * P:(T - 1) * P, :], ("out", T)
            return Pg.x_d[T * P:(T + 1) * P, :], ("x_d", T)
        Pg.phase_out_mlp(0, V0, Pg.ev_w_out, Pg.y_tile_L0, dst0)
    S.barrier()
    S.finish([])
    S.close()
    return nc, Pg


def host_inputs(inputs):
    f = lambda a: np.ascontiguousarray(np.asarray(a, dtype=np.float32))
    dr_idx, dc_full, mask = _attn_tables()
    rpb = f(inputs["ev_rpb"])[0]
    tab = rpb[:, dr_idx, dc_full[None, :, :]]
    tab = np.ascontiguousarray(tab.transpose(2, 0, 1, 3).reshape(128, 96, 128))
    msk = np.ascontiguousarray(mask.transpose(1, 0, 2))
    bands = np.ascontiguousarray(_pool_bands().transpose(2, 0, 1, 3).reshape(128, 20, 128))
    shared = {
        "ada_w": f(inputs["ada_w"]), "ada_b": f(inputs["ada_b"]), "norm_g": f(inputs["norm_g"]),
        "mlp_w1": f(inputs["mlp_w1"]), "mlp_w2": f(inputs["mlp_w2"]),
        "ev_w_in": f(inputs["ev_w_in"])[0], "ev_w_out": f(inputs["ev_w_out"])[0],
        "ev_pool_w": f(inputs["ev_pool_w"])[0], "ev_pool_scale": f(inputs["ev_pool_scale"])[0],
        "rpb_tab": tab, "msk_tab": msk, "bands": bands, "ident": np.eye(128, dtype=np.float32),
    }
    x = f(inputs["x"]); c = f(inputs["c"]); ctx = f(inputs["ctx"]); cc = f(inputs["c_ctx"])
    maps = []
    for b in range(x.shape[0]):
        cv = np.stack([c[b].reshape(8, 128).T, cc.reshape(8, 128).T], axis=-1)
        m = dict(shared)
        m.update({"x": x[b], "ctx": ctx[b], "cvec": np.ascontiguousarray(cv)})
        maps.append(m)
    return maps


_CACHE = {}


def kernel(**inputs):
    maps = host_inputs(inputs)
    if "nc" not in _CACHE:
        _CACHE["nc"] = build_program()
    nc, Pg = _CACHE["nc"]
    res = run_bass_kernel_spmd(nc, maps, core_ids=list(range(8)))
    return np.stack([np.asarray(r["out"]) for r in res.results], axis=0)

LWC = -0.6065306597126334
GN_EPS = 64e-5


def _scan_consts():
    s = np.arange(128)[:, None]
    t = np.arange(128)[None, :]
    tri = np.stack([(s <= t), (s >= t)]).astype(np.float32)
    strict = np.stack([(s < t), (s > t)]).astype(np.float32)
    mT = strict.transpose(0, 2, 1)
    m4 = np.concatenate([tri, strict, tri, mT], axis=2)
    lm = []
    for l in range(7):
        b = 1 << l
        lm.append(((s // (2 * b)) == (t // (2 * b))) & (((s // b) % 2) == 0) & (((t // b) % 2) == 1))
    lm = np.stack(lm).astype(np.float32)
    lmT = np.stack([lm.transpose(0, 2, 1), lm])
    return tri, m4, np.ascontiguousarray(lmT)


def _tt(S, eng, out, a, b, op, reads, writes):
    return S.op(eng, lambda e: e.tensor_tensor(out=out, in0=a, in1=b, op=op), reads=reads, writes=writes)


def _stt(S, eng, out, a, sc, b, op0, op1, reads, writes):
    return S.op("dve", lambda e: e.scalar_tensor_tensor(out=out, in0=a, scalar=sc, in1=b, op0=op0, op1=op1),
                reads=reads, writes=writes)


def _act(S, out, in_, func, reads, writes, **kw):
    return S.op("act", lambda e: e.activation(out=out, in_=in_, func=func, **kw), reads=reads, writes=writes)


def _h3(ap):
    return ap.rearrange("p (h k) -> p h k", k=64)


class Prog1(Prog):
    def __init__(self, nc, debug=()):
        super().__init__(nc, debug)
        dt = nc.dram_tensor
        I = lambda name, shape: dt(name, list(shape), F32, kind="ExternalInput").ap()
        self.rw_mu = I("rw_mu", [6, D])
        self.rw_wr = I("rw_wr", [D, D]); self.rw_wk = I("rw_wk", [D, D])
        self.rw_wv = I("rw_wv", [D, D]); self.rw_wo = I("rw_wo", [D, D])
        self.rw_w0 = I("rw_w0", [2, D]); self.rw_a0 = I("rw_a0", [2, D])
        self.w1cat = I("w1cat", [D, P]); self.a1cat = I("a1cat", [D, P]); self.rw_g1 = I("rw_g1", [D, P])
        self.w2cat = I("w2cat", [P, D]); self.a2cat = I("a2cat", [P, D]); self.rw_g2 = I("rw_g2", [P, D])
        self.rw_kk = I("rw_kk", [1, D]); self.rw_ka = I("rw_ka", [1, D]); self.rw_rk = I("rw_rk", [1, D])
        self.rw_lng = I("rw_lng", [1, D]); self.rw_lnb = I("rw_lnb", [1, D])
        self.tri_c = I("tri_c", [2, P, P]); self.m4_c = I("m4_c", [2, P, 512]); self.lmT_c = I("lmT_c", [2, 7, P, P])
        X = lambda name, shape, d=F32: (dt(name, list(shape), d, kind="ExternalOutput").ap() if name in debug
                                        else dt(name, list(shape), d).ap())
        self.hT_d = X("hT_d", [8, P, NTOK])
        self.featT_d = X("featT_d", [2, NT, P, 8 * 4 * P], BF16)
        self.vtok_d = X("vtok_d", [NTOK, D], BF16)
        self.bk_d = X("bk_d", [2, NT, P, 2 * D], BF16)
        self.gC_d = X("gC_d", [2, NT, P, 8])
        self.g_d = X("g_d", [NTOK, D])
        self.bonus_d = X("bonus_d", [NTOK, D])
        self.y_d = X("y_d", [2, NTOK, D])

    def phase_R0(self, V):
        S = self.S
        with S.scope():
            NB = self.make_norm_bufs("r0")
            xt = [S.sb(f"r0x{i}", [P, D], F32) for i in range(2)]
            ho = [S.sb(f"r0h{i}", [P, 8, P], F32) for i in range(2)]
            for T in range(NT):
                s = 1 if T < 2 else 0
                b = T % 2
                src, sk = self.x_src(1, T)
                S.dma("sp", xt[b][:], src, reads=[sk], writes=[("r0x", b)])
                self.norm_to_hT(NB, xt[b][:], ("r0x", b), V[("A", 0, s)], V[("B", 0, s)], ho[b][:], ("r0h", b))
                S.dma("sp", self.hT_d[:, :, T * P:(T + 1) * P].rearrange("c p t -> p c t"), ho[b][:],
                      reads=[("r0h", b)], writes=[("hT_d", T)])

    def phase_R1(self):
        S = self.S
        with S.scope():
            W = {}
            for nm, src in (("wr", self.rw_wr), ("wk", self.rw_wk), ("wv", self.rw_wv)):
                W[nm] = S.sb(nm, [P, 8, D], BF16)
                for c in range(0, 8, 4):
                    S.dma("pool", W[nm][:, c:c + 4, :], src[c * P:(c + 4) * P, :].rearrange("(c p) n -> p c n", p=P), writes=[nm])
            for nm, src in (("w1c", self.w1cat), ("a1c", self.a1cat), ("g1", self.rw_g1)):
                W[nm] = S.sb(nm, [P, 8, P], BF16)
                S.dma("pool", W[nm][:], src.rearrange("(c p) n -> p c n", p=P), writes=[nm])
            for nm, src in (("w2c", self.w2cat), ("a2c", self.a2cat), ("g2", self.rw_g2)):
                W[nm] = S.sb(nm, [P, D], BF16)
                S.dma("pool", W[nm][:], src, writes=[nm])
            R = {}
            for nm, src in (("kk_r", self.rw_kk), ("ka_r", self.rw_ka), ("rk_r", self.rw_rk),
                            ("w0_0", self.rw_w0[0:1, :]), ("w0_1", self.rw_w0[1:2, :]),
                            ("a0_0", self.rw_a0[0:1, :]), ("a0_1", self.rw_a0[1:2, :])):
                R[nm] = S.sb(nm, [P, D], F32)
                S.dma("sp", R[nm][:], src.broadcast_to([P, D]), writes=[nm])
            mu = S.sb("mu", [P, 6, 8], F32)
            with self.nc.allow_non_contiguous_dma(reason="tiny"):
                S.dma("sp", mu[:], self.rw_mu.rearrange("j (c p) -> p j c", p=P), writes=["mu"])
            tri = S.sb("tri", [P, 2, P], F32)
            S.dma("sp", tri[:], self.tri_c.rearrange("d s t -> s d t"), writes=["tri"])
            onef = S.sb("onef", [P, P], F32)
            S.op("dve", lambda e: e.memset(onef[:], 1.0), writes=["onef"])
            hbuf = S.sb("hbuf", [P, 8, P + 2], F32)
            xx = S.sb("xx", [P, 8, P], F32)
            mxt = S.sb("mxt", [P, 8, P], F32)
            mix = S.sb("mix", [P, 6, 8, P], BF16)
            hid = S.sb("hid", [P, 3, P], BF16)
            F = {n: S.sb(n, [P, D], F32) for n in ("r_sb", "k_sb", "v_sb", "kkn", "tA", "tB", "lw", "tC", "tD", "kd0", "kd1", "tE", "tF", "tG", "tH")}
            ob = [S.sb(f"ob{i}", [P, D], BF16) for i in range(4)]
            vb = S.sb("vb", [P, D], BF16)
            ft = S.sb("ft", [P, 8, 4, P], BF16)
            bkt = S.sb("bkt", [P, 2, D], BF16)
            st16 = S.sb("st16", [P, 64], F32)
            gcs = S.sb("gcs", [P, 8], F32)
            pA = [[S.ps(f"pA{i}_{h}", [P, 512], F32) for h in range(2)] for i in range(2)]
            pCl = [S.ps(f"pCl{h}", [P, 512], F32) for h in range(2)]
            pF = S.ps("pF", [P, 512], F32)
            pT = S.ps("pT", [P, 8, P], BF16)
            ipa = 0

            def proj(lhs_fn, rhs, rkey, K0=0, K=P, nchunks=8, lkeys=()):
                nonlocal ipa
                i = ipa % 2
                ipa += 1
                for hf in range(2):
                    for c in range(nchunks):
                        mm(S, pA[i][hf][:], lhs_fn(c), rhs(c, hf), c == 0, c == nchunks - 1,
                           reads=list(lkeys) + [rkey], writes=[("pA", i, hf)])
                return pA[i], [("pA", i, 0), ("pA", i, 1)]

            def evac2(fn_half):
                for hf in range(2):
                    fn_half(hf, slice(hf * 512, (hf + 1) * 512))

            for T in range(NT):
                seq_lo, seq_hi = (0, NCTX) if T < 2 else (NCTX, NTOK)
                t0 = T * P
                lo = max(t0 - 1, seq_lo); hi = min(t0 + P + 1, seq_hi)
                if lo > t0 - 1:
                    S.op("pool", lambda e: e.memset(hbuf[:, :, 0:1], 0.0), writes=["hbuf"])
                if hi < t0 + P + 1:
                    S.op("pool", lambda e: e.memset(hbuf[:, :, P + 1:P + 2], 0.0), writes=["hbuf"])
                S.dma("sp", hbuf[:, :, lo - (t0 - 1):hi - (t0 - 1)], self.hT_d[:, :, lo:hi].rearrange("c p t -> p c t"),
                      reads=[("hT_d", q) for q in range(max(T - 1, 0), min(T + 2, NT))], writes=["hbuf"])
                _tt(S, "dve", xx[:], hbuf[:, :, 0:P], hbuf[:, :, 2:P + 2], ALU.add, ["hbuf"], ["xx"])
                _stt(S, "dve", xx[:], xx[:], 0.5, hbuf[:, :, 1:P + 1], ALU.mult, ALU.subtract, ["xx", "hbuf"], ["xx"])
                for j in range(6):
                    _tt(S, "dve", mxt[:], xx[:], mu[:, j, :][:, :, None].broadcast_to([P, 8, P]), ALU.mult, ["xx", "mu"], ["mxt"])
                    _tt(S, "dve", mix[:, j, :, :], mxt[:], hbuf[:, :, 1:P + 1], ALU.add, ["mxt", "hbuf"], [("mix", j)])
                for hi_, (wn, mj, fn) in enumerate((("w1c", 1, AF.Tanh), ("a1c", 4, AF.Copy), ("g1", 5, AF.Sigmoid))):
                    for c in range(8):
                        mm(S, pF[:, 0:P], W[wn][:, c, :], mix[:, mj, c, :], c == 0, c == 7,
                           reads=[wn, ("mix", mj)], writes=["pF"])
                    _act(S, hid[:, hi_, :], pF[:, 0:P], fn, ["pF"], [("hid", hi_)])
                for nm, mj, wn in (("r_sb", 0, "wr"), ("k_sb", 2, "wk"), ("v_sb", 3, "wv")):
                    ps, pk = proj(lambda c: mix[:, mj, c, :], lambda c, hf: W[wn][:, c, hf * 512:(hf + 1) * 512], wn,
                                  lkeys=[("mix", mj)])
                    evac2(lambda hf, sl: _act(S, F[nm][:, sl], ps[hf][:], AF.Copy, [pk[hf]], [nm]))
                S.op("pool", lambda e: e.tensor_copy(out=vb[:], in_=F["v_sb"][:]), reads=["v_sb"], writes=["vb"])
                S.dma("sp", self.vtok_d[t0:t0 + P, :], vb[:], reads=["vb"], writes=[("vtok", T)])
                ps, pk = proj(lambda c: hid[:, 2, :], lambda c, hf: W["g2"][:, hf * 512:(hf + 1) * 512], "g2", nchunks=1,
                              lkeys=[("hid", 2)])
                evac2(lambda hf, sl: _act(S, F["tA"][:, sl], ps[hf][:], AF.Copy, [pk[hf]], ["tA"]))
                S.dma("sp", self.g_d[t0:t0 + P, :], F["tA"][:], reads=["tA"], writes=[("g_d", T)])
                _tt(S, "dve", F["tA"][:], F["k_sb"][:], R["kk_r"][:], ALU.mult, ["k_sb", "kk_r"], ["tA"])
                _tt(S, "pool", F["tB"][:], F["tA"][:], F["tA"][:], ALU.mult, ["tA"], ["tB"])
                S.op("dve", lambda e: e.tensor_reduce(out=st16[:, 0:16], in_=_h3(F["tB"][:]), axis=AX.X, op=ALU.add),
                     reads=["tB"], writes=["st16"])
                S.op("dve", lambda e: e.tensor_scalar(out=st16[:, 0:16], in0=st16[:, 0:16], scalar1=1e-24, scalar2=None, op0=ALU.max),
                     reads=["st16"], writes=["st16"])
                _act(S, st16[:, 0:16], st16[:, 0:16], AF.Sqrt, ["st16"], ["st16"])
                S.op("dve", lambda e: e.reciprocal(out=st16[:, 16:32], in_=st16[:, 0:16]), reads=["st16"], writes=["st16"])
                _tt(S, "dve", _h3(F["kkn"][:]), _h3(F["tA"][:]), st16[:, 16:32][:, :, None].broadcast_to([P, 16, 64]), ALU.mult,
                    ["tA", "st16"], ["kkn"])
                for d in range(2):
                    ps, pk = proj(lambda c: hid[d * 64:(d + 1) * 64, 0, :], lambda c, hf: W["w2c"][d * 64:(d + 1) * 64, hf * 512:(hf + 1) * 512],
                                  "w2c", nchunks=1, lkeys=[("hid", 0)])
                    evac2(lambda hf, sl: _tt(S, "dve", F["tB"][:, sl], ps[hf][:], R[f"w0_{d}"][:, sl], ALU.add, [pk[hf], f"w0_{d}"], ["tB"]))
                    _act(S, F["tB"][:], F["tB"][:], AF.Sigmoid, ["tB"], ["tB"])
                    _act(S, F["lw"][:], F["tB"][:], AF.Copy, ["tB"], ["lw"], scale=LWC)
                    ps, pk = proj(lambda c: hid[d * 64:(d + 1) * 64, 1, :], lambda c, hf: W["a2c"][d * 64:(d + 1) * 64, hf * 512:(hf + 1) * 512],
                                  "a2c", nchunks=1, lkeys=[("hid", 1)])
                    evac2(lambda hf, sl: _tt(S, "dve", F["tC"][:, sl], ps[hf][:], R[f"a0_{d}"][:, sl], ALU.add, [pk[hf], f"a0_{d}"], ["tC"]))
                    _act(S, F["tC"][:], F["tC"][:], AF.Sigmoid, ["tC"], ["tC"])
                    kd = F[f"kd{d}"]; kdk = f"kd{d}"
                    _stt(S, "dve", F["tD"][:], F["tC"][:], -1.0, R["ka_r"][:], ALU.add, ALU.mult, ["tC", "ka_r"], ["tD"])
                    _stt(S, "pool", kd[:], F["tD"][:], 1.0, F["k_sb"][:], ALU.add, ALU.mult, ["tD", "k_sb"], [kdk])
                    _tt(S, "pool", F["tC"][:], F["kkn"][:], F["tC"][:], ALU.mult, ["kkn", "tC"], ["tC"])
                    for hf in range(2):
                        mm(S, pCl[hf][:], tri[:, d, :], F["lw"][:, hf * 512:(hf + 1) * 512], True, True,
                           reads=["tri", "lw"], writes=[("pCl", hf)])
                    evac2(lambda hf, sl: _act(S, F["tE"][:, sl], pCl[hf][:], AF.Exp, [("pCl", hf)], ["tE"]))
                    evac2(lambda hf, sl: _act(S, F["tF"][:, sl], pCl[hf][:], AF.Exp, [("pCl", hf)], ["tF"], scale=-1.0))
                    for hf in range(2):
                        mm(S, pCl[hf][:], onef[:], F["lw"][:, hf * 512:(hf + 1) * 512], True, True,
                           reads=["onef", "lw"], writes=[("pCl", hf)])
                    evac2(lambda hf, sl: _act(S, F["tH"][:, sl], pCl[hf][:], AF.Exp, [("pCl", hf)], ["tH"]))
                    _act(S, F["tG"][:], F["lw"][:], AF.Exp, ["lw"], ["tG"], scale=-1.0)
                    _tt(S, "dve", F["tG"][:], F["tG"][:], F["tE"][:], ALU.mult, ["tG", "tE"], ["tG"])
                    _tt(S, "pool", F["tH"][:], F["tH"][:], F["tF"][:], ALU.mult, ["tH", "tF"], ["tH"])
                    for j in range(8):
                        mm(S, pF[:, 256 + j:257 + j], F["lw"][:, j * P:(j + 1) * P], onef[:, 0:1], True, True,
                           reads=["lw", "onef"], writes=["pF"])
                    _act(S, gcs[:], pF[:, 256:264], AF.Exp, ["pF"], ["gcs"])
                    S.dma("sp", self.gC_d[d, T], gcs[:], reads=["gcs"], writes=[("gC_d", d, T)])
                    _stt(S, "dve", ob[0][:], F["kkn"][:], -1.0, F["tG"][:], ALU.mult, ALU.mult, ["kkn", "tG"], [("ob", 0)])
                    _tt(S, "pool", ob[1][:], F["r_sb"][:], F["tE"][:], ALU.mult, ["r_sb", "tE"], [("ob", 1)])
                    _tt(S, "dve", ob[2][:], F["tC"][:], F["tF"][:], ALU.mult, ["tC", "tF"], [("ob", 2)])
                    _tt(S, "pool", ob[3][:], kd[:], F["tF"][:], ALU.mult, [kdk, "tF"], [("ob", 3)])
                    _tt(S, "dve", bkt[:, 0, :], F["tC"][:], F["tH"][:], ALU.mult, ["tC", "tH"], ["bkt"])
                    _tt(S, "pool", bkt[:, 1, :], kd[:], F["tH"][:], ALU.mult, [kdk, "tH"], ["bkt"])
                    S.dma("sp", self.bk_d[d, T], bkt[:].rearrange("p a n -> p (a n)"), reads=["bkt"], writes=[("bk_d", d, T)])
                    for q in range(4):
                        for c in range(8):
                            S.op("pe", lambda e: e.transpose(out=pT[:, c, :], in_=ob[q][:, c * P:(c + 1) * P], identity=self.idb[:]),
                                 reads=[("ob", q), "idb"], writes=["pT"], accum=(c > 0))
                        if q % 2 == 0:
                            _act(S, ft[:, :, q, :], pT[:], AF.Copy, ["pT"], ["ft"])
                        else:
                            S.op("dve", lambda e: e.tensor_copy(out=ft[:, :, q, :], in_=pT[:]), reads=["pT"], writes=["ft"])
                    S.dma("sp", self.featT_d[d, T], ft[:].rearrange("p j q t -> p (j q t)"), reads=["ft"], writes=[("featT_d", d, T)])
                _tt(S, "pool", F["tD"][:], F["kd0"][:], F["kd1"][:], ALU.add, ["kd0", "kd1"], ["tD"])
                _tt(S, "pool", F["tD"][:], F["tD"][:], F["r_sb"][:], ALU.mult, ["tD", "r_sb"], ["tD"])
                _tt(S, "pool", F["tD"][:], F["tD"][:], R["rk_r"][:], ALU.mult, ["tD", "rk_r"], ["tD"])
                S.op("dve", lambda e: e.tensor_reduce(out=st16[:, 32:48], in_=_h3(F["tD"][:]), axis=AX.X, op=ALU.add),
                     reads=["tD"], writes=["st16"])
                _tt(S, "dve", _h3(F["tD"][:]), _h3(F["v_sb"][:]), st16[:, 32:48][:, :, None].broadcast_to([P, 16, 64]), ALU.mult,
                    ["v_sb", "st16"], ["tD"])
                S.dma("sp", self.bonus_d[t0:t0 + P, :], F["tD"][:], reads=["tD"], writes=[("bonus_d", T)])

    def phase_R2(self):
        S = self.S
        with S.scope():
            m4 = S.sb("m4", [P, 2, 512], F32)
            lmT = S.sb("lmT", [P, 2, 7, P], BF16)
            S.dma("sp", m4[:], self.m4_c.rearrange("d s n -> s d n"), writes=["m4"])
            S.dma("pool", lmT[:], self.lmT_c.rearrange("d l s n -> s d l n"), writes=["lmT"])
            idb = self.idb
            ST32 = S.sb("ST32", [P, 8, 64], F32)
            STb = S.sb("STb", [P, 8, 64], BF16)
            Fb = [S.sb(f"Fb{i}", [P, 8, 4, P], BF16) for i in range(2)]
            Vb = [S.sb(f"Vb{i}", [P, D], BF16) for i in range(2)]
            BKb = [S.sb(f"BKb{i}", [P, 2, D], BF16) for i in range(2)]
            gCb = [S.sb(f"gCb{i}", [P, 8], F32) for i in range(2)]
            ysb = [S.sb(f"ysb{i}", [P, D], F32) for i in range(2)]
            NG = 4
            GMa = [S.sb(f"GMa{i}", [P, NG, 512], BF16) for i in range(2)]
            MLT = [[S.sb(f"MLT{i}_{l}", [P, NG, P], BF16) for l in range(7)] for i in range(2)]
            XTb = [S.sb(f"XT{i}", [P, NG, P], BF16) for i in range(2)]
            Xb = [S.sb(f"X{i}", [P, NG, P], BF16) for i in range(2)]
            T1s = S.sb("T1s", [P, NG, P], BF16)
            Zq = S.sb("Zq", [P, NG, 64], BF16)
            Pb = S.sb("Pb", [P, NG, 64], BF16)
            pG = [S.ps(f"pG{i}", [P, 512], F32) for i in range(2)]
            pT1 = S.ps("pT1", [P, NG, P], F32)
            pU = S.ps("pU", [P, NG, P], F32)
            pUT = S.ps("pUT", [P, NG, P], F32)
            pZS = S.ps("pZS", [P, 8, 64], F32)
            pY = [S.ps(f"pY{h}", [P, 512], F32) for h in range(2)]
            idbc = idb[:, None, :].broadcast_to([P, NG, P])

            items = []
            it = 0
            for d in range(2):
                order = list(range(NT)) if d == 0 else [1, 0] + list(range(NT - 1, 1, -1))
                for ci, T in enumerate(order):
                    for g0 in range(0, 16, NG):
                        items.append(dict(d=d, T=T, g0=g0, b=it % 2, first=(ci == 0 and g0 == 0), gi=len(items)))
                    it += 1

            def heads_of(g0):
                return [(g, g0 + g, (g0 + g) // 2, ((g0 + g) % 2) * 64) for g in range(NG)]

            def load_chunk(w):
                d, T, b = w["d"], w["T"], w["b"]
                S.dma("sp", Fb[b][:].rearrange("p j q t -> p (j q t)"), self.featT_d[d, T], reads=[("featT_d", d, T)], writes=[("Fb", b)])
                S.dma("sp", Vb[b][:], self.vtok_d[T * P:(T + 1) * P, :], reads=[("vtok", T)], writes=[("Vb", b)])
                S.dma("sp", BKb[b][:].rearrange("p a n -> p (a n)"), self.bk_d[d, T], reads=[("bk_d", d, T)], writes=[("BKb", b)])
                S.dma("sp", gCb[b][:], self.gC_d[d, T], reads=[("gC_d", d, T)], writes=[("gCb", b)])

            def stageG(w):
                d, b, gi = w["d"], w["b"], w["gi"]
                if w["g0"] == 0:
                    load_chunk(w)
                Fk = ("Fb", b)
                GM = GMa[gi % 2]; gmk = ("GMa", gi % 2)
                ML = MLT[gi % 2]
                for (g, h, j, pb_) in heads_of(w["g0"]):
                    bank = pG[g % 2]; bkk = ("pG", g % 2)
                    F_ = Fb[b]
                    AR = F_[pb_:pb_ + 64, j, 0:2, :].rearrange("p q t -> p (q t)")
                    mm(S, bank[:, 0:128], F_[pb_:pb_ + 64, j, 2, :], F_[pb_:pb_ + 64, j, 1, :], True, True, reads=[Fk], writes=[bkk])
                    mm(S, bank[:, 128:384], F_[pb_:pb_ + 64, j, 3, :], AR, True, True, reads=[Fk], writes=[bkk])
                    mm(S, bank[:, 384:512], F_[pb_:pb_ + 64, j, 0, :], F_[pb_:pb_ + 64, j, 2, :], True, True, reads=[Fk], writes=[bkk])
                    _tt(S, "dve", GM[:, g, :], bank[:], m4[:, d, :], ALU.mult, ["m4"], [bkk, gmk])
                MTv = GM[:, :, 384:512]
                for l in range(7):
                    _tt(S, "pool", ML[l][:], MTv, lmT[:, d, l:l + 1, :].broadcast_to([P, NG, P]), ALU.mult, [gmk, "lmT"], [("MLT", gi % 2, l)])

            def stageL(w):
                d, b, gi, g0 = w["d"], w["b"], w["gi"], w["g0"]
                Fk, Vk = ("Fb", b), ("Vb", b)
                GM = GMa[gi % 2]; gmk = ("GMa", gi % 2)
                ML = MLT[gi % 2]
                hs = heads_of(g0)
                if w["first"]:
                    S.op("dve", lambda e: e.memset(ST32[:], 0.0), writes=["ST32"])
                    S.op("dve", lambda e: e.memset(STb[:], 0.0), writes=["STb"])
                for (g, h, j, pb_) in hs:
                    mm(S, pZS[:, g, :], Fb[b][pb_:pb_ + 64, j, 0, :], STb[pb_:pb_ + 64, j, :], True, False, reads=[Fk, "STb"], writes=["pZS"])
                    mm(S, pZS[:, g, :], GM[:, g, 128:256], Vb[b][:, h * 64:(h + 1) * 64], False, True, reads=[gmk, Vk], writes=["pZS"])
                S.op("dve", lambda e: e.tensor_copy(out=Zq[:], in_=pZS[:, 0:NG, :]), reads=[], writes=["pZS", "Zq"])
                xi = 0
                _tt(S, "pool", XTb[xi][:], ML[0][:], idbc, ALU.add, [("MLT", gi % 2, 0), "idb"], [("XT", xi)])
                for (g, h, j, pb_) in hs:
                    mm(S, pU[:, g, :], ML[0][:, g, :], idb[:], True, True, reads=[("MLT", gi % 2, 0), "idb"], writes=["pU"])
                _tt(S, "dve", Xb[xi][:], pU[:], idbc, ALU.add, ["idb"], ["pU", ("X", xi)])
                for l in range(1, 7):
                    X, XT = Xb[xi], XTb[xi]
                    Xn, XTn = Xb[1 - xi], XTb[1 - xi]
                    for (g, h, j, pb_) in hs:
                        mm(S, pT1[:, g, :], ML[l][:, g, :], X[:, g, :], True, True, reads=[("MLT", gi % 2, l), ("X", xi)], writes=["pT1"])
                    _tt(S, "dve", T1s[:], pT1[:], idbc, ALU.add, ["idb"], ["pT1", "T1s"])
                    for (g, h, j, pb_) in hs:
                        mm(S, pU[:, g, :], XT[:, g, :], T1s[:, g, :], True, True, reads=[("XT", xi), "T1s"], writes=["pU"])
                    if l < 6:
                        for (g, h, j, pb_) in hs:
                            mm(S, pUT[:, g, :], T1s[:, g, :], XT[:, g, :], True, True, reads=[("XT", xi), "T1s"], writes=["pUT"])
                    _act(S, Xn[:], pU[:], AF.Copy, [], ["pU", ("X", 1 - xi)])
                    if l < 6:
                        S.op("dve", lambda e: e.tensor_copy(out=XTn[:], in_=pUT[:]), reads=[], writes=["pUT", ("XT", 1 - xi)])
                    xi = 1 - xi
                X = Xb[xi]
                for (g, h, j, pb_) in hs:
                    mm(S, pZS[:, g, :], X[:, g, :], Zq[:, g, :], True, True, reads=[("X", xi), "Zq"], writes=["pZS"])
                S.op("dve", lambda e: e.tensor_copy(out=Pb[:], in_=pZS[:, 0:NG, :]), reads=[], writes=["pZS", "Pb"])

            def stageF(w):
                d, T, b, gi, g0 = w["d"], w["T"], w["b"], w["gi"], w["g0"]
                Fk, Vk, BKk, gk = ("Fb", b), ("Vb", b), ("BKb", b), ("gCb", b)
                GM = GMa[gi % 2]; gmk = ("GMa", gi % 2)
                hs = heads_of(g0)
                for (g, h, j, pb_) in hs:
                    yk = ("pY", h // 8)
                    yo = pY[h // 8][:, (h % 8) * 64:(h % 8 + 1) * 64]
                    mm(S, yo, GM[:, g, 0:128], Pb[:, g, :], True, False, reads=[gmk, "Pb"], writes=[yk])
                    mm(S, yo, GM[:, g, 256:384], Vb[b][:, h * 64:(h + 1) * 64], False, False, reads=[gmk, Vk], writes=[yk])
                    mm(S, yo, Fb[b][pb_:pb_ + 64, j, 1, :], STb[pb_:pb_ + 64, j, :], False, True, reads=[Fk, "STb"], writes=[yk])
                for (g, h, j, pb_) in hs:
                    mm(S, pZS[:, 4 + g, :], BKb[b][:, 0, j * P:(j + 1) * P], Pb[:, g, :], True, False, reads=[BKk, "Pb"], writes=["pZS"])
                    mm(S, pZS[:, 4 + g, :], BKb[b][:, 1, j * P:(j + 1) * P], Vb[b][:, h * 64:(h + 1) * 64], False, True,
                       reads=[BKk, Vk], writes=["pZS"])
                for (g, h, j, pb_) in hs:
                    _stt(S, "dve", ST32[pb_:pb_ + 64, j, :], ST32[pb_:pb_ + 64, j, :], gCb[b][pb_:pb_ + 64, j:j + 1],
                         pZS[pb_:pb_ + 64, 4 + g, :], ALU.mult, ALU.add, [gk], ["ST32", "pZS"])
                S.op("pool", lambda e: e.tensor_copy(out=STb[:, g0 // 2:g0 // 2 + 2, :], in_=ST32[:, g0 // 2:g0 // 2 + 2, :]),
                     reads=["ST32"], writes=["STb"])
                if g0 + NG == 16:
                    yb_ = ysb[b]
                    for hf in range(2):
                        _act(S, yb_[:, hf * 512:(hf + 1) * 512], pY[hf][:], AF.Copy, [], [("pY", hf), ("ysb", b)])
                    S.dma("sp", self.y_d[d, T * P:(T + 1) * P, :], yb_[:], reads=[("ysb", b)], writes=[("y_d", d, T)])

            stageG(items[0])
            for k, w in enumerate(items):
                if k + 1 < len(items):
                    stageG(items[k + 1])
                stageL(w)
                stageF(w)

    def phase_R3(self):
        S = self.S
        with S.scope():
            R = {}
            for nm, src in (("lng_r", self.rw_lng), ("lnb_r", self.rw_lnb)):
                R[nm] = S.sb(nm, [P, D], F32)
                S.dma("sp", R[nm][:], src.broadcast_to([P, D]), writes=[nm])
            B = [{n: S.sb(f"{n}{i}", [P, D], F32) for n in ("yf", "yb", "gg", "bo")} for i in range(2)]
            zb = [S.sb(f"zb{i}", [P, D], BF16) for i in range(2)]
            zt = [S.sb(f"zt3_{i}", [P, 8, P], BF16) for i in range(2)]
            st = [S.sb(f"st3_{i}", [P, 64], F32) for i in range(2)]
            pT = [S.ps(f"pT3_{i}", [P, 8, P], BF16) for i in range(2)]
            for T in range(2, NT):
                b = T % 2
                Bf = B[b]
                k = lambda n: (n, b)
                t0 = T * P
                S.dma("sp", Bf["yf"][:], self.y_d[0, t0:t0 + P, :], reads=[("y_d", 0, T)], writes=[k("yf")])
                S.dma("sp", Bf["yb"][:], self.y_d[1, t0:t0 + P, :], reads=[("y_d", 1, T)], writes=[k("yb")])
                S.dma("sp", Bf["gg"][:], self.g_d[t0:t0 + P, :], reads=[("g_d", T)], writes=[k("gg")])
                S.dma("sp", Bf["bo"][:], self.bonus_d[t0:t0 + P, :], reads=[("bonus_d", T)], writes=[k("bo")])
                y = Bf["yf"]; t2 = Bf["yb"]
                _tt(S, "dve", y[:], y[:], t2[:], ALU.add, [k("yf"), k("yb")], [k("yf")])
                S.op("dve", lambda e: e.tensor_reduce(out=st[b][:, 0:16], in_=_h3(y[:]), axis=AX.X, op=ALU.add), reads=[k("yf")], writes=[k("st")])
                S.op("dve", lambda e: e.tensor_scalar(out=st[b][:, 0:16], in0=st[b][:, 0:16], scalar1=-1.0 / 64, scalar2=None, op0=ALU.mult),
                     reads=[k("st")], writes=[k("st")])
                _tt(S, "dve", _h3(y[:]), _h3(y[:]), st[b][:, 0:16][:, :, None].broadcast_to([P, 16, 64]), ALU.add, [k("yf"), k("st")], [k("yf")])
                _tt(S, "pool", t2[:], y[:], y[:], ALU.mult, [k("yf")], [k("yb")])
                S.op("dve", lambda e: e.tensor_reduce(out=st[b][:, 16:32], in_=_h3(t2[:]), axis=AX.X, op=ALU.add), reads=[k("yb")], writes=[k("st")])
                S.op("dve", lambda e: e.tensor_scalar(out=st[b][:, 16:32], in0=st[b][:, 16:32], scalar1=1.0 / 64, scalar2=GN_EPS, op0=ALU.mult, op1=ALU.add),
                     reads=[k("st")], writes=[k("st")])
                _act(S, st[b][:, 16:32], st[b][:, 16:32], AF.Sqrt, [k("st")], [k("st")])
                S.op("dve", lambda e: e.reciprocal(out=st[b][:, 32:48], in_=st[b][:, 16:32]), reads=[k("st")], writes=[k("st")])
                _tt(S, "dve", _h3(y[:]), _h3(y[:]), st[b][:, 32:48][:, :, None].broadcast_to([P, 16, 64]), ALU.mult, [k("yf"), k("st")], [k("yf")])
                _tt(S, "pool", y[:], y[:], R["lng_r"][:], ALU.mult, [k("yf"), "lng_r"], [k("yf")])
                _tt(S, "pool", y[:], y[:], R["lnb_r"][:], ALU.add, [k("yf"), "lnb_r"], [k("yf")])
                _tt(S, "dve", y[:], y[:], Bf["bo"][:], ALU.add, [k("yf"), k("bo")], [k("yf")])
                _tt(S, "dve", zb[b][:], y[:], Bf["gg"][:], ALU.mult, [k("yf"), k("gg")], [k("zb")])
                for c in range(8):
                    S.op("pe", lambda e: e.transpose(out=pT[b][:, c, :], in_=zb[b][:, c * P:(c + 1) * P], identity=self.idb[:]),
                         reads=[k("zb"), "idb"], writes=[k("pT3")], accum=(c > 0))
                _act(S, zt[b][:], pT[b][:], AF.Copy, [k("pT3")], [k("zt3")])
                S.dma("sp", self.zT_d[:, :, t0:t0 + P].rearrange("c p t -> p c t"), zt[b][:], reads=[k("zt3")],
                      writes=[("zT", c, T) for c in range(8)])


def build_program(stop_after=None, debug=(), phases="M0ABCD1abcde"):
    nc = bass.Bass("TRN2", target_bir_lowering=False)
    Pg = Prog1(nc, debug)
    S = Pg.S
    Pg.consts()
    if "M" in phases:
        Pg.phase_mod()
    if "0" in phases:
      with S.scope():
        V0 = Pg.load_layer_vecs(0)
        if "A" in phases: Pg.phase_L0_proj(V0)
        if "B" in phases: Pg.phase_L0_pool()
        if "C" in phases: Pg.phase_L0_attn()
        if "D" in phases: Pg.phase_out_mlp(0, V0, Pg.ev_w_out, Pg.y_tile_L0, lambda T: (Pg.x_d[T * P:(T + 1) * P, :], ("x_d", T)))
    if "1" in phases:
      with S.scope():
        V1 = Pg.load_layer_vecs(1)
        if "a" in phases: Pg.phase_R0(V1)
        if "b" in phases: Pg.phase_R1()
        if "c" in phases: Pg.phase_R2()
        if "d" in phases: Pg.phase_R3()
        if "e" in phases: Pg.phase_out_mlp(1, V1, Pg.rw_wo, Pg.y_tile_L0, lambda T: (Pg.out[(T - 2) * P:(T - 1) * P, :], ("out", T)))
    S.barrier()
    S.finish([])
    S.close()
    return nc, Pg


_host_inputs0 = host_inputs


def host_inputs(inputs):
    maps = _host_inputs0(inputs)
    f = lambda a: np.ascontiguousarray(np.asarray(a, dtype=np.float32))
    tri, m4, lmT = _scan_consts()
    sh = {
        "rw_mu": f(inputs["rw_mu"])[0], "rw_wr": f(inputs["rw_wr"])[0], "rw_wk": f(inputs["rw_wk"])[0],
        "rw_wv": f(inputs["rw_wv"])[0], "rw_wo": f(inputs["rw_wo"])[0],
        "rw_w0": f(inputs["rw_w0"])[0], "rw_a0": f(inputs["rw_a0"])[0],
        "w1cat": f(np.concatenate([inputs["rw_w1"][0, 0], inputs["rw_w1"][0, 1]], axis=1)),
        "a1cat": f(np.concatenate([inputs["rw_a1"][0, 0], inputs["rw_a1"][0, 1]], axis=1)),
        "rw_g1": f(inputs["rw_g1"])[0],
        "w2cat": f(np.asarray(inputs["rw_w2"])[0].reshape(128, 1024)), "a2cat": f(np.asarray(inputs["rw_a2"])[0].reshape(128, 1024)),
        "rw_g2": f(inputs["rw_g2"])[0],
        "rw_kk": f(inputs["rw_kk"]).reshape(1, 1024), "rw_ka": f(inputs["rw_ka"]).reshape(1, 1024),
        "rw_rk": f(inputs["rw_rk"]).reshape(1, 1024), "rw_lng": f(inputs["rw_lng"]).reshape(1, 1024),
        "rw_lnb": f(inputs["rw_lnb"]).reshape(1, 1024),
        "tri_c": f(tri), "m4_c": f(m4), "lmT_c": f(lmT),
    }
    for m in maps:
        m.update(sh)
    return maps
```

```python
import contextlib
import numpy as np
import concourse.bass as bass
import concourse.mybir as mybir

F32 = mybir.dt.float32
BF16 = mybir.dt.bfloat16
AF = mybir.ActivationFunctionType
ALU = mybir.AluOpType
AX = mybir.AxisListType

SEM_LIMIT = 10000


class _Ctr:
    def __init__(self, S, name, step):
        self.S = S
        self.name = name
        self.step = step
        self.gen = 0
        self.sem = S._newsem(f"{name}_0")
        self.val = 0

    def next_event(self):
        if self.val + self.step > SEM_LIMIT:
            self.gen += 1
            self.sem = self.S._newsem(f"{self.name}_{self.gen}")
            self.val = 0
        self.val += self.step
        return (self.sem, self.val)


class _PsView:
    def __init__(self, t, shape):
        self.t = t
        self.n1 = shape[1]

    def __getitem__(self, key):
        if not isinstance(key, tuple):
            key = (key,)
        key = list(key)
        if len(key) < 2:
            key.append(slice(None))
        k1 = key[1]
        if isinstance(k1, slice):
            start, stop, step = k1.indices(self.n1)
            key[1] = slice(start, stop, step)
        return self.t[tuple(key)]


class _Eng:
    def __init__(self, S, name, obj):
        self.name = name
        self.obj = obj
        self.ctr = _Ctr(S, "s_" + name, 1)
        self.seen = {}
        self.n_issued = 0
        self.last_ins = None
        self.last_has_inc = False
        self.inc_idx = []
        self.inc_ev = []


class LazyEv:
    __slots__ = ("eng", "idx")

    def __init__(self, eng, idx):
        self.eng = eng
        self.idx = idx


class _Res:
    __slots__ = ("w", "r")

    def __init__(self):
        self.w = None
        self.r = {}


class Sched:
    def __init__(self, nc, n_dma_slots=8):
        self.nc = nc
        self.stack = contextlib.ExitStack()
        self.scopes = [self.stack]
        self.res = {}
        self.engs = {
            "pe": _Eng(self, "pe", nc.tensor),
            "act": _Eng(self, "act", nc.scalar),
            "dve": _Eng(self, "dve", nc.vector),
            "pool": _Eng(self, "pool", nc.gpsimd),
            "sp": _Eng(self, "sp", nc.sync),
        }
        self.dma_slots = {}
        for q in ("sp", "pool"):
            self.dma_slots[q] = [_Ctr(self, f"d_{q}{i}", 16) for i in range(n_dma_slots)]
        self.dma_rr = {"sp": 0, "pool": 0}
        self.n_inst = 0
        self.uid = 0
        self.pending = None
        self.lazy_engines = ()

    def _newsem(self, name):
        return self.stack.enter_context(self.nc.semaphore(name))

    def sb(self, name, shape, dt):
        self.uid += 1
        return self.scopes[-1].enter_context(self.nc.sbuf_tensor(f"sb{self.uid}_{name}", list(shape), dt))

    def ps(self, name, shape, dt=F32):
        self.uid += 1
        esz = 4 if dt == F32 else 2
        per_part = esz
        for d_ in shape[1:]:
            per_part *= d_
        assert per_part <= 2048, (name, shape)
        shape = list(shape)
        if per_part < 2048:
            rest = per_part // shape[1]
            assert 2048 % rest == 0, (name, shape)
            full = [shape[0], 2048 // rest] + shape[2:]
            t = self.scopes[-1].enter_context(self.nc.psum_tensor(f"ps{self.uid}_{name}", full, dt))
            return _PsView(t, shape)
        return self.scopes[-1].enter_context(self.nc.psum_tensor(f"ps{self.uid}_{name}", shape, dt))

    @contextlib.contextmanager
    def scope(self):
        st = contextlib.ExitStack()
        self.scopes.append(st)
        try:
            yield
        finally:
            self.barrier()
            self.scopes.pop()
            st.close()

    def barrier(self):
        evs = []
        for e in self.engs.values():
            if e.n_issued > 0:
                evs.append(self._resolve(LazyEv(e, e.n_issued - 1)))
        for q in self.dma_slots:
            for ctr in self.dma_slots[q]:
                if ctr.val > 0:
                    evs.append((ctr.sem, ctr.val))
        for e in self.engs.values():
            for ev in evs:
                self._wait(e, ev)

    def _r(self, key):
        r = self.res.get(key)
        if r is None:
            r = self.res[key] = _Res()
        return r

    def _resolve(self, ev):
        if not isinstance(ev, LazyEv):
            return ev
        import bisect
        e = ev.eng
        k = bisect.bisect_left(e.inc_idx, ev.idx)
        if k < len(e.inc_idx):
            return e.inc_ev[k]
        assert e.last_ins is not None and not e.last_has_inc and e.n_issued - 1 >= ev.idx
        sv = e.ctr.next_event()
        e.last_ins.then_inc(sv[0], 1)
        e.last_has_inc = True
        e.inc_idx.append(e.n_issued - 1)
        e.inc_ev.append(sv)
        return sv

    def _wait(self, eng, ev):
        if ev is None:
            return
        sem, val = self._resolve(ev)
        k = id(sem)
        if eng.seen.get(k, 0) >= val:
            return
        if self.pending is not None:
            cur = self.pending.get(k)
            if cur is None or cur[1] < val:
                self.pending[k] = (sem, val)
            return
        eng.obj.wait_ge(sem, val)
        eng.seen[k] = val

    def _flush(self, eng):
        pend = list(self.pending.values())
        self.pending = None
        for (sem, val) in pend[:-1]:
            eng.obj.wait_ge(sem, val)
            eng.seen[id(sem)] = val
        if pend:
            sem, val = pend[-1]
            eng.seen[id(sem)] = val
            return (sem, val)
        return None

    def _deps(self, eng, reads, writes, skip_same_eng_write=False):
        for key in reads:
            r = self._r(key)
            self._wait(eng, r.w)
        for key in writes:
            r = self._r(key)
            if not (skip_same_eng_write and isinstance(r.w, LazyEv) and r.w.eng is eng):
                self._wait(eng, r.w)
            for ev in r.r.values():
                self._wait(eng, ev)

    def _commit(self, ev, reads, writes):
        rk = ev.eng.name if isinstance(ev, LazyEv) else id(ev[0])
        for key in reads:
            self._r(key).r[rk] = ev
        for key in writes:
            r = self._r(key)
            r.w = ev
            r.r = {}

    def op(self, engname, fn, reads=(), writes=(), accum=False):
        eng = self.engs[engname]
        self.pending = {}
        self._deps(eng, reads, writes, skip_same_eng_write=accum)
        last = self._flush(eng)
        ins = fn(eng.obj)
        if last is not None:
            ins._wait_ge(last[0], last[1])
        eng.last_ins = ins
        eng.last_has_inc = False
        ev = LazyEv(eng, eng.n_issued)
        eng.n_issued += 1
        if engname not in self.lazy_engines:
            self._resolve(ev)
        self._commit(ev, reads, writes)
        self.n_inst += 1
        return ev

    def dma(self, q, out, in_, reads=(), writes=(), **kw):
        eng = self.engs[q]
        slots = self.dma_slots[q]
        i = self.dma_rr[q]
        self.dma_rr[q] = (i + 1) % len(slots)
        ctr = slots[i]
        self.pending = {}
        if ctr.val > 0:
            self._wait(eng, (ctr.sem, ctr.val))
        self._deps(eng, reads, writes)
        last = self._flush(eng)
        ev = ctr.next_event()
        ins = eng.obj.dma_start(out=out, in_=in_, **kw)
        if last is not None:
            ins._wait_ge(last[0], last[1])
        ins.then_inc(ev[0], 16)
        self._commit(ev, reads, writes)
        self.n_inst += 1
        return ev

    def finish(self, final_keys):
        eng = self.engs["sp"]
        for key in final_keys:
            r = self._r(key)
            self._wait(eng, r.w)
        for q in self.dma_slots:
            for ctr in self.dma_slots[q]:
                if ctr.val > 0:
                    self._wait(eng, (ctr.sem, ctr.val))

    def close(self):
        self.stack.close()

from concourse.bass_utils import run_bass_kernel_spmd

D = 1024
NCTX = 256
NLAT = 4096
NTOK = NCTX + NLAT
NT = NTOK // 128
EPS = 1e-6
P = 128


def _pool_bands():
    L = 1024
    out = np.zeros((4, 5, 128, 128), np.float32)
    for g, w in enumerate((2, 4, 8, 16)):
        def full(L):
            t = np.arange(L)
            lo = np.clip(t - w // 2, 0, L)
            hi = np.clip(t + w // 2, 0, L)
            s = np.arange(L)[:, None]
            m = ((s >= lo[None, :]) & (s < hi[None, :])).astype(np.float64) / (hi - lo)[None, :]
            m -= np.eye(L)
            return m
        m = full(L)
        out[g, 0] = m[3 * 128:4 * 128, 4 * 128:5 * 128]
        out[g, 1] = m[5 * 128:6 * 128, 4 * 128:5 * 128]
        out[g, 2] = m[4 * 128:5 * 128, 4 * 128:5 * 128]
        out[g, 3] = m[0:128, 0:128]
        out[g, 4] = m[L - 128:, L - 128:]
    return out


_VARS = [(-2, "pm"), (-1, "f"), (0, "f"), (1, "f"), (2, "pp")] + [(d, "f") for d in range(-3, 4)]


def _attn_tables():
    kc = np.arange(64)
    qc = np.arange(64)
    c_start = np.clip(qc - 8, 0, 48)
    col_ok = (kc[:, None] >= c_start[None, :]) & (kc[:, None] < c_start[None, :] + 16)
    dc_idx = np.clip(kc[:, None] - qc[None, :], -15, 15) + 15
    dr_idx = np.zeros((12, 128, 128), np.int64)
    dc_full = np.zeros((128, 128), np.int64)
    mask = np.zeros((12, 128, 128), np.float32)
    for a in range(2):
        for b in range(2):
            dc_full[a * 64:(a + 1) * 64, b * 64:(b + 1) * 64] = dc_idx
    for v, (dl, kind) in enumerate(_VARS):
        for a in range(2):
            for b in range(2):
                dr = 2 * dl + a - b + 7
                vis = True
                if kind == "pm":
                    vis = not (a == 0 and b == 1)
                elif kind == "pp":
                    vis = (a == 0 and b == 1)
                dr_idx[v, a * 64:(a + 1) * 64, b * 64:(b + 1) * 64] = min(max(dr, 0), 14)
                if vis and 0 <= dr <= 14:
                    mask[v, a * 64:(a + 1) * 64, b * 64:(b + 1) * 64] = col_ok
    return dr_idx, dc_full, mask


def mm(S, out, lhsT, rhs, start, stop, reads, writes):
    return S.op("pe", lambda e: e.matmul(out, lhsT=lhsT, rhs=rhs, start=start, stop=stop),
                reads=reads, writes=writes, accum=not start)


class Prog:
    def __init__(self, nc, debug=()):
        self.nc = nc
        self.S = Sched(nc)
        self.debug = debug
        self.dbg_out = {}
        dt = nc.dram_tensor
        I = lambda name, shape: dt(name, list(shape), F32, kind="ExternalInput").ap()
        self.x_in = I("x", [NLAT, D])
        self.ctx_in = I("ctx", [NCTX, D])
        self.cvec = I("cvec", [P, 8, 2])
        self.ada_w = I("ada_w", [2, D, 6 * D])
        self.ada_b = I("ada_b", [2, 6 * D])
        self.norm_g = I("norm_g", [2, 4, D])
        self.mlp_w1 = I("mlp_w1", [2, D, 4 * D])
        self.mlp_w2 = I("mlp_w2", [2, 4 * D, D])
        self.ev_w_in = I("ev_w_in", [D, 2 * D])
        self.ev_w_out = I("ev_w_out", [D, D])
        self.ev_pool_w = I("ev_pool_w", [4, P, P])
        self.ev_pool_scale = I("ev_pool_scale", [512])
        self.rpb_tab = I("rpb_tab", [P, 8 * 12, P])
        self.msk_tab = I("msk_tab", [P, 12, P])
        self.bands = I("bands", [P, 20, P])
        self.ident = I("ident", [P, P])
        self.out = dt("out", [NLAT, D], F32, kind="ExternalOutput").ap()
        X = lambda name, shape, d=F32: (dt(name, list(shape), d, kind="ExternalOutput").ap() if name in debug
                                        else dt(name, list(shape), d).ap())
        self.modd = X("modd", [2, 2, 6 * D])
        self.x_d = X("x_d", [NTOK, D])
        self.upool_d = X("upool_d", [NTOK, 512], BF16)
        self.v_d = X("v_d", [NTOK, 512], BF16)
        self.qT_d = X("qT_d", [4, P, NTOK], BF16)
        self.kT_d = X("kT_d", [4, P, NTOK], BF16)
        self.zT_d = X("zT_d", [8, P, NTOK], BF16)

    def dbg(self, name, shape, dtp=F32):
        t = self.nc.dram_tensor("dbg_" + name, list(shape), dtp, kind="ExternalOutput").ap()
        self.dbg_out[name] = t
        return t

    def consts(self):
        S = self.S
        self.idb = S.sb("idb", [P, P], BF16)
        S.dma("pool", self.idb[:], self.ident, writes=["idb"])
        self.ones_bf = S.sb("ones_bf", [P, P], BF16)
        S.op("dve", lambda e: e.memset(self.ones_bf[:], 1.0), writes=["ones_bf"])
        self.eps_t = S.sb("eps_t", [P, 1], F32)
        S.op("dve", lambda e: e.memset(self.eps_t[:], EPS), writes=["eps_t"])

    def phase_mod(self):
        S = self.S
        with S.scope():
            cv = S.sb("cv", [P, 8, 2], F32)
            cvb = S.sb("cvb", [P, 8, 2], BF16)
            S.dma("sp", cv[:], self.cvec, writes=["cv"])
            S.op("act", lambda e: e.activation(out=cvb[:], in_=cv[:], func=AF.Silu), reads=["cv"], writes=["cvb"])
            aw = S.sb("aw", [P, 8, 6 * D], BF16)
            ab = S.sb("ab", [2, 6 * D], F32)
            mrow = S.sb("mrow", [2, 6 * D], F32)
            pss = [S.ps(f"pm{i}", [2, 512], F32) for i in range(4)]
            for l in range(2):
                for c in range(8):
                    S.dma("pool", aw[:, c, :], self.ada_w[l, c * P:(c + 1) * P, :], writes=[("aw", c)])
                S.dma("sp", ab[:], self.ada_b[l:l + 1, :].broadcast_to([2, 6 * D]), writes=["ab"])
                for n in range(12):
                    ps = pss[n % 4]
                    k = ("pm", n % 4)
                    for c in range(8):
                        mm(S, ps[:], cvb[:, c, :], aw[:, c, n * 512:(n + 1) * 512], c == 0, c == 7,
                           reads=["cvb", ("aw", c)], writes=[k])
                    S.op("dve", lambda e: e.tensor_tensor(out=mrow[:, n * 512:(n + 1) * 512], in0=ps[:],
                                                          in1=ab[:, n * 512:(n + 1) * 512], op=ALU.add),
                         reads=[k, "ab"], writes=["mrow"])
                S.dma("sp", self.modd[l], mrow[:], reads=["mrow"], writes=[("modd", l)])

    def load_layer_vecs(self, l):
        S = self.S
        V = {}
        for s in range(2):
            for which in range(2):
                V[("A", which, s)] = (S.sb(f"A{which}_{s}", [P, 8], F32), f"A{which}_{s}")
                V[("B", which, s)] = (S.sb(f"B{which}_{s}", [P, 8], F32), f"B{which}_{s}")
                if not (l == 1 and s == 1):
                    V[("G", which, s)] = (S.sb(f"GG{which}_{s}", [P, D], F32), f"GG{which}_{s}")
        with S.scope(), self.nc.allow_non_contiguous_dma(reason="tiny per-feature vectors"):
            tmp = S.sb("lv_tmp", [P, 8], F32)
            rowt = S.sb("lv_row", [P, D], F32)
            for s in range(2):
                for which, (ish, isc, ig) in enumerate(((0, 1, 0), (3, 4, 2))):
                    A, ka = V[("A", which, s)]
                    B, kb = V[("B", which, s)]
                    S.dma("sp", B[:], self.modd[l, s, ish * D:(ish + 1) * D].rearrange("(c p) -> p c", p=P),
                          reads=[("modd", l)], writes=[kb])
                    S.dma("sp", A[:], self.modd[l, s, isc * D:(isc + 1) * D].rearrange("(c p) -> p c", p=P),
                          reads=[("modd", l)], writes=[ka])
                    S.dma("sp", tmp[:], self.norm_g[l, ig, :].rearrange("(c p) -> p c", p=P), writes=["lv_tmp"])
                    S.op("dve", lambda e: e.scalar_tensor_tensor(out=A[:], in0=A[:], scalar=1.0, in1=tmp[:],
                                                                 op0=ALU.add, op1=ALU.mult),
                         reads=[ka, "lv_tmp"], writes=[ka])
                for which, (igt, ig) in enumerate(((2, 1), (5, 3))):
                    if l == 1 and s == 1:
                        continue
                    G, kg = V[("G", which, s)]
                    S.dma("sp", G[:], self.modd[l, s:s + 1, igt * D:(igt + 1) * D].broadcast_to([P, D]),
                          reads=[("modd", l)], writes=[kg])
                    S.dma("sp", rowt[:], self.norm_g[l, ig:ig + 1, :].broadcast_to([P, D]), writes=["lv_row"])
                    S.op("dve", lambda e: e.tensor_tensor(out=G[:], in0=G[:], in1=rowt[:], op=ALU.mult),
                         reads=[kg, "lv_row"], writes=[kg])
        return V

    def make_norm_bufs(self, tag, nb=2):
        S = self.S
        B = {"i": 0, "nb": nb, "tag": tag}
        B["sq"] = [S.sb(f"{tag}_sq{i}", [P, D], BF16) for i in range(1)] * nb
        B["st"] = [S.sb(f"{tag}_st{i}", [P, 4], F32) for i in range(nb)]
        B["xn"] = [S.sb(f"{tag}_xn{i}", [P, D], BF16) for i in range(nb)]
        B["tp"] = [S.ps(f"{tag}_tp{i}", [P, 8, P], BF16) for i in range(nb)]
        B["tm"] = [S.sb(f"{tag}_tm{i}", [P, 8, P], F32) for i in range(nb)]
        return B

    def norm_to_hT(self, B, x_sb, xkey, A, B_, out_ap, out_key, out2_ap=None, out2_key=None):
        S = self.S
        i = B["i"] % B["nb"]
        B["i"] += 1
        tag = B["tag"]
        sq, st, xn, tp, tm = B["sq"][i], B["st"][i], B["xn"][i], B["tp"][i], B["tm"][i]
        ksq, kst, kxn, ktp, ktm = [(tag, n, i) for n in ("sq", "st", "xn", "tp", "tm")]
        ksq = (tag, "sq", 0)
        S.op("pool", lambda e: e.memset(st[:], 0.0), writes=[kst])
        S.op("act", lambda e: e.activation(out=sq[:], in_=x_sb, func=AF.Square, accum_out=st[:, 0:1]),
             reads=[xkey], writes=[ksq, kst])
        S.op("act", lambda e: e.activation(out=st[:, 1:2], in_=st[:, 0:1], func=AF.Sqrt, scale=1.0 / D,
                                           bias=self.eps_t[:, 0:1]), reads=[kst, "eps_t"], writes=[kst])
        S.op("dve", lambda e: e.reciprocal(out=st[:, 2:3], in_=st[:, 1:2]), reads=[kst], writes=[kst])
        S.op("dve", lambda e: e.tensor_scalar(out=xn[:], in0=x_sb, scalar1=st[:, 2:3], scalar2=None, op0=ALU.mult),
             reads=[xkey, kst], writes=[kxn])
        for c in range(8):
            S.op("pe", lambda e: e.transpose(out=tp[:, c, :], in_=xn[:, c * P:(c + 1) * P], identity=self.idb[:]),
                 reads=[kxn, "idb"], writes=[ktp], accum=(c > 0))
        Aap = A[0][:, :, None].broadcast_to([P, 8, P])
        Bap = B_[0][:, :, None].broadcast_to([P, 8, P])
        S.op("dve", lambda e: e.tensor_tensor(out=tm[:], in0=tp[:], in1=Aap, op=ALU.mult),
             reads=[ktp, A[1]], writes=[ktm])
        S.op("pool", lambda e: e.tensor_tensor(out=out_ap, in0=tm[:], in1=Bap, op=ALU.add),
             reads=[ktm, B_[1]], writes=[out_key])
        if out2_ap is not None:
            S.op("pool", lambda e: e.tensor_tensor(out=out2_ap, in0=tm[:], in1=Bap, op=ALU.add),
                 reads=[ktm, B_[1]], writes=[out2_key])

    def x_src(self, layer, T):
        if layer == 0:
            if T < 2:
                return self.ctx_in[T * P:(T + 1) * P, :], None
            return self.x_in[(T - 2) * P:(T - 1) * P, :], None
        return self.x_d[T * P:(T + 1) * P, :], ("x_d", T)

    def phase_L0_proj(self, V):
        S = self.S
        with S.scope():
            w = S.sb("w_in", [P, 8, 2 * D], BF16)
            for c in range(8):
                S.dma("pool", w[:, c, :], self.ev_w_in[c * P:(c + 1) * P, :], writes=[("w_in", c)])
            wk = [("w_in", c) for c in range(8)]
            NB = self.make_norm_bufs("n0")
            xt = [S.sb(f"xt{i}", [P, D], F32) for i in range(2)]
            hT = [S.sb(f"hT{i}", [P, 8, 512], BF16) for i in range(2)]
            ptok = [S.ps(f"ptok{i}", [P, 512], F32) for i in range(2)]
            pft = [S.ps(f"pft{i}", [P, 512], F32) for i in range(2)]
            otok = [S.sb(f"otok{i}", [P, 512], BF16) for i in range(2)]
            oft = [S.sb(f"oft{i}", [P, 512], BF16) for i in range(2)]
            supers = [(0, 2)] + [(2 + 4 * i, 4) for i in range(8)]
            cnt = 0
            ctok = 0
            cft = 0
            for si, (T0, nt) in enumerate(supers):
                hb = hT[si % 2]
                hk = ("hT", si % 2)
                s = 1 if T0 < 2 else 0
                for t in range(nt):
                    T = T0 + t
                    xb = xt[cnt % 2]
                    xk = ("xt", cnt % 2)
                    cnt += 1
                    src, sk = self.x_src(0, T)
                    S.dma("sp", xb[:], src, reads=[sk] if sk else [], writes=[xk])
                    self.norm_to_hT(NB, xb[:], xk, V[("A", 0, s)], V[("B", 0, s)],
                                    hb[:, :, t * P:(t + 1) * P], (hk, t))
                n = nt * P
                hks = [(hk, t) for t in range(nt)]
                for t in range(nt):
                    T = T0 + t
                    for (c0, dst, dk) in ((0, self.upool_d, "upool"), (1536, self.v_d, "v")):
                        ps = ptok[ctok % 2]; pk = ("ptok", ctok % 2)
                        ob = otok[ctok % 2]; ok = ("otok", ctok % 2)
                        ctok += 1
                        for c in range(8):
                            mm(S, ps[:], hb[:, c, t * P:(t + 1) * P], w[:, c, c0:c0 + 512], c == 0, c == 7,
                               reads=[(hk, t), wk[c]], writes=[pk])
                        S.op("act", lambda e: e.activation(out=ob[:], in_=ps[:], func=AF.Copy), reads=[pk], writes=[ok])
                        S.dma("sp", dst[T * P:(T + 1) * P, :], ob[:], reads=[ok], writes=[(dk, T)])
                for jb in range(8):
                    c0 = 512 + jb * P
                    ps = pft[cft % 2]; pk = ("pft", cft % 2)
                    ob = oft[cft % 2]; ok = ("oft", cft % 2)
                    cft += 1
                    for c in range(8):
                        mm(S, ps[:, :n], w[:, c, c0:c0 + P], hb[:, c, :n], c == 0, c == 7,
                           reads=hks + [wk[c]], writes=[pk])
                    sc = 0.125 if jb < 4 else 1.0
                    S.op("act", lambda e: e.activation(out=ob[:, :n], in_=ps[:, :n], func=AF.Copy, scale=sc),
                         reads=[pk], writes=[ok])
                    dst = self.qT_d if jb < 4 else self.kT_d
                    dk = "qT" if jb < 4 else "kT"
                    S.dma("sp", dst[jb % 4, :, T0 * P:T0 * P + n], ob[:, :n], reads=[ok],
                          writes=[(dk, jb % 4, T0 + t) for t in range(nt)])

    def phase_L0_pool(self):
        S = self.S
        with S.scope():
            up = S.sb("up_all", [P, NT, 512], BF16)
            for q in range(0, NT, 2):
                S.dma("sp", up[:, q:q + 2, :], self.upool_d[q * P:(q + 2) * P, :].rearrange("(n p) f -> p n f", p=P),
                      reads=[("upool", q), ("upool", q + 1)], writes=[("up", q), ("up", q + 1)])
            bd = S.sb("bands", [P, 20, P], BF16)
            S.dma("pool", bd[:], self.bands, writes=["bands"])
            pw = S.sb("pool_w", [P, 4, P], BF16)
            S.dma("pool", pw[:], self.ev_pool_w.rearrange("g c o -> c g o"), writes=["pool_w"])
            psc = S.sb("pool_sc", [P, 4], F32)
            with self.nc.allow_non_contiguous_dma(reason="tiny"):
                S.dma("sp", psc[:], self.ev_pool_scale.rearrange("(g p) -> p g", p=P), writes=["pool_sc"])
            pb = [S.ps(f"pb{i}", [P, 4, P], F32) for i in range(2)]
            pc = [S.ps(f"pc{i}", [P, 4, P], F32) for i in range(2)]
            pm = [S.sb(f"pmx{i}", [P, 4, P], BF16) for i in range(2)]
            zp = [S.sb(f"zp{i}", [P, 4, P], BF16) for i in range(2)]
            it = 0
            for (T0, n) in ((0, 2), (2, 32)):
                for i in range(n):
                    T = T0 + i
                    b = it % 2
                    it += 1
                    for g in range(4):
                        srcs = []
                        if i > 0:
                            srcs.append((T - 1, 0))
                        cv = 3 if i == 0 else (4 if i == n - 1 else 2)
                        srcs.append((T, cv))
                        if i < n - 1:
                            srcs.append((T + 1, 1))
                        for si, (Ts, v) in enumerate(srcs):
                            mm(S, pb[b][:, g, :], up[:, Ts, g * P:(g + 1) * P], bd[:, g * 5 + v, :],
                               si == 0, si == len(srcs) - 1, reads=[("up", Ts), "bands"], writes=[("pb", b)])
                    S.op("dve", lambda e: e.tensor_copy(out=pm[b][:], in_=pb[b][:]), reads=[("pb", b)], writes=[("pmx", b)])
                    for g in range(4):
                        mm(S, pc[b][:, g, :], pw[:, g, :], pm[b][:, g, :], True, True,
                           reads=["pool_w", ("pmx", b)], writes=[("pc", b)])
                    S.op("dve", lambda e: e.tensor_tensor(out=zp[b][:], in0=pc[b][:],
                                                          in1=psc[:, :, None].broadcast_to([P, 4, P]), op=ALU.mult),
                         reads=[("pc", b), "pool_sc"], writes=[("zp", b)])
                    S.dma("sp", self.zT_d[0:4, :, T * P:(T + 1) * P].rearrange("c p t -> p c t"), zp[b][:],
                          reads=[("zp", b)], writes=[("zT", c, T) for c in range(4)])

    def phase_L0_attn(self):
        S = self.S
        with S.scope():
            kT = S.sb("kT_all", [P, 4, NTOK], BF16)
            qT = S.sb("qT_all", [P, 4, NTOK], BF16)
            va = S.sb("v_all", [P, NT, 512], BF16)
            for j in range(4):
                S.dma("sp", kT[:, j, :], self.kT_d[j], reads=[("kT", j, T) for T in range(NT)], writes=[("kTa", j)])
                S.dma("sp", qT[:, j, :], self.qT_d[j], reads=[("qT", j, T) for T in range(NT)], writes=[("qTa", j)])
            for q in range(0, NT, 2):
                S.dma("sp", va[:, q:q + 2, :], self.v_d[q * P:(q + 2) * P, :].rearrange("(n p) f -> p n f", p=P),
                      reads=[("v", q), ("v", q + 1)], writes=[("va", q), ("va", q + 1)])
            E = S.sb("Etab", [P, 96, P], BF16)
            with S.scope():
                rt = S.sb("rt", [P, 96, P], F32)
                mk = S.sb("mk", [P, 12, P], F32)
                S.dma("sp", rt[:], self.rpb_tab, writes=["rt"])
                S.dma("sp", mk[:], self.msk_tab, writes=["mk"])
                S.op("act", lambda e: e.activation(out=rt[:], in_=rt[:], func=AF.Exp), reads=["rt"], writes=["rt"])
                for h in range(8):
                    S.op("dve", lambda e: e.tensor_tensor(out=E[:, h * 12:(h + 1) * 12, :], in0=rt[:, h * 12:(h + 1) * 12, :],
                                                          in1=mk[:], op=ALU.mult), reads=["rt", "mk"], writes=["Etab"])
            pss = [[S.ps(f"pss{i}_{k}", [P, 512], F32) for k in range(2)] for i in range(2)]
            pso = [S.ps(f"pso{i}", [P, 2, P], F32) for i in range(2)]
            pex = [S.sb(f"pex{i}", [P, 7, P], BF16) for i in range(2)]
            pT = [S.sb(f"pT{i}", [P, 5, P], BF16) for i in range(2)]
            rc = [S.sb(f"rc{i}", [P, P], F32) for i in range(2)]
            zo = [S.sb(f"zo{i}", [P, P], BF16) for i in range(2)]
            it = 0
            izo = 0
            for T in range(NT):
                if T < 2:
                    chunks = [(0, None), (1, None)]
                else:
                    i = T - 2
                    if 2 <= i <= 29:
                        lat = [(T + d, v) for v, d in enumerate((-2, -1, 0, 1, 2))]
                    elif i == 0:
                        lat = [(T + d, 8 + d) for d in (0, 1, 2, 3)]
                    elif i == 1:
                        lat = [(T + d, 8 + d) for d in (-1, 0, 1, 2)]
                    elif i == 30:
                        lat = [(T + d, 8 + d) for d in (-2, -1, 0, 1)]
                    else:
                        lat = [(T + d, 8 + d) for d in (-3, -2, -1, 0)]
                    chunks = [(0, None), (1, None)] + lat
                nk = len(chunks)
                nlat = nk - 2
                for j in range(4):
                    zb = zo[izo % 2]; zk = ("zo", izo % 2)
                    izo += 1
                    for hh in range(2):
                        h = 2 * j + hh
                        pb_ = hh * 64
                        b = it % 2
                        it += 1
                        for ci, (Tk, v) in enumerate(chunks):
                            bank = pss[b][ci // 4]
                            mm(S, bank[:, (ci % 4) * P:(ci % 4 + 1) * P],
                               kT[pb_:pb_ + 64, j, Tk * P:(Tk + 1) * P], qT[pb_:pb_ + 64, j, T * P:(T + 1) * P],
                               True, True, reads=[("kTa", j), ("qTa", j)], writes=[("pss", b, ci // 4)])
                        n0 = min(nk, 4)
                        S.op("act", lambda e: e.activation(out=pex[b][:, 0:n0, :], in_=pss[b][0][:, 0:n0 * P].rearrange("p (c q) -> p c q", q=P), func=AF.Exp),
                             reads=[("pss", b, 0)], writes=[("pex", b)])
                        if nk > 4:
                            S.op("act", lambda e: e.activation(out=pex[b][:, 4:nk, :], in_=pss[b][1][:, 0:(nk - 4) * P].rearrange("p (c q) -> p c q", q=P), func=AF.Exp),
                                 reads=[("pss", b, 1)], writes=[("pex", b)])
                        if nlat > 0:
                            v0 = chunks[2][1]
                            S.op("dve", lambda e: e.tensor_tensor(out=pT[b][:, 0:nlat, :], in0=pex[b][:, 2:nk, :],
                                                                  in1=E[:, h * 12 + v0:h * 12 + v0 + nlat, :], op=ALU.mult),
                                 reads=[("pex", b), "Etab"], writes=[("pT", b)])
                        for ci, (Tk, v) in enumerate(chunks):
                            rhs = pex[b][:, ci, :] if v is None else pT[b][:, ci - 2, :]
                            rk = [("pex", b)] if v is None else [("pT", b)]
                            mm(S, pso[b][:, 0, :], va[:, Tk, j * P:(j + 1) * P], rhs, ci == 0, ci == nk - 1,
                               reads=[("va", Tk)] + rk, writes=[("pso", b)])
                        for ci, (Tk, v) in enumerate(chunks):
                            rhs = pex[b][:, ci, :] if v is None else pT[b][:, ci - 2, :]
                            rk = [("pex", b)] if v is None else [("pT", b)]
                            mm(S, pso[b][:, 1, :], self.ones_bf[:], rhs, ci == 0, ci == nk - 1,
                               reads=["ones_bf"] + rk, writes=[("pso", b)])
                        S.op("dve", lambda e: e.reciprocal(out=rc[b][pb_:pb_ + 64, :], in_=pso[b][pb_:pb_ + 64, 1, :]),
                             reads=[("pso", b)], writes=[("rc", b)])
                        S.op("dve", lambda e: e.tensor_tensor(out=zb[pb_:pb_ + 64, :], in0=pso[b][pb_:pb_ + 64, 0, :],
                                                              in1=rc[b][pb_:pb_ + 64, :], op=ALU.mult),
                             reads=[("pso", b), ("rc", b)], writes=[zk])
                    S.dma("sp", self.zT_d[4 + j, :, T * P:(T + 1) * P], zb[:], reads=[zk], writes=[("zT", 4 + j, T)])

    def phase_out_mlp(self, layer, V, w_out_ap, y_tile_fn, dst_fn):
        S = self.S
        with S.scope():
            wo = S.sb("wo", [P, 8, D], BF16)
            for c in range(8):
                S.dma("pool", wo[:, c, :], w_out_ap[c * P:(c + 1) * P, :], writes=[("wo", c)])
            w1 = S.sb("w1", [P, 8, 4 * D], BF16)
            w2 = S.sb("w2", [P, 32, D], BF16)
            for c in range(8):
                S.dma("pool", w1[:, c, :], self.mlp_w1[layer, c * P:(c + 1) * P, :], writes=[("w1", c)])
            for f in range(0, 32, 4):
                S.dma("pool", w2[:, f:f + 4, :], self.mlp_w2[layer, f * P:(f + 4) * P, :].rearrange("(n p) d -> p n d", p=P),
                      writes=[("w2", f + q) for q in range(4)])
            NB = self.make_norm_bufs("nm", nb=1)
            zt = [S.sb(f"zt{i}", [P, 8, P], BF16) for i in range(2)]
            xt = [S.sb(f"xo{i}", [P, D], F32) for i in range(2)]
            x1 = [S.sb(f"x1_{i}", [P, D], F32) for i in range(2)]
            tmp = [S.sb(f"tg{i}", [P, D], F32) for i in range(2)]
            sq = NB["sq"][0]
            stt = [S.sb(f"ost{i}", [P, 4], F32) for i in range(4)]
            hT = [S.sb(f"hm{i}", [P, 8, 256], BF16) for i in range(1)] * 2
            py = [[S.ps(f"py{t}_{hf}", [P, 512], F32) for hf in range(2)] for t in range(2)]
            pa = [S.ps(f"pa{i}", [P, 256], F32) for i in range(2)]
            r32 = [S.sb(f"r32_{i}", [P, 256], F32) for i in range(2)]
            aT = [S.sb(f"aT{i}", [P, 256], BF16) for i in range(2)]
            ist = 0

            def norm_gate_res(t, G, xin, xin_key, xout, xout_key):
                nonlocal ist
                st = stt[ist % 4]; sk = ("ost", ist % 4)
                ist += 1
                S.op("pool", lambda e: e.memset(st[:], 0.0), writes=[sk])
                for hf in range(2):
                    S.op("act", lambda e: e.activation(out=sq[:, hf * 512:(hf + 1) * 512], in_=py[t][hf][:], func=AF.Square,
                                                       accum_out=st[:, hf:hf + 1]), reads=[("py", t, hf)], writes=[("nm", "sq", 0), sk])
                S.op("dve", lambda e: e.tensor_tensor(out=st[:, 2:3], in0=st[:, 0:1], in1=st[:, 1:2], op=ALU.add),
                     reads=[sk], writes=[sk])
                S.op("act", lambda e: e.activation(out=st[:, 2:3], in_=st[:, 2:3], func=AF.Sqrt, scale=1.0 / D,
                                                   bias=self.eps_t[:, 0:1]), reads=[sk, "eps_t"], writes=[sk])
                S.op("dve", lambda e: e.reciprocal(out=st[:, 3:4], in_=st[:, 2:3]), reads=[sk], writes=[sk])
                tb = tmp[t]; tk = ("tg", t)
                for hf in range(2):
                    S.op("dve", lambda e: e.scalar_tensor_tensor(out=tb[:, hf * 512:(hf + 1) * 512], in0=py[t][hf][:],
                                                                 scalar=st[:, 3:4], in1=G[0][:, hf * 512:(hf + 1) * 512],
                                                                 op0=ALU.mult, op1=ALU.mult),
                         reads=[("py", t, hf), sk, G[1]], writes=[tk])
                S.op("pool", lambda e: e.tensor_tensor(out=xout, in0=tb[:], in1=xin, op=ALU.add),
                     reads=[tk, xin_key], writes=[xout_key])

            ia = 0
            for sidx in range(NT // 2):
                T0 = 2 * sidx
                s = 1 if T0 < 2 else 0
                if layer == 1 and s == 1:
                    continue
                hb = hT[0]; hk = ("hm", 0)
                for t in range(2):
                    T = T0 + t
                    src, skey = self.x_src(layer, T)
                    S.dma("sp", xt[t][:], src, reads=[skey] if skey else [], writes=[("xo", t)])
                    y_tile_fn(T, t, zt[t], ("zt", t), wo, py[t])
                    norm_gate_res(t, V[("G", 0, s)], xt[t][:], ("xo", t), x1[t][:], ("x1", t))
                    self.norm_to_hT(NB, x1[t][:], ("x1", t), V[("A", 1, s)], V[("B", 1, s)],
                                    hb[:, :, t * P:(t + 1) * P], (hk, t))
                def mm1(f):
                    a = (ia + f) % 2
                    for c in range(8):
                        mm(S, pa[a][:], w1[:, c, f * P:(f + 1) * P], hb[:, c, :], c == 0, c == 7,
                           reads=[("w1", c), (hk, 0), (hk, 1)], writes=[("pa", a)])
                    S.op("act", lambda e: e.activation(out=r32[a][:], in_=pa[a][:], func=AF.Relu),
                         reads=[("pa", a)], writes=[("r32", a)])
                    S.op("dve", lambda e: e.tensor_tensor(out=aT[a][:], in0=r32[a][:], in1=r32[a][:], op=ALU.mult),
                         reads=[("r32", a)], writes=[("aT", a)])

                def mm2(f):
                    a = (ia + f) % 2
                    for t in range(2):
                        for hf in range(2):
                            mm(S, py[t][hf][:], aT[a][:, t * P:(t + 1) * P], w2[:, f, hf * 512:(hf + 1) * 512],
                               f == 0, f == 31, reads=[("aT", a), ("w2", f)], writes=[("py", t, hf)])
                mm1(0)
                for f in range(32):
                    if f + 1 < 32:
                        mm1(f + 1)
                    mm2(f)
                for t in range(2):
                    T = T0 + t
                    norm_gate_res(t, V[("G", 1, s)], x1[t][:], ("x1", t), tmp[t][:], ("tg", t))
                    dst, dkey = dst_fn(T)
                    S.dma("sp", dst, tmp[t][:], reads=[("tg", t)], writes=[dkey])

    def y_tile_L0(self, T, t, zt, zk, wo, py):
        S = self.S
        S.dma("sp", zt[:], self.zT_d[:, :, T * P:(T + 1) * P].rearrange("c p t -> p c t"),
              reads=[("zT", c, T) for c in range(8)], writes=[zk])
        for hf in range(2):
            for c in range(8):
                mm(S, py[hf][:], zt[:, c, :], wo[:, c, hf * 512:(hf + 1) * 512], c == 0, c == 7,
                   reads=[zk, ("wo", c)], writes=[("py", t, hf)])


def build_program(stop_after=None, debug=()):
    nc = bass.Bass("TRN2", target_bir_lowering=False)
    Pg = Prog(nc, debug)
    S = Pg.S
    Pg.consts()
    Pg.phase_mod()
    final_keys = []
    with S.scope():
        V0 = Pg.load_layer_vecs(0)
        Pg.phase_L0_proj(V0)
        Pg.phase_L0_pool()
        Pg.phase_L0_attn()

        def dst0(T):
            if stop_after == "L0":
                if T < 2:
                    return Pg.x_d[T * P:(T + 1) * P, :], ("x_d", T)
                return Pg.out[(T - 2) * P:(T - 1) * P, :], ("out", T)
            return Pg.x_d[T * P:(T + 1) * P, :], ("x_d", T)
        Pg.phase_out_mlp(0, V0, Pg.ev_w_out, Pg.y_tile_L0, dst0)
    S.barrier()
    S.finish([])
    S.close()
    return nc, Pg


def host_inputs(inputs):
    f = lambda a: np.ascontiguousarray(np.asarray(a, dtype=np.float32))
    dr_idx, dc_full, mask = _attn_tables()
    rpb = f(inputs["ev_rpb"])[0]
    tab = rpb[:, dr_idx, dc_full[None, :, :]]
    tab = np.ascontiguousarray(tab.transpose(2, 0, 1, 3).reshape(128, 96, 128))
    msk = np.ascontiguousarray(mask.transpose(1, 0, 2))
    bands = np.ascontiguousarray(_pool_bands().transpose(2, 0, 1, 3).reshape(128, 20, 128))
    shared = {
        "ada_w": f(inputs["ada_w"]), "ada_b": f(inputs["ada_b"]), "norm_g": f(inputs["norm_g"]),
        "mlp_w1": f(inputs["mlp_w1"]), "mlp_w2": f(inputs["mlp_w2"]),
        "ev_w_in": f(inputs["ev_w_in"])[0], "ev_w_out": f(inputs["ev_w_out"])[0],
        "ev_pool_w": f(inputs["ev_pool_w"])[0], "ev_pool_scale": f(inputs["ev_pool_scale"])[0],
        "rpb_tab": tab, "msk_tab": msk, "bands": bands, "ident": np.eye(128, dtype=np.float32),
    }
    x = f(inputs["x"]); c = f(inputs["c"]); ctx = f(inputs["ctx"]); cc = f(inputs["c_ctx"])
    maps = []
    for b in range(x.shape[0]):
        cv = np.stack([c[b].reshape(8, 128).T, cc.reshape(8, 128).T], axis=-1)
        m = dict(shared)
        m.update({"x": x[b], "ctx": ctx[b], "cvec": np.ascontiguousarray(cv)})
        maps.append(m)
    return maps


_CACHE = {}


def kernel(**inputs):
    maps = host_inputs(inputs)
    if "nc" not in _CACHE:
        _CACHE["nc"] = build_program()
    nc, Pg = _CACHE["nc"]
    res = run_bass_kernel_spmd(nc, maps, core_ids=list(range(8)))
    return np.stack([np.asarray(r["out"]) for r in res.results], axis=0)

LWC = -0.6065306597126334
GN_EPS = 64e-5


def _scan_consts():
    s = np.arange(128)[:, None]
    t = np.arange(128)[None, :]
    tri = np.stack([(s <= t), (s >= t)]).astype(np.float32)
    strict = np.stack([(s < t), (s > t)]).astype(np.float32)
    mT = strict.transpose(0, 2, 1)
    m4 = np.concatenate([tri, strict, tri, mT], axis=2)
    lm = []
    for l in range(7):
        b = 1 << l
        lm.append(((s // (2 * b)) == (t // (2 * b))) & (((s // b) % 2) == 0) & (((t // b) % 2) == 1))
    lm = np.stack(lm).astype(np.float32)
    lmN = np.stack([lm, lm.transpose(0, 2, 1)]) + np.eye(128, dtype=np.float32)[None, None]
    return tri, m4, np.ascontiguousarray(lmN)


def _tt(S, eng, out, a, b, op, reads, writes):
    return S.op(eng, lambda e: e.tensor_tensor(out=out, in0=a, in1=b, op=op), reads=reads, writes=writes)


def _stt(S, eng, out, a, sc, b, op0, op1, reads, writes):
    return S.op("dve", lambda e: e.scalar_tensor_tensor(out=out, in0=a, scalar=sc, in1=b, op0=op0, op1=op1),
                reads=reads, writes=writes)


def _act(S, out, in_, func, reads, writes, **kw):
    return S.op("act", lambda e: e.activation(out=out, in_=in_, func=func, **kw), reads=reads, writes=writes)


def _h3(ap):
    return ap.rearrange("p (h k) -> p h k", k=64)


class Prog1(Prog):
    def __init__(self, nc, debug=()):
        super().__init__(nc, debug)
        dt = nc.dram_tensor
        I = lambda name, shape: dt(name, list(shape), F32, kind="ExternalInput").ap()
        self.rw_mu = I("rw_mu", [6, D])
        self.rw_wr = I("rw_wr", [D, D]); self.rw_wk = I("rw_wk", [D, D])
        self.rw_wv = I("rw_wv", [D, D]); self.rw_wo = I("rw_wo", [D, D])
        self.rw_w0 = I("rw_w0", [2, D]); self.rw_a0 = I("rw_a0", [2, D])
        self.w1cat = I("w1cat", [D, P]); self.a1cat = I("a1cat", [D, P]); self.rw_g1 = I("rw_g1", [D, P])
        self.w2cat = I("w2cat", [P, D]); self.a2cat = I("a2cat", [P, D]); self.rw_g2 = I("rw_g2", [P, D])
        self.rw_kk = I("rw_kk", [1, D]); self.rw_ka = I("rw_ka", [1, D]); self.rw_rk = I("rw_rk", [1, D])
        self.rw_lng = I("rw_lng", [1, D]); self.rw_lnb = I("rw_lnb", [1, D])
        self.tri_c = I("tri_c", [2, P, P]); self.m4_c = I("m4_c", [2, P, 512]); self.lmT_c = I("lmT_c", [2, 7, P, P])
        X = lambda name, shape, d=F32: (dt(name, list(shape), d, kind="ExternalOutput").ap() if name in debug
                                        else dt(name, list(shape), d).ap())
        self.hT_d = X("hT_d", [8, P, NTOK])
        self.featT_d = X("featT_d", [2, NT, P, 8 * 4 * P], BF16)
        self.vtok_d = X("vtok_d", [NTOK, D], BF16)
        self.bk_d = X("bk_d", [2, NT, P, 2 * D], BF16)
        self.gC_d = X("gC_d", [2, NT, P, 8])
        self.g_d = X("g_d", [NTOK, D])
        self.bonus_d = X("bonus_d", [NTOK, D])
        self.y_d = X("y_d", [2, NTOK, D])

    def phase_R0(self, V):
        S = self.S
        with S.scope():
            NB = self.make_norm_bufs("r0")
            xt = [S.sb(f"r0x{i}", [P, D], F32) for i in range(2)]
            ho = [S.sb(f"r0h{i}", [P, 8, P], F32) for i in range(2)]
            for T in range(NT):
                s = 1 if T < 2 else 0
                b = T % 2
                src, sk = self.x_src(1, T)
                S.dma("sp", xt[b][:], src, reads=[sk], writes=[("r0x", b)])
                self.norm_to_hT(NB, xt[b][:], ("r0x", b), V[("A", 0, s)], V[("B", 0, s)], ho[b][:], ("r0h", b))
                S.dma("sp", self.hT_d[:, :, T * P:(T + 1) * P].rearrange("c p t -> p c t"), ho[b][:],
                      reads=[("r0h", b)], writes=[("hT_d", T)])

    def phase_R1(self):
        S = self.S
        with S.scope():
            W = {}
            for nm, src in (("wr", self.rw_wr), ("wk", self.rw_wk), ("wv", self.rw_wv)):
                W[nm] = S.sb(nm, [P, 8, D], BF16)
                for c in range(0, 8, 4):
                    S.dma("pool", W[nm][:, c:c + 4, :], src[c * P:(c + 4) * P, :].rearrange("(c p) n -> p c n", p=P), writes=[nm])
            for nm, src in (("w1c", self.w1cat), ("a1c", self.a1cat), ("g1", self.rw_g1)):
                W[nm] = S.sb(nm, [P, 8, P], BF16)
                S.dma("pool", W[nm][:], src.rearrange("(c p) n -> p c n", p=P), writes=[nm])
            for nm, src in (("w2c", self.w2cat), ("a2c", self.a2cat), ("g2", self.rw_g2)):
                W[nm] = S.sb(nm, [P, D], BF16)
                S.dma("pool", W[nm][:], src, writes=[nm])
            R = {}
            for nm, src in (("kk_r", self.rw_kk), ("ka_r", self.rw_ka), ("rk_r", self.rw_rk),
                            ("w0_0", self.rw_w0[0:1, :]), ("w0_1", self.rw_w0[1:2, :]),
                            ("a0_0", self.rw_a0[0:1, :]), ("a0_1", self.rw_a0[1:2, :])):
                R[nm] = S.sb(nm, [P, D], F32)
                S.dma("sp", R[nm][:], src.broadcast_to([P, D]), writes=[nm])
            mu = S.sb("mu", [P, 6, 8], F32)
            with self.nc.allow_non_contiguous_dma(reason="tiny"):
                S.dma("sp", mu[:], self.rw_mu.rearrange("j (c p) -> p j c", p=P), writes=["mu"])
            tri = S.sb("tri", [P, 2, P], F32)
            S.dma("sp", tri[:], self.tri_c.rearrange("d s t -> s d t"), writes=["tri"])
            onef = S.sb("onef", [P, P], F32)
            S.op("dve", lambda e: e.memset(onef[:], 1.0), writes=["onef"])
            hbuf = S.sb("hbuf", [P, 8, P + 2], F32)
            xx = S.sb("xx", [P, 8, P], F32)
            mxt = S.sb("mxt", [P, 8, P], F32)
            mix = S.sb("mix", [P, 6, 8, P], BF16)
            hid = S.sb("hid", [P, 3, P], BF16)
            F = {n: S.sb(n, [P, D], F32) for n in ("r_sb", "k_sb", "v_sb", "kkn", "tA", "tB", "lw", "tC", "tD", "kd0", "kd1", "tE", "tF", "tG", "tH")}
            ob = [S.sb(f"ob{i}", [P, D], BF16) for i in range(4)]
            vb = S.sb("vb", [P, D], BF16)
            ft = S.sb("ft", [P, 8, 4, P], BF16)
            bkt = S.sb("bkt", [P, 2, D], BF16)
            st16 = S.sb("st16", [P, 64], F32)
            gcs = S.sb("gcs", [P, 8], F32)
            pA = [[S.ps(f"pA{i}_{h}", [P, 512], F32) for h in range(2)] for i in range(2)]
            pCl = [S.ps(f"pCl{h}", [P, 512], F32) for h in range(2)]
            pF = S.ps("pF", [P, 512], F32)
            pT = S.ps("pT", [P, 8, P], BF16)
            ipa = 0

            def proj(lhs_fn, rhs, rkey, K0=0, K=P, nchunks=8, lkeys=()):
                nonlocal ipa
                i = ipa % 2
                ipa += 1
                for hf in range(2):
                    for c in range(nchunks):
                        mm(S, pA[i][hf][:], lhs_fn(c), rhs(c, hf), c == 0, c == nchunks - 1,
                           reads=list(lkeys) + [rkey], writes=[("pA", i, hf)])
                return pA[i], [("pA", i, 0), ("pA", i, 1)]

            def evac2(fn_half):
                for hf in range(2):
                    fn_half(hf, slice(hf * 512, (hf + 1) * 512))

            for T in range(NT):
                seq_lo, seq_hi = (0, NCTX) if T < 2 else (NCTX, NTOK)
                t0 = T * P
                lo = max(t0 - 1, seq_lo); hi = min(t0 + P + 1, seq_hi)
                if lo > t0 - 1:
                    S.op("pool", lambda e: e.memset(hbuf[:, :, 0:1], 0.0), writes=["hbuf"])
                if hi < t0 + P + 1:
                    S.op("pool", lambda e: e.memset(hbuf[:, :, P + 1:P + 2], 0.0), writes=["hbuf"])
                S.dma("sp", hbuf[:, :, lo - (t0 - 1):hi - (t0 - 1)], self.hT_d[:, :, lo:hi].rearrange("c p t -> p c t"),
                      reads=[("hT_d", q) for q in range(max(T - 1, 0), min(T + 2, NT))], writes=["hbuf"])
                _tt(S, "dve", xx[:], hbuf[:, :, 0:P], hbuf[:, :, 2:P + 2], ALU.add, ["hbuf"], ["xx"])
                _stt(S, "dve", xx[:], xx[:], 0.5, hbuf[:, :, 1:P + 1], ALU.mult, ALU.subtract, ["xx", "hbuf"], ["xx"])
                for j in range(6):
                    _tt(S, "dve", mxt[:], xx[:], mu[:, j, :][:, :, None].broadcast_to([P, 8, P]), ALU.mult, ["xx", "mu"], ["mxt"])
                    _tt(S, "dve", mix[:, j, :, :], mxt[:], hbuf[:, :, 1:P + 1], ALU.add, ["mxt", "hbuf"], [("mix", j)])
                for hi_, (wn, mj, fn) in enumerate((("w1c", 1, AF.Tanh), ("a1c", 4, AF.Copy), ("g1", 5, AF.Sigmoid))):
                    for c in range(8):
                        mm(S, pF[:, 0:P], W[wn][:, c, :], mix[:, mj, c, :], c == 0, c == 7,
                           reads=[wn, ("mix", mj)], writes=["pF"])
                    _act(S, hid[:, hi_, :], pF[:, 0:P], fn, ["pF"], [("hid", hi_)])
                for nm, mj, wn in (("r_sb", 0, "wr"), ("k_sb", 2, "wk"), ("v_sb", 3, "wv")):
                    ps, pk = proj(lambda c: mix[:, mj, c, :], lambda c, hf: W[wn][:, c, hf * 512:(hf + 1) * 512], wn,
                                  lkeys=[("mix", mj)])
                    evac2(lambda hf, sl: _act(S, F[nm][:, sl], ps[hf][:], AF.Copy, [pk[hf]], [nm]))
                S.op("pool", lambda e: e.tensor_copy(out=vb[:], in_=F["v_sb"][:]), reads=["v_sb"], writes=["vb"])
                S.dma("sp", self.vtok_d[t0:t0 + P, :], vb[:], reads=["vb"], writes=[("vtok", T)])
                ps, pk = proj(lambda c: hid[:, 2, :], lambda c, hf: W["g2"][:, hf * 512:(hf + 1) * 512], "g2", nchunks=1,
                              lkeys=[("hid", 2)])
                evac2(lambda hf, sl: _act(S, F["tA"][:, sl], ps[hf][:], AF.Copy, [pk[hf]], ["tA"]))
                S.dma("sp", self.g_d[t0:t0 + P, :], F["tA"][:], reads=["tA"], writes=[("g_d", T)])
                _tt(S, "dve", F["tA"][:], F["k_sb"][:], R["kk_r"][:], ALU.mult, ["k_sb", "kk_r"], ["tA"])
                _tt(S, "pool", F["tB"][:], F["tA"][:], F["tA"][:], ALU.mult, ["tA"], ["tB"])
                S.op("dve", lambda e: e.tensor_reduce(out=st16[:, 0:16], in_=_h3(F["tB"][:]), axis=AX.X, op=ALU.add),
                     reads=["tB"], writes=["st16"])
                S.op("dve", lambda e: e.tensor_scalar(out=st16[:, 0:16], in0=st16[:, 0:16], scalar1=1e-24, scalar2=None, op0=ALU.max),
                     reads=["st16"], writes=["st16"])
                _act(S, st16[:, 0:16], st16[:, 0:16], AF.Sqrt, ["st16"], ["st16"])
                S.op("dve", lambda e: e.reciprocal(out=st16[:, 16:32], in_=st16[:, 0:16]), reads=["st16"], writes=["st16"])
                _tt(S, "dve", _h3(F["kkn"][:]), _h3(F["tA"][:]), st16[:, 16:32][:, :, None].broadcast_to([P, 16, 64]), ALU.mult,
                    ["tA", "st16"], ["kkn"])
                for d in range(2):
                    ps, pk = proj(lambda c: hid[d * 64:(d + 1) * 64, 0, :], lambda c, hf: W["w2c"][d * 64:(d + 1) * 64, hf * 512:(hf + 1) * 512],
                                  "w2c", nchunks=1, lkeys=[("hid", 0)])
                    evac2(lambda hf, sl: _tt(S, "dve", F["tB"][:, sl], ps[hf][:], R[f"w0_{d}"][:, sl], ALU.add, [pk[hf], f"w0_{d}"], ["tB"]))
                    _act(S, F["tB"][:], F["tB"][:], AF.Sigmoid, ["tB"], ["tB"])
                    _act(S, F["lw"][:], F["tB"][:], AF.Copy, ["tB"], ["lw"], scale=LWC)
                    ps, pk = proj(lambda c: hid[d * 64:(d + 1) * 64, 1, :], lambda c, hf: W["a2c"][d * 64:(d + 1) * 64, hf * 512:(hf + 1) * 512],
                                  "a2c", nchunks=1, lkeys=[("hid", 1)])
                    evac2(lambda hf, sl: _tt(S, "dve", F["tC"][:, sl], ps[hf][:], R[f"a0_{d}"][:, sl], ALU.add, [pk[hf], f"a0_{d}"], ["tC"]))
                    _act(S, F["tC"][:], F["tC"][:], AF.Sigmoid, ["tC"], ["tC"])
                    kd = F[f"kd{d}"]; kdk = f"kd{d}"
                    _stt(S, "dve", F["tD"][:], F["tC"][:], -1.0, R["ka_r"][:], ALU.add, ALU.mult, ["tC", "ka_r"], ["tD"])
                    _stt(S, "pool", kd[:], F["tD"][:], 1.0, F["k_sb"][:], ALU.add, ALU.mult, ["tD", "k_sb"], [kdk])
                    _tt(S, "pool", F["tC"][:], F["kkn"][:], F["tC"][:], ALU.mult, ["kkn", "tC"], ["tC"])
                    for hf in range(2):
                        mm(S, pCl[hf][:], tri[:, d, :], F["lw"][:, hf * 512:(hf + 1) * 512], True, True,
                           reads=["tri", "lw"], writes=[("pCl", hf)])
                    evac2(lambda hf, sl: _act(S, F["tE"][:, sl], pCl[hf][:], AF.Exp, [("pCl", hf)], ["tE"]))
                    evac2(lambda hf, sl: _act(S, F["tF"][:, sl], pCl[hf][:], AF.Exp, [("pCl", hf)], ["tF"], scale=-1.0))
                    for hf in range(2):
                        mm(S, pCl[hf][:], onef[:], F["lw"][:, hf * 512:(hf + 1) * 512], True, True,
                           reads=["onef", "lw"], writes=[("pCl", hf)])
                    evac2(lambda hf, sl: _act(S, F["tH"][:, sl], pCl[hf][:], AF.Exp, [("pCl", hf)], ["tH"]))
                    _act(S, F["tG"][:], F["lw"][:], AF.Exp, ["lw"], ["tG"], scale=-1.0)
                    _tt(S, "dve", F["tG"][:], F["tG"][:], F["tE"][:], ALU.mult, ["tG", "tE"], ["tG"])
                    _tt(S, "pool", F["tH"][:], F["tH"][:], F["tF"][:], ALU.mult, ["tH", "tF"], ["tH"])
                    for j in range(8):
                        mm(S, pF[:, 256 + j:257 + j], F["lw"][:, j * P:(j + 1) * P], onef[:, 0:1], True, True,
                           reads=["lw", "onef"], writes=["pF"])
                    _act(S, gcs[:], pF[:, 256:264], AF.Exp, ["pF"], ["gcs"])
                    S.dma("sp", self.gC_d[d, T], gcs[:], reads=["gcs"], writes=[("gC_d", d, T)])
                    _stt(S, "dve", ob[0][:], F["kkn"][:], -1.0, F["tG"][:], ALU.mult, ALU.mult, ["kkn", "tG"], [("ob", 0)])
                    _tt(S, "pool", ob[1][:], F["r_sb"][:], F["tE"][:], ALU.mult, ["r_sb", "tE"], [("ob", 1)])
                    _tt(S, "dve", ob[2][:], F["tC"][:], F["tF"][:], ALU.mult, ["tC", "tF"], [("ob", 2)])
                    _tt(S, "pool", ob[3][:], kd[:], F["tF"][:], ALU.mult, [kdk, "tF"], [("ob", 3)])
                    _tt(S, "dve", bkt[:, 0, :], F["tC"][:], F["tH"][:], ALU.mult, ["tC", "tH"], ["bkt"])
                    _tt(S, "pool", bkt[:, 1, :], kd[:], F["tH"][:], ALU.mult, [kdk, "tH"], ["bkt"])
                    S.dma("sp", self.bk_d[d, T], bkt[:].rearrange("p a n -> p (a n)"), reads=["bkt"], writes=[("bk_d", d, T)])
                    for q in range(4):
                        for c in range(8):
                            S.op("pe", lambda e: e.transpose(out=pT[:, c, :], in_=ob[q][:, c * P:(c + 1) * P], identity=self.idb[:]),
                                 reads=[("ob", q), "idb"], writes=["pT"], accum=(c > 0))
                        if q % 2 == 0:
                            _act(S, ft[:, :, q, :], pT[:], AF.Copy, ["pT"], ["ft"])
                        else:
                            S.op("dve", lambda e: e.tensor_copy(out=ft[:, :, q, :], in_=pT[:]), reads=["pT"], writes=["ft"])
                    S.dma("sp", self.featT_d[d, T], ft[:].rearrange("p j q t -> p (j q t)"), reads=["ft"], writes=[("featT_d", d, T)])
                _tt(S, "pool", F["tD"][:], F["kd0"][:], F["kd1"][:], ALU.add, ["kd0", "kd1"], ["tD"])
                _tt(S, "pool", F["tD"][:], F["tD"][:], F["r_sb"][:], ALU.mult, ["tD", "r_sb"], ["tD"])
                _tt(S, "pool", F["tD"][:], F["tD"][:], R["rk_r"][:], ALU.mult, ["tD", "rk_r"], ["tD"])
                S.op("dve", lambda e: e.tensor_reduce(out=st16[:, 32:48], in_=_h3(F["tD"][:]), axis=AX.X, op=ALU.add),
                     reads=["tD"], writes=["st16"])
                _tt(S, "dve", _h3(F["tD"][:]), _h3(F["v_sb"][:]), st16[:, 32:48][:, :, None].broadcast_to([P, 16, 64]), ALU.mult,
                    ["v_sb", "st16"], ["tD"])
                S.dma("sp", self.bonus_d[t0:t0 + P, :], F["tD"][:], reads=["tD"], writes=[("bonus_d", T)])

    def phase_R2(self):
        S = self.S
        with S.scope():
            m4 = S.sb("m4", [P, 2, 512], F32)
            lmN = S.sb("lmN", [P, 2, 7, P], F32)
            S.dma("sp", m4[:], self.m4_c.rearrange("d s n -> s d n"), writes=["m4"])
            S.dma("sp", lmN[:], self.lmT_c.rearrange("d l s n -> s d l n"), writes=["lmN"])
            idb = self.idb
            NG = 4
            ST32 = [S.sb(f"ST32_{d}", [P, 8, 64], F32) for d in range(2)]
            STb = [S.sb(f"STb_{d}", [P, 8, 64], BF16) for d in range(2)]
            for d in range(2):
                S.op("dve", lambda e: e.memset(ST32[d][:], 0.0), writes=[("ST32", d)])
                S.op("dve", lambda e: e.memset(STb[d][:], 0.0), writes=[("STb", d)])
            NBUF = 3
            Fb = [S.sb(f"Fb{i}", [P, 8, 4, P], BF16) for i in range(NBUF)]
            Vb = [S.sb(f"Vb{i}", [P, D], BF16) for i in range(NBUF)]
            BKb = [S.sb(f"BKb{i}", [P, 2, D], BF16) for i in range(NBUF)]
            gCb = [S.sb(f"gCb{i}", [P, 8], F32) for i in range(NBUF)]
            ysb = [S.sb(f"ysb{i}", [P, D], F32) for i in range(NBUF)]
            SL = []
            for sl in range(2):
                R_ = dict(
                    GM=S.sb(f"GM{sl}", [P, NG, 512], BF16),
                    X=[S.sb(f"X{sl}_{i}", [P, NG, P], BF16) for i in range(2)],
                    XT=[S.sb(f"XT{sl}_{i}", [P, NG, P], BF16) for i in range(2)],
                    T1s=S.sb(f"T1s{sl}", [P, NG, P], BF16),
                    Zq=S.sb(f"Zq{sl}", [P, NG, 64], BF16),
                    Pb=S.sb(f"Pb{sl}", [P, NG, 64], BF16),
                    bk=[S.ps(f"bk{sl}_{i}", [P, NG, P], F32) for i in range(3)],
                    bz=S.ps(f"bz{sl}", [P, 8, 64], F32),
                    sl=sl)
                SL.append(R_)

            items = []
            it = 0
            for d in range(2):
                order = list(range(NT)) if d == 0 else [1, 0] + list(range(NT - 1, 1, -1))
                for ci, T in enumerate(order):
                    for g0 in range(0, 16, NG):
                        items.append(dict(d=d, T=T, g0=g0, b=it % NBUF))
                    it += 1

            def heads_of(g0):
                return [(g, g0 + g, (g0 + g) // 2, ((g0 + g) % 2) * 64) for g in range(NG)]

            def load_chunk(w):
                d, T, b = w["d"], w["T"], w["b"]
                S.dma("sp", Fb[b][:].rearrange("p j q t -> p (j q t)"), self.featT_d[d, T], reads=[("featT_d", d, T)], writes=[("Fb", b)])
                S.dma("sp", Vb[b][:], self.vtok_d[T * P:(T + 1) * P, :], reads=[("vtok", T)], writes=[("Vb", b)])
                S.dma("sp", BKb[b][:].rearrange("p a n -> p (a n)"), self.bk_d[d, T], reads=[("bk_d", d, T)], writes=[("BKb", b)])
                S.dma("sp", gCb[b][:], self.gC_d[d, T], reads=[("gC_d", d, T)], writes=[("gCb", b)])

            def run_group(w, R_):
                d, T, b, g0, sl = w["d"], w["T"], w["b"], w["g0"], R_["sl"]
                if g0 == 0:
                    load_chunk(w)
                Fk, Vk, BKk, gk = ("Fb", b), ("Vb", b), ("BKb", b), ("gCb", b)
                GM, X, XT, T1s, Zq, Pb, bk, bz = (R_[n] for n in ("GM", "X", "XT", "T1s", "Zq", "Pb", "bk", "bz"))
                K = lambda n, *a: (n, sl) + a
                hs = heads_of(g0)
                F_ = Fb[b]
                st32, stb = ST32[d], STb[d]
                for (g, h, j, pb_) in hs:
                    bank = bk[g % 3]; bkk = K("bk", g % 3)
                    bv = bank[:].rearrange("p g t -> p (g t)")
                    AR = F_[pb_:pb_ + 64, j, 0:2, :].rearrange("p q t -> p (q t)")
                    mm(S, bv[:, 0:128], F_[pb_:pb_ + 64, j, 2, :], F_[pb_:pb_ + 64, j, 1, :], True, True, reads=[Fk], writes=[bkk])
                    mm(S, bv[:, 128:384], F_[pb_:pb_ + 64, j, 3, :], AR, True, True, reads=[Fk], writes=[bkk])
                    mm(S, bv[:, 384:512], F_[pb_:pb_ + 64, j, 0, :], F_[pb_:pb_ + 64, j, 2, :], True, True, reads=[Fk], writes=[bkk])
                    _tt(S, "dve", GM[:, g, :], bv, m4[:, d, :], ALU.mult, ["m4"], [bkk, K("GM")])
                yield
                for (g, h, j, pb_) in hs:
                    mm(S, bz[:, g, :], F_[pb_:pb_ + 64, j, 0, :], stb[pb_:pb_ + 64, j, :], True, False, reads=[Fk, ("STb", d)], writes=[K("bz")])
                    mm(S, bz[:, g, :], GM[:, g, 128:256], Vb[b][:, h * 64:(h + 1) * 64], False, True, reads=[K("GM"), Vk], writes=[K("bz")])
                S.op("dve", lambda e: e.tensor_copy(out=Zq[:], in_=bz[:, 0:NG, :]), reads=[], writes=[K("bz"), K("Zq")])
                yield
                xi = 0
                for (g, h, j, pb_) in hs:
                    mm(S, bk[0][:, g, :], GM[:, g, 384:512], idb[:], True, False, reads=[K("GM"), "idb"], writes=[K("bk", 0)])
                    mm(S, bk[0][:, g, :], idb[:], idb[:], False, True, reads=["idb"], writes=[K("bk", 0)])
                _tt(S, "dve", X[xi][:], bk[0][:], lmN[:, d, 0:1, :].broadcast_to([P, NG, P]), ALU.mult, ["lmN"], [K("bk", 0), K("X", xi)])
                yield
                for (g, h, j, pb_) in hs:
                    mm(S, bk[2][:, g, :], X[xi][:, g, :], idb[:], True, True, reads=[K("X", xi), "idb"], writes=[K("bk", 2)])
                _act(S, XT[xi][:], bk[2][:], AF.Copy, [], [K("bk", 2), K("XT", xi)])
                yield
                for l in range(1, 7):
                    for (g, h, j, pb_) in hs:
                        mm(S, bk[0][:, g, :], GM[:, g, 384:512], X[xi][:, g, :], True, False, reads=[K("GM"), K("X", xi)], writes=[K("bk", 0)])
                        mm(S, bk[0][:, g, :], idb[:], idb[:], False, True, reads=["idb"], writes=[K("bk", 0)])
                    _tt(S, "dve", T1s[:], bk[0][:], lmN[:, d, l:l + 1, :].broadcast_to([P, NG, P]), ALU.mult, ["lmN"], [K("bk", 0), K("T1s")])
                    yield
                    for (g, h, j, pb_) in hs:
                        mm(S, bk[1][:, g, :], XT[xi][:, g, :], T1s[:, g, :], True, True, reads=[K("XT", xi), K("T1s")], writes=[K("bk", 1)])
                    if l < 6:
                        for (g, h, j, pb_) in hs:
                            mm(S, bk[2][:, g, :], T1s[:, g, :], XT[xi][:, g, :], True, True, reads=[K("XT", xi), K("T1s")], writes=[K("bk", 2)])
                    _act(S, X[1 - xi][:], bk[1][:], AF.Copy, [], [K("bk", 1), K("X", 1 - xi)])
                    if l < 6:
                        if l % 2 == 0:
                            _act(S, XT[1 - xi][:], bk[2][:], AF.Copy, [], [K("bk", 2), K("XT", 1 - xi)])
                        else:
                            S.op("dve", lambda e: e.tensor_copy(out=XT[1 - xi][:], in_=bk[2][:]), reads=[], writes=[K("bk", 2), K("XT", 1 - xi)])
                    xi = 1 - xi
                    yield
                for (g, h, j, pb_) in hs:
                    mm(S, bz[:, g, :], X[xi][:, g, :], Zq[:, g, :], True, True, reads=[K("X", xi), K("Zq")], writes=[K("bz")])
                S.op("dve", lambda e: e.tensor_copy(out=Pb[:], in_=bz[:, 0:NG, :]), reads=[], writes=[K("bz"), K("Pb")])
                yield
                for (g, h, j, pb_) in hs:
                    yo = bz[:, g, :]
                    mm(S, yo, GM[:, g, 0:128], Pb[:, g, :], True, False, reads=[K("GM"), K("Pb")], writes=[K("bz")])
                    mm(S, yo, GM[:, g, 256:384], Vb[b][:, h * 64:(h + 1) * 64], False, False, reads=[K("GM"), Vk], writes=[K("bz")])
                    mm(S, yo, F_[pb_:pb_ + 64, j, 1, :], stb[pb_:pb_ + 64, j, :], False, True, reads=[Fk, ("STb", d)], writes=[K("bz")])
                for (g, h, j, pb_) in hs:
                    mm(S, bz[:, 4 + g, :], BKb[b][:, 0, j * P:(j + 1) * P], Pb[:, g, :], True, False, reads=[BKk, K("Pb")], writes=[K("bz")])
                    mm(S, bz[:, 4 + g, :], BKb[b][:, 1, j * P:(j + 1) * P], Vb[b][:, h * 64:(h + 1) * 64], False, True,
                       reads=[BKk, Vk], writes=[K("bz")])
                _act(S, ysb[b][:, g0 * 64:(g0 + NG) * 64].rearrange("p (g v) -> p g v", v=64), bz[:, 0:NG, :], AF.Copy, [], [K("bz"), ("ysb", b)])
                for (g, h, j, pb_) in hs:
                    _stt(S, "dve", st32[pb_:pb_ + 64, j, :], st32[pb_:pb_ + 64, j, :], gCb[b][pb_:pb_ + 64, j:j + 1],
                         bz[pb_:pb_ + 64, 4 + g, :], ALU.mult, ALU.add, [gk], [("ST32", d), K("bz")])
                S.op("pool", lambda e: e.tensor_copy(out=stb[:, g0 // 2:g0 // 2 + 2, :], in_=st32[:, g0 // 2:g0 // 2 + 2, :]),
                     reads=[("ST32", d)], writes=[("STb", d)])
                if g0 + NG == 16:
                    S.dma("sp", self.y_d[d, T * P:(T + 1) * P, :], ysb[b][:], reads=[("ysb", b)], writes=[("y_d", d, T)])
                yield

            nxt = 0
            active = [None, None]
            while True:
                progressed = False
                for sl in range(2):
                    if active[sl] is None and nxt < len(items):
                        active[sl] = run_group(items[nxt], SL[sl])
                        nxt += 1
                    if active[sl] is not None:
                        progressed = True
                        try:
                            next(active[sl])
                        except StopIteration:
                            active[sl] = None
                if not progressed:
                    break

    def phase_R3(self):
        S = self.S
        with S.scope():
            R = {}
            for nm, src in (("lng_r", self.rw_lng), ("lnb_r", self.rw_lnb)):
                R[nm] = S.sb(nm, [P, D], F32)
                S.dma("sp", R[nm][:], src.broadcast_to([P, D]), writes=[nm])
            B = [{n: S.sb(f"{n}{i}", [P, D], F32) for n in ("yf", "yb", "gg", "bo")} for i in range(2)]
            zb = [S.sb(f"zb{i}", [P, D], BF16) for i in range(2)]
            zt = [S.sb(f"zt3_{i}", [P, 8, P], BF16) for i in range(2)]
            st = [S.sb(f"st3_{i}", [P, 64], F32) for i in range(2)]
            pT = [S.ps(f"pT3_{i}", [P, 8, P], BF16) for i in range(2)]
            for T in range(2, NT):
                b = T % 2
                Bf = B[b]
                k = lambda n: (n, b)
                t0 = T * P
                S.dma("sp", Bf["yf"][:], self.y_d[0, t0:t0 + P, :], reads=[("y_d", 0, T)], writes=[k("yf")])
                S.dma("sp", Bf["yb"][:], self.y_d[1, t0:t0 + P, :], reads=[("y_d", 1, T)], writes=[k("yb")])
                S.dma("sp", Bf["gg"][:], self.g_d[t0:t0 + P, :], reads=[("g_d", T)], writes=[k("gg")])
                S.dma("sp", Bf["bo"][:], self.bonus_d[t0:t0 + P, :], reads=[("bonus_d", T)], writes=[k("bo")])
                y = Bf["yf"]; t2 = Bf["yb"]
                _tt(S, "dve", y[:], y[:], t2[:], ALU.add, [k("yf"), k("yb")], [k("yf")])
                S.op("dve", lambda e: e.tensor_reduce(out=st[b][:, 0:16], in_=_h3(y[:]), axis=AX.X, op=ALU.add), reads=[k("yf")], writes=[k("st")])
                S.op("dve", lambda e: e.tensor_scalar(out=st[b][:, 0:16], in0=st[b][:, 0:16], scalar1=-1.0 / 64, scalar2=None, op0=ALU.mult),
                     reads=[k("st")], writes=[k("st")])
                _tt(S, "dve", _h3(y[:]), _h3(y[:]), st[b][:, 0:16][:, :, None].broadcast_to([P, 16, 64]), ALU.add, [k("yf"), k("st")], [k("yf")])
                _tt(S, "pool", t2[:], y[:], y[:], ALU.mult, [k("yf")], [k("yb")])
                S.op("dve", lambda e: e.tensor_reduce(out=st[b][:, 16:32], in_=_h3(t2[:]), axis=AX.X, op=ALU.add), reads=[k("yb")], writes=[k("st")])
                S.op("dve", lambda e: e.tensor_scalar(out=st[b][:, 16:32], in0=st[b][:, 16:32], scalar1=1.0 / 64, scalar2=GN_EPS, op0=ALU.mult, op1=ALU.add),
                     reads=[k("st")], writes=[k("st")])
                _act(S, st[b][:, 16:32], st[b][:, 16:32], AF.Sqrt, [k("st")], [k("st")])
                S.op("dve", lambda e: e.reciprocal(out=st[b][:, 32:48], in_=st[b][:, 16:32]), reads=[k("st")], writes=[k("st")])
                _tt(S, "dve", _h3(y[:]), _h3(y[:]), st[b][:, 32:48][:, :, None].broadcast_to([P, 16, 64]), ALU.mult, [k("yf"), k("st")], [k("yf")])
                _tt(S, "pool", y[:], y[:], R["lng_r"][:], ALU.mult, [k("yf"), "lng_r"], [k("yf")])
                _tt(S, "pool", y[:], y[:], R["lnb_r"][:], ALU.add, [k("yf"), "lnb_r"], [k("yf")])
                _tt(S, "dve", y[:], y[:], Bf["bo"][:], ALU.add, [k("yf"), k("bo")], [k("yf")])
                _tt(S, "dve", zb[b][:], y[:], Bf["gg"][:], ALU.mult, [k("yf"), k("gg")], [k("zb")])
                for c in range(8):
                    S.op("pe", lambda e: e.transpose(out=pT[b][:, c, :], in_=zb[b][:, c * P:(c + 1) * P], identity=self.idb[:]),
                         reads=[k("zb"), "idb"], writes=[k("pT3")], accum=(c > 0))
                _act(S, zt[b][:], pT[b][:], AF.Copy, [k("pT3")], [k("zt3")])
                S.dma("sp", self.zT_d[:, :, t0:t0 + P].rearrange("c p t -> p c t"), zt[b][:], reads=[k("zt3")],
                      writes=[("zT", c, T) for c in range(8)])


def build_program(stop_after=None, debug=(), phases="M0ABCD1abcde"):
    nc = bass.Bass("TRN2", target_bir_lowering=False)
    Pg = Prog1(nc, debug)
    S = Pg.S
    Pg.consts()
    if "M" in phases:
        Pg.phase_mod()
    if "0" in phases:
      with S.scope():
        V0 = Pg.load_layer_vecs(0)
        if "A" in phases: Pg.phase_L0_proj(V0)
        if "B" in phases: Pg.phase_L0_pool()
        if "C" in phases: Pg.phase_L0_attn()
        if "D" in phases: Pg.phase_out_mlp(0, V0, Pg.ev_w_out, Pg.y_tile_L0, lambda T: (Pg.x_d[T * P:(T + 1) * P, :], ("x_d", T)))
    if "1" in phases:
      with S.scope():
        V1 = Pg.load_layer_vecs(1)
        if "a" in phases: Pg.phase_R0(V1)
        if "b" in phases: Pg.phase_R1()
        if "c" in phases: Pg.phase_R2()
        if "d" in phases: Pg.phase_R3()
        if "e" in phases: Pg.phase_out_mlp(1, V1, Pg.rw_wo, Pg.y_tile_L0, lambda T: (Pg.out[(T - 2) * P:(T - 1) * P, :], ("out", T)))
    S.barrier()
    S.finish([])
    S.close()
    return nc, Pg


_host_inputs0 = host_inputs


def host_inputs(inputs):
    maps = _host_inputs0(inputs)
    f = lambda a: np.ascontiguousarray(np.asarray(a, dtype=np.float32))
    tri, m4, lmT = _scan_consts()
    sh = {
        "rw_mu": f(inputs["rw_mu"])[0], "rw_wr": f(inputs["rw_wr"])[0], "rw_wk": f(inputs["rw_wk"])[0],
        "rw_wv": f(inputs["rw_wv"])[0], "rw_wo": f(inputs["rw_wo"])[0],
        "rw_w0": f(inputs["rw_w0"])[0], "rw_a0": f(inputs["rw_a0"])[0],
        "w1cat": f(np.concatenate([inputs["rw_w1"][0, 0], inputs["rw_w1"][0, 1]], axis=1)),
        "a1cat": f(np.concatenate([inputs["rw_a1"][0, 0], inputs["rw_a1"][0, 1]], axis=1)),
        "rw_g1": f(inputs["rw_g1"])[0],
        "w2cat": f(np.asarray(inputs["rw_w2"])[0].reshape(128, 1024)), "a2cat": f(np.asarray(inputs["rw_a2"])[0].reshape(128, 1024)),
        "rw_g2": f(inputs["rw_g2"])[0],
        "rw_kk": f(inputs["rw_kk"]).reshape(1, 1024), "rw_ka": f(inputs["rw_ka"]).reshape(1, 1024),
        "rw_rk": f(inputs["rw_rk"]).reshape(1, 1024), "rw_lng": f(inputs["rw_lng"]).reshape(1, 1024),
        "rw_lnb": f(inputs["rw_lnb"]).reshape(1, 1024),
        "tri_c": f(tri), "m4_c": f(m4), "lmT_c": f(lmT),
    }
    for m in maps:
        m.update(sh)
    return maps
```

```python
import contextlib
import numpy as np
import concourse.bass as bass
import concourse.mybir as mybir

F32 = mybir.dt.float32
BF16 = mybir.dt.bfloat16
AF = mybir.ActivationFunctionType
ALU = mybir.AluOpType
AX = mybir.AxisListType

SEM_LIMIT = 10000


class _Ctr:
    def __init__(self, S, name, step):
        self.S = S
        self.name = name
        self.step = step
        self.gen = 0
        self.sem = S._newsem(f"{name}_0")
        self.val = 0

    def next_event(self):
        if self.val + self.step > SEM_LIMIT:
            self.gen += 1
            self.sem = self.S._newsem(f"{self.name}_{self.gen}")
            self.val = 0
        self.val += self.step
        return (self.sem, self.val)


class _PsView:
    def __init__(self, t, shape):
        self.t = t
        self.n1 = shape[1]

    def __getitem__(self, key):
        if not isinstance(key, tuple):
            key = (key,)
        key = list(key)
        if len(key) < 2:
            key.append(slice(None))
        k1 = key[1]
        if isinstance(k1, slice):
            start, stop, step = k1.indices(self.n1)
            key[1] = slice(start, stop, step)
        return self.t[tuple(key)]


class _Eng:
    def __init__(self, S, name, obj):
        self.name = name
        self.obj = obj
        self.ctr = _Ctr(S, "s_" + name, 1)
        self.seen = {}
        self.n_issued = 0
        self.last_ins = None
        self.last_has_inc = False
        self.inc_idx = []
        self.inc_ev = []


class LazyEv:
    __slots__ = ("eng", "idx")

    def __init__(self, eng, idx):
        self.eng = eng
        self.idx = idx


class _Res:
    __slots__ = ("w", "r")

    def __init__(self):
        self.w = None
        self.r = {}


class Sched:
    def __init__(self, nc, n_dma_slots=8):
        self.nc = nc
        self.stack = contextlib.ExitStack()
        self.scopes = [self.stack]
        self.res = {}
        self.engs = {
            "pe": _Eng(self, "pe", nc.tensor),
            "act": _Eng(self, "act", nc.scalar),
            "dve": _Eng(self, "dve", nc.vector),
            "pool": _Eng(self, "pool", nc.gpsimd),
            "sp": _Eng(self, "sp", nc.sync),
        }
        self.dma_slots = {}
        for q in ("sp", "pool"):
            self.dma_slots[q] = [_Ctr(self, f"d_{q}{i}", 16) for i in range(n_dma_slots)]
        self.dma_rr = {"sp": 0, "pool": 0}
        self.n_inst = 0
        self.uid = 0
        self.pending = None
        self.lazy_engines = ()

    def _newsem(self, name):
        return self.stack.enter_context(self.nc.semaphore(name))

    def sb(self, name, shape, dt):
        self.uid += 1
        return self.scopes[-1].enter_context(self.nc.sbuf_tensor(f"sb{self.uid}_{name}", list(shape), dt))

    def ps(self, name, shape, dt=F32):
        self.uid += 1
        esz = 4 if dt == F32 else 2
        per_part = esz
        for d_ in shape[1:]:
            per_part *= d_
        assert per_part <= 2048, (name, shape)
        shape = list(shape)
        if per_part < 2048:
            rest = per_part // shape[1]
            assert 2048 % rest == 0, (name, shape)
            full = [shape[0], 2048 // rest] + shape[2:]
            t = self.scopes[-1].enter_context(self.nc.psum_tensor(f"ps{self.uid}_{name}", full, dt))
            return _PsView(t, shape)
        return self.scopes[-1].enter_context(self.nc.psum_tensor(f"ps{self.uid}_{name}", shape, dt))

    @contextlib.contextmanager
    def scope(self):
        st = contextlib.ExitStack()
        self.scopes.append(st)
        try:
            yield
        finally:
            self.barrier()
            self.scopes.pop()
            st.close()

    def barrier(self):
        evs = []
        for e in self.engs.values():
            if e.n_issued > 0:
                evs.append(self._resolve(LazyEv(e, e.n_issued - 1)))
        for q in self.dma_slots:
            for ctr in self.dma_slots[q]:
                if ctr.val > 0:
                    evs.append((ctr.sem, ctr.val))
        for e in self.engs.values():
            for ev in evs:
                self._wait(e, ev)

    def _r(self, key):
        r = self.res.get(key)
        if r is None:
            r = self.res[key] = _Res()
        return r

    def _resolve(self, ev):
        if not isinstance(ev, LazyEv):
            return ev
        import bisect
        e = ev.eng
        k = bisect.bisect_left(e.inc_idx, ev.idx)
        if k < len(e.inc_idx):
            return e.inc_ev[k]
        assert e.last_ins is not None and not e.last_has_inc and e.n_issued - 1 >= ev.idx
        sv = e.ctr.next_event()
        e.last_ins.then_inc(sv[0], 1)
        e.last_has_inc = True
        e.inc_idx.append(e.n_issued - 1)
        e.inc_ev.append(sv)
        return sv

    def _wait(self, eng, ev):
        if ev is None:
            return
        if isinstance(ev, LazyEv) and ev.eng is eng and eng.name == "pe":
            return
        sem, val = self._resolve(ev)
        k = id(sem)
        if eng.seen.get(k, 0) >= val:
            return
        if self.pending is not None:
            cur = self.pending.get(k)
            if cur is None or cur[1] < val:
                self.pending[k] = (sem, val)
            return
        eng.obj.wait_ge(sem, val)
        eng.seen[k] = val

    def _flush(self, eng):
        pend = list(self.pending.values())
        self.pending = None
        for (sem, val) in pend[:-1]:
            eng.obj.wait_ge(sem, val)
            eng.seen[id(sem)] = val
        if pend:
            sem, val = pend[-1]
            eng.seen[id(sem)] = val
            return (sem, val)
        return None

    def _deps(self, eng, reads, writes, skip_same_eng_write=False):
        for key in reads:
            r = self._r(key)
            self._wait(eng, r.w)
        for key in writes:
            r = self._r(key)
            if not (skip_same_eng_write and isinstance(r.w, LazyEv) and r.w.eng is eng):
                self._wait(eng, r.w)
            for ev in r.r.values():
                self._wait(eng, ev)

    def _commit(self, ev, reads, writes):
        rk = ev.eng.name if isinstance(ev, LazyEv) else id(ev[0])
        for key in reads:
            self._r(key).r[rk] = ev
        for key in writes:
            r = self._r(key)
            r.w = ev
            r.r = {}

    def op(self, engname, fn, reads=(), writes=(), accum=False):
        eng = self.engs[engname]
        self.pending = {}
        self._deps(eng, reads, writes, skip_same_eng_write=accum)
        last = self._flush(eng)
        ins = fn(eng.obj)
        if last is not None:
            ins._wait_ge(last[0], last[1])
        eng.last_ins = ins
        eng.last_has_inc = False
        ev = LazyEv(eng, eng.n_issued)
        eng.n_issued += 1
        if engname not in self.lazy_engines:
            self._resolve(ev)
        self._commit(ev, reads, writes)
        self.n_inst += 1
        return ev

    def dma(self, q, out, in_, reads=(), writes=(), **kw):
        eng = self.engs[q]
        slots = self.dma_slots[q]
        i = self.dma_rr[q]
        self.dma_rr[q] = (i + 1) % len(slots)
        ctr = slots[i]
        self.pending = {}
        if ctr.val > 0:
            self._wait(eng, (ctr.sem, ctr.val))
        self._deps(eng, reads, writes)
        last = self._flush(eng)
        ev = ctr.next_event()
        ins = eng.obj.dma_start(out=out, in_=in_, **kw)
        if last is not None:
            ins._wait_ge(last[0], last[1])
        ins.then_inc(ev[0], 16)
        self._commit(ev, reads, writes)
        self.n_inst += 1
        return ev

    def finish(self, final_keys):
        eng = self.engs["sp"]
        for key in final_keys:
            r = self._r(key)
            self._wait(eng, r.w)
        for q in self.dma_slots:
            for ctr in self.dma_slots[q]:
                if ctr.val > 0:
                    self._wait(eng, (ctr.sem, ctr.val))

    def close(self):
        self.stack.close()

from concourse.bass_utils import run_bass_kernel_spmd

D = 1024
NCTX = 256
NLAT = 4096
NTOK = NCTX + NLAT
NT = NTOK // 128
EPS = 1e-6
P = 128


def _pool_bands():
    L = 1024
    out = np.zeros((4, 5, 128, 128), np.float32)
    for g, w in enumerate((2, 4, 8, 16)):
        def full(L):
            t = np.arange(L)
            lo = np.clip(t - w // 2, 0, L)
            hi = np.clip(t + w // 2, 0, L)
            s = np.arange(L)[:, None]
            m = ((s >= lo[None, :]) & (s < hi[None, :])).astype(np.float64) / (hi - lo)[None, :]
            m -= np.eye(L)
            return m
        m = full(L)
        out[g, 0] = m[3 * 128:4 * 128, 4 * 128:5 * 128]
        out[g, 1] = m[5 * 128:6 * 128, 4 * 128:5 * 128]
        out[g, 2] = m[4 * 128:5 * 128, 4 * 128:5 * 128]
        out[g, 3] = m[0:128, 0:128]
        out[g, 4] = m[L - 128:, L - 128:]
    return out


_VARS = [(-2, "pm"), (-1, "f"), (0, "f"), (1, "f"), (2, "pp")] + [(d, "f") for d in range(-3, 4)]


def _attn_tables():
    kc = np.arange(64)
    qc = np.arange(64)
    c_start = np.clip(qc - 8, 0, 48)
    col_ok = (kc[:, None] >= c_start[None, :]) & (kc[:, None] < c_start[None, :] + 16)
    dc_idx = np.clip(kc[:, None] - qc[None, :], -15, 15) + 15
    dr_idx = np.zeros((12, 128, 128), np.int64)
    dc_full = np.zeros((128, 128), np.int64)
    mask = np.zeros((12, 128, 128), np.float32)
    for a in range(2):
        for b in range(2):
            dc_full[a * 64:(a + 1) * 64, b * 64:(b + 1) * 64] = dc_idx
    for v, (dl, kind) in enumerate(_VARS):
        for a in range(2):
            for b in range(2):
                dr = 2 * dl + a - b + 7
                vis = True
                if kind == "pm":
                    vis = not (a == 0 and b == 1)
                elif kind == "pp":
                    vis = (a == 0 and b == 1)
                dr_idx[v, a * 64:(a + 1) * 64, b * 64:(b + 1) * 64] = min(max(dr, 0), 14)
                if vis and 0 <= dr <= 14:
                    mask[v, a * 64:(a + 1) * 64, b * 64:(b + 1) * 64] = col_ok
    return dr_idx, dc_full, mask


def mm(S, out, lhsT, rhs, start, stop, reads, writes):
    return S.op("pe", lambda e: e.matmul(out, lhsT=lhsT, rhs=rhs, start=start, stop=stop),
                reads=reads, writes=writes, accum=not start)


class Prog:
    def __init__(self, nc, debug=()):
        self.nc = nc
        self.S = Sched(nc)
        self.debug = debug
        self.dbg_out = {}
        dt = nc.dram_tensor
        I = lambda name, shape: dt(name, list(shape), F32, kind="ExternalInput").ap()
        self.x_in = I("x", [NLAT, D])
        self.ctx_in = I("ctx", [NCTX, D])
        self.cvec = I("cvec", [P, 8, 2])
        self.ada_w = I("ada_w", [2, D, 6 * D])
        self.ada_b = I("ada_b", [2, 6 * D])
        self.norm_g = I("norm_g", [2, 4, D])
        self.mlp_w1 = I("mlp_w1", [2, D, 4 * D])
        self.mlp_w2 = I("mlp_w2", [2, 4 * D, D])
        self.ev_w_in = I("ev_w_in", [D, 2 * D])
        self.ev_w_out = I("ev_w_out", [D, D])
        self.ev_pool_w = I("ev_pool_w", [4, P, P])
        self.ev_pool_scale = I("ev_pool_scale", [512])
        self.rpb_tab = I("rpb_tab", [P, 8 * 12, P])
        self.msk_tab = I("msk_tab", [P, 12, P])
        self.bands = I("bands", [P, 20, P])
        self.ident = I("ident", [P, P])
        self.out = dt("out", [NLAT, D], F32, kind="ExternalOutput").ap()
        X = lambda name, shape, d=F32: (dt(name, list(shape), d, kind="ExternalOutput").ap() if name in debug
                                        else dt(name, list(shape), d).ap())
        self.modd = X("modd", [2, 2, 6 * D])
        self.x_d = X("x_d", [NTOK, D])
        self.upool_d = X("upool_d", [NTOK, 512], BF16)
        self.v_d = X("v_d", [NTOK, 512], BF16)
        self.qT_d = X("qT_d", [4, P, NTOK], BF16)
        self.kT_d = X("kT_d", [4, P, NTOK], BF16)
        self.zT_d = X("zT_d", [8, P, NTOK], BF16)

    def dbg(self, name, shape, dtp=F32):
        t = self.nc.dram_tensor("dbg_" + name, list(shape), dtp, kind="ExternalOutput").ap()
        self.dbg_out[name] = t
        return t

    def consts(self):
        S = self.S
        self.idb = S.sb("idb", [P, P], BF16)
        S.dma("pool", self.idb[:], self.ident, writes=["idb"])
        self.ones_bf = S.sb("ones_bf", [P, P], BF16)
        S.op("dve", lambda e: e.memset(self.ones_bf[:], 1.0), writes=["ones_bf"])
        self.eps_t = S.sb("eps_t", [P, 1], F32)
        S.op("dve", lambda e: e.memset(self.eps_t[:], EPS), writes=["eps_t"])

    def phase_mod(self):
        S = self.S
        with S.scope():
            cv = S.sb("cv", [P, 8, 2], F32)
            cvb = S.sb("cvb", [P, 8, 2], BF16)
            S.dma("sp", cv[:], self.cvec, writes=["cv"])
            S.op("act", lambda e: e.activation(out=cvb[:], in_=cv[:], func=AF.Silu), reads=["cv"], writes=["cvb"])
            aw = S.sb("aw", [P, 8, 6 * D], BF16)
            ab = S.sb("ab", [2, 6 * D], F32)
            mrow = S.sb("mrow", [2, 6 * D], F32)
            pss = [S.ps(f"pm{i}", [2, 512], F32) for i in range(4)]
            for l in range(2):
                for c in range(8):
                    S.dma("pool", aw[:, c, :], self.ada_w[l, c * P:(c + 1) * P, :], writes=[("aw", c)])
                S.dma("sp", ab[:], self.ada_b[l:l + 1, :].broadcast_to([2, 6 * D]), writes=["ab"])
                for n in range(12):
                    ps = pss[n % 4]
                    k = ("pm", n % 4)
                    for c in range(8):
                        mm(S, ps[:], cvb[:, c, :], aw[:, c, n * 512:(n + 1) * 512], c == 0, c == 7,
                           reads=["cvb", ("aw", c)], writes=[k])
                    S.op("dve", lambda e: e.tensor_tensor(out=mrow[:, n * 512:(n + 1) * 512], in0=ps[:],
                                                          in1=ab[:, n * 512:(n + 1) * 512], op=ALU.add),
                         reads=[k, "ab"], writes=["mrow"])
                S.dma("sp", self.modd[l], mrow[:], reads=["mrow"], writes=[("modd", l)])

    def load_layer_vecs(self, l):
        S = self.S
        V = {}
        for s in range(2):
            for which in range(2):
                V[("A", which, s)] = (S.sb(f"A{which}_{s}", [P, 8], F32), f"A{which}_{s}")
                V[("B", which, s)] = (S.sb(f"B{which}_{s}", [P, 8], F32), f"B{which}_{s}")
                if not (l == 1 and s == 1):
                    V[("G", which, s)] = (S.sb(f"GG{which}_{s}", [P, D], F32), f"GG{which}_{s}")
        with S.scope(), self.nc.allow_non_contiguous_dma(reason="tiny per-feature vectors"):
            tmp = S.sb("lv_tmp", [P, 8], F32)
            rowt = S.sb("lv_row", [P, D], F32)
            for s in range(2):
                for which, (ish, isc, ig) in enumerate(((0, 1, 0), (3, 4, 2))):
                    A, ka = V[("A", which, s)]
                    B, kb = V[("B", which, s)]
                    S.dma("sp", B[:], self.modd[l, s, ish * D:(ish + 1) * D].rearrange("(c p) -> p c", p=P),
                          reads=[("modd", l)], writes=[kb])
                    S.dma("sp", A[:], self.modd[l, s, isc * D:(isc + 1) * D].rearrange("(c p) -> p c", p=P),
                          reads=[("modd", l)], writes=[ka])
                    S.dma("sp", tmp[:], self.norm_g[l, ig, :].rearrange("(c p) -> p c", p=P), writes=["lv_tmp"])
                    S.op("dve", lambda e: e.scalar_tensor_tensor(out=A[:], in0=A[:], scalar=1.0, in1=tmp[:],
                                                                 op0=ALU.add, op1=ALU.mult),
                         reads=[ka, "lv_tmp"], writes=[ka])
                for which, (igt, ig) in enumerate(((2, 1), (5, 3))):
                    if l == 1 and s == 1:
                        continue
                    G, kg = V[("G", which, s)]
                    S.dma("sp", G[:], self.modd[l, s:s + 1, igt * D:(igt + 1) * D].broadcast_to([P, D]),
                          reads=[("modd", l)], writes=[kg])
                    S.dma("sp", rowt[:], self.norm_g[l, ig:ig + 1, :].broadcast_to([P, D]), writes=["lv_row"])
                    S.op("dve", lambda e: e.tensor_tensor(out=G[:], in0=G[:], in1=rowt[:], op=ALU.mult),
                         reads=[kg, "lv_row"], writes=[kg])
        return V

    def make_norm_bufs(self, tag, nb=2):
        S = self.S
        B = {"i": 0, "nb": nb, "tag": tag}
        B["sq"] = [S.sb(f"{tag}_sq{i}", [P, D], BF16) for i in range(1)] * nb
        B["st"] = [S.sb(f"{tag}_st{i}", [P, 4], F32) for i in range(nb)]
        B["xn"] = [S.sb(f"{tag}_xn{i}", [P, D], BF16) for i in range(nb)]
        B["tp"] = [S.ps(f"{tag}_tp{i}", [P, 8, P], BF16) for i in range(nb)]
        B["tm"] = [S.sb(f"{tag}_tm{i}", [P, 8, P], F32) for i in range(nb)]
        return B

    def norm_to_hT(self, B, x_sb, xkey, A, B_, out_ap, out_key, out2_ap=None, out2_key=None):
        S = self.S
        i = B["i"] % B["nb"]
        B["i"] += 1
        tag = B["tag"]
        sq, st, xn, tp, tm = B["sq"][i], B["st"][i], B["xn"][i], B["tp"][i], B["tm"][i]
        ksq, kst, kxn, ktp, ktm = [(tag, n, i) for n in ("sq", "st", "xn", "tp", "tm")]
        ksq = (tag, "sq", 0)
        S.op("pool", lambda e: e.memset(st[:], 0.0), writes=[kst])
        S.op("act", lambda e: e.activation(out=sq[:], in_=x_sb, func=AF.Square, accum_out=st[:, 0:1]),
             reads=[xkey], writes=[ksq, kst])
        S.op("act", lambda e: e.activation(out=st[:, 1:2], in_=st[:, 0:1], func=AF.Sqrt, scale=1.0 / D,
                                           bias=self.eps_t[:, 0:1]), reads=[kst, "eps_t"], writes=[kst])
        S.op("dve", lambda e: e.reciprocal(out=st[:, 2:3], in_=st[:, 1:2]), reads=[kst], writes=[kst])
        S.op("dve", lambda e: e.tensor_scalar(out=xn[:], in0=x_sb, scalar1=st[:, 2:3], scalar2=None, op0=ALU.mult),
             reads=[xkey, kst], writes=[kxn])
        for c in range(8):
            S.op("pe", lambda e: e.transpose(out=tp[:, c, :], in_=xn[:, c * P:(c + 1) * P], identity=self.idb[:]),
                 reads=[kxn, "idb"], writes=[ktp], accum=(c > 0))
        Aap = A[0][:, :, None].broadcast_to([P, 8, P])
        Bap = B_[0][:, :, None].broadcast_to([P, 8, P])
        S.op("dve", lambda e: e.tensor_tensor(out=tm[:], in0=tp[:], in1=Aap, op=ALU.mult),
             reads=[ktp, A[1]], writes=[ktm])
        S.op("pool", lambda e: e.tensor_tensor(out=out_ap, in0=tm[:], in1=Bap, op=ALU.add),
             reads=[ktm, B_[1]], writes=[out_key])
        if out2_ap is not None:
            S.op("pool", lambda e: e.tensor_tensor(out=out2_ap, in0=tm[:], in1=Bap, op=ALU.add),
                 reads=[ktm, B_[1]], writes=[out2_key])

    def x_src(self, layer, T):
        if layer == 0:
            if T < 2:
                return self.ctx_in[T * P:(T + 1) * P, :], None
            return self.x_in[(T - 2) * P:(T - 1) * P, :], None
        return self.x_d[T * P:(T + 1) * P, :], ("x_d", T)

    def phase_L0_proj(self, V):
        S = self.S
        with S.scope():
            w = S.sb("w_in", [P, 8, 2 * D], BF16)
            for c in range(8):
                S.dma("pool", w[:, c, :], self.ev_w_in[c * P:(c + 1) * P, :], writes=[("w_in", c)])
            wk = [("w_in", c) for c in range(8)]
            NB = self.make_norm_bufs("n0")
            xt = [S.sb(f"xt{i}", [P, D], F32) for i in range(2)]
            hT = [S.sb(f"hT{i}", [P, 8, 512], BF16) for i in range(2)]
            ptok = [S.ps(f"ptok{i}", [P, 512], F32) for i in range(2)]
            pft = [S.ps(f"pft{i}", [P, 512], F32) for i in range(2)]
            otok = [S.sb(f"otok{i}", [P, 512], BF16) for i in range(2)]
            oft = [S.sb(f"oft{i}", [P, 512], BF16) for i in range(2)]
            supers = [(0, 2)] + [(2 + 4 * i, 4) for i in range(8)]
            cnt = 0
            ctok = 0
            cft = 0
            for si, (T0, nt) in enumerate(supers):
                hb = hT[si % 2]
                hk = ("hT", si % 2)
                s = 1 if T0 < 2 else 0
                for t in range(nt):
                    T = T0 + t
                    xb = xt[cnt % 2]
                    xk = ("xt", cnt % 2)
                    cnt += 1
                    src, sk = self.x_src(0, T)
                    S.dma("sp", xb[:], src, reads=[sk] if sk else [], writes=[xk])
                    self.norm_to_hT(NB, xb[:], xk, V[("A", 0, s)], V[("B", 0, s)],
                                    hb[:, :, t * P:(t + 1) * P], (hk, t))
                n = nt * P
                hks = [(hk, t) for t in range(nt)]
                for t in range(nt):
                    T = T0 + t
                    for (c0, dst, dk) in ((0, self.upool_d, "upool"), (1536, self.v_d, "v")):
                        ps = ptok[ctok % 2]; pk = ("ptok", ctok % 2)
                        ob = otok[ctok % 2]; ok = ("otok", ctok % 2)
                        ctok += 1
                        for c in range(8):
                            mm(S, ps[:], hb[:, c, t * P:(t + 1) * P], w[:, c, c0:c0 + 512], c == 0, c == 7,
                               reads=[(hk, t), wk[c]], writes=[pk])
                        S.op("act", lambda e: e.activation(out=ob[:], in_=ps[:], func=AF.Copy), reads=[pk], writes=[ok])
                        S.dma("sp", dst[T * P:(T + 1) * P, :], ob[:], reads=[ok], writes=[(dk, T)])
                for jb in range(8):
                    c0 = 512 + jb * P
                    ps = pft[cft % 2]; pk = ("pft", cft % 2)
                    ob = oft[cft % 2]; ok = ("oft", cft % 2)
                    cft += 1
                    for c in range(8):
                        mm(S, ps[:, :n], w[:, c, c0:c0 + P], hb[:, c, :n], c == 0, c == 7,
                           reads=hks + [wk[c]], writes=[pk])
                    sc = 0.125 if jb < 4 else 1.0
                    S.op("act", lambda e: e.activation(out=ob[:, :n], in_=ps[:, :n], func=AF.Copy, scale=sc),
                         reads=[pk], writes=[ok])
                    dst = self.qT_d if jb < 4 else self.kT_d
                    dk = "qT" if jb < 4 else "kT"
                    S.dma("sp", dst[jb % 4, :, T0 * P:T0 * P + n], ob[:, :n], reads=[ok],
                          writes=[(dk, jb % 4, T0 + t) for t in range(nt)])

    def phase_L0_pool(self):
        S = self.S
        with S.scope():
            up = S.sb("up_all", [P, NT, 512], BF16)
            for q in range(0, NT, 2):
                S.dma("sp", up[:, q:q + 2, :], self.upool_d[q * P:(q + 2) * P, :].rearrange("(n p) f -> p n f", p=P),
                      reads=[("upool", q), ("upool", q + 1)], writes=[("up", q), ("up", q + 1)])
            bd = S.sb("bands", [P, 20, P], BF16)
            S.dma("pool", bd[:], self.bands, writes=["bands"])
            pw = S.sb("pool_w", [P, 4, P], BF16)
            S.dma("pool", pw[:], self.ev_pool_w.rearrange("g c o -> c g o"), writes=["pool_w"])
            psc = S.sb("pool_sc", [P, 4], F32)
            with self.nc.allow_non_contiguous_dma(reason="tiny"):
                S.dma("sp", psc[:], self.ev_pool_scale.rearrange("(g p) -> p g", p=P), writes=["pool_sc"])
            pb = [S.ps(f"pb{i}", [P, 4, P], F32) for i in range(2)]
            pc = [S.ps(f"pc{i}", [P, 4, P], F32) for i in range(2)]
            pm = [S.sb(f"pmx{i}", [P, 4, P], BF16) for i in range(2)]
            zp = [S.sb(f"zp{i}", [P, 4, P], BF16) for i in range(2)]
            it = 0
            for (T0, n) in ((0, 2), (2, 32)):
                for i in range(n):
                    T = T0 + i
                    b = it % 2
                    it += 1
                    for g in range(4):
                        srcs = []
                        if i > 0:
                            srcs.append((T - 1, 0))
                        cv = 3 if i == 0 else (4 if i == n - 1 else 2)
                        srcs.append((T, cv))
                        if i < n - 1:
                            srcs.append((T + 1, 1))
                        for si, (Ts, v) in enumerate(srcs):
                            mm(S, pb[b][:, g, :], up[:, Ts, g * P:(g + 1) * P], bd[:, g * 5 + v, :],
                               si == 0, si == len(srcs) - 1, reads=[("up", Ts), "bands"], writes=[("pb", b)])
                    S.op("dve", lambda e: e.tensor_copy(out=pm[b][:], in_=pb[b][:]), reads=[("pb", b)], writes=[("pmx", b)])
                    for g in range(4):
                        mm(S, pc[b][:, g, :], pw[:, g, :], pm[b][:, g, :], True, True,
                           reads=["pool_w", ("pmx", b)], writes=[("pc", b)])
                    S.op("dve", lambda e: e.tensor_tensor(out=zp[b][:], in0=pc[b][:],
                                                          in1=psc[:, :, None].broadcast_to([P, 4, P]), op=ALU.mult),
                         reads=[("pc", b), "pool_sc"], writes=[("zp", b)])
                    S.dma("sp", self.zT_d[0:4, :, T * P:(T + 1) * P].rearrange("c p t -> p c t"), zp[b][:],
                          reads=[("zp", b)], writes=[("zT", c, T) for c in range(4)])

    def phase_L0_attn(self):
        S = self.S
        with S.scope():
            kT = S.sb("kT_all", [P, 4, NTOK], BF16)
            qT = S.sb("qT_all", [P, 4, NTOK], BF16)
            va = S.sb("v_all", [P, NT, 512], BF16)
            for j in range(4):
                S.dma("sp", kT[:, j, :], self.kT_d[j], reads=[("kT", j, T) for T in range(NT)], writes=[("kTa", j)])
                S.dma("sp", qT[:, j, :], self.qT_d[j], reads=[("qT", j, T) for T in range(NT)], writes=[("qTa", j)])
            for q in range(0, NT, 2):
                S.dma("sp", va[:, q:q + 2, :], self.v_d[q * P:(q + 2) * P, :].rearrange("(n p) f -> p n f", p=P),
                      reads=[("v", q), ("v", q + 1)], writes=[("va", q), ("va", q + 1)])
            E = S.sb("Etab", [P, 96, P], BF16)
            with S.scope():
                rt = S.sb("rt", [P, 96, P], F32)
                mk = S.sb("mk", [P, 12, P], F32)
                S.dma("sp", rt[:], self.rpb_tab, writes=["rt"])
                S.dma("sp", mk[:], self.msk_tab, writes=["mk"])
                S.op("act", lambda e: e.activation(out=rt[:], in_=rt[:], func=AF.Exp), reads=["rt"], writes=["rt"])
                for h in range(8):
                    S.op("dve", lambda e: e.tensor_tensor(out=E[:, h * 12:(h + 1) * 12, :], in0=rt[:, h * 12:(h + 1) * 12, :],
                                                          in1=mk[:], op=ALU.mult), reads=["rt", "mk"], writes=["Etab"])
            pss = [[S.ps(f"pss{i}_{k}", [P, 512], F32) for k in range(2)] for i in range(2)]
            pso = [S.ps(f"pso{i}", [P, 2, P], F32) for i in range(2)]
            pex = [S.sb(f"pex{i}", [P, 7, P], BF16) for i in range(2)]
            pT = [S.sb(f"pT{i}", [P, 5, P], BF16) for i in range(2)]
            rc = [S.sb(f"rc{i}", [P, P], F32) for i in range(2)]
            zo = [S.sb(f"zo{i}", [P, P], BF16) for i in range(2)]
            it = 0
            izo = 0
            for T in range(NT):
                if T < 2:
                    chunks = [(0, None), (1, None)]
                else:
                    i = T - 2
                    if 2 <= i <= 29:
                        lat = [(T + d, v) for v, d in enumerate((-2, -1, 0, 1, 2))]
                    elif i == 0:
                        lat = [(T + d, 8 + d) for d in (0, 1, 2, 3)]
                    elif i == 1:
                        lat = [(T + d, 8 + d) for d in (-1, 0, 1, 2)]
                    elif i == 30:
                        lat = [(T + d, 8 + d) for d in (-2, -1, 0, 1)]
                    else:
                        lat = [(T + d, 8 + d) for d in (-3, -2, -1, 0)]
                    chunks = [(0, None), (1, None)] + lat
                nk = len(chunks)
                nlat = nk - 2
                for j in range(4):
                    zb = zo[izo % 2]; zk = ("zo", izo % 2)
                    izo += 1
                    for hh in range(2):
                        h = 2 * j + hh
                        pb_ = hh * 64
                        b = it % 2
                        it += 1
                        for ci, (Tk, v) in enumerate(chunks):
                            bank = pss[b][ci // 4]
                            mm(S, bank[:, (ci % 4) * P:(ci % 4 + 1) * P],
                               kT[pb_:pb_ + 64, j, Tk * P:(Tk + 1) * P], qT[pb_:pb_ + 64, j, T * P:(T + 1) * P],
                               True, True, reads=[("kTa", j), ("qTa", j)], writes=[("pss", b, ci // 4)])
                        n0 = min(nk, 4)
                        S.op("act", lambda e: e.activation(out=pex[b][:, 0:n0, :], in_=pss[b][0][:, 0:n0 * P].rearrange("p (c q) -> p c q", q=P), func=AF.Exp),
                             reads=[("pss", b, 0)], writes=[("pex", b)])
                        if nk > 4:
                            S.op("act", lambda e: e.activation(out=pex[b][:, 4:nk, :], in_=pss[b][1][:, 0:(nk - 4) * P].rearrange("p (c q) -> p c q", q=P), func=AF.Exp),
                                 reads=[("pss", b, 1)], writes=[("pex", b)])
                        if nlat > 0:
                            v0 = chunks[2][1]
                            S.op("dve", lambda e: e.tensor_tensor(out=pT[b][:, 0:nlat, :], in0=pex[b][:, 2:nk, :],
                                                                  in1=E[:, h * 12 + v0:h * 12 + v0 + nlat, :], op=ALU.mult),
                                 reads=[("pex", b), "Etab"], writes=[("pT", b)])
                        for ci, (Tk, v) in enumerate(chunks):
                            rhs = pex[b][:, ci, :] if v is None else pT[b][:, ci - 2, :]
                            rk = [("pex", b)] if v is None else [("pT", b)]
                            mm(S, pso[b][:, 0, :], va[:, Tk, j * P:(j + 1) * P], rhs, ci == 0, ci == nk - 1,
                               reads=[("va", Tk)] + rk, writes=[("pso", b)])
                        for ci, (Tk, v) in enumerate(chunks):
                            rhs = pex[b][:, ci, :] if v is None else pT[b][:, ci - 2, :]
                            rk = [("pex", b)] if v is None else [("pT", b)]
                            mm(S, pso[b][:, 1, :], self.ones_bf[:], rhs, ci == 0, ci == nk - 1,
                               reads=["ones_bf"] + rk, writes=[("pso", b)])
                        S.op("dve", lambda e: e.reciprocal(out=rc[b][pb_:pb_ + 64, :], in_=pso[b][pb_:pb_ + 64, 1, :]),
                             reads=[("pso", b)], writes=[("rc", b)])
                        S.op("dve", lambda e: e.tensor_tensor(out=zb[pb_:pb_ + 64, :], in0=pso[b][pb_:pb_ + 64, 0, :],
                                                              in1=rc[b][pb_:pb_ + 64, :], op=ALU.mult),
                             reads=[("pso", b), ("rc", b)], writes=[zk])
                    S.dma("sp", self.zT_d[4 + j, :, T * P:(T + 1) * P], zb[:], reads=[zk], writes=[("zT", 4 + j, T)])

    def phase_out_mlp(self, layer, V, w_out_ap, y_tile_fn, dst_fn):
        S = self.S
        with S.scope():
            wo = S.sb("wo", [P, 8, D], BF16)
            for c in range(8):
                S.dma("pool", wo[:, c, :], w_out_ap[c * P:(c + 1) * P, :], writes=[("wo", c)])
            w1 = S.sb("w1", [P, 8, 4 * D], BF16)
            w2 = S.sb("w2", [P, 32, D], BF16)
            for c in range(8):
                S.dma("pool", w1[:, c, :], self.mlp_w1[layer, c * P:(c + 1) * P, :], writes=[("w1", c)])
            for f in range(0, 32, 4):
                S.dma("pool", w2[:, f:f + 4, :], self.mlp_w2[layer, f * P:(f + 4) * P, :].rearrange("(n p) d -> p n d", p=P),
                      writes=[("w2", f + q) for q in range(4)])
            NB = self.make_norm_bufs("nm", nb=1)
            zt = [S.sb(f"zt{i}", [P, 8, P], BF16) for i in range(2)]
            xt = [S.sb(f"xo{i}", [P, D], F32) for i in range(2)]
            x1 = [S.sb(f"x1_{i}", [P, D], F32) for i in range(2)]
            tmp = [S.sb(f"tg{i}", [P, D], F32) for i in range(2)]
            sq = NB["sq"][0]
            stt = [S.sb(f"ost{i}", [P, 4], F32) for i in range(4)]
            hT = [S.sb(f"hm{i}", [P, 8, 256], BF16) for i in range(1)] * 2
            py = [[S.ps(f"py{t}_{hf}", [P, 512], F32) for hf in range(2)] for t in range(2)]
            pa = [S.ps(f"pa{i}", [P, 256], F32) for i in range(2)]
            r32 = [S.sb(f"r32_{i}", [P, 256], F32) for i in range(2)]
            aT = [S.sb(f"aT{i}", [P, 256], BF16) for i in range(2)]
            ist = 0

            def norm_gate_res(t, G, xin, xin_key, xout, xout_key):
                nonlocal ist
                st = stt[ist % 4]; sk = ("ost", ist % 4)
                ist += 1
                S.op("pool", lambda e: e.memset(st[:], 0.0), writes=[sk])
                for hf in range(2):
                    S.op("act", lambda e: e.activation(out=sq[:, hf * 512:(hf + 1) * 512], in_=py[t][hf][:], func=AF.Square,
                                                       accum_out=st[:, hf:hf + 1]), reads=[("py", t, hf)], writes=[("nm", "sq", 0), sk])
                S.op("dve", lambda e: e.tensor_tensor(out=st[:, 2:3], in0=st[:, 0:1], in1=st[:, 1:2], op=ALU.add),
                     reads=[sk], writes=[sk])
                S.op("act", lambda e: e.activation(out=st[:, 2:3], in_=st[:, 2:3], func=AF.Sqrt, scale=1.0 / D,
                                                   bias=self.eps_t[:, 0:1]), reads=[sk, "eps_t"], writes=[sk])
                S.op("dve", lambda e: e.reciprocal(out=st[:, 3:4], in_=st[:, 2:3]), reads=[sk], writes=[sk])
                tb = tmp[t]; tk = ("tg", t)
                for hf in range(2):
                    S.op("dve", lambda e: e.scalar_tensor_tensor(out=tb[:, hf * 512:(hf + 1) * 512], in0=py[t][hf][:],
                                                                 scalar=st[:, 3:4], in1=G[0][:, hf * 512:(hf + 1) * 512],
                                                                 op0=ALU.mult, op1=ALU.mult),
                         reads=[("py", t, hf), sk, G[1]], writes=[tk])
                S.op("pool", lambda e: e.tensor_tensor(out=xout, in0=tb[:], in1=xin, op=ALU.add),
                     reads=[tk, xin_key], writes=[xout_key])

            ia = 0
            for sidx in range(NT // 2):
                T0 = 2 * sidx
                s = 1 if T0 < 2 else 0
                if layer == 1 and s == 1:
                    continue
                hb = hT[0]; hk = ("hm", 0)
                for t in range(2):
                    T = T0 + t
                    src, skey = self.x_src(layer, T)
                    S.dma("sp", xt[t][:], src, reads=[skey] if skey else [], writes=[("xo", t)])
                    y_tile_fn(T, t, zt[t], ("zt", t), wo, py[t])
                    norm_gate_res(t, V[("G", 0, s)], xt[t][:], ("xo", t), x1[t][:], ("x1", t))
                    self.norm_to_hT(NB, x1[t][:], ("x1", t), V[("A", 1, s)], V[("B", 1, s)],
                                    hb[:, :, t * P:(t + 1) * P], (hk, t))
                def mm1(f):
                    a = (ia + f) % 2
                    for c in range(8):
                        mm(S, pa[a][:], w1[:, c, f * P:(f + 1) * P], hb[:, c, :], c == 0, c == 7,
                           reads=[("w1", c), (hk, 0), (hk, 1)], writes=[("pa", a)])
                    S.op("act", lambda e: e.activation(out=r32[a][:], in_=pa[a][:], func=AF.Relu),
                         reads=[("pa", a)], writes=[("r32", a)])
                    S.op("dve", lambda e: e.tensor_tensor(out=aT[a][:], in0=r32[a][:], in1=r32[a][:], op=ALU.mult),
                         reads=[("r32", a)], writes=[("aT", a)])

                def mm2(f):
                    a = (ia + f) % 2
                    for t in range(2):
                        for hf in range(2):
                            mm(S, py[t][hf][:], aT[a][:, t * P:(t + 1) * P], w2[:, f, hf * 512:(hf + 1) * 512],
                               f == 0, f == 31, reads=[("aT", a), ("w2", f)], writes=[("py", t, hf)])
                mm1(0)
                for f in range(32):
                    if f + 1 < 32:
                        mm1(f + 1)
                    mm2(f)
                for t in range(2):
                    T = T0 + t
                    norm_gate_res(t, V[("G", 1, s)], x1[t][:], ("x1", t), tmp[t][:], ("tg", t))
                    dst, dkey = dst_fn(T)
                    S.dma("sp", dst, tmp[t][:], reads=[("tg", t)], writes=[dkey])

    def y_tile_L0(self, T, t, zt, zk, wo, py):
        S = self.S
        S.dma("sp", zt[:], self.zT_d[:, :, T * P:(T + 1) * P].rearrange("c p t -> p c t"),
              reads=[("zT", c, T) for c in range(8)], writes=[zk])
        for hf in range(2):
            for c in range(8):
                mm(S, py[hf][:], zt[:, c, :], wo[:, c, hf * 512:(hf + 1) * 512], c == 0, c == 7,
                   reads=[zk, ("wo", c)], writes=[("py", t, hf)])


def build_program(stop_after=None, debug=()):
    nc = bass.Bass("TRN2", target_bir_lowering=False)
    Pg = Prog(nc, debug)
    S = Pg.S
    Pg.consts()
    Pg.phase_mod()
    final_keys = []
    with S.scope():
        V0 = Pg.load_layer_vecs(0)
        Pg.phase_L0_proj(V0)
        Pg.phase_L0_pool()
        Pg.phase_L0_attn()

        def dst0(T):
            if stop_after == "L0":
                if T < 2:
                    return Pg.x_d[T * P:(T + 1) * P, :], ("x_d", T)
                return Pg.out[(T - 2) * P:(T - 1) * P, :], ("out", T)
            return Pg.x_d[T * P:(T + 1) * P, :], ("x_d", T)
        Pg.phase_out_mlp(0, V0, Pg.ev_w_out, Pg.y_tile_L0, dst0)
    S.barrier()
    S.finish([])
    S.close()
    return nc, Pg


def host_inputs(inputs):
    f = lambda a: np.ascontiguousarray(np.asarray(a, dtype=np.float32))
    dr_idx, dc_full, mask = _attn_tables()
    rpb = f(inputs["ev_rpb"])[0]
    tab = rpb[:, dr_idx, dc_full[None, :, :]]
    tab = np.ascontiguousarray(tab.transpose(2, 0, 1, 3).reshape(128, 96, 128))
    msk = np.ascontiguousarray(mask.transpose(1, 0, 2))
    bands = np.ascontiguousarray(_pool_bands().transpose(2, 0, 1, 3).reshape(128, 20, 128))
    shared = {
        "ada_w": f(inputs["ada_w"]), "ada_b": f(inputs["ada_b"]), "norm_g": f(inputs["norm_g"]),
        "mlp_w1": f(inputs["mlp_w1"]), "mlp_w2": f(inputs["mlp_w2"]),
        "ev_w_in": f(inputs["ev_w_in"])[0], "ev_w_out": f(inputs["ev_w_out"])[0],
        "ev_pool_w": f(inputs["ev_pool_w"])[0], "ev_pool_scale": f(inputs["ev_pool_scale"])[0],
        "rpb_tab": tab, "msk_tab": msk, "bands": bands, "ident": np.eye(128, dtype=np.float32),
    }
    x = f(inputs["x"]); c = f(inputs["c"]); ctx = f(inputs["ctx"]); cc = f(inputs["c_ctx"])
    maps = []
    for b in range(x.shape[0]):
        cv = np.stack([c[b].reshape(8, 128).T, cc.reshape(8, 128).T], axis=-1)
        m = dict(shared)
        m.update({"x": x[b], "ctx": ctx[b], "cvec": np.ascontiguousarray(cv)})
        maps.append(m)
    return maps


_CACHE = {}


def kernel(**inputs):
    maps = host_inputs(inputs)
    if "nc" not in _CACHE:
        _CACHE["nc"] = build_program()
    nc, Pg = _CACHE["nc"]
    res = run_bass_kernel_spmd(nc, maps, core_ids=list(range(8)))
    return np.stack([np.asarray(r["out"]) for r in res.results], axis=0)

LWC = -0.6065306597126334
GN_EPS = 64e-5


def _scan_consts():
    s = np.arange(128)[:, None]
    t = np.arange(128)[None, :]
    tri = np.stack([(s <= t), (s >= t)]).astype(np.float32)
    strict = np.stack([(s < t), (s > t)]).astype(np.float32)
    mT = strict.transpose(0, 2, 1)
    m4 = np.concatenate([tri, strict, tri, mT], axis=2)
    lm = []
    for l in range(7):
        b = 1 << l
        lm.append(((s // (2 * b)) == (t // (2 * b))) & (((s // b) % 2) == 0) & (((t // b) % 2) == 1))
    lm = np.stack(lm).astype(np.float32)
    lmN = np.stack([lm, lm.transpose(0, 2, 1)]) + np.eye(128, dtype=np.float32)[None, None]
    return tri, m4, np.ascontiguousarray(lmN)


def _tt(S, eng, out, a, b, op, reads, writes):
    return S.op(eng, lambda e: e.tensor_tensor(out=out, in0=a, in1=b, op=op), reads=reads, writes=writes)


def _stt(S, eng, out, a, sc, b, op0, op1, reads, writes):
    return S.op("dve", lambda e: e.scalar_tensor_tensor(out=out, in0=a, scalar=sc, in1=b, op0=op0, op1=op1),
                reads=reads, writes=writes)


def _act(S, out, in_, func, reads, writes, **kw):
    return S.op("act", lambda e: e.activation(out=out, in_=in_, func=func, **kw), reads=reads, writes=writes)


def _h3(ap):
    return ap.rearrange("p (h k) -> p h k", k=64)


class Prog1(Prog):
    def __init__(self, nc, debug=()):
        super().__init__(nc, debug)
        dt = nc.dram_tensor
        I = lambda name, shape: dt(name, list(shape), F32, kind="ExternalInput").ap()
        self.rw_mu = I("rw_mu", [6, D])
        self.rw_wr = I("rw_wr", [D, D]); self.rw_wk = I("rw_wk", [D, D])
        self.rw_wv = I("rw_wv", [D, D]); self.rw_wo = I("rw_wo", [D, D])
        self.rw_w0 = I("rw_w0", [2, D]); self.rw_a0 = I("rw_a0", [2, D])
        self.w1cat = I("w1cat", [D, P]); self.a1cat = I("a1cat", [D, P]); self.rw_g1 = I("rw_g1", [D, P])
        self.w2cat = I("w2cat", [P, D]); self.a2cat = I("a2cat", [P, D]); self.rw_g2 = I("rw_g2", [P, D])
        self.rw_kk = I("rw_kk", [1, D]); self.rw_ka = I("rw_ka", [1, D]); self.rw_rk = I("rw_rk", [1, D])
        self.rw_lng = I("rw_lng", [1, D]); self.rw_lnb = I("rw_lnb", [1, D])
        self.tri_c = I("tri_c", [2, P, P]); self.m4_c = I("m4_c", [2, P, 512]); self.lmT_c = I("lmT_c", [2, 7, P, P])
        X = lambda name, shape, d=F32: (dt(name, list(shape), d, kind="ExternalOutput").ap() if name in debug
                                        else dt(name, list(shape), d).ap())
        self.hT_d = X("hT_d", [8, P, NTOK])
        self.featT_d = X("featT_d", [2, NT, P, 8 * 4 * P], BF16)
        self.vtok_d = X("vtok_d", [NTOK, D], BF16)
        self.bk_d = X("bk_d", [2, NT, P, 2 * D], BF16)
        self.gC_d = X("gC_d", [2, NT, P, 8])
        self.g_d = X("g_d", [NTOK, D])
        self.bonus_d = X("bonus_d", [NTOK, D])
        self.y_d = X("y_d", [2, NTOK, D])

    def phase_R0(self, V):
        S = self.S
        with S.scope():
            NB = self.make_norm_bufs("r0")
            xt = [S.sb(f"r0x{i}", [P, D], F32) for i in range(2)]
            ho = [S.sb(f"r0h{i}", [P, 8, P], F32) for i in range(2)]
            for T in range(NT):
                s = 1 if T < 2 else 0
                b = T % 2
                src, sk = self.x_src(1, T)
                S.dma("sp", xt[b][:], src, reads=[sk], writes=[("r0x", b)])
                self.norm_to_hT(NB, xt[b][:], ("r0x", b), V[("A", 0, s)], V[("B", 0, s)], ho[b][:], ("r0h", b))
                S.dma("sp", self.hT_d[:, :, T * P:(T + 1) * P].rearrange("c p t -> p c t"), ho[b][:],
                      reads=[("r0h", b)], writes=[("hT_d", T)])

    def phase_R1(self):
        S = self.S
        with S.scope():
            W = {}
            for nm, src in (("wr", self.rw_wr), ("wk", self.rw_wk), ("wv", self.rw_wv)):
                W[nm] = S.sb(nm, [P, 8, D], BF16)
                for c in range(0, 8, 4):
                    S.dma("pool", W[nm][:, c:c + 4, :], src[c * P:(c + 4) * P, :].rearrange("(c p) n -> p c n", p=P), writes=[nm])
            for nm, src in (("w1c", self.w1cat), ("a1c", self.a1cat), ("g1", self.rw_g1)):
                W[nm] = S.sb(nm, [P, 8, P], BF16)
                S.dma("pool", W[nm][:], src.rearrange("(c p) n -> p c n", p=P), writes=[nm])
            for nm, src in (("w2c", self.w2cat), ("a2c", self.a2cat), ("g2", self.rw_g2)):
                W[nm] = S.sb(nm, [P, D], BF16)
                S.dma("pool", W[nm][:], src, writes=[nm])
            R = {}
            for nm, src in (("kk_r", self.rw_kk), ("ka_r", self.rw_ka), ("rk_r", self.rw_rk),
                            ("w0_0", self.rw_w0[0:1, :]), ("w0_1", self.rw_w0[1:2, :]),
                            ("a0_0", self.rw_a0[0:1, :]), ("a0_1", self.rw_a0[1:2, :])):
                R[nm] = S.sb(nm, [P, D], F32)
                S.dma("sp", R[nm][:], src.broadcast_to([P, D]), writes=[nm])
            mu = S.sb("mu", [P, 6, 8], F32)
            with self.nc.allow_non_contiguous_dma(reason="tiny"):
                S.dma("sp", mu[:], self.rw_mu.rearrange("j (c p) -> p j c", p=P), writes=["mu"])
            tri = S.sb("tri", [P, 2, P], F32)
            S.dma("sp", tri[:], self.tri_c.rearrange("d s t -> s d t"), writes=["tri"])
            onef = S.sb("onef", [P, P], F32)
            S.op("dve", lambda e: e.memset(onef[:], 1.0), writes=["onef"])
            hbuf = S.sb("hbuf", [P, 8, P + 2], F32)
            xx = S.sb("xx", [P, 8, P], F32)
            mxt = S.sb("mxt", [P, 8, P], F32)
            mix = S.sb("mix", [P, 6, 8, P], BF16)
            hid = S.sb("hid", [P, 3, P], BF16)
            F = {n: S.sb(n, [P, D], F32) for n in ("r_sb", "k_sb", "v_sb", "kkn", "tA", "tB", "lw", "tC", "tD", "kd0", "kd1", "tE", "tF", "tG", "tH")}
            ob = [S.sb(f"ob{i}", [P, D], BF16) for i in range(4)]
            vb = S.sb("vb", [P, D], BF16)
            ft = S.sb("ft", [P, 8, 4, P], BF16)
            bkt = S.sb("bkt", [P, 2, D], BF16)
            st16 = S.sb("st16", [P, 64], F32)
            gcs = S.sb("gcs", [P, 8], F32)
            pA = [[S.ps(f"pA{i}_{h}", [P, 512], F32) for h in range(2)] for i in range(2)]
            pCl = [S.ps(f"pCl{h}", [P, 512], F32) for h in range(2)]
            pF = S.ps("pF", [P, 512], F32)
            pT = S.ps("pT", [P, 8, P], BF16)
            ipa = 0

            def proj(lhs_fn, rhs, rkey, K0=0, K=P, nchunks=8, lkeys=()):
                nonlocal ipa
                i = ipa % 2
                ipa += 1
                for hf in range(2):
                    for c in range(nchunks):
                        mm(S, pA[i][hf][:], lhs_fn(c), rhs(c, hf), c == 0, c == nchunks - 1,
                           reads=list(lkeys) + [rkey], writes=[("pA", i, hf)])
                return pA[i], [("pA", i, 0), ("pA", i, 1)]

            def evac2(fn_half):
                for hf in range(2):
                    fn_half(hf, slice(hf * 512, (hf + 1) * 512))

            for T in range(NT):
                seq_lo, seq_hi = (0, NCTX) if T < 2 else (NCTX, NTOK)
                t0 = T * P
                lo = max(t0 - 1, seq_lo); hi = min(t0 + P + 1, seq_hi)
                if lo > t0 - 1:
                    S.op("pool", lambda e: e.memset(hbuf[:, :, 0:1], 0.0), writes=["hbuf"])
                if hi < t0 + P + 1:
                    S.op("pool", lambda e: e.memset(hbuf[:, :, P + 1:P + 2], 0.0), writes=["hbuf"])
                S.dma("sp", hbuf[:, :, lo - (t0 - 1):hi - (t0 - 1)], self.hT_d[:, :, lo:hi].rearrange("c p t -> p c t"),
                      reads=[("hT_d", q) for q in range(max(T - 1, 0), min(T + 2, NT))], writes=["hbuf"])
                _tt(S, "dve", xx[:], hbuf[:, :, 0:P], hbuf[:, :, 2:P + 2], ALU.add, ["hbuf"], ["xx"])
                _stt(S, "dve", xx[:], xx[:], 0.5, hbuf[:, :, 1:P + 1], ALU.mult, ALU.subtract, ["xx", "hbuf"], ["xx"])
                for j in range(6):
                    _tt(S, "dve", mxt[:], xx[:], mu[:, j, :][:, :, None].broadcast_to([P, 8, P]), ALU.mult, ["xx", "mu"], ["mxt"])
                    _tt(S, "dve", mix[:, j, :, :], mxt[:], hbuf[:, :, 1:P + 1], ALU.add, ["mxt", "hbuf"], [("mix", j)])
                for hi_, (wn, mj, fn) in enumerate((("w1c", 1, AF.Tanh), ("a1c", 4, AF.Copy), ("g1", 5, AF.Sigmoid))):
                    for c in range(8):
                        mm(S, pF[:, 0:P], W[wn][:, c, :], mix[:, mj, c, :], c == 0, c == 7,
                           reads=[wn, ("mix", mj)], writes=["pF"])
                    _act(S, hid[:, hi_, :], pF[:, 0:P], fn, ["pF"], [("hid", hi_)])
                for nm, mj, wn in (("r_sb", 0, "wr"), ("k_sb", 2, "wk"), ("v_sb", 3, "wv")):
                    ps, pk = proj(lambda c: mix[:, mj, c, :], lambda c, hf: W[wn][:, c, hf * 512:(hf + 1) * 512], wn,
                                  lkeys=[("mix", mj)])
                    evac2(lambda hf, sl: _act(S, F[nm][:, sl], ps[hf][:], AF.Copy, [pk[hf]], [nm]))
                S.op("pool", lambda e: e.tensor_copy(out=vb[:], in_=F["v_sb"][:]), reads=["v_sb"], writes=["vb"])
                S.dma("sp", self.vtok_d[t0:t0 + P, :], vb[:], reads=["vb"], writes=[("vtok", T)])
                ps, pk = proj(lambda c: hid[:, 2, :], lambda c, hf: W["g2"][:, hf * 512:(hf + 1) * 512], "g2", nchunks=1,
                              lkeys=[("hid", 2)])
                evac2(lambda hf, sl: _act(S, F["tA"][:, sl], ps[hf][:], AF.Copy, [pk[hf]], ["tA"]))
                S.dma("sp", self.g_d[t0:t0 + P, :], F["tA"][:], reads=["tA"], writes=[("g_d", T)])
                _tt(S, "dve", F["tA"][:], F["k_sb"][:], R["kk_r"][:], ALU.mult, ["k_sb", "kk_r"], ["tA"])
                _tt(S, "pool", F["tB"][:], F["tA"][:], F["tA"][:], ALU.mult, ["tA"], ["tB"])
                S.op("dve", lambda e: e.tensor_reduce(out=st16[:, 0:16], in_=_h3(F["tB"][:]), axis=AX.X, op=ALU.add),
                     reads=["tB"], writes=["st16"])
                S.op("dve", lambda e: e.tensor_scalar(out=st16[:, 0:16], in0=st16[:, 0:16], scalar1=1e-24, scalar2=None, op0=ALU.max),
                     reads=["st16"], writes=["st16"])
                _act(S, st16[:, 0:16], st16[:, 0:16], AF.Sqrt, ["st16"], ["st16"])
                S.op("dve", lambda e: e.reciprocal(out=st16[:, 16:32], in_=st16[:, 0:16]), reads=["st16"], writes=["st16"])
                _tt(S, "dve", _h3(F["kkn"][:]), _h3(F["tA"][:]), st16[:, 16:32][:, :, None].broadcast_to([P, 16, 64]), ALU.mult,
                    ["tA", "st16"], ["kkn"])
                for d in range(2):
                    ps, pk = proj(lambda c: hid[d * 64:(d + 1) * 64, 0, :], lambda c, hf: W["w2c"][d * 64:(d + 1) * 64, hf * 512:(hf + 1) * 512],
                                  "w2c", nchunks=1, lkeys=[("hid", 0)])
                    evac2(lambda hf, sl: _tt(S, "dve", F["tB"][:, sl], ps[hf][:], R[f"w0_{d}"][:, sl], ALU.add, [pk[hf], f"w0_{d}"], ["tB"]))
                    _act(S, F["tB"][:], F["tB"][:], AF.Sigmoid, ["tB"], ["tB"])
                    _act(S, F["lw"][:], F["tB"][:], AF.Copy, ["tB"], ["lw"], scale=LWC)
                    ps, pk = proj(lambda c: hid[d * 64:(d + 1) * 64, 1, :], lambda c, hf: W["a2c"][d * 64:(d + 1) * 64, hf * 512:(hf + 1) * 512],
                                  "a2c", nchunks=1, lkeys=[("hid", 1)])
                    evac2(lambda hf, sl: _tt(S, "dve", F["tC"][:, sl], ps[hf][:], R[f"a0_{d}"][:, sl], ALU.add, [pk[hf], f"a0_{d}"], ["tC"]))
                    _act(S, F["tC"][:], F["tC"][:], AF.Sigmoid, ["tC"], ["tC"])
                    kd = F[f"kd{d}"]; kdk = f"kd{d}"
                    _stt(S, "dve", F["tD"][:], F["tC"][:], -1.0, R["ka_r"][:], ALU.add, ALU.mult, ["tC", "ka_r"], ["tD"])
                    _stt(S, "pool", kd[:], F["tD"][:], 1.0, F["k_sb"][:], ALU.add, ALU.mult, ["tD", "k_sb"], [kdk])
                    _tt(S, "pool", F["tC"][:], F["kkn"][:], F["tC"][:], ALU.mult, ["kkn", "tC"], ["tC"])
                    for hf in range(2):
                        mm(S, pCl[hf][:], tri[:, d, :], F["lw"][:, hf * 512:(hf + 1) * 512], True, True,
                           reads=["tri", "lw"], writes=[("pCl", hf)])
                    evac2(lambda hf, sl: _act(S, F["tE"][:, sl], pCl[hf][:], AF.Exp, [("pCl", hf)], ["tE"]))
                    evac2(lambda hf, sl: _act(S, F["tF"][:, sl], pCl[hf][:], AF.Exp, [("pCl", hf)], ["tF"], scale=-1.0))
                    for hf in range(2):
                        mm(S, pCl[hf][:], onef[:], F["lw"][:, hf * 512:(hf + 1) * 512], True, True,
                           reads=["onef", "lw"], writes=[("pCl", hf)])
                    evac2(lambda hf, sl: _act(S, F["tH"][:, sl], pCl[hf][:], AF.Exp, [("pCl", hf)], ["tH"]))
                    _act(S, F["tG"][:], F["lw"][:], AF.Exp, ["lw"], ["tG"], scale=-1.0)
                    _tt(S, "dve", F["tG"][:], F["tG"][:], F["tE"][:], ALU.mult, ["tG", "tE"], ["tG"])
                    _tt(S, "pool", F["tH"][:], F["tH"][:], F["tF"][:], ALU.mult, ["tH", "tF"], ["tH"])
                    for j in range(8):
                        mm(S, pF[:, 256 + j:257 + j], F["lw"][:, j * P:(j + 1) * P], onef[:, 0:1], True, True,
                           reads=["lw", "onef"], writes=["pF"])
                    _act(S, gcs[:], pF[:, 256:264], AF.Exp, ["pF"], ["gcs"])
                    S.dma("sp", self.gC_d[d, T], gcs[:], reads=["gcs"], writes=[("gC_d", d, T)])
                    _stt(S, "dve", ob[0][:], F["kkn"][:], -1.0, F["tG"][:], ALU.mult, ALU.mult, ["kkn", "tG"], [("ob", 0)])
                    _tt(S, "pool", ob[1][:], F["r_sb"][:], F["tE"][:], ALU.mult, ["r_sb", "tE"], [("ob", 1)])
                    _tt(S, "dve", ob[2][:], F["tC"][:], F["tF"][:], ALU.mult, ["tC", "tF"], [("ob", 2)])
                    _tt(S, "pool", ob[3][:], kd[:], F["tF"][:], ALU.mult, [kdk, "tF"], [("ob", 3)])
                    _tt(S, "dve", bkt[:, 0, :], F["tC"][:], F["tH"][:], ALU.mult, ["tC", "tH"], ["bkt"])
                    _tt(S, "pool", bkt[:, 1, :], kd[:], F["tH"][:], ALU.mult, [kdk, "tH"], ["bkt"])
                    S.dma("sp", self.bk_d[d, T], bkt[:].rearrange("p a n -> p (a n)"), reads=["bkt"], writes=[("bk_d", d, T)])
                    for q in range(4):
                        for c in range(8):
                            S.op("pe", lambda e: e.transpose(out=pT[:, c, :], in_=ob[q][:, c * P:(c + 1) * P], identity=self.idb[:]),
                                 reads=[("ob", q), "idb"], writes=["pT"], accum=(c > 0))
                        if q % 2 == 0:
                            _act(S, ft[:, :, q, :], pT[:], AF.Copy, ["pT"], ["ft"])
                        else:
                            S.op("dve", lambda e: e.tensor_copy(out=ft[:, :, q, :], in_=pT[:]), reads=["pT"], writes=["ft"])
                    S.dma("sp", self.featT_d[d, T], ft[:].rearrange("p j q t -> p (j q t)"), reads=["ft"], writes=[("featT_d", d, T)])
                _tt(S, "pool", F["tD"][:], F["kd0"][:], F["kd1"][:], ALU.add, ["kd0", "kd1"], ["tD"])
                _tt(S, "pool", F["tD"][:], F["tD"][:], F["r_sb"][:], ALU.mult, ["tD", "r_sb"], ["tD"])
                _tt(S, "pool", F["tD"][:], F["tD"][:], R["rk_r"][:], ALU.mult, ["tD", "rk_r"], ["tD"])
                S.op("dve", lambda e: e.tensor_reduce(out=st16[:, 32:48], in_=_h3(F["tD"][:]), axis=AX.X, op=ALU.add),
                     reads=["tD"], writes=["st16"])
                _tt(S, "dve", _h3(F["tD"][:]), _h3(F["v_sb"][:]), st16[:, 32:48][:, :, None].broadcast_to([P, 16, 64]), ALU.mult,
                    ["v_sb", "st16"], ["tD"])
                S.dma("sp", self.bonus_d[t0:t0 + P, :], F["tD"][:], reads=["tD"], writes=[("bonus_d", T)])

    def phase_R2(self):
        S = self.S
        with S.scope():
            m4 = S.sb("m4", [P, 2, 512], F32)
            lmN = S.sb("lmN", [P, 2, 7, P], F32)
            S.dma("sp", m4[:], self.m4_c.rearrange("d s n -> s d n"), writes=["m4"])
            S.dma("sp", lmN[:], self.lmT_c.rearrange("d l s n -> s d l n"), writes=["lmN"])
            idb = self.idb
            NG = 4
            ST32 = [S.sb(f"ST32_{d}", [P, 8, 64], F32) for d in range(2)]
            STb = [S.sb(f"STb_{d}", [P, 8, 64], BF16) for d in range(2)]
            for d in range(2):
                S.op("dve", lambda e: e.memset(ST32[d][:], 0.0), writes=[("ST32", d)])
                S.op("dve", lambda e: e.memset(STb[d][:], 0.0), writes=[("STb", d)])
            NBUF = 3
            Fb = [S.sb(f"Fb{i}", [P, 8, 4, P], BF16) for i in range(NBUF)]
            Vb = [S.sb(f"Vb{i}", [P, D], BF16) for i in range(NBUF)]
            BKb = [S.sb(f"BKb{i}", [P, 2, D], BF16) for i in range(NBUF)]
            gCb = [S.sb(f"gCb{i}", [P, 8], F32) for i in range(NBUF)]
            ysb = [S.sb(f"ysb{i}", [P, D], F32) for i in range(NBUF)]
            SL = []
            for sl in range(2):
                R_ = dict(
                    GM=S.sb(f"GM{sl}", [P, NG, 512], BF16),
                    X=[S.sb(f"X{sl}_{i}", [P, NG, P], BF16) for i in range(2)],
                    XT=[S.sb(f"XT{sl}_{i}", [P, NG, P], BF16) for i in range(2)],
                    T1s=S.sb(f"T1s{sl}", [P, NG, P], BF16),
                    Zq=S.sb(f"Zq{sl}", [P, NG, 64], BF16),
                    Pb=S.sb(f"Pb{sl}", [P, NG, 64], BF16),
                    bk=[S.ps(f"bk{sl}_{i}", [P, NG, P], F32) for i in range(3)],
                    bz=S.ps(f"bz{sl}", [P, 8, 64], F32),
                    sl=sl)
                SL.append(R_)

            items = []
            it = 0
            for d in range(2):
                order = list(range(NT)) if d == 0 else [1, 0] + list(range(NT - 1, 1, -1))
                for ci, T in enumerate(order):
                    for g0 in range(0, 16, NG):
                        items.append(dict(d=d, T=T, g0=g0, b=it % NBUF))
                    it += 1

            def heads_of(g0):
                return [(g, g0 + g, (g0 + g) // 2, ((g0 + g) % 2) * 64) for g in range(NG)]

            def load_chunk(w):
                d, T, b = w["d"], w["T"], w["b"]
                S.dma("sp", Fb[b][:].rearrange("p j q t -> p (j q t)"), self.featT_d[d, T], reads=[("featT_d", d, T)], writes=[("Fb", b)])
                S.dma("sp", Vb[b][:], self.vtok_d[T * P:(T + 1) * P, :], reads=[("vtok", T)], writes=[("Vb", b)])
                S.dma("sp", BKb[b][:].rearrange("p a n -> p (a n)"), self.bk_d[d, T], reads=[("bk_d", d, T)], writes=[("BKb", b)])
                S.dma("sp", gCb[b][:], self.gC_d[d, T], reads=[("gC_d", d, T)], writes=[("gCb", b)])

            def run_group(w, R_):
                d, T, b, g0, sl = w["d"], w["T"], w["b"], w["g0"], R_["sl"]
                if g0 == 0:
                    load_chunk(w)
                Fk, Vk, BKk, gk = ("Fb", b), ("Vb", b), ("BKb", b), ("gCb", b)
                GM, X, XT, T1s, Zq, Pb, bk, bz = (R_[n] for n in ("GM", "X", "XT", "T1s", "Zq", "Pb", "bk", "bz"))
                K = lambda n, *a: (n, sl) + a
                hs = heads_of(g0)
                F_ = Fb[b]
                st32, stb = ST32[d], STb[d]
                for (g, h, j, pb_) in hs:
                    bank = bk[g % 3]; bkk = K("bk", g % 3)
                    bv = bank[:].rearrange("p g t -> p (g t)")
                    AR = F_[pb_:pb_ + 64, j, 0:2, :].rearrange("p q t -> p (q t)")
                    mm(S, bv[:, 0:128], F_[pb_:pb_ + 64, j, 2, :], F_[pb_:pb_ + 64, j, 1, :], True, True, reads=[Fk], writes=[bkk])
                    mm(S, bv[:, 128:384], F_[pb_:pb_ + 64, j, 3, :], AR, True, True, reads=[Fk], writes=[bkk])
                    mm(S, bv[:, 384:512], F_[pb_:pb_ + 64, j, 0, :], F_[pb_:pb_ + 64, j, 2, :], True, True, reads=[Fk], writes=[bkk])
                    _tt(S, "dve", GM[:, g, :], bv, m4[:, d, :], ALU.mult, ["m4"], [bkk, K("GM")])
                yield
                for (g, h, j, pb_) in hs:
                    mm(S, bz[:, g, :], F_[pb_:pb_ + 64, j, 0, :], stb[pb_:pb_ + 64, j, :], True, False, reads=[Fk, ("STb", d)], writes=[K("bz")])
                    mm(S, bz[:, g, :], GM[:, g, 128:256], Vb[b][:, h * 64:(h + 1) * 64], False, True, reads=[K("GM"), Vk], writes=[K("bz")])
                S.op("dve", lambda e: e.tensor_copy(out=Zq[:], in_=bz[:, 0:NG, :]), reads=[], writes=[K("bz"), K("Zq")])
                yield
                xi = 0
                for (g, h, j, pb_) in hs:
                    mm(S, bk[0][:, g, :], GM[:, g, 384:512], idb[:], True, False, reads=[K("GM"), "idb"], writes=[K("bk", 0)])
                    mm(S, bk[0][:, g, :], idb[:], idb[:], False, True, reads=["idb"], writes=[K("bk", 0)])
                _tt(S, "dve", X[xi][:], bk[0][:], lmN[:, d, 0:1, :].broadcast_to([P, NG, P]), ALU.mult, ["lmN"], [K("bk", 0), K("X", xi)])
                yield
                for (g, h, j, pb_) in hs:
                    mm(S, bk[2][:, g, :], X[xi][:, g, :], idb[:], True, True, reads=[K("X", xi), "idb"], writes=[K("bk", 2)])
                _act(S, XT[xi][:], bk[2][:], AF.Copy, [], [K("bk", 2), K("XT", xi)])
                yield
                for l in range(1, 7):
                    for (g, h, j, pb_) in hs:
                        mm(S, bk[0][:, g, :], GM[:, g, 384:512], X[xi][:, g, :], True, False, reads=[K("GM"), K("X", xi)], writes=[K("bk", 0)])
                        mm(S, bk[0][:, g, :], idb[:], idb[:], False, True, reads=["idb"], writes=[K("bk", 0)])
                    _tt(S, "dve", T1s[:], bk[0][:], lmN[:, d, l:l + 1, :].broadcast_to([P, NG, P]), ALU.mult, ["lmN"], [K("bk", 0), K("T1s")])
                    yield
                    for (g, h, j, pb_) in hs:
                        mm(S, bk[1][:, g, :], XT[xi][:, g, :], T1s[:, g, :], True, True, reads=[K("XT", xi), K("T1s")], writes=[K("bk", 1)])
                    if l < 6:
                        for (g, h, j, pb_) in hs:
                            mm(S, bk[2][:, g, :], T1s[:, g, :], XT[xi][:, g, :], True, True, reads=[K("XT", xi), K("T1s")], writes=[K("bk", 2)])
                    _act(S, X[1 - xi][:], bk[1][:], AF.Copy, [], [K("bk", 1), K("X", 1 - xi)])
                    if l < 6:
                        if l % 2 == 0:
                            _act(S, XT[1 - xi][:], bk[2][:], AF.Copy, [], [K("bk", 2), K("XT", 1 - xi)])
                        else:
                            S.op("dve", lambda e: e.tensor_copy(out=XT[1 - xi][:], in_=bk[2][:]), reads=[], writes=[K("bk", 2), K("XT", 1 - xi)])
                    xi = 1 - xi
                    yield
                for (g, h, j, pb_) in hs:
                    mm(S, bz[:, g, :], X[xi][:, g, :], Zq[:, g, :], True, True, reads=[K("X", xi), K("Zq")], writes=[K("bz")])
                S.op("dve", lambda e: e.tensor_copy(out=Pb[:], in_=bz[:, 0:NG, :]), reads=[], writes=[K("bz"), K("Pb")])
                yield
                for (g, h, j, pb_) in hs:
                    yo = bz[:, g, :]
                    mm(S, yo, GM[:, g, 0:128], Pb[:, g, :], True, False, reads=[K("GM"), K("Pb")], writes=[K("bz")])
                    mm(S, yo, GM[:, g, 256:384], Vb[b][:, h * 64:(h + 1) * 64], False, False, reads=[K("GM"), Vk], writes=[K("bz")])
                    mm(S, yo, F_[pb_:pb_ + 64, j, 1, :], stb[pb_:pb_ + 64, j, :], False, True, reads=[Fk, ("STb", d)], writes=[K("bz")])
                for (g, h, j, pb_) in hs:
                    mm(S, bz[:, 4 + g, :], BKb[b][:, 0, j * P:(j + 1) * P], Pb[:, g, :], True, False, reads=[BKk, K("Pb")], writes=[K("bz")])
                    mm(S, bz[:, 4 + g, :], BKb[b][:, 1, j * P:(j + 1) * P], Vb[b][:, h * 64:(h + 1) * 64], False, True,
                       reads=[BKk, Vk], writes=[K("bz")])
                _act(S, ysb[b][:, g0 * 64:(g0 + NG) * 64].rearrange("p (g v) -> p g v", v=64), bz[:, 0:NG, :], AF.Copy, [], [K("bz"), ("ysb", b)])
                for (g, h, j, pb_) in hs:
                    _stt(S, "dve", st32[pb_:pb_ + 64, j, :], st32[pb_:pb_ + 64, j, :], gCb[b][pb_:pb_ + 64, j:j + 1],
                         bz[pb_:pb_ + 64, 4 + g, :], ALU.mult, ALU.add, [gk], [("ST32", d), K("bz")])
                S.op("pool", lambda e: e.tensor_copy(out=stb[:, g0 // 2:g0 // 2 + 2, :], in_=st32[:, g0 // 2:g0 // 2 + 2, :]),
                     reads=[("ST32", d)], writes=[("STb", d)])
                if g0 + NG == 16:
                    S.dma("sp", self.y_d[d, T * P:(T + 1) * P, :], ysb[b][:], reads=[("ysb", b)], writes=[("y_d", d, T)])
                yield

            nxt = 0
            active = [None, None]
            while True:
                progressed = False
                for sl in range(2):
                    if active[sl] is None and nxt < len(items):
                        active[sl] = run_group(items[nxt], SL[sl])
                        nxt += 1
                    if active[sl] is not None:
                        progressed = True
                        try:
                            next(active[sl])
                        except StopIteration:
                            active[sl] = None
                if not progressed:
                    break

    def phase_R3(self):
        S = self.S
        with S.scope():
            R = {}
            for nm, src in (("lng_r", self.rw_lng), ("lnb_r", self.rw_lnb)):
                R[nm] = S.sb(nm, [P, D], F32)
                S.dma("sp", R[nm][:], src.broadcast_to([P, D]), writes=[nm])
            B = [{n: S.sb(f"{n}{i}", [P, D], F32) for n in ("yf", "yb", "gg", "bo")} for i in range(2)]
            zb = [S.sb(f"zb{i}", [P, D], BF16) for i in range(2)]
            zt = [S.sb(f"zt3_{i}", [P, 8, P], BF16) for i in range(2)]
            st = [S.sb(f"st3_{i}", [P, 64], F32) for i in range(2)]
            pT = [S.ps(f"pT3_{i}", [P, 8, P], BF16) for i in range(2)]
            for T in range(2, NT):
                b = T % 2
                Bf = B[b]
                k = lambda n: (n, b)
                t0 = T * P
                S.dma("sp", Bf["yf"][:], self.y_d[0, t0:t0 + P, :], reads=[("y_d", 0, T)], writes=[k("yf")])
                S.dma("sp", Bf["yb"][:], self.y_d[1, t0:t0 + P, :], reads=[("y_d", 1, T)], writes=[k("yb")])
                S.dma("sp", Bf["gg"][:], self.g_d[t0:t0 + P, :], reads=[("g_d", T)], writes=[k("gg")])
                S.dma("sp", Bf["bo"][:], self.bonus_d[t0:t0 + P, :], reads=[("bonus_d", T)], writes=[k("bo")])
                y = Bf["yf"]; t2 = Bf["yb"]
                _tt(S, "dve", y[:], y[:], t2[:], ALU.add, [k("yf"), k("yb")], [k("yf")])
                S.op("dve", lambda e: e.tensor_reduce(out=st[b][:, 0:16], in_=_h3(y[:]), axis=AX.X, op=ALU.add), reads=[k("yf")], writes=[k("st")])
                S.op("dve", lambda e: e.tensor_scalar(out=st[b][:, 0:16], in0=st[b][:, 0:16], scalar1=-1.0 / 64, scalar2=None, op0=ALU.mult),
                     reads=[k("st")], writes=[k("st")])
                _tt(S, "dve", _h3(y[:]), _h3(y[:]), st[b][:, 0:16][:, :, None].broadcast_to([P, 16, 64]), ALU.add, [k("yf"), k("st")], [k("yf")])
                _tt(S, "pool", t2[:], y[:], y[:], ALU.mult, [k("yf")], [k("yb")])
                S.op("dve", lambda e: e.tensor_reduce(out=st[b][:, 16:32], in_=_h3(t2[:]), axis=AX.X, op=ALU.add), reads=[k("yb")], writes=[k("st")])
                S.op("dve", lambda e: e.tensor_scalar(out=st[b][:, 16:32], in0=st[b][:, 16:32], scalar1=1.0 / 64, scalar2=GN_EPS, op0=ALU.mult, op1=ALU.add),
                     reads=[k("st")], writes=[k("st")])
                _act(S, st[b][:, 16:32], st[b][:, 16:32], AF.Sqrt, [k("st")], [k("st")])
                S.op("dve", lambda e: e.reciprocal(out=st[b][:, 32:48], in_=st[b][:, 16:32]), reads=[k("st")], writes=[k("st")])
                _tt(S, "dve", _h3(y[:]), _h3(y[:]), st[b][:, 32:48][:, :, None].broadcast_to([P, 16, 64]), ALU.mult, [k("yf"), k("st")], [k("yf")])
                _tt(S, "pool", y[:], y[:], R["lng_r"][:], ALU.mult, [k("yf"), "lng_r"], [k("yf")])
                _tt(S, "pool", y[:], y[:], R["lnb_r"][:], ALU.add, [k("yf"), "lnb_r"], [k("yf")])
                _tt(S, "dve", y[:], y[:], Bf["bo"][:], ALU.add, [k("yf"), k("bo")], [k("yf")])
                _tt(S, "dve", zb[b][:], y[:], Bf["gg"][:], ALU.mult, [k("yf"), k("gg")], [k("zb")])
                for c in range(8):
                    S.op("pe", lambda e: e.transpose(out=pT[b][:, c, :], in_=zb[b][:, c * P:(c + 1) * P], identity=self.idb[:]),
                         reads=[k("zb"), "idb"], writes=[k("pT3")], accum=(c > 0))
                _act(S, zt[b][:], pT[b][:], AF.Copy, [k("pT3")], [k("zt3")])
                S.dma("sp", self.zT_d[:, :, t0:t0 + P].rearrange("c p t -> p c t"), zt[b][:], reads=[k("zt3")],
                      writes=[("zT", c, T) for c in range(8)])


def build_program(stop_after=None, debug=(), phases="M0ABCD1abcde"):
    nc = bass.Bass("TRN2", target_bir_lowering=False)
    Pg = Prog1(nc, debug)
    S = Pg.S
    Pg.consts()
    if "M" in phases:
        Pg.phase_mod()
    if "0" in phases:
      with S.scope():
        V0 = Pg.load_layer_vecs(0)
        if "A" in phases: Pg.phase_L0_proj(V0)
        if "B" in phases: Pg.phase_L0_pool()
        if "C" in phases: Pg.phase_L0_attn()
        if "D" in phases: Pg.phase_out_mlp(0, V0, Pg.ev_w_out, Pg.y_tile_L0, lambda T: (Pg.x_d[T * P:(T + 1) * P, :], ("x_d", T)))
    if "1" in phases:
      with S.scope():
        V1 = Pg.load_layer_vecs(1)
        if "a" in phases: Pg.phase_R0(V1)
        if "b" in phases: Pg.phase_R1()
        if "c" in phases: Pg.phase_R2()
        if "d" in phases: Pg.phase_R3()
        if "e" in phases: Pg.phase_out_mlp(1, V1, Pg.rw_wo, Pg.y_tile_L0, lambda T: (Pg.out[(T - 2) * P:(T - 1) * P, :], ("out", T)))
    S.barrier()
    S.finish([])
    S.close()
    return nc, Pg


_host_inputs0 = host_inputs


def host_inputs(inputs):
    maps = _host_inputs0(inputs)
    f = lambda a: np.ascontiguousarray(np.asarray(a, dtype=np.float32))
    tri, m4, lmT = _scan_consts()
    sh = {
        "rw_mu": f(inputs["rw_mu"])[0], "rw_wr": f(inputs["rw_wr"])[0], "rw_wk": f(inputs["rw_wk"])[0],
        "rw_wv": f(inputs["rw_wv"])[0], "rw_wo": f(inputs["rw_wo"])[0],
        "rw_w0": f(inputs["rw_w0"])[0], "rw_a0": f(inputs["rw_a0"])[0],
        "w1cat": f(np.concatenate([inputs["rw_w1"][0, 0], inputs["rw_w1"][0, 1]], axis=1)),
        "a1cat": f(np.concatenate([inputs["rw_a1"][0, 0], inputs["rw_a1"][0, 1]], axis=1)),
        "rw_g1": f(inputs["rw_g1"])[0],
        "w2cat": f(np.asarray(inputs["rw_w2"])[0].reshape(128, 1024)), "a2cat": f(np.asarray(inputs["rw_a2"])[0].reshape(128, 1024)),
        "rw_g2": f(inputs["rw_g2"])[0],
        "rw_kk": f(inputs["rw_kk"]).reshape(1, 1024), "rw_ka": f(inputs["rw_ka"]).reshape(1, 1024),
        "rw_rk": f(inputs["rw_rk"]).reshape(1, 1024), "rw_lng": f(inputs["rw_lng"]).reshape(1, 1024),
        "rw_lnb": f(inputs["rw_lnb"]).reshape(1, 1024),
        "tri_c": f(tri), "m4_c": f(m4), "lmT_c": f(lmT),
    }
    for m in maps:
        m.update(sh)
    return maps
```

```python
import contextlib
import numpy as np
import concourse.bass as bass
import concourse.mybir as mybir

F32 = mybir.dt.float32
BF16 = mybir.dt.bfloat16
AF = mybir.ActivationFunctionType
ALU = mybir.AluOpType
AX = mybir.AxisListType

SEM_LIMIT = 10000


class _Ctr:
    def __init__(self, S, name, step):
        self.S = S
        self.name = name
        self.step = step
        self.gen = 0
        self.sem = S._newsem(f"{name}_0")
        self.val = 0

    def next_event(self):
        if self.val + self.step > SEM_LIMIT:
            self.gen += 1
            self.sem = self.S._newsem(f"{self.name}_{self.gen}")
            self.val = 0
        self.val += self.step
        return (self.sem, self.val)


class _PsView:
    def __init__(self, t, shape):
        self.t = t
        self.n1 = shape[1]

    def __getitem__(self, key):
        if not isinstance(key, tuple):
            key = (key,)
        key = list(key)
        if len(key) < 2:
            key.append(slice(None))
        k1 = key[1]
        if isinstance(k1, slice):
            start, stop, step = k1.indices(self.n1)
            key[1] = slice(start, stop, step)
        return self.t[tuple(key)]


class _Eng:
    def __init__(self, S, name, obj):
        self.name = name
        self.obj = obj
        self.ctr = _Ctr(S, "s_" + name, 1)
        self.seen = {}
        self.n_issued = 0
        self.last_ins = None
        self.last_has_inc = False
        self.inc_idx = []
        self.inc_ev = []


class LazyEv:
    __slots__ = ("eng", "idx")

    def __init__(self, eng, idx):
        self.eng = eng
        self.idx = idx


class _Res:
    __slots__ = ("w", "r")

    def __init__(self):
        self.w = None
        self.r = {}


class Sched:
    def __init__(self, nc, n_dma_slots=8):
        self.nc = nc
        self.stack = contextlib.ExitStack()
        self.scopes = [self.stack]
        self.res = {}
        self.engs = {
            "pe": _Eng(self, "pe", nc.tensor),
            "act": _Eng(self, "act", nc.scalar),
            "dve": _Eng(self, "dve", nc.vector),
            "pool": _Eng(self, "pool", nc.gpsimd),
            "sp": _Eng(self, "sp", nc.sync),
        }
        self.dma_slots = {}
        for q in ("sp", "pool"):
            self.dma_slots[q] = [_Ctr(self, f"d_{q}{i}", 16) for i in range(n_dma_slots)]
        self.dma_rr = {"sp": 0, "pool": 0}
        self.n_inst = 0
        self.uid = 0
        self.pending = None
        self.lazy_engines = ()

    def _newsem(self, name):
        return self.stack.enter_context(self.nc.semaphore(name))

    def sb(self, name, shape, dt):
        self.uid += 1
        return self.scopes[-1].enter_context(self.nc.sbuf_tensor(f"sb{self.uid}_{name}", list(shape), dt))

    def ps(self, name, shape, dt=F32):
        self.uid += 1
        esz = 4 if dt == F32 else 2
        per_part = esz
        for d_ in shape[1:]:
            per_part *= d_
        assert per_part <= 2048, (name, shape)
        shape = list(shape)
        if per_part < 2048:
            rest = per_part // shape[1]
            assert 2048 % rest == 0, (name, shape)
            full = [shape[0], 2048 // rest] + shape[2:]
            t = self.scopes[-1].enter_context(self.nc.psum_tensor(f"ps{self.uid}_{name}", full, dt))
            return _PsView(t, shape)
        return self.scopes[-1].enter_context(self.nc.psum_tensor(f"ps{self.uid}_{name}", shape, dt))

    @contextlib.contextmanager
    def scope(self):
        st = contextlib.ExitStack()
        self.scopes.append(st)
        try:
            yield
        finally:
            self.barrier()
            self.scopes.pop()
            st.close()

    def barrier(self):
        evs = []
        for e in self.engs.values():
            if e.n_issued > 0:
                evs.append(self._resolve(LazyEv(e, e.n_issued - 1)))
        for q in self.dma_slots:
            for ctr in self.dma_slots[q]:
                if ctr.val > 0:
                    evs.append((ctr.sem, ctr.val))
        for e in self.engs.values():
            for ev in evs:
                self._wait(e, ev)

    def _r(self, key):
        r = self.res.get(key)
        if r is None:
            r = self.res[key] = _Res()
        return r

    def _resolve(self, ev):
        if not isinstance(ev, LazyEv):
            return ev
        import bisect
        e = ev.eng
        k = bisect.bisect_left(e.inc_idx, ev.idx)
        if k < len(e.inc_idx):
            return e.inc_ev[k]
        assert e.last_ins is not None and not e.last_has_inc and e.n_issued - 1 >= ev.idx
        sv = e.ctr.next_event()
        e.last_ins.then_inc(sv[0], 1)
        e.last_has_inc = True
        e.inc_idx.append(e.n_issued - 1)
        e.inc_ev.append(sv)
        return sv

    def _wait(self, eng, ev):
        if ev is None:
            return
        if isinstance(ev, LazyEv) and ev.eng is eng and eng.name == "pe":
            return
        sem, val = self._resolve(ev)
        k = id(sem)
        if eng.seen.get(k, 0) >= val:
            return
        if self.pending is not None:
            cur = self.pending.get(k)
            if cur is None or cur[1] < val:
                self.pending[k] = (sem, val)
            return
        eng.obj.wait_ge(sem, val)
        eng.seen[k] = val

    def _flush(self, eng):
        pend = list(self.pending.values())
        self.pending = None
        for (sem, val) in pend[:-1]:
            eng.obj.wait_ge(sem, val)
            eng.seen[id(sem)] = val
        if pend:
            sem, val = pend[-1]
            eng.seen[id(sem)] = val
            return (sem, val)
        return None

    def _deps(self, eng, reads, writes, skip_same_eng_write=False):
        for key in reads:
            r = self._r(key)
            self._wait(eng, r.w)
        inorder = eng.name in ("act", "dve")
        for key in writes:
            r = self._r(key)
            if not ((skip_same_eng_write or inorder) and isinstance(r.w, LazyEv) and r.w.eng is eng):
                self._wait(eng, r.w)
            for ev in r.r.values():
                if inorder and isinstance(ev, LazyEv) and ev.eng is eng:
                    continue
                self._wait(eng, ev)

    def _commit(self, ev, reads, writes):
        rk = ev.eng.name if isinstance(ev, LazyEv) else id(ev[0])
        for key in reads:
            self._r(key).r[rk] = ev
        for key in writes:
            r = self._r(key)
            r.w = ev
            r.r = {}

    def op(self, engname, fn, reads=(), writes=(), accum=False):
        eng = self.engs[engname]
        self.pending = {}
        self._deps(eng, reads, writes, skip_same_eng_write=accum)
        last = self._flush(eng)
        ins = fn(eng.obj)
        if last is not None:
            ins._wait_ge(last[0], last[1])
        eng.last_ins = ins
        eng.last_has_inc = False
        ev = LazyEv(eng, eng.n_issued)
        eng.n_issued += 1
        if engname not in self.lazy_engines:
            self._resolve(ev)
        self._commit(ev, reads, writes)
        self.n_inst += 1
        return ev

    def dma(self, q, out, in_, reads=(), writes=(), **kw):
        eng = self.engs[q]
        slots = self.dma_slots[q]
        i = self.dma_rr[q]
        self.dma_rr[q] = (i + 1) % len(slots)
        ctr = slots[i]
        self.pending = {}
        if ctr.val > 0:
            self._wait(eng, (ctr.sem, ctr.val))
        self._deps(eng, reads, writes)
        last = self._flush(eng)
        ev = ctr.next_event()
        ins = eng.obj.dma_start(out=out, in_=in_, **kw)
        if last is not None:
            ins._wait_ge(last[0], last[1])
        ins.then_inc(ev[0], 16)
        self._commit(ev, reads, writes)
        self.n_inst += 1
        return ev

    def finish(self, final_keys):
        eng = self.engs["sp"]
        for key in final_keys:
            r = self._r(key)
            self._wait(eng, r.w)
        for q in self.dma_slots:
            for ctr in self.dma_slots[q]:
                if ctr.val > 0:
                    self._wait(eng, (ctr.sem, ctr.val))

    def close(self):
        self.stack.close()

from concourse.bass_utils import run_bass_kernel_spmd

D = 1024
NCTX = 256
NLAT = 4096
NTOK = NCTX + NLAT
NT = NTOK // 128
EPS = 1e-6
P = 128


def _pool_bands():
    L = 1024
    out = np.zeros((4, 5, 128, 128), np.float32)
    for g, w in enumerate((2, 4, 8, 16)):
        def full(L):
            t = np.arange(L)
            lo = np.clip(t - w // 2, 0, L)
            hi = np.clip(t + w // 2, 0, L)
            s = np.arange(L)[:, None]
            m = ((s >= lo[None, :]) & (s < hi[None, :])).astype(np.float64) / (hi - lo)[None, :]
            m -= np.eye(L)
            return m
        m = full(L)
        out[g, 0] = m[3 * 128:4 * 128, 4 * 128:5 * 128]
        out[g, 1] = m[5 * 128:6 * 128, 4 * 128:5 * 128]
        out[g, 2] = m[4 * 128:5 * 128, 4 * 128:5 * 128]
        out[g, 3] = m[0:128, 0:128]
        out[g, 4] = m[L - 128:, L - 128:]
    return out


_VARS = [(-2, "pm"), (-1, "f"), (0, "f"), (1, "f"), (2, "pp")] + [(d, "f") for d in range(-3, 4)]


def _attn_tables():
    kc = np.arange(64)
    qc = np.arange(64)
    c_start = np.clip(qc - 8, 0, 48)
    col_ok = (kc[:, None] >= c_start[None, :]) & (kc[:, None] < c_start[None, :] + 16)
    dc_idx = np.clip(kc[:, None] - qc[None, :], -15, 15) + 15
    dr_idx = np.zeros((12, 128, 128), np.int64)
    dc_full = np.zeros((128, 128), np.int64)
    mask = np.zeros((12, 128, 128), np.float32)
    for a in range(2):
        for b in range(2):
            dc_full[a * 64:(a + 1) * 64, b * 64:(b + 1) * 64] = dc_idx
    for v, (dl, kind) in enumerate(_VARS):
        for a in range(2):
            for b in range(2):
                dr = 2 * dl + a - b + 7
                vis = True
                if kind == "pm":
                    vis = not (a == 0 and b == 1)
                elif kind == "pp":
                    vis = (a == 0 and b == 1)
                dr_idx[v, a * 64:(a + 1) * 64, b * 64:(b + 1) * 64] = min(max(dr, 0), 14)
                if vis and 0 <= dr <= 14:
                    mask[v, a * 64:(a + 1) * 64, b * 64:(b + 1) * 64] = col_ok
    return dr_idx, dc_full, mask


def mm(S, out, lhsT, rhs, start, stop, reads, writes):
    return S.op("pe", lambda e: e.matmul(out, lhsT=lhsT, rhs=rhs, start=start, stop=stop),
                reads=reads, writes=writes, accum=not start)


class Prog:
    def __init__(self, nc, debug=()):
        self.nc = nc
        self.S = Sched(nc)
        self.debug = debug
        self.dbg_out = {}
        dt = nc.dram_tensor
        I = lambda name, shape: dt(name, list(shape), F32, kind="ExternalInput").ap()
        self.x_in = I("x", [NLAT, D])
        self.ctx_in = I("ctx", [NCTX, D])
        self.cvec = I("cvec", [P, 8, 2])
        self.ada_w = I("ada_w", [2, D, 6 * D])
        self.ada_b = I("ada_b", [2, 6 * D])
        self.norm_g = I("norm_g", [2, 4, D])
        self.mlp_w1 = I("mlp_w1", [2, D, 4 * D])
        self.mlp_w2 = I("mlp_w2", [2, 4 * D, D])
        self.ev_w_in = I("ev_w_in", [D, 2 * D])
        self.ev_w_out = I("ev_w_out", [D, D])
        self.ev_pool_w = I("ev_pool_w", [4, P, P])
        self.ev_pool_scale = I("ev_pool_scale", [512])
        self.rpb_tab = I("rpb_tab", [P, 8 * 12, P])
        self.msk_tab = I("msk_tab", [P, 12, P])
        self.bands = I("bands", [P, 20, P])
        self.ident = I("ident", [P, P])
        self.out = dt("out", [NLAT, D], F32, kind="ExternalOutput").ap()
        X = lambda name, shape, d=F32: (dt(name, list(shape), d, kind="ExternalOutput").ap() if name in debug
                                        else dt(name, list(shape), d).ap())
        self.modd = X("modd", [2, 2, 6 * D])
        self.x_d = X("x_d", [NTOK, D])
        self.upool_d = X("upool_d", [NTOK, 512], BF16)
        self.v_d = X("v_d", [NTOK, 512], BF16)
        self.qT_d = X("qT_d", [4, P, NTOK], BF16)
        self.kT_d = X("kT_d", [4, P, NTOK], BF16)
        self.zT_d = X("zT_d", [8, P, NTOK], BF16)

    def dbg(self, name, shape, dtp=F32):
        t = self.nc.dram_tensor("dbg_" + name, list(shape), dtp, kind="ExternalOutput").ap()
        self.dbg_out[name] = t
        return t

    def consts(self):
        S = self.S
        self.idb = S.sb("idb", [P, P], BF16)
        S.dma("pool", self.idb[:], self.ident, writes=["idb"])
        self.ones_bf = S.sb("ones_bf", [P, P], BF16)
        S.op("dve", lambda e: e.memset(self.ones_bf[:], 1.0), writes=["ones_bf"])
        self.eps_t = S.sb("eps_t", [P, 1], F32)
        S.op("dve", lambda e: e.memset(self.eps_t[:], EPS), writes=["eps_t"])

    def phase_mod(self):
        S = self.S
        with S.scope():
            cv = S.sb("cv", [P, 8, 2], F32)
            cvb = S.sb("cvb", [P, 8, 2], BF16)
            S.dma("sp", cv[:], self.cvec, writes=["cv"])
            S.op("act", lambda e: e.activation(out=cvb[:], in_=cv[:], func=AF.Silu), reads=["cv"], writes=["cvb"])
            aw = S.sb("aw", [P, 8, 6 * D], BF16)
            ab = S.sb("ab", [2, 6 * D], F32)
            mrow = S.sb("mrow", [2, 6 * D], F32)
            pss = [S.ps(f"pm{i}", [2, 512], F32) for i in range(4)]
            for l in range(2):
                for c in range(8):
                    S.dma("pool", aw[:, c, :], self.ada_w[l, c * P:(c + 1) * P, :], writes=[("aw", c)])
                S.dma("sp", ab[:], self.ada_b[l:l + 1, :].broadcast_to([2, 6 * D]), writes=["ab"])
                for n in range(12):
                    ps = pss[n % 4]
                    k = ("pm", n % 4)
                    for c in range(8):
                        mm(S, ps[:], cvb[:, c, :], aw[:, c, n * 512:(n + 1) * 512], c == 0, c == 7,
                           reads=["cvb", ("aw", c)], writes=[k])
                    S.op("dve", lambda e: e.tensor_tensor(out=mrow[:, n * 512:(n + 1) * 512], in0=ps[:],
                                                          in1=ab[:, n * 512:(n + 1) * 512], op=ALU.add),
                         reads=[k, "ab"], writes=["mrow"])
                S.dma("sp", self.modd[l], mrow[:], reads=["mrow"], writes=[("modd", l)])

    def load_layer_vecs(self, l):
        S = self.S
        V = {}
        for s in range(2):
            for which in range(2):
                V[("A", which, s)] = (S.sb(f"A{which}_{s}", [P, 8], F32), f"A{which}_{s}")
                V[("B", which, s)] = (S.sb(f"B{which}_{s}", [P, 8], F32), f"B{which}_{s}")
                if not (l == 1 and s == 1):
                    V[("G", which, s)] = (S.sb(f"GG{which}_{s}", [P, D], F32), f"GG{which}_{s}")
        with S.scope(), self.nc.allow_non_contiguous_dma(reason="tiny per-feature vectors"):
            tmp = S.sb("lv_tmp", [P, 8], F32)
            rowt = S.sb("lv_row", [P, D], F32)
            for s in range(2):
                for which, (ish, isc, ig) in enumerate(((0, 1, 0), (3, 4, 2))):
                    A, ka = V[("A", which, s)]
                    B, kb = V[("B", which, s)]
                    S.dma("sp", B[:], self.modd[l, s, ish * D:(ish + 1) * D].rearrange("(c p) -> p c", p=P),
                          reads=[("modd", l)], writes=[kb])
                    S.dma("sp", A[:], self.modd[l, s, isc * D:(isc + 1) * D].rearrange("(c p) -> p c", p=P),
                          reads=[("modd", l)], writes=[ka])
                    S.dma("sp", tmp[:], self.norm_g[l, ig, :].rearrange("(c p) -> p c", p=P), writes=["lv_tmp"])
                    S.op("dve", lambda e: e.scalar_tensor_tensor(out=A[:], in0=A[:], scalar=1.0, in1=tmp[:],
                                                                 op0=ALU.add, op1=ALU.mult),
                         reads=[ka, "lv_tmp"], writes=[ka])
                for which, (igt, ig) in enumerate(((2, 1), (5, 3))):
                    if l == 1 and s == 1:
                        continue
                    G, kg = V[("G", which, s)]
                    S.dma("sp", G[:], self.modd[l, s:s + 1, igt * D:(igt + 1) * D].broadcast_to([P, D]),
                          reads=[("modd", l)], writes=[kg])
                    S.dma("sp", rowt[:], self.norm_g[l, ig:ig + 1, :].broadcast_to([P, D]), writes=["lv_row"])
                    S.op("dve", lambda e: e.tensor_tensor(out=G[:], in0=G[:], in1=rowt[:], op=ALU.mult),
                         reads=[kg, "lv_row"], writes=[kg])
        return V

    def make_norm_bufs(self, tag, nb=2):
        S = self.S
        B = {"i": 0, "nb": nb, "tag": tag}
        B["sq"] = [S.sb(f"{tag}_sq{i}", [P, D], BF16) for i in range(1)] * nb
        B["st"] = [S.sb(f"{tag}_st{i}", [P, 4], F32) for i in range(nb)]
        B["xn"] = [S.sb(f"{tag}_xn{i}", [P, D], BF16) for i in range(nb)]
        B["tp"] = [S.ps(f"{tag}_tp{i}", [P, 8, P], BF16) for i in range(nb)]
        B["tm"] = [S.sb(f"{tag}_tm{i}", [P, 8, P], F32) for i in range(nb)]
        return B

    def norm_to_hT(self, B, x_sb, xkey, A, B_, out_ap, out_key, out2_ap=None, out2_key=None):
        S = self.S
        i = B["i"] % B["nb"]
        B["i"] += 1
        tag = B["tag"]
        sq, st, xn, tp, tm = B["sq"][i], B["st"][i], B["xn"][i], B["tp"][i], B["tm"][i]
        ksq, kst, kxn, ktp, ktm = [(tag, n, i) for n in ("sq", "st", "xn", "tp", "tm")]
        ksq = (tag, "sq", 0)
        S.op("pool", lambda e: e.memset(st[:], 0.0), writes=[kst])
        S.op("act", lambda e: e.activation(out=sq[:], in_=x_sb, func=AF.Square, accum_out=st[:, 0:1]),
             reads=[xkey], writes=[ksq, kst])
        S.op("act", lambda e: e.activation(out=st[:, 1:2], in_=st[:, 0:1], func=AF.Sqrt, scale=1.0 / D,
                                           bias=self.eps_t[:, 0:1]), reads=[kst, "eps_t"], writes=[kst])
        S.op("dve", lambda e: e.reciprocal(out=st[:, 2:3], in_=st[:, 1:2]), reads=[kst], writes=[kst])
        S.op("dve", lambda e: e.tensor_scalar(out=xn[:], in0=x_sb, scalar1=st[:, 2:3], scalar2=None, op0=ALU.mult),
             reads=[xkey, kst], writes=[kxn])
        for c in range(8):
            S.op("pe", lambda e: e.transpose(out=tp[:, c, :], in_=xn[:, c * P:(c + 1) * P], identity=self.idb[:]),
                 reads=[kxn, "idb"], writes=[ktp], accum=(c > 0))
        Aap = A[0][:, :, None].broadcast_to([P, 8, P])
        Bap = B_[0][:, :, None].broadcast_to([P, 8, P])
        S.op("dve", lambda e: e.tensor_tensor(out=tm[:], in0=tp[:], in1=Aap, op=ALU.mult),
             reads=[ktp, A[1]], writes=[ktm])
        S.op("pool", lambda e: e.tensor_tensor(out=out_ap, in0=tm[:], in1=Bap, op=ALU.add),
             reads=[ktm, B_[1]], writes=[out_key])
        if out2_ap is not None:
            S.op("pool", lambda e: e.tensor_tensor(out=out2_ap, in0=tm[:], in1=Bap, op=ALU.add),
                 reads=[ktm, B_[1]], writes=[out2_key])

    def x_src(self, layer, T):
        if layer == 0:
            if T < 2:
                return self.ctx_in[T * P:(T + 1) * P, :], None
            return self.x_in[(T - 2) * P:(T - 1) * P, :], None
        return self.x_d[T * P:(T + 1) * P, :], ("x_d", T)

    def phase_L0_proj(self, V):
        S = self.S
        with S.scope():
            w = S.sb("w_in", [P, 8, 2 * D], BF16)
            for c in range(8):
                S.dma("pool", w[:, c, :], self.ev_w_in[c * P:(c + 1) * P, :], writes=[("w_in", c)])
            wk = [("w_in", c) for c in range(8)]
            NB = self.make_norm_bufs("n0")
            xt = [S.sb(f"xt{i}", [P, D], F32) for i in range(2)]
            hT = [S.sb(f"hT{i}", [P, 8, 512], BF16) for i in range(2)]
            ptok = [S.ps(f"ptok{i}", [P, 512], F32) for i in range(2)]
            pft = [S.ps(f"pft{i}", [P, 512], F32) for i in range(2)]
            otok = [S.sb(f"otok{i}", [P, 512], BF16) for i in range(2)]
            oft = [S.sb(f"oft{i}", [P, 512], BF16) for i in range(2)]
            supers = [(0, 2)] + [(2 + 4 * i, 4) for i in range(8)]
            cnt = 0
            ctok = 0
            cft = 0
            for si, (T0, nt) in enumerate(supers):
                hb = hT[si % 2]
                hk = ("hT", si % 2)
                s = 1 if T0 < 2 else 0
                for t in range(nt):
                    T = T0 + t
                    xb = xt[cnt % 2]
                    xk = ("xt", cnt % 2)
                    cnt += 1
                    src, sk = self.x_src(0, T)
                    S.dma("sp", xb[:], src, reads=[sk] if sk else [], writes=[xk])
                    self.norm_to_hT(NB, xb[:], xk, V[("A", 0, s)], V[("B", 0, s)],
                                    hb[:, :, t * P:(t + 1) * P], (hk, t))
                n = nt * P
                hks = [(hk, t) for t in range(nt)]
                for t in range(nt):
                    T = T0 + t
                    for (c0, dst, dk) in ((0, self.upool_d, "upool"), (1536, self.v_d, "v")):
                        ps = ptok[ctok % 2]; pk = ("ptok", ctok % 2)
                        ob = otok[ctok % 2]; ok = ("otok", ctok % 2)
                        ctok += 1
                        for c in range(8):
                            mm(S, ps[:], hb[:, c, t * P:(t + 1) * P], w[:, c, c0:c0 + 512], c == 0, c == 7,
                               reads=[(hk, t), wk[c]], writes=[pk])
                        S.op("act", lambda e: e.activation(out=ob[:], in_=ps[:], func=AF.Copy), reads=[pk], writes=[ok])
                        S.dma("sp", dst[T * P:(T + 1) * P, :], ob[:], reads=[ok], writes=[(dk, T)])
                for jb in range(8):
                    c0 = 512 + jb * P
                    ps = pft[cft % 2]; pk = ("pft", cft % 2)
                    ob = oft[cft % 2]; ok = ("oft", cft % 2)
                    cft += 1
                    for c in range(8):
                        mm(S, ps[:, :n], w[:, c, c0:c0 + P], hb[:, c, :n], c == 0, c == 7,
                           reads=hks + [wk[c]], writes=[pk])
                    sc = 0.125 if jb < 4 else 1.0
                    S.op("act", lambda e: e.activation(out=ob[:, :n], in_=ps[:, :n], func=AF.Copy, scale=sc),
                         reads=[pk], writes=[ok])
                    dst = self.qT_d if jb < 4 else self.kT_d
                    dk = "qT" if jb < 4 else "kT"
                    S.dma("sp", dst[jb % 4, :, T0 * P:T0 * P + n], ob[:, :n], reads=[ok],
                          writes=[(dk, jb % 4, T0 + t) for t in range(nt)])

    def phase_L0_pool(self):
        S = self.S
        with S.scope():
            up = S.sb("up_all", [P, NT, 512], BF16)
            for q in range(0, NT, 2):
                S.dma("sp", up[:, q:q + 2, :], self.upool_d[q * P:(q + 2) * P, :].rearrange("(n p) f -> p n f", p=P),
                      reads=[("upool", q), ("upool", q + 1)], writes=[("up", q), ("up", q + 1)])
            bd = S.sb("bands", [P, 20, P], BF16)
            S.dma("pool", bd[:], self.bands, writes=["bands"])
            pw = S.sb("pool_w", [P, 4, P], BF16)
            S.dma("pool", pw[:], self.ev_pool_w.rearrange("g c o -> c g o"), writes=["pool_w"])
            psc = S.sb("pool_sc", [P, 4], F32)
            with self.nc.allow_non_contiguous_dma(reason="tiny"):
                S.dma("sp", psc[:], self.ev_pool_scale.rearrange("(g p) -> p g", p=P), writes=["pool_sc"])
            pb = [S.ps(f"pb{i}", [P, 4, P], F32) for i in range(2)]
            pc = [S.ps(f"pc{i}", [P, 4, P], F32) for i in range(2)]
            pm = [S.sb(f"pmx{i}", [P, 4, P], BF16) for i in range(2)]
            zp = [S.sb(f"zp{i}", [P, 4, P], BF16) for i in range(2)]
            it = 0
            for (T0, n) in ((0, 2), (2, 32)):
                for i in range(n):
                    T = T0 + i
                    b = it % 2
                    it += 1
                    for g in range(4):
                        srcs = []
                        if i > 0:
                            srcs.append((T - 1, 0))
                        cv = 3 if i == 0 else (4 if i == n - 1 else 2)
                        srcs.append((T, cv))
                        if i < n - 1:
                            srcs.append((T + 1, 1))
                        for si, (Ts, v) in enumerate(srcs):
                            mm(S, pb[b][:, g, :], up[:, Ts, g * P:(g + 1) * P], bd[:, g * 5 + v, :],
                               si == 0, si == len(srcs) - 1, reads=[("up", Ts), "bands"], writes=[("pb", b)])
                    S.op("dve", lambda e: e.tensor_copy(out=pm[b][:], in_=pb[b][:]), reads=[("pb", b)], writes=[("pmx", b)])
                    for g in range(4):
                        mm(S, pc[b][:, g, :], pw[:, g, :], pm[b][:, g, :], True, True,
                           reads=["pool_w", ("pmx", b)], writes=[("pc", b)])
                    S.op("dve", lambda e: e.tensor_tensor(out=zp[b][:], in0=pc[b][:],
                                                          in1=psc[:, :, None].broadcast_to([P, 4, P]), op=ALU.mult),
                         reads=[("pc", b), "pool_sc"], writes=[("zp", b)])
                    S.dma("sp", self.zT_d[0:4, :, T * P:(T + 1) * P].rearrange("c p t -> p c t"), zp[b][:],
                          reads=[("zp", b)], writes=[("zT", c, T) for c in range(4)])

    def phase_L0_attn(self):
        S = self.S
        with S.scope():
            kT = S.sb("kT_all", [P, 4, NTOK], BF16)
            qT = S.sb("qT_all", [P, 4, NTOK], BF16)
            va = S.sb("v_all", [P, NT, 512], BF16)
            for j in range(4):
                S.dma("sp", kT[:, j, :], self.kT_d[j], reads=[("kT", j, T) for T in range(NT)], writes=[("kTa", j)])
                S.dma("sp", qT[:, j, :], self.qT_d[j], reads=[("qT", j, T) for T in range(NT)], writes=[("qTa", j)])
            for q in range(0, NT, 2):
                S.dma("sp", va[:, q:q + 2, :], self.v_d[q * P:(q + 2) * P, :].rearrange("(n p) f -> p n f", p=P),
                      reads=[("v", q), ("v", q + 1)], writes=[("va", q), ("va", q + 1)])
            E = S.sb("Etab", [P, 96, P], BF16)
            with S.scope():
                rt = S.sb("rt", [P, 96, P], F32)
                mk = S.sb("mk", [P, 12, P], F32)
                S.dma("sp", rt[:], self.rpb_tab, writes=["rt"])
                S.dma("sp", mk[:], self.msk_tab, writes=["mk"])
                S.op("act", lambda e: e.activation(out=rt[:], in_=rt[:], func=AF.Exp), reads=["rt"], writes=["rt"])
                for h in range(8):
                    S.op("dve", lambda e: e.tensor_tensor(out=E[:, h * 12:(h + 1) * 12, :], in0=rt[:, h * 12:(h + 1) * 12, :],
                                                          in1=mk[:], op=ALU.mult), reads=["rt", "mk"], writes=["Etab"])
            pss = [[S.ps(f"pss{i}_{k}", [P, 512], F32) for k in range(2)] for i in range(2)]
            pso = [S.ps(f"pso{i}", [P, 2, P], F32) for i in range(2)]
            pex = [S.sb(f"pex{i}", [P, 7, P], BF16) for i in range(2)]
            pT = [S.sb(f"pT{i}", [P, 5, P], BF16) for i in range(2)]
            rc = [S.sb(f"rc{i}", [P, P], F32) for i in range(2)]
            zo = [S.sb(f"zo{i}", [P, P], BF16) for i in range(2)]
            it = 0
            izo = 0
            for T in range(NT):
                if T < 2:
                    chunks = [(0, None), (1, None)]
                else:
                    i = T - 2
                    if 2 <= i <= 29:
                        lat = [(T + d, v) for v, d in enumerate((-2, -1, 0, 1, 2))]
                    elif i == 0:
                        lat = [(T + d, 8 + d) for d in (0, 1, 2, 3)]
                    elif i == 1:
                        lat = [(T + d, 8 + d) for d in (-1, 0, 1, 2)]
                    elif i == 30:
                        lat = [(T + d, 8 + d) for d in (-2, -1, 0, 1)]
                    else:
                        lat = [(T + d, 8 + d) for d in (-3, -2, -1, 0)]
                    chunks = [(0, None), (1, None)] + lat
                nk = len(chunks)
                nlat = nk - 2
                for j in range(4):
                    zb = zo[izo % 2]; zk = ("zo", izo % 2)
                    izo += 1
                    for hh in range(2):
                        h = 2 * j + hh
                        pb_ = hh * 64
                        b = it % 2
                        it += 1
                        for ci, (Tk, v) in enumerate(chunks):
                            bank = pss[b][ci // 4]
                            mm(S, bank[:, (ci % 4) * P:(ci % 4 + 1) * P],
                               kT[pb_:pb_ + 64, j, Tk * P:(Tk + 1) * P], qT[pb_:pb_ + 64, j, T * P:(T + 1) * P],
                               True, True, reads=[("kTa", j), ("qTa", j)], writes=[("pss", b, ci // 4)])
                        n0 = min(nk, 4)
                        S.op("act", lambda e: e.activation(out=pex[b][:, 0:n0, :], in_=pss[b][0][:, 0:n0 * P].rearrange("p (c q) -> p c q", q=P), func=AF.Exp),
                             reads=[("pss", b, 0)], writes=[("pex", b)])
                        if nk > 4:
                            S.op("act", lambda e: e.activation(out=pex[b][:, 4:nk, :], in_=pss[b][1][:, 0:(nk - 4) * P].rearrange("p (c q) -> p c q", q=P), func=AF.Exp),
                                 reads=[("pss", b, 1)], writes=[("pex", b)])
                        if nlat > 0:
                            v0 = chunks[2][1]
                            S.op("dve", lambda e: e.tensor_tensor(out=pT[b][:, 0:nlat, :], in0=pex[b][:, 2:nk, :],
                                                                  in1=E[:, h * 12 + v0:h * 12 + v0 + nlat, :], op=ALU.mult),
                                 reads=[("pex", b), "Etab"], writes=[("pT", b)])
                        for ci, (Tk, v) in enumerate(chunks):
                            rhs = pex[b][:, ci, :] if v is None else pT[b][:, ci - 2, :]
                            rk = [("pex", b)] if v is None else [("pT", b)]
                            mm(S, pso[b][:, 0, :], va[:, Tk, j * P:(j + 1) * P], rhs, ci == 0, ci == nk - 1,
                               reads=[("va", Tk)] + rk, writes=[("pso", b)])
                        for ci, (Tk, v) in enumerate(chunks):
                            rhs = pex[b][:, ci, :] if v is None else pT[b][:, ci - 2, :]
                            rk = [("pex", b)] if v is None else [("pT", b)]
                            mm(S, pso[b][:, 1, :], self.ones_bf[:], rhs, ci == 0, ci == nk - 1,
                               reads=["ones_bf"] + rk, writes=[("pso", b)])
                        S.op("dve", lambda e: e.reciprocal(out=rc[b][pb_:pb_ + 64, :], in_=pso[b][pb_:pb_ + 64, 1, :]),
                             reads=[("pso", b)], writes=[("rc", b)])
                        S.op("dve", lambda e: e.tensor_tensor(out=zb[pb_:pb_ + 64, :], in0=pso[b][pb_:pb_ + 64, 0, :],
                                                              in1=rc[b][pb_:pb_ + 64, :], op=ALU.mult),
                             reads=[("pso", b), ("rc", b)], writes=[zk])
                    S.dma("sp", self.zT_d[4 + j, :, T * P:(T + 1) * P], zb[:], reads=[zk], writes=[("zT", 4 + j, T)])

    def phase_out_mlp(self, layer, V, w_out_ap, y_tile_fn, dst_fn):
        S = self.S
        with S.scope():
            wo = S.sb("wo", [P, 8, D], BF16)
            for c in range(8):
                S.dma("pool", wo[:, c, :], w_out_ap[c * P:(c + 1) * P, :], writes=[("wo", c)])
            w1 = S.sb("w1", [P, 8, 4 * D], BF16)
            w2 = S.sb("w2", [P, 32, D], BF16)
            for c in range(8):
                S.dma("pool", w1[:, c, :], self.mlp_w1[layer, c * P:(c + 1) * P, :], writes=[("w1", c)])
            for f in range(0, 32, 4):
                S.dma("pool", w2[:, f:f + 4, :], self.mlp_w2[layer, f * P:(f + 4) * P, :].rearrange("(n p) d -> p n d", p=P),
                      writes=[("w2", f + q) for q in range(4)])
            NB = self.make_norm_bufs("nm", nb=1)
            zt = [S.sb(f"zt{i}", [P, 8, P], BF16) for i in range(2)]
            xt = [S.sb(f"xo{i}", [P, D], F32) for i in range(2)]
            x1 = [S.sb(f"x1_{i}", [P, D], F32) for i in range(2)]
            tmp = [S.sb(f"tg{i}", [P, D], F32) for i in range(2)]
            sq = NB["sq"][0]
            stt = [S.sb(f"ost{i}", [P, 4], F32) for i in range(4)]
            hT = [S.sb(f"hm{i}", [P, 8, 256], BF16) for i in range(1)] * 2
            py = [[S.ps(f"py{t}_{hf}", [P, 512], F32) for hf in range(2)] for t in range(2)]
            pa = [S.ps(f"pa{i}", [P, 256], F32) for i in range(2)]
            r32 = [S.sb(f"r32_{i}", [P, 256], F32) for i in range(2)]
            aT = [S.sb(f"aT{i}", [P, 256], BF16) for i in range(2)]
            ist = 0

            def norm_gate_res(t, G, xin, xin_key, xout, xout_key):
                nonlocal ist
                st = stt[ist % 4]; sk = ("ost", ist % 4)
                ist += 1
                S.op("pool", lambda e: e.memset(st[:], 0.0), writes=[sk])
                for hf in range(2):
                    S.op("act", lambda e: e.activation(out=sq[:, hf * 512:(hf + 1) * 512], in_=py[t][hf][:], func=AF.Square,
                                                       accum_out=st[:, hf:hf + 1]), reads=[("py", t, hf)], writes=[("nm", "sq", 0), sk])
                S.op("dve", lambda e: e.tensor_tensor(out=st[:, 2:3], in0=st[:, 0:1], in1=st[:, 1:2], op=ALU.add),
                     reads=[sk], writes=[sk])
                S.op("act", lambda e: e.activation(out=st[:, 2:3], in_=st[:, 2:3], func=AF.Sqrt, scale=1.0 / D,
                                                   bias=self.eps_t[:, 0:1]), reads=[sk, "eps_t"], writes=[sk])
                S.op("dve", lambda e: e.reciprocal(out=st[:, 3:4], in_=st[:, 2:3]), reads=[sk], writes=[sk])
                tb = tmp[t]; tk = ("tg", t)
                for hf in range(2):
                    S.op("dve", lambda e: e.scalar_tensor_tensor(out=tb[:, hf * 512:(hf + 1) * 512], in0=py[t][hf][:],
                                                                 scalar=st[:, 3:4], in1=G[0][:, hf * 512:(hf + 1) * 512],
                                                                 op0=ALU.mult, op1=ALU.mult),
                         reads=[("py", t, hf), sk, G[1]], writes=[tk])
                S.op("pool", lambda e: e.tensor_tensor(out=xout, in0=tb[:], in1=xin, op=ALU.add),
                     reads=[tk, xin_key], writes=[xout_key])

            ia = 0
            for sidx in range(NT // 2):
                T0 = 2 * sidx
                s = 1 if T0 < 2 else 0
                if layer == 1 and s == 1:
                    continue
                hb = hT[0]; hk = ("hm", 0)
                for t in range(2):
                    T = T0 + t
                    src, skey = self.x_src(layer, T)
                    S.dma("sp", xt[t][:], src, reads=[skey] if skey else [], writes=[("xo", t)])
                    y_tile_fn(T, t, zt[t], ("zt", t), wo, py[t])
                    norm_gate_res(t, V[("G", 0, s)], xt[t][:], ("xo", t), x1[t][:], ("x1", t))
                    self.norm_to_hT(NB, x1[t][:], ("x1", t), V[("A", 1, s)], V[("B", 1, s)],
                                    hb[:, :, t * P:(t + 1) * P], (hk, t))
                def mm1(f):
                    a = (ia + f) % 2
                    for c in range(8):
                        mm(S, pa[a][:], w1[:, c, f * P:(f + 1) * P], hb[:, c, :], c == 0, c == 7,
                           reads=[("w1", c), (hk, 0), (hk, 1)], writes=[("pa", a)])
                    S.op("act", lambda e: e.activation(out=r32[a][:], in_=pa[a][:], func=AF.Relu),
                         reads=[("pa", a)], writes=[("r32", a)])
                    S.op("dve", lambda e: e.tensor_tensor(out=aT[a][:], in0=r32[a][:], in1=r32[a][:], op=ALU.mult),
                         reads=[("r32", a)], writes=[("aT", a)])

                def mm2(f):
                    a = (ia + f) % 2
                    for t in range(2):
                        for hf in range(2):
                            mm(S, py[t][hf][:], aT[a][:, t * P:(t + 1) * P], w2[:, f, hf * 512:(hf + 1) * 512],
                               f == 0, f == 31, reads=[("aT", a), ("w2", f)], writes=[("py", t, hf)])
                mm1(0)
                for f in range(32):
                    if f + 1 < 32:
                        mm1(f + 1)
                    mm2(f)
                for t in range(2):
                    T = T0 + t
                    norm_gate_res(t, V[("G", 1, s)], x1[t][:], ("x1", t), tmp[t][:], ("tg", t))
                    dst, dkey = dst_fn(T)
                    S.dma("sp", dst, tmp[t][:], reads=[("tg", t)], writes=[dkey])

    def y_tile_L0(self, T, t, zt, zk, wo, py):
        S = self.S
        S.dma("sp", zt[:], self.zT_d[:, :, T * P:(T + 1) * P].rearrange("c p t -> p c t"),
              reads=[("zT", c, T) for c in range(8)], writes=[zk])
        for hf in range(2):
            for c in range(8):
                mm(S, py[hf][:], zt[:, c, :], wo[:, c, hf * 512:(hf + 1) * 512], c == 0, c == 7,
                   reads=[zk, ("wo", c)], writes=[("py", t, hf)])


def build_program(stop_after=None, debug=()):
    nc = bass.Bass("TRN2", target_bir_lowering=False)
    Pg = Prog(nc, debug)
    S = Pg.S
    Pg.consts()
    Pg.phase_mod()
    final_keys = []
    with S.scope():
        V0 = Pg.load_layer_vecs(0)
        Pg.phase_L0_proj(V0)
        Pg.phase_L0_pool()
        Pg.phase_L0_attn()

        def dst0(T):
            if stop_after == "L0":
                if T < 2:
                    return Pg.x_d[T * P:(T + 1) * P, :], ("x_d", T)
                return Pg.out[(T - 2) * P:(T - 1) * P, :], ("out", T)
            return Pg.x_d[T * P:(T + 1) * P, :], ("x_d", T)
        Pg.phase_out_mlp(0, V0, Pg.ev_w_out, Pg.y_tile_L0, dst0)
    S.barrier()
    S.finish([])
    S.close()
    return nc, Pg


def host_inputs(inputs):
    f = lambda a: np.ascontiguousarray(np.asarray(a, dtype=np.float32))
    dr_idx, dc_full, mask = _attn_tables()
    rpb = f(inputs["ev_rpb"])[0]
    tab = rpb[:, dr_idx, dc_full[None, :, :]]
    tab = np.ascontiguousarray(tab.transpose(2, 0, 1, 3).reshape(128, 96, 128))
    msk = np.ascontiguousarray(mask.transpose(1, 0, 2))
    bands = np.ascontiguousarray(_pool_bands().transpose(2, 0, 1, 3).reshape(128, 20, 128))
    shared = {
        "ada_w": f(inputs["ada_w"]), "ada_b": f(inputs["ada_b"]), "norm_g": f(inputs["norm_g"]),
        "mlp_w1": f(inputs["mlp_w1"]), "mlp_w2": f(inputs["mlp_w2"]),
        "ev_w_in": f(inputs["ev_w_in"])[0], "ev_w_out": f(inputs["ev_w_out"])[0],
        "ev_pool_w": f(inputs["ev_pool_w"])[0], "ev_pool_scale": f(inputs["ev_pool_scale"])[0],
        "rpb_tab": tab, "msk_tab": msk, "bands": bands, "ident": np.eye(128, dtype=np.float32),
    }
    x = f(inputs["x"]); c = f(inputs["c"]); ctx = f(inputs["ctx"]); cc = f(inputs["c_ctx"])
    maps = []
    for b in range(x.shape[0]):
        cv = np.stack([c[b].reshape(8, 128).T, cc.reshape(8, 128).T], axis=-1)
        m = dict(shared)
        m.update({"x": x[b], "ctx": ctx[b], "cvec": np.ascontiguousarray(cv)})
        maps.append(m)
    return maps


_CACHE = {}


def kernel(**inputs):
    maps = host_inputs(inputs)
    if "nc" not in _CACHE:
        _CACHE["nc"] = build_program()
    nc, Pg = _CACHE["nc"]
    res = run_bass_kernel_spmd(nc, maps, core_ids=list(range(8)))
    return np.stack([np.asarray(r["out"]) for r in res.results], axis=0)

LWC = -0.6065306597126334
GN_EPS = 64e-5


def _scan_consts():
    s = np.arange(128)[:, None]
    t = np.arange(128)[None, :]
    tri = np.stack([(s <= t), (s >= t)]).astype(np.float32)
    strict = np.stack([(s < t), (s > t)]).astype(np.float32)
    mT = strict.transpose(0, 2, 1)
    m4 = np.concatenate([tri, strict, tri, mT], axis=2)
    lm = []
    for l in range(7):
        b = 1 << l
        lm.append(((s // (2 * b)) == (t // (2 * b))) & (((s // b) % 2) == 0) & (((t // b) % 2) == 1))
    lm = np.stack(lm).astype(np.float32)
    lmN = np.stack([lm, lm.transpose(0, 2, 1)]) + np.eye(128, dtype=np.float32)[None, None]
    return tri, m4, np.ascontiguousarray(lmN)


def _tt(S, eng, out, a, b, op, reads, writes):
    return S.op(eng, lambda e: e.tensor_tensor(out=out, in0=a, in1=b, op=op), reads=reads, writes=writes)


def _stt(S, eng, out, a, sc, b, op0, op1, reads, writes):
    return S.op("dve", lambda e: e.scalar_tensor_tensor(out=out, in0=a, scalar=sc, in1=b, op0=op0, op1=op1),
                reads=reads, writes=writes)


def _act(S, out, in_, func, reads, writes, **kw):
    return S.op("act", lambda e: e.activation(out=out, in_=in_, func=func, **kw), reads=reads, writes=writes)


def _h3(ap):
    return ap.rearrange("p (h k) -> p h k", k=64)


class Prog1(Prog):
    def __init__(self, nc, debug=()):
        super().__init__(nc, debug)
        dt = nc.dram_tensor
        I = lambda name, shape: dt(name, list(shape), F32, kind="ExternalInput").ap()
        self.rw_mu = I("rw_mu", [6, D])
        self.rw_wr = I("rw_wr", [D, D]); self.rw_wk = I("rw_wk", [D, D])
        self.rw_wv = I("rw_wv", [D, D]); self.rw_wo = I("rw_wo", [D, D])
        self.rw_w0 = I("rw_w0", [2, D]); self.rw_a0 = I("rw_a0", [2, D])
        self.w1cat = I("w1cat", [D, P]); self.a1cat = I("a1cat", [D, P]); self.rw_g1 = I("rw_g1", [D, P])
        self.w2cat = I("w2cat", [P, D]); self.a2cat = I("a2cat", [P, D]); self.rw_g2 = I("rw_g2", [P, D])
        self.rw_kk = I("rw_kk", [1, D]); self.rw_ka = I("rw_ka", [1, D]); self.rw_rk = I("rw_rk", [1, D])
        self.rw_lng = I("rw_lng", [1, D]); self.rw_lnb = I("rw_lnb", [1, D])
        self.tri_c = I("tri_c", [2, P, P]); self.m4_c = I("m4_c", [2, P, 512]); self.lmT_c = I("lmT_c", [2, 7, P, P])
        X = lambda name, shape, d=F32: (dt(name, list(shape), d, kind="ExternalOutput").ap() if name in debug
                                        else dt(name, list(shape), d).ap())
        self.hT_d = X("hT_d", [8, P, NTOK])
        self.featT_d = X("featT_d", [2, NT, P, 8 * 4 * P], BF16)
        self.vtok_d = X("vtok_d", [NTOK, D], BF16)
        self.bk_d = X("bk_d", [2, NT, P, 2 * D], BF16)
        self.gC_d = X("gC_d", [2, NT, P, 8])
        self.g_d = X("g_d", [NTOK, D])
        self.bonus_d = X("bonus_d", [NTOK, D])
        self.y_d = X("y_d", [2, NTOK, D])

    def phase_R0(self, V):
        S = self.S
        with S.scope():
            NB = self.make_norm_bufs("r0")
            xt = [S.sb(f"r0x{i}", [P, D], F32) for i in range(2)]
            ho = [S.sb(f"r0h{i}", [P, 8, P], F32) for i in range(2)]
            for T in range(NT):
                s = 1 if T < 2 else 0
                b = T % 2
                src, sk = self.x_src(1, T)
                S.dma("sp", xt[b][:], src, reads=[sk], writes=[("r0x", b)])
                self.norm_to_hT(NB, xt[b][:], ("r0x", b), V[("A", 0, s)], V[("B", 0, s)], ho[b][:], ("r0h", b))
                S.dma("sp", self.hT_d[:, :, T * P:(T + 1) * P].rearrange("c p t -> p c t"), ho[b][:],
                      reads=[("r0h", b)], writes=[("hT_d", T)])

    def phase_R1(self):
        S = self.S
        with S.scope():
            W = {}
            for nm, src in (("wr", self.rw_wr), ("wk", self.rw_wk), ("wv", self.rw_wv)):
                W[nm] = S.sb(nm, [P, 8, D], BF16)
                for c in range(0, 8, 4):
                    S.dma("pool", W[nm][:, c:c + 4, :], src[c * P:(c + 4) * P, :].rearrange("(c p) n -> p c n", p=P), writes=[nm])
            for nm, src in (("w1c", self.w1cat), ("a1c", self.a1cat), ("g1", self.rw_g1)):
                W[nm] = S.sb(nm, [P, 8, P], BF16)
                S.dma("pool", W[nm][:], src.rearrange("(c p) n -> p c n", p=P), writes=[nm])
            for nm, src in (("w2c", self.w2cat), ("a2c", self.a2cat), ("g2", self.rw_g2)):
                W[nm] = S.sb(nm, [P, D], BF16)
                S.dma("pool", W[nm][:], src, writes=[nm])
            R = {}
            for nm, src in (("kk_r", self.rw_kk), ("ka_r", self.rw_ka), ("rk_r", self.rw_rk),
                            ("w0_0", self.rw_w0[0:1, :]), ("w0_1", self.rw_w0[1:2, :]),
                            ("a0_0", self.rw_a0[0:1, :]), ("a0_1", self.rw_a0[1:2, :])):
                R[nm] = S.sb(nm, [P, D], F32)
                S.dma("sp", R[nm][:], src.broadcast_to([P, D]), writes=[nm])
            mu = S.sb("mu", [P, 6, 8], F32)
            with self.nc.allow_non_contiguous_dma(reason="tiny"):
                S.dma("sp", mu[:], self.rw_mu.rearrange("j (c p) -> p j c", p=P), writes=["mu"])
            tri = S.sb("tri", [P, 2, P], F32)
            S.dma("sp", tri[:], self.tri_c.rearrange("d s t -> s d t"), writes=["tri"])
            onef = S.sb("onef", [P, P], F32)
            S.op("dve", lambda e: e.memset(onef[:], 1.0), writes=["onef"])
            hbuf = S.sb("hbuf", [P, 8, P + 2], F32)
            xx = S.sb("xx", [P, 8, P], F32)
            mxt = S.sb("mxt", [P, 8, P], F32)
            mix = S.sb("mix", [P, 6, 8, P], BF16)
            hid = S.sb("hid", [P, 3, P], BF16)
            F = {n: S.sb(n, [P, D], F32) for n in ("r_sb", "k_sb", "v_sb", "kkn", "tA", "tB", "lw", "tC", "tD", "kd0", "kd1", "tE", "tF", "tG", "tH")}
            ob = [S.sb(f"ob{i}", [P, D], BF16) for i in range(4)]
            vb = S.sb("vb", [P, D], BF16)
            ft = S.sb("ft", [P, 8, 4, P], BF16)
            bkt = S.sb("bkt", [P, 2, D], BF16)
            st16 = S.sb("st16", [P, 64], F32)
            gcs = S.sb("gcs", [P, 8], F32)
            pA = [[S.ps(f"pA{i}_{h}", [P, 512], F32) for h in range(2)] for i in range(2)]
            pCl = [S.ps(f"pCl{h}", [P, 512], F32) for h in range(2)]
            pF = S.ps("pF", [P, 512], F32)
            pT = S.ps("pT", [P, 8, P], BF16)
            ipa = 0

            def proj(lhs_fn, rhs, rkey, K0=0, K=P, nchunks=8, lkeys=()):
                nonlocal ipa
                i = ipa % 2
                ipa += 1
                for hf in range(2):
                    for c in range(nchunks):
                        mm(S, pA[i][hf][:], lhs_fn(c), rhs(c, hf), c == 0, c == nchunks - 1,
                           reads=list(lkeys) + [rkey], writes=[("pA", i, hf)])
                return pA[i], [("pA", i, 0), ("pA", i, 1)]

            def evac2(fn_half):
                for hf in range(2):
                    fn_half(hf, slice(hf * 512, (hf + 1) * 512))

            for T in range(NT):
                seq_lo, seq_hi = (0, NCTX) if T < 2 else (NCTX, NTOK)
                t0 = T * P
                lo = max(t0 - 1, seq_lo); hi = min(t0 + P + 1, seq_hi)
                if lo > t0 - 1:
                    S.op("pool", lambda e: e.memset(hbuf[:, :, 0:1], 0.0), writes=["hbuf"])
                if hi < t0 + P + 1:
                    S.op("pool", lambda e: e.memset(hbuf[:, :, P + 1:P + 2], 0.0), writes=["hbuf"])
                S.dma("sp", hbuf[:, :, lo - (t0 - 1):hi - (t0 - 1)], self.hT_d[:, :, lo:hi].rearrange("c p t -> p c t"),
                      reads=[("hT_d", q) for q in range(max(T - 1, 0), min(T + 2, NT))], writes=["hbuf"])
                _tt(S, "dve", xx[:], hbuf[:, :, 0:P], hbuf[:, :, 2:P + 2], ALU.add, ["hbuf"], ["xx"])
                _stt(S, "dve", xx[:], xx[:], 0.5, hbuf[:, :, 1:P + 1], ALU.mult, ALU.subtract, ["xx", "hbuf"], ["xx"])
                for j in range(6):
                    _tt(S, "dve", mxt[:], xx[:], mu[:, j, :][:, :, None].broadcast_to([P, 8, P]), ALU.mult, ["xx", "mu"], ["mxt"])
                    _tt(S, "dve", mix[:, j, :, :], mxt[:], hbuf[:, :, 1:P + 1], ALU.add, ["mxt", "hbuf"], [("mix", j)])
                for hi_, (wn, mj, fn) in enumerate((("w1c", 1, AF.Tanh), ("a1c", 4, AF.Copy), ("g1", 5, AF.Sigmoid))):
                    for c in range(8):
                        mm(S, pF[:, 0:P], W[wn][:, c, :], mix[:, mj, c, :], c == 0, c == 7,
                           reads=[wn, ("mix", mj)], writes=["pF"])
                    _act(S, hid[:, hi_, :], pF[:, 0:P], fn, ["pF"], [("hid", hi_)])
                for nm, mj, wn in (("r_sb", 0, "wr"), ("k_sb", 2, "wk"), ("v_sb", 3, "wv")):
                    ps, pk = proj(lambda c: mix[:, mj, c, :], lambda c, hf: W[wn][:, c, hf * 512:(hf + 1) * 512], wn,
                                  lkeys=[("mix", mj)])
                    evac2(lambda hf, sl: _act(S, F[nm][:, sl], ps[hf][:], AF.Copy, [pk[hf]], [nm]))
                S.op("pool", lambda e: e.tensor_copy(out=vb[:], in_=F["v_sb"][:]), reads=["v_sb"], writes=["vb"])
                S.dma("sp", self.vtok_d[t0:t0 + P, :], vb[:], reads=["vb"], writes=[("vtok", T)])
                ps, pk = proj(lambda c: hid[:, 2, :], lambda c, hf: W["g2"][:, hf * 512:(hf + 1) * 512], "g2", nchunks=1,
                              lkeys=[("hid", 2)])
                evac2(lambda hf, sl: _act(S, F["tA"][:, sl], ps[hf][:], AF.Copy, [pk[hf]], ["tA"]))
                S.dma("sp", self.g_d[t0:t0 + P, :], F["tA"][:], reads=["tA"], writes=[("g_d", T)])
                _tt(S, "dve", F["tA"][:], F["k_sb"][:], R["kk_r"][:], ALU.mult, ["k_sb", "kk_r"], ["tA"])
                _tt(S, "pool", F["tB"][:], F["tA"][:], F["tA"][:], ALU.mult, ["tA"], ["tB"])
                S.op("dve", lambda e: e.tensor_reduce(out=st16[:, 0:16], in_=_h3(F["tB"][:]), axis=AX.X, op=ALU.add),
                     reads=["tB"], writes=["st16"])
                S.op("dve", lambda e: e.tensor_scalar(out=st16[:, 0:16], in0=st16[:, 0:16], scalar1=1e-24, scalar2=None, op0=ALU.max),
                     reads=["st16"], writes=["st16"])
                _act(S, st16[:, 0:16], st16[:, 0:16], AF.Sqrt, ["st16"], ["st16"])
                S.op("dve", lambda e: e.reciprocal(out=st16[:, 16:32], in_=st16[:, 0:16]), reads=["st16"], writes=["st16"])
                _tt(S, "dve", _h3(F["kkn"][:]), _h3(F["tA"][:]), st16[:, 16:32][:, :, None].broadcast_to([P, 16, 64]), ALU.mult,
                    ["tA", "st16"], ["kkn"])
                for d in range(2):
                    ps, pk = proj(lambda c: hid[d * 64:(d + 1) * 64, 0, :], lambda c, hf: W["w2c"][d * 64:(d + 1) * 64, hf * 512:(hf + 1) * 512],
                                  "w2c", nchunks=1, lkeys=[("hid", 0)])
                    evac2(lambda hf, sl: _tt(S, "dve", F["tB"][:, sl], ps[hf][:], R[f"w0_{d}"][:, sl], ALU.add, [pk[hf], f"w0_{d}"], ["tB"]))
                    _act(S, F["tB"][:], F["tB"][:], AF.Sigmoid, ["tB"], ["tB"])
                    _act(S, F["lw"][:], F["tB"][:], AF.Copy, ["tB"], ["lw"], scale=LWC)
                    ps, pk = proj(lambda c: hid[d * 64:(d + 1) * 64, 1, :], lambda c, hf: W["a2c"][d * 64:(d + 1) * 64, hf * 512:(hf + 1) * 512],
                                  "a2c", nchunks=1, lkeys=[("hid", 1)])
                    evac2(lambda hf, sl: _tt(S, "dve", F["tC"][:, sl], ps[hf][:], R[f"a0_{d}"][:, sl], ALU.add, [pk[hf], f"a0_{d}"], ["tC"]))
                    _act(S, F["tC"][:], F["tC"][:], AF.Sigmoid, ["tC"], ["tC"])
                    kd = F[f"kd{d}"]; kdk = f"kd{d}"
                    _stt(S, "dve", F["tD"][:], F["tC"][:], -1.0, R["ka_r"][:], ALU.add, ALU.mult, ["tC", "ka_r"], ["tD"])
                    _stt(S, "pool", kd[:], F["tD"][:], 1.0, F["k_sb"][:], ALU.add, ALU.mult, ["tD", "k_sb"], [kdk])
                    _tt(S, "pool", F["tC"][:], F["kkn"][:], F["tC"][:], ALU.mult, ["kkn", "tC"], ["tC"])
                    for hf in range(2):
                        mm(S, pCl[hf][:], tri[:, d, :], F["lw"][:, hf * 512:(hf + 1) * 512], True, True,
                           reads=["tri", "lw"], writes=[("pCl", hf)])
                    evac2(lambda hf, sl: _act(S, F["tE"][:, sl], pCl[hf][:], AF.Exp, [("pCl", hf)], ["tE"]))
                    evac2(lambda hf, sl: _act(S, F["tF"][:, sl], pCl[hf][:], AF.Exp, [("pCl", hf)], ["tF"], scale=-1.0))
                    for hf in range(2):
                        mm(S, pCl[hf][:], onef[:], F["lw"][:, hf * 512:(hf + 1) * 512], True, True,
                           reads=["onef", "lw"], writes=[("pCl", hf)])
                    evac2(lambda hf, sl: _act(S, F["tH"][:, sl], pCl[hf][:], AF.Exp, [("pCl", hf)], ["tH"]))
                    _act(S, F["tG"][:], F["lw"][:], AF.Exp, ["lw"], ["tG"], scale=-1.0)
                    _tt(S, "dve", F["tG"][:], F["tG"][:], F["tE"][:], ALU.mult, ["tG", "tE"], ["tG"])
                    _tt(S, "pool", F["tH"][:], F["tH"][:], F["tF"][:], ALU.mult, ["tH", "tF"], ["tH"])
                    for j in range(8):
                        mm(S, pF[:, 256 + j:257 + j], F["lw"][:, j * P:(j + 1) * P], onef[:, 0:1], True, True,
                           reads=["lw", "onef"], writes=["pF"])
                    _act(S, gcs[:], pF[:, 256:264], AF.Exp, ["pF"], ["gcs"])
                    S.dma("sp", self.gC_d[d, T], gcs[:], reads=["gcs"], writes=[("gC_d", d, T)])
                    _stt(S, "dve", ob[0][:], F["kkn"][:], -1.0, F["tG"][:], ALU.mult, ALU.mult, ["kkn", "tG"], [("ob", 0)])
                    _tt(S, "pool", ob[1][:], F["r_sb"][:], F["tE"][:], ALU.mult, ["r_sb", "tE"], [("ob", 1)])
                    _tt(S, "dve", ob[2][:], F["tC"][:], F["tF"][:], ALU.mult, ["tC", "tF"], [("ob", 2)])
                    _tt(S, "pool", ob[3][:], kd[:], F["tF"][:], ALU.mult, [kdk, "tF"], [("ob", 3)])
                    _tt(S, "dve", bkt[:, 0, :], F["tC"][:], F["tH"][:], ALU.mult, ["tC", "tH"], ["bkt"])
                    _tt(S, "pool", bkt[:, 1, :], kd[:], F["tH"][:], ALU.mult, [kdk, "tH"], ["bkt"])
                    S.dma("sp", self.bk_d[d, T], bkt[:].rearrange("p a n -> p (a n)"), reads=["bkt"], writes=[("bk_d", d, T)])
                    for q in range(4):
                        for c in range(8):
                            S.op("pe", lambda e: e.transpose(out=pT[:, c, :], in_=ob[q][:, c * P:(c + 1) * P], identity=self.idb[:]),
                                 reads=[("ob", q), "idb"], writes=["pT"], accum=(c > 0))
                        if q % 2 == 0:
                            _act(S, ft[:, :, q, :], pT[:], AF.Copy, ["pT"], ["ft"])
                        else:
                            S.op("dve", lambda e: e.tensor_copy(out=ft[:, :, q, :], in_=pT[:]), reads=["pT"], writes=["ft"])
                    S.dma("sp", self.featT_d[d, T], ft[:].rearrange("p j q t -> p (j q t)"), reads=["ft"], writes=[("featT_d", d, T)])
                _tt(S, "pool", F["tD"][:], F["kd0"][:], F["kd1"][:], ALU.add, ["kd0", "kd1"], ["tD"])
                _tt(S, "pool", F["tD"][:], F["tD"][:], F["r_sb"][:], ALU.mult, ["tD", "r_sb"], ["tD"])
                _tt(S, "pool", F["tD"][:], F["tD"][:], R["rk_r"][:], ALU.mult, ["tD", "rk_r"], ["tD"])
                S.op("dve", lambda e: e.tensor_reduce(out=st16[:, 32:48], in_=_h3(F["tD"][:]), axis=AX.X, op=ALU.add),
                     reads=["tD"], writes=["st16"])
                _tt(S, "dve", _h3(F["tD"][:]), _h3(F["v_sb"][:]), st16[:, 32:48][:, :, None].broadcast_to([P, 16, 64]), ALU.mult,
                    ["v_sb", "st16"], ["tD"])
                S.dma("sp", self.bonus_d[t0:t0 + P, :], F["tD"][:], reads=["tD"], writes=[("bonus_d", T)])

    def phase_R2(self):
        S = self.S
        with S.scope():
            m4 = S.sb("m4", [P, 2, 512], F32)
            lmN = S.sb("lmN", [P, 2, 7, P], F32)
            S.dma("sp", m4[:], self.m4_c.rearrange("d s n -> s d n"), writes=["m4"])
            S.dma("sp", lmN[:], self.lmT_c.rearrange("d l s n -> s d l n"), writes=["lmN"])
            idb = self.idb
            NG = 4
            I4 = S.sb("I4", [P, NG, P], BF16)
            for g in range(NG):
                S.op("pool", lambda e: e.tensor_copy(out=I4[:, g, :], in_=idb[:]), reads=["idb"], writes=["I4"])
            I4f = I4[:].rearrange("p g t -> p (g t)")
            ST32 = [S.sb(f"ST32_{d}", [P, 8, 64], F32) for d in range(2)]
            STb = [S.sb(f"STb_{d}", [P, 8, 64], BF16) for d in range(2)]
            for d in range(2):
                S.op("dve", lambda e: e.memset(ST32[d][:], 0.0), writes=[("ST32", d)])
                S.op("dve", lambda e: e.memset(STb[d][:], 0.0), writes=[("STb", d)])
            NBUF = 3
            Fb = [S.sb(f"Fb{i}", [P, 8, 4, P], BF16) for i in range(NBUF)]
            Vb = [S.sb(f"Vb{i}", [P, D], BF16) for i in range(NBUF)]
            BKb = [S.sb(f"BKb{i}", [P, 2, D], BF16) for i in range(NBUF)]
            gCb = [S.sb(f"gCb{i}", [P, 8], F32) for i in range(NBUF)]
            ysb = [S.sb(f"ysb{i}", [P, D], F32) for i in range(NBUF)]
            SL = []
            for sl in range(2):
                R_ = dict(
                    GM=S.sb(f"GM{sl}", [P, NG, 512], BF16),
                    X=[S.sb(f"X{sl}_{i}", [P, NG, P], BF16) for i in range(2)],
                    XT=[S.sb(f"XT{sl}_{i}", [P, NG, P], BF16) for i in range(2)],
                    T1s=S.sb(f"T1s{sl}", [P, NG, P], BF16),
                    Zq=S.sb(f"Zq{sl}", [P, NG, 64], BF16),
                    Pb=S.sb(f"Pb{sl}", [P, NG, 64], BF16),
                    bk=[S.ps(f"bk{sl}_{i}", [P, NG, P], F32) for i in range(3)],
                    bz=S.ps(f"bz{sl}", [P, 8, 64], F32),
                    sl=sl)
                SL.append(R_)

            items = []
            it = 0
            for d in range(2):
                order = list(range(NT)) if d == 0 else [1, 0] + list(range(NT - 1, 1, -1))
                for ci, T in enumerate(order):
                    for g0 in range(0, 16, NG):
                        items.append(dict(d=d, T=T, g0=g0, b=it % NBUF))
                    it += 1

            def heads_of(g0):
                return [(g, g0 + g, (g0 + g) // 2, ((g0 + g) % 2) * 64) for g in range(NG)]

            def load_chunk(w):
                d, T, b = w["d"], w["T"], w["b"]
                S.dma("sp", Fb[b][:].rearrange("p j q t -> p (j q t)"), self.featT_d[d, T], reads=[("featT_d", d, T)], writes=[("Fb", b)])
                S.dma("sp", Vb[b][:], self.vtok_d[T * P:(T + 1) * P, :], reads=[("vtok", T)], writes=[("Vb", b)])
                S.dma("sp", BKb[b][:].rearrange("p a n -> p (a n)"), self.bk_d[d, T], reads=[("bk_d", d, T)], writes=[("BKb", b)])
                S.dma("sp", gCb[b][:], self.gC_d[d, T], reads=[("gC_d", d, T)], writes=[("gCb", b)])

            def run_group(w, R_):
                d, T, b, g0, sl = w["d"], w["T"], w["b"], w["g0"], R_["sl"]
                if g0 == 0:
                    load_chunk(w)
                Fk, Vk, BKk, gk = ("Fb", b), ("Vb", b), ("BKb", b), ("gCb", b)
                GM, X, XT, T1s, Zq, Pb, bk, bz = (R_[n] for n in ("GM", "X", "XT", "T1s", "Zq", "Pb", "bk", "bz"))
                K = lambda n, *a: (n, sl) + a
                hs = heads_of(g0)
                F_ = Fb[b]
                st32, stb = ST32[d], STb[d]
                for (g, h, j, pb_) in hs:
                    bank = bk[g % 3]; bkk = K("bk", g % 3)
                    bv = bank[:].rearrange("p g t -> p (g t)")
                    AR = F_[pb_:pb_ + 64, j, 0:2, :].rearrange("p q t -> p (q t)")
                    mm(S, bv[:, 0:128], F_[pb_:pb_ + 64, j, 2, :], F_[pb_:pb_ + 64, j, 1, :], True, True, reads=[Fk], writes=[bkk])
                    mm(S, bv[:, 128:384], F_[pb_:pb_ + 64, j, 3, :], AR, True, True, reads=[Fk], writes=[bkk])
                    mm(S, bv[:, 384:512], F_[pb_:pb_ + 64, j, 0, :], F_[pb_:pb_ + 64, j, 2, :], True, True, reads=[Fk], writes=[bkk])
                    _tt(S, "dve", GM[:, g, :], bv, m4[:, d, :], ALU.mult, ["m4"], [bkk, K("GM")])
                yield
                for (g, h, j, pb_) in hs:
                    mm(S, bz[:, g, :], F_[pb_:pb_ + 64, j, 0, :], stb[pb_:pb_ + 64, j, :], True, False, reads=[Fk, ("STb", d)], writes=[K("bz")])
                    mm(S, bz[:, g, :], GM[:, g, 128:256], Vb[b][:, h * 64:(h + 1) * 64], False, True, reads=[K("GM"), Vk], writes=[K("bz")])
                _act(S, Zq[:], bz[:, 0:NG, :], AF.Copy, [], [K("bz"), K("Zq")])
                yield
                xi = 0
                mm(S, bk[0][:].rearrange("p g t -> p (g t)"), idb[:], I4f, True, False, reads=["idb", "I4"], writes=[K("bk", 0)])
                for (g, h, j, pb_) in hs:
                    mm(S, bk[0][:, g, :], GM[:, g, 384:512], idb[:], False, True, reads=[K("GM"), "idb"], writes=[K("bk", 0)])
                _tt(S, "dve", X[xi][:], bk[0][:], lmN[:, d, 0:1, :].broadcast_to([P, NG, P]), ALU.mult, ["lmN"], [K("bk", 0), K("X", xi)])
                yield
                for (g, h, j, pb_) in hs:
                    mm(S, bk[2][:, g, :], X[xi][:, g, :], idb[:], True, True, reads=[K("X", xi), "idb"], writes=[K("bk", 2)])
                _act(S, XT[xi][:], bk[2][:], AF.Copy, [], [K("bk", 2), K("XT", xi)])
                yield
                for l in range(1, 7):
                    mm(S, bk[0][:].rearrange("p g t -> p (g t)"), idb[:], I4f, True, False, reads=["idb", "I4"], writes=[K("bk", 0)])
                    for (g, h, j, pb_) in hs:
                        mm(S, bk[0][:, g, :], GM[:, g, 384:512], X[xi][:, g, :], False, True, reads=[K("GM"), K("X", xi)], writes=[K("bk", 0)])
                    _tt(S, "dve", T1s[:], bk[0][:], lmN[:, d, l:l + 1, :].broadcast_to([P, NG, P]), ALU.mult, ["lmN"], [K("bk", 0), K("T1s")])
                    yield
                    for (g, h, j, pb_) in hs:
                        mm(S, bk[1][:, g, :], XT[xi][:, g, :], T1s[:, g, :], True, True, reads=[K("XT", xi), K("T1s")], writes=[K("bk", 1)])
                    if l < 6:
                        for (g, h, j, pb_) in hs:
                            mm(S, bk[2][:, g, :], T1s[:, g, :], XT[xi][:, g, :], True, True, reads=[K("XT", xi), K("T1s")], writes=[K("bk", 2)])
                    _act(S, X[1 - xi][:], bk[1][:], AF.Copy, [], [K("bk", 1), K("X", 1 - xi)])
                    if l < 6:
                        if l % 3 != 0:
                            _act(S, XT[1 - xi][:], bk[2][:], AF.Copy, [], [K("bk", 2), K("XT", 1 - xi)])
                        else:
                            S.op("dve", lambda e: e.tensor_copy(out=XT[1 - xi][:], in_=bk[2][:]), reads=[], writes=[K("bk", 2), K("XT", 1 - xi)])
                    xi = 1 - xi
                    yield
                for (g, h, j, pb_) in hs:
                    mm(S, bz[:, g, :], X[xi][:, g, :], Zq[:, g, :], True, True, reads=[K("X", xi), K("Zq")], writes=[K("bz")])
                S.op("dve", lambda e: e.tensor_copy(out=Pb[:], in_=bz[:, 0:NG, :]), reads=[], writes=[K("bz"), K("Pb")])
                yield
                for (g, h, j, pb_) in hs:
                    yo = bz[:, g, :]
                    mm(S, yo, GM[:, g, 0:128], Pb[:, g, :], True, False, reads=[K("GM"), K("Pb")], writes=[K("bz")])
                    mm(S, yo, GM[:, g, 256:384], Vb[b][:, h * 64:(h + 1) * 64], False, False, reads=[K("GM"), Vk], writes=[K("bz")])
                    mm(S, yo, F_[pb_:pb_ + 64, j, 1, :], stb[pb_:pb_ + 64, j, :], False, True, reads=[Fk, ("STb", d)], writes=[K("bz")])
                for (g, h, j, pb_) in hs:
                    mm(S, bz[:, 4 + g, :], BKb[b][:, 0, j * P:(j + 1) * P], Pb[:, g, :], True, False, reads=[BKk, K("Pb")], writes=[K("bz")])
                    mm(S, bz[:, 4 + g, :], BKb[b][:, 1, j * P:(j + 1) * P], Vb[b][:, h * 64:(h + 1) * 64], False, True,
                       reads=[BKk, Vk], writes=[K("bz")])
                _act(S, ysb[b][:, g0 * 64:(g0 + NG) * 64].rearrange("p (g v) -> p g v", v=64), bz[:, 0:NG, :], AF.Copy, [], [K("bz"), ("ysb", b)])
                for (g, h, j, pb_) in hs:
                    _stt(S, "dve", st32[pb_:pb_ + 64, j, :], st32[pb_:pb_ + 64, j, :], gCb[b][pb_:pb_ + 64, j:j + 1],
                         bz[pb_:pb_ + 64, 4 + g, :], ALU.mult, ALU.add, [gk], [("ST32", d), K("bz")])
                S.op("pool", lambda e: e.tensor_copy(out=stb[:, g0 // 2:g0 // 2 + 2, :], in_=st32[:, g0 // 2:g0 // 2 + 2, :]),
                     reads=[("ST32", d)], writes=[("STb", d)])
                if g0 + NG == 16:
                    S.dma("sp", self.y_d[d, T * P:(T + 1) * P, :], ysb[b][:], reads=[("ysb", b)], writes=[("y_d", d, T)])
                yield

            nxt = 0
            active = [None, None]
            while True:
                progressed = False
                for sl in range(2):
                    if active[sl] is None and nxt < len(items):
                        active[sl] = run_group(items[nxt], SL[sl])
                        nxt += 1
                    if active[sl] is not None:
                        progressed = True
                        try:
                            next(active[sl])
                        except StopIteration:
                            active[sl] = None
                if not progressed:
                    break

    def phase_R3(self):
        S = self.S
        with S.scope():
            R = {}
            for nm, src in (("lng_r", self.rw_lng), ("lnb_r", self.rw_lnb)):
                R[nm] = S.sb(nm, [P, D], F32)
                S.dma("sp", R[nm][:], src.broadcast_to([P, D]), writes=[nm])
            B = [{n: S.sb(f"{n}{i}", [P, D], F32) for n in ("yf", "yb", "gg", "bo")} for i in range(2)]
            zb = [S.sb(f"zb{i}", [P, D], BF16) for i in range(2)]
            zt = [S.sb(f"zt3_{i}", [P, 8, P], BF16) for i in range(2)]
            st = [S.sb(f"st3_{i}", [P, 64], F32) for i in range(2)]
            pT = [S.ps(f"pT3_{i}", [P, 8, P], BF16) for i in range(2)]
            for T in range(2, NT):
                b = T % 2
                Bf = B[b]
                k = lambda n: (n, b)
                t0 = T * P
                S.dma("sp", Bf["yf"][:], self.y_d[0, t0:t0 + P, :], reads=[("y_d", 0, T)], writes=[k("yf")])
                S.dma("sp", Bf["yb"][:], self.y_d[1, t0:t0 + P, :], reads=[("y_d", 1, T)], writes=[k("yb")])
                S.dma("sp", Bf["gg"][:], self.g_d[t0:t0 + P, :], reads=[("g_d", T)], writes=[k("gg")])
                S.dma("sp", Bf["bo"][:], self.bonus_d[t0:t0 + P, :], reads=[("bonus_d", T)], writes=[k("bo")])
                y = Bf["yf"]; t2 = Bf["yb"]
                _tt(S, "dve", y[:], y[:], t2[:], ALU.add, [k("yf"), k("yb")], [k("yf")])
                S.op("dve", lambda e: e.tensor_reduce(out=st[b][:, 0:16], in_=_h3(y[:]), axis=AX.X, op=ALU.add), reads=[k("yf")], writes=[k("st")])
                S.op("dve", lambda e: e.tensor_scalar(out=st[b][:, 0:16], in0=st[b][:, 0:16], scalar1=-1.0 / 64, scalar2=None, op0=ALU.mult),
                     reads=[k("st")], writes=[k("st")])
                _tt(S, "dve", _h3(y[:]), _h3(y[:]), st[b][:, 0:16][:, :, None].broadcast_to([P, 16, 64]), ALU.add, [k("yf"), k("st")], [k("yf")])
                _tt(S, "pool", t2[:], y[:], y[:], ALU.mult, [k("yf")], [k("yb")])
                S.op("dve", lambda e: e.tensor_reduce(out=st[b][:, 16:32], in_=_h3(t2[:]), axis=AX.X, op=ALU.add), reads=[k("yb")], writes=[k("st")])
                S.op("dve", lambda e: e.tensor_scalar(out=st[b][:, 16:32], in0=st[b][:, 16:32], scalar1=1.0 / 64, scalar2=GN_EPS, op0=ALU.mult, op1=ALU.add),
                     reads=[k("st")], writes=[k("st")])
                _act(S, st[b][:, 16:32], st[b][:, 16:32], AF.Sqrt, [k("st")], [k("st")])
                S.op("dve", lambda e: e.reciprocal(out=st[b][:, 32:48], in_=st[b][:, 16:32]), reads=[k("st")], writes=[k("st")])
                _tt(S, "dve", _h3(y[:]), _h3(y[:]), st[b][:, 32:48][:, :, None].broadcast_to([P, 16, 64]), ALU.mult, [k("yf"), k("st")], [k("yf")])
                _tt(S, "pool", y[:], y[:], R["lng_r"][:], ALU.mult, [k("yf"), "lng_r"], [k("yf")])
                _tt(S, "pool", y[:], y[:], R["lnb_r"][:], ALU.add, [k("yf"), "lnb_r"], [k("yf")])
                _tt(S, "dve", y[:], y[:], Bf["bo"][:], ALU.add, [k("yf"), k("bo")], [k("yf")])
                _tt(S, "dve", zb[b][:], y[:], Bf["gg"][:], ALU.mult, [k("yf"), k("gg")], [k("zb")])
                for c in range(8):
                    S.op("pe", lambda e: e.transpose(out=pT[b][:, c, :], in_=zb[b][:, c * P:(c + 1) * P], identity=self.idb[:]),
                         reads=[k("zb"), "idb"], writes=[k("pT3")], accum=(c > 0))
                _act(S, zt[b][:], pT[b][:], AF.Copy, [k("pT3")], [k("zt3")])
                S.dma("sp", self.zT_d[:, :, t0:t0 + P].rearrange("c p t -> p c t"), zt[b][:], reads=[k("zt3")],
                      writes=[("zT", c, T) for c in range(8)])


def build_program(stop_after=None, debug=(), phases="M0ABCD1abcde"):
    nc = bass.Bass("TRN2", target_bir_lowering=False)
    Pg = Prog1(nc, debug)
    S = Pg.S
    Pg.consts()
    if "M" in phases:
        Pg.phase_mod()
    if "0" in phases:
      with S.scope():
        V0 = Pg.load_layer_vecs(0)
        if "A" in phases: Pg.phase_L0_proj(V0)
        if "B" in phases: Pg.phase_L0_pool()
        if "C" in phases: Pg.phase_L0_attn()
        if "D" in phases: Pg.phase_out_mlp(0, V0, Pg.ev_w_out, Pg.y_tile_L0, lambda T: (Pg.x_d[T * P:(T + 1) * P, :], ("x_d", T)))
    if "1" in phases:
      with S.scope():
        V1 = Pg.load_layer_vecs(1)
        if "a" in phases: Pg.phase_R0(V1)
        if "b" in phases: Pg.phase_R1()
        if "c" in phases: Pg.phase_R2()
        if "d" in phases: Pg.phase_R3()
        if "e" in phases: Pg.phase_out_mlp(1, V1, Pg.rw_wo, Pg.y_tile_L0, lambda T: (Pg.out[(T - 2) * P:(T - 1) * P, :], ("out", T)))
    S.barrier()
    S.finish([])
    S.close()
    return nc, Pg


_host_inputs0 = host_inputs


def host_inputs(inputs):
    maps = _host_inputs0(inputs)
    f = lambda a: np.ascontiguousarray(np.asarray(a, dtype=np.float32))
    tri, m4, lmT = _scan_consts()
    sh = {
        "rw_mu": f(inputs["rw_mu"])[0], "rw_wr": f(inputs["rw_wr"])[0], "rw_wk": f(inputs["rw_wk"])[0],
        "rw_wv": f(inputs["rw_wv"])[0], "rw_wo": f(inputs["rw_wo"])[0],
        "rw_w0": f(inputs["rw_w0"])[0], "rw_a0": f(inputs["rw_a0"])[0],
        "w1cat": f(np.concatenate([inputs["rw_w1"][0, 0], inputs["rw_w1"][0, 1]], axis=1)),
        "a1cat": f(np.concatenate([inputs["rw_a1"][0, 0], inputs["rw_a1"][0, 1]], axis=1)),
        "rw_g1": f(inputs["rw_g1"])[0],
        "w2cat": f(np.asarray(inputs["rw_w2"])[0].reshape(128, 1024)), "a2cat": f(np.asarray(inputs["rw_a2"])[0].reshape(128, 1024)),
        "rw_g2": f(inputs["rw_g2"])[0],
        "rw_kk": f(inputs["rw_kk"]).reshape(1, 1024), "rw_ka": f(inputs["rw_ka"]).reshape(1, 1024),
        "rw_rk": f(inputs["rw_rk"]).reshape(1, 1024), "rw_lng": f(inputs["rw_lng"]).reshape(1, 1024),
        "rw_lnb": f(inputs["rw_lnb"]).reshape(1, 1024),
        "tri_c": f(tri), "m4_c": f(m4), "lmT_c": f(lmT),
    }
    for m in maps:
        m.update(sh)
    return maps
```

```python
import contextlib
import numpy as np
import concourse.bass as bass
import concourse.mybir as mybir

F32 = mybir.dt.float32
BF16 = mybir.dt.bfloat16
AF = mybir.ActivationFunctionType
ALU = mybir.AluOpType
AX = mybir.AxisListType

SEM_LIMIT = 10000


class _Ctr:
    def __init__(self, S, name, step):
        self.S = S
        self.name = name
        self.step = step
        self.gen = 0
        self.sem = S._newsem(f"{name}_0")
        self.val = 0

    def next_event(self):
        if self.val + self.step > SEM_LIMIT:
            self.gen += 1
            self.sem = self.S._newsem(f"{self.name}_{self.gen}")
            self.val = 0
        self.val += self.step
        return (self.sem, self.val)


class _PsView:
    def __init__(self, t, shape):
        self.t = t
        self.n1 = shape[1]

    def __getitem__(self, key):
        if not isinstance(key, tuple):
            key = (key,)
        key = list(key)
        if len(key) < 2:
            key.append(slice(None))
        k1 = key[1]
        if isinstance(k1, slice):
            start, stop, step = k1.indices(self.n1)
            key[1] = slice(start, stop, step)
        return self.t[tuple(key)]


class _Eng:
    def __init__(self, S, name, obj):
        self.name = name
        self.obj = obj
        self.ctr = _Ctr(S, "s_" + name, 1)
        self.seen = {}
        self.n_issued = 0
        self.last_ins = None
        self.last_has_inc = False
        self.inc_idx = []
        self.inc_ev = []


class LazyEv:
    __slots__ = ("eng", "idx")

    def __init__(self, eng, idx):
        self.eng = eng
        self.idx = idx


class _Res:
    __slots__ = ("w", "r")

    def __init__(self):
        self.w = None
        self.r = {}


class Sched:
    def __init__(self, nc, n_dma_slots=8):
        self.nc = nc
        self.stack = contextlib.ExitStack()
        self.scopes = [self.stack]
        self.res = {}
        self.engs = {
            "pe": _Eng(self, "pe", nc.tensor),
            "act": _Eng(self, "act", nc.scalar),
            "dve": _Eng(self, "dve", nc.vector),
            "pool": _Eng(self, "pool", nc.gpsimd),
            "sp": _Eng(self, "sp", nc.sync),
        }
        self.dma_slots = {}
        for q in ("sp", "pool", "act"):
            self.dma_slots[q] = [_Ctr(self, f"d_{q}{i}", 16) for i in range(n_dma_slots)]
        self.dma_rr = {"sp": 0, "pool": 0, "act": 0}
        self.n_inst = 0
        self.uid = 0
        self.pending = None
        self.lazy_engines = ()

    def _newsem(self, name):
        return self.stack.enter_context(self.nc.semaphore(name))

    def sb(self, name, shape, dt):
        self.uid += 1
        return self.scopes[-1].enter_context(self.nc.sbuf_tensor(f"sb{self.uid}_{name}", list(shape), dt))

    def ps(self, name, shape, dt=F32):
        self.uid += 1
        esz = 4 if dt == F32 else 2
        per_part = esz
        for d_ in shape[1:]:
            per_part *= d_
        assert per_part <= 2048, (name, shape)
        shape = list(shape)
        if per_part < 2048:
            rest = per_part // shape[1]
            assert 2048 % rest == 0, (name, shape)
            full = [shape[0], 2048 // rest] + shape[2:]
            t = self.scopes[-1].enter_context(self.nc.psum_tensor(f"ps{self.uid}_{name}", full, dt))
            return _PsView(t, shape)
        return self.scopes[-1].enter_context(self.nc.psum_tensor(f"ps{self.uid}_{name}", shape, dt))

    @contextlib.contextmanager
    def scope(self):
        st = contextlib.ExitStack()
        self.scopes.append(st)
        try:
            yield
        finally:
            self.barrier()
            self.scopes.pop()
            st.close()

    def barrier(self):
        evs = []
        for e in self.engs.values():
            if e.n_issued > 0:
                evs.append(self._resolve(LazyEv(e, e.n_issued - 1)))
        for q in self.dma_slots:
            for ctr in self.dma_slots[q]:
                if ctr.val > 0:
                    evs.append((ctr.sem, ctr.val))
        for e in self.engs.values():
            for ev in evs:
                self._wait(e, ev)

    def _r(self, key):
        r = self.res.get(key)
        if r is None:
            r = self.res[key] = _Res()
        return r

    def _resolve(self, ev):
        if not isinstance(ev, LazyEv):
            return ev
        import bisect
        e = ev.eng
        k = bisect.bisect_left(e.inc_idx, ev.idx)
        if k < len(e.inc_idx):
            return e.inc_ev[k]
        assert e.last_ins is not None and not e.last_has_inc and e.n_issued - 1 >= ev.idx
        sv = e.ctr.next_event()
        e.last_ins.then_inc(sv[0], 1)
        e.last_has_inc = True
        e.inc_idx.append(e.n_issued - 1)
        e.inc_ev.append(sv)
        return sv

    def _wait(self, eng, ev):
        if ev is None:
            return
        if isinstance(ev, LazyEv) and ev.eng is eng and eng.name == "pe":
            return
        sem, val = self._resolve(ev)
        k = id(sem)
        if eng.seen.get(k, 0) >= val:
            return
        if self.pending is not None:
            cur = self.pending.get(k)
            if cur is None or cur[1] < val:
                self.pending[k] = (sem, val)
            return
        eng.obj.wait_ge(sem, val)
        eng.seen[k] = val

    def _flush(self, eng):
        pend = list(self.pending.values())
        self.pending = None
        for (sem, val) in pend[:-1]:
            eng.obj.wait_ge(sem, val)
            eng.seen[id(sem)] = val
        if pend:
            sem, val = pend[-1]
            eng.seen[id(sem)] = val
            return (sem, val)
        return None

    def _deps(self, eng, reads, writes, skip_same_eng_write=False):
        for key in reads:
            r = self._r(key)
            self._wait(eng, r.w)
        inorder = eng.name in ("act", "dve")
        for key in writes:
            r = self._r(key)
            if not ((skip_same_eng_write or inorder) and isinstance(r.w, LazyEv) and r.w.eng is eng):
                self._wait(eng, r.w)
            for ev in r.r.values():
                if inorder and isinstance(ev, LazyEv) and ev.eng is eng:
                    continue
                self._wait(eng, ev)

    def _commit(self, ev, reads, writes):
        rk = ev.eng.name if isinstance(ev, LazyEv) else id(ev[0])
        for key in reads:
            self._r(key).r[rk] = ev
        for key in writes:
            r = self._r(key)
            r.w = ev
            r.r = {}

    def op(self, engname, fn, reads=(), writes=(), accum=False):
        eng = self.engs[engname]
        self.pending = {}
        self._deps(eng, reads, writes, skip_same_eng_write=accum)
        last = self._flush(eng)
        ins = fn(eng.obj)
        if last is not None:
            ins._wait_ge(last[0], last[1])
        eng.last_ins = ins
        eng.last_has_inc = False
        ev = LazyEv(eng, eng.n_issued)
        eng.n_issued += 1
        if engname not in self.lazy_engines:
            self._resolve(ev)
        self._commit(ev, reads, writes)
        self.n_inst += 1
        return ev

    def dma(self, q, out, in_, reads=(), writes=(), **kw):
        eng = self.engs[q]
        slots = self.dma_slots[q]
        i = self.dma_rr[q]
        self.dma_rr[q] = (i + 1) % len(slots)
        ctr = slots[i]
        self.pending = {}
        if ctr.val > 0:
            self._wait(eng, (ctr.sem, ctr.val))
        self._deps(eng, reads, writes)
        last = self._flush(eng)
        ev = ctr.next_event()
        ins = eng.obj.dma_start(out=out, in_=in_, **kw)
        if last is not None:
            ins._wait_ge(last[0], last[1])
        ins.then_inc(ev[0], 16)
        self._commit(ev, reads, writes)
        self.n_inst += 1
        return ev

    def finish(self, final_keys):
        eng = self.engs["sp"]
        for key in final_keys:
            r = self._r(key)
            self._wait(eng, r.w)
        for q in self.dma_slots:
            for ctr in self.dma_slots[q]:
                if ctr.val > 0:
                    self._wait(eng, (ctr.sem, ctr.val))

    def close(self):
        self.stack.close()

from concourse.bass_utils import run_bass_kernel_spmd

D = 1024
NCTX = 256
NLAT = 4096
NTOK = NCTX + NLAT
NT = NTOK // 128
EPS = 1e-6
P = 128


def _pool_bands():
    L = 1024
    out = np.zeros((4, 5, 128, 128), np.float32)
    for g, w in enumerate((2, 4, 8, 16)):
        def full(L):
            t = np.arange(L)
            lo = np.clip(t - w // 2, 0, L)
            hi = np.clip(t + w // 2, 0, L)
            s = np.arange(L)[:, None]
            m = ((s >= lo[None, :]) & (s < hi[None, :])).astype(np.float64) / (hi - lo)[None, :]
            m -= np.eye(L)
            return m
        m = full(L)
        out[g, 0] = m[3 * 128:4 * 128, 4 * 128:5 * 128]
        out[g, 1] = m[5 * 128:6 * 128, 4 * 128:5 * 128]
        out[g, 2] = m[4 * 128:5 * 128, 4 * 128:5 * 128]
        out[g, 3] = m[0:128, 0:128]
        out[g, 4] = m[L - 128:, L - 128:]
    return out


_VARS = [(-2, "pm"), (-1, "f"), (0, "f"), (1, "f"), (2, "pp")] + [(d, "f") for d in range(-3, 4)]


def _attn_tables():
    kc = np.arange(64)
    qc = np.arange(64)
    c_start = np.clip(qc - 8, 0, 48)
    col_ok = (kc[:, None] >= c_start[None, :]) & (kc[:, None] < c_start[None, :] + 16)
    dc_idx = np.clip(kc[:, None] - qc[None, :], -15, 15) + 15
    dr_idx = np.zeros((12, 128, 128), np.int64)
    dc_full = np.zeros((128, 128), np.int64)
    mask = np.zeros((12, 128, 128), np.float32)
    for a in range(2):
        for b in range(2):
            dc_full[a * 64:(a + 1) * 64, b * 64:(b + 1) * 64] = dc_idx
    for v, (dl, kind) in enumerate(_VARS):
        for a in range(2):
            for b in range(2):
                dr = 2 * dl + a - b + 7
                vis = True
                if kind == "pm":
                    vis = not (a == 0 and b == 1)
                elif kind == "pp":
                    vis = (a == 0 and b == 1)
                dr_idx[v, a * 64:(a + 1) * 64, b * 64:(b + 1) * 64] = min(max(dr, 0), 14)
                if vis and 0 <= dr <= 14:
                    mask[v, a * 64:(a + 1) * 64, b * 64:(b + 1) * 64] = col_ok
    return dr_idx, dc_full, mask


def mm(S, out, lhsT, rhs, start, stop, reads, writes):
    return S.op("pe", lambda e: e.matmul(out, lhsT=lhsT, rhs=rhs, start=start, stop=stop),
                reads=reads, writes=writes, accum=not start)


class Prog:
    def __init__(self, nc, debug=()):
        self.nc = nc
        self.S = Sched(nc)
        self.debug = debug
        self.dbg_out = {}
        dt = nc.dram_tensor
        I = lambda name, shape: dt(name, list(shape), F32, kind="ExternalInput").ap()
        self.x_in = I("x", [NLAT, D])
        self.ctx_in = I("ctx", [NCTX, D])
        self.cvec = I("cvec", [P, 8, 2])
        self.ada_w = I("ada_w", [2, D, 6 * D])
        self.ada_b = I("ada_b", [2, 6 * D])
        self.norm_g = I("norm_g", [2, 4, D])
        self.mlp_w1 = I("mlp_w1", [2, D, 4 * D])
        self.mlp_w2 = I("mlp_w2", [2, 4 * D, D])
        self.ev_w_in = I("ev_w_in", [D, 2 * D])
        self.ev_w_out = I("ev_w_out", [D, D])
        self.ev_pool_w = I("ev_pool_w", [4, P, P])
        self.ev_pool_scale = I("ev_pool_scale", [512])
        self.rpb_tab = I("rpb_tab", [P, 8 * 12, P])
        self.msk_tab = I("msk_tab", [P, 12, P])
        self.bands = I("bands", [P, 20, P])
        self.ident = I("ident", [P, P])
        self.out = dt("out", [NLAT, D], F32, kind="ExternalOutput").ap()
        X = lambda name, shape, d=F32: (dt(name, list(shape), d, kind="ExternalOutput").ap() if name in debug
                                        else dt(name, list(shape), d).ap())
        self.modd = X("modd", [2, 2, 6 * D])
        self.x_d = X("x_d", [NTOK, D])
        self.upool_d = X("upool_d", [NTOK, 512], BF16)
        self.v_d = X("v_d", [NTOK, 512], BF16)
        self.qT_d = X("qT_d", [4, P, NTOK], BF16)
        self.kT_d = X("kT_d", [4, P, NTOK], BF16)
        self.zT_d = X("zT_d", [8, P, NTOK], BF16)

    def dbg(self, name, shape, dtp=F32):
        t = self.nc.dram_tensor("dbg_" + name, list(shape), dtp, kind="ExternalOutput").ap()
        self.dbg_out[name] = t
        return t

    def consts(self):
        S = self.S
        self.idb = S.sb("idb", [P, P], BF16)
        S.dma("pool", self.idb[:], self.ident, writes=["idb"])
        self.ones_bf = S.sb("ones_bf", [P, P], BF16)
        S.op("dve", lambda e: e.memset(self.ones_bf[:], 1.0), writes=["ones_bf"])
        self.eps_t = S.sb("eps_t", [P, 1], F32)
        S.op("dve", lambda e: e.memset(self.eps_t[:], EPS), writes=["eps_t"])

    def phase_mod(self):
        S = self.S
        with S.scope():
            cv = S.sb("cv", [P, 8, 2], F32)
            cvb = S.sb("cvb", [P, 8, 2], BF16)
            S.dma("sp", cv[:], self.cvec, writes=["cv"])
            S.op("act", lambda e: e.activation(out=cvb[:], in_=cv[:], func=AF.Silu), reads=["cv"], writes=["cvb"])
            aw = S.sb("aw", [P, 8, 6 * D], BF16)
            ab = S.sb("ab", [2, 6 * D], F32)
            mrow = S.sb("mrow", [2, 6 * D], F32)
            pss = [S.ps(f"pm{i}", [2, 512], F32) for i in range(4)]
            for l in range(2):
                for c in range(8):
                    S.dma("pool", aw[:, c, :], self.ada_w[l, c * P:(c + 1) * P, :], writes=[("aw", c)])
                S.dma("sp", ab[:], self.ada_b[l:l + 1, :].broadcast_to([2, 6 * D]), writes=["ab"])
                for n in range(12):
                    ps = pss[n % 4]
                    k = ("pm", n % 4)
                    for c in range(8):
                        mm(S, ps[:], cvb[:, c, :], aw[:, c, n * 512:(n + 1) * 512], c == 0, c == 7,
                           reads=["cvb", ("aw", c)], writes=[k])
                    S.op("dve", lambda e: e.tensor_tensor(out=mrow[:, n * 512:(n + 1) * 512], in0=ps[:],
                                                          in1=ab[:, n * 512:(n + 1) * 512], op=ALU.add),
                         reads=[k, "ab"], writes=["mrow"])
                S.dma("sp", self.modd[l], mrow[:], reads=["mrow"], writes=[("modd", l)])

    def load_layer_vecs(self, l):
        S = self.S
        V = {}
        for s in range(2):
            for which in range(2):
                V[("A", which, s)] = (S.sb(f"A{which}_{s}", [P, 8], F32), f"A{which}_{s}")
                V[("B", which, s)] = (S.sb(f"B{which}_{s}", [P, 8], F32), f"B{which}_{s}")
                if not (l == 1 and s == 1):
                    V[("G", which, s)] = (S.sb(f"GG{which}_{s}", [P, D], F32), f"GG{which}_{s}")
        with S.scope(), self.nc.allow_non_contiguous_dma(reason="tiny per-feature vectors"):
            tmp = S.sb("lv_tmp", [P, 8], F32)
            rowt = S.sb("lv_row", [P, D], F32)
            for s in range(2):
                for which, (ish, isc, ig) in enumerate(((0, 1, 0), (3, 4, 2))):
                    A, ka = V[("A", which, s)]
                    B, kb = V[("B", which, s)]
                    S.dma("sp", B[:], self.modd[l, s, ish * D:(ish + 1) * D].rearrange("(c p) -> p c", p=P),
                          reads=[("modd", l)], writes=[kb])
                    S.dma("sp", A[:], self.modd[l, s, isc * D:(isc + 1) * D].rearrange("(c p) -> p c", p=P),
                          reads=[("modd", l)], writes=[ka])
                    S.dma("sp", tmp[:], self.norm_g[l, ig, :].rearrange("(c p) -> p c", p=P), writes=["lv_tmp"])
                    S.op("dve", lambda e: e.scalar_tensor_tensor(out=A[:], in0=A[:], scalar=1.0, in1=tmp[:],
                                                                 op0=ALU.add, op1=ALU.mult),
                         reads=[ka, "lv_tmp"], writes=[ka])
                for which, (igt, ig) in enumerate(((2, 1), (5, 3))):
                    if l == 1 and s == 1:
                        continue
                    G, kg = V[("G", which, s)]
                    S.dma("sp", G[:], self.modd[l, s:s + 1, igt * D:(igt + 1) * D].broadcast_to([P, D]),
                          reads=[("modd", l)], writes=[kg])
                    S.dma("sp", rowt[:], self.norm_g[l, ig:ig + 1, :].broadcast_to([P, D]), writes=["lv_row"])
                    S.op("dve", lambda e: e.tensor_tensor(out=G[:], in0=G[:], in1=rowt[:], op=ALU.mult),
                         reads=[kg, "lv_row"], writes=[kg])
        return V

    def make_norm_bufs(self, tag, nb=2):
        S = self.S
        B = {"i": 0, "nb": nb, "tag": tag}
        B["sq"] = [S.sb(f"{tag}_sq{i}", [P, D], BF16) for i in range(1)] * nb
        B["st"] = [S.sb(f"{tag}_st{i}", [P, 4], F32) for i in range(nb)]
        B["xn"] = [S.sb(f"{tag}_xn{i}", [P, D], BF16) for i in range(nb)]
        B["tp"] = [S.ps(f"{tag}_tp{i}", [P, 8, P], BF16) for i in range(nb)]
        return B

    def norm_to_hT(self, B, x_sb, xkey, A, B_, out_ap, out_key, out2_ap=None, out2_key=None):
        S = self.S
        i = B["i"] % B["nb"]
        B["i"] += 1
        tag = B["tag"]
        sq, st, xn, tp = B["sq"][i], B["st"][i], B["xn"][i], B["tp"][i]
        ksq, kst, kxn, ktp, ktm = [(tag, n, i) for n in ("sq", "st", "xn", "tp", "tm")]
        ksq = (tag, "sq", 0)
        S.op("pool", lambda e: e.memset(st[:], 0.0), writes=[kst])
        S.op("act", lambda e: e.activation(out=sq[:], in_=x_sb, func=AF.Square, accum_out=st[:, 0:1]),
             reads=[xkey], writes=[ksq, kst])
        S.op("act", lambda e: e.activation(out=st[:, 1:2], in_=st[:, 0:1], func=AF.Sqrt, scale=1.0 / D,
                                           bias=self.eps_t[:, 0:1]), reads=[kst, "eps_t"], writes=[kst])
        S.op("dve", lambda e: e.reciprocal(out=st[:, 2:3], in_=st[:, 1:2]), reads=[kst], writes=[kst])
        S.op("dve", lambda e: e.tensor_scalar(out=xn[:], in0=x_sb, scalar1=st[:, 2:3], scalar2=None, op0=ALU.mult),
             reads=[xkey, kst], writes=[kxn])
        for c in range(8):
            S.op("pe", lambda e: e.transpose(out=tp[:, c, :], in_=xn[:, c * P:(c + 1) * P], identity=self.idb[:]),
                 reads=[kxn, "idb"], writes=[ktp], accum=(c > 0))
        for c in range(8):
            S.op("dve", lambda e: e.tensor_scalar(out=out_ap[:, c, :], in0=tp[:, c, :], scalar1=A[0][:, c:c + 1],
                                                  scalar2=B_[0][:, c:c + 1], op0=ALU.mult, op1=ALU.add),
                 reads=[ktp, A[1], B_[1]], writes=[out_key])

    def x_src(self, layer, T):
        if layer == 0:
            if T < 2:
                return self.ctx_in[T * P:(T + 1) * P, :], None
            return self.x_in[(T - 2) * P:(T - 1) * P, :], None
        return self.x_d[T * P:(T + 1) * P, :], ("x_d", T)

    def phase_L0_proj(self, V):
        S = self.S
        with S.scope():
            w = S.sb("w_in", [P, 8, 2 * D], BF16)
            for c in range(8):
                S.dma("pool", w[:, c, :], self.ev_w_in[c * P:(c + 1) * P, :], writes=[("w_in", c)])
            wk = [("w_in", c) for c in range(8)]
            NB = self.make_norm_bufs("n0")
            xt = [S.sb(f"xt{i}", [P, D], F32) for i in range(2)]
            hT = [S.sb(f"hT{i}", [P, 8, 512], BF16) for i in range(2)]
            ptok = [S.ps(f"ptok{i}", [P, 512], F32) for i in range(2)]
            pft = [S.ps(f"pft{i}", [P, 512], F32) for i in range(2)]
            otok = [S.sb(f"otok{i}", [P, 512], BF16) for i in range(2)]
            oft = [S.sb(f"oft{i}", [P, 512], BF16) for i in range(2)]
            supers = [(0, 2)] + [(2 + 4 * i, 4) for i in range(8)]
            cnt = 0
            ctok = 0
            cft = 0
            for si, (T0, nt) in enumerate(supers):
                hb = hT[si % 2]
                hk = ("hT", si % 2)
                s = 1 if T0 < 2 else 0
                for t in range(nt):
                    T = T0 + t
                    xb = xt[cnt % 2]
                    xk = ("xt", cnt % 2)
                    cnt += 1
                    src, sk = self.x_src(0, T)
                    S.dma("act", xb[:], src, reads=[sk] if sk else [], writes=[xk])
                    self.norm_to_hT(NB, xb[:], xk, V[("A", 0, s)], V[("B", 0, s)],
                                    hb[:, :, t * P:(t + 1) * P], (hk, t))
                n = nt * P
                hks = [(hk, t) for t in range(nt)]
                for t in range(nt):
                    T = T0 + t
                    for (c0, dst, dk) in ((0, self.upool_d, "upool"), (1536, self.v_d, "v")):
                        ps = ptok[ctok % 2]; pk = ("ptok", ctok % 2)
                        ob = otok[ctok % 2]; ok = ("otok", ctok % 2)
                        ctok += 1
                        for c in range(8):
                            mm(S, ps[:], hb[:, c, t * P:(t + 1) * P], w[:, c, c0:c0 + 512], c == 0, c == 7,
                               reads=[(hk, t), wk[c]], writes=[pk])
                        S.op("act", lambda e: e.activation(out=ob[:], in_=ps[:], func=AF.Copy), reads=[pk], writes=[ok])
                        S.dma("sp", dst[T * P:(T + 1) * P, :], ob[:], reads=[ok], writes=[(dk, T)])
                for jb in range(8):
                    c0 = 512 + jb * P
                    ps = pft[cft % 2]; pk = ("pft", cft % 2)
                    ob = oft[cft % 2]; ok = ("oft", cft % 2)
                    cft += 1
                    for c in range(8):
                        mm(S, ps[:, :n], w[:, c, c0:c0 + P], hb[:, c, :n], c == 0, c == 7,
                           reads=hks + [wk[c]], writes=[pk])
                    sc = 0.125 if jb < 4 else 1.0
                    S.op("act", lambda e: e.activation(out=ob[:, :n], in_=ps[:, :n], func=AF.Copy, scale=sc),
                         reads=[pk], writes=[ok])
                    dst = self.qT_d if jb < 4 else self.kT_d
                    dk = "qT" if jb < 4 else "kT"
                    S.dma("sp", dst[jb % 4, :, T0 * P:T0 * P + n], ob[:, :n], reads=[ok],
                          writes=[(dk, jb % 4, T0 + t) for t in range(nt)])

    def phase_L0_pool(self):
        S = self.S
        with S.scope():
            up = S.sb("up_all", [P, NT, 512], BF16)
            for q in range(0, NT, 2):
                S.dma("sp", up[:, q:q + 2, :], self.upool_d[q * P:(q + 2) * P, :].rearrange("(n p) f -> p n f", p=P),
                      reads=[("upool", q), ("upool", q + 1)], writes=[("up", q), ("up", q + 1)])
            bd = S.sb("bands", [P, 20, P], BF16)
            S.dma("pool", bd[:], self.bands, writes=["bands"])
            pw = S.sb("pool_w", [P, 4, P], BF16)
            S.dma("pool", pw[:], self.ev_pool_w.rearrange("g c o -> c g o"), writes=["pool_w"])
            psc = S.sb("pool_sc", [P, 4], F32)
            with self.nc.allow_non_contiguous_dma(reason="tiny"):
                S.dma("sp", psc[:], self.ev_pool_scale.rearrange("(g p) -> p g", p=P), writes=["pool_sc"])
            pb = [S.ps(f"pb{i}", [P, 4, P], F32) for i in range(2)]
            pc = [S.ps(f"pc{i}", [P, 4, P], F32) for i in range(2)]
            pm = [S.sb(f"pmx{i}", [P, 4, P], BF16) for i in range(2)]
            zp = [S.sb(f"zp{i}", [P, 4, P], BF16) for i in range(2)]
            it = 0
            for (T0, n) in ((0, 2), (2, 32)):
                for i in range(n):
                    T = T0 + i
                    b = it % 2
                    it += 1
                    for g in range(4):
                        srcs = []
                        if i > 0:
                            srcs.append((T - 1, 0))
                        cv = 3 if i == 0 else (4 if i == n - 1 else 2)
                        srcs.append((T, cv))
                        if i < n - 1:
                            srcs.append((T + 1, 1))
                        for si, (Ts, v) in enumerate(srcs):
                            mm(S, pb[b][:, g, :], up[:, Ts, g * P:(g + 1) * P], bd[:, g * 5 + v, :],
                               si == 0, si == len(srcs) - 1, reads=[("up", Ts), "bands"], writes=[("pb", b)])
                    S.op("dve", lambda e: e.tensor_copy(out=pm[b][:], in_=pb[b][:]), reads=[("pb", b)], writes=[("pmx", b)])
                    for g in range(4):
                        mm(S, pc[b][:, g, :], pw[:, g, :], pm[b][:, g, :], True, True,
                           reads=["pool_w", ("pmx", b)], writes=[("pc", b)])
                    S.op("dve", lambda e: e.tensor_tensor(out=zp[b][:], in0=pc[b][:],
                                                          in1=psc[:, :, None].broadcast_to([P, 4, P]), op=ALU.mult),
                         reads=[("pc", b), "pool_sc"], writes=[("zp", b)])
                    S.dma("sp", self.zT_d[0:4, :, T * P:(T + 1) * P].rearrange("c p t -> p c t"), zp[b][:],
                          reads=[("zp", b)], writes=[("zT", c, T) for c in range(4)])

    def phase_L0_attn(self):
        S = self.S
        with S.scope():
            kT = S.sb("kT_all", [P, 4, NTOK], BF16)
            qT = S.sb("qT_all", [P, 4, NTOK], BF16)
            va = S.sb("v_all", [P, NT, 512], BF16)
            for j in range(4):
                S.dma("sp", kT[:, j, :], self.kT_d[j], reads=[("kT", j, T) for T in range(NT)], writes=[("kTa", j)])
                S.dma("sp", qT[:, j, :], self.qT_d[j], reads=[("qT", j, T) for T in range(NT)], writes=[("qTa", j)])
            for q in range(0, NT, 2):
                S.dma("sp", va[:, q:q + 2, :], self.v_d[q * P:(q + 2) * P, :].rearrange("(n p) f -> p n f", p=P),
                      reads=[("v", q), ("v", q + 1)], writes=[("va", q), ("va", q + 1)])
            E = S.sb("Etab", [P, 96, P], BF16)
            with S.scope():
                rt = S.sb("rt", [P, 96, P], F32)
                mk = S.sb("mk", [P, 12, P], F32)
                S.dma("sp", rt[:], self.rpb_tab, writes=["rt"])
                S.dma("sp", mk[:], self.msk_tab, writes=["mk"])
                S.op("act", lambda e: e.activation(out=rt[:], in_=rt[:], func=AF.Exp), reads=["rt"], writes=["rt"])
                for h in range(8):
                    S.op("dve", lambda e: e.tensor_tensor(out=E[:, h * 12:(h + 1) * 12, :], in0=rt[:, h * 12:(h + 1) * 12, :],
                                                          in1=mk[:], op=ALU.mult), reads=["rt", "mk"], writes=["Etab"])
            pss = [[S.ps(f"pss{i}_{k}", [P, 512], F32) for k in range(2)] for i in range(2)]
            pso = [S.ps(f"pso{i}", [P, 2, P], F32) for i in range(2)]
            pex = [S.sb(f"pex{i}", [P, 7, P], BF16) for i in range(2)]
            pT = [S.sb(f"pT{i}", [P, 5, P], BF16) for i in range(2)]
            rc = [S.sb(f"rc{i}", [P, P], F32) for i in range(2)]
            zo = [S.sb(f"zo{i}", [P, P], BF16) for i in range(2)]
            it = 0
            izo = 0
            for T in range(NT):
                if T < 2:
                    chunks = [(0, None), (1, None)]
                else:
                    i = T - 2
                    if 2 <= i <= 29:
                        lat = [(T + d, v) for v, d in enumerate((-2, -1, 0, 1, 2))]
                    elif i == 0:
                        lat = [(T + d, 8 + d) for d in (0, 1, 2, 3)]
                    elif i == 1:
                        lat = [(T + d, 8 + d) for d in (-1, 0, 1, 2)]
                    elif i == 30:
                        lat = [(T + d, 8 + d) for d in (-2, -1, 0, 1)]
                    else:
                        lat = [(T + d, 8 + d) for d in (-3, -2, -1, 0)]
                    chunks = [(0, None), (1, None)] + lat
                nk = len(chunks)
                nlat = nk - 2
                for j in range(4):
                    zb = zo[izo % 2]; zk = ("zo", izo % 2)
                    izo += 1
                    for hh in range(2):
                        h = 2 * j + hh
                        pb_ = hh * 64
                        b = it % 2
                        it += 1
                        for ci, (Tk, v) in enumerate(chunks):
                            bank = pss[b][ci // 4]
                            mm(S, bank[:, (ci % 4) * P:(ci % 4 + 1) * P],
                               kT[pb_:pb_ + 64, j, Tk * P:(Tk + 1) * P], qT[pb_:pb_ + 64, j, T * P:(T + 1) * P],
                               True, True, reads=[("kTa", j), ("qTa", j)], writes=[("pss", b, ci // 4)])
                        n0 = min(nk, 4)
                        S.op("act", lambda e: e.activation(out=pex[b][:, 0:n0, :], in_=pss[b][0][:, 0:n0 * P].rearrange("p (c q) -> p c q", q=P), func=AF.Exp),
                             reads=[("pss", b, 0)], writes=[("pex", b)])
                        if nk > 4:
                            S.op("act", lambda e: e.activation(out=pex[b][:, 4:nk, :], in_=pss[b][1][:, 0:(nk - 4) * P].rearrange("p (c q) -> p c q", q=P), func=AF.Exp),
                                 reads=[("pss", b, 1)], writes=[("pex", b)])
                        if nlat > 0:
                            v0 = chunks[2][1]
                            S.op("dve", lambda e: e.tensor_tensor(out=pT[b][:, 0:nlat, :], in0=pex[b][:, 2:nk, :],
                                                                  in1=E[:, h * 12 + v0:h * 12 + v0 + nlat, :], op=ALU.mult),
                                 reads=[("pex", b), "Etab"], writes=[("pT", b)])
                        for ci, (Tk, v) in enumerate(chunks):
                            rhs = pex[b][:, ci, :] if v is None else pT[b][:, ci - 2, :]
                            rk = [("pex", b)] if v is None else [("pT", b)]
                            mm(S, pso[b][:, 0, :], va[:, Tk, j * P:(j + 1) * P], rhs, ci == 0, ci == nk - 1,
                               reads=[("va", Tk)] + rk, writes=[("pso", b)])
                        for ci, (Tk, v) in enumerate(chunks):
                            rhs = pex[b][:, ci, :] if v is None else pT[b][:, ci - 2, :]
                            rk = [("pex", b)] if v is None else [("pT", b)]
                            mm(S, pso[b][:, 1, :], self.ones_bf[:], rhs, ci == 0, ci == nk - 1,
                               reads=["ones_bf"] + rk, writes=[("pso", b)])
                        S.op("dve", lambda e: e.reciprocal(out=rc[b][pb_:pb_ + 64, :], in_=pso[b][pb_:pb_ + 64, 1, :]),
                             reads=[("pso", b)], writes=[("rc", b)])
                        S.op("dve", lambda e: e.tensor_tensor(out=zb[pb_:pb_ + 64, :], in0=pso[b][pb_:pb_ + 64, 0, :],
                                                              in1=rc[b][pb_:pb_ + 64, :], op=ALU.mult),
                             reads=[("pso", b), ("rc", b)], writes=[zk])
                    S.dma("sp", self.zT_d[4 + j, :, T * P:(T + 1) * P], zb[:], reads=[zk], writes=[("zT", 4 + j, T)])

    def phase_out_mlp(self, layer, V, w_out_ap, y_tile_fn, dst_fn):
        S = self.S
        with S.scope():
            wo = S.sb("wo", [P, 8, D], BF16)
            for c in range(8):
                S.dma("pool", wo[:, c, :], w_out_ap[c * P:(c + 1) * P, :], writes=[("wo", c)])
            w1 = S.sb("w1", [P, 8, 4 * D], BF16)
            w2 = S.sb("w2", [P, 32, D], BF16)
            for c in range(8):
                S.dma("pool", w1[:, c, :], self.mlp_w1[layer, c * P:(c + 1) * P, :], writes=[("w1", c)])
            for f in range(0, 32, 4):
                S.dma("pool", w2[:, f:f + 4, :], self.mlp_w2[layer, f * P:(f + 4) * P, :].rearrange("(n p) d -> p n d", p=P),
                      writes=[("w2", f + q) for q in range(4)])
            NB = self.make_norm_bufs("nm", nb=1)
            zt = [S.sb(f"zt{i}", [P, 8, P], BF16) for i in range(2)]
            xt = [S.sb(f"xo{i}", [P, D], F32) for i in range(2)]
            x1 = [S.sb(f"x1_{i}", [P, D], F32) for i in range(2)]
            tmp = [S.sb(f"tg{i}", [P, D], F32) for i in range(2)]
            sq = NB["sq"][0]
            stt = [S.sb(f"ost{i}", [P, 4], F32) for i in range(4)]
            hT = [S.sb(f"hm{i}", [P, 8, 256], BF16) for i in range(1)] * 2
            py = [[S.ps(f"py{t}_{hf}", [P, 512], F32) for hf in range(2)] for t in range(2)]
            pa = [S.ps(f"pa{i}", [P, 256], F32) for i in range(2)]
            r32 = [S.sb(f"r32_{i}", [P, 256], F32) for i in range(2)]
            aT = [S.sb(f"aT{i}", [P, 256], BF16) for i in range(2)]
            ist = 0

            def norm_gate_res(t, G, xin, xin_key, xout, xout_key):
                nonlocal ist
                st = stt[ist % 4]; sk = ("ost", ist % 4)
                ist += 1
                S.op("pool", lambda e: e.memset(st[:], 0.0), writes=[sk])
                for hf in range(2):
                    S.op("act", lambda e: e.activation(out=sq[:, hf * 512:(hf + 1) * 512], in_=py[t][hf][:], func=AF.Square,
                                                       accum_out=st[:, hf:hf + 1]), reads=[("py", t, hf)], writes=[("nm", "sq", 0), sk])
                S.op("dve", lambda e: e.tensor_tensor(out=st[:, 2:3], in0=st[:, 0:1], in1=st[:, 1:2], op=ALU.add),
                     reads=[sk], writes=[sk])
                S.op("act", lambda e: e.activation(out=st[:, 2:3], in_=st[:, 2:3], func=AF.Sqrt, scale=1.0 / D,
                                                   bias=self.eps_t[:, 0:1]), reads=[sk, "eps_t"], writes=[sk])
                S.op("dve", lambda e: e.reciprocal(out=st[:, 3:4], in_=st[:, 2:3]), reads=[sk], writes=[sk])
                tb = tmp[t]; tk = ("tg", t)
                for hf in range(2):
                    S.op("dve", lambda e: e.scalar_tensor_tensor(out=tb[:, hf * 512:(hf + 1) * 512], in0=py[t][hf][:],
                                                                 scalar=st[:, 3:4], in1=G[0][:, hf * 512:(hf + 1) * 512],
                                                                 op0=ALU.mult, op1=ALU.mult),
                         reads=[("py", t, hf), sk, G[1]], writes=[tk])
                S.op("dve", lambda e: e.tensor_tensor(out=xout, in0=tb[:], in1=xin, op=ALU.add),
                     reads=[tk, xin_key], writes=[xout_key])

            ia = 0
            loaded = {}

            def load_inputs(sidx):
                loaded[sidx] = True
                for t in range(2):
                    T = 2 * sidx + t
                    src, skey = self.x_src(layer, T)
                    S.dma("sp", xt[t][:], src, reads=[skey] if skey else [], writes=[("xo", t)])
                    S.dma("sp", zt[t][:], self.zT_d[:, :, T * P:(T + 1) * P].rearrange("c p t -> p c t"),
                          reads=[("zT", c, T) for c in range(8)], writes=[("zt", t)])

            for sidx in range(NT // 2):
                T0 = 2 * sidx
                s = 1 if T0 < 2 else 0
                if layer == 1 and s == 1:
                    continue
                hb = hT[0]; hk = ("hm", 0)
                if not loaded.get(sidx):
                    load_inputs(sidx)
                for t in range(2):
                    T = T0 + t
                    y_tile_fn(T, t, zt[t], ("zt", t), wo, py[t])
                    norm_gate_res(t, V[("G", 0, s)], xt[t][:], ("xo", t), x1[t][:], ("x1", t))
                    self.norm_to_hT(NB, x1[t][:], ("x1", t), V[("A", 1, s)], V[("B", 1, s)],
                                    hb[:, :, t * P:(t + 1) * P], (hk, t))
                nxt = sidx + 1
                if nxt < NT // 2 and not (layer == 1 and nxt == 0):
                    load_inputs(nxt)
                def mm1(f):
                    a = (ia + f) % 2
                    for c in range(8):
                        mm(S, pa[a][:], w1[:, c, f * P:(f + 1) * P], hb[:, c, :], c == 0, c == 7,
                           reads=[("w1", c), (hk, 0), (hk, 1)], writes=[("pa", a)])
                    S.op("act", lambda e: e.activation(out=r32[a][:], in_=pa[a][:], func=AF.Relu),
                         reads=[("pa", a)], writes=[("r32", a)])
                    S.op("dve", lambda e: e.tensor_tensor(out=aT[a][:], in0=r32[a][:], in1=r32[a][:], op=ALU.mult),
                         reads=[("r32", a)], writes=[("aT", a)])

                def mm2(f):
                    a = (ia + f) % 2
                    for t in range(2):
                        for hf in range(2):
                            mm(S, py[t][hf][:], aT[a][:, t * P:(t + 1) * P], w2[:, f, hf * 512:(hf + 1) * 512],
                               f == 0, f == 31, reads=[("aT", a), ("w2", f)], writes=[("py", t, hf)])
                mm1(0)
                for f in range(32):
                    if f + 1 < 32:
                        mm1(f + 1)
                    mm2(f)
                for t in range(2):
                    T = T0 + t
                    norm_gate_res(t, V[("G", 1, s)], x1[t][:], ("x1", t), tmp[t][:], ("tg", t))
                    dst, dkey = dst_fn(T)
                    S.dma("sp", dst, tmp[t][:], reads=[("tg", t)], writes=[dkey])

    def y_tile_L0(self, T, t, zt, zk, wo, py):
        S = self.S
        for hf in range(2):
            for c in range(8):
                mm(S, py[hf][:], zt[:, c, :], wo[:, c, hf * 512:(hf + 1) * 512], c == 0, c == 7,
                   reads=[zk, ("wo", c)], writes=[("py", t, hf)])


def build_program(stop_after=None, debug=()):
    nc = bass.Bass("TRN2", target_bir_lowering=False)
    Pg = Prog(nc, debug)
    S = Pg.S
    Pg.consts()
    Pg.phase_mod()
    final_keys = []
    with S.scope():
        V0 = Pg.load_layer_vecs(0)
        Pg.phase_L0_proj(V0)
        Pg.phase_L0_pool()
        Pg.phase_L0_attn()

        def dst0(T):
            if stop_after == "L0":
                if T < 2:
                    return Pg.x_d[T * P:(T + 1) * P, :], ("x_d", T)
                return Pg.out[(T - 2) * P:(T - 1) * P, :], ("out", T)
            return Pg.x_d[T * P:(T + 1) * P, :], ("x_d", T)
        Pg.phase_out_mlp(0, V0, Pg.ev_w_out, Pg.y_tile_L0, dst0)
    S.barrier()
    S.finish([])
    S.close()
    return nc, Pg


def host_inputs(inputs):
    f = lambda a: np.ascontiguousarray(np.asarray(a, dtype=np.float32))
    dr_idx, dc_full, mask = _attn_tables()
    rpb = f(inputs["ev_rpb"])[0]
    tab = rpb[:, dr_idx, dc_full[None, :, :]]
    tab = np.ascontiguousarray(tab.transpose(2, 0, 1, 3).reshape(128, 96, 128))
    msk = np.ascontiguousarray(mask.transpose(1, 0, 2))
    bands = np.ascontiguousarray(_pool_bands().transpose(2, 0, 1, 3).reshape(128, 20, 128))
    shared = {
        "ada_w": f(inputs["ada_w"]), "ada_b": f(inputs["ada_b"]), "norm_g": f(inputs["norm_g"]),
        "mlp_w1": f(inputs["mlp_w1"]), "mlp_w2": f(inputs["mlp_w2"]),
        "ev_w_in": f(inputs["ev_w_in"])[0], "ev_w_out": f(inputs["ev_w_out"])[0],
        "ev_pool_w": f(inputs["ev_pool_w"])[0], "ev_pool_scale": f(inputs["ev_pool_scale"])[0],
        "rpb_tab": tab, "msk_tab": msk, "bands": bands, "ident": np.eye(128, dtype=np.float32),
    }
    x = f(inputs["x"]); c = f(inputs["c"]); ctx = f(inputs["ctx"]); cc = f(inputs["c_ctx"])
    maps = []
    for b in range(x.shape[0]):
        cv = np.stack([c[b].reshape(8, 128).T, cc.reshape(8, 128).T], axis=-1)
        m = dict(shared)
        m.update({"x": x[b], "ctx": ctx[b], "cvec": np.ascontiguousarray(cv)})
        maps.append(m)
    return maps


_CACHE = {}


def kernel(**inputs):
    maps = host_inputs(inputs)
    if "nc" not in _CACHE:
        _CACHE["nc"] = build_program()
    nc, Pg = _CACHE["nc"]
    res = run_bass_kernel_spmd(nc, maps, core_ids=list(range(8)))
    return np.stack([np.asarray(r["out"]) for r in res.results], axis=0)

LWC = -0.6065306597126334
GN_EPS = 64e-5


def _scan_consts():
    s = np.arange(128)[:, None]
    t = np.arange(128)[None, :]
    tri = np.stack([(s <= t), (s >= t)]).astype(np.float32)
    strict = np.stack([(s < t), (s > t)]).astype(np.float32)
    mT = strict.transpose(0, 2, 1)
    m4 = np.concatenate([tri, strict, tri, mT], axis=2)
    lm = []
    for l in range(7):
        b = 1 << l
        lm.append(((s // (2 * b)) == (t // (2 * b))) & (((s // b) % 2) == 0) & (((t // b) % 2) == 1))
    lm = np.stack(lm).astype(np.float32)
    lmN = np.stack([lm, lm.transpose(0, 2, 1)]) + np.eye(128, dtype=np.float32)[None, None]
    return tri, m4, np.ascontiguousarray(lmN)


def _tt(S, eng, out, a, b, op, reads, writes):
    return S.op(eng, lambda e: e.tensor_tensor(out=out, in0=a, in1=b, op=op), reads=reads, writes=writes)


def _stt(S, eng, out, a, sc, b, op0, op1, reads, writes):
    return S.op("dve", lambda e: e.scalar_tensor_tensor(out=out, in0=a, scalar=sc, in1=b, op0=op0, op1=op1),
                reads=reads, writes=writes)


def _act(S, out, in_, func, reads, writes, **kw):
    return S.op("act", lambda e: e.activation(out=out, in_=in_, func=func, **kw), reads=reads, writes=writes)


def _h3(ap):
    return ap.rearrange("p (h k) -> p h k", k=64)


class Prog1(Prog):
    def __init__(self, nc, debug=()):
        super().__init__(nc, debug)
        dt = nc.dram_tensor
        I = lambda name, shape: dt(name, list(shape), F32, kind="ExternalInput").ap()
        self.rw_mu = I("rw_mu", [6, D])
        self.rw_wr = I("rw_wr", [D, D]); self.rw_wk = I("rw_wk", [D, D])
        self.rw_wv = I("rw_wv", [D, D]); self.rw_wo = I("rw_wo", [D, D])
        self.rw_w0 = I("rw_w0", [2, D]); self.rw_a0 = I("rw_a0", [2, D])
        self.w1cat = I("w1cat", [D, P]); self.a1cat = I("a1cat", [D, P]); self.rw_g1 = I("rw_g1", [D, P])
        self.w2cat = I("w2cat", [P, D]); self.a2cat = I("a2cat", [P, D]); self.rw_g2 = I("rw_g2", [P, D])
        self.rw_kk = I("rw_kk", [1, D]); self.rw_ka = I("rw_ka", [1, D]); self.rw_rk = I("rw_rk", [1, D])
        self.rw_lng = I("rw_lng", [1, D]); self.rw_lnb = I("rw_lnb", [1, D])
        self.tri_c = I("tri_c", [2, P, P]); self.m4_c = I("m4_c", [2, P, 512]); self.lmT_c = I("lmT_c", [2, 7, P, P])
        X = lambda name, shape, d=F32: (dt(name, list(shape), d, kind="ExternalOutput").ap() if name in debug
                                        else dt(name, list(shape), d).ap())
        self.hT_d = X("hT_d", [8, P, NTOK])
        self.featT_d = X("featT_d", [2, NT, P, 8 * 4 * P], BF16)
        self.vtok_d = X("vtok_d", [NTOK, D], BF16)
        self.bk_d = X("bk_d", [2, NT, P, 2 * D], BF16)
        self.gC_d = X("gC_d", [2, NT, P, 8])
        self.g_d = X("g_d", [NTOK, D])
        self.bonus_d = X("bonus_d", [NTOK, D])
        self.y_d = X("y_d", [2, NTOK, D])

    def phase_R0(self, V):
        S = self.S
        with S.scope():
            NB = self.make_norm_bufs("r0")
            xt = [S.sb(f"r0x{i}", [P, D], F32) for i in range(2)]
            ho = [S.sb(f"r0h{i}", [P, 8, P], F32) for i in range(2)]
            for T in range(NT):
                s = 1 if T < 2 else 0
                b = T % 2
                src, sk = self.x_src(1, T)
                S.dma("sp", xt[b][:], src, reads=[sk], writes=[("r0x", b)])
                self.norm_to_hT(NB, xt[b][:], ("r0x", b), V[("A", 0, s)], V[("B", 0, s)], ho[b][:], ("r0h", b))
                S.dma("sp", self.hT_d[:, :, T * P:(T + 1) * P].rearrange("c p t -> p c t"), ho[b][:],
                      reads=[("r0h", b)], writes=[("hT_d", T)])

    def phase_R1(self):
        S = self.S
        with S.scope():
            W = {}
            for nm, src in (("wr", self.rw_wr), ("wk", self.rw_wk), ("wv", self.rw_wv)):
                W[nm] = S.sb(nm, [P, 8, D], BF16)
                for c in range(0, 8, 4):
                    S.dma("pool", W[nm][:, c:c + 4, :], src[c * P:(c + 4) * P, :].rearrange("(c p) n -> p c n", p=P), writes=[nm])
            for nm, src in (("w1c", self.w1cat), ("a1c", self.a1cat), ("g1", self.rw_g1)):
                W[nm] = S.sb(nm, [P, 8, P], BF16)
                S.dma("pool", W[nm][:], src.rearrange("(c p) n -> p c n", p=P), writes=[nm])
            for nm, src in (("w2c", self.w2cat), ("a2c", self.a2cat), ("g2", self.rw_g2)):
                W[nm] = S.sb(nm, [P, D], BF16)
                S.dma("pool", W[nm][:], src, writes=[nm])
            R = {}
            for nm, src in (("kk_r", self.rw_kk), ("ka_r", self.rw_ka), ("rk_r", self.rw_rk),
                            ("w0_0", self.rw_w0[0:1, :]), ("w0_1", self.rw_w0[1:2, :]),
                            ("a0_0", self.rw_a0[0:1, :]), ("a0_1", self.rw_a0[1:2, :])):
                R[nm] = S.sb(nm, [P, D], F32)
                S.dma("sp", R[nm][:], src.broadcast_to([P, D]), writes=[nm])
            mu = S.sb("mu", [P, 6, 8], F32)
            with self.nc.allow_non_contiguous_dma(reason="tiny"):
                S.dma("sp", mu[:], self.rw_mu.rearrange("j (c p) -> p j c", p=P), writes=["mu"])
            tri = S.sb("tri", [P, 2, P], F32)
            S.dma("sp", tri[:], self.tri_c.rearrange("d s t -> s d t"), writes=["tri"])
            onef = S.sb("onef", [P, P], F32)
            S.op("dve", lambda e: e.memset(onef[:], 1.0), writes=["onef"])
            hbuf = S.sb("hbuf", [P, 8, P + 2], F32)
            xx = S.sb("xx", [P, 8, P], F32)
            mxt = S.sb("mxt", [P, 8, P], F32)
            mix = S.sb("mix", [P, 6, 8, P], BF16)
            hid = S.sb("hid", [P, 3, P], BF16)
            F = {n: S.sb(n, [P, D], F32) for n in ("r_sb", "k_sb", "v_sb", "kkn", "tA", "tB", "lw", "tC", "tD", "kd0", "kd1", "tE", "tF", "tG", "tH")}
            ob = [S.sb(f"ob{i}", [P, D], BF16) for i in range(4)]
            vb = S.sb("vb", [P, D], BF16)
            ft = S.sb("ft", [P, 8, 4, P], BF16)
            bkt = S.sb("bkt", [P, 2, D], BF16)
            st16 = S.sb("st16", [P, 64], F32)
            gcs = S.sb("gcs", [P, 8], F32)
            pA = [[S.ps(f"pA{i}_{h}", [P, 512], F32) for h in range(2)] for i in range(2)]
            pCl = [S.ps(f"pCl{h}", [P, 512], F32) for h in range(2)]
            pF = S.ps("pF", [P, 512], F32)
            pT = S.ps("pT", [P, 8, P], BF16)
            ipa = 0

            def proj(lhs_fn, rhs, rkey, K0=0, K=P, nchunks=8, lkeys=()):
                nonlocal ipa
                i = ipa % 2
                ipa += 1
                for hf in range(2):
                    for c in range(nchunks):
                        mm(S, pA[i][hf][:], lhs_fn(c), rhs(c, hf), c == 0, c == nchunks - 1,
                           reads=list(lkeys) + [rkey], writes=[("pA", i, hf)])
                return pA[i], [("pA", i, 0), ("pA", i, 1)]

            def evac2(fn_half):
                for hf in range(2):
                    fn_half(hf, slice(hf * 512, (hf + 1) * 512))

            for T in range(NT):
                seq_lo, seq_hi = (0, NCTX) if T < 2 else (NCTX, NTOK)
                t0 = T * P
                lo = max(t0 - 1, seq_lo); hi = min(t0 + P + 1, seq_hi)
                if lo > t0 - 1:
                    S.op("pool", lambda e: e.memset(hbuf[:, :, 0:1], 0.0), writes=["hbuf"])
                if hi < t0 + P + 1:
                    S.op("pool", lambda e: e.memset(hbuf[:, :, P + 1:P + 2], 0.0), writes=["hbuf"])
                S.dma("pool", hbuf[:, :, lo - (t0 - 1):hi - (t0 - 1)], self.hT_d[:, :, lo:hi].rearrange("c p t -> p c t"),
                      reads=[("hT_d", q) for q in range(max(T - 1, 0), min(T + 2, NT))], writes=["hbuf"])
                _tt(S, "dve", xx[:], hbuf[:, :, 0:P], hbuf[:, :, 2:P + 2], ALU.add, ["hbuf"], ["xx"])
                _stt(S, "dve", xx[:], xx[:], 0.5, hbuf[:, :, 1:P + 1], ALU.mult, ALU.subtract, ["xx", "hbuf"], ["xx"])
                for j in range(6):
                    _tt(S, "dve", mxt[:], xx[:], mu[:, j, :][:, :, None].broadcast_to([P, 8, P]), ALU.mult, ["xx", "mu"], ["mxt"])
                    _tt(S, "dve", mix[:, j, :, :], mxt[:], hbuf[:, :, 1:P + 1], ALU.add, ["mxt", "hbuf"], [("mix", j)])
                for hi_, (wn, mj, fn) in enumerate((("w1c", 1, AF.Tanh), ("a1c", 4, AF.Copy), ("g1", 5, AF.Sigmoid))):
                    for c in range(8):
                        mm(S, pF[:, 0:P], W[wn][:, c, :], mix[:, mj, c, :], c == 0, c == 7,
                           reads=[wn, ("mix", mj)], writes=["pF"])
                    _act(S, hid[:, hi_, :], pF[:, 0:P], fn, ["pF"], [("hid", hi_)])
                for nm, mj, wn in (("r_sb", 0, "wr"), ("k_sb", 2, "wk"), ("v_sb", 3, "wv")):
                    ps, pk = proj(lambda c: mix[:, mj, c, :], lambda c, hf: W[wn][:, c, hf * 512:(hf + 1) * 512], wn,
                                  lkeys=[("mix", mj)])
                    evac2(lambda hf, sl: _act(S, F[nm][:, sl], ps[hf][:], AF.Copy, [pk[hf]], [nm]))
                S.op("pool", lambda e: e.tensor_copy(out=vb[:], in_=F["v_sb"][:]), reads=["v_sb"], writes=["vb"])
                S.dma("sp", self.vtok_d[t0:t0 + P, :], vb[:], reads=["vb"], writes=[("vtok", T)])
                ps, pk = proj(lambda c: hid[:, 2, :], lambda c, hf: W["g2"][:, hf * 512:(hf + 1) * 512], "g2", nchunks=1,
                              lkeys=[("hid", 2)])
                evac2(lambda hf, sl: _act(S, F["tA"][:, sl], ps[hf][:], AF.Copy, [pk[hf]], ["tA"]))
                S.dma("sp", self.g_d[t0:t0 + P, :], F["tA"][:], reads=["tA"], writes=[("g_d", T)])
                _tt(S, "dve", F["tA"][:], F["k_sb"][:], R["kk_r"][:], ALU.mult, ["k_sb", "kk_r"], ["tA"])
                _tt(S, "pool", F["tB"][:], F["tA"][:], F["tA"][:], ALU.mult, ["tA"], ["tB"])
                S.op("dve", lambda e: e.tensor_reduce(out=st16[:, 0:16], in_=_h3(F["tB"][:]), axis=AX.X, op=ALU.add),
                     reads=["tB"], writes=["st16"])
                S.op("dve", lambda e: e.tensor_scalar(out=st16[:, 0:16], in0=st16[:, 0:16], scalar1=1e-24, scalar2=None, op0=ALU.max),
                     reads=["st16"], writes=["st16"])
                _act(S, st16[:, 0:16], st16[:, 0:16], AF.Sqrt, ["st16"], ["st16"])
                S.op("dve", lambda e: e.reciprocal(out=st16[:, 16:32], in_=st16[:, 0:16]), reads=["st16"], writes=["st16"])
                _tt(S, "dve", _h3(F["kkn"][:]), _h3(F["tA"][:]), st16[:, 16:32][:, :, None].broadcast_to([P, 16, 64]), ALU.mult,
                    ["tA", "st16"], ["kkn"])
                for d in range(2):
                    ps, pk = proj(lambda c: hid[d * 64:(d + 1) * 64, 0, :], lambda c, hf: W["w2c"][d * 64:(d + 1) * 64, hf * 512:(hf + 1) * 512],
                                  "w2c", nchunks=1, lkeys=[("hid", 0)])
                    evac2(lambda hf, sl: _tt(S, "dve", F["tB"][:, sl], ps[hf][:], R[f"w0_{d}"][:, sl], ALU.add, [pk[hf], f"w0_{d}"], ["tB"]))
                    _act(S, F["tB"][:], F["tB"][:], AF.Sigmoid, ["tB"], ["tB"])
                    _act(S, F["lw"][:], F["tB"][:], AF.Copy, ["tB"], ["lw"], scale=LWC)
                    ps, pk = proj(lambda c: hid[d * 64:(d + 1) * 64, 1, :], lambda c, hf: W["a2c"][d * 64:(d + 1) * 64, hf * 512:(hf + 1) * 512],
                                  "a2c", nchunks=1, lkeys=[("hid", 1)])
                    evac2(lambda hf, sl: _tt(S, "dve", F["tC"][:, sl], ps[hf][:], R[f"a0_{d}"][:, sl], ALU.add, [pk[hf], f"a0_{d}"], ["tC"]))
                    _act(S, F["tC"][:], F["tC"][:], AF.Sigmoid, ["tC"], ["tC"])
                    kd = F[f"kd{d}"]; kdk = f"kd{d}"
                    _stt(S, "dve", F["tD"][:], F["tC"][:], -1.0, R["ka_r"][:], ALU.add, ALU.mult, ["tC", "ka_r"], ["tD"])
                    _stt(S, "pool", kd[:], F["tD"][:], 1.0, F["k_sb"][:], ALU.add, ALU.mult, ["tD", "k_sb"], [kdk])
                    _tt(S, "pool", F["tC"][:], F["kkn"][:], F["tC"][:], ALU.mult, ["kkn", "tC"], ["tC"])
                    for hf in range(2):
                        mm(S, pCl[hf][:], tri[:, d, :], F["lw"][:, hf * 512:(hf + 1) * 512], True, True,
                           reads=["tri", "lw"], writes=[("pCl", hf)])
                    evac2(lambda hf, sl: _act(S, F["tE"][:, sl], pCl[hf][:], AF.Exp, [("pCl", hf)], ["tE"]))
                    evac2(lambda hf, sl: _act(S, F["tF"][:, sl], pCl[hf][:], AF.Exp, [("pCl", hf)], ["tF"], scale=-1.0))
                    for hf in range(2):
                        mm(S, pCl[hf][:], onef[:], F["lw"][:, hf * 512:(hf + 1) * 512], True, True,
                           reads=["onef", "lw"], writes=[("pCl", hf)])
                    evac2(lambda hf, sl: _act(S, F["tH"][:, sl], pCl[hf][:], AF.Exp, [("pCl", hf)], ["tH"]))
                    _act(S, F["tG"][:], F["lw"][:], AF.Exp, ["lw"], ["tG"], scale=-1.0)
                    _tt(S, "dve", F["tG"][:], F["tG"][:], F["tE"][:], ALU.mult, ["tG", "tE"], ["tG"])
                    _tt(S, "pool", F["tH"][:], F["tH"][:], F["tF"][:], ALU.mult, ["tH", "tF"], ["tH"])
                    for j in range(8):
                        mm(S, pF[:, 256 + j:257 + j], F["lw"][:, j * P:(j + 1) * P], onef[:, 0:1], True, True,
                           reads=["lw", "onef"], writes=["pF"])
                    _act(S, gcs[:], pF[:, 256:264], AF.Exp, ["pF"], ["gcs"])
                    S.dma("sp", self.gC_d[d, T], gcs[:], reads=["gcs"], writes=[("gC_d", d, T)])
                    _stt(S, "dve", ob[0][:], F["kkn"][:], -1.0, F["tG"][:], ALU.mult, ALU.mult, ["kkn", "tG"], [("ob", 0)])
                    _tt(S, "pool", ob[1][:], F["r_sb"][:], F["tE"][:], ALU.mult, ["r_sb", "tE"], [("ob", 1)])
                    _tt(S, "dve", ob[2][:], F["tC"][:], F["tF"][:], ALU.mult, ["tC", "tF"], [("ob", 2)])
                    _tt(S, "pool", ob[3][:], kd[:], F["tF"][:], ALU.mult, [kdk, "tF"], [("ob", 3)])
                    _tt(S, "dve", bkt[:, 0, :], F["tC"][:], F["tH"][:], ALU.mult, ["tC", "tH"], ["bkt"])
                    _tt(S, "pool", bkt[:, 1, :], kd[:], F["tH"][:], ALU.mult, [kdk, "tH"], ["bkt"])
                    S.dma("sp", self.bk_d[d, T], bkt[:].rearrange("p a n -> p (a n)"), reads=["bkt"], writes=[("bk_d", d, T)])
                    for q in range(4):
                        for c in range(8):
                            S.op("pe", lambda e: e.transpose(out=pT[:, c, :], in_=ob[q][:, c * P:(c + 1) * P], identity=self.idb[:]),
                                 reads=[("ob", q), "idb"], writes=["pT"], accum=(c > 0))
                        if q % 2 == 0:
                            _act(S, ft[:, :, q, :], pT[:], AF.Copy, ["pT"], ["ft"])
                        else:
                            S.op("dve", lambda e: e.tensor_copy(out=ft[:, :, q, :], in_=pT[:]), reads=["pT"], writes=["ft"])
                    S.dma("sp", self.featT_d[d, T], ft[:].rearrange("p j q t -> p (j q t)"), reads=["ft"], writes=[("featT_d", d, T)])
                _tt(S, "pool", F["tD"][:], F["kd0"][:], F["kd1"][:], ALU.add, ["kd0", "kd1"], ["tD"])
                _tt(S, "pool", F["tD"][:], F["tD"][:], F["r_sb"][:], ALU.mult, ["tD", "r_sb"], ["tD"])
                _tt(S, "pool", F["tD"][:], F["tD"][:], R["rk_r"][:], ALU.mult, ["tD", "rk_r"], ["tD"])
                S.op("dve", lambda e: e.tensor_reduce(out=st16[:, 32:48], in_=_h3(F["tD"][:]), axis=AX.X, op=ALU.add),
                     reads=["tD"], writes=["st16"])
                _tt(S, "dve", _h3(F["tD"][:]), _h3(F["v_sb"][:]), st16[:, 32:48][:, :, None].broadcast_to([P, 16, 64]), ALU.mult,
                    ["v_sb", "st16"], ["tD"])
                S.dma("sp", self.bonus_d[t0:t0 + P, :], F["tD"][:], reads=["tD"], writes=[("bonus_d", T)])

    def phase_R2(self):
        S = self.S
        with S.scope():
            m4 = S.sb("m4", [P, 2, 512], F32)
            lmN = S.sb("lmN", [P, 2, 7, P], F32)
            S.dma("sp", m4[:], self.m4_c.rearrange("d s n -> s d n"), writes=["m4"])
            S.dma("sp", lmN[:], self.lmT_c.rearrange("d l s n -> s d l n"), writes=["lmN"])
            idb = self.idb
            NG = 4
            I4 = S.sb("I4", [P, NG, P], BF16)
            for g in range(NG):
                S.op("pool", lambda e: e.tensor_copy(out=I4[:, g, :], in_=idb[:]), reads=["idb"], writes=["I4"])
            I4f = I4[:].rearrange("p g t -> p (g t)")
            ST32 = [S.sb(f"ST32_{d}", [P, 8, 64], F32) for d in range(2)]
            STb = [S.sb(f"STb_{d}", [P, 8, 64], BF16) for d in range(2)]
            for d in range(2):
                S.op("dve", lambda e: e.memset(ST32[d][:], 0.0), writes=[("ST32", d)])
                S.op("dve", lambda e: e.memset(STb[d][:], 0.0), writes=[("STb", d)])
            NBUF = 3
            Fb = [S.sb(f"Fb{i}", [P, 8, 4, P], BF16) for i in range(NBUF)]
            Vb = [S.sb(f"Vb{i}", [P, D], BF16) for i in range(NBUF)]
            BKb = [S.sb(f"BKb{i}", [P, 2, D], BF16) for i in range(NBUF)]
            gCb = [S.sb(f"gCb{i}", [P, 8], F32) for i in range(NBUF)]
            ysb = [S.sb(f"ysb{i}", [P, D], F32) for i in range(NBUF)]
            SL = []
            for sl in range(2):
                R_ = dict(
                    GM=S.sb(f"GM{sl}", [P, NG, 512], BF16),
                    X=[S.sb(f"X{sl}_{i}", [P, NG, P], BF16) for i in range(2)],
                    XT=[S.sb(f"XT{sl}_{i}", [P, NG, P], BF16) for i in range(2)],
                    T1s=S.sb(f"T1s{sl}", [P, NG, P], BF16),
                    Zq=S.sb(f"Zq{sl}", [P, NG, 64], BF16),
                    Pb=S.sb(f"Pb{sl}", [P, NG, 64], BF16),
                    bk=[S.ps(f"bk{sl}_{i}", [P, NG, P], F32) for i in range(3)],
                    bz=S.ps(f"bz{sl}", [P, 8, 64], F32),
                    sl=sl)
                SL.append(R_)

            items = []
            it = 0
            for d in range(2):
                order = list(range(NT)) if d == 0 else [1, 0] + list(range(NT - 1, 1, -1))
                for ci, T in enumerate(order):
                    for g0 in range(0, 16, NG):
                        items.append(dict(d=d, T=T, g0=g0, b=it % NBUF))
                    it += 1

            def heads_of(g0):
                return [(g, g0 + g, (g0 + g) // 2, ((g0 + g) % 2) * 64) for g in range(NG)]

            def load_chunk(w):
                d, T, b = w["d"], w["T"], w["b"]
                S.dma("sp", Fb[b][:].rearrange("p j q t -> p (j q t)"), self.featT_d[d, T], reads=[("featT_d", d, T)], writes=[("Fb", b)])
                S.dma("sp", Vb[b][:], self.vtok_d[T * P:(T + 1) * P, :], reads=[("vtok", T)], writes=[("Vb", b)])
                S.dma("sp", BKb[b][:].rearrange("p a n -> p (a n)"), self.bk_d[d, T], reads=[("bk_d", d, T)], writes=[("BKb", b)])
                S.dma("sp", gCb[b][:], self.gC_d[d, T], reads=[("gC_d", d, T)], writes=[("gCb", b)])

            def run_group(w, R_):
                d, T, b, g0, sl = w["d"], w["T"], w["b"], w["g0"], R_["sl"]
                if g0 == 0:
                    load_chunk(w)
                Fk, Vk, BKk, gk = ("Fb", b), ("Vb", b), ("BKb", b), ("gCb", b)
                GM, X, XT, T1s, Zq, Pb, bk, bz = (R_[n] for n in ("GM", "X", "XT", "T1s", "Zq", "Pb", "bk", "bz"))
                K = lambda n, *a: (n, sl) + a
                hs = heads_of(g0)
                F_ = Fb[b]
                st32, stb = ST32[d], STb[d]
                for (g, h, j, pb_) in hs:
                    bank = bk[g % 3]; bkk = K("bk", g % 3)
                    bv = bank[:].rearrange("p g t -> p (g t)")
                    AR = F_[pb_:pb_ + 64, j, 0:2, :].rearrange("p q t -> p (q t)")
                    mm(S, bv[:, 0:128], F_[pb_:pb_ + 64, j, 2, :], F_[pb_:pb_ + 64, j, 1, :], True, True, reads=[Fk], writes=[bkk])
                    mm(S, bv[:, 128:384], F_[pb_:pb_ + 64, j, 3, :], AR, True, True, reads=[Fk], writes=[bkk])
                    mm(S, bv[:, 384:512], F_[pb_:pb_ + 64, j, 0, :], F_[pb_:pb_ + 64, j, 2, :], True, True, reads=[Fk], writes=[bkk])
                    _tt(S, "dve", GM[:, g, :], bv, m4[:, d, :], ALU.mult, ["m4"], [bkk, K("GM")])
                yield
                for (g, h, j, pb_) in hs:
                    mm(S, bz[:, g, :], F_[pb_:pb_ + 64, j, 0, :], stb[pb_:pb_ + 64, j, :], True, False, reads=[Fk, ("STb", d)], writes=[K("bz")])
                    mm(S, bz[:, g, :], GM[:, g, 128:256], Vb[b][:, h * 64:(h + 1) * 64], False, True, reads=[K("GM"), Vk], writes=[K("bz")])
                _act(S, Zq[:], bz[:, 0:NG, :], AF.Copy, [], [K("bz"), K("Zq")])
                yield
                xi = 0
                mm(S, bk[0][:].rearrange("p g t -> p (g t)"), idb[:], I4f, True, False, reads=["idb", "I4"], writes=[K("bk", 0)])
                for (g, h, j, pb_) in hs:
                    mm(S, bk[0][:, g, :], GM[:, g, 384:512], idb[:], False, True, reads=[K("GM"), "idb"], writes=[K("bk", 0)])
                _tt(S, "dve", X[xi][:], bk[0][:], lmN[:, d, 0:1, :].broadcast_to([P, NG, P]), ALU.mult, ["lmN"], [K("bk", 0), K("X", xi)])
                yield
                for (g, h, j, pb_) in hs:
                    mm(S, bk[2][:, g, :], X[xi][:, g, :], idb[:], True, True, reads=[K("X", xi), "idb"], writes=[K("bk", 2)])
                _act(S, XT[xi][:], bk[2][:], AF.Copy, [], [K("bk", 2), K("XT", xi)])
                yield
                for l in range(1, 7):
                    mm(S, bk[0][:].rearrange("p g t -> p (g t)"), idb[:], I4f, True, False, reads=["idb", "I4"], writes=[K("bk", 0)])
                    for (g, h, j, pb_) in hs:
                        mm(S, bk[0][:, g, :], GM[:, g, 384:512], X[xi][:, g, :], False, True, reads=[K("GM"), K("X", xi)], writes=[K("bk", 0)])
                    _tt(S, "dve", T1s[:], bk[0][:], lmN[:, d, l:l + 1, :].broadcast_to([P, NG, P]), ALU.mult, ["lmN"], [K("bk", 0), K("T1s")])
                    yield
                    for (g, h, j, pb_) in hs:
                        mm(S, bk[1][:, g, :], XT[xi][:, g, :], T1s[:, g, :], True, True, reads=[K("XT", xi), K("T1s")], writes=[K("bk", 1)])
                    if l < 6:
                        for (g, h, j, pb_) in hs:
                            mm(S, bk[2][:, g, :], T1s[:, g, :], XT[xi][:, g, :], True, True, reads=[K("XT", xi), K("T1s")], writes=[K("bk", 2)])
                    _act(S, X[1 - xi][:], bk[1][:], AF.Copy, [], [K("bk", 1), K("X", 1 - xi)])
                    if l < 6:
                        if l % 3 != 0:
                            _act(S, XT[1 - xi][:], bk[2][:], AF.Copy, [], [K("bk", 2), K("XT", 1 - xi)])
                        else:
                            S.op("dve", lambda e: e.tensor_copy(out=XT[1 - xi][:], in_=bk[2][:]), reads=[], writes=[K("bk", 2), K("XT", 1 - xi)])
                    xi = 1 - xi
                    yield
                for (g, h, j, pb_) in hs:
                    mm(S, bz[:, g, :], X[xi][:, g, :], Zq[:, g, :], True, True, reads=[K("X", xi), K("Zq")], writes=[K("bz")])
                S.op("dve", lambda e: e.tensor_copy(out=Pb[:], in_=bz[:, 0:NG, :]), reads=[], writes=[K("bz"), K("Pb")])
                yield
                for (g, h, j, pb_) in hs:
                    yo = bz[:, g, :]
                    mm(S, yo, GM[:, g, 0:128], Pb[:, g, :], True, False, reads=[K("GM"), K("Pb")], writes=[K("bz")])
                    mm(S, yo, GM[:, g, 256:384], Vb[b][:, h * 64:(h + 1) * 64], False, False, reads=[K("GM"), Vk], writes=[K("bz")])
                    mm(S, yo, F_[pb_:pb_ + 64, j, 1, :], stb[pb_:pb_ + 64, j, :], False, True, reads=[Fk, ("STb", d)], writes=[K("bz")])
                for (g, h, j, pb_) in hs:
                    mm(S, bz[:, 4 + g, :], BKb[b][:, 0, j * P:(j + 1) * P], Pb[:, g, :], True, False, reads=[BKk, K("Pb")], writes=[K("bz")])
                    mm(S, bz[:, 4 + g, :], BKb[b][:, 1, j * P:(j + 1) * P], Vb[b][:, h * 64:(h + 1) * 64], False, True,
                       reads=[BKk, Vk], writes=[K("bz")])
                _act(S, ysb[b][:, g0 * 64:(g0 + NG) * 64].rearrange("p (g v) -> p g v", v=64), bz[:, 0:NG, :], AF.Copy, [], [K("bz"), ("ysb", b)])
                for (g, h, j, pb_) in hs:
                    _stt(S, "dve", st32[pb_:pb_ + 64, j, :], st32[pb_:pb_ + 64, j, :], gCb[b][pb_:pb_ + 64, j:j + 1],
                         bz[pb_:pb_ + 64, 4 + g, :], ALU.mult, ALU.add, [gk], [("ST32", d), K("bz")])
                S.op("pool", lambda e: e.tensor_copy(out=stb[:, g0 // 2:g0 // 2 + 2, :], in_=st32[:, g0 // 2:g0 // 2 + 2, :]),
                     reads=[("ST32", d)], writes=[("STb", d)])
                if g0 + NG == 16:
                    S.dma("pool", self.y_d[d, T * P:(T + 1) * P, :], ysb[b][:], reads=[("ysb", b)], writes=[("y_d", d, T)])
                yield

            nxt = 0
            active = [None, None]
            while True:
                progressed = False
                for sl in range(2):
                    if active[sl] is None and nxt < len(items):
                        active[sl] = run_group(items[nxt], SL[sl])
                        nxt += 1
                    if active[sl] is not None:
                        progressed = True
                        try:
                            next(active[sl])
                        except StopIteration:
                            active[sl] = None
                if not progressed:
                    break

    def phase_R3(self):
        S = self.S
        with S.scope():
            R = {}
            for nm, src in (("lng_r", self.rw_lng), ("lnb_r", self.rw_lnb)):
                R[nm] = S.sb(nm, [P, D], F32)
                S.dma("sp", R[nm][:], src.broadcast_to([P, D]), writes=[nm])
            B = [{n: S.sb(f"{n}{i}", [P, D], F32) for n in ("yf", "yb", "gg", "bo")} for i in range(2)]
            zb = [S.sb(f"zb{i}", [P, D], BF16) for i in range(2)]
            zt = [S.sb(f"zt3_{i}", [P, 8, P], BF16) for i in range(2)]
            st = [S.sb(f"st3_{i}", [P, 64], F32) for i in range(2)]
            pT = [S.ps(f"pT3_{i}", [P, 8, P], BF16) for i in range(2)]
            for T in range(2, NT):
                b = T % 2
                Bf = B[b]
                k = lambda n: (n, b)
                t0 = T * P
                S.dma("pool", Bf["yf"][:], self.y_d[0, t0:t0 + P, :], reads=[("y_d", 0, T)], writes=[k("yf")])
                S.dma("pool", Bf["yb"][:], self.y_d[1, t0:t0 + P, :], reads=[("y_d", 1, T)], writes=[k("yb")])
                S.dma("pool", Bf["gg"][:], self.g_d[t0:t0 + P, :], reads=[("g_d", T)], writes=[k("gg")])
                S.dma("pool", Bf["bo"][:], self.bonus_d[t0:t0 + P, :], reads=[("bonus_d", T)], writes=[k("bo")])
                y = Bf["yf"]; t2 = Bf["yb"]
                _tt(S, "dve", y[:], y[:], t2[:], ALU.add, [k("yf"), k("yb")], [k("yf")])
                S.op("dve", lambda e: e.tensor_reduce(out=st[b][:, 0:16], in_=_h3(y[:]), axis=AX.X, op=ALU.add), reads=[k("yf")], writes=[k("st")])
                S.op("dve", lambda e: e.tensor_scalar(out=st[b][:, 0:16], in0=st[b][:, 0:16], scalar1=-1.0 / 64, scalar2=None, op0=ALU.mult),
                     reads=[k("st")], writes=[k("st")])
                _tt(S, "dve", _h3(y[:]), _h3(y[:]), st[b][:, 0:16][:, :, None].broadcast_to([P, 16, 64]), ALU.add, [k("yf"), k("st")], [k("yf")])
                _tt(S, "pool", t2[:], y[:], y[:], ALU.mult, [k("yf")], [k("yb")])
                S.op("dve", lambda e: e.tensor_reduce(out=st[b][:, 16:32], in_=_h3(t2[:]), axis=AX.X, op=ALU.add), reads=[k("yb")], writes=[k("st")])
                S.op("dve", lambda e: e.tensor_scalar(out=st[b][:, 16:32], in0=st[b][:, 16:32], scalar1=1.0 / 64, scalar2=GN_EPS, op0=ALU.mult, op1=ALU.add),
                     reads=[k("st")], writes=[k("st")])
                _act(S, st[b][:, 16:32], st[b][:, 16:32], AF.Sqrt, [k("st")], [k("st")])
                S.op("dve", lambda e: e.reciprocal(out=st[b][:, 32:48], in_=st[b][:, 16:32]), reads=[k("st")], writes=[k("st")])
                _tt(S, "dve", _h3(y[:]), _h3(y[:]), st[b][:, 32:48][:, :, None].broadcast_to([P, 16, 64]), ALU.mult, [k("yf"), k("st")], [k("yf")])
                _tt(S, "pool", y[:], y[:], R["lng_r"][:], ALU.mult, [k("yf"), "lng_r"], [k("yf")])
                _tt(S, "pool", y[:], y[:], R["lnb_r"][:], ALU.add, [k("yf"), "lnb_r"], [k("yf")])
                _tt(S, "dve", y[:], y[:], Bf["bo"][:], ALU.add, [k("yf"), k("bo")], [k("yf")])
                _tt(S, "dve", zb[b][:], y[:], Bf["gg"][:], ALU.mult, [k("yf"), k("gg")], [k("zb")])
                for c in range(8):
                    S.op("pe", lambda e: e.transpose(out=pT[b][:, c, :], in_=zb[b][:, c * P:(c + 1) * P], identity=self.idb[:]),
                         reads=[k("zb"), "idb"], writes=[k("pT3")], accum=(c > 0))
                _act(S, zt[b][:], pT[b][:], AF.Copy, [k("pT3")], [k("zt3")])
                S.dma("sp", self.zT_d[:, :, t0:t0 + P].rearrange("c p t -> p c t"), zt[b][:], reads=[k("zt3")],
                      writes=[("zT", c, T) for c in range(8)])


def build_program(stop_after=None, debug=(), phases="M0ABCD1abcde"):
    nc = bass.Bass("TRN2", target_bir_lowering=False)
    Pg = Prog1(nc, debug)
    S = Pg.S
    Pg.consts()
    if "M" in phases:
        Pg.phase_mod()
    if "0" in phases:
      with S.scope():
        V0 = Pg.load_layer_vecs(0)
        if "A" in phases: Pg.phase_L0_proj(V0)
        if "B" in phases: Pg.phase_L0_pool()
        if "C" in phases: Pg.phase_L0_attn()
        if "D" in phases: Pg.phase_out_mlp(0, V0, Pg.ev_w_out, Pg.y_tile_L0, lambda T: (Pg.x_d[T * P:(T + 1) * P, :], ("x_d", T)))
    if "1" in phases:
      with S.scope():
        V1 = Pg.load_layer_vecs(1)
        if "a" in phases: Pg.phase_R0(V1)
        if "b" in phases: Pg.phase_R1()
        if "c" in phases: Pg.phase_R2()
        if "d" in phases: Pg.phase_R3()
        if "e" in phases: Pg.phase_out_mlp(1, V1, Pg.rw_wo, Pg.y_tile_L0, lambda T: (Pg.out[(T - 2) * P:(T - 1) * P, :], ("out", T)))
    S.barrier()
    S.finish([])
    S.close()
    return nc, Pg


_host_inputs0 = host_inputs


def host_inputs(inputs):
    maps = _host_inputs0(inputs)
    f = lambda a: np.ascontiguousarray(np.asarray(a, dtype=np.float32))
    tri, m4, lmT = _scan_consts()
    sh = {
        "rw_mu": f(inputs["rw_mu"])[0], "rw_wr": f(inputs["rw_wr"])[0], "rw_wk": f(inputs["rw_wk"])[0],
        "rw_wv": f(inputs["rw_wv"])[0], "rw_wo": f(inputs["rw_wo"])[0],
        "rw_w0": f(inputs["rw_w0"])[0], "rw_a0": f(inputs["rw_a0"])[0],
        "w1cat": f(np.concatenate([inputs["rw_w1"][0, 0], inputs["rw_w1"][0, 1]], axis=1)),
        "a1cat": f(np.concatenate([inputs["rw_a1"][0, 0], inputs["rw_a1"][0, 1]], axis=1)),
        "rw_g1": f(inputs["rw_g1"])[0],
        "w2cat": f(np.asarray(inputs["rw_w2"])[0].reshape(128, 1024)), "a2cat": f(np.asarray(inputs["rw_a2"])[0].reshape(128, 1024)),
        "rw_g2": f(inputs["rw_g2"])[0],
        "rw_kk": f(inputs["rw_kk"]).reshape(1, 1024), "rw_ka": f(inputs["rw_ka"]).reshape(1, 1024),
        "rw_rk": f(inputs["rw_rk"]).reshape(1, 1024), "rw_lng": f(inputs["rw_lng"]).reshape(1, 1024),
        "rw_lnb": f(inputs["rw_lnb"]).reshape(1, 1024),
        "tri_c": f(tri), "m4_c": f(m4), "lmT_c": f(lmT),
    }
    for m in maps:
        m.update(sh)
    return maps
```

```python
import contextlib
import numpy as np
import concourse.bass as bass
import concourse.mybir as mybir

F32 = mybir.dt.float32
BF16 = mybir.dt.bfloat16
AF = mybir.ActivationFunctionType
ALU = mybir.AluOpType
AX = mybir.AxisListType

SEM_LIMIT = 10000


class _Ctr:
    def __init__(self, S, name, step):
        self.S = S
        self.name = name
        self.step = step
        self.gen = 0
        self.sem = S._newsem(f"{name}_0")
        self.val = 0

    def next_event(self):
        if self.val + self.step > SEM_LIMIT:
            self.gen += 1
            self.sem = self.S._newsem(f"{self.name}_{self.gen}")
            self.val = 0
        self.val += self.step
        return (self.sem, self.val)


class _PsView:
    def __init__(self, t, shape):
        self.t = t
        self.n1 = shape[1]

    def __getitem__(self, key):
        if not isinstance(key, tuple):
            key = (key,)
        key = list(key)
        if len(key) < 2:
            key.append(slice(None))
        k1 = key[1]
        if isinstance(k1, slice):
            start, stop, step = k1.indices(self.n1)
            key[1] = slice(start, stop, step)
        return self.t[tuple(key)]


class _Eng:
    def __init__(self, S, name, obj):
        self.name = name
        self.obj = obj
        self.ctr = _Ctr(S, "s_" + name, 1)
        self.seen = {}
        self.n_issued = 0
        self.last_ins = None
        self.last_has_inc = False
        self.inc_idx = []
        self.inc_ev = []


class LazyEv:
    __slots__ = ("eng", "idx")

    def __init__(self, eng, idx):
        self.eng = eng
        self.idx = idx


class _Res:
    __slots__ = ("w", "r")

    def __init__(self):
        self.w = None
        self.r = {}


class Sched:
    def __init__(self, nc, n_dma_slots=8):
        self.nc = nc
        self.stack = contextlib.ExitStack()
        self.scopes = [self.stack]
        self.res = {}
        self.engs = {
            "pe": _Eng(self, "pe", nc.tensor),
            "act": _Eng(self, "act", nc.scalar),
            "dve": _Eng(self, "dve", nc.vector),
            "pool": _Eng(self, "pool", nc.gpsimd),
            "sp": _Eng(self, "sp", nc.sync),
        }
        self.dma_slots = {}
        for q in ("sp", "pool", "act"):
            self.dma_slots[q] = [_Ctr(self, f"d_{q}{i}", 16) for i in range(n_dma_slots)]
        self.dma_rr = {"sp": 0, "pool": 0, "act": 0}
        self.n_inst = 0
        self.uid = 0
        self.pending = None
        self.lazy_engines = ()

    def _newsem(self, name):
        return self.stack.enter_context(self.nc.semaphore(name))

    def sb(self, name, shape, dt):
        self.uid += 1
        return self.scopes[-1].enter_context(self.nc.sbuf_tensor(f"sb{self.uid}_{name}", list(shape), dt))

    def ps(self, name, shape, dt=F32):
        self.uid += 1
        esz = 4 if dt == F32 else 2
        per_part = esz
        for d_ in shape[1:]:
            per_part *= d_
        assert per_part <= 2048, (name, shape)
        shape = list(shape)
        if per_part < 2048:
            rest = per_part // shape[1]
            assert 2048 % rest == 0, (name, shape)
            full = [shape[0], 2048 // rest] + shape[2:]
            t = self.scopes[-1].enter_context(self.nc.psum_tensor(f"ps{self.uid}_{name}", full, dt))
            return _PsView(t, shape)
        return self.scopes[-1].enter_context(self.nc.psum_tensor(f"ps{self.uid}_{name}", shape, dt))

    @contextlib.contextmanager
    def scope(self):
        st = contextlib.ExitStack()
        self.scopes.append(st)
        try:
            yield
        finally:
            self.barrier()
            self.scopes.pop()
            st.close()

    def barrier(self):
        evs = []
        for e in self.engs.values():
            if e.n_issued > 0:
                evs.append(self._resolve(LazyEv(e, e.n_issued - 1)))
        for q in self.dma_slots:
            for ctr in self.dma_slots[q]:
                if ctr.val > 0:
                    evs.append((ctr.sem, ctr.val))
        for e in self.engs.values():
            for ev in evs:
                self._wait(e, ev)

    def _r(self, key):
        r = self.res.get(key)
        if r is None:
            r = self.res[key] = _Res()
        return r

    def _resolve(self, ev):
        if not isinstance(ev, LazyEv):
            return ev
        import bisect
        e = ev.eng
        k = bisect.bisect_left(e.inc_idx, ev.idx)
        if k < len(e.inc_idx):
            return e.inc_ev[k]
        assert e.last_ins is not None and not e.last_has_inc and e.n_issued - 1 >= ev.idx
        sv = e.ctr.next_event()
        e.last_ins.then_inc(sv[0], 1)
        e.last_has_inc = True
        e.inc_idx.append(e.n_issued - 1)
        e.inc_ev.append(sv)
        return sv

    def _wait(self, eng, ev):
        if ev is None:
            return
        if isinstance(ev, LazyEv) and ev.eng is eng and eng.name == "pe":
            return
        sem, val = self._resolve(ev)
        k = id(sem)
        if eng.seen.get(k, 0) >= val:
            return
        if self.pending is not None:
            cur = self.pending.get(k)
            if cur is None or cur[1] < val:
                self.pending[k] = (sem, val)
            return
        eng.obj.wait_ge(sem, val)
        eng.seen[k] = val

    def _flush(self, eng):
        pend = list(self.pending.values())
        self.pending = None
        for (sem, val) in pend[:-1]:
            eng.obj.wait_ge(sem, val)
            eng.seen[id(sem)] = val
        if pend:
            sem, val = pend[-1]
            eng.seen[id(sem)] = val
            return (sem, val)
        return None

    def _deps(self, eng, reads, writes, skip_same_eng_write=False):
        for key in reads:
            r = self._r(key)
            self._wait(eng, r.w)
        inorder = eng.name in ("act", "dve")
        for key in writes:
            r = self._r(key)
            if not ((skip_same_eng_write or inorder) and isinstance(r.w, LazyEv) and r.w.eng is eng):
                self._wait(eng, r.w)
            for ev in r.r.values():
                if inorder and isinstance(ev, LazyEv) and ev.eng is eng:
                    continue
                self._wait(eng, ev)

    def _commit(self, ev, reads, writes):
        rk = ev.eng.name if isinstance(ev, LazyEv) else id(ev[0])
        for key in reads:
            self._r(key).r[rk] = ev
        for key in writes:
            r = self._r(key)
            r.w = ev
            r.r = {}

    def op(self, engname, fn, reads=(), writes=(), accum=False):
        eng = self.engs[engname]
        self.pending = {}
        self._deps(eng, reads, writes, skip_same_eng_write=accum)
        last = self._flush(eng)
        ins = fn(eng.obj)
        if last is not None:
            ins._wait_ge(last[0], last[1])
        eng.last_ins = ins
        eng.last_has_inc = False
        ev = LazyEv(eng, eng.n_issued)
        eng.n_issued += 1
        if engname not in self.lazy_engines:
            self._resolve(ev)
        self._commit(ev, reads, writes)
        self.n_inst += 1
        return ev

    def dma(self, q, out, in_, reads=(), writes=(), **kw):
        eng = self.engs[q]
        slots = self.dma_slots[q]
        i = self.dma_rr[q]
        self.dma_rr[q] = (i + 1) % len(slots)
        ctr = slots[i]
        self.pending = {}
        if ctr.val > 0:
            self._wait(eng, (ctr.sem, ctr.val))
        self._deps(eng, reads, writes)
        last = self._flush(eng)
        ev = ctr.next_event()
        ins = eng.obj.dma_start(out=out, in_=in_, **kw)
        if last is not None:
            ins._wait_ge(last[0], last[1])
        ins.then_inc(ev[0], 16)
        self._commit(ev, reads, writes)
        self.n_inst += 1
        return ev

    def finish(self, final_keys):
        eng = self.engs["sp"]
        for key in final_keys:
            r = self._r(key)
            self._wait(eng, r.w)
        for q in self.dma_slots:
            for ctr in self.dma_slots[q]:
                if ctr.val > 0:
                    self._wait(eng, (ctr.sem, ctr.val))

    def close(self):
        self.stack.close()

from concourse.bass_utils import run_bass_kernel_spmd

D = 1024
NCTX = 256
NLAT = 4096
NTOK = NCTX + NLAT
NT = NTOK // 128
EPS = 1e-6
P = 128


def _pool_bands():
    L = 1024
    out = np.zeros((4, 5, 128, 128), np.float32)
    for g, w in enumerate((2, 4, 8, 16)):
        def full(L):
            t = np.arange(L)
            lo = np.clip(t - w // 2, 0, L)
            hi = np.clip(t + w // 2, 0, L)
            s = np.arange(L)[:, None]
            m = ((s >= lo[None, :]) & (s < hi[None, :])).astype(np.float64) / (hi - lo)[None, :]
            m -= np.eye(L)
            return m
        m = full(L)
        out[g, 0] = m[3 * 128:4 * 128, 4 * 128:5 * 128]
        out[g, 1] = m[5 * 128:6 * 128, 4 * 128:5 * 128]
        out[g, 2] = m[4 * 128:5 * 128, 4 * 128:5 * 128]
        out[g, 3] = m[0:128, 0:128]
        out[g, 4] = m[L - 128:, L - 128:]
    return out


_VARS = [(-2, "pm"), (-1, "f"), (0, "f"), (1, "f"), (2, "pp")] + [(d, "f") for d in range(-3, 4)]


def _attn_tables():
    kc = np.arange(64)
    qc = np.arange(64)
    c_start = np.clip(qc - 8, 0, 48)
    col_ok = (kc[:, None] >= c_start[None, :]) & (kc[:, None] < c_start[None, :] + 16)
    dc_idx = np.clip(kc[:, None] - qc[None, :], -15, 15) + 15
    dr_idx = np.zeros((12, 128, 128), np.int64)
    dc_full = np.zeros((128, 128), np.int64)
    mask = np.zeros((12, 128, 128), np.float32)
    for a in range(2):
        for b in range(2):
            dc_full[a * 64:(a + 1) * 64, b * 64:(b + 1) * 64] = dc_idx
    for v, (dl, kind) in enumerate(_VARS):
        for a in range(2):
            for b in range(2):
                dr = 2 * dl + a - b + 7
                vis = True
                if kind == "pm":
                    vis = not (a == 0 and b == 1)
                elif kind == "pp":
                    vis = (a == 0 and b == 1)
                dr_idx[v, a * 64:(a + 1) * 64, b * 64:(b + 1) * 64] = min(max(dr, 0), 14)
                if vis and 0 <= dr <= 14:
                    mask[v, a * 64:(a + 1) * 64, b * 64:(b + 1) * 64] = col_ok
    return dr_idx, dc_full, mask


def mm(S, out, lhsT, rhs, start, stop, reads, writes):
    return S.op("pe", lambda e: e.matmul(out, lhsT=lhsT, rhs=rhs, start=start, stop=stop),
                reads=reads, writes=writes, accum=not start)


class Prog:
    def __init__(self, nc, debug=()):
        self.nc = nc
        self.S = Sched(nc)
        self.debug = debug
        self.dbg_out = {}
        dt = nc.dram_tensor
        I = lambda name, shape: dt(name, list(shape), F32, kind="ExternalInput").ap()
        self.x_in = I("x", [NLAT, D])
        self.ctx_in = I("ctx", [NCTX, D])
        self.cvec = I("cvec", [P, 8, 2])
        self.ada_w = I("ada_w", [2, D, 6 * D])
        self.ada_b = I("ada_b", [2, 6 * D])
        self.norm_g = I("norm_g", [2, 4, D])
        self.mlp_w1 = I("mlp_w1", [2, D, 4 * D])
        self.mlp_w2 = I("mlp_w2", [2, 4 * D, D])
        self.ev_w_in = I("ev_w_in", [D, 2 * D])
        self.ev_w_out = I("ev_w_out", [D, D])
        self.ev_pool_w = I("ev_pool_w", [4, P, P])
        self.ev_pool_scale = I("ev_pool_scale", [512])
        self.rpb_tab = I("rpb_tab", [P, 8 * 12, P])
        self.msk_tab = I("msk_tab", [P, 12, P])
        self.bands = I("bands", [P, 20, P])
        self.ident = I("ident", [P, P])
        self.out = dt("out", [NLAT, D], F32, kind="ExternalOutput").ap()
        X = lambda name, shape, d=F32: (dt(name, list(shape), d, kind="ExternalOutput").ap() if name in debug
                                        else dt(name, list(shape), d).ap())
        self.modd = X("modd", [2, 2, 6 * D])
        self.x_d = X("x_d", [NTOK, D])
        self.upool_d = X("upool_d", [NTOK, 512], BF16)
        self.v_d = X("v_d", [NTOK, 512], BF16)
        self.qT_d = X("qT_d", [4, P, NTOK], BF16)
        self.kT_d = X("kT_d", [4, P, NTOK], BF16)
        self.zT_d = X("zT_d", [8, P, NTOK], BF16)

    def dbg(self, name, shape, dtp=F32):
        t = self.nc.dram_tensor("dbg_" + name, list(shape), dtp, kind="ExternalOutput").ap()
        self.dbg_out[name] = t
        return t

    def consts(self):
        S = self.S
        self.idb = S.sb("idb", [P, P], BF16)
        S.dma("pool", self.idb[:], self.ident, writes=["idb"])
        self.ones_bf = S.sb("ones_bf", [P, P], BF16)
        S.op("dve", lambda e: e.memset(self.ones_bf[:], 1.0), writes=["ones_bf"])
        self.eps_t = S.sb("eps_t", [P, 1], F32)
        S.op("dve", lambda e: e.memset(self.eps_t[:], EPS), writes=["eps_t"])

    def phase_mod(self):
        S = self.S
        with S.scope():
            cv = S.sb("cv", [P, 8, 2], F32)
            cvb = S.sb("cvb", [P, 8, 2], BF16)
            S.dma("sp", cv[:], self.cvec, writes=["cv"])
            S.op("act", lambda e: e.activation(out=cvb[:], in_=cv[:], func=AF.Silu), reads=["cv"], writes=["cvb"])
            aw = S.sb("aw", [P, 8, 6 * D], BF16)
            ab = S.sb("ab", [2, 6 * D], F32)
            mrow = S.sb("mrow", [2, 6 * D], F32)
            pss = [S.ps(f"pm{i}", [2, 512], F32) for i in range(4)]
            for l in range(2):
                for c in range(8):
                    S.dma("pool", aw[:, c, :], self.ada_w[l, c * P:(c + 1) * P, :], writes=[("aw", c)])
                S.dma("sp", ab[:], self.ada_b[l:l + 1, :].broadcast_to([2, 6 * D]), writes=["ab"])
                for n in range(12):
                    ps = pss[n % 4]
                    k = ("pm", n % 4)
                    for c in range(8):
                        mm(S, ps[:], cvb[:, c, :], aw[:, c, n * 512:(n + 1) * 512], c == 0, c == 7,
                           reads=["cvb", ("aw", c)], writes=[k])
                    S.op("dve", lambda e: e.tensor_tensor(out=mrow[:, n * 512:(n + 1) * 512], in0=ps[:],
                                                          in1=ab[:, n * 512:(n + 1) * 512], op=ALU.add),
                         reads=[k, "ab"], writes=["mrow"])
                S.dma("sp", self.modd[l], mrow[:], reads=["mrow"], writes=[("modd", l)])

    def load_layer_vecs(self, l):
        S = self.S
        V = {}
        for s in range(2):
            for which in range(2):
                V[("A", which, s)] = (S.sb(f"A{which}_{s}", [P, 8], F32), f"A{which}_{s}")
                V[("B", which, s)] = (S.sb(f"B{which}_{s}", [P, 8], F32), f"B{which}_{s}")
                if not (l == 1 and s == 1):
                    V[("G", which, s)] = (S.sb(f"GG{which}_{s}", [P, D], F32), f"GG{which}_{s}")
        with S.scope(), self.nc.allow_non_contiguous_dma(reason="tiny per-feature vectors"):
            tmp = S.sb("lv_tmp", [P, 8], F32)
            rowt = S.sb("lv_row", [P, D], F32)
            for s in range(2):
                for which, (ish, isc, ig) in enumerate(((0, 1, 0), (3, 4, 2))):
                    A, ka = V[("A", which, s)]
                    B, kb = V[("B", which, s)]
                    S.dma("sp", B[:], self.modd[l, s, ish * D:(ish + 1) * D].rearrange("(c p) -> p c", p=P),
                          reads=[("modd", l)], writes=[kb])
                    S.dma("sp", A[:], self.modd[l, s, isc * D:(isc + 1) * D].rearrange("(c p) -> p c", p=P),
                          reads=[("modd", l)], writes=[ka])
                    S.dma("sp", tmp[:], self.norm_g[l, ig, :].rearrange("(c p) -> p c", p=P), writes=["lv_tmp"])
                    S.op("dve", lambda e: e.scalar_tensor_tensor(out=A[:], in0=A[:], scalar=1.0, in1=tmp[:],
                                                                 op0=ALU.add, op1=ALU.mult),
                         reads=[ka, "lv_tmp"], writes=[ka])
                for which, (igt, ig) in enumerate(((2, 1), (5, 3))):
                    if l == 1 and s == 1:
                        continue
                    G, kg = V[("G", which, s)]
                    S.dma("sp", G[:], self.modd[l, s:s + 1, igt * D:(igt + 1) * D].broadcast_to([P, D]),
                          reads=[("modd", l)], writes=[kg])
                    S.dma("sp", rowt[:], self.norm_g[l, ig:ig + 1, :].broadcast_to([P, D]), writes=["lv_row"])
                    S.op("dve", lambda e: e.tensor_tensor(out=G[:], in0=G[:], in1=rowt[:], op=ALU.mult),
                         reads=[kg, "lv_row"], writes=[kg])
        return V

    def load_ab(self, l, which, tag):
        S = self.S
        ish, isc, ig = ((0, 1, 0), (3, 4, 2))[which]
        out = {}
        with self.nc.allow_non_contiguous_dma(reason="tiny per-feature vectors"):
            tmp = S.sb(f"{tag}_t", [P, 8], F32)
            for s in range(2):
                A = S.sb(f"{tag}_A{s}", [P, 8], F32); ka = f"{tag}_A{s}"
                B = S.sb(f"{tag}_B{s}", [P, 8], F32); kb = f"{tag}_B{s}"
                S.dma("sp", B[:], self.modd[l, s, ish * D:(ish + 1) * D].rearrange("(c p) -> p c", p=P),
                      reads=[("modd", l)], writes=[kb])
                S.dma("sp", A[:], self.modd[l, s, isc * D:(isc + 1) * D].rearrange("(c p) -> p c", p=P),
                      reads=[("modd", l)], writes=[ka])
                S.dma("sp", tmp[:], self.norm_g[l, ig, :].rearrange("(c p) -> p c", p=P), writes=[f"{tag}_t"])
                S.op("dve", lambda e: e.scalar_tensor_tensor(out=A[:], in0=A[:], scalar=1.0, in1=tmp[:],
                                                             op0=ALU.add, op1=ALU.mult),
                     reads=[ka, f"{tag}_t"], writes=[ka])
                out[s] = ((A, ka), (B, kb))
        return out

    def make_norm_bufs(self, tag, nb=2):
        S = self.S
        B = {"i": 0, "nb": nb, "tag": tag}
        B["sq"] = [S.sb(f"{tag}_sq{i}", [P, D], BF16) for i in range(1)] * nb
        B["st"] = [S.sb(f"{tag}_st{i}", [P, 4], F32) for i in range(nb)]
        B["xn"] = [S.sb(f"{tag}_xn{i}", [P, D], BF16) for i in range(nb)]
        B["tp"] = [S.ps(f"{tag}_tp{i}", [P, 8, P], BF16) for i in range(nb)]
        return B

    def norm_to_hT(self, B, x_sb, xkey, A, B_, out_ap, out_key, out2_ap=None, out2_key=None):
        S = self.S
        i = B["i"] % B["nb"]
        B["i"] += 1
        tag = B["tag"]
        sq, st, xn, tp = B["sq"][i], B["st"][i], B["xn"][i], B["tp"][i]
        ksq, kst, kxn, ktp, ktm = [(tag, n, i) for n in ("sq", "st", "xn", "tp", "tm")]
        ksq = (tag, "sq", 0)
        S.op("pool", lambda e: e.memset(st[:], 0.0), writes=[kst])
        S.op("act", lambda e: e.activation(out=sq[:], in_=x_sb, func=AF.Square, accum_out=st[:, 0:1]),
             reads=[xkey], writes=[ksq, kst])
        S.op("act", lambda e: e.activation(out=st[:, 1:2], in_=st[:, 0:1], func=AF.Sqrt, scale=1.0 / D,
                                           bias=self.eps_t[:, 0:1]), reads=[kst, "eps_t"], writes=[kst])
        S.op("dve", lambda e: e.reciprocal(out=st[:, 2:3], in_=st[:, 1:2]), reads=[kst], writes=[kst])
        S.op("dve", lambda e: e.tensor_scalar(out=xn[:], in0=x_sb, scalar1=st[:, 2:3], scalar2=None, op0=ALU.mult),
             reads=[xkey, kst], writes=[kxn])
        for c in range(8):
            S.op("pe", lambda e: e.transpose(out=tp[:, c, :], in_=xn[:, c * P:(c + 1) * P], identity=self.idb[:]),
                 reads=[kxn, "idb"], writes=[ktp], accum=(c > 0))
        for c in range(8):
            S.op("dve", lambda e: e.tensor_scalar(out=out_ap[:, c, :], in0=tp[:, c, :], scalar1=A[0][:, c:c + 1],
                                                  scalar2=B_[0][:, c:c + 1], op0=ALU.mult, op1=ALU.add),
                 reads=[ktp, A[1], B_[1]], writes=[out_key])

    def x_src(self, layer, T):
        if layer == 0:
            if T < 2:
                return self.ctx_in[T * P:(T + 1) * P, :], None
            return self.x_in[(T - 2) * P:(T - 1) * P, :], None
        return self.x_d[T * P:(T + 1) * P, :], ("x_d", T)

    def phase_L0_proj(self, V):
        S = self.S
        with S.scope():
            w = S.sb("w_in", [P, 8, 2 * D], BF16)
            for c in range(8):
                S.dma("pool", w[:, c, :], self.ev_w_in[c * P:(c + 1) * P, :], writes=[("w_in", c)])
            wk = [("w_in", c) for c in range(8)]
            NB = self.make_norm_bufs("n0")
            xt = [S.sb(f"xt{i}", [P, D], F32) for i in range(2)]
            hT = [S.sb(f"hT{i}", [P, 8, 512], BF16) for i in range(2)]
            ptok = [S.ps(f"ptok{i}", [P, 512], F32) for i in range(2)]
            pft = [S.ps(f"pft{i}", [P, 512], F32) for i in range(2)]
            otok = [S.sb(f"otok{i}", [P, 512], BF16) for i in range(2)]
            oft = [S.sb(f"oft{i}", [P, 512], BF16) for i in range(2)]
            supers = [(0, 2)] + [(2 + 4 * i, 4) for i in range(8)]
            cnt = 0
            ctok = 0
            cft = 0
            for si, (T0, nt) in enumerate(supers):
                hb = hT[si % 2]
                hk = ("hT", si % 2)
                s = 1 if T0 < 2 else 0
                for t in range(nt):
                    T = T0 + t
                    xb = xt[cnt % 2]
                    xk = ("xt", cnt % 2)
                    cnt += 1
                    src, sk = self.x_src(0, T)
                    S.dma("act", xb[:], src, reads=[sk] if sk else [], writes=[xk])
                    self.norm_to_hT(NB, xb[:], xk, V[("A", 0, s)], V[("B", 0, s)],
                                    hb[:, :, t * P:(t + 1) * P], (hk, t))
                n = nt * P
                hks = [(hk, t) for t in range(nt)]
                for t in range(nt):
                    T = T0 + t
                    for (c0, dst, dk) in ((0, self.upool_d, "upool"), (1536, self.v_d, "v")):
                        ps = ptok[ctok % 2]; pk = ("ptok", ctok % 2)
                        ob = otok[ctok % 2]; ok = ("otok", ctok % 2)
                        ctok += 1
                        for c in range(8):
                            mm(S, ps[:], hb[:, c, t * P:(t + 1) * P], w[:, c, c0:c0 + 512], c == 0, c == 7,
                               reads=[(hk, t), wk[c]], writes=[pk])
                        S.op("act", lambda e: e.activation(out=ob[:], in_=ps[:], func=AF.Copy), reads=[pk], writes=[ok])
                        S.dma("sp", dst[T * P:(T + 1) * P, :], ob[:], reads=[ok], writes=[(dk, T)])
                for jb in range(8):
                    c0 = 512 + jb * P
                    ps = pft[cft % 2]; pk = ("pft", cft % 2)
                    ob = oft[cft % 2]; ok = ("oft", cft % 2)
                    cft += 1
                    for c in range(8):
                        mm(S, ps[:, :n], w[:, c, c0:c0 + P], hb[:, c, :n], c == 0, c == 7,
                           reads=hks + [wk[c]], writes=[pk])
                    sc = 0.125 if jb < 4 else 1.0
                    S.op("act", lambda e: e.activation(out=ob[:, :n], in_=ps[:, :n], func=AF.Copy, scale=sc),
                         reads=[pk], writes=[ok])
                    dst = self.qT_d if jb < 4 else self.kT_d
                    dk = "qT" if jb < 4 else "kT"
                    S.dma("sp", dst[jb % 4, :, T0 * P:T0 * P + n], ob[:, :n], reads=[ok],
                          writes=[(dk, jb % 4, T0 + t) for t in range(nt)])

    def phase_L0_pool(self):
        S = self.S
        with S.scope():
            up = S.sb("up_all", [P, NT, 512], BF16)
            for q in range(0, NT, 2):
                S.dma("sp", up[:, q:q + 2, :], self.upool_d[q * P:(q + 2) * P, :].rearrange("(n p) f -> p n f", p=P),
                      reads=[("upool", q), ("upool", q + 1)], writes=[("up", q), ("up", q + 1)])
            bd = S.sb("bands", [P, 20, P], BF16)
            S.dma("pool", bd[:], self.bands, writes=["bands"])
            pw = S.sb("pool_w", [P, 4, P], BF16)
            S.dma("pool", pw[:], self.ev_pool_w.rearrange("g c o -> c g o"), writes=["pool_w"])
            psc = S.sb("pool_sc", [P, 4], F32)
            with self.nc.allow_non_contiguous_dma(reason="tiny"):
                S.dma("sp", psc[:], self.ev_pool_scale.rearrange("(g p) -> p g", p=P), writes=["pool_sc"])
            pb = [S.ps(f"pb{i}", [P, 4, P], F32) for i in range(2)]
            pc = [S.ps(f"pc{i}", [P, 4, P], F32) for i in range(2)]
            pm = [S.sb(f"pmx{i}", [P, 4, P], BF16) for i in range(2)]
            zp = [S.sb(f"zp{i}", [P, 4, P], BF16) for i in range(2)]
            it = 0
            for (T0, n) in ((0, 2), (2, 32)):
                for i in range(n):
                    T = T0 + i
                    b = it % 2
                    it += 1
                    for g in range(4):
                        srcs = []
                        if i > 0:
                            srcs.append((T - 1, 0))
                        cv = 3 if i == 0 else (4 if i == n - 1 else 2)
                        srcs.append((T, cv))
                        if i < n - 1:
                            srcs.append((T + 1, 1))
                        for si, (Ts, v) in enumerate(srcs):
                            mm(S, pb[b][:, g, :], up[:, Ts, g * P:(g + 1) * P], bd[:, g * 5 + v, :],
                               si == 0, si == len(srcs) - 1, reads=[("up", Ts), "bands"], writes=[("pb", b)])
                    S.op("dve", lambda e: e.tensor_copy(out=pm[b][:], in_=pb[b][:]), reads=[("pb", b)], writes=[("pmx", b)])
                    for g in range(4):
                        mm(S, pc[b][:, g, :], pw[:, g, :], pm[b][:, g, :], True, True,
                           reads=["pool_w", ("pmx", b)], writes=[("pc", b)])
                    S.op("dve", lambda e: e.tensor_tensor(out=zp[b][:], in0=pc[b][:],
                                                          in1=psc[:, :, None].broadcast_to([P, 4, P]), op=ALU.mult),
                         reads=[("pc", b), "pool_sc"], writes=[("zp", b)])
                    S.dma("sp", self.zT_d[0:4, :, T * P:(T + 1) * P].rearrange("c p t -> p c t"), zp[b][:],
                          reads=[("zp", b)], writes=[("zT", c, T) for c in range(4)])

    def phase_L0_attn(self):
        S = self.S
        with S.scope():
            kT = S.sb("kT_all", [P, 4, NTOK], BF16)
            qT = S.sb("qT_all", [P, 4, NTOK], BF16)
            va = S.sb("v_all", [P, NT, 512], BF16)
            for j in range(4):
                S.dma("sp", kT[:, j, :], self.kT_d[j], reads=[("kT", j, T) for T in range(NT)], writes=[("kTa", j)])
                S.dma("sp", qT[:, j, :], self.qT_d[j], reads=[("qT", j, T) for T in range(NT)], writes=[("qTa", j)])
            for q in range(0, NT, 2):
                S.dma("sp", va[:, q:q + 2, :], self.v_d[q * P:(q + 2) * P, :].rearrange("(n p) f -> p n f", p=P),
                      reads=[("v", q), ("v", q + 1)], writes=[("va", q), ("va", q + 1)])
            E = S.sb("Etab", [P, 96, P], BF16)
            with S.scope():
                rt = S.sb("rt", [P, 96, P], F32)
                mk = S.sb("mk", [P, 12, P], F32)
                S.dma("sp", rt[:], self.rpb_tab, writes=["rt"])
                S.dma("sp", mk[:], self.msk_tab, writes=["mk"])
                S.op("act", lambda e: e.activation(out=rt[:], in_=rt[:], func=AF.Exp), reads=["rt"], writes=["rt"])
                for h in range(8):
                    S.op("dve", lambda e: e.tensor_tensor(out=E[:, h * 12:(h + 1) * 12, :], in0=rt[:, h * 12:(h + 1) * 12, :],
                                                          in1=mk[:], op=ALU.mult), reads=["rt", "mk"], writes=["Etab"])
            pss = [[S.ps(f"pss{i}_{k}", [P, 512], F32) for k in range(2)] for i in range(2)]
            pso = [S.ps(f"pso{i}", [P, 2, P], F32) for i in range(2)]
            pex = [S.sb(f"pex{i}", [P, 7, P], BF16) for i in range(2)]
            pT = [S.sb(f"pT{i}", [P, 5, P], BF16) for i in range(2)]
            rc = [S.sb(f"rc{i}", [P, P], F32) for i in range(2)]
            zo = [S.sb(f"zo{i}", [P, P], BF16) for i in range(2)]
            it = 0
            izo = 0
            for T in range(NT):
                if T < 2:
                    chunks = [(0, None), (1, None)]
                else:
                    i = T - 2
                    if 2 <= i <= 29:
                        lat = [(T + d, v) for v, d in enumerate((-2, -1, 0, 1, 2))]
                    elif i == 0:
                        lat = [(T + d, 8 + d) for d in (0, 1, 2, 3)]
                    elif i == 1:
                        lat = [(T + d, 8 + d) for d in (-1, 0, 1, 2)]
                    elif i == 30:
                        lat = [(T + d, 8 + d) for d in (-2, -1, 0, 1)]
                    else:
                        lat = [(T + d, 8 + d) for d in (-3, -2, -1, 0)]
                    chunks = [(0, None), (1, None)] + lat
                nk = len(chunks)
                nlat = nk - 2
                for j in range(4):
                    zb = zo[izo % 2]; zk = ("zo", izo % 2)
                    izo += 1
                    for hh in range(2):
                        h = 2 * j + hh
                        pb_ = hh * 64
                        b = it % 2
                        it += 1
                        for ci, (Tk, v) in enumerate(chunks):
                            bank = pss[b][ci // 4]
                            mm(S, bank[:, (ci % 4) * P:(ci % 4 + 1) * P],
                               kT[pb_:pb_ + 64, j, Tk * P:(Tk + 1) * P], qT[pb_:pb_ + 64, j, T * P:(T + 1) * P],
                               True, True, reads=[("kTa", j), ("qTa", j)], writes=[("pss", b, ci // 4)])
                        n0 = min(nk, 4)
                        S.op("act", lambda e: e.activation(out=pex[b][:, 0:n0, :], in_=pss[b][0][:, 0:n0 * P].rearrange("p (c q) -> p c q", q=P), func=AF.Exp),
                             reads=[("pss", b, 0)], writes=[("pex", b)])
                        if nk > 4:
                            S.op("act", lambda e: e.activation(out=pex[b][:, 4:nk, :], in_=pss[b][1][:, 0:(nk - 4) * P].rearrange("p (c q) -> p c q", q=P), func=AF.Exp),
                                 reads=[("pss", b, 1)], writes=[("pex", b)])
                        if nlat > 0:
                            v0 = chunks[2][1]
                            S.op("dve", lambda e: e.tensor_tensor(out=pT[b][:, 0:nlat, :], in0=pex[b][:, 2:nk, :],
                                                                  in1=E[:, h * 12 + v0:h * 12 + v0 + nlat, :], op=ALU.mult),
                                 reads=[("pex", b), "Etab"], writes=[("pT", b)])
                        for ci, (Tk, v) in enumerate(chunks):
                            rhs = pex[b][:, ci, :] if v is None else pT[b][:, ci - 2, :]
                            rk = [("pex", b)] if v is None else [("pT", b)]
                            mm(S, pso[b][:, 0, :], va[:, Tk, j * P:(j + 1) * P], rhs, ci == 0, ci == nk - 1,
                               reads=[("va", Tk)] + rk, writes=[("pso", b)])
                        for ci, (Tk, v) in enumerate(chunks):
                            rhs = pex[b][:, ci, :] if v is None else pT[b][:, ci - 2, :]
                            rk = [("pex", b)] if v is None else [("pT", b)]
                            mm(S, pso[b][:, 1, :], self.ones_bf[:], rhs, ci == 0, ci == nk - 1,
                               reads=["ones_bf"] + rk, writes=[("pso", b)])
                        S.op("dve", lambda e: e.reciprocal(out=rc[b][pb_:pb_ + 64, :], in_=pso[b][pb_:pb_ + 64, 1, :]),
                             reads=[("pso", b)], writes=[("rc", b)])
                        S.op("dve", lambda e: e.tensor_tensor(out=zb[pb_:pb_ + 64, :], in0=pso[b][pb_:pb_ + 64, 0, :],
                                                              in1=rc[b][pb_:pb_ + 64, :], op=ALU.mult),
                             reads=[("pso", b), ("rc", b)], writes=[zk])
                    S.dma("sp", self.zT_d[4 + j, :, T * P:(T + 1) * P], zb[:], reads=[zk], writes=[("zT", 4 + j, T)])

    def phase_out_mlp(self, layer, V, w_out_ap, y_tile_fn, dst_fn, next_ab=None):
        S = self.S
        with S.scope():
            wo = S.sb("wo", [P, 8, D], BF16)
            for c in range(8):
                S.dma("pool", wo[:, c, :], w_out_ap[c * P:(c + 1) * P, :], writes=[("wo", c)])
            w1 = S.sb("w1", [P, 8, 4 * D], BF16)
            w2 = S.sb("w2", [P, 32, D], BF16)
            for c in range(8):
                S.dma("pool", w1[:, c, :], self.mlp_w1[layer, c * P:(c + 1) * P, :], writes=[("w1", c)])
            for f in range(0, 32, 4):
                S.dma("pool", w2[:, f:f + 4, :], self.mlp_w2[layer, f * P:(f + 4) * P, :].rearrange("(n p) d -> p n d", p=P),
                      writes=[("w2", f + q) for q in range(4)])
            NB = self.make_norm_bufs("nm", nb=1)
            hob = S.sb("hob", [P, 8, P], F32) if next_ab is not None else None
            zt = [S.sb(f"zt{i}", [P, 8, P], BF16) for i in range(2)]
            xt = [S.sb(f"xo{i}", [P, D], F32) for i in range(2)]
            x1 = [S.sb(f"x1_{i}", [P, D], F32) for i in range(2)]
            tmp = [S.sb(f"tg{i}", [P, D], F32) for i in range(2)]
            sq = NB["sq"][0]
            stt = [S.sb(f"ost{i}", [P, 4], F32) for i in range(4)]
            hT = [S.sb(f"hm{i}", [P, 8, 256], BF16) for i in range(1)] * 2
            py = [[S.ps(f"py{t}_{hf}", [P, 512], F32) for hf in range(2)] for t in range(2)]
            pa = [S.ps(f"pa{i}", [P, 256], F32) for i in range(2)]
            r32 = [S.sb(f"r32_{i}", [P, 256], F32) for i in range(2)]
            aT = [S.sb(f"aT{i}", [P, 256], BF16) for i in range(2)]
            ist = 0

            def norm_gate_res(t, G, xin, xin_key, xout, xout_key):
                nonlocal ist
                st = stt[ist % 4]; sk = ("ost", ist % 4)
                ist += 1
                S.op("pool", lambda e: e.memset(st[:], 0.0), writes=[sk])
                for hf in range(2):
                    S.op("act", lambda e: e.activation(out=sq[:, hf * 512:(hf + 1) * 512], in_=py[t][hf][:], func=AF.Square,
                                                       accum_out=st[:, hf:hf + 1]), reads=[("py", t, hf)], writes=[("nm", "sq", 0), sk])
                S.op("dve", lambda e: e.tensor_tensor(out=st[:, 2:3], in0=st[:, 0:1], in1=st[:, 1:2], op=ALU.add),
                     reads=[sk], writes=[sk])
                S.op("act", lambda e: e.activation(out=st[:, 2:3], in_=st[:, 2:3], func=AF.Sqrt, scale=1.0 / D,
                                                   bias=self.eps_t[:, 0:1]), reads=[sk, "eps_t"], writes=[sk])
                S.op("dve", lambda e: e.reciprocal(out=st[:, 3:4], in_=st[:, 2:3]), reads=[sk], writes=[sk])
                tb = tmp[t]; tk = ("tg", t)
                for hf in range(2):
                    S.op("dve", lambda e: e.scalar_tensor_tensor(out=tb[:, hf * 512:(hf + 1) * 512], in0=py[t][hf][:],
                                                                 scalar=st[:, 3:4], in1=G[0][:, hf * 512:(hf + 1) * 512],
                                                                 op0=ALU.mult, op1=ALU.mult),
                         reads=[("py", t, hf), sk, G[1]], writes=[tk])
                S.op("dve", lambda e: e.tensor_tensor(out=xout, in0=tb[:], in1=xin, op=ALU.add),
                     reads=[tk, xin_key], writes=[xout_key])

            ia = 0
            loaded = {}

            def load_inputs(sidx):
                loaded[sidx] = True
                for t in range(2):
                    T = 2 * sidx + t
                    src, skey = self.x_src(layer, T)
                    S.dma("sp", xt[t][:], src, reads=[skey] if skey else [], writes=[("xo", t)])
                    S.dma("sp", zt[t][:], self.zT_d[:, :, T * P:(T + 1) * P].rearrange("c p t -> p c t"),
                          reads=[("zT", c, T) for c in range(8)], writes=[("zt", t)])

            for sidx in range(NT // 2):
                T0 = 2 * sidx
                s = 1 if T0 < 2 else 0
                if layer == 1 and s == 1:
                    continue
                hb = hT[0]; hk = ("hm", 0)
                if not loaded.get(sidx):
                    load_inputs(sidx)
                for t in range(2):
                    T = T0 + t
                    y_tile_fn(T, t, zt[t], ("zt", t), wo, py[t])
                    norm_gate_res(t, V[("G", 0, s)], xt[t][:], ("xo", t), x1[t][:], ("x1", t))
                    self.norm_to_hT(NB, x1[t][:], ("x1", t), V[("A", 1, s)], V[("B", 1, s)],
                                    hb[:, :, t * P:(t + 1) * P], (hk, t))
                nxt = sidx + 1
                if nxt < NT // 2 and not (layer == 1 and nxt == 0):
                    load_inputs(nxt)
                def mm1(f):
                    a = (ia + f) % 2
                    for c in range(8):
                        mm(S, pa[a][:], w1[:, c, f * P:(f + 1) * P], hb[:, c, :], c == 0, c == 7,
                           reads=[("w1", c), (hk, 0), (hk, 1)], writes=[("pa", a)])
                    S.op("act", lambda e: e.activation(out=r32[a][:], in_=pa[a][:], func=AF.Relu),
                         reads=[("pa", a)], writes=[("r32", a)])
                    S.op("dve", lambda e: e.tensor_tensor(out=aT[a][:], in0=r32[a][:], in1=r32[a][:], op=ALU.mult),
                         reads=[("r32", a)], writes=[("aT", a)])

                def mm2(f):
                    a = (ia + f) % 2
                    for t in range(2):
                        for hf in range(2):
                            mm(S, py[t][hf][:], aT[a][:, t * P:(t + 1) * P], w2[:, f, hf * 512:(hf + 1) * 512],
                               f == 0, f == 31, reads=[("aT", a), ("w2", f)], writes=[("py", t, hf)])
                mm1(0)
                for f in range(32):
                    if f + 1 < 32:
                        mm1(f + 1)
                    mm2(f)
                for t in range(2):
                    T = T0 + t
                    norm_gate_res(t, V[("G", 1, s)], x1[t][:], ("x1", t), tmp[t][:], ("tg", t))
                    dst, dkey = dst_fn(T)
                    S.dma("sp", dst, tmp[t][:], reads=[("tg", t)], writes=[dkey])
                    if next_ab is not None:
                        ab = next_ab[s]
                        self.norm_to_hT(NB, tmp[t][:], ("tg", t), ab[0], ab[1], hob[:], "hob")
                        S.dma("sp", self.hT_d[:, :, T * P:(T + 1) * P].rearrange("c p t -> p c t"), hob[:],
                              reads=["hob"], writes=[("hT_d", T)])

    def y_tile_L0(self, T, t, zt, zk, wo, py):
        S = self.S
        for hf in range(2):
            for c in range(8):
                mm(S, py[hf][:], zt[:, c, :], wo[:, c, hf * 512:(hf + 1) * 512], c == 0, c == 7,
                   reads=[zk, ("wo", c)], writes=[("py", t, hf)])


def build_program(stop_after=None, debug=()):
    nc = bass.Bass("TRN2", target_bir_lowering=False)
    Pg = Prog(nc, debug)
    S = Pg.S
    Pg.consts()
    Pg.phase_mod()
    final_keys = []
    with S.scope():
        V0 = Pg.load_layer_vecs(0)
        Pg.phase_L0_proj(V0)
        Pg.phase_L0_pool()
        Pg.phase_L0_attn()

        def dst0(T):
            if stop_after == "L0":
                if T < 2:
                    return Pg.x_d[T * P:(T + 1) * P, :], ("x_d", T)
                return Pg.out[(T - 2) * P:(T - 1) * P, :], ("out", T)
            return Pg.x_d[T * P:(T + 1) * P, :], ("x_d", T)
        Pg.phase_out_mlp(0, V0, Pg.ev_w_out, Pg.y_tile_L0, dst0)
    S.barrier()
    S.finish([])
    S.close()
    return nc, Pg


def host_inputs(inputs):
    f = lambda a: np.ascontiguousarray(np.asarray(a, dtype=np.float32))
    dr_idx, dc_full, mask = _attn_tables()
    rpb = f(inputs["ev_rpb"])[0]
    tab = rpb[:, dr_idx, dc_full[None, :, :]]
    tab = np.ascontiguousarray(tab.transpose(2, 0, 1, 3).reshape(128, 96, 128))
    msk = np.ascontiguousarray(mask.transpose(1, 0, 2))
    bands = np.ascontiguousarray(_pool_bands().transpose(2, 0, 1, 3).reshape(128, 20, 128))
    shared = {
        "ada_w": f(inputs["ada_w"]), "ada_b": f(inputs["ada_b"]), "norm_g": f(inputs["norm_g"]),
        "mlp_w1": f(inputs["mlp_w1"]), "mlp_w2": f(inputs["mlp_w2"]),
        "ev_w_in": f(inputs["ev_w_in"])[0], "ev_w_out": f(inputs["ev_w_out"])[0],
        "ev_pool_w": f(inputs["ev_pool_w"])[0], "ev_pool_scale": f(inputs["ev_pool_scale"])[0],
        "rpb_tab": tab, "msk_tab": msk, "bands": bands, "ident": np.eye(128, dtype=np.float32),
    }
    x = f(inputs["x"]); c = f(inputs["c"]); ctx = f(inputs["ctx"]); cc = f(inputs["c_ctx"])
    maps = []
    for b in range(x.shape[0]):
        cv = np.stack([c[b].reshape(8, 128).T, cc.reshape(8, 128).T], axis=-1)
        m = dict(shared)
        m.update({"x": x[b], "ctx": ctx[b], "cvec": np.ascontiguousarray(cv)})
        maps.append(m)
    return maps


_CACHE = {}


def kernel(**inputs):
    maps = host_inputs(inputs)
    if "nc" not in _CACHE:
        _CACHE["nc"] = build_program()
    nc, Pg = _CACHE["nc"]
    res = run_bass_kernel_spmd(nc, maps, core_ids=list(range(8)))
    return np.stack([np.asarray(r["out"]) for r in res.results], axis=0)

LWC = -0.6065306597126334
GN_EPS = 64e-5


def _scan_consts():
    s = np.arange(128)[:, None]
    t = np.arange(128)[None, :]
    tri = np.stack([(s <= t), (s >= t)]).astype(np.float32)
    strict = np.stack([(s < t), (s > t)]).astype(np.float32)
    mT = strict.transpose(0, 2, 1)
    m4 = np.concatenate([tri, strict, tri, mT], axis=2)
    lm = []
    for l in range(7):
        b = 1 << l
        lm.append(((s // (2 * b)) == (t // (2 * b))) & (((s // b) % 2) == 0) & (((t // b) % 2) == 1))
    lm = np.stack(lm).astype(np.float32)
    lmN = np.stack([lm, lm.transpose(0, 2, 1)]) + np.eye(128, dtype=np.float32)[None, None]
    return tri, m4, np.ascontiguousarray(lmN)


def _tt(S, eng, out, a, b, op, reads, writes):
    return S.op(eng, lambda e: e.tensor_tensor(out=out, in0=a, in1=b, op=op), reads=reads, writes=writes)


def _stt(S, eng, out, a, sc, b, op0, op1, reads, writes):
    return S.op("dve", lambda e: e.scalar_tensor_tensor(out=out, in0=a, scalar=sc, in1=b, op0=op0, op1=op1),
                reads=reads, writes=writes)


def _act(S, out, in_, func, reads, writes, **kw):
    return S.op("act", lambda e: e.activation(out=out, in_=in_, func=func, **kw), reads=reads, writes=writes)


def _h3(ap):
    return ap.rearrange("p (h k) -> p h k", k=64)


class Prog1(Prog):
    def __init__(self, nc, debug=()):
        super().__init__(nc, debug)
        dt = nc.dram_tensor
        I = lambda name, shape: dt(name, list(shape), F32, kind="ExternalInput").ap()
        self.rw_mu = I("rw_mu", [6, D])
        self.rw_wr = I("rw_wr", [D, D]); self.rw_wk = I("rw_wk", [D, D])
        self.rw_wv = I("rw_wv", [D, D]); self.rw_wo = I("rw_wo", [D, D])
        self.rw_w0 = I("rw_w0", [2, D]); self.rw_a0 = I("rw_a0", [2, D])
        self.w1cat = I("w1cat", [D, P]); self.a1cat = I("a1cat", [D, P]); self.rw_g1 = I("rw_g1", [D, P])
        self.w2cat = I("w2cat", [P, D]); self.a2cat = I("a2cat", [P, D]); self.rw_g2 = I("rw_g2", [P, D])
        self.rw_kk = I("rw_kk", [1, D]); self.rw_ka = I("rw_ka", [1, D]); self.rw_rk = I("rw_rk", [1, D])
        self.rw_lng = I("rw_lng", [1, D]); self.rw_lnb = I("rw_lnb", [1, D])
        self.tri_c = I("tri_c", [2, P, P]); self.m4_c = I("m4_c", [2, P, 512]); self.lmT_c = I("lmT_c", [2, 7, P, P])
        X = lambda name, shape, d=F32: (dt(name, list(shape), d, kind="ExternalOutput").ap() if name in debug
                                        else dt(name, list(shape), d).ap())
        self.hT_d = X("hT_d", [8, P, NTOK])
        self.featT_d = X("featT_d", [2, NT, P, 8 * 4 * P], BF16)
        self.vtok_d = X("vtok_d", [NTOK, D], BF16)
        self.bk_d = X("bk_d", [2, NT, P, 2 * D], BF16)
        self.gC_d = X("gC_d", [2, NT, P, 8])
        self.g_d = X("g_d", [NTOK, D])
        self.bonus_d = X("bonus_d", [NTOK, D])
        self.y_d = X("y_d", [2, NTOK, D])

    def phase_R0(self, V):
        S = self.S
        with S.scope():
            NB = self.make_norm_bufs("r0")
            xt = [S.sb(f"r0x{i}", [P, D], F32) for i in range(2)]
            ho = [S.sb(f"r0h{i}", [P, 8, P], F32) for i in range(2)]
            for T in range(NT):
                s = 1 if T < 2 else 0
                b = T % 2
                src, sk = self.x_src(1, T)
                S.dma("sp", xt[b][:], src, reads=[sk], writes=[("r0x", b)])
                self.norm_to_hT(NB, xt[b][:], ("r0x", b), V[("A", 0, s)], V[("B", 0, s)], ho[b][:], ("r0h", b))
                S.dma("sp", self.hT_d[:, :, T * P:(T + 1) * P].rearrange("c p t -> p c t"), ho[b][:],
                      reads=[("r0h", b)], writes=[("hT_d", T)])

    def phase_R1(self):
        S = self.S
        with S.scope():
            W = {}
            for nm, src in (("wr", self.rw_wr), ("wk", self.rw_wk), ("wv", self.rw_wv)):
                W[nm] = S.sb(nm, [P, 8, D], BF16)
                for c in range(0, 8, 4):
                    S.dma("pool", W[nm][:, c:c + 4, :], src[c * P:(c + 4) * P, :].rearrange("(c p) n -> p c n", p=P), writes=[nm])
            for nm, src in (("w1c", self.w1cat), ("a1c", self.a1cat), ("g1", self.rw_g1)):
                W[nm] = S.sb(nm, [P, 8, P], BF16)
                S.dma("pool", W[nm][:], src.rearrange("(c p) n -> p c n", p=P), writes=[nm])
            for nm, src in (("w2c", self.w2cat), ("a2c", self.a2cat), ("g2", self.rw_g2)):
                W[nm] = S.sb(nm, [P, D], BF16)
                S.dma("pool", W[nm][:], src, writes=[nm])
            R = {}
            for nm, src in (("kk_r", self.rw_kk), ("ka_r", self.rw_ka), ("rk_r", self.rw_rk),
                            ("w0_0", self.rw_w0[0:1, :]), ("w0_1", self.rw_w0[1:2, :]),
                            ("a0_0", self.rw_a0[0:1, :]), ("a0_1", self.rw_a0[1:2, :])):
                R[nm] = S.sb(nm, [P, D], F32)
                S.dma("sp", R[nm][:], src.broadcast_to([P, D]), writes=[nm])
            mu = S.sb("mu", [P, 6, 8], F32)
            with self.nc.allow_non_contiguous_dma(reason="tiny"):
                S.dma("sp", mu[:], self.rw_mu.rearrange("j (c p) -> p j c", p=P), writes=["mu"])
            tri = S.sb("tri", [P, 2, P], F32)
            S.dma("sp", tri[:], self.tri_c.rearrange("d s t -> s d t"), writes=["tri"])
            onef = S.sb("onef", [P, P], F32)
            S.op("dve", lambda e: e.memset(onef[:], 1.0), writes=["onef"])
            hbuf = S.sb("hbuf", [P, 8, P + 2], F32)
            xx = S.sb("xx", [P, 8, P], F32)
            mxt = S.sb("mxt", [P, 8, P], F32)
            mix = S.sb("mix", [P, 6, 8, P], BF16)
            hid = S.sb("hid", [P, 3, P], BF16)
            F = {n: S.sb(n, [P, D], F32) for n in ("r_sb", "k_sb", "v_sb", "kkn", "tA", "tB", "lw", "tC", "tD", "kd0", "kd1", "tE", "tF", "tG", "tH")}
            ob = [S.sb(f"ob{i}", [P, D], BF16) for i in range(4)]
            vb = S.sb("vb", [P, D], BF16)
            ft = S.sb("ft", [P, 8, 4, P], BF16)
            bkt = S.sb("bkt", [P, 2, D], BF16)
            st16 = S.sb("st16", [P, 64], F32)
            gcs = S.sb("gcs", [P, 8], F32)
            pA = [[S.ps(f"pA{i}_{h}", [P, 512], F32) for h in range(2)] for i in range(2)]
            pCl = [S.ps(f"pCl{h}", [P, 512], F32) for h in range(2)]
            pF = S.ps("pF", [P, 512], F32)
            pT = S.ps("pT", [P, 8, P], BF16)
            ipa = 0

            def proj(lhs_fn, rhs, rkey, K0=0, K=P, nchunks=8, lkeys=()):
                nonlocal ipa
                i = ipa % 2
                ipa += 1
                for hf in range(2):
                    for c in range(nchunks):
                        mm(S, pA[i][hf][:], lhs_fn(c), rhs(c, hf), c == 0, c == nchunks - 1,
                           reads=list(lkeys) + [rkey], writes=[("pA", i, hf)])
                return pA[i], [("pA", i, 0), ("pA", i, 1)]

            def evac2(fn_half):
                for hf in range(2):
                    fn_half(hf, slice(hf * 512, (hf + 1) * 512))

            for T in range(NT):
                seq_lo, seq_hi = (0, NCTX) if T < 2 else (NCTX, NTOK)
                t0 = T * P
                lo = max(t0 - 1, seq_lo); hi = min(t0 + P + 1, seq_hi)
                if lo > t0 - 1:
                    S.op("pool", lambda e: e.memset(hbuf[:, :, 0:1], 0.0), writes=["hbuf"])
                if hi < t0 + P + 1:
                    S.op("pool", lambda e: e.memset(hbuf[:, :, P + 1:P + 2], 0.0), writes=["hbuf"])
                S.dma("pool", hbuf[:, :, lo - (t0 - 1):hi - (t0 - 1)], self.hT_d[:, :, lo:hi].rearrange("c p t -> p c t"),
                      reads=[("hT_d", q) for q in range(max(T - 1, 0), min(T + 2, NT))], writes=["hbuf"])
                _tt(S, "dve", xx[:], hbuf[:, :, 0:P], hbuf[:, :, 2:P + 2], ALU.add, ["hbuf"], ["xx"])
                _stt(S, "dve", xx[:], xx[:], 0.5, hbuf[:, :, 1:P + 1], ALU.mult, ALU.subtract, ["xx", "hbuf"], ["xx"])
                for j in range(6):
                    _tt(S, "dve", mxt[:], xx[:], mu[:, j, :][:, :, None].broadcast_to([P, 8, P]), ALU.mult, ["xx", "mu"], ["mxt"])
                    _tt(S, "dve", mix[:, j, :, :], mxt[:], hbuf[:, :, 1:P + 1], ALU.add, ["mxt", "hbuf"], [("mix", j)])
                for hi_, (wn, mj, fn) in enumerate((("w1c", 1, AF.Tanh), ("a1c", 4, AF.Copy), ("g1", 5, AF.Sigmoid))):
                    for c in range(8):
                        mm(S, pF[:, 0:P], W[wn][:, c, :], mix[:, mj, c, :], c == 0, c == 7,
                           reads=[wn, ("mix", mj)], writes=["pF"])
                    _act(S, hid[:, hi_, :], pF[:, 0:P], fn, ["pF"], [("hid", hi_)])
                for nm, mj, wn in (("r_sb", 0, "wr"), ("k_sb", 2, "wk"), ("v_sb", 3, "wv")):
                    ps, pk = proj(lambda c: mix[:, mj, c, :], lambda c, hf: W[wn][:, c, hf * 512:(hf + 1) * 512], wn,
                                  lkeys=[("mix", mj)])
                    evac2(lambda hf, sl: _act(S, F[nm][:, sl], ps[hf][:], AF.Copy, [pk[hf]], [nm]))
                S.op("pool", lambda e: e.tensor_copy(out=vb[:], in_=F["v_sb"][:]), reads=["v_sb"], writes=["vb"])
                S.dma("sp", self.vtok_d[t0:t0 + P, :], vb[:], reads=["vb"], writes=[("vtok", T)])
                ps, pk = proj(lambda c: hid[:, 2, :], lambda c, hf: W["g2"][:, hf * 512:(hf + 1) * 512], "g2", nchunks=1,
                              lkeys=[("hid", 2)])
                evac2(lambda hf, sl: _act(S, F["tA"][:, sl], ps[hf][:], AF.Copy, [pk[hf]], ["tA"]))
                S.dma("sp", self.g_d[t0:t0 + P, :], F["tA"][:], reads=["tA"], writes=[("g_d", T)])
                _tt(S, "dve", F["tA"][:], F["k_sb"][:], R["kk_r"][:], ALU.mult, ["k_sb", "kk_r"], ["tA"])
                _tt(S, "pool", F["tB"][:], F["tA"][:], F["tA"][:], ALU.mult, ["tA"], ["tB"])
                S.op("dve", lambda e: e.tensor_reduce(out=st16[:, 0:16], in_=_h3(F["tB"][:]), axis=AX.X, op=ALU.add),
                     reads=["tB"], writes=["st16"])
                S.op("dve", lambda e: e.tensor_scalar(out=st16[:, 0:16], in0=st16[:, 0:16], scalar1=1e-24, scalar2=None, op0=ALU.max),
                     reads=["st16"], writes=["st16"])
                _act(S, st16[:, 0:16], st16[:, 0:16], AF.Sqrt, ["st16"], ["st16"])
                S.op("dve", lambda e: e.reciprocal(out=st16[:, 16:32], in_=st16[:, 0:16]), reads=["st16"], writes=["st16"])
                _tt(S, "dve", _h3(F["kkn"][:]), _h3(F["tA"][:]), st16[:, 16:32][:, :, None].broadcast_to([P, 16, 64]), ALU.mult,
                    ["tA", "st16"], ["kkn"])
                for d in range(2):
                    ps, pk = proj(lambda c: hid[d * 64:(d + 1) * 64, 0, :], lambda c, hf: W["w2c"][d * 64:(d + 1) * 64, hf * 512:(hf + 1) * 512],
                                  "w2c", nchunks=1, lkeys=[("hid", 0)])
                    evac2(lambda hf, sl: _tt(S, "dve", F["tB"][:, sl], ps[hf][:], R[f"w0_{d}"][:, sl], ALU.add, [pk[hf], f"w0_{d}"], ["tB"]))
                    _act(S, F["tB"][:], F["tB"][:], AF.Sigmoid, ["tB"], ["tB"])
                    _act(S, F["lw"][:], F["tB"][:], AF.Copy, ["tB"], ["lw"], scale=LWC)
                    ps, pk = proj(lambda c: hid[d * 64:(d + 1) * 64, 1, :], lambda c, hf: W["a2c"][d * 64:(d + 1) * 64, hf * 512:(hf + 1) * 512],
                                  "a2c", nchunks=1, lkeys=[("hid", 1)])
                    evac2(lambda hf, sl: _tt(S, "dve", F["tC"][:, sl], ps[hf][:], R[f"a0_{d}"][:, sl], ALU.add, [pk[hf], f"a0_{d}"], ["tC"]))
                    _act(S, F["tC"][:], F["tC"][:], AF.Sigmoid, ["tC"], ["tC"])
                    kd = F[f"kd{d}"]; kdk = f"kd{d}"
                    _stt(S, "dve", F["tD"][:], F["tC"][:], -1.0, R["ka_r"][:], ALU.add, ALU.mult, ["tC", "ka_r"], ["tD"])
                    _stt(S, "pool", kd[:], F["tD"][:], 1.0, F["k_sb"][:], ALU.add, ALU.mult, ["tD", "k_sb"], [kdk])
                    _tt(S, "pool", F["tC"][:], F["kkn"][:], F["tC"][:], ALU.mult, ["kkn", "tC"], ["tC"])
                    for hf in range(2):
                        mm(S, pCl[hf][:], tri[:, d, :], F["lw"][:, hf * 512:(hf + 1) * 512], True, True,
                           reads=["tri", "lw"], writes=[("pCl", hf)])
                    evac2(lambda hf, sl: _act(S, F["tE"][:, sl], pCl[hf][:], AF.Exp, [("pCl", hf)], ["tE"]))
                    evac2(lambda hf, sl: _act(S, F["tF"][:, sl], pCl[hf][:], AF.Exp, [("pCl", hf)], ["tF"], scale=-1.0))
                    for hf in range(2):
                        mm(S, pCl[hf][:], onef[:], F["lw"][:, hf * 512:(hf + 1) * 512], True, True,
                           reads=["onef", "lw"], writes=[("pCl", hf)])
                    evac2(lambda hf, sl: _act(S, F["tH"][:, sl], pCl[hf][:], AF.Exp, [("pCl", hf)], ["tH"]))
                    _act(S, F["tG"][:], F["lw"][:], AF.Exp, ["lw"], ["tG"], scale=-1.0)
                    _tt(S, "dve", F["tG"][:], F["tG"][:], F["tE"][:], ALU.mult, ["tG", "tE"], ["tG"])
                    _tt(S, "pool", F["tH"][:], F["tH"][:], F["tF"][:], ALU.mult, ["tH", "tF"], ["tH"])
                    for j in range(8):
                        mm(S, pF[:, 256 + j:257 + j], F["lw"][:, j * P:(j + 1) * P], onef[:, 0:1], True, True,
                           reads=["lw", "onef"], writes=["pF"])
                    _act(S, gcs[:], pF[:, 256:264], AF.Exp, ["pF"], ["gcs"])
                    S.dma("sp", self.gC_d[d, T], gcs[:], reads=["gcs"], writes=[("gC_d", d, T)])
                    _stt(S, "dve", ob[0][:], F["kkn"][:], -1.0, F["tG"][:], ALU.mult, ALU.mult, ["kkn", "tG"], [("ob", 0)])
                    _tt(S, "pool", ob[1][:], F["r_sb"][:], F["tE"][:], ALU.mult, ["r_sb", "tE"], [("ob", 1)])
                    _tt(S, "dve", ob[2][:], F["tC"][:], F["tF"][:], ALU.mult, ["tC", "tF"], [("ob", 2)])
                    _tt(S, "pool", ob[3][:], kd[:], F["tF"][:], ALU.mult, [kdk, "tF"], [("ob", 3)])
                    _tt(S, "dve", bkt[:, 0, :], F["tC"][:], F["tH"][:], ALU.mult, ["tC", "tH"], ["bkt"])
                    _tt(S, "pool", bkt[:, 1, :], kd[:], F["tH"][:], ALU.mult, [kdk, "tH"], ["bkt"])
                    S.dma("sp", self.bk_d[d, T], bkt[:].rearrange("p a n -> p (a n)"), reads=["bkt"], writes=[("bk_d", d, T)])
                    for q in range(4):
                        for c in range(8):
                            S.op("pe", lambda e: e.transpose(out=pT[:, c, :], in_=ob[q][:, c * P:(c + 1) * P], identity=self.idb[:]),
                                 reads=[("ob", q), "idb"], writes=["pT"], accum=(c > 0))
                        if q % 2 == 0:
                            _act(S, ft[:, :, q, :], pT[:], AF.Copy, ["pT"], ["ft"])
                        else:
                            S.op("dve", lambda e: e.tensor_copy(out=ft[:, :, q, :], in_=pT[:]), reads=["pT"], writes=["ft"])
                    S.dma("sp", self.featT_d[d, T], ft[:].rearrange("p j q t -> p (j q t)"), reads=["ft"], writes=[("featT_d", d, T)])
                _tt(S, "pool", F["tD"][:], F["kd0"][:], F["kd1"][:], ALU.add, ["kd0", "kd1"], ["tD"])
                _tt(S, "pool", F["tD"][:], F["tD"][:], F["r_sb"][:], ALU.mult, ["tD", "r_sb"], ["tD"])
                _tt(S, "pool", F["tD"][:], F["tD"][:], R["rk_r"][:], ALU.mult, ["tD", "rk_r"], ["tD"])
                S.op("dve", lambda e: e.tensor_reduce(out=st16[:, 32:48], in_=_h3(F["tD"][:]), axis=AX.X, op=ALU.add),
                     reads=["tD"], writes=["st16"])
                _tt(S, "dve", _h3(F["tD"][:]), _h3(F["v_sb"][:]), st16[:, 32:48][:, :, None].broadcast_to([P, 16, 64]), ALU.mult,
                    ["v_sb", "st16"], ["tD"])
                S.dma("sp", self.bonus_d[t0:t0 + P, :], F["tD"][:], reads=["tD"], writes=[("bonus_d", T)])

    def phase_R2(self):
        S = self.S
        with S.scope():
            m4 = S.sb("m4", [P, 2, 512], F32)
            lmN = S.sb("lmN", [P, 2, 7, P], F32)
            S.dma("sp", m4[:], self.m4_c.rearrange("d s n -> s d n"), writes=["m4"])
            S.dma("sp", lmN[:], self.lmT_c.rearrange("d l s n -> s d l n"), writes=["lmN"])
            idb = self.idb
            NG = 4
            I4 = S.sb("I4", [P, NG, P], BF16)
            for g in range(NG):
                S.op("pool", lambda e: e.tensor_copy(out=I4[:, g, :], in_=idb[:]), reads=["idb"], writes=["I4"])
            I4f = I4[:].rearrange("p g t -> p (g t)")
            ST32 = [S.sb(f"ST32_{d}", [P, 8, 64], F32) for d in range(2)]
            STb = [S.sb(f"STb_{d}", [P, 8, 64], BF16) for d in range(2)]
            for d in range(2):
                S.op("dve", lambda e: e.memset(ST32[d][:], 0.0), writes=[("ST32", d)])
                S.op("dve", lambda e: e.memset(STb[d][:], 0.0), writes=[("STb", d)])
            NBUF = 3
            Fb = [S.sb(f"Fb{i}", [P, 8, 4, P], BF16) for i in range(NBUF)]
            Vb = [S.sb(f"Vb{i}", [P, D], BF16) for i in range(NBUF)]
            BKb = [S.sb(f"BKb{i}", [P, 2, D], BF16) for i in range(NBUF)]
            gCb = [S.sb(f"gCb{i}", [P, 8], F32) for i in range(NBUF)]
            ysb = [S.sb(f"ysb{i}", [P, D], F32) for i in range(NBUF)]
            SL = []
            for sl in range(2):
                R_ = dict(
                    GM=S.sb(f"GM{sl}", [P, NG, 512], BF16),
                    X=[S.sb(f"X{sl}_{i}", [P, NG, P], BF16) for i in range(2)],
                    XT=[S.sb(f"XT{sl}_{i}", [P, NG, P], BF16) for i in range(2)],
                    T1s=S.sb(f"T1s{sl}", [P, NG, P], BF16),
                    Zq=S.sb(f"Zq{sl}", [P, NG, 64], BF16),
                    Pb=S.sb(f"Pb{sl}", [P, NG, 64], BF16),
                    bk=[S.ps(f"bk{sl}_{i}", [P, NG, P], F32) for i in range(3)],
                    bz=S.ps(f"bz{sl}", [P, 8, 64], F32),
                    sl=sl)
                SL.append(R_)

            items = []
            it = 0
            for d in range(2):
                order = list(range(NT)) if d == 0 else [1, 0] + list(range(NT - 1, 1, -1))
                for ci, T in enumerate(order):
                    for g0 in range(0, 16, NG):
                        items.append(dict(d=d, T=T, g0=g0, b=it % NBUF))
                    it += 1

            def heads_of(g0):
                return [(g, g0 + g, (g0 + g) // 2, ((g0 + g) % 2) * 64) for g in range(NG)]

            def load_chunk(w):
                d, T, b = w["d"], w["T"], w["b"]
                S.dma("sp", Fb[b][:].rearrange("p j q t -> p (j q t)"), self.featT_d[d, T], reads=[("featT_d", d, T)], writes=[("Fb", b)])
                S.dma("sp", Vb[b][:], self.vtok_d[T * P:(T + 1) * P, :], reads=[("vtok", T)], writes=[("Vb", b)])
                S.dma("sp", BKb[b][:].rearrange("p a n -> p (a n)"), self.bk_d[d, T], reads=[("bk_d", d, T)], writes=[("BKb", b)])
                S.dma("sp", gCb[b][:], self.gC_d[d, T], reads=[("gC_d", d, T)], writes=[("gCb", b)])

            def run_group(w, R_):
                d, T, b, g0, sl = w["d"], w["T"], w["b"], w["g0"], R_["sl"]
                if g0 == 0:
                    load_chunk(w)
                Fk, Vk, BKk, gk = ("Fb", b), ("Vb", b), ("BKb", b), ("gCb", b)
                GM, X, XT, T1s, Zq, Pb, bk, bz = (R_[n] for n in ("GM", "X", "XT", "T1s", "Zq", "Pb", "bk", "bz"))
                K = lambda n, *a: (n, sl) + a
                hs = heads_of(g0)
                F_ = Fb[b]
                st32, stb = ST32[d], STb[d]
                for (g, h, j, pb_) in hs:
                    bank = bk[g % 3]; bkk = K("bk", g % 3)
                    bv = bank[:].rearrange("p g t -> p (g t)")
                    AR = F_[pb_:pb_ + 64, j, 0:2, :].rearrange("p q t -> p (q t)")
                    mm(S, bv[:, 0:128], F_[pb_:pb_ + 64, j, 2, :], F_[pb_:pb_ + 64, j, 1, :], True, True, reads=[Fk], writes=[bkk])
                    mm(S, bv[:, 128:384], F_[pb_:pb_ + 64, j, 3, :], AR, True, True, reads=[Fk], writes=[bkk])
                    mm(S, bv[:, 384:512], F_[pb_:pb_ + 64, j, 0, :], F_[pb_:pb_ + 64, j, 2, :], True, True, reads=[Fk], writes=[bkk])
                    _tt(S, "dve", GM[:, g, :], bv, m4[:, d, :], ALU.mult, ["m4"], [bkk, K("GM")])
                yield
                for (g, h, j, pb_) in hs:
                    mm(S, bz[:, g, :], F_[pb_:pb_ + 64, j, 0, :], stb[pb_:pb_ + 64, j, :], True, False, reads=[Fk, ("STb", d)], writes=[K("bz")])
                    mm(S, bz[:, g, :], GM[:, g, 128:256], Vb[b][:, h * 64:(h + 1) * 64], False, True, reads=[K("GM"), Vk], writes=[K("bz")])
                _act(S, Zq[:], bz[:, 0:NG, :], AF.Copy, [], [K("bz"), K("Zq")])
                yield
                xi = 0
                mm(S, bk[0][:].rearrange("p g t -> p (g t)"), idb[:], I4f, True, False, reads=["idb", "I4"], writes=[K("bk", 0)])
                for (g, h, j, pb_) in hs:
                    mm(S, bk[0][:, g, :], GM[:, g, 384:512], idb[:], False, True, reads=[K("GM"), "idb"], writes=[K("bk", 0)])
                _tt(S, "dve", X[xi][:], bk[0][:], lmN[:, d, 0:1, :].broadcast_to([P, NG, P]), ALU.mult, ["lmN"], [K("bk", 0), K("X", xi)])
                yield
                for (g, h, j, pb_) in hs:
                    mm(S, bk[2][:, g, :], X[xi][:, g, :], idb[:], True, True, reads=[K("X", xi), "idb"], writes=[K("bk", 2)])
                _act(S, XT[xi][:], bk[2][:], AF.Copy, [], [K("bk", 2), K("XT", xi)])
                yield
                for l in range(1, 7):
                    mm(S, bk[0][:].rearrange("p g t -> p (g t)"), idb[:], I4f, True, False, reads=["idb", "I4"], writes=[K("bk", 0)])
                    for (g, h, j, pb_) in hs:
                        mm(S, bk[0][:, g, :], GM[:, g, 384:512], X[xi][:, g, :], False, True, reads=[K("GM"), K("X", xi)], writes=[K("bk", 0)])
                    _tt(S, "dve", T1s[:], bk[0][:], lmN[:, d, l:l + 1, :].broadcast_to([P, NG, P]), ALU.mult, ["lmN"], [K("bk", 0), K("T1s")])
                    yield
                    for (g, h, j, pb_) in hs:
                        mm(S, bk[1][:, g, :], XT[xi][:, g, :], T1s[:, g, :], True, True, reads=[K("XT", xi), K("T1s")], writes=[K("bk", 1)])
                    if l < 6:
                        for (g, h, j, pb_) in hs:
                            mm(S, bk[2][:, g, :], T1s[:, g, :], XT[xi][:, g, :], True, True, reads=[K("XT", xi), K("T1s")], writes=[K("bk", 2)])
                    _act(S, X[1 - xi][:], bk[1][:], AF.Copy, [], [K("bk", 1), K("X", 1 - xi)])
                    if l < 6:
                        if l % 3 != 0:
                            _act(S, XT[1 - xi][:], bk[2][:], AF.Copy, [], [K("bk", 2), K("XT", 1 - xi)])
                        else:
                            S.op("dve", lambda e: e.tensor_copy(out=XT[1 - xi][:], in_=bk[2][:]), reads=[], writes=[K("bk", 2), K("XT", 1 - xi)])
                    xi = 1 - xi
                    yield
                for (g, h, j, pb_) in hs:
                    mm(S, bz[:, g, :], X[xi][:, g, :], Zq[:, g, :], True, True, reads=[K("X", xi), K("Zq")], writes=[K("bz")])
                S.op("dve", lambda e: e.tensor_copy(out=Pb[:], in_=bz[:, 0:NG, :]), reads=[], writes=[K("bz"), K("Pb")])
                yield
                for (g, h, j, pb_) in hs:
                    yo = bz[:, g, :]
                    mm(S, yo, GM[:, g, 0:128], Pb[:, g, :], True, False, reads=[K("GM"), K("Pb")], writes=[K("bz")])
                    mm(S, yo, GM[:, g, 256:384], Vb[b][:, h * 64:(h + 1) * 64], False, False, reads=[K("GM"), Vk], writes=[K("bz")])
                    mm(S, yo, F_[pb_:pb_ + 64, j, 1, :], stb[pb_:pb_ + 64, j, :], False, True, reads=[Fk, ("STb", d)], writes=[K("bz")])
                for (g, h, j, pb_) in hs:
                    mm(S, bz[:, 4 + g, :], BKb[b][:, 0, j * P:(j + 1) * P], Pb[:, g, :], True, False, reads=[BKk, K("Pb")], writes=[K("bz")])
                    mm(S, bz[:, 4 + g, :], BKb[b][:, 1, j * P:(j + 1) * P], Vb[b][:, h * 64:(h + 1) * 64], False, True,
                       reads=[BKk, Vk], writes=[K("bz")])
                _act(S, ysb[b][:, g0 * 64:(g0 + NG) * 64].rearrange("p (g v) -> p g v", v=64), bz[:, 0:NG, :], AF.Copy, [], [K("bz"), ("ysb", b)])
                for (g, h, j, pb_) in hs:
                    _stt(S, "dve", st32[pb_:pb_ + 64, j, :], st32[pb_:pb_ + 64, j, :], gCb[b][pb_:pb_ + 64, j:j + 1],
                         bz[pb_:pb_ + 64, 4 + g, :], ALU.mult, ALU.add, [gk], [("ST32", d), K("bz")])
                S.op("pool", lambda e: e.tensor_copy(out=stb[:, g0 // 2:g0 // 2 + 2, :], in_=st32[:, g0 // 2:g0 // 2 + 2, :]),
                     reads=[("ST32", d)], writes=[("STb", d)])
                if g0 + NG == 16:
                    S.dma("pool", self.y_d[d, T * P:(T + 1) * P, :], ysb[b][:], reads=[("ysb", b)], writes=[("y_d", d, T)])
                yield

            nxt = 0
            active = [None, None]
            while True:
                progressed = False
                for sl in range(2):
                    if active[sl] is None and nxt < len(items):
                        active[sl] = run_group(items[nxt], SL[sl])
                        nxt += 1
                    if active[sl] is not None:
                        progressed = True
                        try:
                            next(active[sl])
                        except StopIteration:
                            active[sl] = None
                if not progressed:
                    break

    def phase_R3(self):
        S = self.S
        with S.scope():
            R = {}
            for nm, src in (("lng_r", self.rw_lng), ("lnb_r", self.rw_lnb)):
                R[nm] = S.sb(nm, [P, D], F32)
                S.dma("sp", R[nm][:], src.broadcast_to([P, D]), writes=[nm])
            B = [{n: S.sb(f"{n}{i}", [P, D], F32) for n in ("yf", "yb", "gg", "bo")} for i in range(2)]
            zb = [S.sb(f"zb{i}", [P, D], BF16) for i in range(2)]
            zt = [S.sb(f"zt3_{i}", [P, 8, P], BF16) for i in range(2)]
            st = [S.sb(f"st3_{i}", [P, 64], F32) for i in range(2)]
            pT = [S.ps(f"pT3_{i}", [P, 8, P], BF16) for i in range(2)]
            for T in range(2, NT):
                b = T % 2
                Bf = B[b]
                k = lambda n: (n, b)
                t0 = T * P
                S.dma("pool", Bf["yf"][:], self.y_d[0, t0:t0 + P, :], reads=[("y_d", 0, T)], writes=[k("yf")])
                S.dma("pool", Bf["yb"][:], self.y_d[1, t0:t0 + P, :], reads=[("y_d", 1, T)], writes=[k("yb")])
                S.dma("pool", Bf["gg"][:], self.g_d[t0:t0 + P, :], reads=[("g_d", T)], writes=[k("gg")])
                S.dma("pool", Bf["bo"][:], self.bonus_d[t0:t0 + P, :], reads=[("bonus_d", T)], writes=[k("bo")])
                y = Bf["yf"]; t2 = Bf["yb"]
                _tt(S, "dve", y[:], y[:], t2[:], ALU.add, [k("yf"), k("yb")], [k("yf")])
                S.op("dve", lambda e: e.tensor_reduce(out=st[b][:, 0:16], in_=_h3(y[:]), axis=AX.X, op=ALU.add), reads=[k("yf")], writes=[k("st")])
                S.op("dve", lambda e: e.tensor_scalar(out=st[b][:, 0:16], in0=st[b][:, 0:16], scalar1=-1.0 / 64, scalar2=None, op0=ALU.mult),
                     reads=[k("st")], writes=[k("st")])
                _tt(S, "dve", _h3(y[:]), _h3(y[:]), st[b][:, 0:16][:, :, None].broadcast_to([P, 16, 64]), ALU.add, [k("yf"), k("st")], [k("yf")])
                _tt(S, "pool", t2[:], y[:], y[:], ALU.mult, [k("yf")], [k("yb")])
                S.op("dve", lambda e: e.tensor_reduce(out=st[b][:, 16:32], in_=_h3(t2[:]), axis=AX.X, op=ALU.add), reads=[k("yb")], writes=[k("st")])
                S.op("dve", lambda e: e.tensor_scalar(out=st[b][:, 16:32], in0=st[b][:, 16:32], scalar1=1.0 / 64, scalar2=GN_EPS, op0=ALU.mult, op1=ALU.add),
                     reads=[k("st")], writes=[k("st")])
                _act(S, st[b][:, 16:32], st[b][:, 16:32], AF.Sqrt, [k("st")], [k("st")])
                S.op("dve", lambda e: e.reciprocal(out=st[b][:, 32:48], in_=st[b][:, 16:32]), reads=[k("st")], writes=[k("st")])
                _tt(S, "dve", _h3(y[:]), _h3(y[:]), st[b][:, 32:48][:, :, None].broadcast_to([P, 16, 64]), ALU.mult, [k("yf"), k("st")], [k("yf")])
                _tt(S, "pool", y[:], y[:], R["lng_r"][:], ALU.mult, [k("yf"), "lng_r"], [k("yf")])
                _tt(S, "pool", y[:], y[:], R["lnb_r"][:], ALU.add, [k("yf"), "lnb_r"], [k("yf")])
                _tt(S, "dve", y[:], y[:], Bf["bo"][:], ALU.add, [k("yf"), k("bo")], [k("yf")])
                _tt(S, "dve", zb[b][:], y[:], Bf["gg"][:], ALU.mult, [k("yf"), k("gg")], [k("zb")])
                for c in range(8):
                    S.op("pe", lambda e: e.transpose(out=pT[b][:, c, :], in_=zb[b][:, c * P:(c + 1) * P], identity=self.idb[:]),
                         reads=[k("zb"), "idb"], writes=[k("pT3")], accum=(c > 0))
                _act(S, zt[b][:], pT[b][:], AF.Copy, [k("pT3")], [k("zt3")])
                S.dma("sp", self.zT_d[:, :, t0:t0 + P].rearrange("c p t -> p c t"), zt[b][:], reads=[k("zt3")],
                      writes=[("zT", c, T) for c in range(8)])


def build_program(stop_after=None, debug=(), phases="M0ABCD1abcde"):
    nc = bass.Bass("TRN2", target_bir_lowering=False)
    Pg = Prog1(nc, debug)
    S = Pg.S
    Pg.consts()
    if "M" in phases:
        Pg.phase_mod()
    if "0" in phases:
      with S.scope():
        V0 = Pg.load_layer_vecs(0)
        if "A" in phases: Pg.phase_L0_proj(V0)
        if "B" in phases: Pg.phase_L0_pool()
        if "C" in phases: Pg.phase_L0_attn()
        if "D" in phases:
            ab1 = Pg.load_ab(1, 0, "ab1")
            Pg.phase_out_mlp(0, V0, Pg.ev_w_out, Pg.y_tile_L0, lambda T: (Pg.x_d[T * P:(T + 1) * P, :], ("x_d", T)), next_ab=ab1)
    if "1" in phases:
      with S.scope():
        V1 = Pg.load_layer_vecs(1)
        if "a" in phases and "D" not in phases: Pg.phase_R0(V1)
        if "b" in phases: Pg.phase_R1()
        if "c" in phases: Pg.phase_R2()
        if "d" in phases: Pg.phase_R3()
        if "e" in phases: Pg.phase_out_mlp(1, V1, Pg.rw_wo, Pg.y_tile_L0, lambda T: (Pg.out[(T - 2) * P:(T - 1) * P, :], ("out", T)))
    S.barrier()
    S.finish([])
    S.close()
    return nc, Pg


_host_inputs0 = host_inputs


def host_inputs(inputs):
    maps = _host_inputs0(inputs)
    f = lambda a: np.ascontiguousarray(np.asarray(a, dtype=np.float32))
    tri, m4, lmT = _scan_consts()
    sh = {
        "rw_mu": f(inputs["rw_mu"])[0], "rw_wr": f(inputs["rw_wr"])[0], "rw_wk": f(inputs["rw_wk"])[0],
        "rw_wv": f(inputs["rw_wv"])[0], "rw_wo": f(inputs["rw_wo"])[0],
        "rw_w0": f(inputs["rw_w0"])[0], "rw_a0": f(inputs["rw_a0"])[0],
        "w1cat": f(np.concatenate([inputs["rw_w1"][0, 0], inputs["rw_w1"][0, 1]], axis=1)),
        "a1cat": f(np.concatenate([inputs["rw_a1"][0, 0], inputs["rw_a1"][0, 1]], axis=1)),
        "rw_g1": f(inputs["rw_g1"])[0],
        "w2cat": f(np.asarray(inputs["rw_w2"])[0].reshape(128, 1024)), "a2cat": f(np.asarray(inputs["rw_a2"])[0].reshape(128, 1024)),
        "rw_g2": f(inputs["rw_g2"])[0],
        "rw_kk": f(inputs["rw_kk"]).reshape(1, 1024), "rw_ka": f(inputs["rw_ka"]).reshape(1, 1024),
        "rw_rk": f(inputs["rw_rk"]).reshape(1, 1024), "rw_lng": f(inputs["rw_lng"]).reshape(1, 1024),
        "rw_lnb": f(inputs["rw_lnb"]).reshape(1, 1024),
        "tri_c": f(tri), "m4_c": f(m4), "lmT_c": f(lmT),
    }
    for m in maps:
        m.update(sh)
    return maps
```

```python
import contextlib
import numpy as np
import concourse.bass as bass
import concourse.mybir as mybir

F32 = mybir.dt.float32
BF16 = mybir.dt.bfloat16
AF = mybir.ActivationFunctionType
ALU = mybir.AluOpType
AX = mybir.AxisListType

SEM_LIMIT = 10000


class _Ctr:
    def __init__(self, S, name, step):
        self.S = S
        self.name = name
        self.step = step
        self.gen = 0
        self.sem = S._newsem(f"{name}_0")
        self.val = 0

    def next_event(self):
        if self.val + self.step > SEM_LIMIT:
            self.gen += 1
            self.sem = self.S._newsem(f"{self.name}_{self.gen}")
            self.val = 0
        self.val += self.step
        return (self.sem, self.val)


class _PsView:
    def __init__(self, t, shape):
        self.t = t
        self.n1 = shape[1]

    def __getitem__(self, key):
        if not isinstance(key, tuple):
            key = (key,)
        key = list(key)
        if len(key) < 2:
            key.append(slice(None))
        k1 = key[1]
        if isinstance(k1, slice):
            start, stop, step = k1.indices(self.n1)
            key[1] = slice(start, stop, step)
        return self.t[tuple(key)]


class _Eng:
    def __init__(self, S, name, obj):
        self.name = name
        self.obj = obj
        self.ctr = _Ctr(S, "s_" + name, 1)
        self.seen = {}
        self.n_issued = 0
        self.last_ins = None
        self.last_has_inc = False
        self.inc_idx = []
        self.inc_ev = []


class LazyEv:
    __slots__ = ("eng", "idx")

    def __init__(self, eng, idx):
        self.eng = eng
        self.idx = idx


class _Res:
    __slots__ = ("w", "r")

    def __init__(self):
        self.w = None
        self.r = {}


class Sched:
    def __init__(self, nc, n_dma_slots=8):
        self.nc = nc
        self.stack = contextlib.ExitStack()
        self.scopes = [self.stack]
        self.res = {}
        self.engs = {
            "pe": _Eng(self, "pe", nc.tensor),
            "act": _Eng(self, "act", nc.scalar),
            "dve": _Eng(self, "dve", nc.vector),
            "pool": _Eng(self, "pool", nc.gpsimd),
            "sp": _Eng(self, "sp", nc.sync),
        }
        self.dma_slots = {}
        for q in ("sp", "pool", "act"):
            self.dma_slots[q] = [_Ctr(self, f"d_{q}{i}", 16) for i in range(n_dma_slots)]
        self.dma_rr = {"sp": 0, "pool": 0, "act": 0}
        self.n_inst = 0
        self.uid = 0
        self.pending = None
        self.lazy_engines = ()

    def _newsem(self, name):
        return self.stack.enter_context(self.nc.semaphore(name))

    def sb(self, name, shape, dt):
        self.uid += 1
        return self.scopes[-1].enter_context(self.nc.sbuf_tensor(f"sb{self.uid}_{name}", list(shape), dt))

    def ps(self, name, shape, dt=F32):
        self.uid += 1
        esz = 4 if dt == F32 else 2
        per_part = esz
        for d_ in shape[1:]:
            per_part *= d_
        assert per_part <= 2048, (name, shape)
        shape = list(shape)
        if per_part < 2048:
            rest = per_part // shape[1]
            assert 2048 % rest == 0, (name, shape)
            full = [shape[0], 2048 // rest] + shape[2:]
            t = self.scopes[-1].enter_context(self.nc.psum_tensor(f"ps{self.uid}_{name}", full, dt))
            return _PsView(t, shape)
        return self.scopes[-1].enter_context(self.nc.psum_tensor(f"ps{self.uid}_{name}", shape, dt))

    @contextlib.contextmanager
    def scope(self):
        st = contextlib.ExitStack()
        self.scopes.append(st)
        try:
            yield
        finally:
            self.barrier()
            self.scopes.pop()
            st.close()

    def barrier(self):
        evs = []
        for e in self.engs.values():
            if e.n_issued > 0:
                evs.append(self._resolve(LazyEv(e, e.n_issued - 1)))
        for q in self.dma_slots:
            for ctr in self.dma_slots[q]:
                if ctr.val > 0:
                    evs.append((ctr.sem, ctr.val))
        for e in self.engs.values():
            for ev in evs:
                self._wait(e, ev)

    def _r(self, key):
        r = self.res.get(key)
        if r is None:
            r = self.res[key] = _Res()
        return r

    def _resolve(self, ev):
        if not isinstance(ev, LazyEv):
            return ev
        import bisect
        e = ev.eng
        k = bisect.bisect_left(e.inc_idx, ev.idx)
        if k < len(e.inc_idx):
            return e.inc_ev[k]
        assert e.last_ins is not None and not e.last_has_inc and e.n_issued - 1 >= ev.idx
        sv = e.ctr.next_event()
        e.last_ins.then_inc(sv[0], 1)
        e.last_has_inc = True
        e.inc_idx.append(e.n_issued - 1)
        e.inc_ev.append(sv)
        return sv

    def _wait(self, eng, ev):
        if ev is None:
            return
        if isinstance(ev, LazyEv) and ev.eng is eng and eng.name == "pe":
            return
        sem, val = self._resolve(ev)
        k = id(sem)
        if eng.seen.get(k, 0) >= val:
            return
        if self.pending is not None:
            cur = self.pending.get(k)
            if cur is None or cur[1] < val:
                self.pending[k] = (sem, val)
            return
        eng.obj.wait_ge(sem, val)
        eng.seen[k] = val

    def _flush(self, eng):
        pend = list(self.pending.values())
        self.pending = None
        for (sem, val) in pend[:-1]:
            eng.obj.wait_ge(sem, val)
            eng.seen[id(sem)] = val
        if pend:
            sem, val = pend[-1]
            eng.seen[id(sem)] = val
            return (sem, val)
        return None

    def _deps(self, eng, reads, writes, skip_same_eng_write=False):
        for key in reads:
            r = self._r(key)
            self._wait(eng, r.w)
        inorder = eng.name in ("act", "dve")
        for key in writes:
            r = self._r(key)
            if not ((skip_same_eng_write or inorder) and isinstance(r.w, LazyEv) and r.w.eng is eng):
                self._wait(eng, r.w)
            for ev in r.r.values():
                if inorder and isinstance(ev, LazyEv) and ev.eng is eng:
                    continue
                self._wait(eng, ev)

    def _commit(self, ev, reads, writes):
        rk = ev.eng.name if isinstance(ev, LazyEv) else id(ev[0])
        for key in reads:
            self._r(key).r[rk] = ev
        for key in writes:
            r = self._r(key)
            r.w = ev
            r.r = {}

    def op(self, engname, fn, reads=(), writes=(), accum=False):
        eng = self.engs[engname]
        self.pending = {}
        self._deps(eng, reads, writes, skip_same_eng_write=accum)
        last = self._flush(eng)
        ins = fn(eng.obj)
        if last is not None:
            ins._wait_ge(last[0], last[1])
        eng.last_ins = ins
        eng.last_has_inc = False
        ev = LazyEv(eng, eng.n_issued)
        eng.n_issued += 1
        if engname not in self.lazy_engines:
            self._resolve(ev)
        self._commit(ev, reads, writes)
        self.n_inst += 1
        return ev

    def dma(self, q, out, in_, reads=(), writes=(), **kw):
        eng = self.engs[q]
        slots = self.dma_slots[q]
        i = self.dma_rr[q]
        self.dma_rr[q] = (i + 1) % len(slots)
        ctr = slots[i]
        self.pending = {}
        if ctr.val > 0:
            self._wait(eng, (ctr.sem, ctr.val))
        self._deps(eng, reads, writes)
        last = self._flush(eng)
        ev = ctr.next_event()
        ins = eng.obj.dma_start(out=out, in_=in_, **kw)
        if last is not None:
            ins._wait_ge(last[0], last[1])
        ins.then_inc(ev[0], 16)
        self._commit(ev, reads, writes)
        self.n_inst += 1
        return ev

    def finish(self, final_keys):
        eng = self.engs["sp"]
        for key in final_keys:
            r = self._r(key)
            self._wait(eng, r.w)
        for q in self.dma_slots:
            for ctr in self.dma_slots[q]:
                if ctr.val > 0:
                    self._wait(eng, (ctr.sem, ctr.val))

    def close(self):
        self.stack.close()

from concourse.bass_utils import run_bass_kernel_spmd

D = 1024
NCTX = 256
NLAT = 4096
NTOK = NCTX + NLAT
NT = NTOK // 128
EPS = 1e-6
P = 128


def _pool_bands():
    L = 1024
    out = np.zeros((4, 5, 128, 128), np.float32)
    for g, w in enumerate((2, 4, 8, 16)):
        def full(L):
            t = np.arange(L)
            lo = np.clip(t - w // 2, 0, L)
            hi = np.clip(t + w // 2, 0, L)
            s = np.arange(L)[:, None]
            m = ((s >= lo[None, :]) & (s < hi[None, :])).astype(np.float64) / (hi - lo)[None, :]
            m -= np.eye(L)
            return m
        m = full(L)
        out[g, 0] = m[3 * 128:4 * 128, 4 * 128:5 * 128]
        out[g, 1] = m[5 * 128:6 * 128, 4 * 128:5 * 128]
        out[g, 2] = m[4 * 128:5 * 128, 4 * 128:5 * 128]
        out[g, 3] = m[0:128, 0:128]
        out[g, 4] = m[L - 128:, L - 128:]
    return out


_VARS = [(-2, "pm"), (-1, "f"), (0, "f"), (1, "f"), (2, "pp")] + [(d, "f") for d in range(-3, 4)]


def _attn_tables():
    kc = np.arange(64)
    qc = np.arange(64)
    c_start = np.clip(qc - 8, 0, 48)
    col_ok = (kc[:, None] >= c_start[None, :]) & (kc[:, None] < c_start[None, :] + 16)
    dc_idx = np.clip(kc[:, None] - qc[None, :], -15, 15) + 15
    dr_idx = np.zeros((12, 128, 128), np.int64)
    dc_full = np.zeros((128, 128), np.int64)
    mask = np.zeros((12, 128, 128), np.float32)
    for a in range(2):
        for b in range(2):
            dc_full[a * 64:(a + 1) * 64, b * 64:(b + 1) * 64] = dc_idx
    for v, (dl, kind) in enumerate(_VARS):
        for a in range(2):
            for b in range(2):
                dr = 2 * dl + a - b + 7
                vis = True
                if kind == "pm":
                    vis = not (a == 0 and b == 1)
                elif kind == "pp":
                    vis = (a == 0 and b == 1)
                dr_idx[v, a * 64:(a + 1) * 64, b * 64:(b + 1) * 64] = min(max(dr, 0), 14)
                if vis and 0 <= dr <= 14:
                    mask[v, a * 64:(a + 1) * 64, b * 64:(b + 1) * 64] = col_ok
    return dr_idx, dc_full, mask


def mm(S, out, lhsT, rhs, start, stop, reads, writes):
    return S.op("pe", lambda e: e.matmul(out, lhsT=lhsT, rhs=rhs, start=start, stop=stop),
                reads=reads, writes=writes, accum=not start)


class Prog:
    def __init__(self, nc, debug=()):
        self.nc = nc
        self.S = Sched(nc)
        self.debug = debug
        self.dbg_out = {}
        dt = nc.dram_tensor
        I = lambda name, shape: dt(name, list(shape), F32, kind="ExternalInput").ap()
        self.x_in = I("x", [NLAT, D])
        self.ctx_in = I("ctx", [NCTX, D])
        self.cvec = I("cvec", [P, 8, 2])
        self.ada_w = I("ada_w", [2, D, 6 * D])
        self.ada_b = I("ada_b", [2, 6 * D])
        self.norm_g = I("norm_g", [2, 4, D])
        self.mlp_w1 = I("mlp_w1", [2, D, 4 * D])
        self.mlp_w2 = I("mlp_w2", [2, 4 * D, D])
        self.ev_w_in = I("ev_w_in", [D, 2 * D])
        self.ev_w_out = I("ev_w_out", [D, D])
        self.ev_pool_w = I("ev_pool_w", [4, P, P])
        self.ev_pool_scale = I("ev_pool_scale", [512])
        self.rpb_tab = I("rpb_tab", [P, 8 * 12, P])
        self.msk_tab = I("msk_tab", [P, 12, P])
        self.bands = I("bands", [P, 20, P])
        self.ident = I("ident", [P, P])
        self.out = dt("out", [NLAT, D], F32, kind="ExternalOutput").ap()
        X = lambda name, shape, d=F32: (dt(name, list(shape), d, kind="ExternalOutput").ap() if name in debug
                                        else dt(name, list(shape), d).ap())
        self.modd = X("modd", [2, 2, 6 * D])
        self.x_d = X("x_d", [NTOK, D])
        self.upool_d = X("upool_d", [NTOK, 512], BF16)
        self.v_d = X("v_d", [NTOK, 512], BF16)
        self.qT_d = X("qT_d", [4, P, NTOK], BF16)
        self.kT_d = X("kT_d", [4, P, NTOK], BF16)
        self.zT_d = X("zT_d", [8, P, NTOK], BF16)

    def dbg(self, name, shape, dtp=F32):
        t = self.nc.dram_tensor("dbg_" + name, list(shape), dtp, kind="ExternalOutput").ap()
        self.dbg_out[name] = t
        return t

    def consts(self):
        S = self.S
        self.idb = S.sb("idb", [P, P], BF16)
        S.dma("pool", self.idb[:], self.ident, writes=["idb"])
        self.ones_bf = S.sb("ones_bf", [P, P], BF16)
        S.op("dve", lambda e: e.memset(self.ones_bf[:], 1.0), writes=["ones_bf"])
        self.eps_t = S.sb("eps_t", [P, 1], F32)
        S.op("dve", lambda e: e.memset(self.eps_t[:], EPS), writes=["eps_t"])

    def phase_mod(self):
        S = self.S
        with S.scope():
            cv = S.sb("cv", [P, 8, 2], F32)
            cvb = S.sb("cvb", [P, 8, 2], BF16)
            S.dma("sp", cv[:], self.cvec, writes=["cv"])
            S.op("act", lambda e: e.activation(out=cvb[:], in_=cv[:], func=AF.Silu), reads=["cv"], writes=["cvb"])
            aw = S.sb("aw", [P, 8, 6 * D], BF16)
            ab = S.sb("ab", [2, 6 * D], F32)
            mrow = S.sb("mrow", [2, 6 * D], F32)
            pss = [S.ps(f"pm{i}", [2, 512], F32) for i in range(4)]
            for l in range(2):
                for c in range(8):
                    S.dma("pool", aw[:, c, :], self.ada_w[l, c * P:(c + 1) * P, :], writes=[("aw", c)])
                S.dma("sp", ab[:], self.ada_b[l:l + 1, :].broadcast_to([2, 6 * D]), writes=["ab"])
                for n in range(12):
                    ps = pss[n % 4]
                    k = ("pm", n % 4)
                    for c in range(8):
                        mm(S, ps[:], cvb[:, c, :], aw[:, c, n * 512:(n + 1) * 512], c == 0, c == 7,
                           reads=["cvb", ("aw", c)], writes=[k])
                    S.op("dve", lambda e: e.tensor_tensor(out=mrow[:, n * 512:(n + 1) * 512], in0=ps[:],
                                                          in1=ab[:, n * 512:(n + 1) * 512], op=ALU.add),
                         reads=[k, "ab"], writes=["mrow"])
                S.dma("sp", self.modd[l], mrow[:], reads=["mrow"], writes=[("modd", l)])

    def load_layer_vecs(self, l, V=None, part="all"):
        S = self.S
        V = {} if V is None else V
        for s in range(2):
            for which in range(2):
                if part in ("all", "ab"):
                    V[("A", which, s)] = (S.sb(f"A{which}_{s}", [P, 8], F32), f"A{which}_{s}")
                    V[("B", which, s)] = (S.sb(f"B{which}_{s}", [P, 8], F32), f"B{which}_{s}")
                if part in ("all", "g") and not (l == 1 and s == 1):
                    V[("G", which, s)] = (S.sb(f"GG{which}_{s}", [P, D], F32), f"GG{which}_{s}")
        with S.scope(), self.nc.allow_non_contiguous_dma(reason="tiny per-feature vectors"):
            tmp = S.sb("lv_tmp", [P, 8], F32)
            rowt = S.sb("lv_row", [P, D], F32)
            for s in range(2):
                for which, (ish, isc, ig) in enumerate(((0, 1, 0), (3, 4, 2))):
                    if part == "g":
                        continue
                    A, ka = V[("A", which, s)]
                    B, kb = V[("B", which, s)]
                    S.dma("sp", B[:], self.modd[l, s, ish * D:(ish + 1) * D].rearrange("(c p) -> p c", p=P),
                          reads=[("modd", l)], writes=[kb])
                    S.dma("sp", A[:], self.modd[l, s, isc * D:(isc + 1) * D].rearrange("(c p) -> p c", p=P),
                          reads=[("modd", l)], writes=[ka])
                    S.dma("sp", tmp[:], self.norm_g[l, ig, :].rearrange("(c p) -> p c", p=P), writes=["lv_tmp"])
                    S.op("dve", lambda e: e.scalar_tensor_tensor(out=A[:], in0=A[:], scalar=1.0, in1=tmp[:],
                                                                 op0=ALU.add, op1=ALU.mult),
                         reads=[ka, "lv_tmp"], writes=[ka])
                for which, (igt, ig) in enumerate(((2, 1), (5, 3))):
                    if (l == 1 and s == 1) or part == "ab":
                        continue
                    G, kg = V[("G", which, s)]
                    S.dma("sp", G[:], self.modd[l, s:s + 1, igt * D:(igt + 1) * D].broadcast_to([P, D]),
                          reads=[("modd", l)], writes=[kg])
                    S.dma("sp", rowt[:], self.norm_g[l, ig:ig + 1, :].broadcast_to([P, D]), writes=["lv_row"])
                    S.op("dve", lambda e: e.tensor_tensor(out=G[:], in0=G[:], in1=rowt[:], op=ALU.mult),
                         reads=[kg, "lv_row"], writes=[kg])
        return V

    def load_ab(self, l, which, tag):
        S = self.S
        ish, isc, ig = ((0, 1, 0), (3, 4, 2))[which]
        out = {}
        with self.nc.allow_non_contiguous_dma(reason="tiny per-feature vectors"):
            tmp = S.sb(f"{tag}_t", [P, 8], F32)
            for s in range(2):
                A = S.sb(f"{tag}_A{s}", [P, 8], F32); ka = f"{tag}_A{s}"
                B = S.sb(f"{tag}_B{s}", [P, 8], F32); kb = f"{tag}_B{s}"
                S.dma("sp", B[:], self.modd[l, s, ish * D:(ish + 1) * D].rearrange("(c p) -> p c", p=P),
                      reads=[("modd", l)], writes=[kb])
                S.dma("sp", A[:], self.modd[l, s, isc * D:(isc + 1) * D].rearrange("(c p) -> p c", p=P),
                      reads=[("modd", l)], writes=[ka])
                S.dma("sp", tmp[:], self.norm_g[l, ig, :].rearrange("(c p) -> p c", p=P), writes=[f"{tag}_t"])
                S.op("dve", lambda e: e.scalar_tensor_tensor(out=A[:], in0=A[:], scalar=1.0, in1=tmp[:],
                                                             op0=ALU.add, op1=ALU.mult),
                     reads=[ka, f"{tag}_t"], writes=[ka])
                out[s] = ((A, ka), (B, kb))
        return out

    def make_norm_bufs(self, tag, nb=2):
        S = self.S
        B = {"i": 0, "nb": nb, "tag": tag}
        B["sq"] = [S.sb(f"{tag}_sq{i}", [P, D], BF16) for i in range(1)] * nb
        B["st"] = [S.sb(f"{tag}_st{i}", [P, 4], F32) for i in range(nb)]
        B["xn"] = [S.sb(f"{tag}_xn{i}", [P, D], BF16) for i in range(nb)]
        B["tp"] = [S.ps(f"{tag}_tp{i}", [P, 8, P], BF16) for i in range(nb)]
        return B

    def norm_to_hT(self, B, x_sb, xkey, A, B_, out_ap, out_key, out2_ap=None, out2_key=None):
        S = self.S
        i = B["i"] % B["nb"]
        B["i"] += 1
        tag = B["tag"]
        sq, st, xn, tp = B["sq"][i], B["st"][i], B["xn"][i], B["tp"][i]
        ksq, kst, kxn, ktp, ktm = [(tag, n, i) for n in ("sq", "st", "xn", "tp", "tm")]
        ksq = (tag, "sq", 0)
        S.op("pool", lambda e: e.memset(st[:], 0.0), writes=[kst])
        S.op("act", lambda e: e.activation(out=sq[:], in_=x_sb, func=AF.Square, accum_out=st[:, 0:1]),
             reads=[xkey], writes=[ksq, kst])
        S.op("act", lambda e: e.activation(out=st[:, 1:2], in_=st[:, 0:1], func=AF.Sqrt, scale=1.0 / D,
                                           bias=self.eps_t[:, 0:1]), reads=[kst, "eps_t"], writes=[kst])
        S.op("dve", lambda e: e.reciprocal(out=st[:, 2:3], in_=st[:, 1:2]), reads=[kst], writes=[kst])
        S.op("dve", lambda e: e.tensor_scalar(out=xn[:], in0=x_sb, scalar1=st[:, 2:3], scalar2=None, op0=ALU.mult),
             reads=[xkey, kst], writes=[kxn])
        for c in range(8):
            S.op("pe", lambda e: e.transpose(out=tp[:, c, :], in_=xn[:, c * P:(c + 1) * P], identity=self.idb[:]),
                 reads=[kxn, "idb"], writes=[ktp], accum=(c > 0))
        for c in range(8):
            S.op("dve", lambda e: e.tensor_scalar(out=out_ap[:, c, :], in0=tp[:, c, :], scalar1=A[0][:, c:c + 1],
                                                  scalar2=B_[0][:, c:c + 1], op0=ALU.mult, op1=ALU.add),
                 reads=[ktp, A[1], B_[1]], writes=[out_key])

    def x_src(self, layer, T):
        if layer == 0:
            if T < 2:
                return self.ctx_in[T * P:(T + 1) * P, :], None
            return self.x_in[(T - 2) * P:(T - 1) * P, :], None
        return self.x_d[T * P:(T + 1) * P, :], ("x_d", T)

    def phase_L0_proj(self, V):
        S = self.S
        with S.scope():
            w = S.sb("w_in", [P, 8, 2 * D], BF16)
            for c in range(8):
                S.dma("pool", w[:, c, :], self.ev_w_in[c * P:(c + 1) * P, :], writes=[("w_in", c)])
            wk = [("w_in", c) for c in range(8)]
            NB = self.make_norm_bufs("n0")
            xt = [S.sb(f"xt{i}", [P, D], F32) for i in range(2)]
            hT = [S.sb(f"hT{i}", [P, 8, 512], BF16) for i in range(2)]
            ptok = [S.ps(f"ptok{i}", [P, 512], F32) for i in range(2)]
            pft = [S.ps(f"pft{i}", [P, 512], F32) for i in range(2)]
            otok = [S.sb(f"otok{i}", [P, 512], BF16) for i in range(2)]
            oft = [S.sb(f"oft{i}", [P, 512], BF16) for i in range(2)]
            supers = [(0, 2)] + [(2 + 4 * i, 4) for i in range(8)]
            cnt = 0
            ctok = 0
            cft = 0
            for si, (T0, nt) in enumerate(supers):
                hb = hT[si % 2]
                hk = ("hT", si % 2)
                s = 1 if T0 < 2 else 0
                for t in range(nt):
                    T = T0 + t
                    xb = xt[cnt % 2]
                    xk = ("xt", cnt % 2)
                    cnt += 1
                    src, sk = self.x_src(0, T)
                    S.dma("act", xb[:], src, reads=[sk] if sk else [], writes=[xk])
                    self.norm_to_hT(NB, xb[:], xk, V[("A", 0, s)], V[("B", 0, s)],
                                    hb[:, :, t * P:(t + 1) * P], (hk, t))
                n = nt * P
                hks = [(hk, t) for t in range(nt)]
                for t in range(nt):
                    T = T0 + t
                    for (c0, dst, dk) in ((0, self.upool_d, "upool"), (1536, self.v_d, "v")):
                        ps = ptok[ctok % 2]; pk = ("ptok", ctok % 2)
                        ob = otok[ctok % 2]; ok = ("otok", ctok % 2)
                        ctok += 1
                        for c in range(8):
                            mm(S, ps[:], hb[:, c, t * P:(t + 1) * P], w[:, c, c0:c0 + 512], c == 0, c == 7,
                               reads=[(hk, t), wk[c]], writes=[pk])
                        S.op("act", lambda e: e.activation(out=ob[:], in_=ps[:], func=AF.Copy), reads=[pk], writes=[ok])
                        S.dma("sp", dst[T * P:(T + 1) * P, :], ob[:], reads=[ok], writes=[(dk, T)])
                for jb in range(8):
                    c0 = 512 + jb * P
                    ps = pft[cft % 2]; pk = ("pft", cft % 2)
                    ob = oft[cft % 2]; ok = ("oft", cft % 2)
                    cft += 1
                    for c in range(8):
                        mm(S, ps[:, :n], w[:, c, c0:c0 + P], hb[:, c, :n], c == 0, c == 7,
                           reads=hks + [wk[c]], writes=[pk])
                    sc = 0.125 if jb < 4 else 1.0
                    S.op("act", lambda e: e.activation(out=ob[:, :n], in_=ps[:, :n], func=AF.Copy, scale=sc),
                         reads=[pk], writes=[ok])
                    dst = self.qT_d if jb < 4 else self.kT_d
                    dk = "qT" if jb < 4 else "kT"
                    S.dma("sp", dst[jb % 4, :, T0 * P:T0 * P + n], ob[:, :n], reads=[ok],
                          writes=[(dk, jb % 4, T0 + t) for t in range(nt)])

    def phase_L0_pool(self):
        S = self.S
        with S.scope():
            up = S.sb("up_all", [P, NT, 512], BF16)
            for q in range(0, NT, 2):
                S.dma("sp", up[:, q:q + 2, :], self.upool_d[q * P:(q + 2) * P, :].rearrange("(n p) f -> p n f", p=P),
                      reads=[("upool", q), ("upool", q + 1)], writes=[("up", q), ("up", q + 1)])
            bd = S.sb("bands", [P, 20, P], BF16)
            S.dma("pool", bd[:], self.bands, writes=["bands"])
            pw = S.sb("pool_w", [P, 4, P], BF16)
            S.dma("pool", pw[:], self.ev_pool_w.rearrange("g c o -> c g o"), writes=["pool_w"])
            psc = S.sb("pool_sc", [P, 4], F32)
            with self.nc.allow_non_contiguous_dma(reason="tiny"):
                S.dma("sp", psc[:], self.ev_pool_scale.rearrange("(g p) -> p g", p=P), writes=["pool_sc"])
            pb = [S.ps(f"pb{i}", [P, 4, P], F32) for i in range(2)]
            pc = [S.ps(f"pc{i}", [P, 4, P], F32) for i in range(2)]
            pm = [S.sb(f"pmx{i}", [P, 4, P], BF16) for i in range(2)]
            zp = [S.sb(f"zp{i}", [P, 4, P], BF16) for i in range(2)]
            it = 0
            for (T0, n) in ((0, 2), (2, 32)):
                for i in range(n):
                    T = T0 + i
                    b = it % 2
                    it += 1
                    for g in range(4):
                        srcs = []
                        if i > 0:
                            srcs.append((T - 1, 0))
                        cv = 3 if i == 0 else (4 if i == n - 1 else 2)
                        srcs.append((T, cv))
                        if i < n - 1:
                            srcs.append((T + 1, 1))
                        for si, (Ts, v) in enumerate(srcs):
                            mm(S, pb[b][:, g, :], up[:, Ts, g * P:(g + 1) * P], bd[:, g * 5 + v, :],
                               si == 0, si == len(srcs) - 1, reads=[("up", Ts), "bands"], writes=[("pb", b)])
                    S.op("dve", lambda e: e.tensor_copy(out=pm[b][:], in_=pb[b][:]), reads=[("pb", b)], writes=[("pmx", b)])
                    for g in range(4):
                        mm(S, pc[b][:, g, :], pw[:, g, :], pm[b][:, g, :], True, True,
                           reads=["pool_w", ("pmx", b)], writes=[("pc", b)])
                    S.op("dve", lambda e: e.tensor_tensor(out=zp[b][:], in0=pc[b][:],
                                                          in1=psc[:, :, None].broadcast_to([P, 4, P]), op=ALU.mult),
                         reads=[("pc", b), "pool_sc"], writes=[("zp", b)])
                    S.dma("sp", self.zT_d[0:4, :, T * P:(T + 1) * P].rearrange("c p t -> p c t"), zp[b][:],
                          reads=[("zp", b)], writes=[("zT", c, T) for c in range(4)])

    def phase_L0_attn(self):
        S = self.S
        with S.scope():
            kT = S.sb("kT_all", [P, 4, NTOK], BF16)
            qT = S.sb("qT_all", [P, 4, NTOK], BF16)
            va = S.sb("v_all", [P, NT, 512], BF16)
            for j in range(4):
                S.dma("sp", kT[:, j, :], self.kT_d[j], reads=[("kT", j, T) for T in range(NT)], writes=[("kTa", j)])
                S.dma("sp", qT[:, j, :], self.qT_d[j], reads=[("qT", j, T) for T in range(NT)], writes=[("qTa", j)])
            for q in range(0, NT, 2):
                S.dma("sp", va[:, q:q + 2, :], self.v_d[q * P:(q + 2) * P, :].rearrange("(n p) f -> p n f", p=P),
                      reads=[("v", q), ("v", q + 1)], writes=[("va", q), ("va", q + 1)])
            E = S.sb("Etab", [P, 96, P], BF16)
            with S.scope():
                rt = S.sb("rt", [P, 96, P], F32)
                mk = S.sb("mk", [P, 12, P], F32)
                S.dma("sp", rt[:], self.rpb_tab, writes=["rt"])
                S.dma("sp", mk[:], self.msk_tab, writes=["mk"])
                S.op("act", lambda e: e.activation(out=rt[:], in_=rt[:], func=AF.Exp), reads=["rt"], writes=["rt"])
                for h in range(8):
                    S.op("dve", lambda e: e.tensor_tensor(out=E[:, h * 12:(h + 1) * 12, :], in0=rt[:, h * 12:(h + 1) * 12, :],
                                                          in1=mk[:], op=ALU.mult), reads=["rt", "mk"], writes=["Etab"])
            pss = [[S.ps(f"pss{i}_{k}", [P, 512], F32) for k in range(2)] for i in range(2)]
            pso = [S.ps(f"pso{i}", [P, 2, P], F32) for i in range(2)]
            pex = [S.sb(f"pex{i}", [P, 7, P], BF16) for i in range(2)]
            pT = [S.sb(f"pT{i}", [P, 5, P], BF16) for i in range(2)]
            rc = [S.sb(f"rc{i}", [P, P], F32) for i in range(2)]
            zo = [S.sb(f"zo{i}", [P, P], BF16) for i in range(2)]
            it = 0
            izo = 0
            for T in range(NT):
                if T < 2:
                    chunks = [(0, None), (1, None)]
                else:
                    i = T - 2
                    if 2 <= i <= 29:
                        lat = [(T + d, v) for v, d in enumerate((-2, -1, 0, 1, 2))]
                    elif i == 0:
                        lat = [(T + d, 8 + d) for d in (0, 1, 2, 3)]
                    elif i == 1:
                        lat = [(T + d, 8 + d) for d in (-1, 0, 1, 2)]
                    elif i == 30:
                        lat = [(T + d, 8 + d) for d in (-2, -1, 0, 1)]
                    else:
                        lat = [(T + d, 8 + d) for d in (-3, -2, -1, 0)]
                    chunks = [(0, None), (1, None)] + lat
                nk = len(chunks)
                nlat = nk - 2
                for j in range(4):
                    zb = zo[izo % 2]; zk = ("zo", izo % 2)
                    izo += 1
                    for hh in range(2):
                        h = 2 * j + hh
                        pb_ = hh * 64
                        b = it % 2
                        it += 1
                        for ci, (Tk, v) in enumerate(chunks):
                            bank = pss[b][ci // 4]
                            mm(S, bank[:, (ci % 4) * P:(ci % 4 + 1) * P],
                               kT[pb_:pb_ + 64, j, Tk * P:(Tk + 1) * P], qT[pb_:pb_ + 64, j, T * P:(T + 1) * P],
                               True, True, reads=[("kTa", j), ("qTa", j)], writes=[("pss", b, ci // 4)])
                        n0 = min(nk, 4)
                        S.op("act", lambda e: e.activation(out=pex[b][:, 0:n0, :], in_=pss[b][0][:, 0:n0 * P].rearrange("p (c q) -> p c q", q=P), func=AF.Exp),
                             reads=[("pss", b, 0)], writes=[("pex", b)])
                        if nk > 4:
                            S.op("act", lambda e: e.activation(out=pex[b][:, 4:nk, :], in_=pss[b][1][:, 0:(nk - 4) * P].rearrange("p (c q) -> p c q", q=P), func=AF.Exp),
                                 reads=[("pss", b, 1)], writes=[("pex", b)])
                        if nlat > 0:
                            v0 = chunks[2][1]
                            S.op("dve", lambda e: e.tensor_tensor(out=pT[b][:, 0:nlat, :], in0=pex[b][:, 2:nk, :],
                                                                  in1=E[:, h * 12 + v0:h * 12 + v0 + nlat, :], op=ALU.mult),
                                 reads=[("pex", b), "Etab"], writes=[("pT", b)])
                        for ci, (Tk, v) in enumerate(chunks):
                            rhs = pex[b][:, ci, :] if v is None else pT[b][:, ci - 2, :]
                            rk = [("pex", b)] if v is None else [("pT", b)]
                            mm(S, pso[b][:, 0, :], va[:, Tk, j * P:(j + 1) * P], rhs, ci == 0, ci == nk - 1,
                               reads=[("va", Tk)] + rk, writes=[("pso", b)])
                        for ci, (Tk, v) in enumerate(chunks):
                            rhs = pex[b][:, ci, :] if v is None else pT[b][:, ci - 2, :]
                            rk = [("pex", b)] if v is None else [("pT", b)]
                            mm(S, pso[b][:, 1, :], self.ones_bf[:], rhs, ci == 0, ci == nk - 1,
                               reads=["ones_bf"] + rk, writes=[("pso", b)])
                        S.op("dve", lambda e: e.reciprocal(out=rc[b][pb_:pb_ + 64, :], in_=pso[b][pb_:pb_ + 64, 1, :]),
                             reads=[("pso", b)], writes=[("rc", b)])
                        S.op("dve", lambda e: e.tensor_tensor(out=zb[pb_:pb_ + 64, :], in0=pso[b][pb_:pb_ + 64, 0, :],
                                                              in1=rc[b][pb_:pb_ + 64, :], op=ALU.mult),
                             reads=[("pso", b), ("rc", b)], writes=[zk])
                    S.dma("sp", self.zT_d[4 + j, :, T * P:(T + 1) * P], zb[:], reads=[zk], writes=[("zT", 4 + j, T)])

    def phase_out_mlp(self, layer, V, w_out_ap, y_tile_fn, dst_fn, next_ab=None):
        S = self.S
        with S.scope():
            wo = S.sb("wo", [P, 8, D], BF16)
            for c in range(8):
                S.dma("pool", wo[:, c, :], w_out_ap[c * P:(c + 1) * P, :], writes=[("wo", c)])
            w1 = S.sb("w1", [P, 8, 4 * D], BF16)
            w2 = S.sb("w2", [P, 32, D], BF16)
            for c in range(8):
                S.dma("pool", w1[:, c, :], self.mlp_w1[layer, c * P:(c + 1) * P, :], writes=[("w1", c)])
            for f in range(0, 32, 4):
                S.dma("pool", w2[:, f:f + 4, :], self.mlp_w2[layer, f * P:(f + 4) * P, :].rearrange("(n p) d -> p n d", p=P),
                      writes=[("w2", f + q) for q in range(4)])
            NB = self.make_norm_bufs("nm", nb=1)
            hob = S.sb("hob", [P, 8, P], F32) if next_ab is not None else None
            zt = [S.sb(f"zt{i}", [P, 8, P], BF16) for i in range(2)]
            xt = [S.sb(f"xo{i}", [P, D], F32) for i in range(2)]
            x1 = [S.sb(f"x1_{i}", [P, D], F32) for i in range(2)]
            tmp = [S.sb(f"tg{i}", [P, D], F32) for i in range(2)]
            sq = NB["sq"][0]
            stt = [S.sb(f"ost{i}", [P, 4], F32) for i in range(4)]
            hT = [S.sb(f"hm{i}", [P, 8, 256], BF16) for i in range(1)] * 2
            py = [[S.ps(f"py{t}_{hf}", [P, 512], F32) for hf in range(2)] for t in range(2)]
            pa = [S.ps(f"pa{i}", [P, 256], F32) for i in range(2)]
            r32 = [S.sb(f"r32_{i}", [P, 256], F32) for i in range(2)]
            aT = [S.sb(f"aT{i}", [P, 256], BF16) for i in range(2)]
            ist = 0

            def norm_gate_res(t, G, xin, xin_key, xout, xout_key):
                nonlocal ist
                st = stt[ist % 4]; sk = ("ost", ist % 4)
                ist += 1
                S.op("pool", lambda e: e.memset(st[:], 0.0), writes=[sk])
                for hf in range(2):
                    S.op("act", lambda e: e.activation(out=sq[:, hf * 512:(hf + 1) * 512], in_=py[t][hf][:], func=AF.Square,
                                                       accum_out=st[:, hf:hf + 1]), reads=[("py", t, hf)], writes=[("nm", "sq", 0), sk])
                S.op("dve", lambda e: e.tensor_tensor(out=st[:, 2:3], in0=st[:, 0:1], in1=st[:, 1:2], op=ALU.add),
                     reads=[sk], writes=[sk])
                S.op("act", lambda e: e.activation(out=st[:, 2:3], in_=st[:, 2:3], func=AF.Sqrt, scale=1.0 / D,
                                                   bias=self.eps_t[:, 0:1]), reads=[sk, "eps_t"], writes=[sk])
                S.op("dve", lambda e: e.reciprocal(out=st[:, 3:4], in_=st[:, 2:3]), reads=[sk], writes=[sk])
                tb = tmp[t]; tk = ("tg", t)
                for hf in range(2):
                    S.op("dve", lambda e: e.scalar_tensor_tensor(out=tb[:, hf * 512:(hf + 1) * 512], in0=py[t][hf][:],
                                                                 scalar=st[:, 3:4], in1=G[0][:, hf * 512:(hf + 1) * 512],
                                                                 op0=ALU.mult, op1=ALU.mult),
                         reads=[("py", t, hf), sk, G[1]], writes=[tk])
                S.op("dve", lambda e: e.tensor_tensor(out=xout, in0=tb[:], in1=xin, op=ALU.add),
                     reads=[tk, xin_key], writes=[xout_key])

            ia = 0
            loaded = {}

            def load_inputs(sidx):
                loaded[sidx] = True
                for t in range(2):
                    T = 2 * sidx + t
                    src, skey = self.x_src(layer, T)
                    S.dma("sp", xt[t][:], src, reads=[skey] if skey else [], writes=[("xo", t)])
                    S.dma("sp", zt[t][:], self.zT_d[:, :, T * P:(T + 1) * P].rearrange("c p t -> p c t"),
                          reads=[("zT", c, T) for c in range(8)], writes=[("zt", t)])

            for sidx in range(NT // 2):
                T0 = 2 * sidx
                s = 1 if T0 < 2 else 0
                if layer == 1 and s == 1:
                    continue
                hb = hT[0]; hk = ("hm", 0)
                if not loaded.get(sidx):
                    load_inputs(sidx)
                for t in range(2):
                    T = T0 + t
                    y_tile_fn(T, t, zt[t], ("zt", t), wo, py[t])
                    norm_gate_res(t, V[("G", 0, s)], xt[t][:], ("xo", t), x1[t][:], ("x1", t))
                    self.norm_to_hT(NB, x1[t][:], ("x1", t), V[("A", 1, s)], V[("B", 1, s)],
                                    hb[:, :, t * P:(t + 1) * P], (hk, t))
                nxt = sidx + 1
                if nxt < NT // 2 and not (layer == 1 and nxt == 0):
                    load_inputs(nxt)
                def mm1(f):
                    a = (ia + f) % 2
                    for c in range(8):
                        mm(S, pa[a][:], w1[:, c, f * P:(f + 1) * P], hb[:, c, :], c == 0, c == 7,
                           reads=[("w1", c), (hk, 0), (hk, 1)], writes=[("pa", a)])
                    S.op("act", lambda e: e.activation(out=r32[a][:], in_=pa[a][:], func=AF.Relu),
                         reads=[("pa", a)], writes=[("r32", a)])
                    S.op("dve", lambda e: e.tensor_tensor(out=aT[a][:], in0=r32[a][:], in1=r32[a][:], op=ALU.mult),
                         reads=[("r32", a)], writes=[("aT", a)])

                def mm2(f):
                    a = (ia + f) % 2
                    for t in range(2):
                        for hf in range(2):
                            mm(S, py[t][hf][:], aT[a][:, t * P:(t + 1) * P], w2[:, f, hf * 512:(hf + 1) * 512],
                               f == 0, f == 31, reads=[("aT", a), ("w2", f)], writes=[("py", t, hf)])
                mm1(0)
                for f in range(32):
                    if f + 1 < 32:
                        mm1(f + 1)
                    mm2(f)
                for t in range(2):
                    T = T0 + t
                    norm_gate_res(t, V[("G", 1, s)], x1[t][:], ("x1", t), tmp[t][:], ("tg", t))
                    dst, dkey = dst_fn(T)
                    S.dma("sp", dst, tmp[t][:], reads=[("tg", t)], writes=[dkey])
                    if next_ab is not None:
                        ab = next_ab[s]
                        self.norm_to_hT(NB, tmp[t][:], ("tg", t), ab[0], ab[1], hob[:], "hob")
                        S.dma("sp", self.hT_d[:, :, T * P:(T + 1) * P].rearrange("c p t -> p c t"), hob[:],
                              reads=["hob"], writes=[("hT_d", T)])

    def y_tile_L0(self, T, t, zt, zk, wo, py):
        S = self.S
        for hf in range(2):
            for c in range(8):
                mm(S, py[hf][:], zt[:, c, :], wo[:, c, hf * 512:(hf + 1) * 512], c == 0, c == 7,
                   reads=[zk, ("wo", c)], writes=[("py", t, hf)])


def build_program(stop_after=None, debug=()):
    nc = bass.Bass("TRN2", target_bir_lowering=False)
    Pg = Prog(nc, debug)
    S = Pg.S
    Pg.consts()
    Pg.phase_mod()
    final_keys = []
    with S.scope():
        V0 = Pg.load_layer_vecs(0)
        Pg.phase_L0_proj(V0)
        Pg.phase_L0_pool()
        Pg.phase_L0_attn()

        def dst0(T):
            if stop_after == "L0":
                if T < 2:
                    return Pg.x_d[T * P:(T + 1) * P, :], ("x_d", T)
                return Pg.out[(T - 2) * P:(T - 1) * P, :], ("out", T)
            return Pg.x_d[T * P:(T + 1) * P, :], ("x_d", T)
        Pg.phase_out_mlp(0, V0, Pg.ev_w_out, Pg.y_tile_L0, dst0)
    S.barrier()
    S.finish([])
    S.close()
    return nc, Pg


def host_inputs(inputs):
    f = lambda a: np.ascontiguousarray(np.asarray(a, dtype=np.float32))
    dr_idx, dc_full, mask = _attn_tables()
    rpb = f(inputs["ev_rpb"])[0]
    tab = rpb[:, dr_idx, dc_full[None, :, :]]
    tab = np.ascontiguousarray(tab.transpose(2, 0, 1, 3).reshape(128, 96, 128))
    msk = np.ascontiguousarray(mask.transpose(1, 0, 2))
    bands = np.ascontiguousarray(_pool_bands().transpose(2, 0, 1, 3).reshape(128, 20, 128))
    shared = {
        "ada_w": f(inputs["ada_w"]), "ada_b": f(inputs["ada_b"]), "norm_g": f(inputs["norm_g"]),
        "mlp_w1": f(inputs["mlp_w1"]), "mlp_w2": f(inputs["mlp_w2"]),
        "ev_w_in": f(inputs["ev_w_in"])[0], "ev_w_out": f(inputs["ev_w_out"])[0],
        "ev_pool_w": f(inputs["ev_pool_w"])[0], "ev_pool_scale": f(inputs["ev_pool_scale"])[0],
        "rpb_tab": tab, "msk_tab": msk, "bands": bands, "ident": np.eye(128, dtype=np.float32),
    }
    x = f(inputs["x"]); c = f(inputs["c"]); ctx = f(inputs["ctx"]); cc = f(inputs["c_ctx"])
    maps = []
    for b in range(x.shape[0]):
        cv = np.stack([c[b].reshape(8, 128).T, cc.reshape(8, 128).T], axis=-1)
        m = dict(shared)
        m.update({"x": x[b], "ctx": ctx[b], "cvec": np.ascontiguousarray(cv)})
        maps.append(m)
    return maps


_CACHE = {}


def kernel(**inputs):
    maps = host_inputs(inputs)
    if "nc" not in _CACHE:
        _CACHE["nc"] = build_program()
    nc, Pg = _CACHE["nc"]
    res = run_bass_kernel_spmd(nc, maps, core_ids=list(range(8)))
    return np.stack([np.asarray(r["out"]) for r in res.results], axis=0)

LWC = -0.6065306597126334
GN_EPS = 64e-5


def _scan_consts():
    s = np.arange(128)[:, None]
    t = np.arange(128)[None, :]
    tri = np.stack([(s <= t), (s >= t)]).astype(np.float32)
    strict = np.stack([(s < t), (s > t)]).astype(np.float32)
    mT = strict.transpose(0, 2, 1)
    m4 = np.concatenate([tri, strict, tri, mT], axis=2)
    lm = []
    for l in range(7):
        b = 1 << l
        lm.append(((s // (2 * b)) == (t // (2 * b))) & (((s // b) % 2) == 0) & (((t // b) % 2) == 1))
    lm = np.stack(lm).astype(np.float32)
    lmN = np.stack([lm, lm.transpose(0, 2, 1)]) + np.eye(128, dtype=np.float32)[None, None]
    return tri, m4, np.ascontiguousarray(lmN)


def _tt(S, eng, out, a, b, op, reads, writes):
    return S.op(eng, lambda e: e.tensor_tensor(out=out, in0=a, in1=b, op=op), reads=reads, writes=writes)


def _stt(S, eng, out, a, sc, b, op0, op1, reads, writes):
    return S.op("dve", lambda e: e.scalar_tensor_tensor(out=out, in0=a, scalar=sc, in1=b, op0=op0, op1=op1),
                reads=reads, writes=writes)


def _act(S, out, in_, func, reads, writes, **kw):
    return S.op("act", lambda e: e.activation(out=out, in_=in_, func=func, **kw), reads=reads, writes=writes)


def _h3(ap):
    return ap.rearrange("p (h k) -> p h k", k=64)


class Prog1(Prog):
    def __init__(self, nc, debug=()):
        super().__init__(nc, debug)
        dt = nc.dram_tensor
        I = lambda name, shape: dt(name, list(shape), F32, kind="ExternalInput").ap()
        self.rw_mu = I("rw_mu", [6, D])
        self.rw_wr = I("rw_wr", [D, D]); self.rw_wk = I("rw_wk", [D, D])
        self.rw_wv = I("rw_wv", [D, D]); self.rw_wo = I("rw_wo", [D, D])
        self.rw_w0 = I("rw_w0", [2, D]); self.rw_a0 = I("rw_a0", [2, D])
        self.w1cat = I("w1cat", [D, P]); self.a1cat = I("a1cat", [D, P]); self.rw_g1 = I("rw_g1", [D, P])
        self.w2cat = I("w2cat", [P, D]); self.a2cat = I("a2cat", [P, D]); self.rw_g2 = I("rw_g2", [P, D])
        self.rw_kk = I("rw_kk", [1, D]); self.rw_ka = I("rw_ka", [1, D]); self.rw_rk = I("rw_rk", [1, D])
        self.rw_lng = I("rw_lng", [1, D]); self.rw_lnb = I("rw_lnb", [1, D])
        self.tri_c = I("tri_c", [2, P, P]); self.m4_c = I("m4_c", [2, P, 512]); self.lmT_c = I("lmT_c", [2, 7, P, P])
        X = lambda name, shape, d=F32: (dt(name, list(shape), d, kind="ExternalOutput").ap() if name in debug
                                        else dt(name, list(shape), d).ap())
        self.hT_d = X("hT_d", [8, P, NTOK])
        self.featT_d = X("featT_d", [2, NT, P, 8 * 4 * P], BF16)
        self.vtok_d = X("vtok_d", [NTOK, D], BF16)
        self.bk_d = X("bk_d", [2, NT, P, 2 * D], BF16)
        self.gC_d = X("gC_d", [2, NT, P, 8])
        self.g_d = X("g_d", [NTOK, D])
        self.bonus_d = X("bonus_d", [NTOK, D])
        self.y_d = X("y_d", [2, NTOK, D])

    def phase_R0(self, V):
        S = self.S
        with S.scope():
            NB = self.make_norm_bufs("r0")
            xt = [S.sb(f"r0x{i}", [P, D], F32) for i in range(2)]
            ho = [S.sb(f"r0h{i}", [P, 8, P], F32) for i in range(2)]
            for T in range(NT):
                s = 1 if T < 2 else 0
                b = T % 2
                src, sk = self.x_src(1, T)
                S.dma("sp", xt[b][:], src, reads=[sk], writes=[("r0x", b)])
                self.norm_to_hT(NB, xt[b][:], ("r0x", b), V[("A", 0, s)], V[("B", 0, s)], ho[b][:], ("r0h", b))
                S.dma("sp", self.hT_d[:, :, T * P:(T + 1) * P].rearrange("c p t -> p c t"), ho[b][:],
                      reads=[("r0h", b)], writes=[("hT_d", T)])

    def phase_R1(self):
        S = self.S
        with S.scope():
            W = {}
            for nm, src in (("wr", self.rw_wr), ("wk", self.rw_wk), ("wv", self.rw_wv)):
                W[nm] = S.sb(nm, [P, 8, D], BF16)
                for c in range(0, 8, 4):
                    S.dma("pool", W[nm][:, c:c + 4, :], src[c * P:(c + 4) * P, :].rearrange("(c p) n -> p c n", p=P), writes=[nm])
            for nm, src in (("w1c", self.w1cat), ("a1c", self.a1cat), ("g1", self.rw_g1)):
                W[nm] = S.sb(nm, [P, 8, P], BF16)
                S.dma("pool", W[nm][:], src.rearrange("(c p) n -> p c n", p=P), writes=[nm])
            for nm, src in (("w2c", self.w2cat), ("a2c", self.a2cat), ("g2", self.rw_g2)):
                W[nm] = S.sb(nm, [P, D], BF16)
                S.dma("pool", W[nm][:], src, writes=[nm])
            R = {}
            for nm, src in (("kk_r", self.rw_kk), ("ka_r", self.rw_ka), ("rk_r", self.rw_rk),
                            ("w0_0", self.rw_w0[0:1, :]), ("w0_1", self.rw_w0[1:2, :]),
                            ("a0_0", self.rw_a0[0:1, :]), ("a0_1", self.rw_a0[1:2, :])):
                R[nm] = S.sb(nm, [P, D], F32)
                S.dma("sp", R[nm][:], src.broadcast_to([P, D]), writes=[nm])
            mu = S.sb("mu", [P, 6, 8], F32)
            with self.nc.allow_non_contiguous_dma(reason="tiny"):
                S.dma("sp", mu[:], self.rw_mu.rearrange("j (c p) -> p j c", p=P), writes=["mu"])
            tri = S.sb("tri", [P, 2, P], F32)
            S.dma("sp", tri[:], self.tri_c.rearrange("d s t -> s d t"), writes=["tri"])
            onef = S.sb("onef", [P, P], F32)
            S.op("dve", lambda e: e.memset(onef[:], 1.0), writes=["onef"])
            hbuf = S.sb("hbuf", [P, 8, P + 2], F32)
            xx = S.sb("xx", [P, 8, P], F32)
            mix = S.sb("mix", [P, 6, 8, P], BF16)
            hid = S.sb("hid", [P, 3, P], BF16)
            F = {n: S.sb(n, [P, D], F32) for n in ("r_sb", "k_sb", "v_sb", "kkn", "tA", "tB", "lw", "tC0", "tC1", "tD", "kd0", "kd1", "tE0", "tE1", "tF0", "tF1", "tG", "tH")}
            ob = [S.sb(f"ob{i}", [P, D], BF16) for i in range(4)]
            vb = S.sb("vb", [P, D], BF16)
            ft = S.sb("ft", [P, 8, 4, P], BF16)
            bkt = S.sb("bkt", [P, 2, D], BF16)
            st16 = S.sb("st16", [P, 64], F32)
            gcs = S.sb("gcs", [P, 8], F32)
            pA = [[S.ps(f"pA{i}_{h}", [P, 512], F32) for h in range(2)] for i in range(2)]
            pCl = [S.ps(f"pCl{h}", [P, 512], F32) for h in range(2)]
            pF = S.ps("pF", [P, 512], F32)
            pT = S.ps("pT", [P, 8, P], BF16)
            ipa = 0

            def proj(lhs_fn, rhs, rkey, K0=0, K=P, nchunks=8, lkeys=()):
                nonlocal ipa
                i = ipa % 2
                ipa += 1
                for hf in range(2):
                    for c in range(nchunks):
                        mm(S, pA[i][hf][:], lhs_fn(c), rhs(c, hf), c == 0, c == nchunks - 1,
                           reads=list(lkeys) + [rkey], writes=[("pA", i, hf)])
                return pA[i], [("pA", i, 0), ("pA", i, 1)]

            def evac2(fn_half):
                for hf in range(2):
                    fn_half(hf, slice(hf * 512, (hf + 1) * 512))

            for T in range(NT):
                seq_lo, seq_hi = (0, NCTX) if T < 2 else (NCTX, NTOK)
                t0 = T * P
                lo = max(t0 - 1, seq_lo); hi = min(t0 + P + 1, seq_hi)
                if lo > t0 - 1:
                    S.op("pool", lambda e: e.memset(hbuf[:, :, 0:1], 0.0), writes=["hbuf"])
                if hi < t0 + P + 1:
                    S.op("pool", lambda e: e.memset(hbuf[:, :, P + 1:P + 2], 0.0), writes=["hbuf"])
                S.dma("pool", hbuf[:, :, lo - (t0 - 1):hi - (t0 - 1)], self.hT_d[:, :, lo:hi].rearrange("c p t -> p c t"),
                      reads=[("hT_d", q) for q in range(max(T - 1, 0), min(T + 2, NT))], writes=["hbuf"])
                _tt(S, "dve", xx[:], hbuf[:, :, 0:P], hbuf[:, :, 2:P + 2], ALU.add, ["hbuf"], ["xx"])
                _stt(S, "dve", xx[:], xx[:], 0.5, hbuf[:, :, 1:P + 1], ALU.mult, ALU.subtract, ["xx", "hbuf"], ["xx"])
                for j in range(6):
                    mxt = F["tG"][:].rearrange("p (c t) -> p c t", t=P)
                    _tt(S, "dve", mxt, xx[:], mu[:, j, :][:, :, None].broadcast_to([P, 8, P]), ALU.mult, ["xx", "mu"], ["tG"])
                    _tt(S, "dve", mix[:, j, :, :], mxt, hbuf[:, :, 1:P + 1], ALU.add, ["tG", "hbuf"], [("mix", j)])
                for hi_, (wn, mj, fn) in enumerate((("w1c", 1, AF.Tanh), ("a1c", 4, AF.Copy), ("g1", 5, AF.Sigmoid))):
                    for c in range(8):
                        mm(S, pF[:, 0:P], W[wn][:, c, :], mix[:, mj, c, :], c == 0, c == 7,
                           reads=[wn, ("mix", mj)], writes=["pF"])
                    _act(S, hid[:, hi_, :], pF[:, 0:P], fn, ["pF"], [("hid", hi_)])
                for nm, mj, wn in (("r_sb", 0, "wr"), ("k_sb", 2, "wk"), ("v_sb", 3, "wv")):
                    ps, pk = proj(lambda c: mix[:, mj, c, :], lambda c, hf: W[wn][:, c, hf * 512:(hf + 1) * 512], wn,
                                  lkeys=[("mix", mj)])
                    evac2(lambda hf, sl: _act(S, F[nm][:, sl], ps[hf][:], AF.Copy, [pk[hf]], [nm]))
                S.op("pool", lambda e: e.tensor_copy(out=vb[:], in_=F["v_sb"][:]), reads=["v_sb"], writes=["vb"])
                S.dma("sp", self.vtok_d[t0:t0 + P, :], vb[:], reads=["vb"], writes=[("vtok", T)])
                ps, pk = proj(lambda c: hid[:, 2, :], lambda c, hf: W["g2"][:, hf * 512:(hf + 1) * 512], "g2", nchunks=1,
                              lkeys=[("hid", 2)])
                evac2(lambda hf, sl: _act(S, F["tA"][:, sl], ps[hf][:], AF.Copy, [pk[hf]], ["tA"]))
                S.dma("sp", self.g_d[t0:t0 + P, :], F["tA"][:], reads=["tA"], writes=[("g_d", T)])
                _tt(S, "dve", F["tA"][:], F["k_sb"][:], R["kk_r"][:], ALU.mult, ["k_sb", "kk_r"], ["tA"])
                _tt(S, "dve", F["tB"][:], F["tA"][:], F["tA"][:], ALU.mult, ["tA"], ["tB"])
                S.op("dve", lambda e: e.tensor_reduce(out=st16[:, 0:16], in_=_h3(F["tB"][:]), axis=AX.X, op=ALU.add),
                     reads=["tB"], writes=["st16"])
                S.op("dve", lambda e: e.tensor_scalar(out=st16[:, 0:16], in0=st16[:, 0:16], scalar1=1e-24, scalar2=None, op0=ALU.max),
                     reads=["st16"], writes=["st16"])
                _act(S, st16[:, 0:16], st16[:, 0:16], AF.Sqrt, ["st16"], ["st16"])
                S.op("dve", lambda e: e.reciprocal(out=st16[:, 16:32], in_=st16[:, 0:16]), reads=["st16"], writes=["st16"])
                _tt(S, "dve", _h3(F["kkn"][:]), _h3(F["tA"][:]), st16[:, 16:32][:, :, None].broadcast_to([P, 16, 64]), ALU.mult,
                    ["tA", "st16"], ["kkn"])
                for d in range(2):
                    ps, pk = proj(lambda c: hid[d * 64:(d + 1) * 64, 0, :], lambda c, hf: W["w2c"][d * 64:(d + 1) * 64, hf * 512:(hf + 1) * 512],
                                  "w2c", nchunks=1, lkeys=[("hid", 0)])
                    evac2(lambda hf, sl: _tt(S, "dve", F["tB"][:, sl], ps[hf][:], R[f"w0_{d}"][:, sl], ALU.add, [pk[hf], f"w0_{d}"], ["tB"]))
                    _act(S, F["tB"][:], F["tB"][:], AF.Sigmoid, ["tB"], ["tB"])
                    _act(S, F["lw"][:], F["tB"][:], AF.Copy, ["tB"], ["lw"], scale=LWC)
                    ps, pk = proj(lambda c: hid[d * 64:(d + 1) * 64, 1, :], lambda c, hf: W["a2c"][d * 64:(d + 1) * 64, hf * 512:(hf + 1) * 512],
                                  "a2c", nchunks=1, lkeys=[("hid", 1)])
                    evac2(lambda hf, sl: _tt(S, "dve", F[f"tC{d}"][:, sl], ps[hf][:], R[f"a0_{d}"][:, sl], ALU.add, [pk[hf], f"a0_{d}"], [f"tC{d}"]))
                    _act(S, F[f"tC{d}"][:], F[f"tC{d}"][:], AF.Sigmoid, [f"tC{d}"], [f"tC{d}"])
                    kd = F[f"kd{d}"]; kdk = f"kd{d}"
                    _stt(S, "dve", F["tD"][:], F[f"tC{d}"][:], -1.0, R["ka_r"][:], ALU.add, ALU.mult, [f"tC{d}", "ka_r"], ["tD"])
                    _stt(S, "pool", kd[:], F["tD"][:], 1.0, F["k_sb"][:], ALU.add, ALU.mult, ["tD", "k_sb"], [kdk])
                    _tt(S, "dve", F[f"tC{d}"][:], F["kkn"][:], F[f"tC{d}"][:], ALU.mult, ["kkn", f"tC{d}"], [f"tC{d}"])
                    for hf in range(2):
                        mm(S, pCl[hf][:], tri[:, d, :], F["lw"][:, hf * 512:(hf + 1) * 512], True, True,
                           reads=["tri", "lw"], writes=[("pCl", hf)])
                    evac2(lambda hf, sl: _act(S, F[f"tE{d}"][:, sl], pCl[hf][:], AF.Exp, [("pCl", hf)], [f"tE{d}"]))
                    evac2(lambda hf, sl: _act(S, F[f"tF{d}"][:, sl], pCl[hf][:], AF.Exp, [("pCl", hf)], [f"tF{d}"], scale=-1.0))
                    for hf in range(2):
                        mm(S, pCl[hf][:], onef[:], F["lw"][:, hf * 512:(hf + 1) * 512], True, True,
                           reads=["onef", "lw"], writes=[("pCl", hf)])
                    evac2(lambda hf, sl: _act(S, F["tH"][:, sl], pCl[hf][:], AF.Exp, [("pCl", hf)], ["tH"]))
                    _act(S, F["tG"][:], F["lw"][:], AF.Exp, ["lw"], ["tG"], scale=-1.0)
                    _tt(S, "dve", F["tG"][:], F["tG"][:], F[f"tE{d}"][:], ALU.mult, ["tG", f"tE{d}"], ["tG"])
                    _tt(S, "dve", F["tH"][:], F["tH"][:], F[f"tF{d}"][:], ALU.mult, ["tH", f"tF{d}"], ["tH"])
                    for j in range(8):
                        mm(S, pF[:, 256 + j:257 + j], F["lw"][:, j * P:(j + 1) * P], onef[:, 0:1], True, True,
                           reads=["lw", "onef"], writes=["pF"])
                    _act(S, gcs[:], pF[:, 256:264], AF.Exp, ["pF"], ["gcs"])
                    S.dma("sp", self.gC_d[d, T], gcs[:], reads=["gcs"], writes=[("gC_d", d, T)])
                    _stt(S, "dve", ob[0][:], F["kkn"][:], -1.0, F["tG"][:], ALU.mult, ALU.mult, ["kkn", "tG"], [("ob", 0)])
                    _tt(S, "dve", ob[1][:], F["r_sb"][:], F[f"tE{d}"][:], ALU.mult, ["r_sb", f"tE{d}"], [("ob", 1)])
                    _tt(S, "dve", ob[2][:], F[f"tC{d}"][:], F[f"tF{d}"][:], ALU.mult, [f"tC{d}", f"tF{d}"], [("ob", 2)])
                    _tt(S, "dve", ob[3][:], kd[:], F[f"tF{d}"][:], ALU.mult, [kdk, f"tF{d}"], [("ob", 3)])
                    _tt(S, "dve", bkt[:, 0, :], F[f"tC{d}"][:], F["tH"][:], ALU.mult, [f"tC{d}", "tH"], ["bkt"])
                    _tt(S, "dve", bkt[:, 1, :], kd[:], F["tH"][:], ALU.mult, [kdk, "tH"], ["bkt"])
                    S.dma("sp", self.bk_d[d, T], bkt[:].rearrange("p a n -> p (a n)"), reads=["bkt"], writes=[("bk_d", d, T)])
                    for q in range(4):
                        for c in range(8):
                            S.op("pe", lambda e: e.transpose(out=pT[:, c, :], in_=ob[q][:, c * P:(c + 1) * P], identity=self.idb[:]),
                                 reads=[("ob", q), "idb"], writes=["pT"], accum=(c > 0))
                        if q % 2 == 0:
                            _act(S, ft[:, :, q, :], pT[:], AF.Copy, ["pT"], ["ft"])
                        else:
                            S.op("dve", lambda e: e.tensor_copy(out=ft[:, :, q, :], in_=pT[:]), reads=["pT"], writes=["ft"])
                    S.dma("sp", self.featT_d[d, T], ft[:].rearrange("p j q t -> p (j q t)"), reads=["ft"], writes=[("featT_d", d, T)])
                _tt(S, "dve", F["tD"][:], F["kd0"][:], F["kd1"][:], ALU.add, ["kd0", "kd1"], ["tD"])
                _tt(S, "dve", F["tD"][:], F["tD"][:], F["r_sb"][:], ALU.mult, ["tD", "r_sb"], ["tD"])
                _tt(S, "dve", F["tD"][:], F["tD"][:], R["rk_r"][:], ALU.mult, ["tD", "rk_r"], ["tD"])
                S.op("dve", lambda e: e.tensor_reduce(out=st16[:, 32:48], in_=_h3(F["tD"][:]), axis=AX.X, op=ALU.add),
                     reads=["tD"], writes=["st16"])
                _tt(S, "dve", _h3(F["tD"][:]), _h3(F["v_sb"][:]), st16[:, 32:48][:, :, None].broadcast_to([P, 16, 64]), ALU.mult,
                    ["v_sb", "st16"], ["tD"])
                S.dma("sp", self.bonus_d[t0:t0 + P, :], F["tD"][:], reads=["tD"], writes=[("bonus_d", T)])

    def phase_R2(self):
        S = self.S
        with S.scope():
            m4 = S.sb("m4", [P, 2, 512], F32)
            lmN = S.sb("lmN", [P, 2, 7, P], F32)
            S.dma("sp", m4[:], self.m4_c.rearrange("d s n -> s d n"), writes=["m4"])
            S.dma("sp", lmN[:], self.lmT_c.rearrange("d l s n -> s d l n"), writes=["lmN"])
            idb = self.idb
            NG = 4
            I4 = S.sb("I4", [P, NG, P], BF16)
            for g in range(NG):
                S.op("pool", lambda e: e.tensor_copy(out=I4[:, g, :], in_=idb[:]), reads=["idb"], writes=["I4"])
            I4f = I4[:].rearrange("p g t -> p (g t)")
            ST32 = [S.sb(f"ST32_{d}", [P, 8, 64], F32) for d in range(2)]
            STb = [S.sb(f"STb_{d}", [P, 8, 64], BF16) for d in range(2)]
            for d in range(2):
                S.op("dve", lambda e: e.memset(ST32[d][:], 0.0), writes=[("ST32", d)])
                S.op("dve", lambda e: e.memset(STb[d][:], 0.0), writes=[("STb", d)])
            NBUF = 3
            Fb = [S.sb(f"Fb{i}", [P, 8, 4, P], BF16) for i in range(NBUF)]
            Vb = [S.sb(f"Vb{i}", [P, D], BF16) for i in range(NBUF)]
            BKb = [S.sb(f"BKb{i}", [P, 2, D], BF16) for i in range(NBUF)]
            gCb = [S.sb(f"gCb{i}", [P, 8], F32) for i in range(NBUF)]
            ysb = [S.sb(f"ysb{i}", [P, D], F32) for i in range(NBUF)]
            SL = []
            for sl in range(2):
                R_ = dict(
                    GM=S.sb(f"GM{sl}", [P, NG, 512], BF16),
                    X=[S.sb(f"X{sl}_{i}", [P, NG, P], BF16) for i in range(2)],
                    XT=[S.sb(f"XT{sl}_{i}", [P, NG, P], BF16) for i in range(2)],
                    T1s=S.sb(f"T1s{sl}", [P, NG, P], BF16),
                    Zq=S.sb(f"Zq{sl}", [P, NG, 64], BF16),
                    Pb=S.sb(f"Pb{sl}", [P, NG, 64], BF16),
                    bk=[S.ps(f"bk{sl}_{i}", [P, NG, P], F32) for i in range(3)],
                    bz=S.ps(f"bz{sl}", [P, 8, 64], F32),
                    sl=sl)
                SL.append(R_)

            items = []
            it = 0
            for d in range(2):
                order = list(range(NT)) if d == 0 else [1, 0] + list(range(NT - 1, 1, -1))
                for ci, T in enumerate(order):
                    for g0 in range(0, 16, NG):
                        items.append(dict(d=d, T=T, g0=g0, b=it % NBUF))
                    it += 1

            def heads_of(g0):
                return [(g, g0 + g, (g0 + g) // 2, ((g0 + g) % 2) * 64) for g in range(NG)]

            def load_chunk(w):
                d, T, b = w["d"], w["T"], w["b"]
                S.dma("sp", Fb[b][:].rearrange("p j q t -> p (j q t)"), self.featT_d[d, T], reads=[("featT_d", d, T)], writes=[("Fb", b)])
                S.dma("sp", Vb[b][:], self.vtok_d[T * P:(T + 1) * P, :], reads=[("vtok", T)], writes=[("Vb", b)])
                S.dma("sp", BKb[b][:].rearrange("p a n -> p (a n)"), self.bk_d[d, T], reads=[("bk_d", d, T)], writes=[("BKb", b)])
                S.dma("sp", gCb[b][:], self.gC_d[d, T], reads=[("gC_d", d, T)], writes=[("gCb", b)])

            def run_group(w, R_):
                d, T, b, g0, sl = w["d"], w["T"], w["b"], w["g0"], R_["sl"]
                if g0 == 0:
                    load_chunk(w)
                Fk, Vk, BKk, gk = ("Fb", b), ("Vb", b), ("BKb", b), ("gCb", b)
                GM, X, XT, T1s, Zq, Pb, bk, bz = (R_[n] for n in ("GM", "X", "XT", "T1s", "Zq", "Pb", "bk", "bz"))
                K = lambda n, *a: (n, sl) + a
                hs = heads_of(g0)
                F_ = Fb[b]
                st32, stb = ST32[d], STb[d]
                for (g, h, j, pb_) in hs:
                    bank = bk[g % 3]; bkk = K("bk", g % 3)
                    bv = bank[:].rearrange("p g t -> p (g t)")
                    AR = F_[pb_:pb_ + 64, j, 0:2, :].rearrange("p q t -> p (q t)")
                    mm(S, bv[:, 0:128], F_[pb_:pb_ + 64, j, 2, :], F_[pb_:pb_ + 64, j, 1, :], True, True, reads=[Fk], writes=[bkk])
                    mm(S, bv[:, 128:384], F_[pb_:pb_ + 64, j, 3, :], AR, True, True, reads=[Fk], writes=[bkk])
                    mm(S, bv[:, 384:512], F_[pb_:pb_ + 64, j, 0, :], F_[pb_:pb_ + 64, j, 2, :], True, True, reads=[Fk], writes=[bkk])
                    _tt(S, "dve", GM[:, g, :], bv, m4[:, d, :], ALU.mult, ["m4"], [bkk, K("GM")])
                yield
                for (g, h, j, pb_) in hs:
                    mm(S, bz[:, g, :], F_[pb_:pb_ + 64, j, 0, :], stb[pb_:pb_ + 64, j, :], True, False, reads=[Fk, ("STb", d)], writes=[K("bz")])
                    mm(S, bz[:, g, :], GM[:, g, 128:256], Vb[b][:, h * 64:(h + 1) * 64], False, True, reads=[K("GM"), Vk], writes=[K("bz")])
                _act(S, Zq[:], bz[:, 0:NG, :], AF.Copy, [], [K("bz"), K("Zq")])
                yield
                xi = 0
                mm(S, bk[0][:].rearrange("p g t -> p (g t)"), idb[:], I4f, True, False, reads=["idb", "I4"], writes=[K("bk", 0)])
                for (g, h, j, pb_) in hs:
                    mm(S, bk[0][:, g, :], GM[:, g, 384:512], idb[:], False, True, reads=[K("GM"), "idb"], writes=[K("bk", 0)])
                _tt(S, "dve", X[xi][:], bk[0][:], lmN[:, d, 0:1, :].broadcast_to([P, NG, P]), ALU.mult, ["lmN"], [K("bk", 0), K("X", xi)])
                yield
                for (g, h, j, pb_) in hs:
                    mm(S, bk[2][:, g, :], X[xi][:, g, :], idb[:], True, True, reads=[K("X", xi), "idb"], writes=[K("bk", 2)])
                _act(S, XT[xi][:], bk[2][:], AF.Copy, [], [K("bk", 2), K("XT", xi)])
                yield
                for l in range(1, 7):
                    mm(S, bk[0][:].rearrange("p g t -> p (g t)"), idb[:], I4f, True, False, reads=["idb", "I4"], writes=[K("bk", 0)])
                    for (g, h, j, pb_) in hs:
                        mm(S, bk[0][:, g, :], GM[:, g, 384:512], X[xi][:, g, :], False, True, reads=[K("GM"), K("X", xi)], writes=[K("bk", 0)])
                    _tt(S, "dve", T1s[:], bk[0][:], lmN[:, d, l:l + 1, :].broadcast_to([P, NG, P]), ALU.mult, ["lmN"], [K("bk", 0), K("T1s")])
                    yield
                    for (g, h, j, pb_) in hs:
                        mm(S, bk[1][:, g, :], XT[xi][:, g, :], T1s[:, g, :], True, True, reads=[K("XT", xi), K("T1s")], writes=[K("bk", 1)])
                    if l < 6:
                        for (g, h, j, pb_) in hs:
                            mm(S, bk[2][:, g, :], T1s[:, g, :], XT[xi][:, g, :], True, True, reads=[K("XT", xi), K("T1s")], writes=[K("bk", 2)])
                    _act(S, X[1 - xi][:], bk[1][:], AF.Copy, [], [K("bk", 1), K("X", 1 - xi)])
                    if l < 6:
                        if l % 3 != 0:
                            _act(S, XT[1 - xi][:], bk[2][:], AF.Copy, [], [K("bk", 2), K("XT", 1 - xi)])
                        else:
                            S.op("dve", lambda e: e.tensor_copy(out=XT[1 - xi][:], in_=bk[2][:]), reads=[], writes=[K("bk", 2), K("XT", 1 - xi)])
                    xi = 1 - xi
                    yield
                for (g, h, j, pb_) in hs:
                    mm(S, bz[:, g, :], X[xi][:, g, :], Zq[:, g, :], True, True, reads=[K("X", xi), K("Zq")], writes=[K("bz")])
                S.op("dve", lambda e: e.tensor_copy(out=Pb[:], in_=bz[:, 0:NG, :]), reads=[], writes=[K("bz"), K("Pb")])
                yield
                for (g, h, j, pb_) in hs:
                    yo = bz[:, g, :]
                    mm(S, yo, GM[:, g, 0:128], Pb[:, g, :], True, False, reads=[K("GM"), K("Pb")], writes=[K("bz")])
                    mm(S, yo, GM[:, g, 256:384], Vb[b][:, h * 64:(h + 1) * 64], False, False, reads=[K("GM"), Vk], writes=[K("bz")])
                    mm(S, yo, F_[pb_:pb_ + 64, j, 1, :], stb[pb_:pb_ + 64, j, :], False, True, reads=[Fk, ("STb", d)], writes=[K("bz")])
                for (g, h, j, pb_) in hs:
                    mm(S, bz[:, 4 + g, :], BKb[b][:, 0, j * P:(j + 1) * P], Pb[:, g, :], True, False, reads=[BKk, K("Pb")], writes=[K("bz")])
                    mm(S, bz[:, 4 + g, :], BKb[b][:, 1, j * P:(j + 1) * P], Vb[b][:, h * 64:(h + 1) * 64], False, True,
                       reads=[BKk, Vk], writes=[K("bz")])
                _act(S, ysb[b][:, g0 * 64:(g0 + NG) * 64].rearrange("p (g v) -> p g v", v=64), bz[:, 0:NG, :], AF.Copy, [], [K("bz"), ("ysb", b)])
                for (g, h, j, pb_) in hs:
                    _stt(S, "dve", st32[pb_:pb_ + 64, j, :], st32[pb_:pb_ + 64, j, :], gCb[b][pb_:pb_ + 64, j:j + 1],
                         bz[pb_:pb_ + 64, 4 + g, :], ALU.mult, ALU.add, [gk], [("ST32", d), K("bz")])
                S.op("pool", lambda e: e.tensor_copy(out=stb[:, g0 // 2:g0 // 2 + 2, :], in_=st32[:, g0 // 2:g0 // 2 + 2, :]),
                     reads=[("ST32", d)], writes=[("STb", d)])
                if g0 + NG == 16:
                    S.dma("pool", self.y_d[d, T * P:(T + 1) * P, :], ysb[b][:], reads=[("ysb", b)], writes=[("y_d", d, T)])
                yield

            nxt = 0
            active = [None, None]
            while True:
                progressed = False
                for sl in range(2):
                    if active[sl] is None and nxt < len(items):
                        active[sl] = run_group(items[nxt], SL[sl])
                        nxt += 1
                    if active[sl] is not None:
                        progressed = True
                        try:
                            next(active[sl])
                        except StopIteration:
                            active[sl] = None
                if not progressed:
                    break

    def phase_R3(self):
        S = self.S
        with S.scope():
            R = {}
            for nm, src in (("lng_r", self.rw_lng), ("lnb_r", self.rw_lnb)):
                R[nm] = S.sb(nm, [P, D], F32)
                S.dma("sp", R[nm][:], src.broadcast_to([P, D]), writes=[nm])
            B = [{n: S.sb(f"{n}{i}", [P, D], F32) for n in ("yf", "yb", "gg", "bo")} for i in range(2)]
            zb = [S.sb(f"zb{i}", [P, D], BF16) for i in range(2)]
            zt = [S.sb(f"zt3_{i}", [P, 8, P], BF16) for i in range(2)]
            st = [S.sb(f"st3_{i}", [P, 64], F32) for i in range(2)]
            pT = [S.ps(f"pT3_{i}", [P, 8, P], BF16) for i in range(2)]
            for T in range(2, NT):
                b = T % 2
                Bf = B[b]
                k = lambda n: (n, b)
                t0 = T * P
                S.dma("pool", Bf["yf"][:], self.y_d[0, t0:t0 + P, :], reads=[("y_d", 0, T)], writes=[k("yf")])
                S.dma("pool", Bf["yb"][:], self.y_d[1, t0:t0 + P, :], reads=[("y_d", 1, T)], writes=[k("yb")])
                S.dma("pool", Bf["gg"][:], self.g_d[t0:t0 + P, :], reads=[("g_d", T)], writes=[k("gg")])
                S.dma("pool", Bf["bo"][:], self.bonus_d[t0:t0 + P, :], reads=[("bonus_d", T)], writes=[k("bo")])
                y = Bf["yf"]; t2 = Bf["yb"]
                _tt(S, "dve", y[:], y[:], t2[:], ALU.add, [k("yf"), k("yb")], [k("yf")])
                S.op("dve", lambda e: e.tensor_reduce(out=st[b][:, 0:16], in_=_h3(y[:]), axis=AX.X, op=ALU.add), reads=[k("yf")], writes=[k("st")])
                S.op("dve", lambda e: e.tensor_scalar(out=st[b][:, 0:16], in0=st[b][:, 0:16], scalar1=-1.0 / 64, scalar2=None, op0=ALU.mult),
                     reads=[k("st")], writes=[k("st")])
                _tt(S, "dve", _h3(y[:]), _h3(y[:]), st[b][:, 0:16][:, :, None].broadcast_to([P, 16, 64]), ALU.add, [k("yf"), k("st")], [k("yf")])
                _tt(S, "pool", t2[:], y[:], y[:], ALU.mult, [k("yf")], [k("yb")])
                S.op("dve", lambda e: e.tensor_reduce(out=st[b][:, 16:32], in_=_h3(t2[:]), axis=AX.X, op=ALU.add), reads=[k("yb")], writes=[k("st")])
                S.op("dve", lambda e: e.tensor_scalar(out=st[b][:, 16:32], in0=st[b][:, 16:32], scalar1=1.0 / 64, scalar2=GN_EPS, op0=ALU.mult, op1=ALU.add),
                     reads=[k("st")], writes=[k("st")])
                _act(S, st[b][:, 16:32], st[b][:, 16:32], AF.Sqrt, [k("st")], [k("st")])
                S.op("dve", lambda e: e.reciprocal(out=st[b][:, 32:48], in_=st[b][:, 16:32]), reads=[k("st")], writes=[k("st")])
                _tt(S, "dve", _h3(y[:]), _h3(y[:]), st[b][:, 32:48][:, :, None].broadcast_to([P, 16, 64]), ALU.mult, [k("yf"), k("st")], [k("yf")])
                _tt(S, "pool", y[:], y[:], R["lng_r"][:], ALU.mult, [k("yf"), "lng_r"], [k("yf")])
                _tt(S, "pool", y[:], y[:], R["lnb_r"][:], ALU.add, [k("yf"), "lnb_r"], [k("yf")])
                _tt(S, "dve", y[:], y[:], Bf["bo"][:], ALU.add, [k("yf"), k("bo")], [k("yf")])
                _tt(S, "dve", zb[b][:], y[:], Bf["gg"][:], ALU.mult, [k("yf"), k("gg")], [k("zb")])
                for c in range(8):
                    S.op("pe", lambda e: e.transpose(out=pT[b][:, c, :], in_=zb[b][:, c * P:(c + 1) * P], identity=self.idb[:]),
                         reads=[k("zb"), "idb"], writes=[k("pT3")], accum=(c > 0))
                _act(S, zt[b][:], pT[b][:], AF.Copy, [k("pT3")], [k("zt3")])
                S.dma("sp", self.zT_d[:, :, t0:t0 + P].rearrange("c p t -> p c t"), zt[b][:], reads=[k("zt3")],
                      writes=[("zT", c, T) for c in range(8)])


def build_program(stop_after=None, debug=(), phases="M0ABCD1abcde"):
    nc = bass.Bass("TRN2", target_bir_lowering=False)
    Pg = Prog1(nc, debug)
    S = Pg.S
    Pg.consts()
    if "M" in phases:
        Pg.phase_mod()
    if "0" in phases:
      with S.scope():
        V0 = Pg.load_layer_vecs(0)
        if "A" in phases: Pg.phase_L0_proj(V0)
        if "B" in phases: Pg.phase_L0_pool()
        if "C" in phases: Pg.phase_L0_attn()
        if "D" in phases:
            ab1 = Pg.load_ab(1, 0, "ab1")
            Pg.phase_out_mlp(0, V0, Pg.ev_w_out, Pg.y_tile_L0, lambda T: (Pg.x_d[T * P:(T + 1) * P, :], ("x_d", T)), next_ab=ab1)
    if "1" in phases:
      with S.scope():
        V1 = Pg.load_layer_vecs(1, part="ab")
        if "a" in phases and "D" not in phases: Pg.phase_R0(V1)
        if "b" in phases: Pg.phase_R1()
        if "c" in phases: Pg.phase_R2()
        if "d" in phases: Pg.phase_R3()
        if "e" in phases:
            Pg.load_layer_vecs(1, V=V1, part="g")
        if "e" in phases: Pg.phase_out_mlp(1, V1, Pg.rw_wo, Pg.y_tile_L0, lambda T: (Pg.out[(T - 2) * P:(T - 1) * P, :], ("out", T)))
    S.barrier()
    S.finish([])
    S.close()
    return nc, Pg


_host_inputs0 = host_inputs


def host_inputs(inputs):
    maps = _host_inputs0(inputs)
    f = lambda a: np.ascontiguousarray(np.asarray(a, dtype=np.float32))
    tri, m4, lmT = _scan_consts()
    sh = {
        "rw_mu": f(inputs["rw_mu"])[0], "rw_wr": f(inputs["rw_wr"])[0], "rw_wk": f(inputs["rw_wk"])[0],
        "rw_wv": f(inputs["rw_wv"])[0], "rw_wo": f(inputs["rw_wo"])[0],
        "rw_w0": f(inputs["rw_w0"])[0], "rw_a0": f(inputs["rw_a0"])[0],
        "w1cat": f(np.concatenate([inputs["rw_w1"][0, 0], inputs["rw_w1"][0, 1]], axis=1)),
        "a1cat": f(np.concatenate([inputs["rw_a1"][0, 0], inputs["rw_a1"][0, 1]], axis=1)),
        "rw_g1": f(inputs["rw_g1"])[0],
        "w2cat": f(np.asarray(inputs["rw_w2"])[0].reshape(128, 1024)), "a2cat": f(np.asarray(inputs["rw_a2"])[0].reshape(128, 1024)),
        "rw_g2": f(inputs["rw_g2"])[0],
        "rw_kk": f(inputs["rw_kk"]).reshape(1, 1024), "rw_ka": f(inputs["rw_ka"]).reshape(1, 1024),
        "rw_rk": f(inputs["rw_rk"]).reshape(1, 1024), "rw_lng": f(inputs["rw_lng"]).reshape(1, 1024),
        "rw_lnb": f(inputs["rw_lnb"]).reshape(1, 1024),
        "tri_c": f(tri), "m4_c": f(m4), "lmT_c": f(lmT),
    }
    for m in maps:
        m.update(sh)
    return maps
```

```python
import contextlib
import numpy as np
import concourse.bass as bass
import concourse.mybir as mybir

F32 = mybir.dt.float32
BF16 = mybir.dt.bfloat16
AF = mybir.ActivationFunctionType
ALU = mybir.AluOpType
AX = mybir.AxisListType

SEM_LIMIT = 10000


class _Ctr:
    def __init__(self, S, name, step):
        self.S = S
        self.name = name
        self.step = step
        self.gen = 0
        self.sem = S._newsem(f"{name}_0")
        self.val = 0

    def next_event(self):
        if self.val + self.step > SEM_LIMIT:
            self.gen += 1
            self.sem = self.S._newsem(f"{self.name}_{self.gen}")
            self.val = 0
        self.val += self.step
        return (self.sem, self.val)


class _PsView:
    def __init__(self, t, shape):
        self.t = t
        self.n1 = shape[1]

    def __getitem__(self, key):
        if not isinstance(key, tuple):
            key = (key,)
        key = list(key)
        if len(key) < 2:
            key.append(slice(None))
        k1 = key[1]
        if isinstance(k1, slice):
            start, stop, step = k1.indices(self.n1)
            key[1] = slice(start, stop, step)
        return self.t[tuple(key)]


class _Eng:
    def __init__(self, S, name, obj):
        self.name = name
        self.obj = obj
        self.ctr = _Ctr(S, "s_" + name, 1)
        self.seen = {}
        self.n_issued = 0
        self.last_ins = None
        self.last_has_inc = False
        self.inc_idx = []
        self.inc_ev = []


class LazyEv:
    __slots__ = ("eng", "idx")

    def __init__(self, eng, idx):
        self.eng = eng
        self.idx = idx


class _Res:
    __slots__ = ("w", "r")

    def __init__(self):
        self.w = None
        self.r = {}


class Sched:
    def __init__(self, nc, n_dma_slots=8):
        self.nc = nc
        self.stack = contextlib.ExitStack()
        self.scopes = [self.stack]
        self.res = {}
        self.engs = {
            "pe": _Eng(self, "pe", nc.tensor),
            "act": _Eng(self, "act", nc.scalar),
            "dve": _Eng(self, "dve", nc.vector),
            "pool": _Eng(self, "pool", nc.gpsimd),
            "sp": _Eng(self, "sp", nc.sync),
        }
        self.dma_slots = {}
        for q in ("sp", "pool", "act"):
            self.dma_slots[q] = [_Ctr(self, f"d_{q}{i}", 16) for i in range(n_dma_slots)]
        self.dma_rr = {"sp": 0, "pool": 0, "act": 0}
        self.n_inst = 0
        self.uid = 0
        self.pending = None
        self.lazy_engines = ()

    def _newsem(self, name):
        return self.stack.enter_context(self.nc.semaphore(name))

    def sb(self, name, shape, dt):
        self.uid += 1
        return self.scopes[-1].enter_context(self.nc.sbuf_tensor(f"sb{self.uid}_{name}", list(shape), dt))

    def ps(self, name, shape, dt=F32):
        self.uid += 1
        esz = 4 if dt == F32 else 2
        per_part = esz
        for d_ in shape[1:]:
            per_part *= d_
        assert per_part <= 2048, (name, shape)
        shape = list(shape)
        if per_part < 2048:
            rest = per_part // shape[1]
            assert 2048 % rest == 0, (name, shape)
            full = [shape[0], 2048 // rest] + shape[2:]
            t = self.scopes[-1].enter_context(self.nc.psum_tensor(f"ps{self.uid}_{name}", full, dt))
            return _PsView(t, shape)
        return self.scopes[-1].enter_context(self.nc.psum_tensor(f"ps{self.uid}_{name}", shape, dt))

    @contextlib.contextmanager
    def scope(self):
        st = contextlib.ExitStack()
        self.scopes.append(st)
        try:
            yield
        finally:
            self.barrier()
            self.scopes.pop()
            st.close()

    def barrier(self):
        evs = []
        for e in self.engs.values():
            if e.n_issued > 0:
                evs.append(self._resolve(LazyEv(e, e.n_issued - 1)))
        for q in self.dma_slots:
            for ctr in self.dma_slots[q]:
                if ctr.val > 0:
                    evs.append((ctr.sem, ctr.val))
        for e in self.engs.values():
            for ev in evs:
                self._wait(e, ev)

    def _r(self, key):
        r = self.res.get(key)
        if r is None:
            r = self.res[key] = _Res()
        return r

    def _resolve(self, ev):
        if not isinstance(ev, LazyEv):
            return ev
        import bisect
        e = ev.eng
        k = bisect.bisect_left(e.inc_idx, ev.idx)
        if k < len(e.inc_idx):
            return e.inc_ev[k]
        assert e.last_ins is not None and not e.last_has_inc and e.n_issued - 1 >= ev.idx
        sv = e.ctr.next_event()
        e.last_ins.then_inc(sv[0], 1)
        e.last_has_inc = True
        e.inc_idx.append(e.n_issued - 1)
        e.inc_ev.append(sv)
        return sv

    def _wait(self, eng, ev):
        if ev is None:
            return
        if isinstance(ev, LazyEv) and ev.eng is eng and eng.name == "pe":
            return
        sem, val = self._resolve(ev)
        k = id(sem)
        if eng.seen.get(k, 0) >= val:
            return
        if self.pending is not None:
            cur = self.pending.get(k)
            if cur is None or cur[1] < val:
                self.pending[k] = (sem, val)
            return
        eng.obj.wait_ge(sem, val)
        eng.seen[k] = val

    def _flush(self, eng):
        pend = list(self.pending.values())
        self.pending = None
        for (sem, val) in pend[:-1]:
            eng.obj.wait_ge(sem, val)
            eng.seen[id(sem)] = val
        if pend:
            sem, val = pend[-1]
            eng.seen[id(sem)] = val
            return (sem, val)
        return None

    def _deps(self, eng, reads, writes, skip_same_eng_write=False):
        for key in reads:
            r = self._r(key)
            self._wait(eng, r.w)
        inorder = eng.name in ("act", "dve")
        for key in writes:
            r = self._r(key)
            if not ((skip_same_eng_write or inorder) and isinstance(r.w, LazyEv) and r.w.eng is eng):
                self._wait(eng, r.w)
            for ev in r.r.values():
                if inorder and isinstance(ev, LazyEv) and ev.eng is eng:
                    continue
                self._wait(eng, ev)

    def _commit(self, ev, reads, writes):
        rk = ev.eng.name if isinstance(ev, LazyEv) else id(ev[0])
        for key in reads:
            self._r(key).r[rk] = ev
        for key in writes:
            r = self._r(key)
            r.w = ev
            r.r = {}

    def op(self, engname, fn, reads=(), writes=(), accum=False):
        eng = self.engs[engname]
        self.pending = {}
        self._deps(eng, reads, writes, skip_same_eng_write=accum)
        last = self._flush(eng)
        ins = fn(eng.obj)
        if last is not None:
            ins._wait_ge(last[0], last[1])
        eng.last_ins = ins
        eng.last_has_inc = False
        ev = LazyEv(eng, eng.n_issued)
        eng.n_issued += 1
        if engname not in self.lazy_engines:
            self._resolve(ev)
        self._commit(ev, reads, writes)
        self.n_inst += 1
        return ev

    def dma(self, q, out, in_, reads=(), writes=(), **kw):
        eng = self.engs[q]
        slots = self.dma_slots[q]
        i = self.dma_rr[q]
        self.dma_rr[q] = (i + 1) % len(slots)
        ctr = slots[i]
        self.pending = {}
        if ctr.val > 0:
            self._wait(eng, (ctr.sem, ctr.val))
        self._deps(eng, reads, writes)
        last = self._flush(eng)
        ev = ctr.next_event()
        ins = eng.obj.dma_start(out=out, in_=in_, **kw)
        if last is not None:
            ins._wait_ge(last[0], last[1])
        ins.then_inc(ev[0], 16)
        self._commit(ev, reads, writes)
        self.n_inst += 1
        return ev

    def finish(self, final_keys):
        eng = self.engs["sp"]
        for key in final_keys:
            r = self._r(key)
            self._wait(eng, r.w)
        for q in self.dma_slots:
            for ctr in self.dma_slots[q]:
                if ctr.val > 0:
                    self._wait(eng, (ctr.sem, ctr.val))

    def close(self):
        self.stack.close()

from concourse.bass_utils import run_bass_kernel_spmd

D = 1024
NCTX = 256
NLAT = 4096
NTOK = NCTX + NLAT
NT = NTOK // 128
EPS = 1e-6
P = 128


def _pool_bands():
    L = 1024
    out = np.zeros((4, 5, 128, 128), np.float32)
    for g, w in enumerate((2, 4, 8, 16)):
        def full(L):
            t = np.arange(L)
            lo = np.clip(t - w // 2, 0, L)
            hi = np.clip(t + w // 2, 0, L)
            s = np.arange(L)[:, None]
            m = ((s >= lo[None, :]) & (s < hi[None, :])).astype(np.float64) / (hi - lo)[None, :]
            m -= np.eye(L)
            return m
        m = full(L)
        out[g, 0] = m[3 * 128:4 * 128, 4 * 128:5 * 128]
        out[g, 1] = m[5 * 128:6 * 128, 4 * 128:5 * 128]
        out[g, 2] = m[4 * 128:5 * 128, 4 * 128:5 * 128]
        out[g, 3] = m[0:128, 0:128]
        out[g, 4] = m[L - 128:, L - 128:]
    return out


_VARS = [(-2, "pm"), (-1, "f"), (0, "f"), (1, "f"), (2, "pp")] + [(d, "f") for d in range(-3, 4)]


def _attn_tables():
    kc = np.arange(64)
    qc = np.arange(64)
    c_start = np.clip(qc - 8, 0, 48)
    col_ok = (kc[:, None] >= c_start[None, :]) & (kc[:, None] < c_start[None, :] + 16)
    dc_idx = np.clip(kc[:, None] - qc[None, :], -15, 15) + 15
    dr_idx = np.zeros((12, 128, 128), np.int64)
    dc_full = np.zeros((128, 128), np.int64)
    mask = np.zeros((12, 128, 128), np.float32)
    for a in range(2):
        for b in range(2):
            dc_full[a * 64:(a + 1) * 64, b * 64:(b + 1) * 64] = dc_idx
    for v, (dl, kind) in enumerate(_VARS):
        for a in range(2):
            for b in range(2):
                dr = 2 * dl + a - b + 7
                vis = True
                if kind == "pm":
                    vis = not (a == 0 and b == 1)
                elif kind == "pp":
                    vis = (a == 0 and b == 1)
                dr_idx[v, a * 64:(a + 1) * 64, b * 64:(b + 1) * 64] = min(max(dr, 0), 14)
                if vis and 0 <= dr <= 14:
                    mask[v, a * 64:(a + 1) * 64, b * 64:(b + 1) * 64] = col_ok
    return dr_idx, dc_full, mask


def mm(S, out, lhsT, rhs, start, stop, reads, writes):
    return S.op("pe", lambda e: e.matmul(out, lhsT=lhsT, rhs=rhs, start=start, stop=stop),
                reads=reads, writes=writes, accum=not start)


class Prog:
    def __init__(self, nc, debug=()):
        self.nc = nc
        self.S = Sched(nc)
        self.debug = debug
        self.dbg_out = {}
        dt = nc.dram_tensor
        I = lambda name, shape: dt(name, list(shape), F32, kind="ExternalInput").ap()
        self.x_in = I("x", [NLAT, D])
        self.ctx_in = I("ctx", [NCTX, D])
        self.cvec = I("cvec", [P, 8, 2])
        self.ada_w = I("ada_w", [2, D, 6 * D])
        self.ada_b = I("ada_b", [2, 6 * D])
        self.norm_g = I("norm_g", [2, 4, D])
        self.mlp_w1 = I("mlp_w1", [2, D, 4 * D])
        self.mlp_w2 = I("mlp_w2", [2, 4 * D, D])
        self.ev_w_in = I("ev_w_in", [D, 2 * D])
        self.ev_w_out = I("ev_w_out", [D, D])
        self.ev_pool_w = I("ev_pool_w", [4, P, P])
        self.ev_pool_scale = I("ev_pool_scale", [512])
        self.rpb_tab = I("rpb_tab", [P, 8 * 12, P])
        self.msk_tab = I("msk_tab", [P, 12, P])
        self.bands = I("bands", [P, 20, P])
        self.ident = I("ident", [P, P])
        self.out = dt("out", [NLAT, D], F32, kind="ExternalOutput").ap()
        X = lambda name, shape, d=F32: (dt(name, list(shape), d, kind="ExternalOutput").ap() if name in debug
                                        else dt(name, list(shape), d).ap())
        self.modd = X("modd", [2, 2, 6 * D])
        self.x_d = X("x_d", [NTOK, D])
        self.upool_d = X("upool_d", [NTOK, 512], BF16)
        self.v_d = X("v_d", [NTOK, 512], BF16)
        self.qT_d = X("qT_d", [4, P, NTOK], BF16)
        self.kT_d = X("kT_d", [4, P, NTOK], BF16)
        self.zT_d = X("zT_d", [8, P, NTOK], BF16)

    def dbg(self, name, shape, dtp=F32):
        t = self.nc.dram_tensor("dbg_" + name, list(shape), dtp, kind="ExternalOutput").ap()
        self.dbg_out[name] = t
        return t

    def consts(self):
        S = self.S
        self.idb = S.sb("idb", [P, P], BF16)
        S.dma("pool", self.idb[:], self.ident, writes=["idb"])
        self.ones_bf = S.sb("ones_bf", [P, P], BF16)
        S.op("dve", lambda e: e.memset(self.ones_bf[:], 1.0), writes=["ones_bf"])
        self.eps_t = S.sb("eps_t", [P, 1], F32)
        S.op("dve", lambda e: e.memset(self.eps_t[:], EPS), writes=["eps_t"])

    def phase_mod(self):
        S = self.S
        with S.scope():
            cv = S.sb("cv", [P, 8, 2], F32)
            cvb = S.sb("cvb", [P, 8, 2], BF16)
            S.dma("sp", cv[:], self.cvec, writes=["cv"])
            S.op("act", lambda e: e.activation(out=cvb[:], in_=cv[:], func=AF.Silu), reads=["cv"], writes=["cvb"])
            aw = S.sb("aw", [P, 8, 6 * D], BF16)
            ab = S.sb("ab", [2, 6 * D], F32)
            mrow = S.sb("mrow", [2, 6 * D], F32)
            pss = [S.ps(f"pm{i}", [2, 512], F32) for i in range(4)]
            for l in range(2):
                for c in range(8):
                    S.dma("pool", aw[:, c, :], self.ada_w[l, c * P:(c + 1) * P, :], writes=[("aw", c)])
                S.dma("sp", ab[:], self.ada_b[l:l + 1, :].broadcast_to([2, 6 * D]), writes=["ab"])
                for n in range(12):
                    ps = pss[n % 4]
                    k = ("pm", n % 4)
                    for c in range(8):
                        mm(S, ps[:], cvb[:, c, :], aw[:, c, n * 512:(n + 1) * 512], c == 0, c == 7,
                           reads=["cvb", ("aw", c)], writes=[k])
                    S.op("dve", lambda e: e.tensor_tensor(out=mrow[:, n * 512:(n + 1) * 512], in0=ps[:],
                                                          in1=ab[:, n * 512:(n + 1) * 512], op=ALU.add),
                         reads=[k, "ab"], writes=["mrow"])
                S.dma("sp", self.modd[l], mrow[:], reads=["mrow"], writes=[("modd", l)])

    def load_layer_vecs(self, l, V=None, part="all"):
        S = self.S
        V = {} if V is None else V
        for s in range(2):
            for which in range(2):
                if part in ("all", "ab"):
                    V[("A", which, s)] = (S.sb(f"A{which}_{s}", [P, 8], F32), f"A{which}_{s}")
                    V[("B", which, s)] = (S.sb(f"B{which}_{s}", [P, 8], F32), f"B{which}_{s}")
                if part in ("all", "g") and not (l == 1 and s == 1):
                    V[("G", which, s)] = (S.sb(f"GG{which}_{s}", [P, D], F32), f"GG{which}_{s}")
        with S.scope(), self.nc.allow_non_contiguous_dma(reason="tiny per-feature vectors"):
            tmp = S.sb("lv_tmp", [P, 8], F32)
            rowt = S.sb("lv_row", [P, D], F32)
            for s in range(2):
                for which, (ish, isc, ig) in enumerate(((0, 1, 0), (3, 4, 2))):
                    if part == "g":
                        continue
                    A, ka = V[("A", which, s)]
                    B, kb = V[("B", which, s)]
                    S.dma("sp", B[:], self.modd[l, s, ish * D:(ish + 1) * D].rearrange("(c p) -> p c", p=P),
                          reads=[("modd", l)], writes=[kb])
                    S.dma("sp", A[:], self.modd[l, s, isc * D:(isc + 1) * D].rearrange("(c p) -> p c", p=P),
                          reads=[("modd", l)], writes=[ka])
                    S.dma("sp", tmp[:], self.norm_g[l, ig, :].rearrange("(c p) -> p c", p=P), writes=["lv_tmp"])
                    S.op("dve", lambda e: e.scalar_tensor_tensor(out=A[:], in0=A[:], scalar=1.0, in1=tmp[:],
                                                                 op0=ALU.add, op1=ALU.mult),
                         reads=[ka, "lv_tmp"], writes=[ka])
                for which, (igt, ig) in enumerate(((2, 1), (5, 3))):
                    if (l == 1 and s == 1) or part == "ab":
                        continue
                    G, kg = V[("G", which, s)]
                    S.dma("sp", G[:], self.modd[l, s:s + 1, igt * D:(igt + 1) * D].broadcast_to([P, D]),
                          reads=[("modd", l)], writes=[kg])
                    S.dma("sp", rowt[:], self.norm_g[l, ig:ig + 1, :].broadcast_to([P, D]), writes=["lv_row"])
                    S.op("dve", lambda e: e.tensor_tensor(out=G[:], in0=G[:], in1=rowt[:], op=ALU.mult),
                         reads=[kg, "lv_row"], writes=[kg])
        return V

    def load_ab(self, l, which, tag):
        S = self.S
        ish, isc, ig = ((0, 1, 0), (3, 4, 2))[which]
        out = {}
        with self.nc.allow_non_contiguous_dma(reason="tiny per-feature vectors"):
            tmp = S.sb(f"{tag}_t", [P, 8], F32)
            for s in range(2):
                A = S.sb(f"{tag}_A{s}", [P, 8], F32); ka = f"{tag}_A{s}"
                B = S.sb(f"{tag}_B{s}", [P, 8], F32); kb = f"{tag}_B{s}"
                S.dma("sp", B[:], self.modd[l, s, ish * D:(ish + 1) * D].rearrange("(c p) -> p c", p=P),
                      reads=[("modd", l)], writes=[kb])
                S.dma("sp", A[:], self.modd[l, s, isc * D:(isc + 1) * D].rearrange("(c p) -> p c", p=P),
                      reads=[("modd", l)], writes=[ka])
                S.dma("sp", tmp[:], self.norm_g[l, ig, :].rearrange("(c p) -> p c", p=P), writes=[f"{tag}_t"])
                S.op("dve", lambda e: e.scalar_tensor_tensor(out=A[:], in0=A[:], scalar=1.0, in1=tmp[:],
                                                             op0=ALU.add, op1=ALU.mult),
                     reads=[ka, f"{tag}_t"], writes=[ka])
                out[s] = ((A, ka), (B, kb))
        return out

    def make_norm_bufs(self, tag, nb=2):
        S = self.S
        B = {"i": 0, "nb": nb, "tag": tag}
        B["sq"] = [S.sb(f"{tag}_sq{i}", [P, D], BF16) for i in range(1)] * nb
        B["st"] = [S.sb(f"{tag}_st{i}", [P, 4], F32) for i in range(nb)]
        B["xn"] = [S.sb(f"{tag}_xn{i}", [P, D], BF16) for i in range(nb)]
        B["tp"] = [S.ps(f"{tag}_tp{i}", [P, 8, P], BF16) for i in range(nb)]
        return B

    def norm_to_hT(self, B, x_sb, xkey, A, B_, out_ap, out_key, out2_ap=None, out2_key=None):
        S = self.S
        i = B["i"] % B["nb"]
        B["i"] += 1
        tag = B["tag"]
        sq, st, xn, tp = B["sq"][i], B["st"][i], B["xn"][i], B["tp"][i]
        ksq, kst, kxn, ktp, ktm = [(tag, n, i) for n in ("sq", "st", "xn", "tp", "tm")]
        ksq = (tag, "sq", 0)
        S.op("pool", lambda e: e.memset(st[:], 0.0), writes=[kst])
        S.op("act", lambda e: e.activation(out=sq[:], in_=x_sb, func=AF.Square, accum_out=st[:, 0:1]),
             reads=[xkey], writes=[ksq, kst])
        S.op("act", lambda e: e.activation(out=st[:, 1:2], in_=st[:, 0:1], func=AF.Sqrt, scale=1.0 / D,
                                           bias=self.eps_t[:, 0:1]), reads=[kst, "eps_t"], writes=[kst])
        S.op("dve", lambda e: e.reciprocal(out=st[:, 2:3], in_=st[:, 1:2]), reads=[kst], writes=[kst])
        S.op("dve", lambda e: e.tensor_scalar(out=xn[:], in0=x_sb, scalar1=st[:, 2:3], scalar2=None, op0=ALU.mult),
             reads=[xkey, kst], writes=[kxn])
        for c in range(8):
            S.op("pe", lambda e: e.transpose(out=tp[:, c, :], in_=xn[:, c * P:(c + 1) * P], identity=self.idb[:]),
                 reads=[kxn, "idb"], writes=[ktp], accum=(c > 0))
        for c in range(8):
            S.op("dve", lambda e: e.tensor_scalar(out=out_ap[:, c, :], in0=tp[:, c, :], scalar1=A[0][:, c:c + 1],
                                                  scalar2=B_[0][:, c:c + 1], op0=ALU.mult, op1=ALU.add),
                 reads=[ktp, A[1], B_[1]], writes=[out_key])

    def x_src(self, layer, T):
        if layer == 0:
            if T < 2:
                return self.ctx_in[T * P:(T + 1) * P, :], None
            return self.x_in[(T - 2) * P:(T - 1) * P, :], None
        return self.x_d[T * P:(T + 1) * P, :], ("x_d", T)

    def phase_L0_proj(self, V):
        S = self.S
        with S.scope():
            w = S.sb("w_in", [P, 8, 2 * D], BF16)
            for c in range(8):
                S.dma("pool", w[:, c, :], self.ev_w_in[c * P:(c + 1) * P, :], writes=[("w_in", c)])
            wk = [("w_in", c) for c in range(8)]
            NB = self.make_norm_bufs("n0")
            xt = [S.sb(f"xt{i}", [P, D], F32) for i in range(2)]
            hT = [S.sb(f"hT{i}", [P, 8, 512], BF16) for i in range(2)]
            ptok = [S.ps(f"ptok{i}", [P, 512], F32) for i in range(2)]
            pft = [S.ps(f"pft{i}", [P, 512], F32) for i in range(2)]
            otok = [S.sb(f"otok{i}", [P, 512], BF16) for i in range(2)]
            oft = [S.sb(f"oft{i}", [P, 512], BF16) for i in range(2)]
            supers = [(0, 2)] + [(2 + 4 * i, 4) for i in range(8)]
            cnt = 0
            ctok = 0
            cft = 0
            for si, (T0, nt) in enumerate(supers):
                hb = hT[si % 2]
                hk = ("hT", si % 2)
                s = 1 if T0 < 2 else 0
                for t in range(nt):
                    T = T0 + t
                    xb = xt[cnt % 2]
                    xk = ("xt", cnt % 2)
                    cnt += 1
                    src, sk = self.x_src(0, T)
                    S.dma("act", xb[:], src, reads=[sk] if sk else [], writes=[xk])
                    self.norm_to_hT(NB, xb[:], xk, V[("A", 0, s)], V[("B", 0, s)],
                                    hb[:, :, t * P:(t + 1) * P], (hk, t))
                n = nt * P
                hks = [(hk, t) for t in range(nt)]
                for t in range(nt):
                    T = T0 + t
                    for (c0, dst, dk) in ((0, self.upool_d, "upool"), (1536, self.v_d, "v")):
                        ps = ptok[ctok % 2]; pk = ("ptok", ctok % 2)
                        ob = otok[ctok % 2]; ok = ("otok", ctok % 2)
                        ctok += 1
                        for c in range(8):
                            mm(S, ps[:], hb[:, c, t * P:(t + 1) * P], w[:, c, c0:c0 + 512], c == 0, c == 7,
                               reads=[(hk, t), wk[c]], writes=[pk])
                        S.op("act", lambda e: e.activation(out=ob[:], in_=ps[:], func=AF.Copy), reads=[pk], writes=[ok])
                        S.dma("sp", dst[T * P:(T + 1) * P, :], ob[:], reads=[ok], writes=[(dk, T)])
                for jb in range(8):
                    c0 = 512 + jb * P
                    ps = pft[cft % 2]; pk = ("pft", cft % 2)
                    ob = oft[cft % 2]; ok = ("oft", cft % 2)
                    cft += 1
                    for c in range(8):
                        mm(S, ps[:, :n], w[:, c, c0:c0 + P], hb[:, c, :n], c == 0, c == 7,
                           reads=hks + [wk[c]], writes=[pk])
                    sc = 0.125 if jb < 4 else 1.0
                    S.op("act", lambda e: e.activation(out=ob[:, :n], in_=ps[:, :n], func=AF.Copy, scale=sc),
                         reads=[pk], writes=[ok])
                    dst = self.qT_d if jb < 4 else self.kT_d
                    dk = "qT" if jb < 4 else "kT"
                    S.dma("sp", dst[jb % 4, :, T0 * P:T0 * P + n], ob[:, :n], reads=[ok],
                          writes=[(dk, jb % 4, T0 + t) for t in range(nt)])

    def phase_L0_pool(self):
        S = self.S
        with S.scope():
            up = S.sb("up_all", [P, NT, 512], BF16)
            for q in range(0, NT, 2):
                S.dma("sp", up[:, q:q + 2, :], self.upool_d[q * P:(q + 2) * P, :].rearrange("(n p) f -> p n f", p=P),
                      reads=[("upool", q), ("upool", q + 1)], writes=[("up", q), ("up", q + 1)])
            bd = S.sb("bands", [P, 20, P], BF16)
            S.dma("pool", bd[:], self.bands, writes=["bands"])
            pw = S.sb("pool_w", [P, 4, P], BF16)
            S.dma("pool", pw[:], self.ev_pool_w.rearrange("g c o -> c g o"), writes=["pool_w"])
            psc = S.sb("pool_sc", [P, 4], F32)
            with self.nc.allow_non_contiguous_dma(reason="tiny"):
                S.dma("sp", psc[:], self.ev_pool_scale.rearrange("(g p) -> p g", p=P), writes=["pool_sc"])
            pb = [S.ps(f"pb{i}", [P, 4, P], F32) for i in range(2)]
            pc = [S.ps(f"pc{i}", [P, 4, P], F32) for i in range(2)]
            pm = [S.sb(f"pmx{i}", [P, 4, P], BF16) for i in range(2)]
            zp = [S.sb(f"zp{i}", [P, 4, P], BF16) for i in range(2)]
            it = 0
            for (T0, n) in ((0, 2), (2, 32)):
                for i in range(n):
                    T = T0 + i
                    b = it % 2
                    it += 1
                    for g in range(4):
                        srcs = []
                        if i > 0:
                            srcs.append((T - 1, 0))
                        cv = 3 if i == 0 else (4 if i == n - 1 else 2)
                        srcs.append((T, cv))
                        if i < n - 1:
                            srcs.append((T + 1, 1))
                        for si, (Ts, v) in enumerate(srcs):
                            mm(S, pb[b][:, g, :], up[:, Ts, g * P:(g + 1) * P], bd[:, g * 5 + v, :],
                               si == 0, si == len(srcs) - 1, reads=[("up", Ts), "bands"], writes=[("pb", b)])
                    S.op("dve", lambda e: e.tensor_copy(out=pm[b][:], in_=pb[b][:]), reads=[("pb", b)], writes=[("pmx", b)])
                    for g in range(4):
                        mm(S, pc[b][:, g, :], pw[:, g, :], pm[b][:, g, :], True, True,
                           reads=["pool_w", ("pmx", b)], writes=[("pc", b)])
                    S.op("dve", lambda e: e.tensor_tensor(out=zp[b][:], in0=pc[b][:],
                                                          in1=psc[:, :, None].broadcast_to([P, 4, P]), op=ALU.mult),
                         reads=[("pc", b), "pool_sc"], writes=[("zp", b)])
                    S.dma("sp", self.zT_d[0:4, :, T * P:(T + 1) * P].rearrange("c p t -> p c t"), zp[b][:],
                          reads=[("zp", b)], writes=[("zT", c, T) for c in range(4)])

    def phase_L0_attn(self):
        S = self.S
        with S.scope():
            kT = S.sb("kT_all", [P, 4, NTOK], BF16)
            qT = S.sb("qT_all", [P, 4, NTOK], BF16)
            va = S.sb("v_all", [P, NT, 512], BF16)
            for j in range(4):
                S.dma("sp", kT[:, j, :], self.kT_d[j], reads=[("kT", j, T) for T in range(NT)], writes=[("kTa", j)])
                S.dma("sp", qT[:, j, :], self.qT_d[j], reads=[("qT", j, T) for T in range(NT)], writes=[("qTa", j)])
            for q in range(0, NT, 2):
                S.dma("sp", va[:, q:q + 2, :], self.v_d[q * P:(q + 2) * P, :].rearrange("(n p) f -> p n f", p=P),
                      reads=[("v", q), ("v", q + 1)], writes=[("va", q), ("va", q + 1)])
            E = S.sb("Etab", [P, 96, P], BF16)
            with S.scope():
                rt = S.sb("rt", [P, 96, P], F32)
                mk = S.sb("mk", [P, 12, P], F32)
                S.dma("sp", rt[:], self.rpb_tab, writes=["rt"])
                S.dma("sp", mk[:], self.msk_tab, writes=["mk"])
                S.op("act", lambda e: e.activation(out=rt[:], in_=rt[:], func=AF.Exp), reads=["rt"], writes=["rt"])
                for h in range(8):
                    S.op("dve", lambda e: e.tensor_tensor(out=E[:, h * 12:(h + 1) * 12, :], in0=rt[:, h * 12:(h + 1) * 12, :],
                                                          in1=mk[:], op=ALU.mult), reads=["rt", "mk"], writes=["Etab"])
            pss = [[S.ps(f"pss{i}_{k}", [P, 512], F32) for k in range(2)] for i in range(3)]
            pso = [S.ps(f"pso{i}", [P, 2, P], F32) for i in range(2)]
            pex = [S.sb(f"pex{i}", [P, 7, P], BF16) for i in range(3)]
            pT = [S.sb(f"pT{i}", [P, 5, P], BF16) for i in range(3)]
            rc = [S.sb(f"rc{i}", [P, P], F32) for i in range(2)]
            zo = [S.sb(f"zo{i}", [P, P], BF16) for i in range(2)]
            it = 0
            izo = 0
            for T in range(NT):
                if T < 2:
                    chunks = [(0, None), (1, None)]
                else:
                    i = T - 2
                    if 2 <= i <= 29:
                        lat = [(T + d, v) for v, d in enumerate((-2, -1, 0, 1, 2))]
                    elif i == 0:
                        lat = [(T + d, 8 + d) for d in (0, 1, 2, 3)]
                    elif i == 1:
                        lat = [(T + d, 8 + d) for d in (-1, 0, 1, 2)]
                    elif i == 30:
                        lat = [(T + d, 8 + d) for d in (-2, -1, 0, 1)]
                    else:
                        lat = [(T + d, 8 + d) for d in (-3, -2, -1, 0)]
                    chunks = [(0, None), (1, None)] + lat
                nk = len(chunks)
                nlat = nk - 2
                for j in range(4):
                    zb = zo[izo % 2]; zk = ("zo", izo % 2)
                    izo += 1
                    for hh in range(2):
                        h = 2 * j + hh
                        pb_ = hh * 64
                        b = it % 3
                        bo = it % 2
                        it += 1
                        for ci, (Tk, v) in enumerate(chunks):
                            bank = pss[b][ci // 4]
                            mm(S, bank[:, (ci % 4) * P:(ci % 4 + 1) * P],
                               kT[pb_:pb_ + 64, j, Tk * P:(Tk + 1) * P], qT[pb_:pb_ + 64, j, T * P:(T + 1) * P],
                               True, True, reads=[("kTa", j), ("qTa", j)], writes=[("pss", b, ci // 4)])
                        n0 = min(nk, 4)
                        S.op("act", lambda e: e.activation(out=pex[b][:, 0:n0, :], in_=pss[b][0][:, 0:n0 * P].rearrange("p (c q) -> p c q", q=P), func=AF.Exp),
                             reads=[("pss", b, 0)], writes=[("pex", b)])
                        if nk > 4:
                            S.op("act", lambda e: e.activation(out=pex[b][:, 4:nk, :], in_=pss[b][1][:, 0:(nk - 4) * P].rearrange("p (c q) -> p c q", q=P), func=AF.Exp),
                                 reads=[("pss", b, 1)], writes=[("pex", b)])
                        if nlat > 0:
                            v0 = chunks[2][1]
                            S.op("dve", lambda e: e.tensor_tensor(out=pT[b][:, 0:nlat, :], in0=pex[b][:, 2:nk, :],
                                                                  in1=E[:, h * 12 + v0:h * 12 + v0 + nlat, :], op=ALU.mult),
                                 reads=[("pex", b), "Etab"], writes=[("pT", b)])
                        for ci, (Tk, v) in enumerate(chunks):
                            rhs = pex[b][:, ci, :] if v is None else pT[b][:, ci - 2, :]
                            rk = [("pex", b)] if v is None else [("pT", b)]
                            mm(S, pso[bo][:, 0, :], va[:, Tk, j * P:(j + 1) * P], rhs, ci == 0, ci == nk - 1,
                               reads=[("va", Tk)] + rk, writes=[("pso", bo)])
                        for ci, (Tk, v) in enumerate(chunks):
                            rhs = pex[b][:, ci, :] if v is None else pT[b][:, ci - 2, :]
                            rk = [("pex", b)] if v is None else [("pT", b)]
                            mm(S, pso[bo][:, 1, :], self.ones_bf[:], rhs, ci == 0, ci == nk - 1,
                               reads=["ones_bf"] + rk, writes=[("pso", bo)])
                        S.op("dve", lambda e: e.reciprocal(out=rc[bo][pb_:pb_ + 64, :], in_=pso[bo][pb_:pb_ + 64, 1, :]),
                             reads=[("pso", bo)], writes=[("rc", bo)])
                        S.op("dve", lambda e: e.tensor_tensor(out=zb[pb_:pb_ + 64, :], in0=pso[bo][pb_:pb_ + 64, 0, :],
                                                              in1=rc[bo][pb_:pb_ + 64, :], op=ALU.mult),
                             reads=[("pso", bo), ("rc", bo)], writes=[zk])
                    S.dma("sp", self.zT_d[4 + j, :, T * P:(T + 1) * P], zb[:], reads=[zk], writes=[("zT", 4 + j, T)])

    def phase_out_mlp(self, layer, V, w_out_ap, y_tile_fn, dst_fn, next_ab=None):
        S = self.S
        with S.scope():
            wo = S.sb("wo", [P, 8, D], BF16)
            for c in range(8):
                S.dma("pool", wo[:, c, :], w_out_ap[c * P:(c + 1) * P, :], writes=[("wo", c)])
            w1 = S.sb("w1", [P, 8, 4 * D], BF16)
            w2 = S.sb("w2", [P, 32, D], BF16)
            for c in range(8):
                S.dma("pool", w1[:, c, :], self.mlp_w1[layer, c * P:(c + 1) * P, :], writes=[("w1", c)])
            for f in range(0, 32, 4):
                S.dma("pool", w2[:, f:f + 4, :], self.mlp_w2[layer, f * P:(f + 4) * P, :].rearrange("(n p) d -> p n d", p=P),
                      writes=[("w2", f + q) for q in range(4)])
            NB = self.make_norm_bufs("nm", nb=1)
            hob = S.sb("hob", [P, 8, P], F32) if next_ab is not None else None
            zt = [S.sb(f"zt{i}", [P, 8, P], BF16) for i in range(2)]
            xt = [S.sb(f"xo{i}", [P, D], F32) for i in range(2)]
            x1 = [S.sb(f"x1_{i}", [P, D], F32) for i in range(2)]
            tmp = [S.sb(f"tg{i}", [P, D], F32) for i in range(2)]
            sq = NB["sq"][0]
            stt = [S.sb(f"ost{i}", [P, 4], F32) for i in range(4)]
            hT = [S.sb(f"hm{i}", [P, 8, 256], BF16) for i in range(1)] * 2
            py = [[S.ps(f"py{t}_{hf}", [P, 512], F32) for hf in range(2)] for t in range(2)]
            pa = [S.ps(f"pa{i}", [P, 256], F32) for i in range(2)]
            r32 = [S.sb(f"r32_{i}", [P, 256], F32) for i in range(2)]
            aT = [S.sb(f"aT{i}", [P, 256], BF16) for i in range(2)]
            ist = 0

            def norm_gate_res(t, G, xin, xin_key, xout, xout_key):
                nonlocal ist
                st = stt[ist % 4]; sk = ("ost", ist % 4)
                ist += 1
                S.op("pool", lambda e: e.memset(st[:], 0.0), writes=[sk])
                for hf in range(2):
                    S.op("act", lambda e: e.activation(out=sq[:, hf * 512:(hf + 1) * 512], in_=py[t][hf][:], func=AF.Square,
                                                       accum_out=st[:, hf:hf + 1]), reads=[("py", t, hf)], writes=[("nm", "sq", 0), sk])
                S.op("dve", lambda e: e.tensor_tensor(out=st[:, 2:3], in0=st[:, 0:1], in1=st[:, 1:2], op=ALU.add),
                     reads=[sk], writes=[sk])
                S.op("act", lambda e: e.activation(out=st[:, 2:3], in_=st[:, 2:3], func=AF.Sqrt, scale=1.0 / D,
                                                   bias=self.eps_t[:, 0:1]), reads=[sk, "eps_t"], writes=[sk])
                S.op("dve", lambda e: e.reciprocal(out=st[:, 3:4], in_=st[:, 2:3]), reads=[sk], writes=[sk])
                tb = tmp[t]; tk = ("tg", t)
                for hf in range(2):
                    S.op("dve", lambda e: e.scalar_tensor_tensor(out=tb[:, hf * 512:(hf + 1) * 512], in0=py[t][hf][:],
                                                                 scalar=st[:, 3:4], in1=G[0][:, hf * 512:(hf + 1) * 512],
                                                                 op0=ALU.mult, op1=ALU.mult),
                         reads=[("py", t, hf), sk, G[1]], writes=[tk])
                S.op("dve", lambda e: e.tensor_tensor(out=xout, in0=tb[:], in1=xin, op=ALU.add),
                     reads=[tk, xin_key], writes=[xout_key])

            ia = 0
            loaded = {}

            def load_inputs(sidx):
                loaded[sidx] = True
                for t in range(2):
                    T = 2 * sidx + t
                    src, skey = self.x_src(layer, T)
                    S.dma("sp", xt[t][:], src, reads=[skey] if skey else [], writes=[("xo", t)])
                    S.dma("sp", zt[t][:], self.zT_d[:, :, T * P:(T + 1) * P].rearrange("c p t -> p c t"),
                          reads=[("zT", c, T) for c in range(8)], writes=[("zt", t)])

            for sidx in range(NT // 2):
                T0 = 2 * sidx
                s = 1 if T0 < 2 else 0
                if layer == 1 and s == 1:
                    continue
                hb = hT[0]; hk = ("hm", 0)
                if not loaded.get(sidx):
                    load_inputs(sidx)
                for t in range(2):
                    T = T0 + t
                    y_tile_fn(T, t, zt[t], ("zt", t), wo, py[t])
                    norm_gate_res(t, V[("G", 0, s)], xt[t][:], ("xo", t), x1[t][:], ("x1", t))
                    self.norm_to_hT(NB, x1[t][:], ("x1", t), V[("A", 1, s)], V[("B", 1, s)],
                                    hb[:, :, t * P:(t + 1) * P], (hk, t))
                nxt = sidx + 1
                if nxt < NT // 2 and not (layer == 1 and nxt == 0):
                    load_inputs(nxt)
                def mm1(f):
                    a = (ia + f) % 2
                    for c in range(8):
                        mm(S, pa[a][:], w1[:, c, f * P:(f + 1) * P], hb[:, c, :], c == 0, c == 7,
                           reads=[("w1", c), (hk, 0), (hk, 1)], writes=[("pa", a)])
                    S.op("act", lambda e: e.activation(out=r32[a][:], in_=pa[a][:], func=AF.Relu),
                         reads=[("pa", a)], writes=[("r32", a)])
                    S.op("dve", lambda e: e.tensor_tensor(out=aT[a][:], in0=r32[a][:], in1=r32[a][:], op=ALU.mult),
                         reads=[("r32", a)], writes=[("aT", a)])

                def mm2(f):
                    a = (ia + f) % 2
                    for t in range(2):
                        for hf in range(2):
                            mm(S, py[t][hf][:], aT[a][:, t * P:(t + 1) * P], w2[:, f, hf * 512:(hf + 1) * 512],
                               f == 0, f == 31, reads=[("aT", a), ("w2", f)], writes=[("py", t, hf)])
                mm1(0)
                for f in range(32):
                    if f + 1 < 32:
                        mm1(f + 1)
                    mm2(f)
                for t in range(2):
                    T = T0 + t
                    norm_gate_res(t, V[("G", 1, s)], x1[t][:], ("x1", t), tmp[t][:], ("tg", t))
                    dst, dkey = dst_fn(T)
                    S.dma("sp", dst, tmp[t][:], reads=[("tg", t)], writes=[dkey])
                    if next_ab is not None:
                        ab = next_ab[s]
                        self.norm_to_hT(NB, tmp[t][:], ("tg", t), ab[0], ab[1], hob[:], "hob")
                        S.dma("sp", self.hT_d[:, :, T * P:(T + 1) * P].rearrange("c p t -> p c t"), hob[:],
                              reads=["hob"], writes=[("hT_d", T)])

    def y_tile_L0(self, T, t, zt, zk, wo, py):
        S = self.S
        for hf in range(2):
            for c in range(8):
                mm(S, py[hf][:], zt[:, c, :], wo[:, c, hf * 512:(hf + 1) * 512], c == 0, c == 7,
                   reads=[zk, ("wo", c)], writes=[("py", t, hf)])


def build_program(stop_after=None, debug=()):
    nc = bass.Bass("TRN2", target_bir_lowering=False)
    Pg = Prog(nc, debug)
    S = Pg.S
    Pg.consts()
    Pg.phase_mod()
    final_keys = []
    with S.scope():
        V0 = Pg.load_layer_vecs(0)
        Pg.phase_L0_proj(V0)
        Pg.phase_L0_pool()
        Pg.phase_L0_attn()

        def dst0(T):
            if stop_after == "L0":
                if T < 2:
                    return Pg.x_d[T * P:(T + 1) * P, :], ("x_d", T)
                return Pg.out[(T - 2) * P:(T - 1) * P, :], ("out", T)
            return Pg.x_d[T * P:(T + 1) * P, :], ("x_d", T)
        Pg.phase_out_mlp(0, V0, Pg.ev_w_out, Pg.y_tile_L0, dst0)
    S.barrier()
    S.finish([])
    S.close()
    return nc, Pg


def host_inputs(inputs):
    f = lambda a: np.ascontiguousarray(np.asarray(a, dtype=np.float32))
    dr_idx, dc_full, mask = _attn_tables()
    rpb = f(inputs["ev_rpb"])[0]
    tab = rpb[:, dr_idx, dc_full[None, :, :]]
    tab = np.ascontiguousarray(tab.transpose(2, 0, 1, 3).reshape(128, 96, 128))
    msk = np.ascontiguousarray(mask.transpose(1, 0, 2))
    bands = np.ascontiguousarray(_pool_bands().transpose(2, 0, 1, 3).reshape(128, 20, 128))
    shared = {
        "ada_w": f(inputs["ada_w"]), "ada_b": f(inputs["ada_b"]), "norm_g": f(inputs["norm_g"]),
        "mlp_w1": f(inputs["mlp_w1"]), "mlp_w2": f(inputs["mlp_w2"]),
        "ev_w_in": f(inputs["ev_w_in"])[0], "ev_w_out": f(inputs["ev_w_out"])[0],
        "ev_pool_w": f(inputs["ev_pool_w"])[0], "ev_pool_scale": f(inputs["ev_pool_scale"])[0],
        "rpb_tab": tab, "msk_tab": msk, "bands": bands, "ident": np.eye(128, dtype=np.float32),
    }
    x = f(inputs["x"]); c = f(inputs["c"]); ctx = f(inputs["ctx"]); cc = f(inputs["c_ctx"])
    maps = []
    for b in range(x.shape[0]):
        cv = np.stack([c[b].reshape(8, 128).T, cc.reshape(8, 128).T], axis=-1)
        m = dict(shared)
        m.update({"x": x[b], "ctx": ctx[b], "cvec": np.ascontiguousarray(cv)})
        maps.append(m)
    return maps


_CACHE = {}


def kernel(**inputs):
    maps = host_inputs(inputs)
    if "nc" not in _CACHE:
        _CACHE["nc"] = build_program()
    nc, Pg = _CACHE["nc"]
    res = run_bass_kernel_spmd(nc, maps, core_ids=list(range(8)))
    return np.stack([np.asarray(r["out"]) for r in res.results], axis=0)

LWC = -0.6065306597126334
GN_EPS = 64e-5


def _scan_consts():
    s = np.arange(128)[:, None]
    t = np.arange(128)[None, :]
    tri = np.stack([(s <= t), (s >= t)]).astype(np.float32)
    strict = np.stack([(s < t), (s > t)]).astype(np.float32)
    mT = strict.transpose(0, 2, 1)
    m4 = np.concatenate([tri, strict, tri, mT], axis=2)
    lm = []
    for l in range(7):
        b = 1 << l
        lm.append(((s // (2 * b)) == (t // (2 * b))) & (((s // b) % 2) == 0) & (((t // b) % 2) == 1))
    lm = np.stack(lm).astype(np.float32)
    lmN = np.stack([lm, lm.transpose(0, 2, 1)]) + np.eye(128, dtype=np.float32)[None, None]
    return tri, m4, np.ascontiguousarray(lmN)


def _tt(S, eng, out, a, b, op, reads, writes):
    return S.op(eng, lambda e: e.tensor_tensor(out=out, in0=a, in1=b, op=op), reads=reads, writes=writes)


def _stt(S, eng, out, a, sc, b, op0, op1, reads, writes):
    return S.op("dve", lambda e: e.scalar_tensor_tensor(out=out, in0=a, scalar=sc, in1=b, op0=op0, op1=op1),
                reads=reads, writes=writes)


def _act(S, out, in_, func, reads, writes, **kw):
    return S.op("act", lambda e: e.activation(out=out, in_=in_, func=func, **kw), reads=reads, writes=writes)


def _h3(ap):
    return ap.rearrange("p (h k) -> p h k", k=64)


class Prog1(Prog):
    def __init__(self, nc, debug=()):
        super().__init__(nc, debug)
        dt = nc.dram_tensor
        I = lambda name, shape: dt(name, list(shape), F32, kind="ExternalInput").ap()
        self.rw_mu = I("rw_mu", [6, D])
        self.rw_wr = I("rw_wr", [D, D]); self.rw_wk = I("rw_wk", [D, D])
        self.rw_wv = I("rw_wv", [D, D]); self.rw_wo = I("rw_wo", [D, D])
        self.rw_w0 = I("rw_w0", [2, D]); self.rw_a0 = I("rw_a0", [2, D])
        self.w1cat = I("w1cat", [D, P]); self.a1cat = I("a1cat", [D, P]); self.rw_g1 = I("rw_g1", [D, P])
        self.w2cat = I("w2cat", [P, D]); self.a2cat = I("a2cat", [P, D]); self.rw_g2 = I("rw_g2", [P, D])
        self.rw_kk = I("rw_kk", [1, D]); self.rw_ka = I("rw_ka", [1, D]); self.rw_rk = I("rw_rk", [1, D])
        self.rw_lng = I("rw_lng", [1, D]); self.rw_lnb = I("rw_lnb", [1, D])
        self.tri_c = I("tri_c", [2, P, P]); self.m4_c = I("m4_c", [2, P, 512]); self.lmT_c = I("lmT_c", [2, 7, P, P])
        X = lambda name, shape, d=F32: (dt(name, list(shape), d, kind="ExternalOutput").ap() if name in debug
                                        else dt(name, list(shape), d).ap())
        self.hT_d = X("hT_d", [8, P, NTOK])
        self.featT_d = X("featT_d", [2, NT, P, 8 * 4 * P], BF16)
        self.vtok_d = X("vtok_d", [NTOK, D], BF16)
        self.bk_d = X("bk_d", [2, NT, P, 2 * D], BF16)
        self.gC_d = X("gC_d", [2, NT, P, 8])
        self.g_d = X("g_d", [NTOK, D])
        self.bonus_d = X("bonus_d", [NTOK, D])
        self.y_d = X("y_d", [2, NTOK, D])

    def phase_R0(self, V):
        S = self.S
        with S.scope():
            NB = self.make_norm_bufs("r0")
            xt = [S.sb(f"r0x{i}", [P, D], F32) for i in range(2)]
            ho = [S.sb(f"r0h{i}", [P, 8, P], F32) for i in range(2)]
            for T in range(NT):
                s = 1 if T < 2 else 0
                b = T % 2
                src, sk = self.x_src(1, T)
                S.dma("sp", xt[b][:], src, reads=[sk], writes=[("r0x", b)])
                self.norm_to_hT(NB, xt[b][:], ("r0x", b), V[("A", 0, s)], V[("B", 0, s)], ho[b][:], ("r0h", b))
                S.dma("sp", self.hT_d[:, :, T * P:(T + 1) * P].rearrange("c p t -> p c t"), ho[b][:],
                      reads=[("r0h", b)], writes=[("hT_d", T)])

    def phase_R1(self):
        S = self.S
        with S.scope():
            W = {}
            for nm, src in (("wr", self.rw_wr), ("wk", self.rw_wk), ("wv", self.rw_wv)):
                W[nm] = S.sb(nm, [P, 8, D], BF16)
                for c in range(0, 8, 4):
                    S.dma("pool", W[nm][:, c:c + 4, :], src[c * P:(c + 4) * P, :].rearrange("(c p) n -> p c n", p=P), writes=[nm])
            for nm, src in (("w1c", self.w1cat), ("a1c", self.a1cat), ("g1", self.rw_g1)):
                W[nm] = S.sb(nm, [P, 8, P], BF16)
                S.dma("pool", W[nm][:], src.rearrange("(c p) n -> p c n", p=P), writes=[nm])
            for nm, src in (("w2c", self.w2cat), ("a2c", self.a2cat), ("g2", self.rw_g2)):
                W[nm] = S.sb(nm, [P, D], BF16)
                S.dma("pool", W[nm][:], src, writes=[nm])
            R = {}
            for nm, src in (("kk_r", self.rw_kk), ("ka_r", self.rw_ka), ("rk_r", self.rw_rk),
                            ("w0_0", self.rw_w0[0:1, :]), ("w0_1", self.rw_w0[1:2, :]),
                            ("a0_0", self.rw_a0[0:1, :]), ("a0_1", self.rw_a0[1:2, :])):
                R[nm] = S.sb(nm, [P, D], F32)
                S.dma("sp", R[nm][:], src.broadcast_to([P, D]), writes=[nm])
            mu = S.sb("mu", [P, 6, 8], F32)
            with self.nc.allow_non_contiguous_dma(reason="tiny"):
                S.dma("sp", mu[:], self.rw_mu.rearrange("j (c p) -> p j c", p=P), writes=["mu"])
            tri = S.sb("tri", [P, 2, P], F32)
            S.dma("sp", tri[:], self.tri_c.rearrange("d s t -> s d t"), writes=["tri"])
            onef = S.sb("onef", [P, P], F32)
            S.op("dve", lambda e: e.memset(onef[:], 1.0), writes=["onef"])
            hbuf = S.sb("hbuf", [P, 8, P + 2], F32)
            xx = S.sb("xx", [P, 8, P], F32)
            mix = S.sb("mix", [P, 6, 8, P], BF16)
            hid = S.sb("hid", [P, 3, P], BF16)
            F = {n: S.sb(n, [P, D], F32) for n in ("r_sb", "k_sb", "v_sb", "kkn", "tA", "tB", "lw", "tC0", "tC1", "tD", "kd0", "kd1", "tE0", "tE1", "tF0", "tF1", "tG", "tH")}
            ob = [S.sb(f"ob{i}", [P, D], BF16) for i in range(4)]
            vb = S.sb("vb", [P, D], BF16)
            ft = S.sb("ft", [P, 8, 4, P], BF16)
            bkt = S.sb("bkt", [P, 2, D], BF16)
            st16 = S.sb("st16", [P, 64], F32)
            gcs = S.sb("gcs", [P, 8], F32)
            pA = [[S.ps(f"pA{i}_{h}", [P, 512], F32) for h in range(2)] for i in range(2)]
            pCl = [S.ps(f"pCl{h}", [P, 512], F32) for h in range(2)]
            pF = S.ps("pF", [P, 512], F32)
            pT = S.ps("pT", [P, 8, P], BF16)
            ipa = 0

            def proj(lhs_fn, rhs, rkey, K0=0, K=P, nchunks=8, lkeys=()):
                nonlocal ipa
                i = ipa % 2
                ipa += 1
                for hf in range(2):
                    for c in range(nchunks):
                        mm(S, pA[i][hf][:], lhs_fn(c), rhs(c, hf), c == 0, c == nchunks - 1,
                           reads=list(lkeys) + [rkey], writes=[("pA", i, hf)])
                return pA[i], [("pA", i, 0), ("pA", i, 1)]

            def evac2(fn_half):
                for hf in range(2):
                    fn_half(hf, slice(hf * 512, (hf + 1) * 512))

            for T in range(NT):
                seq_lo, seq_hi = (0, NCTX) if T < 2 else (NCTX, NTOK)
                t0 = T * P
                lo = max(t0 - 1, seq_lo); hi = min(t0 + P + 1, seq_hi)
                if lo > t0 - 1:
                    S.op("pool", lambda e: e.memset(hbuf[:, :, 0:1], 0.0), writes=["hbuf"])
                if hi < t0 + P + 1:
                    S.op("pool", lambda e: e.memset(hbuf[:, :, P + 1:P + 2], 0.0), writes=["hbuf"])
                S.dma("pool", hbuf[:, :, lo - (t0 - 1):hi - (t0 - 1)], self.hT_d[:, :, lo:hi].rearrange("c p t -> p c t"),
                      reads=[("hT_d", q) for q in range(max(T - 1, 0), min(T + 2, NT))], writes=["hbuf"])
                _tt(S, "dve", xx[:], hbuf[:, :, 0:P], hbuf[:, :, 2:P + 2], ALU.add, ["hbuf"], ["xx"])
                _stt(S, "dve", xx[:], xx[:], 0.5, hbuf[:, :, 1:P + 1], ALU.mult, ALU.subtract, ["xx", "hbuf"], ["xx"])
                for j in range(6):
                    mxt = F["tG"][:].rearrange("p (c t) -> p c t", t=P)
                    _tt(S, "dve", mxt, xx[:], mu[:, j, :][:, :, None].broadcast_to([P, 8, P]), ALU.mult, ["xx", "mu"], ["tG"])
                    _tt(S, "dve", mix[:, j, :, :], mxt, hbuf[:, :, 1:P + 1], ALU.add, ["tG", "hbuf"], [("mix", j)])
                for hi_, (wn, mj, fn) in enumerate((("w1c", 1, AF.Tanh), ("a1c", 4, AF.Copy), ("g1", 5, AF.Sigmoid))):
                    for c in range(8):
                        mm(S, pF[:, 0:P], W[wn][:, c, :], mix[:, mj, c, :], c == 0, c == 7,
                           reads=[wn, ("mix", mj)], writes=["pF"])
                    _act(S, hid[:, hi_, :], pF[:, 0:P], fn, ["pF"], [("hid", hi_)])
                for nm, mj, wn in (("r_sb", 0, "wr"), ("k_sb", 2, "wk"), ("v_sb", 3, "wv")):
                    ps, pk = proj(lambda c: mix[:, mj, c, :], lambda c, hf: W[wn][:, c, hf * 512:(hf + 1) * 512], wn,
                                  lkeys=[("mix", mj)])
                    evac2(lambda hf, sl: _act(S, F[nm][:, sl], ps[hf][:], AF.Copy, [pk[hf]], [nm]))
                S.op("pool", lambda e: e.tensor_copy(out=vb[:], in_=F["v_sb"][:]), reads=["v_sb"], writes=["vb"])
                S.dma("sp", self.vtok_d[t0:t0 + P, :], vb[:], reads=["vb"], writes=[("vtok", T)])
                ps, pk = proj(lambda c: hid[:, 2, :], lambda c, hf: W["g2"][:, hf * 512:(hf + 1) * 512], "g2", nchunks=1,
                              lkeys=[("hid", 2)])
                evac2(lambda hf, sl: _act(S, F["tA"][:, sl], ps[hf][:], AF.Copy, [pk[hf]], ["tA"]))
                S.dma("sp", self.g_d[t0:t0 + P, :], F["tA"][:], reads=["tA"], writes=[("g_d", T)])
                _tt(S, "dve", F["tA"][:], F["k_sb"][:], R["kk_r"][:], ALU.mult, ["k_sb", "kk_r"], ["tA"])
                _tt(S, "dve", F["tB"][:], F["tA"][:], F["tA"][:], ALU.mult, ["tA"], ["tB"])
                S.op("dve", lambda e: e.tensor_reduce(out=st16[:, 0:16], in_=_h3(F["tB"][:]), axis=AX.X, op=ALU.add),
                     reads=["tB"], writes=["st16"])
                S.op("dve", lambda e: e.tensor_scalar(out=st16[:, 0:16], in0=st16[:, 0:16], scalar1=1e-24, scalar2=None, op0=ALU.max),
                     reads=["st16"], writes=["st16"])
                _act(S, st16[:, 0:16], st16[:, 0:16], AF.Sqrt, ["st16"], ["st16"])
                S.op("dve", lambda e: e.reciprocal(out=st16[:, 16:32], in_=st16[:, 0:16]), reads=["st16"], writes=["st16"])
                _tt(S, "dve", _h3(F["kkn"][:]), _h3(F["tA"][:]), st16[:, 16:32][:, :, None].broadcast_to([P, 16, 64]), ALU.mult,
                    ["tA", "st16"], ["kkn"])
                for d in range(2):
                    ps, pk = proj(lambda c: hid[d * 64:(d + 1) * 64, 0, :], lambda c, hf: W["w2c"][d * 64:(d + 1) * 64, hf * 512:(hf + 1) * 512],
                                  "w2c", nchunks=1, lkeys=[("hid", 0)])
                    evac2(lambda hf, sl: _tt(S, "dve", F["tB"][:, sl], ps[hf][:], R[f"w0_{d}"][:, sl], ALU.add, [pk[hf], f"w0_{d}"], ["tB"]))
                    _act(S, F["tB"][:], F["tB"][:], AF.Sigmoid, ["tB"], ["tB"])
                    _act(S, F["lw"][:], F["tB"][:], AF.Copy, ["tB"], ["lw"], scale=LWC)
                    ps, pk = proj(lambda c: hid[d * 64:(d + 1) * 64, 1, :], lambda c, hf: W["a2c"][d * 64:(d + 1) * 64, hf * 512:(hf + 1) * 512],
                                  "a2c", nchunks=1, lkeys=[("hid", 1)])
                    evac2(lambda hf, sl: _tt(S, "dve", F[f"tC{d}"][:, sl], ps[hf][:], R[f"a0_{d}"][:, sl], ALU.add, [pk[hf], f"a0_{d}"], [f"tC{d}"]))
                    _act(S, F[f"tC{d}"][:], F[f"tC{d}"][:], AF.Sigmoid, [f"tC{d}"], [f"tC{d}"])
                    kd = F[f"kd{d}"]; kdk = f"kd{d}"
                    _stt(S, "dve", F["tD"][:], F[f"tC{d}"][:], -1.0, R["ka_r"][:], ALU.add, ALU.mult, [f"tC{d}", "ka_r"], ["tD"])
                    _stt(S, "pool", kd[:], F["tD"][:], 1.0, F["k_sb"][:], ALU.add, ALU.mult, ["tD", "k_sb"], [kdk])
                    _tt(S, "dve", F[f"tC{d}"][:], F["kkn"][:], F[f"tC{d}"][:], ALU.mult, ["kkn", f"tC{d}"], [f"tC{d}"])
                    for hf in range(2):
                        mm(S, pCl[hf][:], tri[:, d, :], F["lw"][:, hf * 512:(hf + 1) * 512], True, True,
                           reads=["tri", "lw"], writes=[("pCl", hf)])
                    evac2(lambda hf, sl: _act(S, F[f"tE{d}"][:, sl], pCl[hf][:], AF.Exp, [("pCl", hf)], [f"tE{d}"]))
                    evac2(lambda hf, sl: _act(S, F[f"tF{d}"][:, sl], pCl[hf][:], AF.Exp, [("pCl", hf)], [f"tF{d}"], scale=-1.0))
                    for hf in range(2):
                        mm(S, pCl[hf][:], onef[:], F["lw"][:, hf * 512:(hf + 1) * 512], True, True,
                           reads=["onef", "lw"], writes=[("pCl", hf)])
                    evac2(lambda hf, sl: _act(S, F["tH"][:, sl], pCl[hf][:], AF.Exp, [("pCl", hf)], ["tH"]))
                    _act(S, F["tG"][:], F["lw"][:], AF.Exp, ["lw"], ["tG"], scale=-1.0)
                    _tt(S, "dve", F["tG"][:], F["tG"][:], F[f"tE{d}"][:], ALU.mult, ["tG", f"tE{d}"], ["tG"])
                    _tt(S, "dve", F["tH"][:], F["tH"][:], F[f"tF{d}"][:], ALU.mult, ["tH", f"tF{d}"], ["tH"])
                    for j in range(8):
                        mm(S, pF[:, 256 + j:257 + j], F["lw"][:, j * P:(j + 1) * P], onef[:, 0:1], True, True,
                           reads=["lw", "onef"], writes=["pF"])
                    _act(S, gcs[:], pF[:, 256:264], AF.Exp, ["pF"], ["gcs"])
                    S.dma("sp", self.gC_d[d, T], gcs[:], reads=["gcs"], writes=[("gC_d", d, T)])
                    _stt(S, "dve", ob[0][:], F["kkn"][:], -1.0, F["tG"][:], ALU.mult, ALU.mult, ["kkn", "tG"], [("ob", 0)])
                    _tt(S, "dve", ob[1][:], F["r_sb"][:], F[f"tE{d}"][:], ALU.mult, ["r_sb", f"tE{d}"], [("ob", 1)])
                    _tt(S, "dve", ob[2][:], F[f"tC{d}"][:], F[f"tF{d}"][:], ALU.mult, [f"tC{d}", f"tF{d}"], [("ob", 2)])
                    _tt(S, "dve", ob[3][:], kd[:], F[f"tF{d}"][:], ALU.mult, [kdk, f"tF{d}"], [("ob", 3)])
                    _tt(S, "dve", bkt[:, 0, :], F[f"tC{d}"][:], F["tH"][:], ALU.mult, [f"tC{d}", "tH"], ["bkt"])
                    _tt(S, "dve", bkt[:, 1, :], kd[:], F["tH"][:], ALU.mult, [kdk, "tH"], ["bkt"])
                    S.dma("sp", self.bk_d[d, T], bkt[:].rearrange("p a n -> p (a n)"), reads=["bkt"], writes=[("bk_d", d, T)])
                    for q in range(4):
                        for c in range(8):
                            S.op("pe", lambda e: e.transpose(out=pT[:, c, :], in_=ob[q][:, c * P:(c + 1) * P], identity=self.idb[:]),
                                 reads=[("ob", q), "idb"], writes=["pT"], accum=(c > 0))
                        if q % 2 == 0:
                            _act(S, ft[:, :, q, :], pT[:], AF.Copy, ["pT"], ["ft"])
                        else:
                            S.op("dve", lambda e: e.tensor_copy(out=ft[:, :, q, :], in_=pT[:]), reads=["pT"], writes=["ft"])
                    S.dma("sp", self.featT_d[d, T], ft[:].rearrange("p j q t -> p (j q t)"), reads=["ft"], writes=[("featT_d", d, T)])
                _tt(S, "dve", F["tD"][:], F["kd0"][:], F["kd1"][:], ALU.add, ["kd0", "kd1"], ["tD"])
                _tt(S, "dve", F["tD"][:], F["tD"][:], F["r_sb"][:], ALU.mult, ["tD", "r_sb"], ["tD"])
                _tt(S, "dve", F["tD"][:], F["tD"][:], R["rk_r"][:], ALU.mult, ["tD", "rk_r"], ["tD"])
                S.op("dve", lambda e: e.tensor_reduce(out=st16[:, 32:48], in_=_h3(F["tD"][:]), axis=AX.X, op=ALU.add),
                     reads=["tD"], writes=["st16"])
                _tt(S, "dve", _h3(F["tD"][:]), _h3(F["v_sb"][:]), st16[:, 32:48][:, :, None].broadcast_to([P, 16, 64]), ALU.mult,
                    ["v_sb", "st16"], ["tD"])
                S.dma("sp", self.bonus_d[t0:t0 + P, :], F["tD"][:], reads=["tD"], writes=[("bonus_d", T)])

    def phase_R2(self):
        S = self.S
        with S.scope():
            m4 = S.sb("m4", [P, 2, 512], F32)
            lmN = S.sb("lmN", [P, 2, 7, P], F32)
            S.dma("sp", m4[:], self.m4_c.rearrange("d s n -> s d n"), writes=["m4"])
            S.dma("sp", lmN[:], self.lmT_c.rearrange("d l s n -> s d l n"), writes=["lmN"])
            idb = self.idb
            NG = 4
            I4 = S.sb("I4", [P, NG, P], BF16)
            for g in range(NG):
                S.op("pool", lambda e: e.tensor_copy(out=I4[:, g, :], in_=idb[:]), reads=["idb"], writes=["I4"])
            I4f = I4[:].rearrange("p g t -> p (g t)")
            ST32 = [S.sb(f"ST32_{d}", [P, 8, 64], F32) for d in range(2)]
            STb = [S.sb(f"STb_{d}", [P, 8, 64], BF16) for d in range(2)]
            for d in range(2):
                S.op("dve", lambda e: e.memset(ST32[d][:], 0.0), writes=[("ST32", d)])
                S.op("dve", lambda e: e.memset(STb[d][:], 0.0), writes=[("STb", d)])
            NBUF = 3
            Fb = [S.sb(f"Fb{i}", [P, 8, 4, P], BF16) for i in range(NBUF)]
            Vb = [S.sb(f"Vb{i}", [P, D], BF16) for i in range(NBUF)]
            BKb = [S.sb(f"BKb{i}", [P, 2, D], BF16) for i in range(NBUF)]
            gCb = [S.sb(f"gCb{i}", [P, 8], F32) for i in range(NBUF)]
            ysb = [S.sb(f"ysb{i}", [P, D], F32) for i in range(NBUF)]
            SL = []
            for sl in range(2):
                R_ = dict(
                    GM=S.sb(f"GM{sl}", [P, NG, 512], BF16),
                    X=[S.sb(f"X{sl}_{i}", [P, NG, P], BF16) for i in range(2)],
                    XT=[S.sb(f"XT{sl}_{i}", [P, NG, P], BF16) for i in range(2)],
                    T1s=S.sb(f"T1s{sl}", [P, NG, P], BF16),
                    Zq=S.sb(f"Zq{sl}", [P, NG, 64], BF16),
                    Pb=S.sb(f"Pb{sl}", [P, NG, 64], BF16),
                    bk=[S.ps(f"bk{sl}_{i}", [P, NG, P], F32) for i in range(3)],
                    bz=S.ps(f"bz{sl}", [P, 8, 64], F32),
                    sl=sl)
                SL.append(R_)

            items = []
            it = 0
            for d in range(2):
                order = list(range(NT)) if d == 0 else [1, 0] + list(range(NT - 1, 1, -1))
                for ci, T in enumerate(order):
                    for g0 in range(0, 16, NG):
                        items.append(dict(d=d, T=T, g0=g0, b=it % NBUF))
                    it += 1

            def heads_of(g0):
                return [(g, g0 + g, (g0 + g) // 2, ((g0 + g) % 2) * 64) for g in range(NG)]

            def load_chunk(w):
                d, T, b = w["d"], w["T"], w["b"]
                S.dma("sp", Fb[b][:].rearrange("p j q t -> p (j q t)"), self.featT_d[d, T], reads=[("featT_d", d, T)], writes=[("Fb", b)])
                S.dma("sp", Vb[b][:], self.vtok_d[T * P:(T + 1) * P, :], reads=[("vtok", T)], writes=[("Vb", b)])
                S.dma("sp", BKb[b][:].rearrange("p a n -> p (a n)"), self.bk_d[d, T], reads=[("bk_d", d, T)], writes=[("BKb", b)])
                S.dma("sp", gCb[b][:], self.gC_d[d, T], reads=[("gC_d", d, T)], writes=[("gCb", b)])

            def run_group(w, R_):
                d, T, b, g0, sl = w["d"], w["T"], w["b"], w["g0"], R_["sl"]
                if g0 == 0:
                    load_chunk(w)
                Fk, Vk, BKk, gk = ("Fb", b), ("Vb", b), ("BKb", b), ("gCb", b)
                GM, X, XT, T1s, Zq, Pb, bk, bz = (R_[n] for n in ("GM", "X", "XT", "T1s", "Zq", "Pb", "bk", "bz"))
                K = lambda n, *a: (n, sl) + a
                hs = heads_of(g0)
                F_ = Fb[b]
                st32, stb = ST32[d], STb[d]
                for (g, h, j, pb_) in hs:
                    bank = bk[g % 3]; bkk = K("bk", g % 3)
                    bv = bank[:].rearrange("p g t -> p (g t)")
                    AR = F_[pb_:pb_ + 64, j, 0:2, :].rearrange("p q t -> p (q t)")
                    mm(S, bv[:, 0:128], F_[pb_:pb_ + 64, j, 2, :], F_[pb_:pb_ + 64, j, 1, :], True, True, reads=[Fk], writes=[bkk])
                    mm(S, bv[:, 128:384], F_[pb_:pb_ + 64, j, 3, :], AR, True, True, reads=[Fk], writes=[bkk])
                    mm(S, bv[:, 384:512], F_[pb_:pb_ + 64, j, 0, :], F_[pb_:pb_ + 64, j, 2, :], True, True, reads=[Fk], writes=[bkk])
                    _tt(S, "dve", GM[:, g, :], bv, m4[:, d, :], ALU.mult, ["m4"], [bkk, K("GM")])
                yield
                for (g, h, j, pb_) in hs:
                    mm(S, bz[:, g, :], F_[pb_:pb_ + 64, j, 0, :], stb[pb_:pb_ + 64, j, :], True, False, reads=[Fk, ("STb", d)], writes=[K("bz")])
                    mm(S, bz[:, g, :], GM[:, g, 128:256], Vb[b][:, h * 64:(h + 1) * 64], False, True, reads=[K("GM"), Vk], writes=[K("bz")])
                _act(S, Zq[:], bz[:, 0:NG, :], AF.Copy, [], [K("bz"), K("Zq")])
                yield
                xi = 0
                mm(S, bk[0][:].rearrange("p g t -> p (g t)"), idb[:], I4f, True, False, reads=["idb", "I4"], writes=[K("bk", 0)])
                for (g, h, j, pb_) in hs:
                    mm(S, bk[0][:, g, :], GM[:, g, 384:512], idb[:], False, True, reads=[K("GM"), "idb"], writes=[K("bk", 0)])
                _tt(S, "dve", X[xi][:], bk[0][:], lmN[:, d, 0:1, :].broadcast_to([P, NG, P]), ALU.mult, ["lmN"], [K("bk", 0), K("X", xi)])
                yield
                _tt(S, "pool", XT[xi][:], GM[:, :, 384:512], lmN[:, 1 - d, 0:1, :].broadcast_to([P, NG, P]), ALU.mult,
                    [K("GM"), "lmN"], [K("XT", xi)])
                _tt(S, "pool", XT[xi][:], XT[xi][:], I4[:], ALU.add, [K("XT", xi), "I4"], [K("XT", xi)])
                yield
                for l in range(1, 7):
                    mm(S, bk[0][:].rearrange("p g t -> p (g t)"), idb[:], I4f, True, False, reads=["idb", "I4"], writes=[K("bk", 0)])
                    for (g, h, j, pb_) in hs:
                        mm(S, bk[0][:, g, :], GM[:, g, 384:512], X[xi][:, g, :], False, True, reads=[K("GM"), K("X", xi)], writes=[K("bk", 0)])
                    _tt(S, "dve", T1s[:], bk[0][:], lmN[:, d, l:l + 1, :].broadcast_to([P, NG, P]), ALU.mult, ["lmN"], [K("bk", 0), K("T1s")])
                    yield
                    for (g, h, j, pb_) in hs:
                        mm(S, bk[1][:, g, :], XT[xi][:, g, :], T1s[:, g, :], True, True, reads=[K("XT", xi), K("T1s")], writes=[K("bk", 1)])
                    if l < 6:
                        for (g, h, j, pb_) in hs:
                            mm(S, bk[2][:, g, :], T1s[:, g, :], XT[xi][:, g, :], True, True, reads=[K("XT", xi), K("T1s")], writes=[K("bk", 2)])
                    _act(S, X[1 - xi][:], bk[1][:], AF.Copy, [], [K("bk", 1), K("X", 1 - xi)])
                    if l < 6:
                        if l % 3 != 0:
                            _act(S, XT[1 - xi][:], bk[2][:], AF.Copy, [], [K("bk", 2), K("XT", 1 - xi)])
                        else:
                            S.op("dve", lambda e: e.tensor_copy(out=XT[1 - xi][:], in_=bk[2][:]), reads=[], writes=[K("bk", 2), K("XT", 1 - xi)])
                    xi = 1 - xi
                    yield
                for (g, h, j, pb_) in hs:
                    mm(S, bz[:, g, :], X[xi][:, g, :], Zq[:, g, :], True, True, reads=[K("X", xi), K("Zq")], writes=[K("bz")])
                S.op("dve", lambda e: e.tensor_copy(out=Pb[:], in_=bz[:, 0:NG, :]), reads=[], writes=[K("bz"), K("Pb")])
                yield
                for (g, h, j, pb_) in hs:
                    yo = bz[:, g, :]
                    mm(S, yo, GM[:, g, 0:128], Pb[:, g, :], True, False, reads=[K("GM"), K("Pb")], writes=[K("bz")])
                    mm(S, yo, GM[:, g, 256:384], Vb[b][:, h * 64:(h + 1) * 64], False, False, reads=[K("GM"), Vk], writes=[K("bz")])
                    mm(S, yo, F_[pb_:pb_ + 64, j, 1, :], stb[pb_:pb_ + 64, j, :], False, True, reads=[Fk, ("STb", d)], writes=[K("bz")])
                for (g, h, j, pb_) in hs:
                    mm(S, bz[:, 4 + g, :], BKb[b][:, 0, j * P:(j + 1) * P], Pb[:, g, :], True, False, reads=[BKk, K("Pb")], writes=[K("bz")])
                    mm(S, bz[:, 4 + g, :], BKb[b][:, 1, j * P:(j + 1) * P], Vb[b][:, h * 64:(h + 1) * 64], False, True,
                       reads=[BKk, Vk], writes=[K("bz")])
                _act(S, ysb[b][:, g0 * 64:(g0 + NG) * 64].rearrange("p (g v) -> p g v", v=64), bz[:, 0:NG, :], AF.Copy, [], [K("bz"), ("ysb", b)])
                for (g, h, j, pb_) in hs:
                    _stt(S, "dve", st32[pb_:pb_ + 64, j, :], st32[pb_:pb_ + 64, j, :], gCb[b][pb_:pb_ + 64, j:j + 1],
                         bz[pb_:pb_ + 64, 4 + g, :], ALU.mult, ALU.add, [gk], [("ST32", d), K("bz")])
                S.op("pool", lambda e: e.tensor_copy(out=stb[:, g0 // 2:g0 // 2 + 2, :], in_=st32[:, g0 // 2:g0 // 2 + 2, :]),
                     reads=[("ST32", d)], writes=[("STb", d)])
                if g0 + NG == 16:
                    S.dma("pool", self.y_d[d, T * P:(T + 1) * P, :], ysb[b][:], reads=[("ysb", b)], writes=[("y_d", d, T)])
                yield

            nxt = 0
            active = [None, None]
            while True:
                progressed = False
                for sl in range(2):
                    if active[sl] is None and nxt < len(items):
                        active[sl] = run_group(items[nxt], SL[sl])
                        nxt += 1
                    if active[sl] is not None:
                        progressed = True
                        try:
                            next(active[sl])
                        except StopIteration:
                            active[sl] = None
                if not progressed:
                    break

    def phase_R3(self):
        S = self.S
        with S.scope():
            R = {}
            for nm, src in (("lng_r", self.rw_lng), ("lnb_r", self.rw_lnb)):
                R[nm] = S.sb(nm, [P, D], F32)
                S.dma("sp", R[nm][:], src.broadcast_to([P, D]), writes=[nm])
            B = [{n: S.sb(f"{n}{i}", [P, D], F32) for n in ("yf", "yb", "gg", "bo")} for i in range(2)]
            zb = [S.sb(f"zb{i}", [P, D], BF16) for i in range(2)]
            zt = [S.sb(f"zt3_{i}", [P, 8, P], BF16) for i in range(2)]
            st = [S.sb(f"st3_{i}", [P, 64], F32) for i in range(2)]
            pT = [S.ps(f"pT3_{i}", [P, 8, P], BF16) for i in range(2)]
            for T in range(2, NT):
                b = T % 2
                Bf = B[b]
                k = lambda n: (n, b)
                t0 = T * P
                S.dma("pool", Bf["yf"][:], self.y_d[0, t0:t0 + P, :], reads=[("y_d", 0, T)], writes=[k("yf")])
                S.dma("pool", Bf["yb"][:], self.y_d[1, t0:t0 + P, :], reads=[("y_d", 1, T)], writes=[k("yb")])
                S.dma("pool", Bf["gg"][:], self.g_d[t0:t0 + P, :], reads=[("g_d", T)], writes=[k("gg")])
                S.dma("pool", Bf["bo"][:], self.bonus_d[t0:t0 + P, :], reads=[("bonus_d", T)], writes=[k("bo")])
                y = Bf["yf"]; t2 = Bf["yb"]
                _tt(S, "dve", y[:], y[:], t2[:], ALU.add, [k("yf"), k("yb")], [k("yf")])
                S.op("dve", lambda e: e.tensor_reduce(out=st[b][:, 0:16], in_=_h3(y[:]), axis=AX.X, op=ALU.add), reads=[k("yf")], writes=[k("st")])
                S.op("dve", lambda e: e.tensor_scalar(out=st[b][:, 0:16], in0=st[b][:, 0:16], scalar1=-1.0 / 64, scalar2=None, op0=ALU.mult),
                     reads=[k("st")], writes=[k("st")])
                _tt(S, "dve", _h3(y[:]), _h3(y[:]), st[b][:, 0:16][:, :, None].broadcast_to([P, 16, 64]), ALU.add, [k("yf"), k("st")], [k("yf")])
                _tt(S, "pool", t2[:], y[:], y[:], ALU.mult, [k("yf")], [k("yb")])
                S.op("dve", lambda e: e.tensor_reduce(out=st[b][:, 16:32], in_=_h3(t2[:]), axis=AX.X, op=ALU.add), reads=[k("yb")], writes=[k("st")])
                S.op("dve", lambda e: e.tensor_scalar(out=st[b][:, 16:32], in0=st[b][:, 16:32], scalar1=1.0 / 64, scalar2=GN_EPS, op0=ALU.mult, op1=ALU.add),
                     reads=[k("st")], writes=[k("st")])
                _act(S, st[b][:, 16:32], st[b][:, 16:32], AF.Sqrt, [k("st")], [k("st")])
                S.op("dve", lambda e: e.reciprocal(out=st[b][:, 32:48], in_=st[b][:, 16:32]), reads=[k("st")], writes=[k("st")])
                _tt(S, "dve", _h3(y[:]), _h3(y[:]), st[b][:, 32:48][:, :, None].broadcast_to([P, 16, 64]), ALU.mult, [k("yf"), k("st")], [k("yf")])
                _tt(S, "pool", y[:], y[:], R["lng_r"][:], ALU.mult, [k("yf"), "lng_r"], [k("yf")])
                _tt(S, "pool", y[:], y[:], R["lnb_r"][:], ALU.add, [k("yf"), "lnb_r"], [k("yf")])
                _tt(S, "dve", y[:], y[:], Bf["bo"][:], ALU.add, [k("yf"), k("bo")], [k("yf")])
                _tt(S, "dve", zb[b][:], y[:], Bf["gg"][:], ALU.mult, [k("yf"), k("gg")], [k("zb")])
                for c in range(8):
                    S.op("pe", lambda e: e.transpose(out=pT[b][:, c, :], in_=zb[b][:, c * P:(c + 1) * P], identity=self.idb[:]),
                         reads=[k("zb"), "idb"], writes=[k("pT3")], accum=(c > 0))
                _act(S, zt[b][:], pT[b][:], AF.Copy, [k("pT3")], [k("zt3")])
                S.dma("sp", self.zT_d[:, :, t0:t0 + P].rearrange("c p t -> p c t"), zt[b][:], reads=[k("zt3")],
                      writes=[("zT", c, T) for c in range(8)])


def build_program(stop_after=None, debug=(), phases="M0ABCD1abcde"):
    nc = bass.Bass("TRN2", target_bir_lowering=False)
    Pg = Prog1(nc, debug)
    S = Pg.S
    Pg.consts()
    if "M" in phases:
        Pg.phase_mod()
    if "0" in phases:
      with S.scope():
        V0 = Pg.load_layer_vecs(0)
        if "A" in phases: Pg.phase_L0_proj(V0)
        if "B" in phases: Pg.phase_L0_pool()
        if "C" in phases: Pg.phase_L0_attn()
        if "D" in phases:
            ab1 = Pg.load_ab(1, 0, "ab1")
            Pg.phase_out_mlp(0, V0, Pg.ev_w_out, Pg.y_tile_L0, lambda T: (Pg.x_d[T * P:(T + 1) * P, :], ("x_d", T)), next_ab=ab1)
    if "1" in phases:
      with S.scope():
        V1 = Pg.load_layer_vecs(1, part="ab")
        if "a" in phases and "D" not in phases: Pg.phase_R0(V1)
        if "b" in phases: Pg.phase_R1()
        if "c" in phases: Pg.phase_R2()
        if "d" in phases: Pg.phase_R3()
        if "e" in phases:
            Pg.load_layer_vecs(1, V=V1, part="g")
        if "e" in phases: Pg.phase_out_mlp(1, V1, Pg.rw_wo, Pg.y_tile_L0, lambda T: (Pg.out[(T - 2) * P:(T - 1) * P, :], ("out", T)))
    S.barrier()
    S.finish([])
    S.close()
    return nc, Pg


_host_inputs0 = host_inputs


def host_inputs(inputs):
    maps = _host_inputs0(inputs)
    f = lambda a: np.ascontiguousarray(np.asarray(a, dtype=np.float32))
    tri, m4, lmT = _scan_consts()
    sh = {
        "rw_mu": f(inputs["rw_mu"])[0], "rw_wr": f(inputs["rw_wr"])[0], "rw_wk": f(inputs["rw_wk"])[0],
        "rw_wv": f(inputs["rw_wv"])[0], "rw_wo": f(inputs["rw_wo"])[0],
        "rw_w0": f(inputs["rw_w0"])[0], "rw_a0": f(inputs["rw_a0"])[0],
        "w1cat": f(np.concatenate([inputs["rw_w1"][0, 0], inputs["rw_w1"][0, 1]], axis=1)),
        "a1cat": f(np.concatenate([inputs["rw_a1"][0, 0], inputs["rw_a1"][0, 1]], axis=1)),
        "rw_g1": f(inputs["rw_g1"])[0],
        "w2cat": f(np.asarray(inputs["rw_w2"])[0].reshape(128, 1024)), "a2cat": f(np.asarray(inputs["rw_a2"])[0].reshape(128, 1024)),
        "rw_g2": f(inputs["rw_g2"])[0],
        "rw_kk": f(inputs["rw_kk"]).reshape(1, 1024), "rw_ka": f(inputs["rw_ka"]).reshape(1, 1024),
        "rw_rk": f(inputs["rw_rk"]).reshape(1, 1024), "rw_lng": f(inputs["rw_lng"]).reshape(1, 1024),
        "rw_lnb": f(inputs["rw_lnb"]).reshape(1, 1024),
        "tri_c": f(tri), "m4_c": f(m4), "lmT_c": f(lmT),
    }
    for m in maps:
        m.update(sh)
    return maps
```

```python
import contextlib
import numpy as np
import concourse.bass as bass
import concourse.mybir as mybir

F32 = mybir.dt.float32
BF16 = mybir.dt.bfloat16
AF = mybir.ActivationFunctionType
ALU = mybir.AluOpType
AX = mybir.AxisListType

SEM_LIMIT = 10000


class _Ctr:
    def __init__(self, S, name, step):
        self.S = S
        self.name = name
        self.step = step
        self.gen = 0
        self.sem = S._newsem(f"{name}_0")
        self.val = 0

    def next_event(self):
        if self.val + self.step > SEM_LIMIT:
            self.gen += 1
            self.sem = self.S._newsem(f"{self.name}_{self.gen}")
            self.val = 0
        self.val += self.step
        return (self.sem, self.val)


class _PsView:
    def __init__(self, t, shape):
        self.t = t
        self.n1 = shape[1]

    def __getitem__(self, key):
        if not isinstance(key, tuple):
            key = (key,)
        key = list(key)
        if len(key) < 2:
            key.append(slice(None))
        k1 = key[1]
        if isinstance(k1, slice):
            start, stop, step = k1.indices(self.n1)
            key[1] = slice(start, stop, step)
        return self.t[tuple(key)]


class _Eng:
    def __init__(self, S, name, obj):
        self.name = name
        self.obj = obj
        self.ctr = _Ctr(S, "s_" + name, 1)
        self.seen = {}
        self.n_issued = 0
        self.last_ins = None
        self.last_has_inc = False
        self.inc_idx = []
        self.inc_ev = []


class LazyEv:
    __slots__ = ("eng", "idx")

    def __init__(self, eng, idx):
        self.eng = eng
        self.idx = idx


class _Res:
    __slots__ = ("w", "r")

    def __init__(self):
        self.w = None
        self.r = {}


class Sched:
    def __init__(self, nc, n_dma_slots=8):
        self.nc = nc
        self.stack = contextlib.ExitStack()
        self.scopes = [self.stack]
        self.res = {}
        self.engs = {
            "pe": _Eng(self, "pe", nc.tensor),
            "act": _Eng(self, "act", nc.scalar),
            "dve": _Eng(self, "dve", nc.vector),
            "pool": _Eng(self, "pool", nc.gpsimd),
            "sp": _Eng(self, "sp", nc.sync),
        }
        self.dma_slots = {}
        for q in ("sp", "pool", "act"):
            self.dma_slots[q] = [_Ctr(self, f"d_{q}{i}", 16) for i in range(n_dma_slots)]
        self.dma_rr = {"sp": 0, "pool": 0, "act": 0}
        self.n_inst = 0
        self.uid = 0
        self.pending = None
        self.lazy_engines = ()

    def _newsem(self, name):
        return self.stack.enter_context(self.nc.semaphore(name))

    def sb(self, name, shape, dt):
        self.uid += 1
        return self.scopes[-1].enter_context(self.nc.sbuf_tensor(f"sb{self.uid}_{name}", list(shape), dt))

    def ps(self, name, shape, dt=F32):
        self.uid += 1
        esz = 4 if dt == F32 else 2
        per_part = esz
        for d_ in shape[1:]:
            per_part *= d_
        assert per_part <= 2048, (name, shape)
        shape = list(shape)
        if per_part < 2048:
            rest = per_part // shape[1]
            assert 2048 % rest == 0, (name, shape)
            full = [shape[0], 2048 // rest] + shape[2:]
            t = self.scopes[-1].enter_context(self.nc.psum_tensor(f"ps{self.uid}_{name}", full, dt))
            return _PsView(t, shape)
        return self.scopes[-1].enter_context(self.nc.psum_tensor(f"ps{self.uid}_{name}", shape, dt))

    @contextlib.contextmanager
    def scope(self):
        st = contextlib.ExitStack()
        self.scopes.append(st)
        try:
            yield
        finally:
            self.barrier()
            self.scopes.pop()
            st.close()

    def barrier(self):
        evs = []
        for e in self.engs.values():
            if e.n_issued > 0:
                evs.append(self._resolve(LazyEv(e, e.n_issued - 1)))
        for q in self.dma_slots:
            for ctr in self.dma_slots[q]:
                if ctr.val > 0:
                    evs.append((ctr.sem, ctr.val))
        for e in self.engs.values():
            for ev in evs:
                self._wait(e, ev)

    def _r(self, key):
        r = self.res.get(key)
        if r is None:
            r = self.res[key] = _Res()
        return r

    def _resolve(self, ev):
        if not isinstance(ev, LazyEv):
            return ev
        import bisect
        e = ev.eng
        k = bisect.bisect_left(e.inc_idx, ev.idx)
        if k < len(e.inc_idx):
            return e.inc_ev[k]
        assert e.last_ins is not None and not e.last_has_inc and e.n_issued - 1 >= ev.idx
        sv = e.ctr.next_event()
        e.last_ins.then_inc(sv[0], 1)
        e.last_has_inc = True
        e.inc_idx.append(e.n_issued - 1)
        e.inc_ev.append(sv)
        return sv

    def _wait(self, eng, ev):
        if ev is None:
            return
        if isinstance(ev, LazyEv) and ev.eng is eng and eng.name == "pe":
            return
        sem, val = self._resolve(ev)
        k = id(sem)
        if eng.seen.get(k, 0) >= val:
            return
        if self.pending is not None:
            cur = self.pending.get(k)
            if cur is None or cur[1] < val:
                self.pending[k] = (sem, val)
            return
        eng.obj.wait_ge(sem, val)
        eng.seen[k] = val

    def _flush(self, eng):
        pend = list(self.pending.values())
        self.pending = None
        for (sem, val) in pend[:-1]:
            eng.obj.wait_ge(sem, val)
            eng.seen[id(sem)] = val
        if pend:
            sem, val = pend[-1]
            eng.seen[id(sem)] = val
            return (sem, val)
        return None

    def _deps(self, eng, reads, writes, skip_same_eng_write=False):
        for key in reads:
            r = self._r(key)
            self._wait(eng, r.w)
        inorder = eng.name in ("act", "dve")
        for key in writes:
            r = self._r(key)
            if not ((skip_same_eng_write or inorder) and isinstance(r.w, LazyEv) and r.w.eng is eng):
                self._wait(eng, r.w)
            for ev in r.r.values():
                if inorder and isinstance(ev, LazyEv) and ev.eng is eng:
                    continue
                self._wait(eng, ev)

    def _commit(self, ev, reads, writes):
        rk = ev.eng.name if isinstance(ev, LazyEv) else id(ev[0])
        for key in reads:
            self._r(key).r[rk] = ev
        for key in writes:
            r = self._r(key)
            r.w = ev
            r.r = {}

    def op(self, engname, fn, reads=(), writes=(), accum=False):
        eng = self.engs[engname]
        self.pending = {}
        self._deps(eng, reads, writes, skip_same_eng_write=accum)
        last = self._flush(eng)
        ins = fn(eng.obj)
        if last is not None:
            ins._wait_ge(last[0], last[1])
        eng.last_ins = ins
        eng.last_has_inc = False
        ev = LazyEv(eng, eng.n_issued)
        eng.n_issued += 1
        if engname not in self.lazy_engines:
            self._resolve(ev)
        self._commit(ev, reads, writes)
        self.n_inst += 1
        return ev

    def dma(self, q, out, in_, reads=(), writes=(), **kw):
        eng = self.engs[q]
        slots = self.dma_slots[q]
        i = self.dma_rr[q]
        self.dma_rr[q] = (i + 1) % len(slots)
        ctr = slots[i]
        self.pending = {}
        if ctr.val > 0:
            self._wait(eng, (ctr.sem, ctr.val))
        self._deps(eng, reads, writes)
        last = self._flush(eng)
        ev = ctr.next_event()
        ins = eng.obj.dma_start(out=out, in_=in_, **kw)
        if last is not None:
            ins._wait_ge(last[0], last[1])
        ins.then_inc(ev[0], 16)
        self._commit(ev, reads, writes)
        self.n_inst += 1
        return ev

    def finish(self, final_keys):
        eng = self.engs["sp"]
        for key in final_keys:
            r = self._r(key)
            self._wait(eng, r.w)
        for q in self.dma_slots:
            for ctr in self.dma_slots[q]:
                if ctr.val > 0:
                    self._wait(eng, (ctr.sem, ctr.val))

    def close(self):
        self.stack.close()

from concourse.bass_utils import run_bass_kernel_spmd

D = 1024
NCTX = 256
NLAT = 4096
NTOK = NCTX + NLAT
NT = NTOK // 128
EPS = 1e-6
P = 128


def _pool_bands():
    L = 1024
    out = np.zeros((4, 5, 128, 128), np.float32)
    for g, w in enumerate((2, 4, 8, 16)):
        def full(L):
            t = np.arange(L)
            lo = np.clip(t - w // 2, 0, L)
            hi = np.clip(t + w // 2, 0, L)
            s = np.arange(L)[:, None]
            m = ((s >= lo[None, :]) & (s < hi[None, :])).astype(np.float64) / (hi - lo)[None, :]
            m -= np.eye(L)
            return m
        m = full(L)
        out[g, 0] = m[3 * 128:4 * 128, 4 * 128:5 * 128]
        out[g, 1] = m[5 * 128:6 * 128, 4 * 128:5 * 128]
        out[g, 2] = m[4 * 128:5 * 128, 4 * 128:5 * 128]
        out[g, 3] = m[0:128, 0:128]
        out[g, 4] = m[L - 128:, L - 128:]
    return out


_VARS = [(-2, "pm"), (-1, "f"), (0, "f"), (1, "f"), (2, "pp")] + [(d, "f") for d in range(-3, 4)]


def _attn_tables():
    kc = np.arange(64)
    qc = np.arange(64)
    c_start = np.clip(qc - 8, 0, 48)
    col_ok = (kc[:, None] >= c_start[None, :]) & (kc[:, None] < c_start[None, :] + 16)
    dc_idx = np.clip(kc[:, None] - qc[None, :], -15, 15) + 15
    dr_idx = np.zeros((12, 128, 128), np.int64)
    dc_full = np.zeros((128, 128), np.int64)
    mask = np.zeros((12, 128, 128), np.float32)
    for a in range(2):
        for b in range(2):
            dc_full[a * 64:(a + 1) * 64, b * 64:(b + 1) * 64] = dc_idx
    for v, (dl, kind) in enumerate(_VARS):
        for a in range(2):
            for b in range(2):
                dr = 2 * dl + a - b + 7
                vis = True
                if kind == "pm":
                    vis = not (a == 0 and b == 1)
                elif kind == "pp":
                    vis = (a == 0 and b == 1)
                dr_idx[v, a * 64:(a + 1) * 64, b * 64:(b + 1) * 64] = min(max(dr, 0), 14)
                if vis and 0 <= dr <= 14:
                    mask[v, a * 64:(a + 1) * 64, b * 64:(b + 1) * 64] = col_ok
    return dr_idx, dc_full, mask


def mm(S, out, lhsT, rhs, start, stop, reads, writes):
    return S.op("pe", lambda e: e.matmul(out, lhsT=lhsT, rhs=rhs, start=start, stop=stop),
                reads=reads, writes=writes, accum=not start)


class Prog:
    def __init__(self, nc, debug=()):
        self.nc = nc
        self.S = Sched(nc)
        self.debug = debug
        self.dbg_out = {}
        dt = nc.dram_tensor
        I = lambda name, shape: dt(name, list(shape), F32, kind="ExternalInput").ap()
        self.x_in = I("x", [NLAT, D])
        self.ctx_in = I("ctx", [NCTX, D])
        self.cvec = I("cvec", [P, 8, 2])
        self.ada_w = I("ada_w", [2, D, 6 * D])
        self.ada_b = I("ada_b", [2, 6 * D])
        self.norm_g = I("norm_g", [2, 4, D])
        self.mlp_w1 = I("mlp_w1", [2, D, 4 * D])
        self.mlp_w2 = I("mlp_w2", [2, 4 * D, D])
        self.ev_w_in = I("ev_w_in", [D, 2 * D])
        self.ev_w_out = I("ev_w_out", [D, D])
        self.ev_pool_w = I("ev_pool_w", [4, P, P])
        self.ev_pool_scale = I("ev_pool_scale", [512])
        self.rpb_tab = I("rpb_tab", [P, 8 * 12, P])
        self.msk_tab = I("msk_tab", [P, 12, P])
        self.bands = I("bands", [P, 20, P])
        self.ident = I("ident", [P, P])
        self.out = dt("out", [NLAT, D], F32, kind="ExternalOutput").ap()
        X = lambda name, shape, d=F32: (dt(name, list(shape), d, kind="ExternalOutput").ap() if name in debug
                                        else dt(name, list(shape), d).ap())
        self.modd = X("modd", [2, 2, 6 * D])
        self.x_d = X("x_d", [NTOK, D])
        self.upool_d = X("upool_d", [NTOK, 512], BF16)
        self.v_d = X("v_d", [NTOK, 512], BF16)
        self.qT_d = X("qT_d", [4, P, NTOK], BF16)
        self.kT_d = X("kT_d", [4, P, NTOK], BF16)
        self.zT_d = X("zT_d", [8, P, NTOK], BF16)

    def dbg(self, name, shape, dtp=F32):
        t = self.nc.dram_tensor("dbg_" + name, list(shape), dtp, kind="ExternalOutput").ap()
        self.dbg_out[name] = t
        return t

    def consts(self):
        S = self.S
        self.idb = S.sb("idb", [P, P], BF16)
        S.dma("pool", self.idb[:], self.ident, writes=["idb"])
        self.ones_bf = S.sb("ones_bf", [P, P], BF16)
        S.op("dve", lambda e: e.memset(self.ones_bf[:], 1.0), writes=["ones_bf"])
        self.eps_t = S.sb("eps_t", [P, 1], F32)
        S.op("dve", lambda e: e.memset(self.eps_t[:], EPS), writes=["eps_t"])

    def phase_mod(self):
        S = self.S
        with S.scope():
            cv = S.sb("cv", [P, 8, 2], F32)
            cvb = S.sb("cvb", [P, 8, 2], BF16)
            S.dma("sp", cv[:], self.cvec, writes=["cv"])
            S.op("act", lambda e: e.activation(out=cvb[:], in_=cv[:], func=AF.Silu), reads=["cv"], writes=["cvb"])
            aw = S.sb("aw", [P, 8, 6 * D], BF16)
            ab = S.sb("ab", [2, 6 * D], F32)
            mrow = S.sb("mrow", [2, 6 * D], F32)
            pss = [S.ps(f"pm{i}", [2, 512], F32) for i in range(4)]
            for l in range(2):
                for c in range(8):
                    S.dma("pool", aw[:, c, :], self.ada_w[l, c * P:(c + 1) * P, :], writes=[("aw", c)])
                S.dma("sp", ab[:], self.ada_b[l:l + 1, :].broadcast_to([2, 6 * D]), writes=["ab"])
                for n in range(12):
                    ps = pss[n % 4]
                    k = ("pm", n % 4)
                    for c in range(8):
                        mm(S, ps[:], cvb[:, c, :], aw[:, c, n * 512:(n + 1) * 512], c == 0, c == 7,
                           reads=["cvb", ("aw", c)], writes=[k])
                    S.op("dve", lambda e: e.tensor_tensor(out=mrow[:, n * 512:(n + 1) * 512], in0=ps[:],
                                                          in1=ab[:, n * 512:(n + 1) * 512], op=ALU.add),
                         reads=[k, "ab"], writes=["mrow"])
                S.dma("sp", self.modd[l], mrow[:], reads=["mrow"], writes=[("modd", l)])

    def load_layer_vecs(self, l, V=None, part="all"):
        S = self.S
        V = {} if V is None else V
        for s in range(2):
            for which in range(2):
                if part in ("all", "ab"):
                    V[("A", which, s)] = (S.sb(f"A{which}_{s}", [P, 8], F32), f"A{which}_{s}")
                    V[("B", which, s)] = (S.sb(f"B{which}_{s}", [P, 8], F32), f"B{which}_{s}")
                if part in ("all", "g") and not (l == 1 and s == 1):
                    V[("G", which, s)] = (S.sb(f"GG{which}_{s}", [P, D], F32), f"GG{which}_{s}")
        with S.scope(), self.nc.allow_non_contiguous_dma(reason="tiny per-feature vectors"):
            tmp = S.sb("lv_tmp", [P, 8], F32)
            rowt = S.sb("lv_row", [P, D], F32)
            for s in range(2):
                for which, (ish, isc, ig) in enumerate(((0, 1, 0), (3, 4, 2))):
                    if part == "g":
                        continue
                    A, ka = V[("A", which, s)]
                    B, kb = V[("B", which, s)]
                    S.dma("sp", B[:], self.modd[l, s, ish * D:(ish + 1) * D].rearrange("(c p) -> p c", p=P),
                          reads=[("modd", l)], writes=[kb])
                    S.dma("sp", A[:], self.modd[l, s, isc * D:(isc + 1) * D].rearrange("(c p) -> p c", p=P),
                          reads=[("modd", l)], writes=[ka])
                    S.dma("sp", tmp[:], self.norm_g[l, ig, :].rearrange("(c p) -> p c", p=P), writes=["lv_tmp"])
                    S.op("dve", lambda e: e.scalar_tensor_tensor(out=A[:], in0=A[:], scalar=1.0, in1=tmp[:],
                                                                 op0=ALU.add, op1=ALU.mult),
                         reads=[ka, "lv_tmp"], writes=[ka])
                for which, (igt, ig) in enumerate(((2, 1), (5, 3))):
                    if (l == 1 and s == 1) or part == "ab":
                        continue
                    G, kg = V[("G", which, s)]
                    S.dma("sp", G[:], self.modd[l, s:s + 1, igt * D:(igt + 1) * D].broadcast_to([P, D]),
                          reads=[("modd", l)], writes=[kg])
                    S.dma("sp", rowt[:], self.norm_g[l, ig:ig + 1, :].broadcast_to([P, D]), writes=["lv_row"])
                    S.op("dve", lambda e: e.tensor_tensor(out=G[:], in0=G[:], in1=rowt[:], op=ALU.mult),
                         reads=[kg, "lv_row"], writes=[kg])
        return V

    def load_ab(self, l, which, tag):
        S = self.S
        ish, isc, ig = ((0, 1, 0), (3, 4, 2))[which]
        out = {}
        with self.nc.allow_non_contiguous_dma(reason="tiny per-feature vectors"):
            tmp = S.sb(f"{tag}_t", [P, 8], F32)
            for s in range(2):
                A = S.sb(f"{tag}_A{s}", [P, 8], F32); ka = f"{tag}_A{s}"
                B = S.sb(f"{tag}_B{s}", [P, 8], F32); kb = f"{tag}_B{s}"
                S.dma("sp", B[:], self.modd[l, s, ish * D:(ish + 1) * D].rearrange("(c p) -> p c", p=P),
                      reads=[("modd", l)], writes=[kb])
                S.dma("sp", A[:], self.modd[l, s, isc * D:(isc + 1) * D].rearrange("(c p) -> p c", p=P),
                      reads=[("modd", l)], writes=[ka])
                S.dma("sp", tmp[:], self.norm_g[l, ig, :].rearrange("(c p) -> p c", p=P), writes=[f"{tag}_t"])
                S.op("dve", lambda e: e.scalar_tensor_tensor(out=A[:], in0=A[:], scalar=1.0, in1=tmp[:],
                                                             op0=ALU.add, op1=ALU.mult),
                     reads=[ka, f"{tag}_t"], writes=[ka])
                out[s] = ((A, ka), (B, kb))
        return out

    def make_norm_bufs(self, tag, nb=2):
        S = self.S
        B = {"i": 0, "nb": nb, "tag": tag}
        B["sq"] = [S.sb(f"{tag}_sq{i}", [P, D], BF16) for i in range(1)] * nb
        B["st"] = [S.sb(f"{tag}_st{i}", [P, 4], F32) for i in range(nb)]
        B["xn"] = [S.sb(f"{tag}_xn{i}", [P, D], BF16) for i in range(nb)]
        B["tp"] = [S.ps(f"{tag}_tp{i}", [P, 8, P], BF16) for i in range(nb)]
        return B

    def norm_to_hT(self, B, x_sb, xkey, A, B_, out_ap, out_key, out2_ap=None, out2_key=None):
        S = self.S
        i = B["i"] % B["nb"]
        B["i"] += 1
        tag = B["tag"]
        sq, st, xn, tp = B["sq"][i], B["st"][i], B["xn"][i], B["tp"][i]
        ksq, kst, kxn, ktp, ktm = [(tag, n, i) for n in ("sq", "st", "xn", "tp", "tm")]
        ksq = (tag, "sq", 0)
        S.op("pool", lambda e: e.memset(st[:], 0.0), writes=[kst])
        S.op("act", lambda e: e.activation(out=sq[:], in_=x_sb, func=AF.Square, accum_out=st[:, 0:1]),
             reads=[xkey], writes=[ksq, kst])
        S.op("act", lambda e: e.activation(out=st[:, 1:2], in_=st[:, 0:1], func=AF.Sqrt, scale=1.0 / D,
                                           bias=self.eps_t[:, 0:1]), reads=[kst, "eps_t"], writes=[kst])
        S.op("dve", lambda e: e.reciprocal(out=st[:, 2:3], in_=st[:, 1:2]), reads=[kst], writes=[kst])
        S.op("dve", lambda e: e.tensor_scalar(out=xn[:], in0=x_sb, scalar1=st[:, 2:3], scalar2=None, op0=ALU.mult),
             reads=[xkey, kst], writes=[kxn])
        for c in range(8):
            S.op("pe", lambda e: e.transpose(out=tp[:, c, :], in_=xn[:, c * P:(c + 1) * P], identity=self.idb[:]),
                 reads=[kxn, "idb"], writes=[ktp], accum=(c > 0))
        for c in range(8):
            S.op("dve", lambda e: e.tensor_scalar(out=out_ap[:, c, :], in0=tp[:, c, :], scalar1=A[0][:, c:c + 1],
                                                  scalar2=B_[0][:, c:c + 1], op0=ALU.mult, op1=ALU.add),
                 reads=[ktp, A[1], B_[1]], writes=[out_key])

    def x_src(self, layer, T):
        if layer == 0:
            if T < 2:
                return self.ctx_in[T * P:(T + 1) * P, :], None
            return self.x_in[(T - 2) * P:(T - 1) * P, :], None
        return self.x_d[T * P:(T + 1) * P, :], ("x_d", T)

    def phase_L0_proj(self, V):
        S = self.S
        with S.scope():
            w = S.sb("w_in", [P, 8, 2 * D], BF16)
            for c in range(8):
                S.dma("pool", w[:, c, :], self.ev_w_in[c * P:(c + 1) * P, :], writes=[("w_in", c)])
            wk = [("w_in", c) for c in range(8)]
            NB = self.make_norm_bufs("n0")
            xt = [S.sb(f"xt{i}", [P, D], F32) for i in range(2)]
            hT = [S.sb(f"hT{i}", [P, 8, 512], BF16) for i in range(2)]
            ptok = [S.ps(f"ptok{i}", [P, 512], F32) for i in range(2)]
            pft = [S.ps(f"pft{i}", [P, 512], F32) for i in range(2)]
            otok = [S.sb(f"otok{i}", [P, 512], BF16) for i in range(2)]
            oft = [S.sb(f"oft{i}", [P, 512], BF16) for i in range(2)]
            supers = [(0, 2)] + [(2 + 4 * i, 4) for i in range(8)]
            cnt = 0
            ctok = 0
            cft = 0
            for si, (T0, nt) in enumerate(supers):
                hb = hT[si % 2]
                hk = ("hT", si % 2)
                s = 1 if T0 < 2 else 0
                for t in range(nt):
                    T = T0 + t
                    xb = xt[cnt % 2]
                    xk = ("xt", cnt % 2)
                    cnt += 1
                    src, sk = self.x_src(0, T)
                    S.dma("act", xb[:], src, reads=[sk] if sk else [], writes=[xk])
                    self.norm_to_hT(NB, xb[:], xk, V[("A", 0, s)], V[("B", 0, s)],
                                    hb[:, :, t * P:(t + 1) * P], (hk, t))
                n = nt * P
                hks = [(hk, t) for t in range(nt)]
                for t in range(nt):
                    T = T0 + t
                    for (c0, dst, dk) in ((0, self.upool_d, "upool"), (1536, self.v_d, "v")):
                        ps = ptok[ctok % 2]; pk = ("ptok", ctok % 2)
                        ob = otok[ctok % 2]; ok = ("otok", ctok % 2)
                        ctok += 1
                        for c in range(8):
                            mm(S, ps[:], hb[:, c, t * P:(t + 1) * P], w[:, c, c0:c0 + 512], c == 0, c == 7,
                               reads=[(hk, t), wk[c]], writes=[pk])
                        S.op("act", lambda e: e.activation(out=ob[:], in_=ps[:], func=AF.Copy), reads=[pk], writes=[ok])
                        S.dma("sp", dst[T * P:(T + 1) * P, :], ob[:], reads=[ok], writes=[(dk, T)])
                for jb in range(8):
                    c0 = 512 + jb * P
                    ps = pft[cft % 2]; pk = ("pft", cft % 2)
                    ob = oft[cft % 2]; ok = ("oft", cft % 2)
                    cft += 1
                    for c in range(8):
                        mm(S, ps[:, :n], w[:, c, c0:c0 + P], hb[:, c, :n], c == 0, c == 7,
                           reads=hks + [wk[c]], writes=[pk])
                    sc = 0.125 if jb < 4 else 1.0
                    S.op("act", lambda e: e.activation(out=ob[:, :n], in_=ps[:, :n], func=AF.Copy, scale=sc),
                         reads=[pk], writes=[ok])
                    dst = self.qT_d if jb < 4 else self.kT_d
                    dk = "qT" if jb < 4 else "kT"
                    S.dma("sp", dst[jb % 4, :, T0 * P:T0 * P + n], ob[:, :n], reads=[ok],
                          writes=[(dk, jb % 4, T0 + t) for t in range(nt)])

    def phase_L0_pool(self):
        S = self.S
        with S.scope():
            up = S.sb("up_all", [P, NT, 512], BF16)
            for q in range(0, NT, 2):
                S.dma("sp", up[:, q:q + 2, :], self.upool_d[q * P:(q + 2) * P, :].rearrange("(n p) f -> p n f", p=P),
                      reads=[("upool", q), ("upool", q + 1)], writes=[("up", q), ("up", q + 1)])
            bd = S.sb("bands", [P, 20, P], BF16)
            S.dma("pool", bd[:], self.bands, writes=["bands"])
            pw = S.sb("pool_w", [P, 4, P], BF16)
            S.dma("pool", pw[:], self.ev_pool_w.rearrange("g c o -> c g o"), writes=["pool_w"])
            psc = S.sb("pool_sc", [P, 4], F32)
            with self.nc.allow_non_contiguous_dma(reason="tiny"):
                S.dma("sp", psc[:], self.ev_pool_scale.rearrange("(g p) -> p g", p=P), writes=["pool_sc"])
            pb = [S.ps(f"pb{i}", [P, 4, P], F32) for i in range(2)]
            pc = [S.ps(f"pc{i}", [P, 4, P], F32) for i in range(2)]
            pm = [S.sb(f"pmx{i}", [P, 4, P], BF16) for i in range(2)]
            zp = [S.sb(f"zp{i}", [P, 4, P], BF16) for i in range(2)]
            it = 0
            for (T0, n) in ((0, 2), (2, 32)):
                for i in range(n):
                    T = T0 + i
                    b = it % 2
                    it += 1
                    for g in range(4):
                        srcs = []
                        if i > 0:
                            srcs.append((T - 1, 0))
                        cv = 3 if i == 0 else (4 if i == n - 1 else 2)
                        srcs.append((T, cv))
                        if i < n - 1:
                            srcs.append((T + 1, 1))
                        for si, (Ts, v) in enumerate(srcs):
                            mm(S, pb[b][:, g, :], up[:, Ts, g * P:(g + 1) * P], bd[:, g * 5 + v, :],
                               si == 0, si == len(srcs) - 1, reads=[("up", Ts), "bands"], writes=[("pb", b)])
                    S.op("dve", lambda e: e.tensor_copy(out=pm[b][:], in_=pb[b][:]), reads=[("pb", b)], writes=[("pmx", b)])
                    for g in range(4):
                        mm(S, pc[b][:, g, :], pw[:, g, :], pm[b][:, g, :], True, True,
                           reads=["pool_w", ("pmx", b)], writes=[("pc", b)])
                    S.op("dve", lambda e: e.tensor_tensor(out=zp[b][:], in0=pc[b][:],
                                                          in1=psc[:, :, None].broadcast_to([P, 4, P]), op=ALU.mult),
                         reads=[("pc", b), "pool_sc"], writes=[("zp", b)])
                    S.dma("sp", self.zT_d[0:4, :, T * P:(T + 1) * P].rearrange("c p t -> p c t"), zp[b][:],
                          reads=[("zp", b)], writes=[("zT", c, T) for c in range(4)])

    def phase_L0_attn(self):
        S = self.S
        with S.scope():
            kT = S.sb("kT_all", [P, 4, NTOK], BF16)
            qT = S.sb("qT_all", [P, 4, NTOK], BF16)
            va = S.sb("v_all", [P, NT, 512], BF16)
            for j in range(4):
                S.dma("sp", kT[:, j, :], self.kT_d[j], reads=[("kT", j, T) for T in range(NT)], writes=[("kTa", j)])
                S.dma("sp", qT[:, j, :], self.qT_d[j], reads=[("qT", j, T) for T in range(NT)], writes=[("qTa", j)])
            for q in range(0, NT, 2):
                S.dma("sp", va[:, q:q + 2, :], self.v_d[q * P:(q + 2) * P, :].rearrange("(n p) f -> p n f", p=P),
                      reads=[("v", q), ("v", q + 1)], writes=[("va", q), ("va", q + 1)])
            E = S.sb("Etab", [P, 96, P], BF16)
            with S.scope():
                rt = S.sb("rt", [P, 96, P], F32)
                mk = S.sb("mk", [P, 12, P], F32)
                S.dma("sp", rt[:], self.rpb_tab, writes=["rt"])
                S.dma("sp", mk[:], self.msk_tab, writes=["mk"])
                S.op("act", lambda e: e.activation(out=rt[:], in_=rt[:], func=AF.Exp), reads=["rt"], writes=["rt"])
                for h in range(8):
                    S.op("dve", lambda e: e.tensor_tensor(out=E[:, h * 12:(h + 1) * 12, :], in0=rt[:, h * 12:(h + 1) * 12, :],
                                                          in1=mk[:], op=ALU.mult), reads=["rt", "mk"], writes=["Etab"])
            pss = [[S.ps(f"pss{i}_{k}", [P, 512], F32) for k in range(2)] for i in range(3)]
            pso = [S.ps(f"pso{i}", [P, 2, P], F32) for i in range(2)]
            pex = [S.sb(f"pex{i}", [P, 7, P], BF16) for i in range(3)]
            pT = [S.sb(f"pT{i}", [P, 5, P], BF16) for i in range(3)]
            rc = [S.sb(f"rc{i}", [P, P], F32) for i in range(2)]
            zo = [S.sb(f"zo{i}", [P, P], BF16) for i in range(2)]
            it = 0
            izo = 0
            for T in range(NT):
                if T < 2:
                    chunks = [(0, None), (1, None)]
                else:
                    i = T - 2
                    if 2 <= i <= 29:
                        lat = [(T + d, v) for v, d in enumerate((-2, -1, 0, 1, 2))]
                    elif i == 0:
                        lat = [(T + d, 8 + d) for d in (0, 1, 2, 3)]
                    elif i == 1:
                        lat = [(T + d, 8 + d) for d in (-1, 0, 1, 2)]
                    elif i == 30:
                        lat = [(T + d, 8 + d) for d in (-2, -1, 0, 1)]
                    else:
                        lat = [(T + d, 8 + d) for d in (-3, -2, -1, 0)]
                    chunks = [(0, None), (1, None)] + lat
                nk = len(chunks)
                nlat = nk - 2
                for j in range(4):
                    zb = zo[izo % 2]; zk = ("zo", izo % 2)
                    izo += 1
                    for hh in range(2):
                        h = 2 * j + hh
                        pb_ = hh * 64
                        b = it % 3
                        bo = it % 2
                        it += 1
                        for ci, (Tk, v) in enumerate(chunks):
                            bank = pss[b][ci // 4]
                            mm(S, bank[:, (ci % 4) * P:(ci % 4 + 1) * P],
                               kT[pb_:pb_ + 64, j, Tk * P:(Tk + 1) * P], qT[pb_:pb_ + 64, j, T * P:(T + 1) * P],
                               True, True, reads=[("kTa", j), ("qTa", j)], writes=[("pss", b, ci // 4)])
                        n0 = min(nk, 4)
                        S.op("act", lambda e: e.activation(out=pex[b][:, 0:n0, :], in_=pss[b][0][:, 0:n0 * P].rearrange("p (c q) -> p c q", q=P), func=AF.Exp),
                             reads=[("pss", b, 0)], writes=[("pex", b)])
                        if nk > 4:
                            S.op("act", lambda e: e.activation(out=pex[b][:, 4:nk, :], in_=pss[b][1][:, 0:(nk - 4) * P].rearrange("p (c q) -> p c q", q=P), func=AF.Exp),
                                 reads=[("pss", b, 1)], writes=[("pex", b)])
                        if nlat > 0:
                            v0 = chunks[2][1]
                            S.op("dve", lambda e: e.tensor_tensor(out=pT[b][:, 0:nlat, :], in0=pex[b][:, 2:nk, :],
                                                                  in1=E[:, h * 12 + v0:h * 12 + v0 + nlat, :], op=ALU.mult),
                                 reads=[("pex", b), "Etab"], writes=[("pT", b)])
                        for ci, (Tk, v) in enumerate(chunks):
                            rhs = pex[b][:, ci, :] if v is None else pT[b][:, ci - 2, :]
                            rk = [("pex", b)] if v is None else [("pT", b)]
                            mm(S, pso[bo][:, 0, :], va[:, Tk, j * P:(j + 1) * P], rhs, ci == 0, ci == nk - 1,
                               reads=[("va", Tk)] + rk, writes=[("pso", bo)])
                        for ci, (Tk, v) in enumerate(chunks):
                            rhs = pex[b][:, ci, :] if v is None else pT[b][:, ci - 2, :]
                            rk = [("pex", b)] if v is None else [("pT", b)]
                            mm(S, pso[bo][:, 1, :], self.ones_bf[:], rhs, ci == 0, ci == nk - 1,
                               reads=["ones_bf"] + rk, writes=[("pso", bo)])
                        S.op("dve", lambda e: e.reciprocal(out=rc[bo][pb_:pb_ + 64, :], in_=pso[bo][pb_:pb_ + 64, 1, :]),
                             reads=[("pso", bo)], writes=[("rc", bo)])
                        S.op("dve", lambda e: e.tensor_tensor(out=zb[pb_:pb_ + 64, :], in0=pso[bo][pb_:pb_ + 64, 0, :],
                                                              in1=rc[bo][pb_:pb_ + 64, :], op=ALU.mult),
                             reads=[("pso", bo), ("rc", bo)], writes=[zk])
                    S.dma("sp", self.zT_d[4 + j, :, T * P:(T + 1) * P], zb[:], reads=[zk], writes=[("zT", 4 + j, T)])

    def phase_out_mlp(self, layer, V, w_out_ap, y_tile_fn, dst_fn, next_ab=None):
        S = self.S
        with S.scope():
            wo = S.sb("wo", [P, 8, D], BF16)
            for c in range(8):
                S.dma("pool", wo[:, c, :], w_out_ap[c * P:(c + 1) * P, :], writes=[("wo", c)])
            w1 = S.sb("w1", [P, 8, 4 * D], BF16)
            w2 = S.sb("w2", [P, 32, D], BF16)
            for c in range(8):
                S.dma("pool", w1[:, c, :], self.mlp_w1[layer, c * P:(c + 1) * P, :], writes=[("w1", c)])
            for f in range(0, 32, 4):
                S.dma("pool", w2[:, f:f + 4, :], self.mlp_w2[layer, f * P:(f + 4) * P, :].rearrange("(n p) d -> p n d", p=P),
                      writes=[("w2", f + q) for q in range(4)])
            NB = self.make_norm_bufs("nm", nb=1)
            hob = S.sb("hob", [P, 8, P], F32) if next_ab is not None else None
            zt = [S.sb(f"zt{i}", [P, 8, P], BF16) for i in range(2)]
            xt = [S.sb(f"xo{i}", [P, D], F32) for i in range(2)]
            x1 = [S.sb(f"x1_{i}", [P, D], F32) for i in range(2)]
            tmp = [S.sb(f"tg{i}", [P, D], F32) for i in range(2)]
            sq = NB["sq"][0]
            stt = [S.sb(f"ost{i}", [P, 4], F32) for i in range(4)]
            hT = [S.sb(f"hm{i}", [P, 8, 256], BF16) for i in range(1)] * 2
            py = [[S.ps(f"py{t}_{hf}", [P, 512], F32) for hf in range(2)] for t in range(2)]
            pa = [S.ps(f"pa{i}", [P, 256], F32) for i in range(2)]
            r32 = [S.sb(f"r32_{i}", [P, 256], F32) for i in range(2)]
            aT = [S.sb(f"aT{i}", [P, 256], BF16) for i in range(2)]
            ist = 0

            def norm_gate_res(t, G, xin, xin_key, xout, xout_key):
                nonlocal ist
                st = stt[ist % 4]; sk = ("ost", ist % 4)
                ist += 1
                S.op("pool", lambda e: e.memset(st[:], 0.0), writes=[sk])
                for hf in range(2):
                    S.op("act", lambda e: e.activation(out=sq[:, hf * 512:(hf + 1) * 512], in_=py[t][hf][:], func=AF.Square,
                                                       accum_out=st[:, hf:hf + 1]), reads=[("py", t, hf)], writes=[("nm", "sq", 0), sk])
                S.op("dve", lambda e: e.tensor_tensor(out=st[:, 2:3], in0=st[:, 0:1], in1=st[:, 1:2], op=ALU.add),
                     reads=[sk], writes=[sk])
                S.op("act", lambda e: e.activation(out=st[:, 2:3], in_=st[:, 2:3], func=AF.Sqrt, scale=1.0 / D,
                                                   bias=self.eps_t[:, 0:1]), reads=[sk, "eps_t"], writes=[sk])
                S.op("dve", lambda e: e.reciprocal(out=st[:, 3:4], in_=st[:, 2:3]), reads=[sk], writes=[sk])
                tb = tmp[t]; tk = ("tg", t)
                for hf in range(2):
                    S.op("dve", lambda e: e.scalar_tensor_tensor(out=tb[:, hf * 512:(hf + 1) * 512], in0=py[t][hf][:],
                                                                 scalar=st[:, 3:4], in1=G[0][:, hf * 512:(hf + 1) * 512],
                                                                 op0=ALU.mult, op1=ALU.mult),
                         reads=[("py", t, hf), sk, G[1]], writes=[tk])
                S.op("dve", lambda e: e.tensor_tensor(out=xout, in0=tb[:], in1=xin, op=ALU.add),
                     reads=[tk, xin_key], writes=[xout_key])

            ia = 0
            loaded = {}

            def load_inputs(sidx):
                loaded[sidx] = True
                for t in range(2):
                    T = 2 * sidx + t
                    src, skey = self.x_src(layer, T)
                    S.dma("sp", xt[t][:], src, reads=[skey] if skey else [], writes=[("xo", t)])
                    S.dma("sp", zt[t][:], self.zT_d[:, :, T * P:(T + 1) * P].rearrange("c p t -> p c t"),
                          reads=[("zT", c, T) for c in range(8)], writes=[("zt", t)])

            for sidx in range(NT // 2):
                T0 = 2 * sidx
                s = 1 if T0 < 2 else 0
                if layer == 1 and s == 1:
                    continue
                hb = hT[0]; hk = ("hm", 0)
                if not loaded.get(sidx):
                    load_inputs(sidx)
                for t in range(2):
                    T = T0 + t
                    y_tile_fn(T, t, zt[t], ("zt", t), wo, py[t])
                    norm_gate_res(t, V[("G", 0, s)], xt[t][:], ("xo", t), x1[t][:], ("x1", t))
                    self.norm_to_hT(NB, x1[t][:], ("x1", t), V[("A", 1, s)], V[("B", 1, s)],
                                    hb[:, :, t * P:(t + 1) * P], (hk, t))
                nxt = sidx + 1
                if nxt < NT // 2 and not (layer == 1 and nxt == 0):
                    load_inputs(nxt)
                def mm1(f):
                    a = (ia + f) % 2
                    for c in range(8):
                        mm(S, pa[a][:], w1[:, c, f * P:(f + 1) * P], hb[:, c, :], c == 0, c == 7,
                           reads=[("w1", c), (hk, 0), (hk, 1)], writes=[("pa", a)])
                    S.op("act", lambda e: e.activation(out=r32[a][:], in_=pa[a][:], func=AF.Relu),
                         reads=[("pa", a)], writes=[("r32", a)])
                    S.op("dve", lambda e: e.tensor_tensor(out=aT[a][:], in0=r32[a][:], in1=r32[a][:], op=ALU.mult),
                         reads=[("r32", a)], writes=[("aT", a)])

                def mm2(f):
                    a = (ia + f) % 2
                    for t in range(2):
                        for hf in range(2):
                            mm(S, py[t][hf][:], aT[a][:, t * P:(t + 1) * P], w2[:, f, hf * 512:(hf + 1) * 512],
                               f == 0, f == 31, reads=[("aT", a), ("w2", f)], writes=[("py", t, hf)])
                mm1(0)
                for f in range(32):
                    if f + 1 < 32:
                        mm1(f + 1)
                    mm2(f)
                for t in range(2):
                    T = T0 + t
                    norm_gate_res(t, V[("G", 1, s)], x1[t][:], ("x1", t), tmp[t][:], ("tg", t))
                    dst, dkey = dst_fn(T)
                    S.dma("sp", dst, tmp[t][:], reads=[("tg", t)], writes=[dkey])
                    if next_ab is not None:
                        ab = next_ab[s]
                        self.norm_to_hT(NB, tmp[t][:], ("tg", t), ab[0], ab[1], hob[:], "hob")
                        S.dma("sp", self.hT_d[:, :, T * P:(T + 1) * P].rearrange("c p t -> p c t"), hob[:],
                              reads=["hob"], writes=[("hT_d", T)])

    def y_tile_L0(self, T, t, zt, zk, wo, py):
        S = self.S
        for hf in range(2):
            for c in range(8):
                mm(S, py[hf][:], zt[:, c, :], wo[:, c, hf * 512:(hf + 1) * 512], c == 0, c == 7,
                   reads=[zk, ("wo", c)], writes=[("py", t, hf)])


def build_program(stop_after=None, debug=()):
    nc = bass.Bass("TRN2", target_bir_lowering=False)
    Pg = Prog(nc, debug)
    S = Pg.S
    Pg.consts()
    Pg.phase_mod()
    final_keys = []
    with S.scope():
        V0 = Pg.load_layer_vecs(0)
        Pg.phase_L0_proj(V0)
        Pg.phase_L0_pool()
        Pg.phase_L0_attn()

        def dst0(T):
            if stop_after == "L0":
                if T < 2:
                    return Pg.x_d[T * P:(T + 1) * P, :], ("x_d", T)
                return Pg.out[(T - 2) * P:(T - 1) * P, :], ("out", T)
            return Pg.x_d[T * P:(T + 1) * P, :], ("x_d", T)
        Pg.phase_out_mlp(0, V0, Pg.ev_w_out, Pg.y_tile_L0, dst0)
    S.barrier()
    S.finish([])
    S.close()
    return nc, Pg


def host_inputs(inputs):
    f = lambda a: np.ascontiguousarray(np.asarray(a, dtype=np.float32))
    dr_idx, dc_full, mask = _attn_tables()
    rpb = f(inputs["ev_rpb"])[0]
    tab = rpb[:, dr_idx, dc_full[None, :, :]]
    tab = np.ascontiguousarray(tab.transpose(2, 0, 1, 3).reshape(128, 96, 128))
    msk = np.ascontiguousarray(mask.transpose(1, 0, 2))
    bands = np.ascontiguousarray(_pool_bands().transpose(2, 0, 1, 3).reshape(128, 20, 128))
    shared = {
        "ada_w": f(inputs["ada_w"]), "ada_b": f(inputs["ada_b"]), "norm_g": f(inputs["norm_g"]),
        "mlp_w1": f(inputs["mlp_w1"]), "mlp_w2": f(inputs["mlp_w2"]),
        "ev_w_in": f(inputs["ev_w_in"])[0], "ev_w_out": f(inputs["ev_w_out"])[0],
        "ev_pool_w": f(inputs["ev_pool_w"])[0], "ev_pool_scale": f(inputs["ev_pool_scale"])[0],
        "rpb_tab": tab, "msk_tab": msk, "bands": bands, "ident": np.eye(128, dtype=np.float32),
    }
    x = f(inputs["x"]); c = f(inputs["c"]); ctx = f(inputs["ctx"]); cc = f(inputs["c_ctx"])
    maps = []
    for b in range(x.shape[0]):
        cv = np.stack([c[b].reshape(8, 128).T, cc.reshape(8, 128).T], axis=-1)
        m = dict(shared)
        m.update({"x": x[b], "ctx": ctx[b], "cvec": np.ascontiguousarray(cv)})
        maps.append(m)
    return maps


_CACHE = {}


def kernel(**inputs):
    maps = host_inputs(inputs)
    if "nc" not in _CACHE:
        _CACHE["nc"] = build_program()
    nc, Pg = _CACHE["nc"]
    res = run_bass_kernel_spmd(nc, maps, core_ids=list(range(8)))
    return np.stack([np.asarray(r["out"]) for r in res.results], axis=0)

LWC = -0.6065306597126334
GN_EPS = 64e-5


def _scan_consts():
    s = np.arange(128)[:, None]
    t = np.arange(128)[None, :]
    tri = np.stack([(s <= t), (s >= t)]).astype(np.float32)
    strict = np.stack([(s < t), (s > t)]).astype(np.float32)
    mT = strict.transpose(0, 2, 1)
    m4 = np.concatenate([tri, strict, tri, mT], axis=2)
    lm = []
    for l in range(7):
        b = 1 << l
        lm.append(((s // (2 * b)) == (t // (2 * b))) & (((s // b) % 2) == 0) & (((t // b) % 2) == 1))
    lm = np.stack(lm).astype(np.float32)
    lmN = np.stack([lm, lm.transpose(0, 2, 1)]) + np.eye(128, dtype=np.float32)[None, None]
    return tri, m4, np.ascontiguousarray(lmN)


def _tt(S, eng, out, a, b, op, reads, writes):
    return S.op(eng, lambda e: e.tensor_tensor(out=out, in0=a, in1=b, op=op), reads=reads, writes=writes)


def _stt(S, eng, out, a, sc, b, op0, op1, reads, writes):
    return S.op("dve", lambda e: e.scalar_tensor_tensor(out=out, in0=a, scalar=sc, in1=b, op0=op0, op1=op1),
                reads=reads, writes=writes)


def _act(S, out, in_, func, reads, writes, **kw):
    return S.op("act", lambda e: e.activation(out=out, in_=in_, func=func, **kw), reads=reads, writes=writes)


def _h3(ap):
    return ap.rearrange("p (h k) -> p h k", k=64)


class Prog1(Prog):
    def __init__(self, nc, debug=()):
        super().__init__(nc, debug)
        dt = nc.dram_tensor
        I = lambda name, shape: dt(name, list(shape), F32, kind="ExternalInput").ap()
        self.rw_mu = I("rw_mu", [6, D])
        self.rw_wr = I("rw_wr", [D, D]); self.rw_wk = I("rw_wk", [D, D])
        self.rw_wv = I("rw_wv", [D, D]); self.rw_wo = I("rw_wo", [D, D])
        self.rw_w0 = I("rw_w0", [2, D]); self.rw_a0 = I("rw_a0", [2, D])
        self.w1cat = I("w1cat", [D, P]); self.a1cat = I("a1cat", [D, P]); self.rw_g1 = I("rw_g1", [D, P])
        self.w2cat = I("w2cat", [P, D]); self.a2cat = I("a2cat", [P, D]); self.rw_g2 = I("rw_g2", [P, D])
        self.rw_kk = I("rw_kk", [1, D]); self.rw_ka = I("rw_ka", [1, D]); self.rw_rk = I("rw_rk", [1, D])
        self.rw_lng = I("rw_lng", [1, D]); self.rw_lnb = I("rw_lnb", [1, D])
        self.tri_c = I("tri_c", [2, P, P]); self.m4_c = I("m4_c", [2, P, 512]); self.lmT_c = I("lmT_c", [2, 7, P, P])
        X = lambda name, shape, d=F32: (dt(name, list(shape), d, kind="ExternalOutput").ap() if name in debug
                                        else dt(name, list(shape), d).ap())
        self.hT_d = X("hT_d", [8, P, NTOK])
        self.featT_d = X("featT_d", [2, NT, P, 8 * 4 * P], BF16)
        self.vtok_d = X("vtok_d", [NTOK, D], BF16)
        self.bk_d = X("bk_d", [2, NT, P, 2 * D], BF16)
        self.gC_d = X("gC_d", [2, NT, P, 8])
        self.g_d = X("g_d", [NTOK, D])
        self.bonus_d = X("bonus_d", [NTOK, D])
        self.y_d = X("y_d", [2, NTOK, D])

    def phase_R0(self, V):
        S = self.S
        with S.scope():
            NB = self.make_norm_bufs("r0")
            xt = [S.sb(f"r0x{i}", [P, D], F32) for i in range(2)]
            ho = [S.sb(f"r0h{i}", [P, 8, P], F32) for i in range(2)]
            for T in range(NT):
                s = 1 if T < 2 else 0
                b = T % 2
                src, sk = self.x_src(1, T)
                S.dma("sp", xt[b][:], src, reads=[sk], writes=[("r0x", b)])
                self.norm_to_hT(NB, xt[b][:], ("r0x", b), V[("A", 0, s)], V[("B", 0, s)], ho[b][:], ("r0h", b))
                S.dma("sp", self.hT_d[:, :, T * P:(T + 1) * P].rearrange("c p t -> p c t"), ho[b][:],
                      reads=[("r0h", b)], writes=[("hT_d", T)])

    def phase_R1(self):
        S = self.S
        with S.scope():
            W = {}
            for nm, src in (("wr", self.rw_wr), ("wk", self.rw_wk), ("wv", self.rw_wv)):
                W[nm] = S.sb(nm, [P, 8, D], BF16)
                for c in range(0, 8, 4):
                    S.dma("pool", W[nm][:, c:c + 4, :], src[c * P:(c + 4) * P, :].rearrange("(c p) n -> p c n", p=P), writes=[nm])
            for nm, src in (("w1c", self.w1cat), ("a1c", self.a1cat), ("g1", self.rw_g1)):
                W[nm] = S.sb(nm, [P, 8, P], BF16)
                S.dma("pool", W[nm][:], src.rearrange("(c p) n -> p c n", p=P), writes=[nm])
            for nm, src in (("w2c", self.w2cat), ("a2c", self.a2cat), ("g2", self.rw_g2)):
                W[nm] = S.sb(nm, [P, D], BF16)
                S.dma("pool", W[nm][:], src, writes=[nm])
            R = {}
            for nm, src in (("kk_r", self.rw_kk), ("ka_r", self.rw_ka), ("rk_r", self.rw_rk),
                            ("w0_0", self.rw_w0[0:1, :]), ("w0_1", self.rw_w0[1:2, :]),
                            ("a0_0", self.rw_a0[0:1, :]), ("a0_1", self.rw_a0[1:2, :])):
                R[nm] = S.sb(nm, [P, D], F32)
                S.dma("sp", R[nm][:], src.broadcast_to([P, D]), writes=[nm])
            mu = S.sb("mu", [P, 6, 8], F32)
            with self.nc.allow_non_contiguous_dma(reason="tiny"):
                S.dma("sp", mu[:], self.rw_mu.rearrange("j (c p) -> p j c", p=P), writes=["mu"])
            tri = S.sb("tri", [P, 2, P], F32)
            S.dma("sp", tri[:], self.tri_c.rearrange("d s t -> s d t"), writes=["tri"])
            onef = S.sb("onef", [P, P], F32)
            S.op("dve", lambda e: e.memset(onef[:], 1.0), writes=["onef"])
            hbuf = S.sb("hbuf", [P, 8, P + 2], F32)
            xx = S.sb("xx", [P, 8, P], F32)
            mix = S.sb("mix", [P, 6, 8, P], BF16)
            hid = S.sb("hid", [P, 3, P], BF16)
            F = {n: S.sb(n, [P, D], F32) for n in ("r_sb", "k_sb", "v_sb", "kkn", "tA", "tB", "lw", "tC0", "tC1", "tD", "kd0", "kd1", "tE0", "tE1", "tF0", "tF1", "tG", "tH")}
            ob = [S.sb(f"ob{i}", [P, D], BF16) for i in range(4)]
            vb = S.sb("vb", [P, D], BF16)
            ft = S.sb("ft", [P, 8, 4, P], BF16)
            bkt = S.sb("bkt", [P, 2, D], BF16)
            st16 = S.sb("st16", [P, 64], F32)
            gcs = S.sb("gcs", [P, 8], F32)
            pA = [[S.ps(f"pA{i}_{h}", [P, 512], F32) for h in range(2)] for i in range(2)]
            pCl = [S.ps(f"pCl{h}", [P, 512], F32) for h in range(2)]
            pF = S.ps("pF", [P, 512], F32)
            pT = S.ps("pT", [P, 8, P], BF16)
            ipa = 0

            def proj(lhs_fn, rhs, rkey, K0=0, K=P, nchunks=8, lkeys=()):
                nonlocal ipa
                i = ipa % 2
                ipa += 1
                for hf in range(2):
                    for c in range(nchunks):
                        mm(S, pA[i][hf][:], lhs_fn(c), rhs(c, hf), c == 0, c == nchunks - 1,
                           reads=list(lkeys) + [rkey], writes=[("pA", i, hf)])
                return pA[i], [("pA", i, 0), ("pA", i, 1)]

            def evac2(fn_half):
                for hf in range(2):
                    fn_half(hf, slice(hf * 512, (hf + 1) * 512))

            def early(T):
                seq_lo, seq_hi = (0, NCTX) if T < 2 else (NCTX, NTOK)
                t0 = T * P
                lo = max(t0 - 1, seq_lo); hi = min(t0 + P + 1, seq_hi)
                if lo > t0 - 1:
                    S.op("pool", lambda e: e.memset(hbuf[:, :, 0:1], 0.0), writes=["hbuf"])
                if hi < t0 + P + 1:
                    S.op("pool", lambda e: e.memset(hbuf[:, :, P + 1:P + 2], 0.0), writes=["hbuf"])
                S.dma("pool", hbuf[:, :, lo - (t0 - 1):hi - (t0 - 1)], self.hT_d[:, :, lo:hi].rearrange("c p t -> p c t"),
                      reads=[("hT_d", q) for q in range(max(T - 1, 0), min(T + 2, NT))], writes=["hbuf"])
                _tt(S, "dve", xx[:], hbuf[:, :, 0:P], hbuf[:, :, 2:P + 2], ALU.add, ["hbuf"], ["xx"])
                _stt(S, "dve", xx[:], xx[:], 0.5, hbuf[:, :, 1:P + 1], ALU.mult, ALU.subtract, ["xx", "hbuf"], ["xx"])
                for j in range(6):
                    mxt = F["tA"][:].rearrange("p (c t) -> p c t", t=P)
                    _tt(S, "dve", mxt, xx[:], mu[:, j, :][:, :, None].broadcast_to([P, 8, P]), ALU.mult, ["xx", "mu"], ["tA"])
                    _tt(S, "dve", mix[:, j, :, :], mxt, hbuf[:, :, 1:P + 1], ALU.add, ["tA", "hbuf"], [("mix", j)])

            early(0)
            for T in range(NT):
                t0 = T * P
                for hi_, (wn, mj, fn) in enumerate((("w1c", 1, AF.Tanh), ("a1c", 4, AF.Copy), ("g1", 5, AF.Sigmoid))):
                    for c in range(8):
                        mm(S, pF[:, 0:P], W[wn][:, c, :], mix[:, mj, c, :], c == 0, c == 7,
                           reads=[wn, ("mix", mj)], writes=["pF"])
                    _act(S, hid[:, hi_, :], pF[:, 0:P], fn, ["pF"], [("hid", hi_)])
                for nm, mj, wn in (("r_sb", 0, "wr"), ("k_sb", 2, "wk"), ("v_sb", 3, "wv")):
                    ps, pk = proj(lambda c: mix[:, mj, c, :], lambda c, hf: W[wn][:, c, hf * 512:(hf + 1) * 512], wn,
                                  lkeys=[("mix", mj)])
                    evac2(lambda hf, sl: _act(S, F[nm][:, sl], ps[hf][:], AF.Copy, [pk[hf]], [nm]))
                S.op("pool", lambda e: e.tensor_copy(out=vb[:], in_=F["v_sb"][:]), reads=["v_sb"], writes=["vb"])
                S.dma("sp", self.vtok_d[t0:t0 + P, :], vb[:], reads=["vb"], writes=[("vtok", T)])
                ps, pk = proj(lambda c: hid[:, 2, :], lambda c, hf: W["g2"][:, hf * 512:(hf + 1) * 512], "g2", nchunks=1,
                              lkeys=[("hid", 2)])
                evac2(lambda hf, sl: _act(S, F["tA"][:, sl], ps[hf][:], AF.Copy, [pk[hf]], ["tA"]))
                S.dma("sp", self.g_d[t0:t0 + P, :], F["tA"][:], reads=["tA"], writes=[("g_d", T)])
                _tt(S, "dve", F["tA"][:], F["k_sb"][:], R["kk_r"][:], ALU.mult, ["k_sb", "kk_r"], ["tA"])
                _tt(S, "dve", F["tB"][:], F["tA"][:], F["tA"][:], ALU.mult, ["tA"], ["tB"])
                S.op("dve", lambda e: e.tensor_reduce(out=st16[:, 0:16], in_=_h3(F["tB"][:]), axis=AX.X, op=ALU.add),
                     reads=["tB"], writes=["st16"])
                S.op("dve", lambda e: e.tensor_scalar(out=st16[:, 0:16], in0=st16[:, 0:16], scalar1=1e-24, scalar2=None, op0=ALU.max),
                     reads=["st16"], writes=["st16"])
                _act(S, st16[:, 0:16], st16[:, 0:16], AF.Sqrt, ["st16"], ["st16"])
                S.op("dve", lambda e: e.reciprocal(out=st16[:, 16:32], in_=st16[:, 0:16]), reads=["st16"], writes=["st16"])
                _tt(S, "dve", _h3(F["kkn"][:]), _h3(F["tA"][:]), st16[:, 16:32][:, :, None].broadcast_to([P, 16, 64]), ALU.mult,
                    ["tA", "st16"], ["kkn"])
                if T + 1 < NT:
                    early(T + 1)
                for d in range(2):
                    ps, pk = proj(lambda c: hid[d * 64:(d + 1) * 64, 0, :], lambda c, hf: W["w2c"][d * 64:(d + 1) * 64, hf * 512:(hf + 1) * 512],
                                  "w2c", nchunks=1, lkeys=[("hid", 0)])
                    evac2(lambda hf, sl: _tt(S, "dve", F["tB"][:, sl], ps[hf][:], R[f"w0_{d}"][:, sl], ALU.add, [pk[hf], f"w0_{d}"], ["tB"]))
                    _act(S, F["tB"][:], F["tB"][:], AF.Sigmoid, ["tB"], ["tB"])
                    _act(S, F["lw"][:], F["tB"][:], AF.Copy, ["tB"], ["lw"], scale=LWC)
                    ps, pk = proj(lambda c: hid[d * 64:(d + 1) * 64, 1, :], lambda c, hf: W["a2c"][d * 64:(d + 1) * 64, hf * 512:(hf + 1) * 512],
                                  "a2c", nchunks=1, lkeys=[("hid", 1)])
                    evac2(lambda hf, sl: _tt(S, "dve", F[f"tC{d}"][:, sl], ps[hf][:], R[f"a0_{d}"][:, sl], ALU.add, [pk[hf], f"a0_{d}"], [f"tC{d}"]))
                    _act(S, F[f"tC{d}"][:], F[f"tC{d}"][:], AF.Sigmoid, [f"tC{d}"], [f"tC{d}"])
                    kd = F[f"kd{d}"]; kdk = f"kd{d}"
                    _stt(S, "dve", F["tD"][:], F[f"tC{d}"][:], -1.0, R["ka_r"][:], ALU.add, ALU.mult, [f"tC{d}", "ka_r"], ["tD"])
                    _stt(S, "pool", kd[:], F["tD"][:], 1.0, F["k_sb"][:], ALU.add, ALU.mult, ["tD", "k_sb"], [kdk])
                    _tt(S, "dve", F[f"tC{d}"][:], F["kkn"][:], F[f"tC{d}"][:], ALU.mult, ["kkn", f"tC{d}"], [f"tC{d}"])
                    for hf in range(2):
                        mm(S, pCl[hf][:], tri[:, d, :], F["lw"][:, hf * 512:(hf + 1) * 512], True, True,
                           reads=["tri", "lw"], writes=[("pCl", hf)])
                    evac2(lambda hf, sl: _act(S, F[f"tE{d}"][:, sl], pCl[hf][:], AF.Exp, [("pCl", hf)], [f"tE{d}"]))
                    evac2(lambda hf, sl: _act(S, F[f"tF{d}"][:, sl], pCl[hf][:], AF.Exp, [("pCl", hf)], [f"tF{d}"], scale=-1.0))
                    for hf in range(2):
                        mm(S, pCl[hf][:], onef[:], F["lw"][:, hf * 512:(hf + 1) * 512], True, True,
                           reads=["onef", "lw"], writes=[("pCl", hf)])
                    evac2(lambda hf, sl: _act(S, F["tH"][:, sl], pCl[hf][:], AF.Exp, [("pCl", hf)], ["tH"]))
                    _act(S, F["tG"][:], F["lw"][:], AF.Exp, ["lw"], ["tG"], scale=-1.0)
                    _tt(S, "dve", F["tG"][:], F["tG"][:], F[f"tE{d}"][:], ALU.mult, ["tG", f"tE{d}"], ["tG"])
                    _tt(S, "dve", F["tH"][:], F["tH"][:], F[f"tF{d}"][:], ALU.mult, ["tH", f"tF{d}"], ["tH"])
                    for j in range(8):
                        mm(S, pF[:, 256 + j:257 + j], F["lw"][:, j * P:(j + 1) * P], onef[:, 0:1], True, True,
                           reads=["lw", "onef"], writes=["pF"])
                    _act(S, gcs[:], pF[:, 256:264], AF.Exp, ["pF"], ["gcs"])
                    S.dma("sp", self.gC_d[d, T], gcs[:], reads=["gcs"], writes=[("gC_d", d, T)])
                    _stt(S, "dve", ob[0][:], F["kkn"][:], -1.0, F["tG"][:], ALU.mult, ALU.mult, ["kkn", "tG"], [("ob", 0)])
                    _tt(S, "dve", ob[1][:], F["r_sb"][:], F[f"tE{d}"][:], ALU.mult, ["r_sb", f"tE{d}"], [("ob", 1)])
                    _tt(S, "dve", ob[2][:], F[f"tC{d}"][:], F[f"tF{d}"][:], ALU.mult, [f"tC{d}", f"tF{d}"], [("ob", 2)])
                    _tt(S, "dve", ob[3][:], kd[:], F[f"tF{d}"][:], ALU.mult, [kdk, f"tF{d}"], [("ob", 3)])
                    _tt(S, "dve", bkt[:, 0, :], F[f"tC{d}"][:], F["tH"][:], ALU.mult, [f"tC{d}", "tH"], ["bkt"])
                    _tt(S, "dve", bkt[:, 1, :], kd[:], F["tH"][:], ALU.mult, [kdk, "tH"], ["bkt"])
                    S.dma("sp", self.bk_d[d, T], bkt[:].rearrange("p a n -> p (a n)"), reads=["bkt"], writes=[("bk_d", d, T)])
                    for q in range(4):
                        for c in range(8):
                            S.op("pe", lambda e: e.transpose(out=pT[:, c, :], in_=ob[q][:, c * P:(c + 1) * P], identity=self.idb[:]),
                                 reads=[("ob", q), "idb"], writes=["pT"], accum=(c > 0))
                        if q % 2 == 0:
                            _act(S, ft[:, :, q, :], pT[:], AF.Copy, ["pT"], ["ft"])
                        else:
                            S.op("dve", lambda e: e.tensor_copy(out=ft[:, :, q, :], in_=pT[:]), reads=["pT"], writes=["ft"])
                    S.dma("sp", self.featT_d[d, T], ft[:].rearrange("p j q t -> p (j q t)"), reads=["ft"], writes=[("featT_d", d, T)])
                _tt(S, "dve", F["tD"][:], F["kd0"][:], F["kd1"][:], ALU.add, ["kd0", "kd1"], ["tD"])
                _tt(S, "dve", F["tD"][:], F["tD"][:], F["r_sb"][:], ALU.mult, ["tD", "r_sb"], ["tD"])
                _tt(S, "dve", F["tD"][:], F["tD"][:], R["rk_r"][:], ALU.mult, ["tD", "rk_r"], ["tD"])
                S.op("dve", lambda e: e.tensor_reduce(out=st16[:, 32:48], in_=_h3(F["tD"][:]), axis=AX.X, op=ALU.add),
                     reads=["tD"], writes=["st16"])
                _tt(S, "dve", _h3(F["tD"][:]), _h3(F["v_sb"][:]), st16[:, 32:48][:, :, None].broadcast_to([P, 16, 64]), ALU.mult,
                    ["v_sb", "st16"], ["tD"])
                S.dma("sp", self.bonus_d[t0:t0 + P, :], F["tD"][:], reads=["tD"], writes=[("bonus_d", T)])

    def phase_R2(self):
        S = self.S
        with S.scope():
            m4 = S.sb("m4", [P, 2, 512], F32)
            lmN = S.sb("lmN", [P, 2, 7, P], F32)
            S.dma("sp", m4[:], self.m4_c.rearrange("d s n -> s d n"), writes=["m4"])
            S.dma("sp", lmN[:], self.lmT_c.rearrange("d l s n -> s d l n"), writes=["lmN"])
            idb = self.idb
            NG = 4
            I4 = S.sb("I4", [P, NG, P], BF16)
            for g in range(NG):
                S.op("pool", lambda e: e.tensor_copy(out=I4[:, g, :], in_=idb[:]), reads=["idb"], writes=["I4"])
            I4f = I4[:].rearrange("p g t -> p (g t)")
            ST32 = [S.sb(f"ST32_{d}", [P, 8, 64], F32) for d in range(2)]
            STb = [S.sb(f"STb_{d}", [P, 8, 64], BF16) for d in range(2)]
            for d in range(2):
                S.op("dve", lambda e: e.memset(ST32[d][:], 0.0), writes=[("ST32", d)])
                S.op("dve", lambda e: e.memset(STb[d][:], 0.0), writes=[("STb", d)])
            NBUF = 3
            Fb = [S.sb(f"Fb{i}", [P, 8, 4, P], BF16) for i in range(NBUF)]
            Vb = [S.sb(f"Vb{i}", [P, D], BF16) for i in range(NBUF)]
            BKb = [S.sb(f"BKb{i}", [P, 2, D], BF16) for i in range(NBUF)]
            gCb = [S.sb(f"gCb{i}", [P, 8], F32) for i in range(NBUF)]
            ysb = [S.sb(f"ysb{i}", [P, D], F32) for i in range(NBUF)]
            SL = []
            for sl in range(2):
                R_ = dict(
                    GM=S.sb(f"GM{sl}", [P, NG, 512], BF16),
                    X=[S.sb(f"X{sl}_{i}", [P, NG, P], BF16) for i in range(2)],
                    XT=[S.sb(f"XT{sl}_{i}", [P, NG, P], BF16) for i in range(2)],
                    T1s=S.sb(f"T1s{sl}", [P, NG, P], BF16),
                    Zq=S.sb(f"Zq{sl}", [P, NG, 64], BF16),
                    Pb=S.sb(f"Pb{sl}", [P, NG, 64], BF16),
                    bk=[S.ps(f"bk{sl}_{i}", [P, NG, P], F32) for i in range(3)],
                    bz=S.ps(f"bz{sl}", [P, 8, 64], F32),
                    sl=sl)
                SL.append(R_)

            items = []
            it = 0
            for d in range(2):
                order = list(range(NT)) if d == 0 else [1, 0] + list(range(NT - 1, 1, -1))
                for ci, T in enumerate(order):
                    for g0 in range(0, 16, NG):
                        items.append(dict(d=d, T=T, g0=g0, b=it % NBUF))
                    it += 1

            def heads_of(g0):
                return [(g, g0 + g, (g0 + g) // 2, ((g0 + g) % 2) * 64) for g in range(NG)]

            def load_chunk(w):
                d, T, b = w["d"], w["T"], w["b"]
                S.dma("sp", Fb[b][:].rearrange("p j q t -> p (j q t)"), self.featT_d[d, T], reads=[("featT_d", d, T)], writes=[("Fb", b)])
                S.dma("sp", Vb[b][:], self.vtok_d[T * P:(T + 1) * P, :], reads=[("vtok", T)], writes=[("Vb", b)])
                S.dma("sp", BKb[b][:].rearrange("p a n -> p (a n)"), self.bk_d[d, T], reads=[("bk_d", d, T)], writes=[("BKb", b)])
                S.dma("sp", gCb[b][:], self.gC_d[d, T], reads=[("gC_d", d, T)], writes=[("gCb", b)])

            def run_group(w, R_):
                d, T, b, g0, sl = w["d"], w["T"], w["b"], w["g0"], R_["sl"]
                if g0 == 0:
                    load_chunk(w)
                Fk, Vk, BKk, gk = ("Fb", b), ("Vb", b), ("BKb", b), ("gCb", b)
                GM, X, XT, T1s, Zq, Pb, bk, bz = (R_[n] for n in ("GM", "X", "XT", "T1s", "Zq", "Pb", "bk", "bz"))
                K = lambda n, *a: (n, sl) + a
                hs = heads_of(g0)
                F_ = Fb[b]
                st32, stb = ST32[d], STb[d]
                for (g, h, j, pb_) in hs:
                    bank = bk[g % 3]; bkk = K("bk", g % 3)
                    bv = bank[:].rearrange("p g t -> p (g t)")
                    AR = F_[pb_:pb_ + 64, j, 0:2, :].rearrange("p q t -> p (q t)")
                    mm(S, bv[:, 0:128], F_[pb_:pb_ + 64, j, 2, :], F_[pb_:pb_ + 64, j, 1, :], True, True, reads=[Fk], writes=[bkk])
                    mm(S, bv[:, 128:384], F_[pb_:pb_ + 64, j, 3, :], AR, True, True, reads=[Fk], writes=[bkk])
                    mm(S, bv[:, 384:512], F_[pb_:pb_ + 64, j, 0, :], F_[pb_:pb_ + 64, j, 2, :], True, True, reads=[Fk], writes=[bkk])
                    _tt(S, "dve", GM[:, g, :], bv, m4[:, d, :], ALU.mult, ["m4"], [bkk, K("GM")])
                yield
                for (g, h, j, pb_) in hs:
                    mm(S, bz[:, g, :], F_[pb_:pb_ + 64, j, 0, :], stb[pb_:pb_ + 64, j, :], True, False, reads=[Fk, ("STb", d)], writes=[K("bz")])
                    mm(S, bz[:, g, :], GM[:, g, 128:256], Vb[b][:, h * 64:(h + 1) * 64], False, True, reads=[K("GM"), Vk], writes=[K("bz")])
                _act(S, Zq[:], bz[:, 0:NG, :], AF.Copy, [], [K("bz"), K("Zq")])
                yield
                xi = 0
                mm(S, bk[0][:].rearrange("p g t -> p (g t)"), idb[:], I4f, True, False, reads=["idb", "I4"], writes=[K("bk", 0)])
                for (g, h, j, pb_) in hs:
                    mm(S, bk[0][:, g, :], GM[:, g, 384:512], idb[:], False, True, reads=[K("GM"), "idb"], writes=[K("bk", 0)])
                _tt(S, "dve", X[xi][:], bk[0][:], lmN[:, d, 0:1, :].broadcast_to([P, NG, P]), ALU.mult, ["lmN"], [K("bk", 0), K("X", xi)])
                yield
                _tt(S, "pool", XT[xi][:], GM[:, :, 384:512], lmN[:, 1 - d, 0:1, :].broadcast_to([P, NG, P]), ALU.mult,
                    [K("GM"), "lmN"], [K("XT", xi)])
                _tt(S, "pool", XT[xi][:], XT[xi][:], I4[:], ALU.add, [K("XT", xi), "I4"], [K("XT", xi)])
                yield
                for l in range(1, 7):
                    mm(S, bk[0][:].rearrange("p g t -> p (g t)"), idb[:], I4f, True, False, reads=["idb", "I4"], writes=[K("bk", 0)])
                    for (g, h, j, pb_) in hs:
                        mm(S, bk[0][:, g, :], GM[:, g, 384:512], X[xi][:, g, :], False, True, reads=[K("GM"), K("X", xi)], writes=[K("bk", 0)])
                    _tt(S, "dve", T1s[:], bk[0][:], lmN[:, d, l:l + 1, :].broadcast_to([P, NG, P]), ALU.mult, ["lmN"], [K("bk", 0), K("T1s")])
                    yield
                    for (g, h, j, pb_) in hs:
                        mm(S, bk[1][:, g, :], XT[xi][:, g, :], T1s[:, g, :], True, True, reads=[K("XT", xi), K("T1s")], writes=[K("bk", 1)])
                    if l < 6:
                        for (g, h, j, pb_) in hs:
                            mm(S, bk[2][:, g, :], T1s[:, g, :], XT[xi][:, g, :], True, True, reads=[K("XT", xi), K("T1s")], writes=[K("bk", 2)])
                    _act(S, X[1 - xi][:], bk[1][:], AF.Copy, [], [K("bk", 1), K("X", 1 - xi)])
                    if l < 6:
                        if l % 3 != 0:
                            _act(S, XT[1 - xi][:], bk[2][:], AF.Copy, [], [K("bk", 2), K("XT", 1 - xi)])
                        else:
                            S.op("dve", lambda e: e.tensor_copy(out=XT[1 - xi][:], in_=bk[2][:]), reads=[], writes=[K("bk", 2), K("XT", 1 - xi)])
                    xi = 1 - xi
                    yield
                for (g, h, j, pb_) in hs:
                    mm(S, bz[:, g, :], X[xi][:, g, :], Zq[:, g, :], True, True, reads=[K("X", xi), K("Zq")], writes=[K("bz")])
                S.op("dve", lambda e: e.tensor_copy(out=Pb[:], in_=bz[:, 0:NG, :]), reads=[], writes=[K("bz"), K("Pb")])
                yield
                for (g, h, j, pb_) in hs:
                    yo = bz[:, g, :]
                    mm(S, yo, GM[:, g, 0:128], Pb[:, g, :], True, False, reads=[K("GM"), K("Pb")], writes=[K("bz")])
                    mm(S, yo, GM[:, g, 256:384], Vb[b][:, h * 64:(h + 1) * 64], False, False, reads=[K("GM"), Vk], writes=[K("bz")])
                    mm(S, yo, F_[pb_:pb_ + 64, j, 1, :], stb[pb_:pb_ + 64, j, :], False, True, reads=[Fk, ("STb", d)], writes=[K("bz")])
                for (g, h, j, pb_) in hs:
                    mm(S, bz[:, 4 + g, :], BKb[b][:, 0, j * P:(j + 1) * P], Pb[:, g, :], True, False, reads=[BKk, K("Pb")], writes=[K("bz")])
                    mm(S, bz[:, 4 + g, :], BKb[b][:, 1, j * P:(j + 1) * P], Vb[b][:, h * 64:(h + 1) * 64], False, True,
                       reads=[BKk, Vk], writes=[K("bz")])
                _act(S, ysb[b][:, g0 * 64:(g0 + NG) * 64].rearrange("p (g v) -> p g v", v=64), bz[:, 0:NG, :], AF.Copy, [], [K("bz"), ("ysb", b)])
                for (g, h, j, pb_) in hs:
                    _stt(S, "dve", st32[pb_:pb_ + 64, j, :], st32[pb_:pb_ + 64, j, :], gCb[b][pb_:pb_ + 64, j:j + 1],
                         bz[pb_:pb_ + 64, 4 + g, :], ALU.mult, ALU.add, [gk], [("ST32", d), K("bz")])
                S.op("pool", lambda e: e.tensor_copy(out=stb[:, g0 // 2:g0 // 2 + 2, :], in_=st32[:, g0 // 2:g0 // 2 + 2, :]),
                     reads=[("ST32", d)], writes=[("STb", d)])
                if g0 + NG == 16:
                    S.dma("pool", self.y_d[d, T * P:(T + 1) * P, :], ysb[b][:], reads=[("ysb", b)], writes=[("y_d", d, T)])
                yield

            nxt = 0
            active = [None, None]
            while True:
                progressed = False
                for sl in range(2):
                    if active[sl] is None and nxt < len(items):
                        active[sl] = run_group(items[nxt], SL[sl])
                        nxt += 1
                    if active[sl] is not None:
                        progressed = True
                        try:
                            next(active[sl])
                        except StopIteration:
                            active[sl] = None
                if not progressed:
                    break

    def phase_R3(self):
        S = self.S
        with S.scope():
            R = {}
            for nm, src in (("lng_r", self.rw_lng), ("lnb_r", self.rw_lnb)):
                R[nm] = S.sb(nm, [P, D], F32)
                S.dma("sp", R[nm][:], src.broadcast_to([P, D]), writes=[nm])
            B = [{n: S.sb(f"{n}{i}", [P, D], F32) for n in ("yf", "yb", "gg", "bo")} for i in range(2)]
            zb = [S.sb(f"zb{i}", [P, D], BF16) for i in range(2)]
            zt = [S.sb(f"zt3_{i}", [P, 8, P], BF16) for i in range(2)]
            st = [S.sb(f"st3_{i}", [P, 64], F32) for i in range(2)]
            pT = [S.ps(f"pT3_{i}", [P, 8, P], BF16) for i in range(2)]
            for T in range(2, NT):
                b = T % 2
                Bf = B[b]
                k = lambda n: (n, b)
                t0 = T * P
                S.dma("pool", Bf["yf"][:], self.y_d[0, t0:t0 + P, :], reads=[("y_d", 0, T)], writes=[k("yf")])
                S.dma("pool", Bf["yb"][:], self.y_d[1, t0:t0 + P, :], reads=[("y_d", 1, T)], writes=[k("yb")])
                S.dma("pool", Bf["gg"][:], self.g_d[t0:t0 + P, :], reads=[("g_d", T)], writes=[k("gg")])
                S.dma("pool", Bf["bo"][:], self.bonus_d[t0:t0 + P, :], reads=[("bonus_d", T)], writes=[k("bo")])
                y = Bf["yf"]; t2 = Bf["yb"]
                _tt(S, "dve", y[:], y[:], t2[:], ALU.add, [k("yf"), k("yb")], [k("yf")])
                S.op("dve", lambda e: e.tensor_reduce(out=st[b][:, 0:16], in_=_h3(y[:]), axis=AX.X, op=ALU.add), reads=[k("yf")], writes=[k("st")])
                S.op("dve", lambda e: e.tensor_scalar(out=st[b][:, 0:16], in0=st[b][:, 0:16], scalar1=-1.0 / 64, scalar2=None, op0=ALU.mult),
                     reads=[k("st")], writes=[k("st")])
                _tt(S, "dve", _h3(y[:]), _h3(y[:]), st[b][:, 0:16][:, :, None].broadcast_to([P, 16, 64]), ALU.add, [k("yf"), k("st")], [k("yf")])
                _tt(S, "pool", t2[:], y[:], y[:], ALU.mult, [k("yf")], [k("yb")])
                S.op("dve", lambda e: e.tensor_reduce(out=st[b][:, 16:32], in_=_h3(t2[:]), axis=AX.X, op=ALU.add), reads=[k("yb")], writes=[k("st")])
                S.op("dve", lambda e: e.tensor_scalar(out=st[b][:, 16:32], in0=st[b][:, 16:32], scalar1=1.0 / 64, scalar2=GN_EPS, op0=ALU.mult, op1=ALU.add),
                     reads=[k("st")], writes=[k("st")])
                _act(S, st[b][:, 16:32], st[b][:, 16:32], AF.Sqrt, [k("st")], [k("st")])
                S.op("dve", lambda e: e.reciprocal(out=st[b][:, 32:48], in_=st[b][:, 16:32]), reads=[k("st")], writes=[k("st")])
                _tt(S, "dve", _h3(y[:]), _h3(y[:]), st[b][:, 32:48][:, :, None].broadcast_to([P, 16, 64]), ALU.mult, [k("yf"), k("st")], [k("yf")])
                _tt(S, "pool", y[:], y[:], R["lng_r"][:], ALU.mult, [k("yf"), "lng_r"], [k("yf")])
                _tt(S, "pool", y[:], y[:], R["lnb_r"][:], ALU.add, [k("yf"), "lnb_r"], [k("yf")])
                _tt(S, "dve", y[:], y[:], Bf["bo"][:], ALU.add, [k("yf"), k("bo")], [k("yf")])
                _tt(S, "dve", zb[b][:], y[:], Bf["gg"][:], ALU.mult, [k("yf"), k("gg")], [k("zb")])
                for c in range(8):
                    S.op("pe", lambda e: e.transpose(out=pT[b][:, c, :], in_=zb[b][:, c * P:(c + 1) * P], identity=self.idb[:]),
                         reads=[k("zb"), "idb"], writes=[k("pT3")], accum=(c > 0))
                _act(S, zt[b][:], pT[b][:], AF.Copy, [k("pT3")], [k("zt3")])
                S.dma("sp", self.zT_d[:, :, t0:t0 + P].rearrange("c p t -> p c t"), zt[b][:], reads=[k("zt3")],
                      writes=[("zT", c, T) for c in range(8)])


def build_program(stop_after=None, debug=(), phases="M0ABCD1abcde"):
    nc = bass.Bass("TRN2", target_bir_lowering=False)
    Pg = Prog1(nc, debug)
    S = Pg.S
    Pg.consts()
    if "M" in phases:
        Pg.phase_mod()
    if "0" in phases:
      with S.scope():
        V0 = Pg.load_layer_vecs(0)
        if "A" in phases: Pg.phase_L0_proj(V0)
        if "B" in phases: Pg.phase_L0_pool()
        if "C" in phases: Pg.phase_L0_attn()
        if "D" in phases:
            ab1 = Pg.load_ab(1, 0, "ab1")
            Pg.phase_out_mlp(0, V0, Pg.ev_w_out, Pg.y_tile_L0, lambda T: (Pg.x_d[T * P:(T + 1) * P, :], ("x_d", T)), next_ab=ab1)
    if "1" in phases:
      with S.scope():
        V1 = Pg.load_layer_vecs(1, part="ab")
        if "a" in phases and "D" not in phases: Pg.phase_R0(V1)
        if "b" in phases: Pg.phase_R1()
        if "c" in phases: Pg.phase_R2()
        if "d" in phases: Pg.phase_R3()
        if "e" in phases:
            Pg.load_layer_vecs(1, V=V1, part="g")
        if "e" in phases: Pg.phase_out_mlp(1, V1, Pg.rw_wo, Pg.y_tile_L0, lambda T: (Pg.out[(T - 2) * P:(T - 1) * P, :], ("out", T)))
    S.barrier()
    S.finish([])
    S.close()
    return nc, Pg


_host_inputs0 = host_inputs


def host_inputs(inputs):
    maps = _host_inputs0(inputs)
    f = lambda a: np.ascontiguousarray(np.asarray(a, dtype=np.float32))
    tri, m4, lmT = _scan_consts()
    sh = {
        "rw_mu": f(inputs["rw_mu"])[0], "rw_wr": f(inputs["rw_wr"])[0], "rw_wk": f(inputs["rw_wk"])[0],
        "rw_wv": f(inputs["rw_wv"])[0], "rw_wo": f(inputs["rw_wo"])[0],
        "rw_w0": f(inputs["rw_w0"])[0], "rw_a0": f(inputs["rw_a0"])[0],
        "w1cat": f(np.concatenate([inputs["rw_w1"][0, 0], inputs["rw_w1"][0, 1]], axis=1)),
        "a1cat": f(np.concatenate([inputs["rw_a1"][0, 0], inputs["rw_a1"][0, 1]], axis=1)),
        "rw_g1": f(inputs["rw_g1"])[0],
        "w2cat": f(np.asarray(inputs["rw_w2"])[0].reshape(128, 1024)), "a2cat": f(np.asarray(inputs["rw_a2"])[0].reshape(128, 1024)),
        "rw_g2": f(inputs["rw_g2"])[0],
        "rw_kk": f(inputs["rw_kk"]).reshape(1, 1024), "rw_ka": f(inputs["rw_ka"]).reshape(1, 1024),
        "rw_rk": f(inputs["rw_rk"]).reshape(1, 1024), "rw_lng": f(inputs["rw_lng"]).reshape(1, 1024),
        "rw_lnb": f(inputs["rw_lnb"]).reshape(1, 1024),
        "tri_c": f(tri), "m4_c": f(m4), "lmT_c": f(lmT),
    }
    for m in maps:
        m.update(sh)
    return maps
```

```python
import contextlib
import numpy as np
import concourse.bass as bass
import concourse.mybir as mybir

F32 = mybir.dt.float32
BF16 = mybir.dt.bfloat16
AF = mybir.ActivationFunctionType
ALU = mybir.AluOpType
AX = mybir.AxisListType

SEM_LIMIT = 10000


class _Ctr:
    def __init__(self, S, name, step):
        self.S = S
        self.name = name
        self.step = step
        self.gen = 0
        self.sem = S._newsem(f"{name}_0")
        self.val = 0

    def next_event(self):
        if self.val + self.step > SEM_LIMIT:
            self.gen += 1
            self.sem = self.S._newsem(f"{self.name}_{self.gen}")
            self.val = 0
        self.val += self.step
        return (self.sem, self.val)


class _PsView:
    def __init__(self, t, shape):
        self.t = t
        self.n1 = shape[1]

    def __getitem__(self, key):
        if not isinstance(key, tuple):
            key = (key,)
        key = list(key)
        if len(key) < 2:
            key.append(slice(None))
        k1 = key[1]
        if isinstance(k1, slice):
            start, stop, step = k1.indices(self.n1)
            key[1] = slice(start, stop, step)
        return self.t[tuple(key)]


class _Eng:
    def __init__(self, S, name, obj):
        self.name = name
        self.obj = obj
        self.ctr = _Ctr(S, "s_" + name, 1)
        self.seen = {}
        self.n_issued = 0
        self.last_ins = None
        self.last_has_inc = False
        self.inc_idx = []
        self.inc_ev = []


class LazyEv:
    __slots__ = ("eng", "idx")

    def __init__(self, eng, idx):
        self.eng = eng
        self.idx = idx


class _Res:
    __slots__ = ("w", "r")

    def __init__(self):
        self.w = None
        self.r = {}


class Sched:
    def __init__(self, nc, n_dma_slots=8):
        self.nc = nc
        self.stack = contextlib.ExitStack()
        self.scopes = [self.stack]
        self.res = {}
        self.engs = {
            "pe": _Eng(self, "pe", nc.tensor),
            "act": _Eng(self, "act", nc.scalar),
            "dve": _Eng(self, "dve", nc.vector),
            "pool": _Eng(self, "pool", nc.gpsimd),
            "sp": _Eng(self, "sp", nc.sync),
        }
        self.dma_slots = {}
        for q in ("sp", "pool", "act"):
            self.dma_slots[q] = [_Ctr(self, f"d_{q}{i}", 16) for i in range(n_dma_slots)]
        self.dma_rr = {"sp": 0, "pool": 0, "act": 0}
        self.n_inst = 0
        self.uid = 0
        self.pending = None
        self.lazy_engines = ()

    def _newsem(self, name):
        return self.stack.enter_context(self.nc.semaphore(name))

    def sb(self, name, shape, dt):
        self.uid += 1
        return self.scopes[-1].enter_context(self.nc.sbuf_tensor(f"sb{self.uid}_{name}", list(shape), dt))

    def ps(self, name, shape, dt=F32):
        self.uid += 1
        esz = 4 if dt == F32 else 2
        per_part = esz
        for d_ in shape[1:]:
            per_part *= d_
        assert per_part <= 2048, (name, shape)
        shape = list(shape)
        if per_part < 2048:
            rest = per_part // shape[1]
            assert 2048 % rest == 0, (name, shape)
            full = [shape[0], 2048 // rest] + shape[2:]
            t = self.scopes[-1].enter_context(self.nc.psum_tensor(f"ps{self.uid}_{name}", full, dt))
            return _PsView(t, shape)
        return self.scopes[-1].enter_context(self.nc.psum_tensor(f"ps{self.uid}_{name}", shape, dt))

    @contextlib.contextmanager
    def scope(self):
        st = contextlib.ExitStack()
        self.scopes.append(st)
        try:
            yield
        finally:
            self.barrier()
            self.scopes.pop()
            st.close()

    def barrier(self):
        evs = []
        for e in self.engs.values():
            if e.n_issued > 0:
                evs.append(self._resolve(LazyEv(e, e.n_issued - 1)))
        for q in self.dma_slots:
            for ctr in self.dma_slots[q]:
                if ctr.val > 0:
                    evs.append((ctr.sem, ctr.val))
        for e in self.engs.values():
            for ev in evs:
                self._wait(e, ev)

    def _r(self, key):
        r = self.res.get(key)
        if r is None:
            r = self.res[key] = _Res()
        return r

    def _resolve(self, ev):
        if not isinstance(ev, LazyEv):
            return ev
        import bisect
        e = ev.eng
        k = bisect.bisect_left(e.inc_idx, ev.idx)
        if k < len(e.inc_idx):
            return e.inc_ev[k]
        assert e.last_ins is not None and not e.last_has_inc and e.n_issued - 1 >= ev.idx
        sv = e.ctr.next_event()
        e.last_ins.then_inc(sv[0], 1)
        e.last_has_inc = True
        e.inc_idx.append(e.n_issued - 1)
        e.inc_ev.append(sv)
        return sv

    def _wait(self, eng, ev):
        if ev is None:
            return
        if isinstance(ev, LazyEv) and ev.eng is eng and eng.name == "pe":
            return
        sem, val = self._resolve(ev)
        k = id(sem)
        if eng.seen.get(k, 0) >= val:
            return
        if self.pending is not None:
            cur = self.pending.get(k)
            if cur is None or cur[1] < val:
                self.pending[k] = (sem, val)
            return
        eng.obj.wait_ge(sem, val)
        eng.seen[k] = val

    def _flush(self, eng):
        pend = list(self.pending.values())
        self.pending = None
        for (sem, val) in pend[:-1]:
            eng.obj.wait_ge(sem, val)
            eng.seen[id(sem)] = val
        if pend:
            sem, val = pend[-1]
            eng.seen[id(sem)] = val
            return (sem, val)
        return None

    def _deps(self, eng, reads, writes, skip_same_eng_write=False):
        for key in reads:
            r = self._r(key)
            self._wait(eng, r.w)
        inorder = eng.name in ("act", "dve")
        for key in writes:
            r = self._r(key)
            if not ((skip_same_eng_write or inorder) and isinstance(r.w, LazyEv) and r.w.eng is eng):
                self._wait(eng, r.w)
            for ev in r.r.values():
                if inorder and isinstance(ev, LazyEv) and ev.eng is eng:
                    continue
                self._wait(eng, ev)

    def _commit(self, ev, reads, writes):
        rk = ev.eng.name if isinstance(ev, LazyEv) else id(ev[0])
        for key in reads:
            self._r(key).r[rk] = ev
        for key in writes:
            r = self._r(key)
            r.w = ev
            r.r = {}

    def op(self, engname, fn, reads=(), writes=(), accum=False):
        eng = self.engs[engname]
        self.pending = {}
        self._deps(eng, reads, writes, skip_same_eng_write=accum)
        last = self._flush(eng)
        ins = fn(eng.obj)
        if last is not None:
            ins._wait_ge(last[0], last[1])
        eng.last_ins = ins
        eng.last_has_inc = False
        ev = LazyEv(eng, eng.n_issued)
        eng.n_issued += 1
        if engname not in self.lazy_engines:
            self._resolve(ev)
        self._commit(ev, reads, writes)
        self.n_inst += 1
        return ev

    def dma(self, q, out, in_, reads=(), writes=(), **kw):
        eng = self.engs[q]
        slots = self.dma_slots[q]
        i = self.dma_rr[q]
        self.dma_rr[q] = (i + 1) % len(slots)
        ctr = slots[i]
        self.pending = {}
        if ctr.val > 0:
            self._wait(eng, (ctr.sem, ctr.val))
        self._deps(eng, reads, writes)
        last = self._flush(eng)
        ev = ctr.next_event()
        ins = eng.obj.dma_start(out=out, in_=in_, **kw)
        if last is not None:
            ins._wait_ge(last[0], last[1])
        ins.then_inc(ev[0], 16)
        self._commit(ev, reads, writes)
        self.n_inst += 1
        return ev

    def finish(self, final_keys):
        eng = self.engs["sp"]
        for key in final_keys:
            r = self._r(key)
            self._wait(eng, r.w)
        for q in self.dma_slots:
            for ctr in self.dma_slots[q]:
                if ctr.val > 0:
                    self._wait(eng, (ctr.sem, ctr.val))

    def close(self):
        self.stack.close()

from concourse.bass_utils import run_bass_kernel_spmd

D = 1024
NCTX = 256
NLAT = 4096
NTOK = NCTX + NLAT
NT = NTOK // 128
EPS = 1e-6
P = 128


def _pool_bands():
    L = 1024
    out = np.zeros((4, 5, 128, 128), np.float32)
    for g, w in enumerate((2, 4, 8, 16)):
        def full(L):
            t = np.arange(L)
            lo = np.clip(t - w // 2, 0, L)
            hi = np.clip(t + w // 2, 0, L)
            s = np.arange(L)[:, None]
            m = ((s >= lo[None, :]) & (s < hi[None, :])).astype(np.float64) / (hi - lo)[None, :]
            m -= np.eye(L)
            return m
        m = full(L)
        out[g, 0] = m[3 * 128:4 * 128, 4 * 128:5 * 128]
        out[g, 1] = m[5 * 128:6 * 128, 4 * 128:5 * 128]
        out[g, 2] = m[4 * 128:5 * 128, 4 * 128:5 * 128]
        out[g, 3] = m[0:128, 0:128]
        out[g, 4] = m[L - 128:, L - 128:]
    return out


_VARS = [(-2, "pm"), (-1, "f"), (0, "f"), (1, "f"), (2, "pp")] + [(d, "f") for d in range(-3, 4)]


def _attn_tables():
    kc = np.arange(64)
    qc = np.arange(64)
    c_start = np.clip(qc - 8, 0, 48)
    col_ok = (kc[:, None] >= c_start[None, :]) & (kc[:, None] < c_start[None, :] + 16)
    dc_idx = np.clip(kc[:, None] - qc[None, :], -15, 15) + 15
    dr_idx = np.zeros((12, 128, 128), np.int64)
    dc_full = np.zeros((128, 128), np.int64)
    mask = np.zeros((12, 128, 128), np.float32)
    for a in range(2):
        for b in range(2):
            dc_full[a * 64:(a + 1) * 64, b * 64:(b + 1) * 64] = dc_idx
    for v, (dl, kind) in enumerate(_VARS):
        for a in range(2):
            for b in range(2):
                dr = 2 * dl + a - b + 7
                vis = True
                if kind == "pm":
                    vis = not (a == 0 and b == 1)
                elif kind == "pp":
                    vis = (a == 0 and b == 1)
                dr_idx[v, a * 64:(a + 1) * 64, b * 64:(b + 1) * 64] = min(max(dr, 0), 14)
                if vis and 0 <= dr <= 14:
                    mask[v, a * 64:(a + 1) * 64, b * 64:(b + 1) * 64] = col_ok
    return dr_idx, dc_full, mask


def mm(S, out, lhsT, rhs, start, stop, reads, writes):
    return S.op("pe", lambda e: e.matmul(out, lhsT=lhsT, rhs=rhs, start=start, stop=stop),
                reads=reads, writes=writes, accum=not start)


class Prog:
    def __init__(self, nc, debug=()):
        self.nc = nc
        self.S = Sched(nc)
        self.debug = debug
        self.dbg_out = {}
        dt = nc.dram_tensor
        I = lambda name, shape: dt(name, list(shape), F32, kind="ExternalInput").ap()
        self.x_in = I("x", [NLAT, D])
        self.ctx_in = I("ctx", [NCTX, D])
        self.cvec = I("cvec", [P, 8, 2])
        self.ada_w = I("ada_w", [2, D, 6 * D])
        self.ada_b = I("ada_b", [2, 6 * D])
        self.norm_g = I("norm_g", [2, 4, D])
        self.mlp_w1 = I("mlp_w1", [2, D, 4 * D])
        self.mlp_w2 = I("mlp_w2", [2, 4 * D, D])
        self.ev_w_in = I("ev_w_in", [D, 2 * D])
        self.ev_w_out = I("ev_w_out", [D, D])
        self.ev_pool_w = I("ev_pool_w", [4, P, P])
        self.ev_pool_scale = I("ev_pool_scale", [512])
        self.rpb_tab = I("rpb_tab", [P, 8 * 12, P])
        self.msk_tab = I("msk_tab", [P, 12, P])
        self.bands = I("bands", [P, 20, P])
        self.ident = I("ident", [P, P])
        self.out = dt("out", [NLAT, D], F32, kind="ExternalOutput").ap()
        X = lambda name, shape, d=F32: (dt(name, list(shape), d, kind="ExternalOutput").ap() if name in debug
                                        else dt(name, list(shape), d).ap())
        self.modd = X("modd", [2, 2, 6 * D])
        self.x_d = X("x_d", [NTOK, D])
        self.upool_d = X("upool_d", [NTOK, 512], BF16)
        self.v_d = X("v_d", [NTOK, 512], BF16)
        self.qT_d = X("qT_d", [4, P, NTOK], BF16)
        self.kT_d = X("kT_d", [4, P, NTOK], BF16)
        self.zT_d = X("zT_d", [8, P, NTOK], BF16)

    def dbg(self, name, shape, dtp=F32):
        t = self.nc.dram_tensor("dbg_" + name, list(shape), dtp, kind="ExternalOutput").ap()
        self.dbg_out[name] = t
        return t

    def consts(self):
        S = self.S
        self.idb = S.sb("idb", [P, P], BF16)
        S.dma("pool", self.idb[:], self.ident, writes=["idb"])
        self.ones_bf = S.sb("ones_bf", [P, P], BF16)
        S.op("dve", lambda e: e.memset(self.ones_bf[:], 1.0), writes=["ones_bf"])
        self.eps_t = S.sb("eps_t", [P, 1], F32)
        S.op("dve", lambda e: e.memset(self.eps_t[:], EPS), writes=["eps_t"])

    def phase_mod(self):
        S = self.S
        with S.scope():
            cv = S.sb("cv", [P, 8, 2], F32)
            cvb = S.sb("cvb", [P, 8, 2], BF16)
            S.dma("sp", cv[:], self.cvec, writes=["cv"])
            S.op("act", lambda e: e.activation(out=cvb[:], in_=cv[:], func=AF.Silu), reads=["cv"], writes=["cvb"])
            aw = S.sb("aw", [P, 8, 6 * D], BF16)
            ab = S.sb("ab", [2, 6 * D], F32)
            mrow = S.sb("mrow", [2, 6 * D], F32)
            pss = [S.ps(f"pm{i}", [2, 512], F32) for i in range(4)]
            for l in range(2):
                for c in range(8):
                    S.dma("pool", aw[:, c, :], self.ada_w[l, c * P:(c + 1) * P, :], writes=[("aw", c)])
                S.dma("sp", ab[:], self.ada_b[l:l + 1, :].broadcast_to([2, 6 * D]), writes=["ab"])
                for n in range(12):
                    ps = pss[n % 4]
                    k = ("pm", n % 4)
                    for c in range(8):
                        mm(S, ps[:], cvb[:, c, :], aw[:, c, n * 512:(n + 1) * 512], c == 0, c == 7,
                           reads=["cvb", ("aw", c)], writes=[k])
                    S.op("dve", lambda e: e.tensor_tensor(out=mrow[:, n * 512:(n + 1) * 512], in0=ps[:],
                                                          in1=ab[:, n * 512:(n + 1) * 512], op=ALU.add),
                         reads=[k, "ab"], writes=["mrow"])
                S.dma("sp", self.modd[l], mrow[:], reads=["mrow"], writes=[("modd", l)])

    def load_layer_vecs(self, l, V=None, part="all"):
        S = self.S
        V = {} if V is None else V
        for s in range(2):
            for which in range(2):
                if part in ("all", "ab"):
                    V[("A", which, s)] = (S.sb(f"A{which}_{s}", [P, 8], F32), f"A{which}_{s}")
                    V[("B", which, s)] = (S.sb(f"B{which}_{s}", [P, 8], F32), f"B{which}_{s}")
                if part in ("all", "g") and not (l == 1 and s == 1):
                    V[("G", which, s)] = (S.sb(f"GG{which}_{s}", [P, D], F32), f"GG{which}_{s}")
        with S.scope(), self.nc.allow_non_contiguous_dma(reason="tiny per-feature vectors"):
            tmp = S.sb("lv_tmp", [P, 8], F32)
            rowt = S.sb("lv_row", [P, D], F32)
            for s in range(2):
                for which, (ish, isc, ig) in enumerate(((0, 1, 0), (3, 4, 2))):
                    if part == "g":
                        continue
                    A, ka = V[("A", which, s)]
                    B, kb = V[("B", which, s)]
                    S.dma("sp", B[:], self.modd[l, s, ish * D:(ish + 1) * D].rearrange("(c p) -> p c", p=P),
                          reads=[("modd", l)], writes=[kb])
                    S.dma("sp", A[:], self.modd[l, s, isc * D:(isc + 1) * D].rearrange("(c p) -> p c", p=P),
                          reads=[("modd", l)], writes=[ka])
                    S.dma("sp", tmp[:], self.norm_g[l, ig, :].rearrange("(c p) -> p c", p=P), writes=["lv_tmp"])
                    S.op("dve", lambda e: e.scalar_tensor_tensor(out=A[:], in0=A[:], scalar=1.0, in1=tmp[:],
                                                                 op0=ALU.add, op1=ALU.mult),
                         reads=[ka, "lv_tmp"], writes=[ka])
                for which, (igt, ig) in enumerate(((2, 1), (5, 3))):
                    if (l == 1 and s == 1) or part == "ab":
                        continue
                    G, kg = V[("G", which, s)]
                    S.dma("sp", G[:], self.modd[l, s:s + 1, igt * D:(igt + 1) * D].broadcast_to([P, D]),
                          reads=[("modd", l)], writes=[kg])
                    S.dma("sp", rowt[:], self.norm_g[l, ig:ig + 1, :].broadcast_to([P, D]), writes=["lv_row"])
                    S.op("dve", lambda e: e.tensor_tensor(out=G[:], in0=G[:], in1=rowt[:], op=ALU.mult),
                         reads=[kg, "lv_row"], writes=[kg])
        return V

    def load_ab(self, l, which, tag):
        S = self.S
        ish, isc, ig = ((0, 1, 0), (3, 4, 2))[which]
        out = {}
        with self.nc.allow_non_contiguous_dma(reason="tiny per-feature vectors"):
            tmp = S.sb(f"{tag}_t", [P, 8], F32)
            for s in range(2):
                A = S.sb(f"{tag}_A{s}", [P, 8], F32); ka = f"{tag}_A{s}"
                B = S.sb(f"{tag}_B{s}", [P, 8], F32); kb = f"{tag}_B{s}"
                S.dma("sp", B[:], self.modd[l, s, ish * D:(ish + 1) * D].rearrange("(c p) -> p c", p=P),
                      reads=[("modd", l)], writes=[kb])
                S.dma("sp", A[:], self.modd[l, s, isc * D:(isc + 1) * D].rearrange("(c p) -> p c", p=P),
                      reads=[("modd", l)], writes=[ka])
                S.dma("sp", tmp[:], self.norm_g[l, ig, :].rearrange("(c p) -> p c", p=P), writes=[f"{tag}_t"])
                S.op("dve", lambda e: e.scalar_tensor_tensor(out=A[:], in0=A[:], scalar=1.0, in1=tmp[:],
                                                             op0=ALU.add, op1=ALU.mult),
                     reads=[ka, f"{tag}_t"], writes=[ka])
                out[s] = ((A, ka), (B, kb))
        return out

    def make_norm_bufs(self, tag, nb=2):
        S = self.S
        B = {"i": 0, "nb": nb, "tag": tag}
        B["sq"] = [S.sb(f"{tag}_sq{i}", [P, D], BF16) for i in range(1)] * nb
        B["st"] = [S.sb(f"{tag}_st{i}", [P, 4], F32) for i in range(nb)]
        B["xn"] = [S.sb(f"{tag}_xn{i}", [P, D], BF16) for i in range(nb)]
        B["tp"] = [S.ps(f"{tag}_tp{i}", [P, 8, P], BF16) for i in range(nb)]
        return B

    def norm_to_hT(self, B, x_sb, xkey, A, B_, out_ap, out_key, out2_ap=None, out2_key=None):
        S = self.S
        i = B["i"] % B["nb"]
        B["i"] += 1
        tag = B["tag"]
        sq, st, xn, tp = B["sq"][i], B["st"][i], B["xn"][i], B["tp"][i]
        ksq, kst, kxn, ktp, ktm = [(tag, n, i) for n in ("sq", "st", "xn", "tp", "tm")]
        ksq = (tag, "sq", 0)
        S.op("pool", lambda e: e.memset(st[:], 0.0), writes=[kst])
        S.op("act", lambda e: e.activation(out=sq[:], in_=x_sb, func=AF.Square, accum_out=st[:, 0:1]),
             reads=[xkey], writes=[ksq, kst])
        S.op("act", lambda e: e.activation(out=st[:, 1:2], in_=st[:, 0:1], func=AF.Sqrt, scale=1.0 / D,
                                           bias=self.eps_t[:, 0:1]), reads=[kst, "eps_t"], writes=[kst])
        S.op("dve", lambda e: e.reciprocal(out=st[:, 2:3], in_=st[:, 1:2]), reads=[kst], writes=[kst])
        S.op("dve", lambda e: e.tensor_scalar(out=xn[:], in0=x_sb, scalar1=st[:, 2:3], scalar2=None, op0=ALU.mult),
             reads=[xkey, kst], writes=[kxn])
        for c in range(8):
            S.op("pe", lambda e: e.transpose(out=tp[:, c, :], in_=xn[:, c * P:(c + 1) * P], identity=self.idb[:]),
                 reads=[kxn, "idb"], writes=[ktp], accum=(c > 0))
        for c in range(8):
            S.op("dve", lambda e: e.tensor_scalar(out=out_ap[:, c, :], in0=tp[:, c, :], scalar1=A[0][:, c:c + 1],
                                                  scalar2=B_[0][:, c:c + 1], op0=ALU.mult, op1=ALU.add),
                 reads=[ktp, A[1], B_[1]], writes=[out_key])

    def x_src(self, layer, T):
        if layer == 0:
            if T < 2:
                return self.ctx_in[T * P:(T + 1) * P, :], None
            return self.x_in[(T - 2) * P:(T - 1) * P, :], None
        return self.x_d[T * P:(T + 1) * P, :], ("x_d", T)

    def phase_L0_proj(self, V):
        S = self.S
        with S.scope():
            w = S.sb("w_in", [P, 8, 2 * D], BF16)
            for c in range(8):
                S.dma("pool", w[:, c, :], self.ev_w_in[c * P:(c + 1) * P, :], writes=[("w_in", c)])
            wk = [("w_in", c) for c in range(8)]
            NB = self.make_norm_bufs("n0")
            xt = [S.sb(f"xt{i}", [P, D], F32) for i in range(2)]
            hT = [S.sb(f"hT{i}", [P, 8, 512], BF16) for i in range(2)]
            ptok = [S.ps(f"ptok{i}", [P, 512], F32) for i in range(2)]
            pft = [S.ps(f"pft{i}", [P, 512], F32) for i in range(2)]
            otok = [S.sb(f"otok{i}", [P, 512], BF16) for i in range(2)]
            oft = [S.sb(f"oft{i}", [P, 512], BF16) for i in range(2)]
            supers = [(0, 2)] + [(2 + 4 * i, 4) for i in range(8)]
            cnt = 0
            ctok = 0
            cft = 0
            for si, (T0, nt) in enumerate(supers):
                hb = hT[si % 2]
                hk = ("hT", si % 2)
                s = 1 if T0 < 2 else 0
                for t in range(nt):
                    T = T0 + t
                    xb = xt[cnt % 2]
                    xk = ("xt", cnt % 2)
                    cnt += 1
                    src, sk = self.x_src(0, T)
                    S.dma("act", xb[:], src, reads=[sk] if sk else [], writes=[xk])
                    self.norm_to_hT(NB, xb[:], xk, V[("A", 0, s)], V[("B", 0, s)],
                                    hb[:, :, t * P:(t + 1) * P], (hk, t))
                n = nt * P
                hks = [(hk, t) for t in range(nt)]
                for t in range(nt):
                    T = T0 + t
                    for (c0, dst, dk) in ((0, self.upool_d, "upool"), (1536, self.v_d, "v")):
                        ps = ptok[ctok % 2]; pk = ("ptok", ctok % 2)
                        ob = otok[ctok % 2]; ok = ("otok", ctok % 2)
                        ctok += 1
                        for c in range(8):
                            mm(S, ps[:], hb[:, c, t * P:(t + 1) * P], w[:, c, c0:c0 + 512], c == 0, c == 7,
                               reads=[(hk, t), wk[c]], writes=[pk])
                        S.op("act", lambda e: e.activation(out=ob[:], in_=ps[:], func=AF.Copy), reads=[pk], writes=[ok])
                        S.dma("sp", dst[T * P:(T + 1) * P, :], ob[:], reads=[ok], writes=[(dk, T)])
                for jb in range(8):
                    c0 = 512 + jb * P
                    ps = pft[cft % 2]; pk = ("pft", cft % 2)
                    ob = oft[cft % 2]; ok = ("oft", cft % 2)
                    cft += 1
                    for c in range(8):
                        mm(S, ps[:, :n], w[:, c, c0:c0 + P], hb[:, c, :n], c == 0, c == 7,
                           reads=hks + [wk[c]], writes=[pk])
                    sc = 0.125 if jb < 4 else 1.0
                    S.op("act", lambda e: e.activation(out=ob[:, :n], in_=ps[:, :n], func=AF.Copy, scale=sc),
                         reads=[pk], writes=[ok])
                    dst = self.qT_d if jb < 4 else self.kT_d
                    dk = "qT" if jb < 4 else "kT"
                    S.dma("sp", dst[jb % 4, :, T0 * P:T0 * P + n], ob[:, :n], reads=[ok],
                          writes=[(dk, jb % 4, T0 + t) for t in range(nt)])

    def phase_L0_pool(self):
        S = self.S
        with S.scope():
            up = S.sb("up_all", [P, NT, 512], BF16)
            for q in range(0, NT, 2):
                S.dma("sp", up[:, q:q + 2, :], self.upool_d[q * P:(q + 2) * P, :].rearrange("(n p) f -> p n f", p=P),
                      reads=[("upool", q), ("upool", q + 1)], writes=[("up", q), ("up", q + 1)])
            bd = S.sb("bands", [P, 20, P], BF16)
            S.dma("pool", bd[:], self.bands, writes=["bands"])
            pw = S.sb("pool_w", [P, 4, P], BF16)
            S.dma("pool", pw[:], self.ev_pool_w.rearrange("g c o -> c g o"), writes=["pool_w"])
            psc = S.sb("pool_sc", [P, 4], F32)
            with self.nc.allow_non_contiguous_dma(reason="tiny"):
                S.dma("sp", psc[:], self.ev_pool_scale.rearrange("(g p) -> p g", p=P), writes=["pool_sc"])
            pb = [S.ps(f"pb{i}", [P, 4, P], F32) for i in range(2)]
            pc = [S.ps(f"pc{i}", [P, 4, P], F32) for i in range(2)]
            pm = [S.sb(f"pmx{i}", [P, 4, P], BF16) for i in range(2)]
            zp = [S.sb(f"zp{i}", [P, 4, P], BF16) for i in range(2)]
            it = 0
            for (T0, n) in ((0, 2), (2, 32)):
                for i in range(n):
                    T = T0 + i
                    b = it % 2
                    it += 1
                    for g in range(4):
                        srcs = []
                        if i > 0:
                            srcs.append((T - 1, 0))
                        cv = 3 if i == 0 else (4 if i == n - 1 else 2)
                        srcs.append((T, cv))
                        if i < n - 1:
                            srcs.append((T + 1, 1))
                        for si, (Ts, v) in enumerate(srcs):
                            mm(S, pb[b][:, g, :], up[:, Ts, g * P:(g + 1) * P], bd[:, g * 5 + v, :],
                               si == 0, si == len(srcs) - 1, reads=[("up", Ts), "bands"], writes=[("pb", b)])
                    S.op("dve", lambda e: e.tensor_copy(out=pm[b][:], in_=pb[b][:]), reads=[("pb", b)], writes=[("pmx", b)])
                    for g in range(4):
                        mm(S, pc[b][:, g, :], pw[:, g, :], pm[b][:, g, :], True, True,
                           reads=["pool_w", ("pmx", b)], writes=[("pc", b)])
                    S.op("dve", lambda e: e.tensor_tensor(out=zp[b][:], in0=pc[b][:],
                                                          in1=psc[:, :, None].broadcast_to([P, 4, P]), op=ALU.mult),
                         reads=[("pc", b), "pool_sc"], writes=[("zp", b)])
                    S.dma("sp", self.zT_d[0:4, :, T * P:(T + 1) * P].rearrange("c p t -> p c t"), zp[b][:],
                          reads=[("zp", b)], writes=[("zT", c, T) for c in range(4)])

    def phase_L0_attn(self):
        S = self.S
        with S.scope():
            kT = S.sb("kT_all", [P, 4, NTOK], BF16)
            qT = S.sb("qT_all", [P, 4, NTOK], BF16)
            va = S.sb("v_all", [P, NT, 512], BF16)
            for j in range(4):
                S.dma("sp", kT[:, j, :], self.kT_d[j], reads=[("kT", j, T) for T in range(NT)], writes=[("kTa", j)])
                S.dma("sp", qT[:, j, :], self.qT_d[j], reads=[("qT", j, T) for T in range(NT)], writes=[("qTa", j)])
            for q in range(0, NT, 2):
                S.dma("sp", va[:, q:q + 2, :], self.v_d[q * P:(q + 2) * P, :].rearrange("(n p) f -> p n f", p=P),
                      reads=[("v", q), ("v", q + 1)], writes=[("va", q), ("va", q + 1)])
            E = S.sb("Etab", [P, 96, P], BF16)
            with S.scope():
                rt = S.sb("rt", [P, 96, P], F32)
                mk = S.sb("mk", [P, 12, P], F32)
                S.dma("sp", rt[:], self.rpb_tab, writes=["rt"])
                S.dma("sp", mk[:], self.msk_tab, writes=["mk"])
                S.op("act", lambda e: e.activation(out=rt[:], in_=rt[:], func=AF.Exp), reads=["rt"], writes=["rt"])
                for h in range(8):
                    S.op("dve", lambda e: e.tensor_tensor(out=E[:, h * 12:(h + 1) * 12, :], in0=rt[:, h * 12:(h + 1) * 12, :],
                                                          in1=mk[:], op=ALU.mult), reads=["rt", "mk"], writes=["Etab"])
            pss = [[S.ps(f"pss{i}_{k}", [P, 512], F32) for k in range(2)] for i in range(3)]
            pso = [S.ps(f"pso{i}", [P, 2, P], F32) for i in range(2)]
            pex = [S.sb(f"pex{i}", [P, 7, P], BF16) for i in range(3)]
            pT = [S.sb(f"pT{i}", [P, 5, P], BF16) for i in range(3)]
            rc = [S.sb(f"rc{i}", [P, P], F32) for i in range(2)]
            zo = [S.sb(f"zo{i}", [P, P], BF16) for i in range(2)]
            it = 0
            izo = 0
            for T in range(NT):
                if T < 2:
                    chunks = [(0, None), (1, None)]
                else:
                    i = T - 2
                    if 2 <= i <= 29:
                        lat = [(T + d, v) for v, d in enumerate((-2, -1, 0, 1, 2))]
                    elif i == 0:
                        lat = [(T + d, 8 + d) for d in (0, 1, 2, 3)]
                    elif i == 1:
                        lat = [(T + d, 8 + d) for d in (-1, 0, 1, 2)]
                    elif i == 30:
                        lat = [(T + d, 8 + d) for d in (-2, -1, 0, 1)]
                    else:
                        lat = [(T + d, 8 + d) for d in (-3, -2, -1, 0)]
                    chunks = [(0, None), (1, None)] + lat
                nk = len(chunks)
                nlat = nk - 2
                for j in range(4):
                    zb = zo[izo % 2]; zk = ("zo", izo % 2)
                    izo += 1
                    for hh in range(2):
                        h = 2 * j + hh
                        pb_ = hh * 64
                        b = it % 3
                        bo = it % 2
                        it += 1
                        for ci, (Tk, v) in enumerate(chunks):
                            bank = pss[b][ci // 4]
                            mm(S, bank[:, (ci % 4) * P:(ci % 4 + 1) * P],
                               kT[pb_:pb_ + 64, j, Tk * P:(Tk + 1) * P], qT[pb_:pb_ + 64, j, T * P:(T + 1) * P],
                               True, True, reads=[("kTa", j), ("qTa", j)], writes=[("pss", b, ci // 4)])
                        n0 = min(nk, 4)
                        S.op("act", lambda e: e.activation(out=pex[b][:, 0:n0, :], in_=pss[b][0][:, 0:n0 * P].rearrange("p (c q) -> p c q", q=P), func=AF.Exp),
                             reads=[("pss", b, 0)], writes=[("pex", b)])
                        if nk > 4:
                            S.op("act", lambda e: e.activation(out=pex[b][:, 4:nk, :], in_=pss[b][1][:, 0:(nk - 4) * P].rearrange("p (c q) -> p c q", q=P), func=AF.Exp),
                                 reads=[("pss", b, 1)], writes=[("pex", b)])
                        if nlat > 0:
                            v0 = chunks[2][1]
                            S.op("dve", lambda e: e.tensor_tensor(out=pT[b][:, 0:nlat, :], in0=pex[b][:, 2:nk, :],
                                                                  in1=E[:, h * 12 + v0:h * 12 + v0 + nlat, :], op=ALU.mult),
                                 reads=[("pex", b), "Etab"], writes=[("pT", b)])
                        for ci, (Tk, v) in enumerate(chunks):
                            rhs = pex[b][:, ci, :] if v is None else pT[b][:, ci - 2, :]
                            rk = [("pex", b)] if v is None else [("pT", b)]
                            mm(S, pso[bo][:, 0, :], va[:, Tk, j * P:(j + 1) * P], rhs, ci == 0, ci == nk - 1,
                               reads=[("va", Tk)] + rk, writes=[("pso", bo)])
                        for ci, (Tk, v) in enumerate(chunks):
                            rhs = pex[b][:, ci, :] if v is None else pT[b][:, ci - 2, :]
                            rk = [("pex", b)] if v is None else [("pT", b)]
                            mm(S, pso[bo][:, 1, :], self.ones_bf[:], rhs, ci == 0, ci == nk - 1,
                               reads=["ones_bf"] + rk, writes=[("pso", bo)])
                        S.op("dve", lambda e: e.reciprocal(out=rc[bo][pb_:pb_ + 64, :], in_=pso[bo][pb_:pb_ + 64, 1, :]),
                             reads=[("pso", bo)], writes=[("rc", bo)])
                        S.op("dve", lambda e: e.tensor_tensor(out=zb[pb_:pb_ + 64, :], in0=pso[bo][pb_:pb_ + 64, 0, :],
                                                              in1=rc[bo][pb_:pb_ + 64, :], op=ALU.mult),
                             reads=[("pso", bo), ("rc", bo)], writes=[zk])
                    S.dma("sp", self.zT_d[4 + j, :, T * P:(T + 1) * P], zb[:], reads=[zk], writes=[("zT", 4 + j, T)])

    def phase_out_mlp(self, layer, V, w_out_ap, y_tile_fn, dst_fn, next_ab=None):
        S = self.S
        with S.scope():
            wo = S.sb("wo", [P, 8, D], BF16)
            for c in range(8):
                S.dma("pool", wo[:, c, :], w_out_ap[c * P:(c + 1) * P, :], writes=[("wo", c)])
            w1 = S.sb("w1", [P, 8, 4 * D], BF16)
            w2 = S.sb("w2", [P, 32, D], BF16)
            for c in range(8):
                S.dma("pool", w1[:, c, :], self.mlp_w1[layer, c * P:(c + 1) * P, :], writes=[("w1", c)])
            for f in range(0, 32, 4):
                S.dma("pool", w2[:, f:f + 4, :], self.mlp_w2[layer, f * P:(f + 4) * P, :].rearrange("(n p) d -> p n d", p=P),
                      writes=[("w2", f + q) for q in range(4)])
            NB = self.make_norm_bufs("nm", nb=1)
            hob = S.sb("hob", [P, 8, P], F32) if next_ab is not None else None
            zt = [S.sb(f"zt{i}", [P, 8, P], BF16) for i in range(2)]
            xt = [S.sb(f"xo{i}", [P, D], F32) for i in range(2)]
            x1 = [S.sb(f"x1_{i}", [P, D], F32) for i in range(2)]
            tmp = [S.sb(f"tg{i}", [P, D], F32) for i in range(2)]
            sq = NB["sq"][0]
            stt = [S.sb(f"ost{i}", [P, 4], F32) for i in range(4)]
            hT = [S.sb(f"hm{i}", [P, 8, 256], BF16) for i in range(1)] * 2
            py = [[S.ps(f"py{t}_{hf}", [P, 512], F32) for hf in range(2)] for t in range(2)]
            pa = [S.ps(f"pa{i}", [P, 256], F32) for i in range(2)]
            r32 = [S.sb(f"r32_{i}", [P, 256], F32) for i in range(2)]
            aT = [S.sb(f"aT{i}", [P, 256], BF16) for i in range(2)]
            ist = 0

            def norm_gate_res(t, G, xin, xin_key, xout, xout_key):
                nonlocal ist
                st = stt[ist % 4]; sk = ("ost", ist % 4)
                ist += 1
                S.op("pool", lambda e: e.memset(st[:], 0.0), writes=[sk])
                for hf in range(2):
                    S.op("act", lambda e: e.activation(out=sq[:, hf * 512:(hf + 1) * 512], in_=py[t][hf][:], func=AF.Square,
                                                       accum_out=st[:, hf:hf + 1]), reads=[("py", t, hf)], writes=[("nm", "sq", 0), sk])
                S.op("dve", lambda e: e.tensor_tensor(out=st[:, 2:3], in0=st[:, 0:1], in1=st[:, 1:2], op=ALU.add),
                     reads=[sk], writes=[sk])
                S.op("act", lambda e: e.activation(out=st[:, 2:3], in_=st[:, 2:3], func=AF.Sqrt, scale=1.0 / D,
                                                   bias=self.eps_t[:, 0:1]), reads=[sk, "eps_t"], writes=[sk])
                S.op("dve", lambda e: e.reciprocal(out=st[:, 3:4], in_=st[:, 2:3]), reads=[sk], writes=[sk])
                tb = tmp[t]; tk = ("tg", t)
                for hf in range(2):
                    S.op("dve", lambda e: e.scalar_tensor_tensor(out=tb[:, hf * 512:(hf + 1) * 512], in0=py[t][hf][:],
                                                                 scalar=st[:, 3:4], in1=G[0][:, hf * 512:(hf + 1) * 512],
                                                                 op0=ALU.mult, op1=ALU.mult),
                         reads=[("py", t, hf), sk, G[1]], writes=[tk])
                S.op("dve", lambda e: e.tensor_tensor(out=xout, in0=tb[:], in1=xin, op=ALU.add),
                     reads=[tk, xin_key], writes=[xout_key])

            ia = 0
            loaded = {}

            def load_inputs(sidx):
                loaded[sidx] = True
                for t in range(2):
                    T = 2 * sidx + t
                    src, skey = self.x_src(layer, T)
                    S.dma("sp", xt[t][:], src, reads=[skey] if skey else [], writes=[("xo", t)])
                    S.dma("sp", zt[t][:], self.zT_d[:, :, T * P:(T + 1) * P].rearrange("c p t -> p c t"),
                          reads=[("zT", c, T) for c in range(8)], writes=[("zt", t)])

            for sidx in range(NT // 2):
                T0 = 2 * sidx
                s = 1 if T0 < 2 else 0
                if layer == 1 and s == 1:
                    continue
                hb = hT[0]; hk = ("hm", 0)
                if not loaded.get(sidx):
                    load_inputs(sidx)
                for t in range(2):
                    T = T0 + t
                    y_tile_fn(T, t, zt[t], ("zt", t), wo, py[t])
                    norm_gate_res(t, V[("G", 0, s)], xt[t][:], ("xo", t), x1[t][:], ("x1", t))
                    self.norm_to_hT(NB, x1[t][:], ("x1", t), V[("A", 1, s)], V[("B", 1, s)],
                                    hb[:, :, t * P:(t + 1) * P], (hk, t))
                nxt = sidx + 1
                if nxt < NT // 2 and not (layer == 1 and nxt == 0):
                    load_inputs(nxt)
                def mm1(f):
                    a = (ia + f) % 2
                    for c in range(8):
                        mm(S, pa[a][:], w1[:, c, f * P:(f + 1) * P], hb[:, c, :], c == 0, c == 7,
                           reads=[("w1", c), (hk, 0), (hk, 1)], writes=[("pa", a)])
                    S.op("act", lambda e: e.activation(out=r32[a][:], in_=pa[a][:], func=AF.Relu),
                         reads=[("pa", a)], writes=[("r32", a)])
                    S.op("dve", lambda e: e.tensor_tensor(out=aT[a][:], in0=r32[a][:], in1=r32[a][:], op=ALU.mult),
                         reads=[("r32", a)], writes=[("aT", a)])

                def mm2(f):
                    a = (ia + f) % 2
                    for t in range(2):
                        for hf in range(2):
                            mm(S, py[t][hf][:], aT[a][:, t * P:(t + 1) * P], w2[:, f, hf * 512:(hf + 1) * 512],
                               f == 0, f == 31, reads=[("aT", a), ("w2", f)], writes=[("py", t, hf)])
                mm1(0)
                for f in range(32):
                    if f + 1 < 32:
                        mm1(f + 1)
                    mm2(f)
                for t in range(2):
                    T = T0 + t
                    norm_gate_res(t, V[("G", 1, s)], x1[t][:], ("x1", t), tmp[t][:], ("tg", t))
                    dst, dkey = dst_fn(T)
                    S.dma("sp", dst, tmp[t][:], reads=[("tg", t)], writes=[dkey])
                    if next_ab is not None:
                        ab = next_ab[s]
                        self.norm_to_hT(NB, tmp[t][:], ("tg", t), ab[0], ab[1], hob[:], "hob")
                        S.dma("sp", self.hT_d[:, :, T * P:(T + 1) * P].rearrange("c p t -> p c t"), hob[:],
                              reads=["hob"], writes=[("hT_d", T)])

    def y_tile_L0(self, T, t, zt, zk, wo, py):
        S = self.S
        for hf in range(2):
            for c in range(8):
                mm(S, py[hf][:], zt[:, c, :], wo[:, c, hf * 512:(hf + 1) * 512], c == 0, c == 7,
                   reads=[zk, ("wo", c)], writes=[("py", t, hf)])


def build_program(stop_after=None, debug=()):
    nc = bass.Bass("TRN2", target_bir_lowering=False)
    Pg = Prog(nc, debug)
    S = Pg.S
    Pg.consts()
    Pg.phase_mod()
    final_keys = []
    with S.scope():
        V0 = Pg.load_layer_vecs(0)
        Pg.phase_L0_proj(V0)
        Pg.phase_L0_pool()
        Pg.phase_L0_attn()

        def dst0(T):
            if stop_after == "L0":
                if T < 2:
                    return Pg.x_d[T * P:(T + 1) * P, :], ("x_d", T)
                return Pg.out[(T - 2) * P:(T - 1) * P, :], ("out", T)
            return Pg.x_d[T * P:(T + 1) * P, :], ("x_d", T)
        Pg.phase_out_mlp(0, V0, Pg.ev_w_out, Pg.y_tile_L0, dst0)
    S.barrier()
    S.finish([])
    S.close()
    return nc, Pg


def host_inputs(inputs):
    f = lambda a: np.ascontiguousarray(np.asarray(a, dtype=np.float32))
    dr_idx, dc_full, mask = _attn_tables()
    rpb = f(inputs["ev_rpb"])[0]
    tab = rpb[:, dr_idx, dc_full[None, :, :]]
    tab = np.ascontiguousarray(tab.transpose(2, 0, 1, 3).reshape(128, 96, 128))
    msk = np.ascontiguousarray(mask.transpose(1, 0, 2))
    bands = np.ascontiguousarray(_pool_bands().transpose(2, 0, 1, 3).reshape(128, 20, 128))
    shared = {
        "ada_w": f(inputs["ada_w"]), "ada_b": f(inputs["ada_b"]), "norm_g": f(inputs["norm_g"]),
        "mlp_w1": f(inputs["mlp_w1"]), "mlp_w2": f(inputs["mlp_w2"]),
        "ev_w_in": f(inputs["ev_w_in"])[0], "ev_w_out": f(inputs["ev_w_out"])[0],
        "ev_pool_w": f(inputs["ev_pool_w"])[0], "ev_pool_scale": f(inputs["ev_pool_scale"])[0],
        "rpb_tab": tab, "msk_tab": msk, "bands": bands, "ident": np.eye(128, dtype=np.float32),
    }
    x = f(inputs["x"]); c = f(inputs["c"]); ctx = f(inputs["ctx"]); cc = f(inputs["c_ctx"])
    maps = []
    for b in range(x.shape[0]):
        cv = np.stack([c[b].reshape(8, 128).T, cc.reshape(8, 128).T], axis=-1)
        m = dict(shared)
        m.update({"x": x[b], "ctx": ctx[b], "cvec": np.ascontiguousarray(cv)})
        maps.append(m)
    return maps


_CACHE = {}


def kernel(**inputs):
    maps = host_inputs(inputs)
    if "nc" not in _CACHE:
        _CACHE["nc"] = build_program()
    nc, Pg = _CACHE["nc"]
    res = run_bass_kernel_spmd(nc, maps, core_ids=list(range(8)))
    return np.stack([np.asarray(r["out"]) for r in res.results], axis=0)

LWC = -0.6065306597126334
GN_EPS = 64e-5


def _scan_consts():
    s = np.arange(128)[:, None]
    t = np.arange(128)[None, :]
    tri = np.stack([(s <= t), (s >= t)]).astype(np.float32)
    strict = np.stack([(s < t), (s > t)]).astype(np.float32)
    mT = strict.transpose(0, 2, 1)
    m4 = np.concatenate([tri, strict, tri, mT], axis=2)
    lm = []
    for l in range(7):
        b = 1 << l
        lm.append(((s // (2 * b)) == (t // (2 * b))) & (((s // b) % 2) == 0) & (((t // b) % 2) == 1))
    lm = np.stack(lm).astype(np.float32)
    lmN = np.stack([lm, lm.transpose(0, 2, 1)]) + np.eye(128, dtype=np.float32)[None, None]
    return tri, m4, np.ascontiguousarray(lmN)


def _tt(S, eng, out, a, b, op, reads, writes):
    return S.op(eng, lambda e: e.tensor_tensor(out=out, in0=a, in1=b, op=op), reads=reads, writes=writes)


def _stt(S, eng, out, a, sc, b, op0, op1, reads, writes):
    return S.op("dve", lambda e: e.scalar_tensor_tensor(out=out, in0=a, scalar=sc, in1=b, op0=op0, op1=op1),
                reads=reads, writes=writes)


def _act(S, out, in_, func, reads, writes, **kw):
    return S.op("act", lambda e: e.activation(out=out, in_=in_, func=func, **kw), reads=reads, writes=writes)


def _h3(ap):
    return ap.rearrange("p (h k) -> p h k", k=64)


class Prog1(Prog):
    def __init__(self, nc, debug=()):
        super().__init__(nc, debug)
        dt = nc.dram_tensor
        I = lambda name, shape: dt(name, list(shape), F32, kind="ExternalInput").ap()
        self.rw_mu = I("rw_mu", [6, D])
        self.rw_wr = I("rw_wr", [D, D]); self.rw_wk = I("rw_wk", [D, D])
        self.rw_wv = I("rw_wv", [D, D]); self.rw_wo = I("rw_wo", [D, D])
        self.rw_w0 = I("rw_w0", [2, D]); self.rw_a0 = I("rw_a0", [2, D])
        self.w1cat = I("w1cat", [D, P]); self.a1cat = I("a1cat", [D, P]); self.rw_g1 = I("rw_g1", [D, P])
        self.w2cat = I("w2cat", [P, D]); self.a2cat = I("a2cat", [P, D]); self.rw_g2 = I("rw_g2", [P, D])
        self.rw_kk = I("rw_kk", [1, D]); self.rw_ka = I("rw_ka", [1, D]); self.rw_rk = I("rw_rk", [1, D])
        self.rw_lng = I("rw_lng", [1, D]); self.rw_lnb = I("rw_lnb", [1, D])
        self.tri_c = I("tri_c", [2, P, P]); self.m4_c = I("m4_c", [2, P, 512]); self.lmT_c = I("lmT_c", [2, 7, P, P])
        X = lambda name, shape, d=F32: (dt(name, list(shape), d, kind="ExternalOutput").ap() if name in debug
                                        else dt(name, list(shape), d).ap())
        self.hT_d = X("hT_d", [8, P, NTOK])
        self.featT_d = X("featT_d", [2, NT, P, 8 * 4 * P], BF16)
        self.vtok_d = X("vtok_d", [NTOK, D], BF16)
        self.bk_d = X("bk_d", [2, NT, P, 2 * D], BF16)
        self.gC_d = X("gC_d", [2, NT, P, 8])
        self.g_d = X("g_d", [NTOK, D])
        self.bonus_d = X("bonus_d", [NTOK, D])
        self.y_d = X("y_d", [2, NTOK, D])

    def phase_R0(self, V):
        S = self.S
        with S.scope():
            NB = self.make_norm_bufs("r0")
            xt = [S.sb(f"r0x{i}", [P, D], F32) for i in range(2)]
            ho = [S.sb(f"r0h{i}", [P, 8, P], F32) for i in range(2)]
            for T in range(NT):
                s = 1 if T < 2 else 0
                b = T % 2
                src, sk = self.x_src(1, T)
                S.dma("sp", xt[b][:], src, reads=[sk], writes=[("r0x", b)])
                self.norm_to_hT(NB, xt[b][:], ("r0x", b), V[("A", 0, s)], V[("B", 0, s)], ho[b][:], ("r0h", b))
                S.dma("sp", self.hT_d[:, :, T * P:(T + 1) * P].rearrange("c p t -> p c t"), ho[b][:],
                      reads=[("r0h", b)], writes=[("hT_d", T)])

    def phase_R1(self):
        S = self.S
        with S.scope():
            W = {}
            for nm, src in (("wr", self.rw_wr), ("wk", self.rw_wk), ("wv", self.rw_wv)):
                W[nm] = S.sb(nm, [P, 8, D], BF16)
                for c in range(0, 8, 4):
                    S.dma("pool", W[nm][:, c:c + 4, :], src[c * P:(c + 4) * P, :].rearrange("(c p) n -> p c n", p=P), writes=[nm])
            for nm, src in (("w1c", self.w1cat), ("a1c", self.a1cat), ("g1", self.rw_g1)):
                W[nm] = S.sb(nm, [P, 8, P], BF16)
                S.dma("pool", W[nm][:], src.rearrange("(c p) n -> p c n", p=P), writes=[nm])
            for nm, src in (("w2c", self.w2cat), ("a2c", self.a2cat), ("g2", self.rw_g2)):
                W[nm] = S.sb(nm, [P, D], BF16)
                S.dma("pool", W[nm][:], src, writes=[nm])
            R = {}
            for nm, src in (("kk_r", self.rw_kk), ("ka_r", self.rw_ka), ("rk_r", self.rw_rk),
                            ("w0_0", self.rw_w0[0:1, :]), ("w0_1", self.rw_w0[1:2, :]),
                            ("a0_0", self.rw_a0[0:1, :]), ("a0_1", self.rw_a0[1:2, :])):
                R[nm] = S.sb(nm, [P, D], F32)
                S.dma("sp", R[nm][:], src.broadcast_to([P, D]), writes=[nm])
            mu = S.sb("mu", [P, 6, 8], F32)
            with self.nc.allow_non_contiguous_dma(reason="tiny"):
                S.dma("sp", mu[:], self.rw_mu.rearrange("j (c p) -> p j c", p=P), writes=["mu"])
            tri = S.sb("tri", [P, 2, P], F32)
            S.dma("sp", tri[:], self.tri_c.rearrange("d s t -> s d t"), writes=["tri"])
            onef = S.sb("onef", [P, P], F32)
            S.op("dve", lambda e: e.memset(onef[:], 1.0), writes=["onef"])
            hbuf = S.sb("hbuf", [P, 8, P + 2], F32)
            xx = S.sb("xx", [P, 8, P], F32)
            mix = S.sb("mix", [P, 6, 8, P], BF16)
            hid = S.sb("hid", [P, 3, P], BF16)
            F = {n: S.sb(n, [P, D], F32) for n in ("r_sb", "k_sb", "v_sb", "kkn", "tA", "tB", "lw", "tC0", "tC1", "tD", "kd0", "kd1", "tE0", "tE1", "tF0", "tF1", "tG", "tH")}
            ob = [S.sb(f"ob{i}", [P, D], BF16) for i in range(4)]
            vb = S.sb("vb", [P, D], BF16)
            ft = S.sb("ft", [P, 8, 4, P], BF16)
            bkt = S.sb("bkt", [P, 2, D], BF16)
            st16 = S.sb("st16", [P, 64], F32)
            gcs = S.sb("gcs", [P, 8], F32)
            pA = [[S.ps(f"pA{i}_{h}", [P, 512], F32) for h in range(2)] for i in range(2)]
            pCl = [S.ps(f"pCl{h}", [P, 512], F32) for h in range(2)]
            pF = S.ps("pF", [P, 512], F32)
            pT = S.ps("pT", [P, 8, P], BF16)
            ipa = 0

            def proj(lhs_fn, rhs, rkey, K0=0, K=P, nchunks=8, lkeys=()):
                nonlocal ipa
                i = ipa % 2
                ipa += 1
                for hf in range(2):
                    for c in range(nchunks):
                        mm(S, pA[i][hf][:], lhs_fn(c), rhs(c, hf), c == 0, c == nchunks - 1,
                           reads=list(lkeys) + [rkey], writes=[("pA", i, hf)])
                return pA[i], [("pA", i, 0), ("pA", i, 1)]

            def evac2(fn_half):
                for hf in range(2):
                    fn_half(hf, slice(hf * 512, (hf + 1) * 512))

            def early(T):
                seq_lo, seq_hi = (0, NCTX) if T < 2 else (NCTX, NTOK)
                t0 = T * P
                lo = max(t0 - 1, seq_lo); hi = min(t0 + P + 1, seq_hi)
                if lo > t0 - 1:
                    S.op("pool", lambda e: e.memset(hbuf[:, :, 0:1], 0.0), writes=["hbuf"])
                if hi < t0 + P + 1:
                    S.op("pool", lambda e: e.memset(hbuf[:, :, P + 1:P + 2], 0.0), writes=["hbuf"])
                S.dma("pool", hbuf[:, :, lo - (t0 - 1):hi - (t0 - 1)], self.hT_d[:, :, lo:hi].rearrange("c p t -> p c t"),
                      reads=[("hT_d", q) for q in range(max(T - 1, 0), min(T + 2, NT))], writes=["hbuf"])
                _tt(S, "dve", xx[:], hbuf[:, :, 0:P], hbuf[:, :, 2:P + 2], ALU.add, ["hbuf"], ["xx"])
                _stt(S, "dve", xx[:], xx[:], 0.5, hbuf[:, :, 1:P + 1], ALU.mult, ALU.subtract, ["xx", "hbuf"], ["xx"])
                for j in range(6):
                    mxt = F["tA"][:].rearrange("p (c t) -> p c t", t=P)
                    _tt(S, "dve", mxt, xx[:], mu[:, j, :][:, :, None].broadcast_to([P, 8, P]), ALU.mult, ["xx", "mu"], ["tA"])
                    _tt(S, "dve", mix[:, j, :, :], mxt, hbuf[:, :, 1:P + 1], ALU.add, ["tA", "hbuf"], [("mix", j)])

            early(0)
            for T in range(NT):
                t0 = T * P
                for hi_, (wn, mj, fn) in enumerate((("w1c", 1, AF.Tanh), ("a1c", 4, AF.Copy), ("g1", 5, AF.Sigmoid))):
                    for c in range(8):
                        mm(S, pF[:, 0:P], W[wn][:, c, :], mix[:, mj, c, :], c == 0, c == 7,
                           reads=[wn, ("mix", mj)], writes=["pF"])
                    _act(S, hid[:, hi_, :], pF[:, 0:P], fn, ["pF"], [("hid", hi_)])
                for nm, mj, wn in (("r_sb", 0, "wr"), ("k_sb", 2, "wk"), ("v_sb", 3, "wv")):
                    ps, pk = proj(lambda c: mix[:, mj, c, :], lambda c, hf: W[wn][:, c, hf * 512:(hf + 1) * 512], wn,
                                  lkeys=[("mix", mj)])
                    evac2(lambda hf, sl: _act(S, F[nm][:, sl], ps[hf][:], AF.Copy, [pk[hf]], [nm]))
                S.op("pool", lambda e: e.tensor_copy(out=vb[:], in_=F["v_sb"][:]), reads=["v_sb"], writes=["vb"])
                S.dma("sp", self.vtok_d[t0:t0 + P, :], vb[:], reads=["vb"], writes=[("vtok", T)])
                ps, pk = proj(lambda c: hid[:, 2, :], lambda c, hf: W["g2"][:, hf * 512:(hf + 1) * 512], "g2", nchunks=1,
                              lkeys=[("hid", 2)])
                evac2(lambda hf, sl: _act(S, F["tA"][:, sl], ps[hf][:], AF.Copy, [pk[hf]], ["tA"]))
                S.dma("sp", self.g_d[t0:t0 + P, :], F["tA"][:], reads=["tA"], writes=[("g_d", T)])
                _tt(S, "dve", F["tA"][:], F["k_sb"][:], R["kk_r"][:], ALU.mult, ["k_sb", "kk_r"], ["tA"])
                _tt(S, "dve", F["tB"][:], F["tA"][:], F["tA"][:], ALU.mult, ["tA"], ["tB"])
                S.op("dve", lambda e: e.tensor_reduce(out=st16[:, 0:16], in_=_h3(F["tB"][:]), axis=AX.X, op=ALU.add),
                     reads=["tB"], writes=["st16"])
                S.op("dve", lambda e: e.tensor_scalar(out=st16[:, 0:16], in0=st16[:, 0:16], scalar1=1e-24, scalar2=None, op0=ALU.max),
                     reads=["st16"], writes=["st16"])
                _act(S, st16[:, 0:16], st16[:, 0:16], AF.Sqrt, ["st16"], ["st16"])
                S.op("dve", lambda e: e.reciprocal(out=st16[:, 16:32], in_=st16[:, 0:16]), reads=["st16"], writes=["st16"])
                _tt(S, "dve", _h3(F["kkn"][:]), _h3(F["tA"][:]), st16[:, 16:32][:, :, None].broadcast_to([P, 16, 64]), ALU.mult,
                    ["tA", "st16"], ["kkn"])
                if T + 1 < NT:
                    early(T + 1)
                for d in range(2):
                    ps, pk = proj(lambda c: hid[d * 64:(d + 1) * 64, 0, :], lambda c, hf: W["w2c"][d * 64:(d + 1) * 64, hf * 512:(hf + 1) * 512],
                                  "w2c", nchunks=1, lkeys=[("hid", 0)])
                    evac2(lambda hf, sl: _tt(S, "dve", F["tB"][:, sl], ps[hf][:], R[f"w0_{d}"][:, sl], ALU.add, [pk[hf], f"w0_{d}"], ["tB"]))
                    _act(S, F["tB"][:], F["tB"][:], AF.Sigmoid, ["tB"], ["tB"])
                    _act(S, F["lw"][:], F["tB"][:], AF.Copy, ["tB"], ["lw"], scale=LWC)
                    ps, pk = proj(lambda c: hid[d * 64:(d + 1) * 64, 1, :], lambda c, hf: W["a2c"][d * 64:(d + 1) * 64, hf * 512:(hf + 1) * 512],
                                  "a2c", nchunks=1, lkeys=[("hid", 1)])
                    evac2(lambda hf, sl: _tt(S, "dve", F[f"tC{d}"][:, sl], ps[hf][:], R[f"a0_{d}"][:, sl], ALU.add, [pk[hf], f"a0_{d}"], [f"tC{d}"]))
                    _act(S, F[f"tC{d}"][:], F[f"tC{d}"][:], AF.Sigmoid, [f"tC{d}"], [f"tC{d}"])
                    kd = F[f"kd{d}"]; kdk = f"kd{d}"
                    _stt(S, "dve", F["tD"][:], F[f"tC{d}"][:], -1.0, R["ka_r"][:], ALU.add, ALU.mult, [f"tC{d}", "ka_r"], ["tD"])
                    _stt(S, "pool", kd[:], F["tD"][:], 1.0, F["k_sb"][:], ALU.add, ALU.mult, ["tD", "k_sb"], [kdk])
                    _tt(S, "dve", F[f"tC{d}"][:], F["kkn"][:], F[f"tC{d}"][:], ALU.mult, ["kkn", f"tC{d}"], [f"tC{d}"])
                    for hf in range(2):
                        mm(S, pCl[hf][:], tri[:, d, :], F["lw"][:, hf * 512:(hf + 1) * 512], True, True,
                           reads=["tri", "lw"], writes=[("pCl", hf)])
                    evac2(lambda hf, sl: _act(S, F[f"tE{d}"][:, sl], pCl[hf][:], AF.Exp, [("pCl", hf)], [f"tE{d}"]))
                    evac2(lambda hf, sl: _act(S, F[f"tF{d}"][:, sl], pCl[hf][:], AF.Exp, [("pCl", hf)], [f"tF{d}"], scale=-1.0))
                    for hf in range(2):
                        mm(S, pCl[hf][:], onef[:], F["lw"][:, hf * 512:(hf + 1) * 512], True, True,
                           reads=["onef", "lw"], writes=[("pCl", hf)])
                    evac2(lambda hf, sl: _act(S, F["tH"][:, sl], pCl[hf][:], AF.Exp, [("pCl", hf)], ["tH"]))
                    _act(S, F["tG"][:], F["lw"][:], AF.Exp, ["lw"], ["tG"], scale=-1.0)
                    _tt(S, "dve", F["tG"][:], F["tG"][:], F[f"tE{d}"][:], ALU.mult, ["tG", f"tE{d}"], ["tG"])
                    _tt(S, "dve", F["tH"][:], F["tH"][:], F[f"tF{d}"][:], ALU.mult, ["tH", f"tF{d}"], ["tH"])
                    for j in range(8):
                        mm(S, pF[:, 256 + j:257 + j], F["lw"][:, j * P:(j + 1) * P], onef[:, 0:1], True, True,
                           reads=["lw", "onef"], writes=["pF"])
                    _act(S, gcs[:], pF[:, 256:264], AF.Exp, ["pF"], ["gcs"])
                    S.dma("sp", self.gC_d[d, T], gcs[:], reads=["gcs"], writes=[("gC_d", d, T)])
                    _stt(S, "dve", ob[0][:], F["kkn"][:], -1.0, F["tG"][:], ALU.mult, ALU.mult, ["kkn", "tG"], [("ob", 0)])
                    _tt(S, "dve", ob[1][:], F["r_sb"][:], F[f"tE{d}"][:], ALU.mult, ["r_sb", f"tE{d}"], [("ob", 1)])
                    _tt(S, "dve", ob[2][:], F[f"tC{d}"][:], F[f"tF{d}"][:], ALU.mult, [f"tC{d}", f"tF{d}"], [("ob", 2)])
                    _tt(S, "dve", ob[3][:], kd[:], F[f"tF{d}"][:], ALU.mult, [kdk, f"tF{d}"], [("ob", 3)])
                    _tt(S, "dve", bkt[:, 0, :], F[f"tC{d}"][:], F["tH"][:], ALU.mult, [f"tC{d}", "tH"], ["bkt"])
                    _tt(S, "dve", bkt[:, 1, :], kd[:], F["tH"][:], ALU.mult, [kdk, "tH"], ["bkt"])
                    S.dma("sp", self.bk_d[d, T], bkt[:].rearrange("p a n -> p (a n)"), reads=["bkt"], writes=[("bk_d", d, T)])
                    for q in range(4):
                        for c in range(8):
                            S.op("pe", lambda e: e.transpose(out=pT[:, c, :], in_=ob[q][:, c * P:(c + 1) * P], identity=self.idb[:]),
                                 reads=[("ob", q), "idb"], writes=["pT"], accum=(c > 0))
                        if q % 2 == 0:
                            _act(S, ft[:, :, q, :], pT[:], AF.Copy, ["pT"], ["ft"])
                        else:
                            S.op("dve", lambda e: e.tensor_copy(out=ft[:, :, q, :], in_=pT[:]), reads=["pT"], writes=["ft"])
                    S.dma("sp", self.featT_d[d, T], ft[:].rearrange("p j q t -> p (j q t)"), reads=["ft"], writes=[("featT_d", d, T)])
                _tt(S, "dve", F["tD"][:], F["kd0"][:], F["kd1"][:], ALU.add, ["kd0", "kd1"], ["tD"])
                _tt(S, "dve", F["tD"][:], F["tD"][:], F["r_sb"][:], ALU.mult, ["tD", "r_sb"], ["tD"])
                _tt(S, "dve", F["tD"][:], F["tD"][:], R["rk_r"][:], ALU.mult, ["tD", "rk_r"], ["tD"])
                S.op("dve", lambda e: e.tensor_reduce(out=st16[:, 32:48], in_=_h3(F["tD"][:]), axis=AX.X, op=ALU.add),
                     reads=["tD"], writes=["st16"])
                _tt(S, "dve", _h3(F["tD"][:]), _h3(F["v_sb"][:]), st16[:, 32:48][:, :, None].broadcast_to([P, 16, 64]), ALU.mult,
                    ["v_sb", "st16"], ["tD"])
                S.dma("sp", self.bonus_d[t0:t0 + P, :], F["tD"][:], reads=["tD"], writes=[("bonus_d", T)])

    def phase_R2(self):
        S = self.S
        with S.scope():
            m4 = S.sb("m4", [P, 2, 512], F32)
            lmN = S.sb("lmN", [P, 2, 7, P], F32)
            S.dma("sp", m4[:], self.m4_c.rearrange("d s n -> s d n"), writes=["m4"])
            S.dma("sp", lmN[:], self.lmT_c.rearrange("d l s n -> s d l n"), writes=["lmN"])
            idb = self.idb
            NG = 4
            I4 = S.sb("I4", [P, NG, P], BF16)
            for g in range(NG):
                S.op("pool", lambda e: e.tensor_copy(out=I4[:, g, :], in_=idb[:]), reads=["idb"], writes=["I4"])
            I4f = I4[:].rearrange("p g t -> p (g t)")
            ST32 = [S.sb(f"ST32_{d}", [P, 8, 64], F32) for d in range(2)]
            STb = [S.sb(f"STb_{d}", [P, 8, 64], BF16) for d in range(2)]
            for d in range(2):
                S.op("dve", lambda e: e.memset(ST32[d][:], 0.0), writes=[("ST32", d)])
                S.op("dve", lambda e: e.memset(STb[d][:], 0.0), writes=[("STb", d)])
            NBUF = 3
            Fb = [S.sb(f"Fb{i}", [P, 8, 4, P], BF16) for i in range(NBUF)]
            Vb = [S.sb(f"Vb{i}", [P, D], BF16) for i in range(NBUF)]
            BKb = [S.sb(f"BKb{i}", [P, 2, D], BF16) for i in range(NBUF)]
            gCb = [S.sb(f"gCb{i}", [P, 8], F32) for i in range(NBUF)]
            ysb = [S.sb(f"ysb{i}", [P, D], F32) for i in range(NBUF)]
            SL = []
            for sl in range(2):
                R_ = dict(
                    GM=S.sb(f"GM{sl}", [P, NG, 512], BF16),
                    X=[S.sb(f"X{sl}_{i}", [P, NG, P], BF16) for i in range(2)],
                    XT=[S.sb(f"XT{sl}_{i}", [P, NG, P], BF16) for i in range(2)],
                    T1s=S.sb(f"T1s{sl}", [P, NG, P], BF16),
                    Zq=S.sb(f"Zq{sl}", [P, NG, 64], BF16),
                    Pb=S.sb(f"Pb{sl}", [P, NG, 64], BF16),
                    bk=[S.ps(f"bk{sl}_{i}", [P, NG, P], F32) for i in range(3)],
                    bz=S.ps(f"bz{sl}", [P, 8, 64], F32),
                    sl=sl)
                SL.append(R_)

            items = []
            it = 0
            for d in range(2):
                order = list(range(NT)) if d == 0 else [1, 0] + list(range(NT - 1, 1, -1))
                for ci, T in enumerate(order):
                    for g0 in range(0, 16, NG):
                        items.append(dict(d=d, T=T, g0=g0, b=it % NBUF))
                    it += 1

            def heads_of(g0):
                return [(g, g0 + g, (g0 + g) // 2, ((g0 + g) % 2) * 64) for g in range(NG)]

            def load_chunk(w):
                d, T, b = w["d"], w["T"], w["b"]
                S.dma("sp", Fb[b][:].rearrange("p j q t -> p (j q t)"), self.featT_d[d, T], reads=[("featT_d", d, T)], writes=[("Fb", b)])
                S.dma("sp", Vb[b][:], self.vtok_d[T * P:(T + 1) * P, :], reads=[("vtok", T)], writes=[("Vb", b)])
                S.dma("sp", BKb[b][:].rearrange("p a n -> p (a n)"), self.bk_d[d, T], reads=[("bk_d", d, T)], writes=[("BKb", b)])
                S.dma("sp", gCb[b][:], self.gC_d[d, T], reads=[("gC_d", d, T)], writes=[("gCb", b)])

            def run_group(w, R_):
                d, T, b, g0, sl = w["d"], w["T"], w["b"], w["g0"], R_["sl"]
                if g0 == 0:
                    load_chunk(w)
                Fk, Vk, BKk, gk = ("Fb", b), ("Vb", b), ("BKb", b), ("gCb", b)
                GM, X, XT, T1s, Zq, Pb, bk, bz = (R_[n] for n in ("GM", "X", "XT", "T1s", "Zq", "Pb", "bk", "bz"))
                K = lambda n, *a: (n, sl) + a
                hs = heads_of(g0)
                F_ = Fb[b]
                st32, stb = ST32[d], STb[d]
                for (g, h, j, pb_) in hs:
                    bank = bk[g % 3]; bkk = K("bk", g % 3)
                    bv = bank[:].rearrange("p g t -> p (g t)")
                    AR = F_[pb_:pb_ + 64, j, 0:2, :].rearrange("p q t -> p (q t)")
                    mm(S, bv[:, 0:128], F_[pb_:pb_ + 64, j, 2, :], F_[pb_:pb_ + 64, j, 1, :], True, True, reads=[Fk], writes=[bkk])
                    mm(S, bv[:, 128:384], F_[pb_:pb_ + 64, j, 3, :], AR, True, True, reads=[Fk], writes=[bkk])
                    mm(S, bv[:, 384:512], F_[pb_:pb_ + 64, j, 0, :], F_[pb_:pb_ + 64, j, 2, :], True, True, reads=[Fk], writes=[bkk])
                    _tt(S, "dve", GM[:, g, :], bv, m4[:, d, :], ALU.mult, ["m4"], [bkk, K("GM")])
                yield
                for (g, h, j, pb_) in hs:
                    mm(S, bz[:, g, :], F_[pb_:pb_ + 64, j, 0, :], stb[pb_:pb_ + 64, j, :], True, False, reads=[Fk, ("STb", d)], writes=[K("bz")])
                    mm(S, bz[:, g, :], GM[:, g, 128:256], Vb[b][:, h * 64:(h + 1) * 64], False, True, reads=[K("GM"), Vk], writes=[K("bz")])
                _act(S, Zq[:], bz[:, 0:NG, :], AF.Copy, [], [K("bz"), K("Zq")])
                yield
                xi = 0
                mm(S, bk[0][:].rearrange("p g t -> p (g t)"), idb[:], I4f, True, False, reads=["idb", "I4"], writes=[K("bk", 0)])
                for (g, h, j, pb_) in hs:
                    mm(S, bk[0][:, g, :], GM[:, g, 384:512], idb[:], False, True, reads=[K("GM"), "idb"], writes=[K("bk", 0)])
                _tt(S, "dve", X[xi][:], bk[0][:], lmN[:, d, 0:1, :].broadcast_to([P, NG, P]), ALU.mult, ["lmN"], [K("bk", 0), K("X", xi)])
                yield
                _tt(S, "pool", XT[xi][:], GM[:, :, 384:512], lmN[:, 1 - d, 0:1, :].broadcast_to([P, NG, P]), ALU.mult,
                    [K("GM"), "lmN"], [K("XT", xi)])
                _tt(S, "pool", XT[xi][:], XT[xi][:], I4[:], ALU.add, [K("XT", xi), "I4"], [K("XT", xi)])
                yield
                for l in range(1, 7):
                    mm(S, bk[0][:].rearrange("p g t -> p (g t)"), idb[:], I4f, True, False, reads=["idb", "I4"], writes=[K("bk", 0)])
                    for (g, h, j, pb_) in hs:
                        mm(S, bk[0][:, g, :], GM[:, g, 384:512], X[xi][:, g, :], False, True, reads=[K("GM"), K("X", xi)], writes=[K("bk", 0)])
                    _tt(S, "dve", T1s[:], bk[0][:], lmN[:, d, l:l + 1, :].broadcast_to([P, NG, P]), ALU.mult, ["lmN"], [K("bk", 0), K("T1s")])
                    yield
                    for (g, h, j, pb_) in hs:
                        mm(S, bk[1][:, g, :], XT[xi][:, g, :], T1s[:, g, :], True, True, reads=[K("XT", xi), K("T1s")], writes=[K("bk", 1)])
                    if l < 6:
                        for (g, h, j, pb_) in hs:
                            mm(S, bk[2][:, g, :], T1s[:, g, :], XT[xi][:, g, :], True, True, reads=[K("XT", xi), K("T1s")], writes=[K("bk", 2)])
                    _act(S, X[1 - xi][:], bk[1][:], AF.Copy, [], [K("bk", 1), K("X", 1 - xi)])
                    if l < 6:
                        if l % 3 != 0:
                            _act(S, XT[1 - xi][:], bk[2][:], AF.Copy, [], [K("bk", 2), K("XT", 1 - xi)])
                        else:
                            S.op("dve", lambda e: e.tensor_copy(out=XT[1 - xi][:], in_=bk[2][:]), reads=[], writes=[K("bk", 2), K("XT", 1 - xi)])
                    xi = 1 - xi
                    yield
                for (g, h, j, pb_) in hs:
                    mm(S, bz[:, g, :], X[xi][:, g, :], Zq[:, g, :], True, True, reads=[K("X", xi), K("Zq")], writes=[K("bz")])
                S.op("dve", lambda e: e.tensor_copy(out=Pb[:], in_=bz[:, 0:NG, :]), reads=[], writes=[K("bz"), K("Pb")])
                yield
                for (g, h, j, pb_) in hs:
                    yo = bz[:, g, :]
                    mm(S, yo, GM[:, g, 0:128], Pb[:, g, :], True, False, reads=[K("GM"), K("Pb")], writes=[K("bz")])
                    mm(S, yo, GM[:, g, 256:384], Vb[b][:, h * 64:(h + 1) * 64], False, False, reads=[K("GM"), Vk], writes=[K("bz")])
                    mm(S, yo, F_[pb_:pb_ + 64, j, 1, :], stb[pb_:pb_ + 64, j, :], False, True, reads=[Fk, ("STb", d)], writes=[K("bz")])
                for (g, h, j, pb_) in hs:
                    mm(S, bz[:, 4 + g, :], BKb[b][:, 0, j * P:(j + 1) * P], Pb[:, g, :], True, False, reads=[BKk, K("Pb")], writes=[K("bz")])
                    mm(S, bz[:, 4 + g, :], BKb[b][:, 1, j * P:(j + 1) * P], Vb[b][:, h * 64:(h + 1) * 64], False, True,
                       reads=[BKk, Vk], writes=[K("bz")])
                _act(S, ysb[b][:, g0 * 64:(g0 + NG) * 64].rearrange("p (g v) -> p g v", v=64), bz[:, 0:NG, :], AF.Copy, [], [K("bz"), ("ysb", b)])
                for (g, h, j, pb_) in hs:
                    _stt(S, "dve", st32[pb_:pb_ + 64, j, :], st32[pb_:pb_ + 64, j, :], gCb[b][pb_:pb_ + 64, j:j + 1],
                         bz[pb_:pb_ + 64, 4 + g, :], ALU.mult, ALU.add, [gk], [("ST32", d), K("bz")])
                S.op("pool", lambda e: e.tensor_copy(out=stb[:, g0 // 2:g0 // 2 + 2, :], in_=st32[:, g0 // 2:g0 // 2 + 2, :]),
                     reads=[("ST32", d)], writes=[("STb", d)])
                if g0 + NG == 16:
                    S.dma("pool", self.y_d[d, T * P:(T + 1) * P, :], ysb[b][:], reads=[("ysb", b)], writes=[("y_d", d, T)])
                yield

            nxt = 0
            active = [None, None]
            while True:
                progressed = False
                for sl in range(2):
                    if active[sl] is None and nxt < len(items):
                        active[sl] = run_group(items[nxt], SL[sl])
                        nxt += 1
                    if active[sl] is not None:
                        progressed = True
                        try:
                            next(active[sl])
                        except StopIteration:
                            active[sl] = None
                if not progressed:
                    break

    def phase_R3(self):
        S = self.S
        with S.scope():
            R = {}
            for nm, src in (("lng_r", self.rw_lng), ("lnb_r", self.rw_lnb)):
                R[nm] = S.sb(nm, [P, D], F32)
                S.dma("sp", R[nm][:], src.broadcast_to([P, D]), writes=[nm])
            B = [{n: S.sb(f"{n}{i}", [P, D], F32) for n in ("yf", "yb", "gg", "bo")} for i in range(2)]
            zb = [S.sb(f"zb{i}", [P, D], BF16) for i in range(2)]
            zt = [S.sb(f"zt3_{i}", [P, 8, P], BF16) for i in range(2)]
            st = [S.sb(f"st3_{i}", [P, 64], F32) for i in range(2)]
            pT = [S.ps(f"pT3_{i}", [P, 8, P], BF16) for i in range(2)]
            for T in range(2, NT):
                b = T % 2
                Bf = B[b]
                k = lambda n: (n, b)
                t0 = T * P
                S.dma("pool", Bf["yf"][:], self.y_d[0, t0:t0 + P, :], reads=[("y_d", 0, T)], writes=[k("yf")])
                S.dma("pool", Bf["yb"][:], self.y_d[1, t0:t0 + P, :], reads=[("y_d", 1, T)], writes=[k("yb")])
                S.dma("pool", Bf["gg"][:], self.g_d[t0:t0 + P, :], reads=[("g_d", T)], writes=[k("gg")])
                S.dma("pool", Bf["bo"][:], self.bonus_d[t0:t0 + P, :], reads=[("bonus_d", T)], writes=[k("bo")])
                y = Bf["yf"]; t2 = Bf["yb"]
                _tt(S, "dve", y[:], y[:], t2[:], ALU.add, [k("yf"), k("yb")], [k("yf")])
                S.op("dve", lambda e: e.tensor_reduce(out=st[b][:, 0:16], in_=_h3(y[:]), axis=AX.X, op=ALU.add), reads=[k("yf")], writes=[k("st")])
                S.op("dve", lambda e: e.tensor_scalar(out=st[b][:, 0:16], in0=st[b][:, 0:16], scalar1=-1.0 / 64, scalar2=None, op0=ALU.mult),
                     reads=[k("st")], writes=[k("st")])
                _tt(S, "dve", _h3(y[:]), _h3(y[:]), st[b][:, 0:16][:, :, None].broadcast_to([P, 16, 64]), ALU.add, [k("yf"), k("st")], [k("yf")])
                _tt(S, "dve", t2[:], y[:], y[:], ALU.mult, [k("yf")], [k("yb")])
                S.op("dve", lambda e: e.tensor_reduce(out=st[b][:, 16:32], in_=_h3(t2[:]), axis=AX.X, op=ALU.add), reads=[k("yb")], writes=[k("st")])
                S.op("dve", lambda e: e.tensor_scalar(out=st[b][:, 16:32], in0=st[b][:, 16:32], scalar1=1.0 / 64, scalar2=GN_EPS, op0=ALU.mult, op1=ALU.add),
                     reads=[k("st")], writes=[k("st")])
                _act(S, st[b][:, 16:32], st[b][:, 16:32], AF.Sqrt, [k("st")], [k("st")])
                S.op("dve", lambda e: e.reciprocal(out=st[b][:, 32:48], in_=st[b][:, 16:32]), reads=[k("st")], writes=[k("st")])
                _tt(S, "dve", _h3(y[:]), _h3(y[:]), st[b][:, 32:48][:, :, None].broadcast_to([P, 16, 64]), ALU.mult, [k("yf"), k("st")], [k("yf")])
                _tt(S, "dve", y[:], y[:], R["lng_r"][:], ALU.mult, [k("yf"), "lng_r"], [k("yf")])
                _tt(S, "dve", y[:], y[:], R["lnb_r"][:], ALU.add, [k("yf"), "lnb_r"], [k("yf")])
                _tt(S, "dve", y[:], y[:], Bf["bo"][:], ALU.add, [k("yf"), k("bo")], [k("yf")])
                _tt(S, "dve", zb[b][:], y[:], Bf["gg"][:], ALU.mult, [k("yf"), k("gg")], [k("zb")])
                for c in range(8):
                    S.op("pe", lambda e: e.transpose(out=pT[b][:, c, :], in_=zb[b][:, c * P:(c + 1) * P], identity=self.idb[:]),
                         reads=[k("zb"), "idb"], writes=[k("pT3")], accum=(c > 0))
                _act(S, zt[b][:], pT[b][:], AF.Copy, [k("pT3")], [k("zt3")])
                S.dma("sp", self.zT_d[:, :, t0:t0 + P].rearrange("c p t -> p c t"), zt[b][:], reads=[k("zt3")],
                      writes=[("zT", c, T) for c in range(8)])


def build_program(stop_after=None, debug=(), phases="M0ABCD1abcde"):
    nc = bass.Bass("TRN2", target_bir_lowering=False)
    Pg = Prog1(nc, debug)
    S = Pg.S
    Pg.consts()
    if "M" in phases:
        Pg.phase_mod()
    if "0" in phases:
      with S.scope():
        V0 = Pg.load_layer_vecs(0)
        if "A" in phases: Pg.phase_L0_proj(V0)
        if "B" in phases: Pg.phase_L0_pool()
        if "C" in phases: Pg.phase_L0_attn()
        if "D" in phases:
            ab1 = Pg.load_ab(1, 0, "ab1")
            Pg.phase_out_mlp(0, V0, Pg.ev_w_out, Pg.y_tile_L0, lambda T: (Pg.x_d[T * P:(T + 1) * P, :], ("x_d", T)), next_ab=ab1)
    if "1" in phases:
      with S.scope():
        V1 = Pg.load_layer_vecs(1, part="ab")
        if "a" in phases and "D" not in phases: Pg.phase_R0(V1)
        if "b" in phases: Pg.phase_R1()
        if "c" in phases: Pg.phase_R2()
        if "d" in phases: Pg.phase_R3()
        if "e" in phases:
            Pg.load_layer_vecs(1, V=V1, part="g")
        if "e" in phases: Pg.phase_out_mlp(1, V1, Pg.rw_wo, Pg.y_tile_L0, lambda T: (Pg.out[(T - 2) * P:(T - 1) * P, :], ("out", T)))
    S.barrier()
    S.finish([])
    S.close()
    return nc, Pg


_host_inputs0 = host_inputs


def host_inputs(inputs):
    maps = _host_inputs0(inputs)
    f = lambda a: np.ascontiguousarray(np.asarray(a, dtype=np.float32))
    tri, m4, lmT = _scan_consts()
    sh = {
        "rw_mu": f(inputs["rw_mu"])[0], "rw_wr": f(inputs["rw_wr"])[0], "rw_wk": f(inputs["rw_wk"])[0],
        "rw_wv": f(inputs["rw_wv"])[0], "rw_wo": f(inputs["rw_wo"])[0],
        "rw_w0": f(inputs["rw_w0"])[0], "rw_a0": f(inputs["rw_a0"])[0],
        "w1cat": f(np.concatenate([inputs["rw_w1"][0, 0], inputs["rw_w1"][0, 1]], axis=1)),
        "a1cat": f(np.concatenate([inputs["rw_a1"][0, 0], inputs["rw_a1"][0, 1]], axis=1)),
        "rw_g1": f(inputs["rw_g1"])[0],
        "w2cat": f(np.asarray(inputs["rw_w2"])[0].reshape(128, 1024)), "a2cat": f(np.asarray(inputs["rw_a2"])[0].reshape(128, 1024)),
        "rw_g2": f(inputs["rw_g2"])[0],
        "rw_kk": f(inputs["rw_kk"]).reshape(1, 1024), "rw_ka": f(inputs["rw_ka"]).reshape(1, 1024),
        "rw_rk": f(inputs["rw_rk"]).reshape(1, 1024), "rw_lng": f(inputs["rw_lng"]).reshape(1, 1024),
        "rw_lnb": f(inputs["rw_lnb"]).reshape(1, 1024),
        "tri_c": f(tri), "m4_c": f(m4), "lmT_c": f(lmT),
    }
    for m in maps:
        m.update(sh)
    return maps
```
